# Optimizing a Trainium2 kernel written in Bass

```python
import jax, jax.numpy as jnp
from jax import lax
import numpy as np

D_MODEL = 1024
BATCH = 4
SEQ = 4096
DEPTH = 1

GRID_W = 64
CTX_LEN = 256
FF_HALF = 2816
F_GROUPS = 4
F_GROUP_DIM = 128
F_WIDTH = F_GROUPS * F_GROUP_DIM
M_HEADS = 4
M_HEAD_DIM = 256
M_WIDTH = M_HEADS * M_HEAD_DIM
CONV_K = 3
CHUNK = 128
N_ADA = 9
EPS = 1e-6

COL_F = 0
COL_Q = COL_F + F_WIDTH
COL_K = COL_Q + M_WIDTH
COL_V = COL_K + M_WIDTH
COL_O = COL_V + M_WIDTH
COL_GATES = COL_O + M_WIDTH
COL_BR = COL_GATES + 4 * M_HEADS
IN_WIDTH = COL_BR + 2 * D_MODEL

kernel_name = "hybrid_fourier_mlstm_dit_layer"


def rmsnorm(x, g):
    xf = x.astype(jnp.float32)
    y = xf * lax.rsqrt(jnp.mean(xf * xf, axis=-1, keepdims=True) + EPS)
    return (y * g.astype(jnp.float32)).astype(x.dtype)


def modulate(x, shift, scale):
    return x * (1 + scale) + shift


def ada_params(cvec, w, b):
    m = jax.nn.silu(cvec) @ w + b
    return m.reshape(cvec.shape[0], N_ADA, 1, D_MODEL)


def swiglu(u, w13, w2):
    a, b = jnp.split(u @ w13, 2, axis=-1)
    return (jax.nn.silu(a) * b) @ w2


def half_ffn(h, shift, scale, gate, g_pre, g_post, w13, w2):
    u = modulate(rmsnorm(h, g_pre), shift, scale)
    return h + 0.5 * gate * rmsnorm(swiglu(u, w13, w2), g_post)


def dwconv(x, w, b):
    y = lax.conv_general_dilated(
        x, w[:, None, :].astype(x.dtype), window_strides=(1,),
        padding=[(CONV_K // 2, CONV_K // 2)],
        dimension_numbers=("NWC", "WIO", "NWC"),
        feature_group_count=x.shape[-1])
    return y + b


def to_heads(x):
    B, T, _ = x.shape
    return x.reshape(B, T, M_HEADS, M_HEAD_DIM).transpose(0, 2, 1, 3)


def fourier_latent(xf):
    B, T, _ = xf.shape
    rows = T // GRID_W
    z = xf.astype(jnp.float32).reshape(B, rows, GRID_W, F_GROUPS, F_GROUP_DIM)
    y = jnp.fft.fftn(z, axes=(1, 2, 4), norm="ortho").real
    return y.reshape(B, T, F_WIDTH).astype(xf.dtype)


def fourier_context(xf):
    B, T, _ = xf.shape
    z = xf.astype(jnp.float32).reshape(B, T, F_GROUPS, F_GROUP_DIM)
    y = jnp.fft.fftn(z, axes=(1, 3), norm="ortho").real
    return y.reshape(B, T, F_WIDTH).astype(xf.dtype)


def project(u, w_in, b_in, conv_w, conv_b):
    B, T, _ = u.shape
    p = u @ w_in + b_in
    xf = p[..., COL_F:COL_Q]
    qk = jax.nn.silu(dwconv(p[..., COL_Q:COL_V], conv_w, conv_b))
    q = to_heads(qk[..., :M_WIDTH])
    k = to_heads(qk[..., M_WIDTH:])
    v = to_heads(p[..., COL_V:COL_O])
    o = p[..., COL_O:COL_GATES]
    gates = p[..., COL_GATES:COL_BR].astype(jnp.float32)
    gates = gates.reshape(B, T, 4, M_HEADS).transpose(2, 0, 3, 1)
    gate_pre = (gates[0], jax.nn.log_sigmoid(gates[1]), gates[2], jax.nn.log_sigmoid(gates[3]))
    g_f = p[..., COL_BR:COL_BR + D_MODEL]
    g_m = p[..., COL_BR + D_MODEL:]
    return xf, q, k, v, o, gate_pre, g_f, g_m


def mlstm_scan(q, k, v, li, lf, state):
    B, H, T, DK = q.shape
    NC = T // CHUNK

    def chunks(a):
        a = a.astype(jnp.float32).reshape(B, H, NC, CHUNK, *a.shape[3:])
        return jnp.moveaxis(a, 2, 0)

    qc, kc, vc, lic, lfc = map(chunks, (q, k * (DK ** -0.5), v, li, lf))
    lower = jnp.tril(jnp.ones((CHUNK, CHUNK), dtype=bool))

    def step(carry, inp):
        C, n, m = carry
        qx, kx, vx, lix, lfx = inp
        b = jnp.cumsum(lfx, axis=-1)
        dmat = b[..., :, None] - b[..., None, :] + lix[..., None, :]
        dmat = jnp.where(lower, dmat, -jnp.inf)
        inter = b + m[..., None]
        m_t = jnp.maximum(inter, jnp.max(dmat, axis=-1))
        w = jnp.exp(dmat - m_t[..., None])
        a = jnp.exp(inter - m_t)
        s = jnp.einsum("bhtd,bhsd->bhts", qx, kx) * w
        num = a[..., None] * jnp.einsum("bhtd,bhde->bhte", qx, C) + jnp.einsum("bhts,bhse->bhte", s, vx)
        den = a * jnp.einsum("bhtd,bhd->bht", qx, n) + jnp.sum(s, axis=-1)
        h = num / jnp.maximum(jnp.abs(den), jnp.exp(-m_t))[..., None]
        bL = b[..., -1]
        g = bL[..., None] - b + lix
        m_new = jnp.maximum(bL + m, jnp.max(g, axis=-1))
        decay = jnp.exp(bL + m - m_new)
        wk = jnp.exp(g - m_new[..., None])
        C_new = decay[..., None, None] * C + jnp.einsum("bhsd,bhse->bhde", kx * wk[..., None], vx)
        n_new = decay[..., None] * n + jnp.einsum("bhs,bhsd->bhd", wk, kx)
        return (C_new, n_new, m_new), h

    state_out, hs = lax.scan(step, state, (qc, kc, vc, lic, lfc))
    h = jnp.moveaxis(hs, 0, 2).reshape(B, H, T, -1)
    return h.astype(q.dtype), state_out


def bidir_mlstm(q, k, v, gate_pre, state_f, state_b):
    li_f, lf_f, li_b, lf_b = gate_pre
    h_f, s_f = mlstm_scan(q, k, v, li_f, lf_f, state_f)
    flip = lambda a: jnp.flip(a, axis=2)
    h_b, s_b = mlstm_scan(flip(q), flip(k), flip(v), flip(li_b), flip(lf_b), state_b)
    return h_f + flip(h_b), s_f, s_b


def merge(y_fourier, h, o, g_f, g_m, head_g, w_four, w_mproj, w_out):
    B, H, T, dh = h.shape
    hn = rmsnorm(h.transpose(0, 2, 1, 3), head_g.reshape(M_HEADS, M_HEAD_DIM)).reshape(B, T, M_WIDTH)
    hm = jax.nn.sigmoid(o) * hn
    y = jax.nn.sigmoid(g_f) * (y_fourier @ w_four) + jax.nn.sigmoid(g_m) * (hm @ w_mproj)
    return y @ w_out


def token_mixing(h_lat, h_ctx, mod_l, mod_c, g_pre, g_post, w_in, b_in, conv_w, conv_b,
                 head_g, w_four, w_mproj, w_out, update_ctx):
    sh_l, sc_l, gt_l = mod_l
    sh_c, sc_c, gt_c = mod_c
    pl = project(modulate(rmsnorm(h_lat, g_pre), sh_l, sc_l), w_in, b_in, conv_w, conv_b)
    pc = project(modulate(rmsnorm(h_ctx, g_pre), sh_c, sc_c), w_in, b_in, conv_w, conv_b)
    B = h_ctx.shape[0]
    zero = (jnp.zeros((B, M_HEADS, M_HEAD_DIM, M_HEAD_DIM), jnp.float32),
            jnp.zeros((B, M_HEADS, M_HEAD_DIM), jnp.float32),
            jnp.zeros((B, M_HEADS), jnp.float32))
    h_c, s_f, s_b = bidir_mlstm(pc[1], pc[2], pc[3], pc[5], zero, zero)
    h_l, _, _ = bidir_mlstm(pl[1], pl[2], pl[3], pl[5], s_f, s_b)
    out_l = merge(fourier_latent(pl[0]), h_l, pl[4], pl[6], pl[7], head_g, w_four, w_mproj, w_out)
    new_lat = h_lat + gt_l * rmsnorm(out_l, g_post)
    new_ctx = None
    if update_ctx:
        out_c = merge(fourier_context(pc[0]), h_c, pc[4], pc[6], pc[7], head_g, w_four, w_mproj, w_out)
        new_ctx = h_ctx + gt_c * rmsnorm(out_c, g_post)
    return new_lat, new_ctx


def setup_inputs(seed: int = 0) -> dict:
    key = jax.random.key(seed)
    ks = jax.random.split(key, 20)
    nrm = lambda k, shape, s: jax.random.normal(k, shape, jnp.float32) * s
    D = D_MODEL
    b_in = nrm(ks[10], (DEPTH, IN_WIDTH), 0.02)
    f_bias = jnp.linspace(3.0, 6.0, M_HEADS, dtype=jnp.float32)
    b_in = b_in.at[:, COL_GATES + M_HEADS:COL_GATES + 2 * M_HEADS].add(f_bias)
    b_in = b_in.at[:, COL_GATES + 3 * M_HEADS:COL_GATES + 4 * M_HEADS].add(f_bias)
    return {
        "x": nrm(ks[0], (BATCH, SEQ, D), 1.0),
        "c": nrm(ks[1], (BATCH, D), 1.0),
        "ctx": nrm(ks[2], (BATCH, CTX_LEN, D), 1.0),
        "c_ctx": nrm(ks[3], (D,), 1.0),
        "w_ada": nrm(ks[4], (DEPTH, D, N_ADA * D), D ** -0.5),
        "b_ada": nrm(ks[5], (DEPTH, N_ADA * D), 0.02),
        "norm_g": 1.0 + nrm(ks[6], (DEPTH, 6, D), 0.05),
        "w13_a": nrm(ks[7], (DEPTH, D, 2 * FF_HALF), D ** -0.5),
        "w2_a": nrm(ks[8], (DEPTH, FF_HALF, D), FF_HALF ** -0.5),
        "w_in": nrm(ks[9], (DEPTH, D, IN_WIDTH), D ** -0.5),
        "b_in": b_in,
        "conv_w": nrm(ks[11], (DEPTH, CONV_K, 2 * M_WIDTH), CONV_K ** -0.5),
        "conv_b": nrm(ks[12], (DEPTH, 2 * M_WIDTH), 0.02),
        "head_g": 1.0 + nrm(ks[13], (DEPTH, M_WIDTH), 0.05),
        "w_four": nrm(ks[14], (DEPTH, F_WIDTH, D), F_WIDTH ** -0.5),
        "w_mproj": nrm(ks[15], (DEPTH, M_WIDTH, D), M_WIDTH ** -0.5),
        "w_out": nrm(ks[16], (DEPTH, D, D), D ** -0.5),
        "w13_b": nrm(ks[17], (DEPTH, D, 2 * FF_HALF), D ** -0.5),
        "w2_b": nrm(ks[18], (DEPTH, FF_HALF, D), FF_HALF ** -0.5),
    }


def reference(x, c, ctx, c_ctx, w_ada, b_ada, norm_g, w13_a, w2_a, w_in, b_in, conv_w, conv_b,
              head_g, w_four, w_mproj, w_out, w13_b, w2_b):
    h_lat = x
    h_ctx = ctx
    for l in range(DEPTH):
        last = l == DEPTH - 1
        ml = ada_params(c, w_ada[l], b_ada[l])
        mc = ada_params(c_ctx[None], w_ada[l], b_ada[l])
        g = norm_g[l]
        h_lat = half_ffn(h_lat, ml[:, 0], ml[:, 1], ml[:, 2], g[0], g[1], w13_a[l], w2_a[l])
        h_ctx = half_ffn(h_ctx, mc[:, 0], mc[:, 1], mc[:, 2], g[0], g[1], w13_a[l], w2_a[l])
        h_lat, h_ctx_new = token_mixing(
            h_lat, h_ctx, (ml[:, 3], ml[:, 4], ml[:, 5]), (mc[:, 3], mc[:, 4], mc[:, 5]),
            g[2], g[3], w_in[l], b_in[l], conv_w[l], conv_b[l], head_g[l],
            w_four[l], w_mproj[l], w_out[l], not last)
        h_lat = half_ffn(h_lat, ml[:, 6], ml[:, 7], ml[:, 8], g[4], g[5], w13_b[l], w2_b[l])
        if not last:
            h_ctx = half_ffn(h_ctx_new, mc[:, 6], mc[:, 7], mc[:, 8], g[4], g[5], w13_b[l], w2_b[l])
    return h_lat
```

```python
import numpy as np
import os as _os
from contextlib import ExitStack
import concourse.bass as bass
import concourse.mybir as mybir
from concourse.bass_utils import run_bass_kernel_spmd

F32 = mybir.dt.float32
BF16 = mybir.dt.bfloat16
AF = mybir.ActivationFunctionType
ALU = mybir.AluOpType

ENGS = ("pe", "act", "dve", "pool", "sp")
EPOCH = 30000

D = 1024
SEQ = 4096
CTX = 256
NT = CTX + SEQ
OWN = 2048
FF = 2816
NJ = 22
EPS = 1e-6
NEG = -30000.0


class Tick:
    __slots__ = ("sem", "val", "know")

    def __init__(self, sem, val, know):
        self.sem = sem
        self.val = val
        self.know = know


class Buf:
    __slots__ = ("name", "w", "r")

    def __init__(self, name=""):
        self.name = name
        self.w = None
        self.r = {}


class Sched:
    def __init__(self, nc, stack, n_dma_sems=48):
        self.nc = nc
        self.stack = stack
        self.q = {e: [] for e in ENGS}
        self.cnt = {e: 0 for e in ENGS}
        self.esems = {e: [] for e in ENGS}
        self.known = {e: {} for e in ENGS}
        self.dsems = [stack.enter_context(nc.semaphore(f"dma{i}")) for i in range(n_dma_sems)]
        self.dcnt = [0] * n_dma_sems
        self.dlast = [None] * n_dma_sems
        self.drr = 0
        self.drr2 = {}

    def _esem(self, eng, idx):
        lst = self.esems[eng]
        while len(lst) <= idx:
            lst.append(self.stack.enter_context(self.nc.semaphore(f"e_{eng}_{len(lst)}")))
        return lst[idx]

    def _collect(self, eng, reads, writes, extra=()):
        kn = self.known[eng]
        waits = {}

        def need(t):
            if t is None:
                return
            if kn.get(t.sem, 0) >= t.val:
                return
            if waits.get(t.sem, (0, None))[0] < t.val:
                waits[t.sem] = (t.val, t)

        for b in reads:
            need(b.w)
        for b in writes:
            need(b.w)
            for t in b.r.values():
                need(t)
        for t in extra:
            need(t)
        items = sorted(waits.items(), key=lambda kv: -kv[1][0])
        final = []
        for sem, (val, t) in items:
            if kn.get(sem, 0) >= val:
                continue
            final.append((sem, val))
            kn[sem] = val
            for s2, v2 in t.know.items():
                if kn.get(s2, 0) < v2:
                    kn[s2] = v2
        return final

    def op(self, eng, fn, reads=(), writes=(), extra=()):
        waits = self._collect(eng, reads, writes, extra)
        c = self.cnt[eng]
        sem = self._esem(eng, c // EPOCH)
        val = c % EPOCH + 1
        self.cnt[eng] = c + 1
        t = Tick(sem, val, dict(self.known[eng]))
        for b in reads:
            b.r[sem] = t
        for b in writes:
            b.w = t
            b.r = {}
        self.q[eng].append((waits, fn, sem, 1))
        return t

    def dma(self, eng, fn, reads=(), writes=(), extra=()):
        n = len(self.dsems)
        lo, hi = (0, n // 3) if eng == "pool" else (n // 3, n)
        rr = self.drr2.get(eng, lo)
        i = rr
        self.drr2[eng] = lo + (rr + 1 - lo) % (hi - lo)
        ex = list(extra)
        if self.dlast[i] is not None:
            ex.append(self.dlast[i])
        waits = self._collect(eng, reads, writes, ex)
        self.dcnt[i] += 1
        sem = self.dsems[i]
        t = Tick(sem, 16 * self.dcnt[i], dict(self.known[eng]))
        self.dlast[i] = t
        for b in reads:
            b.r[sem] = t
        for b in writes:
            b.w = t
            b.r = {}
        self.q[eng].append((waits, fn, sem, 16))
        return t

    def wait_all(self, eng, ticks):
        waits = self._collect(eng, (), (), ticks)
        self.q[eng].append((waits, None, None, 0))

    def barrier(self):
        ticks = []
        for e in ENGS:
            c = self.cnt[e]
            if c > 0:
                ticks.append(Tick(self._esem(e, (c - 1) // EPOCH), (c - 1) % EPOCH + 1, {}))
        for t in self.dlast:
            if t is not None:
                ticks.append(t)
        for e in ENGS:
            self.wait_all(e, ticks)

    def emit(self):
        nc = self.nc
        q = self.q

        def run(engobj, lst):
            for waits, fn, sem, amt in lst:
                for s, v in waits:
                    engobj.wait_ge(s, v)
                if fn is not None:
                    ins = fn(engobj)
                    ins.then_inc(sem, amt)

        with nc.Block() as block:
            @block.tensor
            def _(e):
                run(e, q["pe"])

            @block.scalar
            def _(e):
                run(e, q["act"])

            @block.vector
            def _(e):
                run(e, q["dve"])

            @block.gpsimd
            def _(e):
                run(e, q["pool"])

            @block.sync
            def _(e):
                run(e, q["sp"])


class Arena:
    def __init__(self, nc, name, words):
        self.t = nc.alloc_sbuf_tensor(name, [128, words], F32)
        self.words = words
        self.off = 0

    def mark(self):
        return self.off

    def reset(self, m):
        self.off = m

    def f32(self, n):
        a = self.t[:, self.off:self.off + n]
        self.off += n
        assert self.off <= self.words, ("arena overflow", self.off, self.words)
        return a

    def bf16(self, n):
        w = (n + 1) // 2
        a = self.t[:, self.off:self.off + w].bitcast(BF16)
        self.off += w
        assert self.off <= self.words, ("arena overflow", self.off, self.words)
        return a[:, 0:n]


VEC = {}
_o = 0
for _n, _w in [("bada", 72), ("ng", 48), ("bF", 4), ("bq", 8), ("bk", 8), ("cwq", 24), ("cwk", 24),
               ("cbq", 8), ("cbk", 8), ("bgf", 8), ("bgm", 8)]:
    VEC[_n] = (_o, _w)
    _o += _w
NV = _o


def build(stage=99, dbg=False):
    nc = bass.Bass("TRN2", target_bir_lowering=False)
    dt_in = lambda name, shape, dt=F32: nc.dram_tensor(name, shape, dt, kind="ExternalInput").ap()
    xT = dt_in("xT", [D, SEQ])
    ctxT = dt_in("ctxT", [D, CTX])
    cvec = dt_in("cvec", [128, 16])
    w_ada = dt_in("w_ada", [D, 9 * D])
    vecs = dt_in("vecs", [128, NV])
    rowv = dt_in("rowv", [1, 3 * D])
    bg = dt_in("bg", [36, 2])
    w13a = dt_in("w13a", [NJ, 128, 2048])
    w2a = dt_in("w2a", [8, 128, FF])
    w13b = dt_in("w13b", [NJ, 128, 2048])
    w2b = dt_in("w2b", [8, 128, FF])
    wF = dt_in("wF", [D, 512])
    wq = dt_in("wq", [D, D])
    wk = dt_in("wk", [D, D])
    wv = dt_in("wv", [D, D])
    wo = dt_in("wo", [D, D])
    wgf = dt_in("wgf", [D, D])
    wgm = dt_in("wgm", [D, D])
    wgate_f = dt_in("wgate_f", [D, 8])
    wgate_b = dt_in("wgate_b", [D, 72])
    w_four = dt_in("w_four", [512, D])
    w_mproj = dt_in("w_mproj", [D, D])
    w_out = dt_in("w_out", [D, D])
    cst = dt_in("cst", [128, 128 * 4 + 256 + 128 + 32])
    selc = dt_in("selc", [36, 2 * 8 * 128])
    outT = nc.dram_tensor("outT", [D, OWN], F32, kind="ExternalOutput").ap()
    dbg_out = {}

    def dbg_tensor(name, shape, dt=F32):
        dbg_out[name] = nc.dram_tensor(name, shape, dt, kind="ExternalOutput").ap()
        return dbg_out[name]

    dscr = lambda name, shape, dt: nc.dram_tensor(name, shape, dt, kind="Internal").ap()
    H1 = dscr("H1", [D, OWN], F32)
    AB = dscr("AB", [2, SEQ, 512], F32)
    PQ = dscr("PQ", [2, 64, 64, 512], F32)
    KT = dscr("KT", [D, OWN], BF16)
    QT = dscr("QT", [D, OWN], BF16)
    KTOK = dscr("KTOK", [NT, D], BF16)
    VTOK = dscr("VTOK", [NT, D], BF16)
    B_H1, B_AB, B_PQ, B_KT, B_QT, B_KTOK, B_VTOK = [Buf(n) for n in "H1 AB PQ KT QT KTOK VTOK".split()]

    st = ExitStack()
    with st:
        S = Sched(nc, st)
        ps = [st.enter_context(nc.psum_tensor(f"ps{i}", [128, 512], F32)) for i in range(8)]
        Bps = [Buf(f"ps{i}") for i in range(8)]

        cs_t = nc.alloc_sbuf_tensor("cs", [128, 128 * 4 + 256 + 128 + 32], F32)
        ident = cs_t[:, 0:128]
        maskf = cs_t[:, 128:256]
        maskb = cs_t[:, 256:384]
        CS = cs_t[:, 512:768]
        M2 = cs_t[:, 768:896]
        M3 = cs_t[:, 896:928]
        vec_t = nc.alloc_sbuf_tensor("vec", [128, NV], F32)
        mod_t = nc.alloc_sbuf_tensor("mod", [128, 72 * 2], F32)
        der_t = nc.alloc_sbuf_tensor("der", [128, 16 * 8], F32)
        id16_t = nc.alloc_sbuf_tensor("id16", [128, 128], BF16)
        ones16_t = nc.alloc_sbuf_tensor("ones16", [128, 128], BF16)
        u2_t = nc.alloc_sbuf_tensor("u2", [128, 8 * NT], BF16)
        u2 = u2_t[:, :].rearrange("p (k t) -> p k t", k=8)
        B_const = Buf("const")
        B_mod = Buf("mod")
        B_der = Buf("der")
        tiles = [(0, CTX, 1)] + [(CTX + 512 * i, 512, 0) for i in range(8)]
        B_u2 = [Buf(f"u2_{i}") for i in range(9)]

        def vcol(name, i=0, n=1):
            o, w = VEC[name]
            return vec_t[:, o + i:o + i + n]

        DER = {}
        _d = 0
        for nm in ["A0l", "A0c", "S0l", "S0c", "PAl", "PAc", "A2l", "A2c", "S2l", "S2c", "PMl", "A4l", "S4l", "PBl"]:
            DER[nm] = der_t[:, _d * 8:(_d + 1) * 8]
            _d += 1

        arena = Arena(nc, "arena", 34000)

        S.dma("sp", lambda e: e.dma_start(out=cs_t[:, :], in_=cst), writes=[B_const])
        S.dma("sp", lambda e: e.dma_start(out=vec_t[:, :], in_=vecs), writes=[B_const])
        S.op("act", lambda e: e.copy(id16_t[:, :], ident), reads=[B_const], writes=[B_const])
        S.op("pool", lambda e: e.memset(ones16_t[:, :], 1.0), writes=[B_const])

        m0 = arena.mark()
        cv = arena.f32(16)
        scv = arena.f32(16)
        B_cv = Buf("cv")
        S.dma("sp", lambda e: e.dma_start(out=cv, in_=cvec), writes=[B_cv])
        S.op("act", lambda e: e.activation(scv, cv, AF.Silu), reads=[B_cv], writes=[B_cv])
        wad = [arena.f32(8 * 1024) for _ in range(2)]
        B_wad = [Buf("wad0"), Buf("wad1")]
        w_ada_v = w_ada.rearrange("(k p) n -> p k n", p=128)
        modps = ps[7][:, 0:144]
        for mi in range(9):
            sl = mi % 2
            wv_ = wad[sl].rearrange("p (k n) -> p k n", k=8)
            S.dma("sp", (lambda e, wv_=wv_, mi=mi: e.dma_start(out=wv_, in_=w_ada_v[:, :, mi * 1024:(mi + 1) * 1024])),
                  writes=[B_wad[sl]])
            for dc in range(8):
                def fn(e, wv_=wv_, mi=mi, dc=dc):
                    for k in range(8):
                        ins = e.matmul(modps[:, (mi * 8 + dc) * 2:(mi * 8 + dc) * 2 + 2],
                                       wv_[:, k, dc * 128:(dc + 1) * 128],
                                       scv[:, k * 2:k * 2 + 2], start=(k == 0), stop=(k == 7))
                    return ins
                S.op("pe", fn, reads=[B_wad[sl], B_cv], writes=[Bps[7]])
        modv = mod_t[:, :].rearrange("p (m j) -> p m j", j=2)
        modpsv = modps.rearrange("p (m j) -> p m j", j=2)
        bada = vcol("bada", 0, 72)
        for j in range(2):
            S.op("dve", (lambda e, j=j: e.tensor_tensor(modv[:, :, j], modpsv[:, :, j], bada, ALU.add)),
                 reads=[Bps[7], B_const], writes=[B_mod])

        def modc(mi, j):
            return modv[:, mi * 8:(mi + 1) * 8, j]

        def ng(i):
            return vcol("ng", i * 8, 8)

        def der_scale(name, mi, gi, j):
            S.op("dve", lambda e: e.scalar_tensor_tensor(DER[name], modc(mi, j), 1.0, ng(gi), ALU.add, ALU.mult),
                 reads=[B_mod, B_const], writes=[B_der])

        def der_gate(name, mi, gi, j, f):
            S.op("dve", lambda e: e.scalar_tensor_tensor(DER[name], modc(mi, j), f, ng(gi), ALU.mult, ALU.mult),
                 reads=[B_mod, B_const], writes=[B_der])

        def der_copy(name, mi, j):
            S.op("dve", lambda e: e.tensor_copy(DER[name], modc(mi, j)), reads=[B_mod], writes=[B_der])

        der_scale("A0l", 1, 0, 0); der_scale("A0c", 1, 0, 1)
        der_copy("S0l", 0, 0); der_copy("S0c", 0, 1)
        der_gate("PAl", 2, 1, 0, 0.5); der_gate("PAc", 2, 1, 1, 0.5)
        der_scale("A2l", 4, 2, 0); der_scale("A2c", 4, 2, 1)
        der_copy("S2l", 3, 0); der_copy("S2c", 3, 1)
        der_gate("PMl", 5, 3, 0, 1.0)
        der_scale("A4l", 7, 4, 0); der_copy("S4l", 6, 0); der_gate("PBl", 8, 5, 0, 0.5)
        S.barrier()
        arena.reset(m0)

        def rstd_from(sq_tile, B_sq, W, rstd, B_rstd, pbank):
            def fn(e):
                for k in range(8):
                    ins = e.matmul(ps[pbank][:, 0:W], ones16_t[:, :], sq_tile[:, k, 0:W], start=(k == 0), stop=(k == 7))
                return ins
            S.op("pe", fn, reads=[B_sq, B_const], writes=[Bps[pbank]])
            S.op("act", lambda e: e.activation(rstd[:, 0:W], ps[pbank][:, 0:W], AF.Ln, bias=EPS, scale=1.0 / D),
                 reads=[Bps[pbank]], writes=[B_rstd])
            S.op("act", lambda e: e.activation(rstd[:, 0:W], rstd[:, 0:W], AF.Exp, scale=-0.5),
                 reads=[B_rstd], writes=[B_rstd])

        def norm_mod_thunks(src, B_src, W, rstd, B_rstd, A, Sh, dst_fn, B_dst, tmp, B_tmp):
            def one(k):
                t = tmp[k % 2]
                bt = B_tmp[k % 2]
                S.op("dve", (lambda e: e.tensor_tensor(t[:, 0:W], src[:, k, 0:W], rstd[:, 0:W], ALU.mult)),
                     reads=[B_src, B_rstd], writes=[bt])
                S.op("act", (lambda e: e.activation(dst_fn(k), t[:, 0:W], AF.Identity, bias=Sh[:, k:k + 1], scale=A[:, k:k + 1])),
                     reads=[bt, B_der], writes=[B_dst])
            return [(lambda k=k: one(k)) for k in range(8)]

        def norm_mod(src, B_src, W, rstd, B_rstd, A, Sh, dst_fn, B_dst, tmp, B_tmp):
            for k in range(8):
                t = tmp[k % 2]
                bt = B_tmp[k % 2]
                S.op("dve", (lambda e, k=k, t=t: e.tensor_tensor(t[:, 0:W], src[:, k, 0:W], rstd[:, 0:W], ALU.mult)),
                     reads=[B_src, B_rstd], writes=[bt])
                S.op("act", (lambda e, k=k, t=t: e.activation(dst_fn(k), t[:, 0:W], AF.Identity,
                                                              bias=Sh[:, k:k + 1], scale=A[:, k:k + 1])),
                     reads=[bt, B_der], writes=[B_dst])

        WC = {}

        def wload(dst, B_dst, src_f32, cache, key, ncols):
            if cache is None:
                S.dma("pool", lambda e: e.dma_start(out=dst, in_=src_f32), writes=[B_dst])
                return
            k = (cache,) + key
            if k not in WC:
                sc_ = nc.dram_tensor("wc_" + "_".join(str(z) for z in k), [128, ncols], BF16, kind="Internal").ap()
                WC[k] = (sc_, Buf("wc"))
                S.dma("pool", lambda e: e.dma_start(out=dst, in_=src_f32), writes=[B_dst])
                dflat = dst if len(dst.shape) == 2 else dst.rearrange("p a b -> p (a b)")
                S.dma("sp", lambda e: e.dma_start(out=sc_, in_=dflat), reads=[B_dst], writes=[WC[k][1]])
            else:
                sc_, bsc = WC[k]
                dflat = dst if len(dst.shape) == 2 else dst.rearrange("p a b -> p (a b)")
                S.dma("sp", lambda e: e.dma_start(out=dflat, in_=sc_), reads=[bsc], writes=[B_dst])

        def ffn(src, B_src, W, rstd_pre, B_rstd_pre, A, Sh, PG, w13r, w2r, bufs, cache, do_pre=True, hook1=None, hook2=None, defer_epi=False):
            (sq, B_sq, u, B_u, g, B_g, y, B_y, tmp, B_tmp, rs2, B_rs2, wb13, B_wb13, wb2, B_wb2, sa, B_sa) = bufs
            if do_pre:
                norm_mod(src, B_src, W, rstd_pre, B_rstd_pre, A, Sh, lambda k: u[:, k, 0:W], B_u, tmp, B_tmp)
            n13 = len(wb13)
            for j in range(NJ):
                sl = j % n13
                wbv = wb13[sl].rearrange("p (k c) -> p k c", k=8)
                wload(wb13[sl], B_wb13[sl], w13r[j], cache, ("w13", j), 2048)
                pa = j % 2
                pb = 2 + j % 2

                def fa(e, wbv=wbv, pa=pa):
                    for k in range(8):
                        ins = e.matmul(ps[pa][:, 0:W], wbv[:, k, 0:128], u[:, k, 0:W], start=(k == 0), stop=(k == 7))
                    return ins

                def fb(e, wbv=wbv, pb=pb):
                    for k in range(8):
                        ins = e.matmul(ps[pb][:, 0:W], wbv[:, k, 128:256], u[:, k, 0:W], start=(k == 0), stop=(k == 7))
                    return ins
                S.op("pe", fa, reads=[B_wb13[sl], B_u], writes=[Bps[pa]])
                S.op("pe", fb, reads=[B_wb13[sl], B_u], writes=[Bps[pb]])
                s2 = j % 2
                S.op("act", (lambda e, pa=pa, s2=s2: e.activation(sa[s2][:, 0:W], ps[pa][:, 0:W], AF.Silu)),
                     reads=[Bps[pa]], writes=[B_sa[s2]])
                S.op("dve", (lambda e, pb=pb, s2=s2, j=j: e.tensor_tensor(g[:, j, 0:W], sa[s2][:, 0:W], ps[pb][:, 0:W], ALU.mult)),
                     reads=[B_sa[s2], Bps[pb]], writes=[B_g])
                if hook1 is not None:
                    hook1(j)
            n2 = len(wb2)
            HJ = NJ // 2
            for i in range(8):
                halves = []
                for hf in range(2):
                    sl = (2 * i + hf) % n2
                    wload(wb2[sl], B_wb2[sl], w2r[i][:, hf * HJ * 128:(hf + 1) * HJ * 128], cache, ("w2", i, hf), HJ * 128)
                    halves.append((wb2[sl].rearrange("p (j c) -> p j c", j=HJ), B_wb2[sl]))
                py = 4 + i % 2

                def fy(e, halves=halves, py=py):
                    for j in range(NJ):
                        wv_ = halves[j // HJ][0]
                        ins = e.matmul(ps[py][:, 0:W], wv_[:, j % HJ, :], g[:, j, 0:W], start=(j == 0), stop=(j == NJ - 1))
                    return ins
                S.op("pe", fy, reads=[halves[0][1], halves[1][1], B_g], writes=[Bps[py]])
                S.op("act", (lambda e, py=py, i=i: e.copy(y[:, i, 0:W], ps[py][:, 0:W])), reads=[Bps[py]], writes=[B_y])
                S.op("act", (lambda e, py=py, i=i: e.activation(sq[:, i, 0:W], ps[py][:, 0:W], AF.Square)),
                     reads=[Bps[py]], writes=[B_sq])
                if hook2 is not None:
                    hook2(i)
            rstd_from(sq, B_sq, W, rs2, B_rs2, 7)

            def resid(i):
                t = tmp[i % 2]
                bt = B_tmp[i % 2]
                S.op("dve", (lambda e: e.tensor_tensor(t[:, 0:W], y[:, i, 0:W], rs2[:, 0:W], ALU.mult)),
                     reads=[B_y, B_rs2], writes=[bt])
                S.op("dve", (lambda e: e.scalar_tensor_tensor(src[:, i, 0:W], t[:, 0:W], PG[:, i:i + 1],
                                                              src[:, i, 0:W], ALU.mult, ALU.add)),
                     reads=[bt, B_der], writes=[B_src])
            thunks = [(lambda i=i: resid(i)) for i in range(8)]
            if defer_epi:
                return thunks
            for th in thunks:
                th()
            return []

        def alloc_ffn_bufs():
            sq = arena.bf16(8 * 512).rearrange("p (k t) -> p k t", k=8)
            u = arena.bf16(8 * 512).rearrange("p (k t) -> p k t", k=8)
            g = arena.bf16(NJ * 512).rearrange("p (k t) -> p k t", k=NJ)
            y = arena.f32(8 * 512).rearrange("p (k t) -> p k t", k=8)
            tmp = [arena.f32(512) for _ in range(2)]
            rs2 = arena.f32(512)
            wb13 = [arena.bf16(2048) for _ in range(3)]
            wb2 = [arena.bf16(FF // 2) for _ in range(4)]
            sa = [arena.f32(512) for _ in range(2)]
            return (sq, Buf("sq"), u, Buf("u"), g, Buf("g"), y, Buf("y"), tmp, [Buf("t0"), Buf("t1")],
                    rs2, Buf("rs2"), wb13, [Buf("wb13_%d" % i) for i in range(3)], wb2, [Buf("wb2_%d" % i) for i in range(4)],
                    sa, [Buf("sa0"), Buf("sa1")])

        mA = arena.mark()
        xt = [arena.f32(8 * 512).rearrange("p (k t) -> p k t", k=8) for _ in range(2)]
        B_xt = [Buf("xt0"), Buf("xt1")]
        rs1 = arena.f32(512)
        B_rs1 = Buf("rs1")
        fb = alloc_ffn_bufs()
        tmp, B_tmp, u_, B_u_ = fb[8], fb[9], fb[2], fb[3]
        sqx = arena.bf16(8 * 512).rearrange("p (k t) -> p k t", k=8)
        B_sqx = Buf("sqx")
        xT_v = xT.rearrange("(k p) t -> p k t", p=128)
        ctxT_v = ctxT.rearrange("(k p) t -> p k t", p=128)
        if dbg:
            d_u2 = dbg_tensor("d_u2", [128, 8 * NT], BF16)
        ntile = len(tiles) if stage >= 1 else 0

        def a1_load(ti):
            c0, W, j = tiles[ti]
            x = xt[ti % 2]
            src = ctxT_v[:, :, 0:W] if j == 1 else xT_v[:, :, c0 - CTX:c0 - CTX + W]
            S.dma("pool", lambda e: e.dma_start(out=x[:, :, 0:W], in_=src), writes=[B_xt[ti % 2]])

        def sq_thunks(x, bx, W):
            return [(lambda k=k: S.op("act", (lambda e: e.activation(sqx[:, k, 0:W], x[:, k, 0:W], AF.Square)), reads=[bx], writes=[B_sqx]))
                    for k in range(8)]

        def a1_pre_thunks(ti):
            c0, W, j = tiles[ti]
            x = xt[ti % 2]; bx = B_xt[ti % 2]
            sfx = "c" if j == 1 else "l"
            th = sq_thunks(x, bx, W)
            th.append(lambda: rstd_from(sqx, B_sqx, W, rs1, B_rs1, 6))
            th += norm_mod_thunks(x, bx, W, rs1, B_rs1, DER["A0" + sfx], DER["S0" + sfx], lambda k: u_[:, k, 0:W], B_u_, tmp, B_tmp)
            return th

        def a1_epi2_thunks(ti):
            c0, W, j = tiles[ti]
            x = xt[ti % 2]; bx = B_xt[ti % 2]
            sfx = "c" if j == 1 else "l"
            th = []
            if 1 <= ti <= 4:
                o0 = c0 - CTX
                th.append(lambda: S.dma("pool", lambda e: e.dma_start(out=H1.rearrange("(k p) t -> p k t", p=128)[:, :, o0:o0 + 512], in_=x[:, :, :]),
                                        reads=[bx], writes=[B_H1]))
            th += sq_thunks(x, bx, W)
            th.append(lambda: rstd_from(sqx, B_sqx, W, rs1, B_rs1, 6))
            th += norm_mod_thunks(x, bx, W, rs1, B_rs1, DER["A2" + sfx], DER["S2" + sfx], (lambda k: u2[:, k, c0:c0 + W]), B_u2[ti], tmp, B_tmp)
            return th

        pend1 = []
        pend2 = []
        if ntile:
            a1_load(0)
            for th in a1_pre_thunks(0):
                th()
            if ntile > 1:
                a1_load(1)
        for ti in range(ntile):
            c0, W, j = tiles[ti]
            sfx = "c" if j == 1 else "l"
            if ti + 1 < ntile:
                pend2 = a1_pre_thunks(ti + 1)

            def hook1(j_):
                n = 2 if len(pend1) > (NJ - 1 - j_) else 1
                for _ in range(n):
                    if pend1:
                        pend1.pop(0)()

            def hook2(i_):
                while pend1:
                    pend1.pop(0)()
                if i_ >= 1:
                    for _ in range(3):
                        if pend2:
                            pend2.pop(0)()
            epi1 = ffn(xt[ti % 2], B_xt[ti % 2], W, None, None, None, None, DER["PA" + sfx], w13a, w2a, fb, "A",
                       do_pre=False, hook1=hook1, hook2=hook2, defer_epi=True)
            while pend1:
                pend1.pop(0)()
            while pend2:
                pend2.pop(0)()
            pend1 = list(epi1) + a1_epi2_thunks(ti)
            if ti + 2 < ntile:
                pend1.append(lambda ti=ti: a1_load(ti + 2))
        while pend1:
            pend1.pop(0)()
        S.barrier()
        arena.reset(mA)
        if dbg:
            S.dma("sp", lambda e: e.dma_start(out=d_u2, in_=u2_t[:, :]), reads=B_u2)
            d_h1 = dbg_tensor("d_h1", [D, OWN])
            S.dma("sp", lambda e: e.dma_start(out=d_h1, in_=H1), reads=[B_H1])

        env = dict(locals())
        if stage >= 2:
            build_rest(env)
        S.barrier()
        S.emit()
    return nc, dbg_out


def build_rest3(g_, L):
    AX = mybir.AxisListType
    nc = g_["nc"]; S = g_["S"]; ps = g_["ps"]; Bps = g_["Bps"]; arena = g_["arena"]
    u2 = g_["u2"]; B_u2 = g_["B_u2"]; vcol = g_["vcol"]; DER = g_["DER"]
    id16 = g_["id16"]; B_const = g_["B_const"]; mm = g_["mm"]
    H1, HS = g_["H1"], g_["HS"]; B_H1 = g_["B_H1"]; B_HS = g_["B_HS"]
    yfT, B_yfT, mB = g_["yfT"], g_["B_yfT"], g_["mB"]
    rowv = g_["rowv"]; outT = g_["outT"]
    ffn = g_["ffn"]; rstd_from = g_["rstd_from"]; wload = g_["wload"]
    kp = lambda ap: ap.rearrange("(k p) n -> p k n", p=128)
    A = arena.t
    arena.reset(mB)
    tmp = [A[:, 0:512], A[:, 512:1024]]; B_tmp = [Buf("ct0"), Buf("ct1")]
    rs2 = A[:, 1024:1536]; B_rs2 = Buf("crs2")
    yreg = A[:, 5816:9912]
    B_yreg = Buf("yreg")
    y = yreg.rearrange("p (k t) -> p k t", k=8)
    hs_t = yreg[:, 0:1024]; o_sb = yreg[:, 1024:2048]; sig = yreg[:, 2048:3072]; sqh = yreg[:, 3072:4096]
    B_hsC = Buf("c_hs"); B_osb = Buf("c_osb"); B_sig = Buf("c_sig"); B_sqh = Buf("c_sqh")
    bo_bc = A[:, 9912:10936]; hg_bc = A[:, 10936:11960]
    B_bc = Buf("cbc")
    x = arena.f32(4096).rearrange("p (k t) -> p k t", k=8); B_x = Buf("cx")
    rs1 = arena.f32(512); B_rs1 = Buf("crs1")
    sq = arena.bf16(4096).rearrange("p (k t) -> p k t", k=8); B_sq = Buf("csq")
    u = arena.bf16(4096).rearrange("p (k t) -> p k t", k=8); B_u = Buf("cu")
    hmT = sq; yT = u
    sa = [arena.f32(512), arena.f32(512)]; B_sa = [Buf("csa0"), Buf("csa1")]
    mX = arena.mark()
    g = arena.bf16(NJ * 512).rearrange("p (k t) -> p k t", k=NJ); B_g = Buf("cg")
    wb13m = [arena.bf16(2048), arena.bf16(2048), arena.bf16(2048)]; B_wb13m = [Buf("cw13_0"), Buf("cw13_1"), Buf("cw13_2")]
    wb2m = arena.bf16(FF); B_wb2m = Buf("cw2")
    wb2x = A[:, 11992:11992 + 704].bitcast(BF16), A[:, 11992 + 704:11992 + 1408].bitcast(BF16)
    arena.reset(mX)
    wsl = [arena.bf16(8 * 512).rearrange("p (k n) -> p k n", k=8) for _ in range(4)]; B_wsl = [Buf("wsl%d" % i) for i in range(4)]
    hm = arena.bf16(1024); B_hm = Buf("hm")
    ol = A[:, mX + 4096:mX + 8192].rearrange("p (k t) -> p k t", k=8)
    B_ol = [B_wsl[2], B_wsl[3]]
    ss = rs2[:, 0:8]
    fb = (sq, B_sq, u, B_u, g, B_g, y, B_yreg, tmp, B_tmp, rs2, B_rs2, wb13m, B_wb13m,
          [wb2m[:, 0:FF // 2], wb2m[:, FF // 2:FF], wb2x[0], wb2x[1]], [Buf('cw2a'), Buf('cw2b'), Buf('cw2c'), Buf('cw2d')], sa, B_sa)
    xflat = A[:, mB:mB + 4096]
    rv = xflat[:, 0:3072]; ones1 = xflat[:, 3072:3200]
    S.dma("sp", lambda e: e.dma_start(out=rv[0:1, :], in_=rowv), writes=[B_x])
    S.op("pool", lambda e: e.memset(ones1[0:1, :], 1.0), writes=[B_x])
    for (dst, seg) in ((bo_bc, 1), (hg_bc, 2)):
        for hh in range(2):
            mm(ps[6][:, 0:512], [(ones1[0:1, :], rv[0:1, seg * 1024 + hh * 512:seg * 1024 + hh * 512 + 512])], reads=[B_x], writes=[Bps[6]])
            S.op("act", (lambda e, hh=hh, dst=dst: e.copy(dst[:, hh * 512:(hh + 1) * 512], ps[6][:, 0:512])), reads=[Bps[6]], writes=[B_bc])
    S.barrier()
    w_four, w_mproj, w_out, wo, wgf, wgm = [g_[n] for n in "w_four w_mproj w_out wo wgf wgm".split()]
    w13b, w2b = g_["w13b"], g_["w2b"]
    H1v = H1.rearrange("(k p) t -> p k t", p=128)
    outv = outT.rearrange("(k p) t -> p k t", p=128)
    for T in range(4):
        t0 = 512 * T
        c0 = CTX + t0
        S.dma("sp", (lambda e, t0=t0: e.dma_start(out=x[:, :, :], in_=H1v[:, :, t0:t0 + 512])), reads=[B_H1], writes=[B_x])
        for hh in range(2):
            wload(wsl[hh], B_wsl[hh], kp(wo)[:, :, hh * 512:(hh + 1) * 512], "C", ("wo", hh), 4096)
        for ch in range(4):
            cc = c0 + 128 * ch
            ob = t0 + 128 * ch
            S.dma("sp", (lambda e, ob=ob: e.dma_start(out=hs_t, in_=HS[ob:ob + 128, :])), reads=[B_HS[ob // 128]], writes=[B_hsC])
            for hh in range(2):
                mm(ps[hh][:, 0:512], [(u2[:, k, cc:cc + 128], wsl[hh][:, k, :]) for k in range(8)], reads=[B_wsl[hh]] + B_u2, writes=[Bps[hh]])
                S.op("dve", (lambda e, hh=hh: e.tensor_tensor(o_sb[:, hh * 512:(hh + 1) * 512], ps[hh][:, 0:512], bo_bc[:, hh * 512:(hh + 1) * 512], ALU.add)),
                     reads=[Bps[hh], B_bc], writes=[B_osb])
            S.op("act", lambda e: e.activation(sig, o_sb, AF.Sigmoid), reads=[B_osb], writes=[B_sig])
            S.op("dve", lambda e: e.tensor_tensor(sig, sig, hg_bc, ALU.mult), reads=[B_sig, B_bc], writes=[B_sig])
            S.op("act", lambda e: e.activation(sqh, hs_t, AF.Square), reads=[B_hsC], writes=[B_sqh])
            S.op("dve", lambda e: e.reduce_sum(ss[:, 0:4], sqh.rearrange("p (h e) -> p h e", h=4), AX.X), reads=[B_sqh], writes=[B_rs2])
            S.op("act", lambda e: e.activation(ss[:, 0:4], ss[:, 0:4], AF.Ln, bias=EPS, scale=1.0 / 256), reads=[B_rs2], writes=[B_rs2])
            S.op("act", lambda e: e.activation(ss[:, 0:4], ss[:, 0:4], AF.Exp, scale=-0.5), reads=[B_rs2], writes=[B_rs2])
            for h in range(4):
                S.op("dve", (lambda e, h=h: e.scalar_tensor_tensor(hm[:, h * 256:(h + 1) * 256], hs_t[:, h * 256:(h + 1) * 256], ss[:, h:h + 1],
                                                                  sig[:, h * 256:(h + 1) * 256], ALU.mult, ALU.mult)),
                     reads=[B_hsC, B_sig, B_rs2], writes=[B_hm])
            for half in range(2):
                pb = 2 + half
                def fn(e, half=half, pb=pb):
                    for cq in range(4):
                        c = half * 4 + cq
                        ins = e.matmul(ps[pb][:, cq * 128:(cq + 1) * 128], hm[:, c * 128:(c + 1) * 128], id16[:, :], start=True, stop=True)
                    return ins
                S.op("pe", fn, reads=[B_hm, B_const], writes=[Bps[pb]])
                S.op("act", (lambda e, half=half, pb=pb, ch=ch: e.copy(hmT[:, half * 4:(half + 1) * 4, ch * 128:(ch + 1) * 128],
                                                                      ps[pb][:, 0:512].rearrange("p (c t) -> p c t", c=4))),
                     reads=[Bps[pb]], writes=[B_sq])
        for hh in range(2):
            cs_ = slice(hh * 512, (hh + 1) * 512)
            wload(wsl[0][:, 0:4, :], B_wsl[0], kp(w_four)[:, :, cs_], "C", ("w4", hh), 2048)
            wload(wsl[1], B_wsl[1], kp(w_mproj)[:, :, cs_], "C", ("wm", hh), 4096)
            wload(wsl[2], B_wsl[2], kp(wgf)[:, :, cs_], "C", ("wgf", hh), 4096)
            wload(wsl[3], B_wsl[3], kp(wgm)[:, :, cs_], "C", ("wgm", hh), 4096)
            for ii in range(4):
                i = hh * 4 + ii
                cw = slice(ii * 128, (ii + 1) * 128)
                mm(ps[0][:, 0:512], [(wsl[0][:, gq, cw], yfT[:, gq, t0:t0 + 512]) for gq in range(4)], reads=[B_wsl[0], B_yfT], writes=[Bps[0]])
                mm(ps[1][:, 0:512], [(wsl[1][:, k, cw], hmT[:, k, :]) for k in range(8)], reads=[B_wsl[1], B_sq], writes=[Bps[1]])
                mm(ps[2][:, 0:512], [(wsl[2][:, k, cw], u2[:, k, c0:c0 + 512]) for k in range(8)], reads=[B_wsl[2]] + B_u2, writes=[Bps[2]])
                mm(ps[3][:, 0:512], [(wsl[3][:, k, cw], u2[:, k, c0:c0 + 512]) for k in range(8)], reads=[B_wsl[3]] + B_u2, writes=[Bps[3]])
                S.op("act", (lambda e, i=i: e.activation(sa[0], ps[2][:, 0:512], AF.Sigmoid, bias=vcol("bgf", i, 1))), reads=[Bps[2], B_const], writes=[B_sa[0]])
                S.op("act", (lambda e, i=i: e.activation(sa[1], ps[3][:, 0:512], AF.Sigmoid, bias=vcol("bgm", i, 1))), reads=[Bps[3], B_const], writes=[B_sa[1]])
                S.op("dve", lambda e: e.tensor_tensor(sa[0], sa[0], ps[0][:, 0:512], ALU.mult), reads=[Bps[0], B_sa[0]], writes=[B_sa[0]])
                S.op("dve", lambda e: e.tensor_tensor(sa[1], sa[1], ps[1][:, 0:512], ALU.mult), reads=[Bps[1], B_sa[1]], writes=[B_sa[1]])
                S.op("dve", (lambda e, i=i: e.tensor_tensor(yT[:, i, :], sa[0], sa[1], ALU.add)), reads=B_sa, writes=[B_u])
        for hh in range(2):
            wload(wsl[hh], B_wsl[hh], kp(w_out)[:, :, hh * 512:(hh + 1) * 512], "C", ("wout", hh), 4096)
        for i in range(8):
            pb = 4 + i % 2
            mm(ps[pb][:, 0:512], [(wsl[i // 4][:, k, (i % 4) * 128:(i % 4 + 1) * 128], yT[:, k, :]) for k in range(8)],
               reads=[B_wsl[i // 4], B_u], writes=[Bps[pb]])
            S.op("act", (lambda e, i=i, pb=pb: e.copy(ol[:, i, :], ps[pb][:, 0:512])), reads=[Bps[pb]], writes=B_ol)
            S.op("act", (lambda e, i=i, pb=pb: e.activation(sq[:, i, :], ps[pb][:, 0:512], AF.Square)), reads=[Bps[pb]], writes=[B_sq])
        rstd_from(sq, B_sq, 512, rs1, B_rs1, 6)
        for i in range(8):
            t = tmp[i % 2]; bt = B_tmp[i % 2]
            S.op("dve", (lambda e, i=i, t=t: e.tensor_tensor(t, ol[:, i, :], rs1, ALU.mult)), reads=B_ol + [B_rs1], writes=[bt])
            S.op("dve", (lambda e, i=i, t=t: e.scalar_tensor_tensor(x[:, i, :], t, DER["PMl"][:, i:i + 1], x[:, i, :], ALU.mult, ALU.add)),
                 reads=[bt, g_["B_der"]], writes=[B_x])
        S.barrier()
        S.op("act", lambda e: e.activation(sq[:, :, :], x[:, :, :], AF.Square), reads=[B_x], writes=[B_sq])
        rstd_from(sq, B_sq, 512, rs1, B_rs1, 6)
        ffn(x, B_x, 512, rs1, B_rs1, DER["A4l"], DER["S4l"], DER["PBl"], w13b, w2b, fb, "B")
        S.dma("sp", (lambda e, t0=t0: e.dma_start(out=outv[:, :, t0:t0 + 512], in_=x[:, :, :])), reads=[B_x])
        S.barrier()


def build_rest2(env, L):
    AX = mybir.AxisListType
    g_ = dict(env); g_.update(L)
    nc = g_["nc"]; S = g_["S"]; ps = g_["ps"]; Bps = g_["Bps"]; arena = g_["arena"]
    u2 = g_["u2"]; B_u2 = g_["B_u2"]; vcol = g_["vcol"]; DER = g_["DER"]
    ident = g_["ident"]; maskf = g_["maskf"]; maskb = g_["maskb"]; CS = g_["CS"]; M2 = g_["M2"]; M3 = g_["M3"]
    id16 = g_["id16"]; ones16 = g_["ones16"]; B_const = g_["B_const"]; B_der = g_["B_der"]
    sel = g_["sel"]; negsel = g_["negsel"]; mm = g_["mm"]
    H1, AB, PQ, KT, QT, KTOK, VTOK, GROW, HS = [g_[n] for n in "H1 AB PQ KT QT KTOK VTOK GROW HS".split()]
    B_H1, B_AB, B_PQ, B_KT, B_QT, B_KTOK, B_VTOK, B_GROW = [g_["B_" + n] for n in "H1 AB PQ KT QT KTOK VTOK GROW".split()]
    B_HS = g_["B_HS"]
    Rcol, Gcol, Ecol, gend, acol, wkcol, decay, B_cols = [g_[n] for n in "Rcol Gcol Ecol gend acol wkcol decay B_cols".split()]
    yfT, B_yfT, C32, C16, B_C, mB = [g_[n] for n in "yfT B_yfT C32 C16 B_C mB".split()]
    rowv = g_["rowv"]; outT = g_["outT"]
    tiles = g_["tiles"]
    kp = lambda ap: ap.rearrange("(k p) n -> p k n", p=128)

    wbuf = [arena.bf16(8 * 1024).rearrange("p (k n) -> p k n", k=8) for _ in range(2)]
    B_wbuf = [Buf("wbuf0"), Buf("wbuf1")]
    xf = [arena.f32(512) for _ in range(4)]
    B_xf = [Buf("xf%d" % i) for i in range(4)]
    ab_sb2 = [arena.f32(1024).rearrange("p (x g c) -> p x g c", x=2, g=4) for _ in range(2)]
    B_ab2 = [Buf("ab_sb0"), Buf("ab_sb1")]
    Pt2 = [arena.f32(514), arena.f32(514)]
    B_Pt2 = [Buf("Pt0"), Buf("Pt1")]
    acc2 = [arena.f32(512), arena.f32(512)]
    B_acc2 = [Buf("acc0"), Buf("acc1")]
    kTt = arena.bf16(8 * 512).rearrange("p (k t) -> p k t", k=8)
    B_kTt = Buf("kTt")
    tok2 = [arena.bf16(1024), arena.bf16(1024)]
    B_tok2 = [Buf("tok0"), Buf("tok1")]
    tokctr = [0]
    bv_bc = arena.f32(1024)
    ones1 = arena.f32(128)
    rv = arena.f32(1024)
    B_bc = Buf("bc")
    S.dma("sp", lambda e: e.dma_start(out=rv[0:1, :], in_=rowv[:, 0:1024]), writes=[B_bc])
    S.op("pool", lambda e: e.memset(ones1[0:1, :], 1.0), writes=[B_bc])

    def bcast_row(dst, seg):
        for hh in range(2):
            mm(ps[6][:, 0:512], [(ones1[0:1, :], rv[0:1, seg * 1024 + hh * 512:seg * 1024 + hh * 512 + 512])], reads=[B_bc], writes=[Bps[6]])
            S.op("act", (lambda e, hh=hh: e.copy(dst[:, hh * 512:(hh + 1) * 512], ps[6][:, 0:512])), reads=[Bps[6]], writes=[B_bc])
    bcast_row(bv_bc, 0)

    wF = g_["wF"]
    S.dma("pool", lambda e: e.dma_start(out=wbuf[0][:, :, 0:512], in_=kp(wF)), writes=[B_wbuf[0]])
    for i in range(8):
        c0 = CTX + 512 * i
        for g in range(4):
            pb = g % 2
            mm(ps[pb][:, 0:512], [(wbuf[0][:, k, g * 128:(g + 1) * 128], u2[:, k, c0:c0 + 512]) for k in range(8)],
               reads=[B_wbuf[0]] + B_u2, writes=[Bps[pb]])
            S.op("act", (lambda e, g=g, pb=pb: e.activation(xf[g], ps[pb][:, 0:512], AF.Identity, bias=vcol("bF", g, 1))),
                 reads=[Bps[pb], B_const], writes=[B_xf[g]])
        for tb in range(4):
            ab_sb = ab_sb2[tb % 2]; B_ab = B_ab2[tb % 2]
            for g in range(4):
                bank = 2 + g // 2
                mm(ps[bank][:, (g % 2) * 256:(g % 2) * 256 + 256], [(xf[g][:, tb * 128:(tb + 1) * 128], CS)],
                   reads=[B_xf[g], B_const], writes=[Bps[bank]])
            for bi in range(2):
                S.op("dve", (lambda e, bi=bi, ab_sb=ab_sb: e.tensor_copy(ab_sb[:, :, 2 * bi:2 * bi + 2, :],
                                                            ps[2 + bi][:, 0:512].rearrange("p (g x c) -> p x g c", g=2, x=2))),
                     reads=[Bps[2 + bi]], writes=[B_ab])
            tok0 = 512 * i + 128 * tb
            S.dma("sp", (lambda e, tok0=tok0, ab_sb=ab_sb: e.dma_start(out=AB.rearrange("x t f -> t x f")[tok0:tok0 + 128, :, :],
                                                          in_=ab_sb.rearrange("p x g c -> p x (g c)"))), reads=[B_ab], writes=[B_AB])

    def qk_proj(wdram, bname, cwname, cbname, slot, tlist, is_k):
        S.dma("pool", lambda e: e.dma_start(out=wbuf[slot], in_=kp(wdram)), writes=[B_wbuf[slot]])
        for (ti, c0, W) in tlist:
            islat = ti >= 1
            left = islat and ti > 1
            right = islat and ti < 8
            for c in range(8):
                pb = c % 2
                Pt = Pt2[c % 2]; B_Pt = B_Pt2[c % 2]; acc = acc2[c % 2]; B_acc = B_acc2[c % 2]
                hb = 5 + c % 2
                mm(ps[pb][:, 0:W], [(wbuf[slot][:, k, c * 128:(c + 1) * 128], u2[:, k, c0:c0 + W]) for k in range(8)],
                   reads=[B_wbuf[slot]] + B_u2, writes=[Bps[pb]])
                S.op("act", (lambda e, c=c, pb=pb, W=W, Pt=Pt: e.activation(Pt[:, 1:W + 1], ps[pb][:, 0:W], AF.Identity, bias=vcol(bname, c, 1))),
                     reads=[Bps[pb], B_const], writes=[B_Pt])
                for hi_, (has, col, dstc) in enumerate(((left, c0 - 1, 0), (right, c0 + W, W + 1))):
                    if has:
                        mm(ps[hb][:, hi_:hi_ + 1], [(wbuf[slot][:, k, c * 128:(c + 1) * 128], u2[:, k, col:col + 1]) for k in range(8)],
                           reads=[B_wbuf[slot]] + B_u2, writes=[Bps[hb]])
                        S.op("act", (lambda e, c=c, dstc=dstc, Pt=Pt, hb=hb, hi_=hi_: e.activation(Pt[:, dstc:dstc + 1], ps[hb][:, hi_:hi_ + 1], AF.Identity, bias=vcol(bname, c, 1))),
                             reads=[Bps[hb], B_const], writes=[B_Pt])
                    else:
                        S.op("pool", (lambda e, dstc=dstc, Pt=Pt: e.memset(Pt[:, dstc:dstc + 1], 0.0)), writes=[B_Pt])
                S.op("dve", (lambda e, c=c, W=W, Pt=Pt, acc=acc: e.tensor_scalar(acc[:, 0:W], Pt[:, 0:W], vcol(cwname, c, 1), None, ALU.mult)),
                     reads=[B_Pt, B_const], writes=[B_acc])
                S.op("dve", (lambda e, c=c, W=W, Pt=Pt, acc=acc: e.scalar_tensor_tensor(acc[:, 0:W], Pt[:, 1:W + 1], vcol(cwname, 8 + c, 1), acc[:, 0:W], ALU.mult, ALU.add)),
                     reads=[B_Pt, B_const], writes=[B_acc])
                S.op("dve", (lambda e, c=c, W=W, Pt=Pt, acc=acc: e.scalar_tensor_tensor(acc[:, 0:W], Pt[:, 2:W + 2], vcol(cwname, 16 + c, 1), acc[:, 0:W], ALU.mult, ALU.add)),
                     reads=[B_Pt, B_const], writes=[B_acc])
                S.op("act", (lambda e, c=c, W=W, acc=acc: e.activation(kTt[:, c, 0:W], acc[:, 0:W], AF.Silu, bias=vcol(cbname, c, 1))),
                     reads=[B_acc, B_const], writes=[B_kTt])
            own = 1 <= ti <= 4
            if own:
                o0 = c0 - CTX
                dst = KT if is_k else QT
                S.dma("sp", (lambda e, o0=o0, dst=dst: e.dma_start(out=dst.rearrange("(k p) t -> p k t", p=128)[:, :, o0:o0 + 512], in_=kTt[:, :, :])),
                      reads=[B_kTt], writes=[B_KT if is_k else B_QT])
            if is_k:
                for tb in range(W // 128):
                    tok = tok2[tokctr[0] % 2]; B_tok = B_tok2[tokctr[0] % 2]; tokctr[0] += 1
                    for half in range(2):
                        pb = 3 + half
                        def fn(e, tb=tb, half=half, pb=pb):
                            for cc in range(4):
                                c = half * 4 + cc
                                ins = e.matmul(ps[pb][:, cc * 128:(cc + 1) * 128], kTt[:, c, tb * 128:(tb + 1) * 128], id16[:, :], start=True, stop=True)
                            return ins
                        S.op("pe", fn, reads=[B_kTt, B_const], writes=[Bps[pb]])
                        S.op("dve", (lambda e, half=half, pb=pb, tok=tok: e.tensor_copy(tok[:, half * 512:(half + 1) * 512], ps[pb][:, 0:512])),
                             reads=[Bps[pb]], writes=[B_tok])
                    r0 = c0 + tb * 128
                    S.dma("sp", (lambda e, r0=r0, tok=tok: e.dma_start(out=KTOK[r0:r0 + 128, :], in_=tok)), reads=[B_tok], writes=[B_KTOK])

    tl_all = [(ti, c0, W) for ti, (c0, W, j) in enumerate(tiles)]
    qk_proj(g_["wk"], "bk", "cwk", "cbk", 1, tl_all, True)
    qk_proj(g_["wq"], "bq", "cwq", "cbq", 0, tl_all[1:5], False)
    S.dma("pool", lambda e: e.dma_start(out=wbuf[1], in_=kp(g_["wv"])), writes=[B_wbuf[1]])
    for cb in range(0, NT, 128):
        tok = tok2[tokctr[0] % 2]; B_tok = B_tok2[tokctr[0] % 2]; tokctr[0] += 1
        for half in range(2):
            pb = half + 2 * ((cb // 128) % 2)
            mm(ps[pb][:, 0:512], [(u2[:, k, cb:cb + 128], wbuf[1][:, k, half * 512:(half + 1) * 512]) for k in range(8)],
               reads=[B_wbuf[1]] + B_u2, writes=[Bps[pb]])
            S.op("dve", (lambda e, half=half, pb=pb, tok=tok: e.tensor_tensor(tok[:, half * 512:(half + 1) * 512], ps[pb][:, 0:512],
                                                                     bv_bc[:, half * 512:(half + 1) * 512], ALU.add)),
                 reads=[Bps[pb], B_bc], writes=[B_tok])
        S.dma("sp", (lambda e, cb=cb, tok=tok: e.dma_start(out=VTOK[cb:cb + 128, :], in_=tok)), reads=[B_tok], writes=[B_VTOK])
    S.barrier()
    arena.reset(mB)

    inb2 = [arena.f32(8 * 512).rearrange("p (r f) -> p r f", r=8) for _ in range(2)]
    outb2 = [arena.f32(8 * 512).rearrange("p (r f) -> p r f", r=8) for _ in range(2)]
    B_inb2 = [Buf("inb0"), Buf("inb1")]; B_outb2 = [Buf("outb0"), Buf("outb1")]
    ABv = AB.rearrange("x (r c) f -> x c r f", c=64)
    PQw = PQ
    for rb in range(8):
        inb = inb2[rb % 2]; outb = outb2[rb % 2]; B_inb = B_inb2[rb % 2]; B_outb = B_outb2[rb % 2]
        for x in range(2):
            S.dma("sp", (lambda e, rb=rb, x=x, inb=inb: e.dma_start(out=inb[64 * x:64 * x + 64, :, :], in_=ABv[x, :, rb * 8:rb * 8 + 8, :])),
                  reads=[B_AB], writes=[B_inb])
        for r in range(8):
            pb = r % 4
            mm(ps[pb][:, 0:512], [(M2, inb[:, r, :])], reads=[B_inb, B_const], writes=[Bps[pb]])
            if r % 2 == 0:
                S.op("act", (lambda e, r=r, pb=pb, outb=outb: e.copy(outb[:, r, :], ps[pb][:, 0:512])), reads=[Bps[pb]], writes=[B_outb])
            else:
                S.op("dve", (lambda e, r=r, pb=pb, outb=outb: e.tensor_copy(outb[:, r, :], ps[pb][:, 0:512])), reads=[Bps[pb]], writes=[B_outb])
        for x in range(2):
            S.dma("sp", (lambda e, rb=rb, x=x, outb=outb: e.dma_start(out=PQw[x, :, rb * 8:rb * 8 + 8, :], in_=outb[64 * x:64 * x + 64, :, :])),
                  reads=[B_outb], writes=[B_PQ])
    PQr = PQ.rearrange("x kc r f -> x r kc f")
    for kb in range(8):
        inb = inb2[kb % 2]; B_inb = B_inb2[kb % 2]
        for x in range(2):
            S.dma("sp", (lambda e, kb=kb, x=x, inb=inb: e.dma_start(out=inb[64 * x:64 * x + 64, :, :], in_=PQr[x, :, kb * 8:kb * 8 + 8, :])),
                  reads=[B_PQ], writes=[B_inb])
        for g in range(4):
            pb = 4 + g
            def fn(e, g=g, pb=pb, inb=inb):
                for kc in range(8):
                    ins = e.matmul(ps[pb][:, kc * 32:(kc + 1) * 32], inb[:, kc, g * 128:(g + 1) * 128], M3, start=True, stop=True)
                return ins
            S.op("pe", fn, reads=[B_inb, B_const], writes=[Bps[pb]])
            S.op("act", (lambda e, g=g, pb=pb, kb=kb: e.copy(yfT[:, g, :].rearrange("p (kr kc) -> p kc kr", kc=64)[:, kb * 8:kb * 8 + 8, :],
                                                          ps[pb][:, 0:256].rearrange("p (kc kr) -> p kc kr", kr=32))),
                 reads=[Bps[pb]], writes=[B_yfT])
    S.barrier()
    arena.reset(mB)

    NLS = 3
    NHS = 2
    LD = []
    for i in range(2 * NLS):
        d_ = dict(ktok=arena.bf16(1024), vaug=arena.bf16(4 * 258).rearrange("p (h e) -> p h e", h=4),
                  kT=arena.bf16(1024).rearrange("p (k t) -> p k t", k=8), qT=arena.bf16(1024).rearrange("p (k t) -> p k t", k=8),
                  grow=arena.f32(128), B_ld=Buf("ld%d" % i), B_ldo=Buf("ldo%d" % i))
        S.op("pool", (lambda e, v=d_["vaug"]: e.memset(v[:, :, :], 1.0)), writes=[d_["B_ld"]])
        LD.append(d_)
    HSB = [dict(hs=arena.f32(1024), B_hs=Buf("hs%d" % i)) for i in range(4)]
    HT = []
    B_pCUs = [Buf("pCU0"), Buf("pCU1")]
    for i in range(NHS):
        HT.append(dict(wT=arena.f32(128), STb=arena.bf16(128), P2sb=arena.f32(257), hn=arena.f32(257), dd=arena.f32(2), kw=arena.bf16(256),
                       B_wT=Buf("wT%d" % i), B_ST=Buf("ST%d" % i), B_P2=Buf("P2sb%d" % i), B_hn=Buf("hn%d" % i), B_dd=Buf("dd%d" % i),
                       B_kw=Buf("kw%d" % i),
                       pST=ps[i][:, 0:128], pD=ps[i][:, 128:256], pP2=ps[2 + i][:, 0:257], pP1=ps[4 + i][:, 0:257],
                       pCU=[ps[6][:, 0:257], ps[7][:, 0:257]],
                       B_pSD=Buf("pSD%d" % i), B_pP2=Buf("pP2%d" % i), B_pP1=Buf("pP1%d" % i), B_pCU=B_pCUs))
    B_C32 = [[Buf('C32_%d_%d' % (q, c)) for c in range(2)] for q in range(8)]
    B_C16 = [[Buf('C16_%d_%d' % (q, c)) for c in range(2)] for q in range(8)]
    steps = []
    fw = [(0, 0, None), (1, 128, None)] + [(2 + i, CTX + 128 * i, 128 * i) for i in range(16)]
    bw = [(0, 128, None), (1, 0, None)] + [(2 + i, CTX + 128 * (31 - i), (128 * (31 - i) if 31 - i <= 15 else None)) for i in range(32)]
    for i in range(34):
        if i < 18:
            steps.append((0,) + fw[i])
        steps.append((1,) + bw[i])
    mask16 = [arena.bf16(128), arena.bf16(128)]
    S.op("act", lambda e: e.copy(mask16[0], maskf), reads=[B_const], writes=[B_const])
    S.op("act", lambda e: e.copy(mask16[1], maskb), reads=[B_const], writes=[B_const])

    def emit_loads(si):
        (dr, sc, cb, ob) = steps[si]
        L_ = LD[dr * NLS + sc % NLS]
        ktok_t, vaug, kT_t, qT_t, grow_t = L_["ktok"], L_["vaug"], L_["kT"], L_["qT"], L_["grow"]
        B_ld, B_ldo = L_["B_ld"], L_["B_ldo"]
        S.dma("sp", (lambda e: e.dma_start(out=ktok_t, in_=KTOK[cb:cb + 128, :])), reads=[B_KTOK], writes=[B_ld])
        S.dma("sp", (lambda e: e.dma_start(out=vaug[:, :, 0:256], in_=VTOK[cb:cb + 128, :].rearrange("t (h e) -> t h e", h=4))),
              reads=[B_VTOK], writes=[B_ld])
        if ob is not None:
            S.dma("pool", (lambda e: e.dma_start(out=kT_t, in_=KT.rearrange("(k p) t -> p k t", p=128)[:, :, ob:ob + 128])), reads=[B_KT], writes=[B_ldo])
            S.dma("pool", (lambda e: e.dma_start(out=qT_t, in_=QT.rearrange("(k p) t -> p k t", p=128)[:, :, ob:ob + 128])), reads=[B_QT], writes=[B_ldo])
            S.dma("pool", (lambda e: e.dma_start(out=grow_t[0:36, :], in_=GROW[:, cb:cb + 128])), reads=[B_GROW], writes=[B_ldo])

    def emit_load_hs(si):
        (dr, sc, cb, ob) = steps[si]
        if ob is not None and dr == 1:
            H2 = HSB[dr * 2 + sc % 2]
            S.dma("sp", (lambda e: e.dma_start(out=H2["hs"], in_=HS[ob:ob + 128, :])), reads=[B_HS[ob // 128]], writes=[H2["B_hs"]])

    items = []
    for si, (dr, sc, cb, ob) in enumerate(steps):
        for h in range(4):
            items.append((si, h, len(items)))

    def ctx_of(it):
        si, h, n = it
        (dr, sc, cb, ob) = steps[si]
        L2 = dict(LD[dr * NLS + sc % NLS]); L2.update(HSB[dr * 2 + sc % 2])
        return dr, sc, cb, ob, h, dr * 4 + h, (sc if dr == 0 else 18 + sc), L2, HT[n % NHS]

    def emit_A(it):
        dr, sc, cb, ob, h, q, ci, L_, H_ = ctx_of(it)
        kT_t, qT_t, grow_t, ktok_t = L_["kT"], L_["qT"], L_["grow"], L_["ktok"]
        wT, STb, kw, pST, pD = H_["wT"], H_["STb"], H_["kw"], H_["pST"], H_["pD"]
        S.op("pool", (lambda e: e.tensor_scalar(kw, ktok_t[:, h * 256:(h + 1) * 256], wkcol[:, q, sc:sc + 1], 0.0625, ALU.mult, ALU.mult)),
             reads=[L_["B_ld"], B_cols], writes=[H_["B_kw"]])
        if ob is not None:
            def fsd(e):
                e.matmul(pST, kT_t[:, 2 * h, :], qT_t[:, 2 * h, :], start=True, stop=False)
                e.matmul(pST, kT_t[:, 2 * h + 1, :], qT_t[:, 2 * h + 1, :], start=False, stop=True)
                e.matmul(pD, negsel[0:36, q, :], grow_t[0:36, :], start=True, stop=False)
                return e.matmul(pD, ident, maskf if dr == 0 else maskb, start=False, stop=True)
            S.op("pe", fsd, reads=[L_["B_ldo"], B_const], writes=[H_["B_pSD"]])
            S.op("act", (lambda e: e.activation(wT, pD, AF.Exp, bias=Rcol[:, ci, h:h + 1])),
                 reads=[H_["B_pSD"], B_cols], writes=[H_["B_wT"]])
            S.op("dve", (lambda e: e.scalar_tensor_tensor(STb, pST, 0.0625, wT, ALU.mult, ALU.mult)),
                 reads=[H_["B_pSD"], H_["B_wT"]], writes=[H_["B_ST"]])

    def emit_BC(it):
        dr, sc, cb, ob, h, q, ci, L_, H_ = ctx_of(it)
        vaug, qT_t, hs_t = L_["vaug"], L_["qT"], L_["hs"]
        B_ld, B_ldo, B_hs = L_["B_ld"], L_["B_ldo"], L_["B_hs"]
        STb, P2sb, hn, dd, kw = H_["STb"], H_["P2sb"], H_["hn"], H_["dd"], H_["kw"]
        pP2, pP1, pCU = H_["pP2"], H_["pP1"], H_["pCU"]
        for c in range(2):
            mm(pCU[c], [(kw[:, c * 128:(c + 1) * 128], vaug[:, h, 0:257])], reads=[H_["B_kw"], B_ld], writes=[H_["B_pCU"][c]])
        if ob is not None:
            mm(pP1, [(qT_t[:, 2 * h + c, :], C16[:, q, c, 0:257]) for c in range(2)], reads=[B_ldo] + B_C16[q], writes=[H_["B_pP1"]])
        for c in range(2):
            S.op("dve", (lambda e, c=c: e.scalar_tensor_tensor(C32[:, q, c, :], C32[:, q, c, :], decay[:, q, sc:sc + 1], pCU[c], ALU.mult, ALU.add)),
                 reads=[H_["B_pCU"][c], B_cols], writes=[B_C32[q][c]])
            S.op("act", (lambda e, c=c: e.copy(C16[:, q, c, 0:257], C32[:, q, c, :])), reads=[B_C32[q][c]], writes=[B_C16[q][c]])
        if ob is not None:
            mm(pP2, [(STb, vaug[:, h, 0:257])], reads=[H_["B_ST"], B_ld], writes=[H_["B_pP2"]])
            S.op("act", (lambda e: e.copy(P2sb, pP2)), reads=[H_["B_pP2"]], writes=[H_["B_P2"]])
            S.op("dve", (lambda e: e.scalar_tensor_tensor(hn, pP1, acol[:, q, sc:sc + 1], P2sb, ALU.mult, ALU.add)),
                 reads=[H_["B_pP1"], H_["B_P2"], B_cols], writes=[H_["B_hn"]])
            S.op("dve", (lambda e: e.scalar_tensor_tensor(dd[:, 0:1], hn[:, 256:257], -1.0, hn[:, 256:257], ALU.mult, ALU.max)),
                 reads=[H_["B_hn"]], writes=[H_["B_dd"]])
            S.op("dve", (lambda e: e.tensor_tensor(dd[:, 0:1], dd[:, 0:1], Ecol[:, ci, h:h + 1], ALU.max)),
                 reads=[H_["B_dd"], B_cols], writes=[H_["B_dd"]])
            S.op("dve", (lambda e: e.reciprocal(dd[:, 1:2], dd[:, 0:1])), reads=[H_["B_dd"]], writes=[H_["B_dd"]])
            if dr == 0:
                S.op("act", (lambda e: e.activation(hs_t[:, h * 256:(h + 1) * 256], hn[:, 0:256], AF.Identity, scale=dd[:, 1:2])),
                     reads=[H_["B_hn"], H_["B_dd"]], writes=[B_hs])
            else:
                S.op("dve", (lambda e: e.scalar_tensor_tensor(hs_t[:, h * 256:(h + 1) * 256], hn[:, 0:256], dd[:, 1:2],
                                                              hs_t[:, h * 256:(h + 1) * 256], ALU.mult, ALU.add)),
                     reads=[H_["B_hn"], H_["B_dd"]], writes=[B_hs])
            if h == 3:
                S.dma("sp", (lambda e: e.dma_start(out=HS[ob:ob + 128, :], in_=hs_t)), reads=[B_hs], writes=[B_HS[ob // 128]])

    emit_loads(0); emit_loads(1)
    for n in range(len(items) + 1):
        if n < len(items):
            si, h, _ = items[n]
            if h == 0:
                emit_load_hs(si)
            if h == 1 and si + 2 < len(steps):
                emit_loads(si + 2)
            emit_A(items[n])
        if n >= 1:
            emit_BC(items[n - 1])
    S.barrier()
    if g_["stage"] < 4:
        return
    build_rest3(g_, locals())


def build_rest(env):
    nc = env["nc"]
    S = env["S"]; ps = env["ps"]; Bps = env["Bps"]; arena = env["arena"]
    u2 = env["u2"]; B_u2 = env["B_u2"]; vcol = env["vcol"]; DER = env["DER"]
    ident = env["ident"]; maskf = env["maskf"]; maskb = env["maskb"]; CS = env["CS"]; M2 = env["M2"]; M3 = env["M3"]
    id16 = env["id16_t"]; ones16 = env["ones16_t"]
    B_const = env["B_const"]; B_der = env["B_der"]
    stage = env["stage"]; dbg = env["dbg"]; dbg_tensor = env["dbg_tensor"]
    dscr = env["dscr"]
    H1, AB, PQ, KT, QT, KTOK, VTOK = [env[n] for n in "H1 AB PQ KT QT KTOK VTOK".split()]
    B_H1, B_AB, B_PQ, B_KT, B_QT, B_KTOK, B_VTOK = [env["B_" + n] for n in "H1 AB PQ KT QT KTOK VTOK".split()]
    GROW = dscr("GROW", [36, NT], F32)
    B_GROW = Buf("GROW")
    HS = dscr("HS", [OWN, D], F32)
    B_HS = [Buf("HS%d" % i) for i in range(16)]

    def mm(out, pairs, reads, writes):
        def fn(e):
            n = len(pairs)
            for i, (l, r) in enumerate(pairs):
                ins = e.matmul(out, l, r, start=(i == 0), stop=(i == n - 1))
            return ins
        return S.op("pe", fn, reads=reads, writes=writes)

    NCH = 52
    Rcol = arena.f32(NCH * 4).rearrange("p (c h) -> p c h", h=4)
    Gcol = arena.f32(NCH * 4).rearrange("p (c h) -> p c h", h=4)
    Ecol = arena.f32(NCH * 4).rearrange("p (c h) -> p c h", h=4)
    gend = arena.f32(8 * 35).rearrange("p (q c) -> p q c", q=8)
    acol = arena.f32(8 * 34).rearrange("p (q c) -> p q c", q=8)
    wkcol = arena.f32(8 * 34).rearrange("p (q c) -> p q c", q=8)
    decay = arena.f32(8 * 34).rearrange("p (q c) -> p q c", q=8)
    B_cols = Buf("cols")
    yfT = arena.bf16(4 * OWN).rearrange("p (g t) -> p g t", g=4)
    B_yfT = Buf("yfT")
    C32 = arena.f32(8 * 2 * 257).rearrange("p (q c e) -> p q c e", q=8, c=2)
    C16 = arena.bf16(8 * 2 * 258).rearrange("p (q c e) -> p q c e", q=8, c=2)
    B_C = [Buf("C%d" % i) for i in range(8)]
    sel_t = arena.f32(2048)
    S.dma("sp", lambda e: e.dma_start(out=sel_t[0:36, :], in_=env["selc"]), writes=[B_const])
    sel = sel_t[0:36, 0:1024].rearrange("r (p m) -> r p m", p=8)
    negsel = sel_t[0:36, 1024:2048].rearrange("r (p m) -> r p m", p=8)
    mB = arena.mark()

    aLI = arena.f32(NT)
    aLF = arena.f32(NT)
    aB = arena.f32(NT)
    aG = arena.f32(NT)
    ones_r = arena.f32(512)
    tmpE = arena.f32(512)
    wgf_s = arena.bf16(8 * 8).rearrange("p (k n) -> p k n", k=8)
    wgb_s = arena.bf16(8 * 72).rearrange("p (k n) -> p k n", k=8)
    bgs = arena.f32(4)
    B_rows = Buf("rows")
    B_wg = Buf("wg")
    B_tmpE = Buf("tmpE")
    for a in (aLI, aLF, aB, aG):
        S.op("pool", (lambda e, a=a: e.memset(a, 0.0)), writes=[B_rows])
    S.op("pool", lambda e: e.memset(ones_r, 1.0), writes=[B_wg])
    S.op("pool", lambda e: e.memset(gend[:, :, :], 0.0), writes=[B_cols])
    for q in range(8):
        S.op("pool", (lambda e, q=q: e.memset(C32[:, q, :, :], 0.0)), writes=[B_C[q]])
        S.op("pool", (lambda e, q=q: e.memset(C16[:, q, :, :], 0.0)), writes=[B_C[q]])
    with nc.allow_non_contiguous_dma(reason="tiny gate weights"):
        pass
    wgate_f = env["wgate_f"]; wgate_b = env["wgate_b"]; bg = env["bg"]
    S.dma("pool", lambda e: e.dma_start(out=wgf_s, in_=wgate_f.rearrange("(k p) n -> p k n", p=128)), writes=[B_wg])
    S.dma("pool", lambda e: e.dma_start(out=wgb_s, in_=wgate_b.rearrange("(k p) n -> p k n", p=128)), writes=[B_wg])
    S.dma("sp", lambda e: e.dma_start(out=bgs[0:36, 0:2], in_=bg), writes=[B_wg])
    S.op("dve", lambda e: e.tensor_scalar(bgs[0:36, 2:3], bgs[0:36, 1:2], -1.0, None, ALU.mult), reads=[B_wg], writes=[B_wg])

    def gate_tile(jc0, W, rhs_fn, r0, r1, wl, wl_lf, pbank, rev=False):
        def pv(bank):
            return ps[bank][r0:r1, W - 1::-1] if rev else ps[bank][r0:r1, 0:W]
        mm(ps[pbank][0:r1, 0:W], [(wl(k), rhs_fn(k)) for k in range(8)], reads=[B_wg] + B_u2, writes=[Bps[pbank]])
        S.op("act", lambda e: e.activation(aLI[r0:r1, jc0:jc0 + W], pv(pbank), AF.Identity, bias=bgs[r0:r1, 0:1]),
             reads=[Bps[pbank], B_wg], writes=[B_rows])
        mm(ps[pbank + 1][0:r1, 0:W], [(wl_lf(k), rhs_fn(k)) for k in range(8)], reads=[B_wg] + B_u2, writes=[Bps[pbank + 1]])
        S.op("act", lambda e: e.activation(tmpE[r0:r1, 0:W], pv(pbank + 1), AF.Exp, bias=bgs[r0:r1, 2:3], scale=-1.0),
             reads=[Bps[pbank + 1], B_wg], writes=[B_tmpE])
        S.op("act", lambda e: e.activation(tmpE[r0:r1, 0:W], tmpE[r0:r1, 0:W], AF.Ln, bias=1.0), reads=[B_tmpE], writes=[B_tmpE])
        S.op("dve", lambda e: e.tensor_scalar(aLF[r0:r1, jc0:jc0 + W], tmpE[r0:r1, 0:W], -1.0, None, ALU.mult),
             reads=[B_tmpE], writes=[B_rows])

    ti = 0
    for (jc0, W) in [(0, 256)] + [(256 + 512 * i, 512) for i in range(4)]:
        gate_tile(jc0, W, (lambda k, jc0=jc0, W=W: u2[:, k, jc0:jc0 + W]), 0, 4,
                  (lambda k: wgf_s[:, k, 0:4]), (lambda k: wgf_s[:, k, 4:8]), 2 * (ti % 2))
        ti += 1
    def rev_u2(k, hi, W):
        return u2[:, k, hi - W + 1:hi + 1]
    gate_tile(0, 256, (lambda k: rev_u2(k, 255, 256)), 32, 36,
              (lambda k: wgb_s[:, k, 0:36]), (lambda k: wgb_s[:, k, 36:72]), 2 * (ti % 2), rev=True)
    ti += 1
    for i in range(8):
        jc0 = 256 + 512 * i
        hi = 4607 - jc0
        gate_tile(jc0, 512, (lambda k, hi=hi: rev_u2(k, hi, 512)), 32, 36,
                  (lambda k: wgb_s[:, k, 0:36]), (lambda k: wgb_s[:, k, 36:72]), 2 * (ti % 2), rev=True)
        ti += 1
    pieces = [(0, 256)] + [(256 + 512 * i, 512) for i in range(8)]
    for pi, (c0, W) in enumerate(pieces):
        init = 0.0 if pi == 0 else aB[0:36, c0 - 1:c0]
        S.op("dve", (lambda e, c0=c0, W=W, init=init: e.tensor_tensor_scan(aB[0:36, c0:c0 + W], ones_r[0:36, 0:W], aLF[0:36, c0:c0 + W],
                                                                           init, ALU.mult, ALU.add)),
             reads=[B_rows, B_wg], writes=[B_rows])
    S.op("dve", lambda e: e.tensor_tensor(aLI[0:36, :], aLI[0:36, :], aB[0:36, :], ALU.subtract), reads=[B_rows], writes=[B_rows])
    for pi, (c0, W) in enumerate(pieces):
        init = 0.0 if pi == 0 else aG[0:36, c0 - 1:c0]
        S.op("dve", (lambda e, c0=c0, W=W, init=init: e.tensor_tensor_scan(aG[0:36, c0:c0 + W], ones_r[0:36, 0:W], aLI[0:36, c0:c0 + W],
                                                                           init, ALU.mult, ALU.max)),
             reads=[B_rows, B_wg], writes=[B_rows])
    S.op("dve", lambda e: e.tensor_tensor(aB[0:36, :], aB[0:36, :], aG[0:36, :], ALU.add), reads=[B_rows], writes=[B_rows])
    if dbg:
        d_rows = dbg_tensor("d_rows", [3, 36, NT])
        for i, a in enumerate((aLI, aG, aB)):
            S.dma("sp", (lambda e, i=i, a=a: e.dma_start(out=d_rows[i], in_=a[0:36, :])), reads=[B_rows])
    def n0_of(sc):
        return 128 if sc == 0 else (0 if sc == 1 else 4480 - 128 * sc)
    for ai, (arr, bank) in enumerate(((aLI, 0), (aB, 2), (aG, 1))):
        S.op("dve", (lambda e, arr=arr: e.tensor_copy(aLF[32:36, 0:256], arr[32:36, 255::-1])), reads=[B_rows], writes=[B_rows])
        S.op("dve", (lambda e, arr=arr: e.tensor_copy(aLF[32:36, 256:NT], arr[32:36, NT - 1:255:-1])), reads=[B_rows], writes=[B_rows])
        def fn(e, arr=arr, bank=bank):
            for sc in range(18):
                ins = e.matmul(ps[bank][:, sc * 4:(sc + 1) * 4], arr[0:4, sc * 128:(sc + 1) * 128], ident[0:4, 0:4],
                               start=True, stop=True)
            for sc in range(34):
                n0 = n0_of(sc)
                ins = e.matmul(ps[bank][:, (18 + sc) * 4:(19 + sc) * 4], aLF[32:36, n0:n0 + 128],
                               ident[32:36, 32:36], start=True, stop=True)
            return ins
        S.op("pe", fn, reads=[B_rows, B_const], writes=[Bps[bank]])
    S.op("dve", lambda e: e.tensor_copy(aLF[0:4, :], aG[0:4, :]), reads=[B_rows], writes=[B_rows])
    S.dma("sp", lambda e: e.dma_start(out=GROW, in_=aLF[0:36, :]), reads=[B_rows], writes=[B_GROW])
    S.op("act", lambda e: e.copy(Rcol[:, :, :], ps[0][:, 0:NCH * 4].rearrange("p (c h) -> p c h", h=4)), reads=[Bps[0]], writes=[B_cols])
    S.op("act", lambda e: e.copy(Gcol[:, :, :], ps[1][:, 0:NCH * 4].rearrange("p (c h) -> p c h", h=4)), reads=[Bps[1]], writes=[B_cols])
    S.op("act", lambda e: e.activation(Ecol[:, :, :], ps[2][:, 0:NCH * 4].rearrange("p (c h) -> p c h", h=4), AF.Exp, scale=-1.0),
         reads=[Bps[2]], writes=[B_cols])
    def fn(e):
        for q in range(8):
            n = 18 if q < 4 else 34
            ins = e.matmul(ps[3][:, q * 34:q * 34 + n], sel[0:36, q, :], aG[0:36, 127:127 + 128 * (n - 1) + 1:128], start=True, stop=True)
        return ins
    S.op("pe", fn, reads=[B_rows, B_const], writes=[Bps[3]])
    for q in range(8):
        n = 18 if q < 4 else 34
        S.op("act", (lambda e, q=q, n=n: e.copy(gend[:, q, 1:1 + n], ps[3][:, q * 34:q * 34 + n])), reads=[Bps[3]], writes=[B_cols])
    tq = arena.f32(34)
    B_tq = Buf("tq")
    for q in range(8):
        n = 18 if q < 4 else 34
        base = 0 if q < 4 else 18
        h = q % 4
        S.op("dve", (lambda e, q=q, n=n, base=base, h=h: e.tensor_tensor(tq[:, 0:n], gend[:, q, 0:n], Gcol[:, base:base + n, h], ALU.subtract)),
             reads=[B_cols], writes=[B_tq])
        S.op("act", (lambda e, q=q, n=n: e.activation(acol[:, q, 0:n], tq[:, 0:n], AF.Exp)), reads=[B_tq], writes=[B_cols])
        S.op("dve", (lambda e, q=q, n=n, base=base, h=h: e.tensor_tensor(tq[:, 0:n], Rcol[:, base:base + n, h], gend[:, q, 1:1 + n], ALU.subtract)),
             reads=[B_cols], writes=[B_tq])
        S.op("act", (lambda e, q=q, n=n: e.activation(wkcol[:, q, 0:n], tq[:, 0:n], AF.Exp)), reads=[B_tq], writes=[B_cols])
        S.op("dve", (lambda e, q=q, n=n: e.tensor_tensor(tq[:, 0:n], gend[:, q, 0:n], gend[:, q, 1:1 + n], ALU.subtract)),
             reads=[B_cols], writes=[B_tq])
        S.op("act", (lambda e, q=q, n=n: e.activation(decay[:, q, 0:n], tq[:, 0:n], AF.Exp)), reads=[B_tq], writes=[B_cols])
    S.barrier()
    arena.reset(mB)
    if stage < 3:
        return
    build_rest2(env, locals())


COL_F = 0
COL_Q = 512
COL_K = 1536
COL_V = 2560
COL_O = 3584
COL_GATES = 4608
COL_BR = 4624


def _dft_consts(flip):
    idx = (63 - np.arange(64)) if flip else np.arange(64)
    ang = 2 * np.pi * np.outer(idx, idx) / 64.0
    Cc = np.cos(ang) / 8.0
    Sc = np.sin(ang) / 8.0
    ch = np.arange(128)
    angc = 2 * np.pi * np.outer(ch, ch) / 128.0
    CS = np.concatenate([np.cos(angc), np.sin(angc)], axis=1) / np.sqrt(128.0)
    M2 = np.zeros((128, 128))
    M2[0:64, 0:64] = Cc
    M2[64:128, 0:64] = -Sc
    M2[0:64, 64:128] = Sc
    M2[64:128, 64:128] = Cc
    M3 = np.zeros((128, 32))
    M3[0:64, :] = Cc[:, 0:32]
    M3[64:128, :] = -Sc[:, 0:32]
    return CS, M2, M3


def make_inputs(inp):
    f32 = np.float32
    x = np.asarray(inp["x"], f32)
    ctx = np.asarray(inp["ctx"], f32)
    c = np.asarray(inp["c"], f32)
    c_ctx = np.asarray(inp["c_ctx"], f32)
    w_in = np.asarray(inp["w_in"], f32)[0]
    b_in = np.asarray(inp["b_in"], f32)[0]
    conv_w = np.asarray(inp["conv_w"], f32)[0]
    conv_b = np.asarray(inp["conv_b"], f32)[0]
    norm_g = np.asarray(inp["norm_g"], f32)[0]

    def fm(v):
        return np.ascontiguousarray(v.reshape(-1, 128).T)

    def r13(w):
        return np.ascontiguousarray(w.reshape(8, 128, 2, NJ, 128).transpose(3, 1, 0, 2, 4).reshape(NJ, 128, 2048))

    def r2(w):
        return np.ascontiguousarray(w.reshape(NJ, 128, 8, 128).transpose(2, 1, 0, 3).reshape(8, 128, FF))

    shared = {
        "w_ada": np.ascontiguousarray(np.asarray(inp["w_ada"], f32)[0]),
        "w13a": r13(np.asarray(inp["w13_a"], f32)[0]), "w2a": r2(np.asarray(inp["w2_a"], f32)[0]),
        "w13b": r13(np.asarray(inp["w13_b"], f32)[0]), "w2b": r2(np.asarray(inp["w2_b"], f32)[0]),
        "wF": np.ascontiguousarray(w_in[:, COL_F:COL_Q]), "wq": np.ascontiguousarray(w_in[:, COL_Q:COL_K]),
        "wk": np.ascontiguousarray(w_in[:, COL_K:COL_V]), "wv": np.ascontiguousarray(w_in[:, COL_V:COL_O]),
        "wo": np.ascontiguousarray(w_in[:, COL_O:COL_GATES]),
        "wgf": np.ascontiguousarray(w_in[:, COL_BR:COL_BR + D]), "wgm": np.ascontiguousarray(w_in[:, COL_BR + D:]),
        "w_four": np.ascontiguousarray(np.asarray(inp["w_four"], f32)[0]),
        "w_mproj": np.ascontiguousarray(np.asarray(inp["w_mproj"], f32)[0]),
        "w_out": np.ascontiguousarray(np.asarray(inp["w_out"], f32)[0]),
        "rowv": np.concatenate([b_in[COL_V:COL_O], b_in[COL_O:COL_GATES], np.asarray(inp["head_g"], f32)[0]])[None, :].copy(),
    }
    sel = np.zeros((36, 8, 128), f32)
    for p in range(8):
        row = (p % 4) + (32 if p >= 4 else 0)
        sel[row, p, :] = 1.0
    selc = np.concatenate([sel.reshape(36, -1), -sel.reshape(36, -1)], axis=1)
    s_idx = np.arange(128)[:, None]
    t_idx = np.arange(128)[None, :]
    maskf = np.where(s_idx <= t_idx, 0.0, NEG).astype(f32)
    maskb = np.where(s_idx >= t_idx, 0.0, NEG).astype(f32)
    maps = []
    for core in range(8):
        b, half = core // 2, core % 2
        flip = half == 1
        xb = x[b][::-1] if flip else x[b]
        cb_ = ctx[b][::-1] if flip else ctx[b]
        g = COL_GATES
        if flip:
            gi_f, gf_f, gi_b, gf_b = g + 8, g + 12, g + 0, g + 4
            cw = conv_w[::-1]
        else:
            gi_f, gf_f, gi_b, gf_b = g + 0, g + 4, g + 8, g + 12
            cw = conv_w
        wgate_f = np.concatenate([w_in[:, gi_f:gi_f + 4], w_in[:, gf_f:gf_f + 4]], axis=1)
        wgate_b = np.zeros((D, 72), f32)
        wgate_b[:, 32:36] = w_in[:, gi_b:gi_b + 4]
        wgate_b[:, 36 + 32:36 + 36] = w_in[:, gf_b:gf_b + 4]
        bgv = np.zeros((36, 2), f32)
        bgv[0:4, 0] = b_in[gi_f:gi_f + 4]
        bgv[0:4, 1] = b_in[gf_f:gf_f + 4]
        bgv[32:36, 0] = b_in[gi_b:gi_b + 4]
        bgv[32:36, 1] = b_in[gf_b:gf_b + 4]
        vecs = np.zeros((128, NV), f32)

        def put(name, arr):
            o, w = VEC[name]
            assert arr.shape == (128, w), (name, arr.shape)
            vecs[:, o:o + w] = arr
        put("bada", fm(np.asarray(inp["b_ada"], f32)[0]))
        put("ng", np.concatenate([fm(norm_g[i]) for i in range(6)], axis=1))
        put("bF", fm(b_in[COL_F:COL_Q])); put("bq", fm(b_in[COL_Q:COL_K])); put("bk", fm(b_in[COL_K:COL_V]))
        put("cwq", np.concatenate([fm(cw[t, 0:D]) for t in range(3)], axis=1))
        put("cwk", np.concatenate([fm(cw[t, D:2 * D]) for t in range(3)], axis=1))
        put("cbq", fm(conv_b[0:D])); put("cbk", fm(conv_b[D:2 * D]))
        put("bgf", fm(b_in[COL_BR:COL_BR + D])); put("bgm", fm(b_in[COL_BR + D:]))
        cvec = np.zeros((128, 8, 2), f32)
        cvec[:, :, 0] = fm(c[b])
        cvec[:, :, 1] = fm(c_ctx)
        CS, M2, M3 = _dft_consts(flip)
        cst = np.zeros((128, 128 * 4 + 256 + 128 + 32), f32)
        cst[:, 0:128] = np.eye(128)
        cst[:, 128:256] = maskf
        cst[:, 256:384] = maskb
        cst[:, 512:768] = CS
        cst[:, 768:896] = M2
        cst[:, 896:928] = M3
        m = dict(shared)
        m.update({
            "xT": np.ascontiguousarray(xb.T), "ctxT": np.ascontiguousarray(cb_.T),
            "cvec": cvec.reshape(128, 16), "vecs": vecs, "bg": bgv,
            "wgate_f": np.ascontiguousarray(wgate_f), "wgate_b": wgate_b, "cst": cst, "selc": selc,
        })
        maps.append(m)
    return maps


def kernel(**inputs):
    nc, _ = build()
    maps = make_inputs(inputs)
    res = run_bass_kernel_spmd(nc, maps, core_ids=list(range(8)))
    out = np.zeros((4, SEQ, D), np.float32)
    for core in range(8):
        b, half = core // 2, core % 2
        o = np.asarray(res.results[core]["outT"]).T
        if half == 0:
            out[b, 0:OWN] = o
        else:
            out[b, OWN:] = o[::-1]
    return out
```

```python
import numpy as np
import os as _os
from contextlib import ExitStack
import concourse.bass as bass
import concourse.mybir as mybir
from concourse.bass_utils import run_bass_kernel_spmd

F32 = mybir.dt.float32
BF16 = mybir.dt.bfloat16
AF = mybir.ActivationFunctionType
ALU = mybir.AluOpType

ENGS = ("pe", "act", "dve", "pool", "sp")
EPOCH = 30000

D = 1024
SEQ = 4096
CTX = 256
NT = CTX + SEQ
OWN = 2048
FF = 2816
NJ = 22
EPS = 1e-6
NEG = -30000.0


class Tick:
    __slots__ = ("sem", "val", "know")

    def __init__(self, sem, val, know):
        self.sem = sem
        self.val = val
        self.know = know


class Buf:
    __slots__ = ("name", "w", "r")

    def __init__(self, name=""):
        self.name = name
        self.w = None
        self.r = {}


class Sched:
    def __init__(self, nc, stack, n_dma_sems=48):
        self.nc = nc
        self.stack = stack
        self.q = {e: [] for e in ENGS}
        self.cnt = {e: 0 for e in ENGS}
        self.esems = {e: [] for e in ENGS}
        self.known = {e: {} for e in ENGS}
        self.dsems = [stack.enter_context(nc.semaphore(f"dma{i}")) for i in range(n_dma_sems)]
        self.dcnt = [0] * n_dma_sems
        self.dlast = [None] * n_dma_sems
        self.drr = 0
        self.drr2 = {}

    def _esem(self, eng, idx):
        lst = self.esems[eng]
        while len(lst) <= idx:
            lst.append(self.stack.enter_context(self.nc.semaphore(f"e_{eng}_{len(lst)}")))
        return lst[idx]

    def _collect(self, eng, reads, writes, extra=()):
        kn = self.known[eng]
        waits = {}

        def need(t):
            if t is None:
                return
            if kn.get(t.sem, 0) >= t.val:
                return
            if waits.get(t.sem, (0, None))[0] < t.val:
                waits[t.sem] = (t.val, t)

        for b in reads:
            need(b.w)
        for b in writes:
            need(b.w)
            for t in b.r.values():
                need(t)
        for t in extra:
            need(t)
        items = sorted(waits.items(), key=lambda kv: -kv[1][0])
        final = []
        for sem, (val, t) in items:
            if kn.get(sem, 0) >= val:
                continue
            final.append((sem, val))
            kn[sem] = val
            for s2, v2 in t.know.items():
                if kn.get(s2, 0) < v2:
                    kn[s2] = v2
        return final

    def op(self, eng, fn, reads=(), writes=(), extra=()):
        waits = self._collect(eng, reads, writes, extra)
        c = self.cnt[eng]
        sem = self._esem(eng, c // EPOCH)
        val = c % EPOCH + 1
        self.cnt[eng] = c + 1
        t = Tick(sem, val, dict(self.known[eng]))
        for b in reads:
            b.r[sem] = t
        for b in writes:
            b.w = t
            b.r = {}
        self.q[eng].append((waits, fn, sem, 1))
        return t

    def dma(self, eng, fn, reads=(), writes=(), extra=()):
        n = len(self.dsems)
        lo, hi = (0, n // 3) if eng == "pool" else (n // 3, n)
        rr = self.drr2.get(eng, lo)
        i = rr
        self.drr2[eng] = lo + (rr + 1 - lo) % (hi - lo)
        ex = list(extra)
        if self.dlast[i] is not None:
            ex.append(self.dlast[i])
        waits = self._collect(eng, reads, writes, ex)
        self.dcnt[i] += 1
        sem = self.dsems[i]
        t = Tick(sem, 16 * self.dcnt[i], dict(self.known[eng]))
        self.dlast[i] = t
        for b in reads:
            b.r[sem] = t
        for b in writes:
            b.w = t
            b.r = {}
        self.q[eng].append((waits, fn, sem, 16))
        return t

    def wait_all(self, eng, ticks):
        waits = self._collect(eng, (), (), ticks)
        self.q[eng].append((waits, None, None, 0))

    def barrier(self):
        ticks = []
        for e in ENGS:
            c = self.cnt[e]
            if c > 0:
                ticks.append(Tick(self._esem(e, (c - 1) // EPOCH), (c - 1) % EPOCH + 1, {}))
        for t in self.dlast:
            if t is not None:
                ticks.append(t)
        for e in ENGS:
            self.wait_all(e, ticks)

    def emit(self):
        nc = self.nc
        q = self.q

        def run(engobj, lst):
            for waits, fn, sem, amt in lst:
                for s, v in waits:
                    engobj.wait_ge(s, v)
                if fn is not None:
                    ins = fn(engobj)
                    ins.then_inc(sem, amt)

        with nc.Block() as block:
            @block.tensor
            def _(e):
                run(e, q["pe"])

            @block.scalar
            def _(e):
                run(e, q["act"])

            @block.vector
            def _(e):
                run(e, q["dve"])

            @block.gpsimd
            def _(e):
                run(e, q["pool"])

            @block.sync
            def _(e):
                run(e, q["sp"])


class Arena:
    def __init__(self, nc, name, words):
        self.t = nc.alloc_sbuf_tensor(name, [128, words], F32)
        self.words = words
        self.off = 0

    def mark(self):
        return self.off

    def reset(self, m):
        self.off = m

    def f32(self, n):
        a = self.t[:, self.off:self.off + n]
        self.off += n
        assert self.off <= self.words, ("arena overflow", self.off, self.words)
        return a

    def bf16(self, n):
        w = (n + 1) // 2
        a = self.t[:, self.off:self.off + w].bitcast(BF16)
        self.off += w
        assert self.off <= self.words, ("arena overflow", self.off, self.words)
        return a[:, 0:n]


VEC = {}
_o = 0
for _n, _w in [("bada", 72), ("ng", 48), ("bF", 4), ("bq", 8), ("bk", 8), ("cwq", 24), ("cwk", 24),
               ("cbq", 8), ("cbk", 8), ("bgf", 8), ("bgm", 8)]:
    VEC[_n] = (_o, _w)
    _o += _w
NV = _o


def build(stage=99, dbg=False):
    nc = bass.Bass("TRN2", target_bir_lowering=False)
    dt_in = lambda name, shape, dt=F32: nc.dram_tensor(name, shape, dt, kind="ExternalInput").ap()
    xT = dt_in("xT", [D, SEQ])
    ctxT = dt_in("ctxT", [D, CTX])
    cvec = dt_in("cvec", [128, 16])
    w_ada = dt_in("w_ada", [D, 9 * D])
    vecs = dt_in("vecs", [128, NV])
    rowv = dt_in("rowv", [1, 3 * D])
    bg = dt_in("bg", [36, 2])
    w13a = dt_in("w13a", [NJ, 128, 2048])
    w2a = dt_in("w2a", [8, 128, FF])
    w13b = dt_in("w13b", [NJ, 128, 2048])
    w2b = dt_in("w2b", [8, 128, FF])
    wF = dt_in("wF", [D, 512])
    wq = dt_in("wq", [D, D])
    wk = dt_in("wk", [D, D])
    wv = dt_in("wv", [D, D])
    wo = dt_in("wo", [D, D])
    wgf = dt_in("wgf", [D, D])
    wgm = dt_in("wgm", [D, D])
    wgate_f = dt_in("wgate_f", [D, 8])
    wgate_b = dt_in("wgate_b", [D, 72])
    w_four = dt_in("w_four", [512, D])
    w_mproj = dt_in("w_mproj", [D, D])
    w_out = dt_in("w_out", [D, D])
    cst = dt_in("cst", [128, 128 * 4 + 256 + 128 + 32])
    selc = dt_in("selc", [36, 2 * 8 * 128])
    outT = nc.dram_tensor("outT", [D, OWN], F32, kind="ExternalOutput").ap()
    dbg_out = {}

    def dbg_tensor(name, shape, dt=F32):
        dbg_out[name] = nc.dram_tensor(name, shape, dt, kind="ExternalOutput").ap()
        return dbg_out[name]

    dscr = lambda name, shape, dt: nc.dram_tensor(name, shape, dt, kind="Internal").ap()
    H1 = dscr("H1", [D, OWN], F32)
    AB = dscr("AB", [2, SEQ, 512], F32)
    PQ = dscr("PQ", [2, 64, 64, 512], F32)
    KT = dscr("KT", [D, OWN], BF16)
    QT = dscr("QT", [D, OWN], BF16)
    KTOK = dscr("KTOK", [NT, D], BF16)
    VTOK = dscr("VTOK", [NT, D], BF16)
    B_H1, B_AB, B_PQ, B_KT, B_QT, B_KTOK, B_VTOK = [Buf(n) for n in "H1 AB PQ KT QT KTOK VTOK".split()]

    st = ExitStack()
    with st:
        S = Sched(nc, st)
        ps = [st.enter_context(nc.psum_tensor(f"ps{i}", [128, 512], F32)) for i in range(8)]
        Bps = [Buf(f"ps{i}") for i in range(8)]

        cs_t = nc.alloc_sbuf_tensor("cs", [128, 128 * 4 + 256 + 128 + 32], F32)
        ident = cs_t[:, 0:128]
        maskf = cs_t[:, 128:256]
        maskb = cs_t[:, 256:384]
        CS = cs_t[:, 512:768]
        M2 = cs_t[:, 768:896]
        M3 = cs_t[:, 896:928]
        vec_t = nc.alloc_sbuf_tensor("vec", [128, NV], F32)
        mod_t = nc.alloc_sbuf_tensor("mod", [128, 72 * 2], F32)
        der_t = nc.alloc_sbuf_tensor("der", [128, 16 * 8], F32)
        id16_t = nc.alloc_sbuf_tensor("id16", [128, 128], BF16)
        ones16_t = nc.alloc_sbuf_tensor("ones16", [128, 128], BF16)
        u2_t = nc.alloc_sbuf_tensor("u2", [128, 8 * NT], BF16)
        u2 = u2_t[:, :].rearrange("p (k t) -> p k t", k=8)
        B_const = Buf("const")
        B_mod = Buf("mod")
        B_der = Buf("der")
        tiles = [(0, CTX, 1)] + [(CTX + 512 * i, 512, 0) for i in range(8)]
        B_u2 = [Buf(f"u2_{i}") for i in range(9)]

        def vcol(name, i=0, n=1):
            o, w = VEC[name]
            return vec_t[:, o + i:o + i + n]

        DER = {}
        _d = 0
        for nm in ["A0l", "A0c", "S0l", "S0c", "PAl", "PAc", "A2l", "A2c", "S2l", "S2c", "PMl", "A4l", "S4l", "PBl"]:
            DER[nm] = der_t[:, _d * 8:(_d + 1) * 8]
            _d += 1

        arena = Arena(nc, "arena", 34000)

        S.dma("sp", lambda e: e.dma_start(out=cs_t[:, :], in_=cst), writes=[B_const])
        S.dma("sp", lambda e: e.dma_start(out=vec_t[:, :], in_=vecs), writes=[B_const])
        S.op("act", lambda e: e.copy(id16_t[:, :], ident), reads=[B_const], writes=[B_const])
        S.op("pool", lambda e: e.memset(ones16_t[:, :], 1.0), writes=[B_const])

        m0 = arena.mark()
        cv = arena.f32(16)
        scv = arena.f32(16)
        B_cv = Buf("cv")
        S.dma("sp", lambda e: e.dma_start(out=cv, in_=cvec), writes=[B_cv])
        S.op("act", lambda e: e.activation(scv, cv, AF.Silu), reads=[B_cv], writes=[B_cv])
        wad = [arena.f32(8 * 1024) for _ in range(2)]
        B_wad = [Buf("wad0"), Buf("wad1")]
        w_ada_v = w_ada.rearrange("(k p) n -> p k n", p=128)
        modps = ps[7][:, 0:144]
        for mi in range(9):
            sl = mi % 2
            wv_ = wad[sl].rearrange("p (k n) -> p k n", k=8)
            S.dma("sp", (lambda e, wv_=wv_, mi=mi: e.dma_start(out=wv_, in_=w_ada_v[:, :, mi * 1024:(mi + 1) * 1024])),
                  writes=[B_wad[sl]])
            for dc in range(8):
                def fn(e, wv_=wv_, mi=mi, dc=dc):
                    for k in range(8):
                        ins = e.matmul(modps[:, (mi * 8 + dc) * 2:(mi * 8 + dc) * 2 + 2],
                                       wv_[:, k, dc * 128:(dc + 1) * 128],
                                       scv[:, k * 2:k * 2 + 2], start=(k == 0), stop=(k == 7))
                    return ins
                S.op("pe", fn, reads=[B_wad[sl], B_cv], writes=[Bps[7]])
        modv = mod_t[:, :].rearrange("p (m j) -> p m j", j=2)
        modpsv = modps.rearrange("p (m j) -> p m j", j=2)
        bada = vcol("bada", 0, 72)
        for j in range(2):
            S.op("dve", (lambda e, j=j: e.tensor_tensor(modv[:, :, j], modpsv[:, :, j], bada, ALU.add)),
                 reads=[Bps[7], B_const], writes=[B_mod])

        def modc(mi, j):
            return modv[:, mi * 8:(mi + 1) * 8, j]

        def ng(i):
            return vcol("ng", i * 8, 8)

        def der_scale(name, mi, gi, j):
            S.op("dve", lambda e: e.scalar_tensor_tensor(DER[name], modc(mi, j), 1.0, ng(gi), ALU.add, ALU.mult),
                 reads=[B_mod, B_const], writes=[B_der])

        def der_gate(name, mi, gi, j, f):
            S.op("dve", lambda e: e.scalar_tensor_tensor(DER[name], modc(mi, j), f, ng(gi), ALU.mult, ALU.mult),
                 reads=[B_mod, B_const], writes=[B_der])

        def der_copy(name, mi, j):
            S.op("dve", lambda e: e.tensor_copy(DER[name], modc(mi, j)), reads=[B_mod], writes=[B_der])

        der_scale("A0l", 1, 0, 0); der_scale("A0c", 1, 0, 1)
        der_copy("S0l", 0, 0); der_copy("S0c", 0, 1)
        der_gate("PAl", 2, 1, 0, 0.5); der_gate("PAc", 2, 1, 1, 0.5)
        der_scale("A2l", 4, 2, 0); der_scale("A2c", 4, 2, 1)
        der_copy("S2l", 3, 0); der_copy("S2c", 3, 1)
        der_gate("PMl", 5, 3, 0, 1.0)
        der_scale("A4l", 7, 4, 0); der_copy("S4l", 6, 0); der_gate("PBl", 8, 5, 0, 0.5)
        S.barrier()
        arena.reset(m0)

        def rstd_from(sq_tile, B_sq, W, rstd, B_rstd, pbank):
            def fn(e):
                for k in range(8):
                    ins = e.matmul(ps[pbank][:, 0:W], ones16_t[:, :], sq_tile[:, k, 0:W], start=(k == 0), stop=(k == 7))
                return ins
            S.op("pe", fn, reads=[B_sq, B_const], writes=[Bps[pbank]])
            S.op("act", lambda e: e.activation(rstd[:, 0:W], ps[pbank][:, 0:W], AF.Ln, bias=EPS, scale=1.0 / D),
                 reads=[Bps[pbank]], writes=[B_rstd])
            S.op("act", lambda e: e.activation(rstd[:, 0:W], rstd[:, 0:W], AF.Exp, scale=-0.5),
                 reads=[B_rstd], writes=[B_rstd])

        def norm_mod_thunks(src, B_src, W, rstd, B_rstd, A, Sh, dst_fn, B_dst, tmp, B_tmp):
            def one(k):
                t = tmp[k % 2]
                bt = B_tmp[k % 2]
                S.op("dve", (lambda e: e.tensor_tensor(t[:, 0:W], src[:, k, 0:W], rstd[:, 0:W], ALU.mult)),
                     reads=[B_src, B_rstd], writes=[bt])
                S.op("act", (lambda e: e.activation(dst_fn(k), t[:, 0:W], AF.Identity, bias=Sh[:, k:k + 1], scale=A[:, k:k + 1])),
                     reads=[bt, B_der], writes=[B_dst])
            return [(lambda k=k: one(k)) for k in range(8)]

        def norm_mod(src, B_src, W, rstd, B_rstd, A, Sh, dst_fn, B_dst, tmp, B_tmp):
            for k in range(8):
                t = tmp[k % 2]
                bt = B_tmp[k % 2]
                S.op("dve", (lambda e, k=k, t=t: e.tensor_tensor(t[:, 0:W], src[:, k, 0:W], rstd[:, 0:W], ALU.mult)),
                     reads=[B_src, B_rstd], writes=[bt])
                S.op("act", (lambda e, k=k, t=t: e.activation(dst_fn(k), t[:, 0:W], AF.Identity,
                                                              bias=Sh[:, k:k + 1], scale=A[:, k:k + 1])),
                     reads=[bt, B_der], writes=[B_dst])

        WC = {}

        def wload(dst, B_dst, src_f32, cache, key, ncols):
            if cache is None:
                S.dma("pool", lambda e: e.dma_start(out=dst, in_=src_f32), writes=[B_dst])
                return
            k = (cache,) + key
            if k not in WC:
                sc_ = nc.dram_tensor("wc_" + "_".join(str(z) for z in k), [128, ncols], BF16, kind="Internal").ap()
                WC[k] = (sc_, Buf("wc"))
                S.dma("pool", lambda e: e.dma_start(out=dst, in_=src_f32), writes=[B_dst])
                dflat = dst if len(dst.shape) == 2 else dst.rearrange("p a b -> p (a b)")
                S.dma("sp", lambda e: e.dma_start(out=sc_, in_=dflat), reads=[B_dst], writes=[WC[k][1]])
            else:
                sc_, bsc = WC[k]
                dflat = dst if len(dst.shape) == 2 else dst.rearrange("p a b -> p (a b)")
                S.dma("sp", lambda e: e.dma_start(out=dflat, in_=sc_), reads=[bsc], writes=[B_dst])

        def ffn(src, B_src, W, rstd_pre, B_rstd_pre, A, Sh, PG, w13r, w2r, bufs, cache, do_pre=True, hook1=None, hook2=None, defer_epi=False):
            (sq, B_sq, u, B_u, g, B_g, y, B_y, tmp, B_tmp, rs2, B_rs2, wb13, B_wb13, wb2, B_wb2, sa, B_sa) = bufs
            if do_pre:
                norm_mod(src, B_src, W, rstd_pre, B_rstd_pre, A, Sh, lambda k: u[:, k, 0:W], B_u, tmp, B_tmp)
            n13 = len(wb13)
            for j in range(NJ):
                sl = j % n13
                wbv = wb13[sl].rearrange("p (k c) -> p k c", k=8)
                wload(wb13[sl], B_wb13[sl], w13r[j], cache, ("w13", j), 2048)
                pa = j % 2
                pb = 2 + j % 2

                def fa(e, wbv=wbv, pa=pa):
                    for k in range(8):
                        ins = e.matmul(ps[pa][:, 0:W], wbv[:, k, 0:128], u[:, k, 0:W], start=(k == 0), stop=(k == 7))
                    return ins

                def fb(e, wbv=wbv, pb=pb):
                    for k in range(8):
                        ins = e.matmul(ps[pb][:, 0:W], wbv[:, k, 128:256], u[:, k, 0:W], start=(k == 0), stop=(k == 7))
                    return ins
                S.op("pe", fa, reads=[B_wb13[sl], B_u], writes=[Bps[pa]])
                S.op("pe", fb, reads=[B_wb13[sl], B_u], writes=[Bps[pb]])
                s2 = j % 2
                S.op("act", (lambda e, pa=pa, s2=s2: e.activation(sa[s2][:, 0:W], ps[pa][:, 0:W], AF.Silu)),
                     reads=[Bps[pa]], writes=[B_sa[s2]])
                S.op("dve", (lambda e, pb=pb, s2=s2, j=j: e.tensor_tensor(g[:, j, 0:W], sa[s2][:, 0:W], ps[pb][:, 0:W], ALU.mult)),
                     reads=[B_sa[s2], Bps[pb]], writes=[B_g])
                if hook1 is not None:
                    hook1(j)
            n2 = len(wb2)
            HJ = NJ // 2
            for i in range(8):
                halves = []
                for hf in range(2):
                    sl = (2 * i + hf) % n2
                    wload(wb2[sl], B_wb2[sl], w2r[i][:, hf * HJ * 128:(hf + 1) * HJ * 128], cache, ("w2", i, hf), HJ * 128)
                    halves.append((wb2[sl].rearrange("p (j c) -> p j c", j=HJ), B_wb2[sl]))
                py = 4 + i % 2

                def fy(e, halves=halves, py=py):
                    for j in range(NJ):
                        wv_ = halves[j // HJ][0]
                        ins = e.matmul(ps[py][:, 0:W], wv_[:, j % HJ, :], g[:, j, 0:W], start=(j == 0), stop=(j == NJ - 1))
                    return ins
                S.op("pe", fy, reads=[halves[0][1], halves[1][1], B_g], writes=[Bps[py]])
                S.op("act", (lambda e, py=py, i=i: e.copy(y[:, i, 0:W], ps[py][:, 0:W])), reads=[Bps[py]], writes=[B_y])
                S.op("act", (lambda e, py=py, i=i: e.activation(sq[:, i, 0:W], ps[py][:, 0:W], AF.Square)),
                     reads=[Bps[py]], writes=[B_sq])
                if hook2 is not None:
                    hook2(i)
            rstd_from(sq, B_sq, W, rs2, B_rs2, 7)

            def resid(i):
                t = tmp[i % 2]
                bt = B_tmp[i % 2]
                S.op("dve", (lambda e: e.tensor_tensor(t[:, 0:W], y[:, i, 0:W], rs2[:, 0:W], ALU.mult)),
                     reads=[B_y, B_rs2], writes=[bt])
                S.op("dve", (lambda e: e.scalar_tensor_tensor(src[:, i, 0:W], t[:, 0:W], PG[:, i:i + 1],
                                                              src[:, i, 0:W], ALU.mult, ALU.add)),
                     reads=[bt, B_der], writes=[B_src])
            thunks = [(lambda i=i: resid(i)) for i in range(8)]
            if defer_epi:
                return thunks
            for th in thunks:
                th()
            return []

        def alloc_ffn_bufs():
            sq = arena.bf16(8 * 512).rearrange("p (k t) -> p k t", k=8)
            u = arena.bf16(8 * 512).rearrange("p (k t) -> p k t", k=8)
            g = arena.bf16(NJ * 512).rearrange("p (k t) -> p k t", k=NJ)
            y = arena.f32(8 * 512).rearrange("p (k t) -> p k t", k=8)
            tmp = [arena.f32(512) for _ in range(2)]
            rs2 = arena.f32(512)
            wb13 = [arena.bf16(2048) for _ in range(3)]
            wb2 = [arena.bf16(FF // 2) for _ in range(4)]
            sa = [arena.f32(512) for _ in range(2)]
            return (sq, Buf("sq"), u, Buf("u"), g, Buf("g"), y, Buf("y"), tmp, [Buf("t0"), Buf("t1")],
                    rs2, Buf("rs2"), wb13, [Buf("wb13_%d" % i) for i in range(3)], wb2, [Buf("wb2_%d" % i) for i in range(4)],
                    sa, [Buf("sa0"), Buf("sa1")])

        mA = arena.mark()
        xt = [arena.f32(8 * 512).rearrange("p (k t) -> p k t", k=8) for _ in range(2)]
        B_xt = [Buf("xt0"), Buf("xt1")]
        rs1 = arena.f32(512)
        B_rs1 = Buf("rs1")
        fb = alloc_ffn_bufs()
        tmp, B_tmp, u_, B_u_ = fb[8], fb[9], fb[2], fb[3]
        sqx = arena.bf16(8 * 512).rearrange("p (k t) -> p k t", k=8)
        B_sqx = Buf("sqx")
        xT_v = xT.rearrange("(k p) t -> p k t", p=128)
        ctxT_v = ctxT.rearrange("(k p) t -> p k t", p=128)
        if dbg:
            d_u2 = dbg_tensor("d_u2", [128, 8 * NT], BF16)
        ntile = len(tiles) if stage >= 1 else 0

        def a1_load(ti):
            c0, W, j = tiles[ti]
            x = xt[ti % 2]
            src = ctxT_v[:, :, 0:W] if j == 1 else xT_v[:, :, c0 - CTX:c0 - CTX + W]
            S.dma("pool", lambda e: e.dma_start(out=x[:, :, 0:W], in_=src), writes=[B_xt[ti % 2]])

        def sq_thunks(x, bx, W):
            return [(lambda k=k: S.op("act", (lambda e: e.activation(sqx[:, k, 0:W], x[:, k, 0:W], AF.Square)), reads=[bx], writes=[B_sqx]))
                    for k in range(8)]

        def a1_pre_thunks(ti):
            c0, W, j = tiles[ti]
            x = xt[ti % 2]; bx = B_xt[ti % 2]
            sfx = "c" if j == 1 else "l"
            th = sq_thunks(x, bx, W)
            th.append(lambda: rstd_from(sqx, B_sqx, W, rs1, B_rs1, 6))
            th += norm_mod_thunks(x, bx, W, rs1, B_rs1, DER["A0" + sfx], DER["S0" + sfx], lambda k: u_[:, k, 0:W], B_u_, tmp, B_tmp)
            return th

        def a1_epi2_thunks(ti):
            c0, W, j = tiles[ti]
            x = xt[ti % 2]; bx = B_xt[ti % 2]
            sfx = "c" if j == 1 else "l"
            th = []
            if 1 <= ti <= 4:
                o0 = c0 - CTX
                th.append(lambda: S.dma("pool", lambda e: e.dma_start(out=H1.rearrange("(k p) t -> p k t", p=128)[:, :, o0:o0 + 512], in_=x[:, :, :]),
                                        reads=[bx], writes=[B_H1]))
            th += sq_thunks(x, bx, W)
            th.append(lambda: rstd_from(sqx, B_sqx, W, rs1, B_rs1, 6))
            th += norm_mod_thunks(x, bx, W, rs1, B_rs1, DER["A2" + sfx], DER["S2" + sfx], (lambda k: u2[:, k, c0:c0 + W]), B_u2[ti], tmp, B_tmp)
            return th

        pend1 = []
        pend2 = []
        if ntile:
            a1_load(0)
            for th in a1_pre_thunks(0):
                th()
            if ntile > 1:
                a1_load(1)
        for ti in range(ntile):
            c0, W, j = tiles[ti]
            sfx = "c" if j == 1 else "l"
            if ti + 1 < ntile:
                pend2 = a1_pre_thunks(ti + 1)

            def hook1(j_):
                n = 2 if len(pend1) > (NJ - 1 - j_) else 1
                for _ in range(n):
                    if pend1:
                        pend1.pop(0)()

            def hook2(i_):
                while pend1:
                    pend1.pop(0)()
                if i_ >= 1:
                    for _ in range(3):
                        if pend2:
                            pend2.pop(0)()
            epi1 = ffn(xt[ti % 2], B_xt[ti % 2], W, None, None, None, None, DER["PA" + sfx], w13a, w2a, fb, "A",
                       do_pre=False, hook1=hook1, hook2=hook2, defer_epi=True)
            while pend1:
                pend1.pop(0)()
            while pend2:
                pend2.pop(0)()
            pend1 = list(epi1) + a1_epi2_thunks(ti)
            if ti + 2 < ntile:
                pend1.append(lambda ti=ti: a1_load(ti + 2))
        while pend1:
            pend1.pop(0)()
        S.barrier()
        arena.reset(mA)
        if dbg:
            S.dma("sp", lambda e: e.dma_start(out=d_u2, in_=u2_t[:, :]), reads=B_u2)
            d_h1 = dbg_tensor("d_h1", [D, OWN])
            S.dma("sp", lambda e: e.dma_start(out=d_h1, in_=H1), reads=[B_H1])

        env = dict(locals())
        if stage >= 2:
            build_rest(env)
        S.barrier()
        S.emit()
    return nc, dbg_out


def build_rest3(g_, L):
    AX = mybir.AxisListType
    nc = g_["nc"]; S = g_["S"]; ps = g_["ps"]; Bps = g_["Bps"]; arena = g_["arena"]
    u2 = g_["u2"]; B_u2 = g_["B_u2"]; vcol = g_["vcol"]; DER = g_["DER"]
    id16 = g_["id16"]; B_const = g_["B_const"]; mm = g_["mm"]
    H1, HS = g_["H1"], g_["HS"]; B_H1 = g_["B_H1"]; B_HS = g_["B_HS"]
    yfT, B_yfT, mB = g_["yfT"], g_["B_yfT"], g_["mB"]
    rowv = g_["rowv"]; outT = g_["outT"]
    ffn = g_["ffn"]; rstd_from = g_["rstd_from"]; wload = g_["wload"]
    kp = lambda ap: ap.rearrange("(k p) n -> p k n", p=128)
    A = arena.t
    arena.reset(mB)
    tmp = [A[:, 0:512], A[:, 512:1024]]; B_tmp = [Buf("ct0"), Buf("ct1")]
    rs2 = A[:, 1024:1536]; B_rs2 = Buf("crs2")
    yreg = A[:, 5816:9912]
    B_yreg = Buf("yreg")
    y = yreg.rearrange("p (k t) -> p k t", k=8)
    hs_t = yreg[:, 0:1024]; o_sb = yreg[:, 1024:2048]; sig = yreg[:, 2048:3072]; sqh = yreg[:, 3072:4096]
    B_hsC = Buf("c_hs"); B_osb = Buf("c_osb"); B_sig = Buf("c_sig"); B_sqh = Buf("c_sqh")
    bo_bc = A[:, 9912:10936]; hg_bc = A[:, 10936:11960]
    B_bc = Buf("cbc")
    x = arena.f32(4096).rearrange("p (k t) -> p k t", k=8); B_x = Buf("cx")
    rs1 = arena.f32(512); B_rs1 = Buf("crs1")
    sq = arena.bf16(4096).rearrange("p (k t) -> p k t", k=8); B_sq = Buf("csq")
    u = arena.bf16(4096).rearrange("p (k t) -> p k t", k=8); B_u = Buf("cu")
    hmT = sq; yT = u
    sa = [arena.f32(512), arena.f32(512)]; B_sa = [Buf("csa0"), Buf("csa1")]
    mX = arena.mark()
    g = arena.bf16(NJ * 512).rearrange("p (k t) -> p k t", k=NJ); B_g = Buf("cg")
    wb13m = [arena.bf16(2048), arena.bf16(2048), arena.bf16(2048)]; B_wb13m = [Buf("cw13_0"), Buf("cw13_1"), Buf("cw13_2")]
    wb2m = arena.bf16(FF); B_wb2m = Buf("cw2")
    wb2x = A[:, 11992:11992 + 704].bitcast(BF16), A[:, 11992 + 704:11992 + 1408].bitcast(BF16)
    arena.reset(mX)
    wsl = [arena.bf16(8 * 512).rearrange("p (k n) -> p k n", k=8) for _ in range(4)]; B_wsl = [Buf("wsl%d" % i) for i in range(4)]
    hm = arena.bf16(1024); B_hm = Buf("hm")
    ol = A[:, mX + 4096:mX + 8192].rearrange("p (k t) -> p k t", k=8)
    B_ol = [B_wsl[2], B_wsl[3]]
    ss = rs2[:, 0:8]
    fb = (sq, B_sq, u, B_u, g, B_g, y, B_yreg, tmp, B_tmp, rs2, B_rs2, wb13m, B_wb13m,
          [wb2m[:, 0:FF // 2], wb2m[:, FF // 2:FF], wb2x[0], wb2x[1]], [Buf('cw2a'), Buf('cw2b'), Buf('cw2c'), Buf('cw2d')], sa, B_sa)
    xflat = A[:, mB:mB + 4096]
    rv = xflat[:, 0:3072]; ones1 = xflat[:, 3072:3200]
    S.dma("sp", lambda e: e.dma_start(out=rv[0:1, :], in_=rowv), writes=[B_x])
    S.op("pool", lambda e: e.memset(ones1[0:1, :], 1.0), writes=[B_x])
    for (dst, seg) in ((bo_bc, 1), (hg_bc, 2)):
        for hh in range(2):
            mm(ps[6][:, 0:512], [(ones1[0:1, :], rv[0:1, seg * 1024 + hh * 512:seg * 1024 + hh * 512 + 512])], reads=[B_x], writes=[Bps[6]])
            S.op("act", (lambda e, hh=hh, dst=dst: e.copy(dst[:, hh * 512:(hh + 1) * 512], ps[6][:, 0:512])), reads=[Bps[6]], writes=[B_bc])
    S.barrier()
    w_four, w_mproj, w_out, wo, wgf, wgm = [g_[n] for n in "w_four w_mproj w_out wo wgf wgm".split()]
    w13b, w2b = g_["w13b"], g_["w2b"]
    H1v = H1.rearrange("(k p) t -> p k t", p=128)
    outv = outT.rearrange("(k p) t -> p k t", p=128)
    for T in range(4):
        t0 = 512 * T
        c0 = CTX + t0
        S.dma("sp", (lambda e, t0=t0: e.dma_start(out=x[:, :, :], in_=H1v[:, :, t0:t0 + 512])), reads=[B_H1], writes=[B_x])
        for hh in range(2):
            wload(wsl[hh], B_wsl[hh], kp(wo)[:, :, hh * 512:(hh + 1) * 512], "C", ("wo", hh), 4096)
        for ch in range(4):
            cc = c0 + 128 * ch
            ob = t0 + 128 * ch
            S.dma("sp", (lambda e, ob=ob: e.dma_start(out=hs_t, in_=HS[ob:ob + 128, :])), reads=[B_HS[ob // 128]], writes=[B_hsC])
            for hh in range(2):
                mm(ps[hh][:, 0:512], [(u2[:, k, cc:cc + 128], wsl[hh][:, k, :]) for k in range(8)], reads=[B_wsl[hh]] + B_u2, writes=[Bps[hh]])
                S.op("dve", (lambda e, hh=hh: e.tensor_tensor(o_sb[:, hh * 512:(hh + 1) * 512], ps[hh][:, 0:512], bo_bc[:, hh * 512:(hh + 1) * 512], ALU.add)),
                     reads=[Bps[hh], B_bc], writes=[B_osb])
            S.op("act", lambda e: e.activation(sig, o_sb, AF.Sigmoid), reads=[B_osb], writes=[B_sig])
            S.op("dve", lambda e: e.tensor_tensor(sig, sig, hg_bc, ALU.mult), reads=[B_sig, B_bc], writes=[B_sig])
            S.op("act", lambda e: e.activation(sqh, hs_t, AF.Square), reads=[B_hsC], writes=[B_sqh])
            S.op("dve", lambda e: e.reduce_sum(ss[:, 0:4], sqh.rearrange("p (h e) -> p h e", h=4), AX.X), reads=[B_sqh], writes=[B_rs2])
            S.op("act", lambda e: e.activation(ss[:, 0:4], ss[:, 0:4], AF.Ln, bias=EPS, scale=1.0 / 256), reads=[B_rs2], writes=[B_rs2])
            S.op("act", lambda e: e.activation(ss[:, 0:4], ss[:, 0:4], AF.Exp, scale=-0.5), reads=[B_rs2], writes=[B_rs2])
            for h in range(4):
                S.op("dve", (lambda e, h=h: e.scalar_tensor_tensor(hm[:, h * 256:(h + 1) * 256], hs_t[:, h * 256:(h + 1) * 256], ss[:, h:h + 1],
                                                                  sig[:, h * 256:(h + 1) * 256], ALU.mult, ALU.mult)),
                     reads=[B_hsC, B_sig, B_rs2], writes=[B_hm])
            for half in range(2):
                pb = 2 + half
                def fn(e, half=half, pb=pb):
                    for cq in range(4):
                        c = half * 4 + cq
                        ins = e.matmul(ps[pb][:, cq * 128:(cq + 1) * 128], hm[:, c * 128:(c + 1) * 128], id16[:, :], start=True, stop=True)
                    return ins
                S.op("pe", fn, reads=[B_hm, B_const], writes=[Bps[pb]])
                S.op("act", (lambda e, half=half, pb=pb, ch=ch: e.copy(hmT[:, half * 4:(half + 1) * 4, ch * 128:(ch + 1) * 128],
                                                                      ps[pb][:, 0:512].rearrange("p (c t) -> p c t", c=4))),
                     reads=[Bps[pb]], writes=[B_sq])
        for hh in range(2):
            cs_ = slice(hh * 512, (hh + 1) * 512)
            wload(wsl[0][:, 0:4, :], B_wsl[0], kp(w_four)[:, :, cs_], "C", ("w4", hh), 2048)
            wload(wsl[1], B_wsl[1], kp(w_mproj)[:, :, cs_], "C", ("wm", hh), 4096)
            wload(wsl[2], B_wsl[2], kp(wgf)[:, :, cs_], "C", ("wgf", hh), 4096)
            wload(wsl[3], B_wsl[3], kp(wgm)[:, :, cs_], "C", ("wgm", hh), 4096)
            for ii in range(4):
                i = hh * 4 + ii
                cw = slice(ii * 128, (ii + 1) * 128)
                mm(ps[0][:, 0:512], [(wsl[0][:, gq, cw], yfT[:, gq, t0:t0 + 512]) for gq in range(4)], reads=[B_wsl[0], B_yfT], writes=[Bps[0]])
                mm(ps[1][:, 0:512], [(wsl[1][:, k, cw], hmT[:, k, :]) for k in range(8)], reads=[B_wsl[1], B_sq], writes=[Bps[1]])
                mm(ps[2][:, 0:512], [(wsl[2][:, k, cw], u2[:, k, c0:c0 + 512]) for k in range(8)], reads=[B_wsl[2]] + B_u2, writes=[Bps[2]])
                mm(ps[3][:, 0:512], [(wsl[3][:, k, cw], u2[:, k, c0:c0 + 512]) for k in range(8)], reads=[B_wsl[3]] + B_u2, writes=[Bps[3]])
                S.op("act", (lambda e, i=i: e.activation(sa[0], ps[2][:, 0:512], AF.Sigmoid, bias=vcol("bgf", i, 1))), reads=[Bps[2], B_const], writes=[B_sa[0]])
                S.op("act", (lambda e, i=i: e.activation(sa[1], ps[3][:, 0:512], AF.Sigmoid, bias=vcol("bgm", i, 1))), reads=[Bps[3], B_const], writes=[B_sa[1]])
                S.op("dve", lambda e: e.tensor_tensor(sa[0], sa[0], ps[0][:, 0:512], ALU.mult), reads=[Bps[0], B_sa[0]], writes=[B_sa[0]])
                S.op("dve", lambda e: e.tensor_tensor(sa[1], sa[1], ps[1][:, 0:512], ALU.mult), reads=[Bps[1], B_sa[1]], writes=[B_sa[1]])
                S.op("dve", (lambda e, i=i: e.tensor_tensor(yT[:, i, :], sa[0], sa[1], ALU.add)), reads=B_sa, writes=[B_u])
        for hh in range(2):
            wload(wsl[hh], B_wsl[hh], kp(w_out)[:, :, hh * 512:(hh + 1) * 512], "C", ("wout", hh), 4096)
        for i in range(8):
            pb = 4 + i % 2
            mm(ps[pb][:, 0:512], [(wsl[i // 4][:, k, (i % 4) * 128:(i % 4 + 1) * 128], yT[:, k, :]) for k in range(8)],
               reads=[B_wsl[i // 4], B_u], writes=[Bps[pb]])
            S.op("act", (lambda e, i=i, pb=pb: e.copy(ol[:, i, :], ps[pb][:, 0:512])), reads=[Bps[pb]], writes=B_ol)
            S.op("act", (lambda e, i=i, pb=pb: e.activation(sq[:, i, :], ps[pb][:, 0:512], AF.Square)), reads=[Bps[pb]], writes=[B_sq])
        rstd_from(sq, B_sq, 512, rs1, B_rs1, 6)
        for i in range(8):
            t = tmp[i % 2]; bt = B_tmp[i % 2]
            S.op("dve", (lambda e, i=i, t=t: e.tensor_tensor(t, ol[:, i, :], rs1, ALU.mult)), reads=B_ol + [B_rs1], writes=[bt])
            S.op("dve", (lambda e, i=i, t=t: e.scalar_tensor_tensor(x[:, i, :], t, DER["PMl"][:, i:i + 1], x[:, i, :], ALU.mult, ALU.add)),
                 reads=[bt, g_["B_der"]], writes=[B_x])
        S.barrier()
        S.op("act", lambda e: e.activation(sq[:, :, :], x[:, :, :], AF.Square), reads=[B_x], writes=[B_sq])
        rstd_from(sq, B_sq, 512, rs1, B_rs1, 6)
        ffn(x, B_x, 512, rs1, B_rs1, DER["A4l"], DER["S4l"], DER["PBl"], w13b, w2b, fb, "B")
        S.dma("sp", (lambda e, t0=t0: e.dma_start(out=outv[:, :, t0:t0 + 512], in_=x[:, :, :])), reads=[B_x])
        S.barrier()


def build_rest2(env, L):
    AX = mybir.AxisListType
    g_ = dict(env); g_.update(L)
    nc = g_["nc"]; S = g_["S"]; ps = g_["ps"]; Bps = g_["Bps"]; arena = g_["arena"]
    u2 = g_["u2"]; B_u2 = g_["B_u2"]; vcol = g_["vcol"]; DER = g_["DER"]
    ident = g_["ident"]; maskf = g_["maskf"]; maskb = g_["maskb"]; CS = g_["CS"]; M2 = g_["M2"]; M3 = g_["M3"]
    id16 = g_["id16"]; ones16 = g_["ones16"]; B_const = g_["B_const"]; B_der = g_["B_der"]
    sel = g_["sel"]; negsel = g_["negsel"]; mm = g_["mm"]
    H1, AB, PQ, KT, QT, KTOK, VTOK, GROW, HS = [g_[n] for n in "H1 AB PQ KT QT KTOK VTOK GROW HS".split()]
    B_H1, B_AB, B_PQ, B_KT, B_QT, B_KTOK, B_VTOK, B_GROW = [g_["B_" + n] for n in "H1 AB PQ KT QT KTOK VTOK GROW".split()]
    B_HS = g_["B_HS"]
    Rcol, Gcol, Ecol, gend, acol, wkcol, decay, B_cols = [g_[n] for n in "Rcol Gcol Ecol gend acol wkcol decay B_cols".split()]
    yfT, B_yfT, C32, C16, B_C, mB = [g_[n] for n in "yfT B_yfT C32 C16 B_C mB".split()]
    rowv = g_["rowv"]; outT = g_["outT"]
    tiles = g_["tiles"]
    kp = lambda ap: ap.rearrange("(k p) n -> p k n", p=128)

    wbuf = [arena.bf16(8 * 1024).rearrange("p (k n) -> p k n", k=8) for _ in range(2)]
    B_wbuf = [Buf("wbuf0"), Buf("wbuf1")]
    xf = [arena.f32(512) for _ in range(4)]
    B_xf = [Buf("xf%d" % i) for i in range(4)]
    ab_sb2 = [arena.f32(1024).rearrange("p (x g c) -> p x g c", x=2, g=4) for _ in range(2)]
    B_ab2 = [Buf("ab_sb0"), Buf("ab_sb1")]
    Pt2 = [arena.f32(514), arena.f32(514)]
    B_Pt2 = [Buf("Pt0"), Buf("Pt1")]
    acc2 = [arena.f32(512), arena.f32(512)]
    B_acc2 = [Buf("acc0"), Buf("acc1")]
    kTt = arena.bf16(8 * 512).rearrange("p (k t) -> p k t", k=8)
    B_kTt = Buf("kTt")
    tok2 = [arena.bf16(1024), arena.bf16(1024)]
    B_tok2 = [Buf("tok0"), Buf("tok1")]
    tokctr = [0]
    bv_bc = arena.f32(1024)
    ones1 = arena.f32(128)
    rv = arena.f32(1024)
    B_bc = Buf("bc")
    S.dma("sp", lambda e: e.dma_start(out=rv[0:1, :], in_=rowv[:, 0:1024]), writes=[B_bc])
    S.op("pool", lambda e: e.memset(ones1[0:1, :], 1.0), writes=[B_bc])

    def bcast_row(dst, seg):
        for hh in range(2):
            mm(ps[6][:, 0:512], [(ones1[0:1, :], rv[0:1, seg * 1024 + hh * 512:seg * 1024 + hh * 512 + 512])], reads=[B_bc], writes=[Bps[6]])
            S.op("act", (lambda e, hh=hh: e.copy(dst[:, hh * 512:(hh + 1) * 512], ps[6][:, 0:512])), reads=[Bps[6]], writes=[B_bc])
    bcast_row(bv_bc, 0)

    wF = g_["wF"]
    S.dma("pool", lambda e: e.dma_start(out=wbuf[0][:, :, 0:512], in_=kp(wF)), writes=[B_wbuf[0]])
    for i in range(8):
        c0 = CTX + 512 * i
        for g in range(4):
            pb = g % 2
            mm(ps[pb][:, 0:512], [(wbuf[0][:, k, g * 128:(g + 1) * 128], u2[:, k, c0:c0 + 512]) for k in range(8)],
               reads=[B_wbuf[0]] + B_u2, writes=[Bps[pb]])
            S.op("act", (lambda e, g=g, pb=pb: e.activation(xf[g], ps[pb][:, 0:512], AF.Identity, bias=vcol("bF", g, 1))),
                 reads=[Bps[pb], B_const], writes=[B_xf[g]])
        for tb in range(4):
            ab_sb = ab_sb2[tb % 2]; B_ab = B_ab2[tb % 2]
            for g in range(4):
                bank = 2 + g // 2
                mm(ps[bank][:, (g % 2) * 256:(g % 2) * 256 + 256], [(xf[g][:, tb * 128:(tb + 1) * 128], CS)],
                   reads=[B_xf[g], B_const], writes=[Bps[bank]])
            for bi in range(2):
                S.op("dve", (lambda e, bi=bi, ab_sb=ab_sb: e.tensor_copy(ab_sb[:, :, 2 * bi:2 * bi + 2, :],
                                                            ps[2 + bi][:, 0:512].rearrange("p (g x c) -> p x g c", g=2, x=2))),
                     reads=[Bps[2 + bi]], writes=[B_ab])
            tok0 = 512 * i + 128 * tb
            S.dma("sp", (lambda e, tok0=tok0, ab_sb=ab_sb: e.dma_start(out=AB.rearrange("x t f -> t x f")[tok0:tok0 + 128, :, :],
                                                          in_=ab_sb.rearrange("p x g c -> p x (g c)"))), reads=[B_ab], writes=[B_AB])

    def qk_proj(wdram, bname, cwname, cbname, slot, tlist, is_k):
        S.dma("pool", lambda e: e.dma_start(out=wbuf[slot], in_=kp(wdram)), writes=[B_wbuf[slot]])
        for (ti, c0, W) in tlist:
            islat = ti >= 1
            left = islat and ti > 1
            right = islat and ti < 8
            for c in range(8):
                pb = c % 2
                Pt = Pt2[c % 2]; B_Pt = B_Pt2[c % 2]; acc = acc2[c % 2]; B_acc = B_acc2[c % 2]
                hb = 5 + c % 2
                mm(ps[pb][:, 0:W], [(wbuf[slot][:, k, c * 128:(c + 1) * 128], u2[:, k, c0:c0 + W]) for k in range(8)],
                   reads=[B_wbuf[slot]] + B_u2, writes=[Bps[pb]])
                S.op("act", (lambda e, c=c, pb=pb, W=W, Pt=Pt: e.activation(Pt[:, 1:W + 1], ps[pb][:, 0:W], AF.Identity, bias=vcol(bname, c, 1))),
                     reads=[Bps[pb], B_const], writes=[B_Pt])
                if left and right:
                    mm(ps[hb][:, 0:2], [(wbuf[slot][:, k, c * 128:(c + 1) * 128], u2[:, k, c0 - 1:c0 + W + 1:W + 1]) for k in range(8)],
                       reads=[B_wbuf[slot]] + B_u2, writes=[Bps[hb]])
                    S.op("act", (lambda e, c=c, Pt=Pt, hb=hb, W=W: e.activation(Pt[:, 0:W + 2:W + 1], ps[hb][:, 0:2], AF.Identity, bias=vcol(bname, c, 1))),
                         reads=[Bps[hb], B_const], writes=[B_Pt])
                else:
                    for hi_, (has, col, dstc) in enumerate(((left, c0 - 1, 0), (right, c0 + W, W + 1))):
                        if has:
                            mm(ps[hb][:, hi_:hi_ + 1], [(wbuf[slot][:, k, c * 128:(c + 1) * 128], u2[:, k, col:col + 1]) for k in range(8)],
                               reads=[B_wbuf[slot]] + B_u2, writes=[Bps[hb]])
                            S.op("act", (lambda e, c=c, dstc=dstc, Pt=Pt, hb=hb, hi_=hi_: e.activation(Pt[:, dstc:dstc + 1], ps[hb][:, hi_:hi_ + 1], AF.Identity, bias=vcol(bname, c, 1))),
                                 reads=[Bps[hb], B_const], writes=[B_Pt])
                        else:
                            S.op("pool", (lambda e, dstc=dstc, Pt=Pt: e.memset(Pt[:, dstc:dstc + 1], 0.0)), writes=[B_Pt])
                S.op("act", (lambda e, c=c, W=W, Pt=Pt, acc=acc: e.activation(acc[:, 0:W], Pt[:, 0:W], AF.Identity, scale=vcol(cwname, c, 1))),
                     reads=[B_Pt, B_const], writes=[B_acc])
                S.op("dve", (lambda e, c=c, W=W, Pt=Pt, acc=acc: e.scalar_tensor_tensor(acc[:, 0:W], Pt[:, 1:W + 1], vcol(cwname, 8 + c, 1), acc[:, 0:W], ALU.mult, ALU.add)),
                     reads=[B_Pt, B_const], writes=[B_acc])
                S.op("dve", (lambda e, c=c, W=W, Pt=Pt, acc=acc: e.scalar_tensor_tensor(acc[:, 0:W], Pt[:, 2:W + 2], vcol(cwname, 16 + c, 1), acc[:, 0:W], ALU.mult, ALU.add)),
                     reads=[B_Pt, B_const], writes=[B_acc])
                S.op("act", (lambda e, c=c, W=W, acc=acc: e.activation(kTt[:, c, 0:W], acc[:, 0:W], AF.Silu, bias=vcol(cbname, c, 1))),
                     reads=[B_acc, B_const], writes=[B_kTt])
            own = 1 <= ti <= 4
            if own:
                o0 = c0 - CTX
                dst = KT if is_k else QT
                S.dma("sp", (lambda e, o0=o0, dst=dst: e.dma_start(out=dst.rearrange("(k p) t -> p k t", p=128)[:, :, o0:o0 + 512], in_=kTt[:, :, :])),
                      reads=[B_kTt], writes=[B_KT if is_k else B_QT])
            if is_k:
                for tb in range(W // 128):
                    tok = tok2[tokctr[0] % 2]; B_tok = B_tok2[tokctr[0] % 2]; tokctr[0] += 1
                    for half in range(2):
                        pb = 3 + half
                        def fn(e, tb=tb, half=half, pb=pb):
                            for cc in range(4):
                                c = half * 4 + cc
                                ins = e.matmul(ps[pb][:, cc * 128:(cc + 1) * 128], kTt[:, c, tb * 128:(tb + 1) * 128], id16[:, :], start=True, stop=True)
                            return ins
                        S.op("pe", fn, reads=[B_kTt, B_const], writes=[Bps[pb]])
                        S.op("dve", (lambda e, half=half, pb=pb, tok=tok: e.tensor_copy(tok[:, half * 512:(half + 1) * 512], ps[pb][:, 0:512])),
                             reads=[Bps[pb]], writes=[B_tok])
                    r0 = c0 + tb * 128
                    S.dma("sp", (lambda e, r0=r0, tok=tok: e.dma_start(out=KTOK[r0:r0 + 128, :], in_=tok)), reads=[B_tok], writes=[B_KTOK])

    tl_all = [(ti, c0, W) for ti, (c0, W, j) in enumerate(tiles)]
    qk_proj(g_["wk"], "bk", "cwk", "cbk", 1, tl_all, True)
    qk_proj(g_["wq"], "bq", "cwq", "cbq", 0, tl_all[1:5], False)
    S.dma("pool", lambda e: e.dma_start(out=wbuf[1], in_=kp(g_["wv"])), writes=[B_wbuf[1]])
    for cb in range(0, NT, 128):
        tok = tok2[tokctr[0] % 2]; B_tok = B_tok2[tokctr[0] % 2]; tokctr[0] += 1
        for half in range(2):
            pb = half + 2 * ((cb // 128) % 2)
            mm(ps[pb][:, 0:512], [(u2[:, k, cb:cb + 128], wbuf[1][:, k, half * 512:(half + 1) * 512]) for k in range(8)],
               reads=[B_wbuf[1]] + B_u2, writes=[Bps[pb]])
            S.op("dve", (lambda e, half=half, pb=pb, tok=tok: e.tensor_tensor(tok[:, half * 512:(half + 1) * 512], ps[pb][:, 0:512],
                                                                     bv_bc[:, half * 512:(half + 1) * 512], ALU.add)),
                 reads=[Bps[pb], B_bc], writes=[B_tok])
        S.dma("sp", (lambda e, cb=cb, tok=tok: e.dma_start(out=VTOK[cb:cb + 128, :], in_=tok)), reads=[B_tok], writes=[B_VTOK])
    S.barrier()
    arena.reset(mB)

    inb2 = [arena.f32(8 * 512).rearrange("p (r f) -> p r f", r=8) for _ in range(2)]
    outb2 = [arena.f32(8 * 512).rearrange("p (r f) -> p r f", r=8) for _ in range(2)]
    B_inb2 = [Buf("inb0"), Buf("inb1")]; B_outb2 = [Buf("outb0"), Buf("outb1")]
    ABv = AB.rearrange("x (r c) f -> x c r f", c=64)
    PQw = PQ
    for rb in range(8):
        inb = inb2[rb % 2]; outb = outb2[rb % 2]; B_inb = B_inb2[rb % 2]; B_outb = B_outb2[rb % 2]
        for x in range(2):
            S.dma("sp", (lambda e, rb=rb, x=x, inb=inb: e.dma_start(out=inb[64 * x:64 * x + 64, :, :], in_=ABv[x, :, rb * 8:rb * 8 + 8, :])),
                  reads=[B_AB], writes=[B_inb])
        for r in range(8):
            pb = r % 4
            mm(ps[pb][:, 0:512], [(M2, inb[:, r, :])], reads=[B_inb, B_const], writes=[Bps[pb]])
            if r % 2 == 0:
                S.op("act", (lambda e, r=r, pb=pb, outb=outb: e.copy(outb[:, r, :], ps[pb][:, 0:512])), reads=[Bps[pb]], writes=[B_outb])
            else:
                S.op("dve", (lambda e, r=r, pb=pb, outb=outb: e.tensor_copy(outb[:, r, :], ps[pb][:, 0:512])), reads=[Bps[pb]], writes=[B_outb])
        for x in range(2):
            S.dma("sp", (lambda e, rb=rb, x=x, outb=outb: e.dma_start(out=PQw[x, :, rb * 8:rb * 8 + 8, :], in_=outb[64 * x:64 * x + 64, :, :])),
                  reads=[B_outb], writes=[B_PQ])
    PQr = PQ.rearrange("x kc r f -> x r kc f")
    for kb in range(8):
        inb = inb2[kb % 2]; B_inb = B_inb2[kb % 2]
        for x in range(2):
            S.dma("sp", (lambda e, kb=kb, x=x, inb=inb: e.dma_start(out=inb[64 * x:64 * x + 64, :, :], in_=PQr[x, :, kb * 8:kb * 8 + 8, :])),
                  reads=[B_PQ], writes=[B_inb])
        for g in range(4):
            pb = 4 + g
            def fn(e, g=g, pb=pb, inb=inb):
                for kc in range(8):
                    ins = e.matmul(ps[pb][:, kc * 32:(kc + 1) * 32], inb[:, kc, g * 128:(g + 1) * 128], M3, start=True, stop=True)
                return ins
            S.op("pe", fn, reads=[B_inb, B_const], writes=[Bps[pb]])
            S.op("act", (lambda e, g=g, pb=pb, kb=kb: e.copy(yfT[:, g, :].rearrange("p (kr kc) -> p kc kr", kc=64)[:, kb * 8:kb * 8 + 8, :],
                                                          ps[pb][:, 0:256].rearrange("p (kc kr) -> p kc kr", kr=32))),
                 reads=[Bps[pb]], writes=[B_yfT])
    S.barrier()
    arena.reset(mB)

    NLS = 3
    NHS = 2
    LD = []
    for i in range(2 * NLS):
        d_ = dict(ktok=arena.bf16(1024), vaug=arena.bf16(4 * 258).rearrange("p (h e) -> p h e", h=4),
                  kT=arena.bf16(1024).rearrange("p (k t) -> p k t", k=8), qT=arena.bf16(1024).rearrange("p (k t) -> p k t", k=8),
                  grow=arena.f32(128), B_ld=Buf("ld%d" % i), B_ldo=Buf("ldo%d" % i))
        S.op("pool", (lambda e, v=d_["vaug"]: e.memset(v[:, :, :], 1.0)), writes=[d_["B_ld"]])
        LD.append(d_)
    HSB = [dict(hs=arena.f32(1024), B_hs=Buf("hs%d" % i)) for i in range(4)]
    HT = []
    B_pCUs = [Buf("pCU0"), Buf("pCU1")]
    for i in range(NHS):
        HT.append(dict(wT=arena.f32(128), STb=arena.bf16(128), P2sb=arena.f32(257), hn=arena.f32(257), dd=arena.f32(2), kw=arena.bf16(256),
                       B_wT=Buf("wT%d" % i), B_ST=Buf("ST%d" % i), B_P2=Buf("P2sb%d" % i), B_hn=Buf("hn%d" % i), B_dd=Buf("dd%d" % i),
                       B_kw=Buf("kw%d" % i),
                       pST=ps[i][:, 0:128], pD=ps[i][:, 128:256], pP2=ps[2 + i][:, 0:257], pP1=ps[4 + i][:, 0:257],
                       pCU=[ps[6][:, 0:257], ps[7][:, 0:257]],
                       B_pSD=Buf("pSD%d" % i), B_pP2=Buf("pP2%d" % i), B_pP1=Buf("pP1%d" % i), B_pCU=B_pCUs))
    B_C32 = [[Buf('C32_%d_%d' % (q, c)) for c in range(2)] for q in range(8)]
    B_C16 = [[Buf('C16_%d_%d' % (q, c)) for c in range(2)] for q in range(8)]
    steps = []
    fw = [(0, 0, None), (1, 128, None)] + [(2 + i, CTX + 128 * i, 128 * i) for i in range(16)]
    bw = [(0, 128, None), (1, 0, None)] + [(2 + i, CTX + 128 * (31 - i), (128 * (31 - i) if 31 - i <= 15 else None)) for i in range(32)]
    for i in range(34):
        if i < 18:
            steps.append((0,) + fw[i])
        steps.append((1,) + bw[i])
    mask16 = [arena.bf16(128), arena.bf16(128)]
    S.op("act", lambda e: e.copy(mask16[0], maskf), reads=[B_const], writes=[B_const])
    S.op("act", lambda e: e.copy(mask16[1], maskb), reads=[B_const], writes=[B_const])

    def emit_loads(si):
        (dr, sc, cb, ob) = steps[si]
        L_ = LD[dr * NLS + sc % NLS]
        ktok_t, vaug, kT_t, qT_t, grow_t = L_["ktok"], L_["vaug"], L_["kT"], L_["qT"], L_["grow"]
        B_ld, B_ldo = L_["B_ld"], L_["B_ldo"]
        S.dma("sp", (lambda e: e.dma_start(out=ktok_t, in_=KTOK[cb:cb + 128, :])), reads=[B_KTOK], writes=[B_ld])
        S.dma("sp", (lambda e: e.dma_start(out=vaug[:, :, 0:256], in_=VTOK[cb:cb + 128, :].rearrange("t (h e) -> t h e", h=4))),
              reads=[B_VTOK], writes=[B_ld])
        if ob is not None:
            S.dma("pool", (lambda e: e.dma_start(out=kT_t, in_=KT.rearrange("(k p) t -> p k t", p=128)[:, :, ob:ob + 128])), reads=[B_KT], writes=[B_ldo])
            S.dma("pool", (lambda e: e.dma_start(out=qT_t, in_=QT.rearrange("(k p) t -> p k t", p=128)[:, :, ob:ob + 128])), reads=[B_QT], writes=[B_ldo])
            S.dma("pool", (lambda e: e.dma_start(out=grow_t[0:36, :], in_=GROW[:, cb:cb + 128])), reads=[B_GROW], writes=[B_ldo])

    def emit_load_hs(si):
        (dr, sc, cb, ob) = steps[si]
        if ob is not None and dr == 1:
            H2 = HSB[dr * 2 + sc % 2]
            S.dma("sp", (lambda e: e.dma_start(out=H2["hs"], in_=HS[ob:ob + 128, :])), reads=[B_HS[ob // 128]], writes=[H2["B_hs"]])

    items = []
    for si, (dr, sc, cb, ob) in enumerate(steps):
        for h in range(4):
            items.append((si, h, len(items)))

    def ctx_of(it):
        si, h, n = it
        (dr, sc, cb, ob) = steps[si]
        L2 = dict(LD[dr * NLS + sc % NLS]); L2.update(HSB[dr * 2 + sc % 2])
        return dr, sc, cb, ob, h, dr * 4 + h, (sc if dr == 0 else 18 + sc), L2, HT[n % NHS]

    def emit_A(it):
        dr, sc, cb, ob, h, q, ci, L_, H_ = ctx_of(it)
        kT_t, qT_t, grow_t, ktok_t = L_["kT"], L_["qT"], L_["grow"], L_["ktok"]
        wT, STb, kw, pST, pD = H_["wT"], H_["STb"], H_["kw"], H_["pST"], H_["pD"]
        S.op("pool", (lambda e: e.tensor_scalar(kw, ktok_t[:, h * 256:(h + 1) * 256], wkcol[:, q, sc:sc + 1], 0.0625, ALU.mult, ALU.mult)),
             reads=[L_["B_ld"], B_cols], writes=[H_["B_kw"]])
        if ob is not None:
            def fsd(e):
                e.matmul(pST, kT_t[:, 2 * h, :], qT_t[:, 2 * h, :], start=True, stop=False)
                e.matmul(pST, kT_t[:, 2 * h + 1, :], qT_t[:, 2 * h + 1, :], start=False, stop=True)
                e.matmul(pD, negsel[0:36, q, :], grow_t[0:36, :], start=True, stop=False)
                return e.matmul(pD, ident, maskf if dr == 0 else maskb, start=False, stop=True)
            S.op("pe", fsd, reads=[L_["B_ldo"], B_const], writes=[H_["B_pSD"]])
            S.op("act", (lambda e: e.activation(wT, pD, AF.Exp, bias=Rcol[:, ci, h:h + 1])),
                 reads=[H_["B_pSD"], B_cols], writes=[H_["B_wT"]])
            S.op("dve", (lambda e: e.scalar_tensor_tensor(STb, pST, 0.0625, wT, ALU.mult, ALU.mult)),
                 reads=[H_["B_pSD"], H_["B_wT"]], writes=[H_["B_ST"]])

    def emit_BC(it):
        dr, sc, cb, ob, h, q, ci, L_, H_ = ctx_of(it)
        vaug, qT_t, hs_t = L_["vaug"], L_["qT"], L_["hs"]
        B_ld, B_ldo, B_hs = L_["B_ld"], L_["B_ldo"], L_["B_hs"]
        STb, P2sb, hn, dd, kw = H_["STb"], H_["P2sb"], H_["hn"], H_["dd"], H_["kw"]
        pP2, pP1, pCU = H_["pP2"], H_["pP1"], H_["pCU"]
        for c in range(2):
            mm(pCU[c], [(kw[:, c * 128:(c + 1) * 128], vaug[:, h, 0:257])], reads=[H_["B_kw"], B_ld], writes=[H_["B_pCU"][c]])
        if ob is not None:
            mm(pP1, [(qT_t[:, 2 * h + c, :], C16[:, q, c, 0:257]) for c in range(2)], reads=[B_ldo] + B_C16[q], writes=[H_["B_pP1"]])
        for c in range(2):
            S.op("dve", (lambda e, c=c: e.scalar_tensor_tensor(C32[:, q, c, :], C32[:, q, c, :], decay[:, q, sc:sc + 1], pCU[c], ALU.mult, ALU.add)),
                 reads=[H_["B_pCU"][c], B_cols], writes=[B_C32[q][c]])
            S.op("act", (lambda e, c=c: e.copy(C16[:, q, c, 0:257], C32[:, q, c, :])), reads=[B_C32[q][c]], writes=[B_C16[q][c]])
        if ob is not None:
            mm(pP2, [(STb, vaug[:, h, 0:257])], reads=[H_["B_ST"], B_ld], writes=[H_["B_pP2"]])
            S.op("act", (lambda e: e.copy(P2sb, pP2)), reads=[H_["B_pP2"]], writes=[H_["B_P2"]])
            S.op("dve", (lambda e: e.scalar_tensor_tensor(hn, pP1, acol[:, q, sc:sc + 1], P2sb, ALU.mult, ALU.add)),
                 reads=[H_["B_pP1"], H_["B_P2"], B_cols], writes=[H_["B_hn"]])
            S.op("dve", (lambda e: e.scalar_tensor_tensor(dd[:, 0:1], hn[:, 256:257], -1.0, hn[:, 256:257], ALU.mult, ALU.max)),
                 reads=[H_["B_hn"]], writes=[H_["B_dd"]])
            S.op("dve", (lambda e: e.tensor_tensor(dd[:, 0:1], dd[:, 0:1], Ecol[:, ci, h:h + 1], ALU.max)),
                 reads=[H_["B_dd"], B_cols], writes=[H_["B_dd"]])
            S.op("dve", (lambda e: e.reciprocal(dd[:, 1:2], dd[:, 0:1])), reads=[H_["B_dd"]], writes=[H_["B_dd"]])
            if dr == 0:
                S.op("act", (lambda e: e.activation(hs_t[:, h * 256:(h + 1) * 256], hn[:, 0:256], AF.Identity, scale=dd[:, 1:2])),
                     reads=[H_["B_hn"], H_["B_dd"]], writes=[B_hs])
            else:
                S.op("dve", (lambda e: e.scalar_tensor_tensor(hs_t[:, h * 256:(h + 1) * 256], hn[:, 0:256], dd[:, 1:2],
                                                              hs_t[:, h * 256:(h + 1) * 256], ALU.mult, ALU.add)),
                     reads=[H_["B_hn"], H_["B_dd"]], writes=[B_hs])
            if h == 3:
                S.dma("sp", (lambda e: e.dma_start(out=HS[ob:ob + 128, :], in_=hs_t)), reads=[B_hs], writes=[B_HS[ob // 128]])

    emit_loads(0); emit_loads(1)
    for n in range(len(items) + 1):
        if n < len(items):
            si, h, _ = items[n]
            if h == 0:
                emit_load_hs(si)
            if h == 1 and si + 2 < len(steps):
                emit_loads(si + 2)
            emit_A(items[n])
        if n >= 1:
            emit_BC(items[n - 1])
    S.barrier()
    if g_["stage"] < 4:
        return
    build_rest3(g_, locals())


def build_rest(env):
    nc = env["nc"]
    S = env["S"]; ps = env["ps"]; Bps = env["Bps"]; arena = env["arena"]
    u2 = env["u2"]; B_u2 = env["B_u2"]; vcol = env["vcol"]; DER = env["DER"]
    ident = env["ident"]; maskf = env["maskf"]; maskb = env["maskb"]; CS = env["CS"]; M2 = env["M2"]; M3 = env["M3"]
    id16 = env["id16_t"]; ones16 = env["ones16_t"]
    B_const = env["B_const"]; B_der = env["B_der"]
    stage = env["stage"]; dbg = env["dbg"]; dbg_tensor = env["dbg_tensor"]
    dscr = env["dscr"]
    H1, AB, PQ, KT, QT, KTOK, VTOK = [env[n] for n in "H1 AB PQ KT QT KTOK VTOK".split()]
    B_H1, B_AB, B_PQ, B_KT, B_QT, B_KTOK, B_VTOK = [env["B_" + n] for n in "H1 AB PQ KT QT KTOK VTOK".split()]
    GROW = dscr("GROW", [36, NT], F32)
    B_GROW = Buf("GROW")
    HS = dscr("HS", [OWN, D], F32)
    B_HS = [Buf("HS%d" % i) for i in range(16)]

    def mm(out, pairs, reads, writes):
        def fn(e):
            n = len(pairs)
            for i, (l, r) in enumerate(pairs):
                ins = e.matmul(out, l, r, start=(i == 0), stop=(i == n - 1))
            return ins
        return S.op("pe", fn, reads=reads, writes=writes)

    NCH = 52
    Rcol = arena.f32(NCH * 4).rearrange("p (c h) -> p c h", h=4)
    Gcol = arena.f32(NCH * 4).rearrange("p (c h) -> p c h", h=4)
    Ecol = arena.f32(NCH * 4).rearrange("p (c h) -> p c h", h=4)
    gend = arena.f32(8 * 35).rearrange("p (q c) -> p q c", q=8)
    acol = arena.f32(8 * 34).rearrange("p (q c) -> p q c", q=8)
    wkcol = arena.f32(8 * 34).rearrange("p (q c) -> p q c", q=8)
    decay = arena.f32(8 * 34).rearrange("p (q c) -> p q c", q=8)
    B_cols = Buf("cols")
    yfT = arena.bf16(4 * OWN).rearrange("p (g t) -> p g t", g=4)
    B_yfT = Buf("yfT")
    C32 = arena.f32(8 * 2 * 257).rearrange("p (q c e) -> p q c e", q=8, c=2)
    C16 = arena.bf16(8 * 2 * 258).rearrange("p (q c e) -> p q c e", q=8, c=2)
    B_C = [Buf("C%d" % i) for i in range(8)]
    sel_t = arena.f32(2048)
    S.dma("sp", lambda e: e.dma_start(out=sel_t[0:36, :], in_=env["selc"]), writes=[B_const])
    sel = sel_t[0:36, 0:1024].rearrange("r (p m) -> r p m", p=8)
    negsel = sel_t[0:36, 1024:2048].rearrange("r (p m) -> r p m", p=8)
    mB = arena.mark()

    aLI = arena.f32(NT)
    aLF = arena.f32(NT)
    aB = arena.f32(NT)
    aG = arena.f32(NT)
    ones_r = arena.f32(512)
    tmpE = arena.f32(512)
    wgf_s = arena.bf16(8 * 8).rearrange("p (k n) -> p k n", k=8)
    wgb_s = arena.bf16(8 * 72).rearrange("p (k n) -> p k n", k=8)
    bgs = arena.f32(4)
    B_rows = Buf("rows")
    B_wg = Buf("wg")
    B_tmpE = Buf("tmpE")
    for a in (aLI, aLF, aB, aG):
        S.op("pool", (lambda e, a=a: e.memset(a, 0.0)), writes=[B_rows])
    S.op("pool", lambda e: e.memset(ones_r, 1.0), writes=[B_wg])
    S.op("pool", lambda e: e.memset(gend[:, :, :], 0.0), writes=[B_cols])
    for q in range(8):
        S.op("pool", (lambda e, q=q: e.memset(C32[:, q, :, :], 0.0)), writes=[B_C[q]])
        S.op("pool", (lambda e, q=q: e.memset(C16[:, q, :, :], 0.0)), writes=[B_C[q]])
    with nc.allow_non_contiguous_dma(reason="tiny gate weights"):
        pass
    wgate_f = env["wgate_f"]; wgate_b = env["wgate_b"]; bg = env["bg"]
    S.dma("pool", lambda e: e.dma_start(out=wgf_s, in_=wgate_f.rearrange("(k p) n -> p k n", p=128)), writes=[B_wg])
    S.dma("pool", lambda e: e.dma_start(out=wgb_s, in_=wgate_b.rearrange("(k p) n -> p k n", p=128)), writes=[B_wg])
    S.dma("sp", lambda e: e.dma_start(out=bgs[0:36, 0:2], in_=bg), writes=[B_wg])
    S.op("dve", lambda e: e.tensor_scalar(bgs[0:36, 2:3], bgs[0:36, 1:2], -1.0, None, ALU.mult), reads=[B_wg], writes=[B_wg])

    def gate_tile(jc0, W, rhs_fn, r0, r1, wl, wl_lf, pbank, rev=False):
        def pv(bank):
            return ps[bank][r0:r1, W - 1::-1] if rev else ps[bank][r0:r1, 0:W]
        mm(ps[pbank][0:r1, 0:W], [(wl(k), rhs_fn(k)) for k in range(8)], reads=[B_wg] + B_u2, writes=[Bps[pbank]])
        S.op("act", lambda e: e.activation(aLI[r0:r1, jc0:jc0 + W], pv(pbank), AF.Identity, bias=bgs[r0:r1, 0:1]),
             reads=[Bps[pbank], B_wg], writes=[B_rows])
        mm(ps[pbank + 1][0:r1, 0:W], [(wl_lf(k), rhs_fn(k)) for k in range(8)], reads=[B_wg] + B_u2, writes=[Bps[pbank + 1]])
        S.op("act", lambda e: e.activation(tmpE[r0:r1, 0:W], pv(pbank + 1), AF.Exp, bias=bgs[r0:r1, 2:3], scale=-1.0),
             reads=[Bps[pbank + 1], B_wg], writes=[B_tmpE])
        S.op("act", lambda e: e.activation(tmpE[r0:r1, 0:W], tmpE[r0:r1, 0:W], AF.Ln, bias=1.0), reads=[B_tmpE], writes=[B_tmpE])
        S.op("dve", lambda e: e.tensor_scalar(aLF[r0:r1, jc0:jc0 + W], tmpE[r0:r1, 0:W], -1.0, None, ALU.mult),
             reads=[B_tmpE], writes=[B_rows])

    ti = 0
    for (jc0, W) in [(0, 256)] + [(256 + 512 * i, 512) for i in range(4)]:
        gate_tile(jc0, W, (lambda k, jc0=jc0, W=W: u2[:, k, jc0:jc0 + W]), 0, 4,
                  (lambda k: wgf_s[:, k, 0:4]), (lambda k: wgf_s[:, k, 4:8]), 2 * (ti % 2))
        ti += 1
    def rev_u2(k, hi, W):
        return u2[:, k, hi - W + 1:hi + 1]
    gate_tile(0, 256, (lambda k: rev_u2(k, 255, 256)), 32, 36,
              (lambda k: wgb_s[:, k, 0:36]), (lambda k: wgb_s[:, k, 36:72]), 2 * (ti % 2), rev=True)
    ti += 1
    for i in range(8):
        jc0 = 256 + 512 * i
        hi = 4607 - jc0
        gate_tile(jc0, 512, (lambda k, hi=hi: rev_u2(k, hi, 512)), 32, 36,
                  (lambda k: wgb_s[:, k, 0:36]), (lambda k: wgb_s[:, k, 36:72]), 2 * (ti % 2), rev=True)
        ti += 1
    pieces = [(0, 256)] + [(256 + 512 * i, 512) for i in range(8)]
    for pi, (c0, W) in enumerate(pieces):
        init = 0.0 if pi == 0 else aB[0:36, c0 - 1:c0]
        S.op("dve", (lambda e, c0=c0, W=W, init=init: e.tensor_tensor_scan(aB[0:36, c0:c0 + W], ones_r[0:36, 0:W], aLF[0:36, c0:c0 + W],
                                                                           init, ALU.mult, ALU.add)),
             reads=[B_rows, B_wg], writes=[B_rows])
    S.op("dve", lambda e: e.tensor_tensor(aLI[0:36, :], aLI[0:36, :], aB[0:36, :], ALU.subtract), reads=[B_rows], writes=[B_rows])
    for pi, (c0, W) in enumerate(pieces):
        init = 0.0 if pi == 0 else aG[0:36, c0 - 1:c0]
        S.op("dve", (lambda e, c0=c0, W=W, init=init: e.tensor_tensor_scan(aG[0:36, c0:c0 + W], ones_r[0:36, 0:W], aLI[0:36, c0:c0 + W],
                                                                           init, ALU.mult, ALU.max)),
             reads=[B_rows, B_wg], writes=[B_rows])
    S.op("dve", lambda e: e.tensor_tensor(aB[0:36, :], aB[0:36, :], aG[0:36, :], ALU.add), reads=[B_rows], writes=[B_rows])
    if dbg:
        d_rows = dbg_tensor("d_rows", [3, 36, NT])
        for i, a in enumerate((aLI, aG, aB)):
            S.dma("sp", (lambda e, i=i, a=a: e.dma_start(out=d_rows[i], in_=a[0:36, :])), reads=[B_rows])
    def n0_of(sc):
        return 128 if sc == 0 else (0 if sc == 1 else 4480 - 128 * sc)
    for ai, (arr, bank) in enumerate(((aLI, 0), (aB, 2), (aG, 1))):
        S.op("dve", (lambda e, arr=arr: e.tensor_copy(aLF[32:36, 0:256], arr[32:36, 255::-1])), reads=[B_rows], writes=[B_rows])
        S.op("dve", (lambda e, arr=arr: e.tensor_copy(aLF[32:36, 256:NT], arr[32:36, NT - 1:255:-1])), reads=[B_rows], writes=[B_rows])
        def fn(e, arr=arr, bank=bank):
            for sc in range(18):
                ins = e.matmul(ps[bank][:, sc * 4:(sc + 1) * 4], arr[0:4, sc * 128:(sc + 1) * 128], ident[0:4, 0:4],
                               start=True, stop=True)
            for sc in range(34):
                n0 = n0_of(sc)
                ins = e.matmul(ps[bank][:, (18 + sc) * 4:(19 + sc) * 4], aLF[32:36, n0:n0 + 128],
                               ident[32:36, 32:36], start=True, stop=True)
            return ins
        S.op("pe", fn, reads=[B_rows, B_const], writes=[Bps[bank]])
    S.op("dve", lambda e: e.tensor_copy(aLF[0:4, :], aG[0:4, :]), reads=[B_rows], writes=[B_rows])
    S.dma("sp", lambda e: e.dma_start(out=GROW, in_=aLF[0:36, :]), reads=[B_rows], writes=[B_GROW])
    S.op("act", lambda e: e.copy(Rcol[:, :, :], ps[0][:, 0:NCH * 4].rearrange("p (c h) -> p c h", h=4)), reads=[Bps[0]], writes=[B_cols])
    S.op("act", lambda e: e.copy(Gcol[:, :, :], ps[1][:, 0:NCH * 4].rearrange("p (c h) -> p c h", h=4)), reads=[Bps[1]], writes=[B_cols])
    S.op("act", lambda e: e.activation(Ecol[:, :, :], ps[2][:, 0:NCH * 4].rearrange("p (c h) -> p c h", h=4), AF.Exp, scale=-1.0),
         reads=[Bps[2]], writes=[B_cols])
    def fn(e):
        for q in range(8):
            n = 18 if q < 4 else 34
            ins = e.matmul(ps[3][:, q * 34:q * 34 + n], sel[0:36, q, :], aG[0:36, 127:127 + 128 * (n - 1) + 1:128], start=True, stop=True)
        return ins
    S.op("pe", fn, reads=[B_rows, B_const], writes=[Bps[3]])
    for q in range(8):
        n = 18 if q < 4 else 34
        S.op("act", (lambda e, q=q, n=n: e.copy(gend[:, q, 1:1 + n], ps[3][:, q * 34:q * 34 + n])), reads=[Bps[3]], writes=[B_cols])
    tq = arena.f32(34)
    B_tq = Buf("tq")
    for q in range(8):
        n = 18 if q < 4 else 34
        base = 0 if q < 4 else 18
        h = q % 4
        S.op("dve", (lambda e, q=q, n=n, base=base, h=h: e.tensor_tensor(tq[:, 0:n], gend[:, q, 0:n], Gcol[:, base:base + n, h], ALU.subtract)),
             reads=[B_cols], writes=[B_tq])
        S.op("act", (lambda e, q=q, n=n: e.activation(acol[:, q, 0:n], tq[:, 0:n], AF.Exp)), reads=[B_tq], writes=[B_cols])
        S.op("dve", (lambda e, q=q, n=n, base=base, h=h: e.tensor_tensor(tq[:, 0:n], Rcol[:, base:base + n, h], gend[:, q, 1:1 + n], ALU.subtract)),
             reads=[B_cols], writes=[B_tq])
        S.op("act", (lambda e, q=q, n=n: e.activation(wkcol[:, q, 0:n], tq[:, 0:n], AF.Exp)), reads=[B_tq], writes=[B_cols])
        S.op("dve", (lambda e, q=q, n=n: e.tensor_tensor(tq[:, 0:n], gend[:, q, 0:n], gend[:, q, 1:1 + n], ALU.subtract)),
             reads=[B_cols], writes=[B_tq])
        S.op("act", (lambda e, q=q, n=n: e.activation(decay[:, q, 0:n], tq[:, 0:n], AF.Exp)), reads=[B_tq], writes=[B_cols])
    S.barrier()
    arena.reset(mB)
    if stage < 3:
        return
    build_rest2(env, locals())


COL_F = 0
COL_Q = 512
COL_K = 1536
COL_V = 2560
COL_O = 3584
COL_GATES = 4608
COL_BR = 4624


def _dft_consts(flip):
    idx = (63 - np.arange(64)) if flip else np.arange(64)
    ang = 2 * np.pi * np.outer(idx, idx) / 64.0
    Cc = np.cos(ang) / 8.0
    Sc = np.sin(ang) / 8.0
    ch = np.arange(128)
    angc = 2 * np.pi * np.outer(ch, ch) / 128.0
    CS = np.concatenate([np.cos(angc), np.sin(angc)], axis=1) / np.sqrt(128.0)
    M2 = np.zeros((128, 128))
    M2[0:64, 0:64] = Cc
    M2[64:128, 0:64] = -Sc
    M2[0:64, 64:128] = Sc
    M2[64:128, 64:128] = Cc
    M3 = np.zeros((128, 32))
    M3[0:64, :] = Cc[:, 0:32]
    M3[64:128, :] = -Sc[:, 0:32]
    return CS, M2, M3


def make_inputs(inp):
    f32 = np.float32
    x = np.asarray(inp["x"], f32)
    ctx = np.asarray(inp["ctx"], f32)
    c = np.asarray(inp["c"], f32)
    c_ctx = np.asarray(inp["c_ctx"], f32)
    w_in = np.asarray(inp["w_in"], f32)[0]
    b_in = np.asarray(inp["b_in"], f32)[0]
    conv_w = np.asarray(inp["conv_w"], f32)[0]
    conv_b = np.asarray(inp["conv_b"], f32)[0]
    norm_g = np.asarray(inp["norm_g"], f32)[0]

    def fm(v):
        return np.ascontiguousarray(v.reshape(-1, 128).T)

    def r13(w):
        return np.ascontiguousarray(w.reshape(8, 128, 2, NJ, 128).transpose(3, 1, 0, 2, 4).reshape(NJ, 128, 2048))

    def r2(w):
        return np.ascontiguousarray(w.reshape(NJ, 128, 8, 128).transpose(2, 1, 0, 3).reshape(8, 128, FF))

    shared = {
        "w_ada": np.ascontiguousarray(np.asarray(inp["w_ada"], f32)[0]),
        "w13a": r13(np.asarray(inp["w13_a"], f32)[0]), "w2a": r2(np.asarray(inp["w2_a"], f32)[0]),
        "w13b": r13(np.asarray(inp["w13_b"], f32)[0]), "w2b": r2(np.asarray(inp["w2_b"], f32)[0]),
        "wF": np.ascontiguousarray(w_in[:, COL_F:COL_Q]), "wq": np.ascontiguousarray(w_in[:, COL_Q:COL_K]),
        "wk": np.ascontiguousarray(w_in[:, COL_K:COL_V]), "wv": np.ascontiguousarray(w_in[:, COL_V:COL_O]),
        "wo": np.ascontiguousarray(w_in[:, COL_O:COL_GATES]),
        "wgf": np.ascontiguousarray(w_in[:, COL_BR:COL_BR + D]), "wgm": np.ascontiguousarray(w_in[:, COL_BR + D:]),
        "w_four": np.ascontiguousarray(np.asarray(inp["w_four"], f32)[0]),
        "w_mproj": np.ascontiguousarray(np.asarray(inp["w_mproj"], f32)[0]),
        "w_out": np.ascontiguousarray(np.asarray(inp["w_out"], f32)[0]),
        "rowv": np.concatenate([b_in[COL_V:COL_O], b_in[COL_O:COL_GATES], np.asarray(inp["head_g"], f32)[0]])[None, :].copy(),
    }
    sel = np.zeros((36, 8, 128), f32)
    for p in range(8):
        row = (p % 4) + (32 if p >= 4 else 0)
        sel[row, p, :] = 1.0
    selc = np.concatenate([sel.reshape(36, -1), -sel.reshape(36, -1)], axis=1)
    s_idx = np.arange(128)[:, None]
    t_idx = np.arange(128)[None, :]
    maskf = np.where(s_idx <= t_idx, 0.0, NEG).astype(f32)
    maskb = np.where(s_idx >= t_idx, 0.0, NEG).astype(f32)
    maps = []
    for core in range(8):
        b, half = core // 2, core % 2
        flip = half == 1
        xb = x[b][::-1] if flip else x[b]
        cb_ = ctx[b][::-1] if flip else ctx[b]
        g = COL_GATES
        if flip:
            gi_f, gf_f, gi_b, gf_b = g + 8, g + 12, g + 0, g + 4
            cw = conv_w[::-1]
        else:
            gi_f, gf_f, gi_b, gf_b = g + 0, g + 4, g + 8, g + 12
            cw = conv_w
        wgate_f = np.concatenate([w_in[:, gi_f:gi_f + 4], w_in[:, gf_f:gf_f + 4]], axis=1)
        wgate_b = np.zeros((D, 72), f32)
        wgate_b[:, 32:36] = w_in[:, gi_b:gi_b + 4]
        wgate_b[:, 36 + 32:36 + 36] = w_in[:, gf_b:gf_b + 4]
        bgv = np.zeros((36, 2), f32)
        bgv[0:4, 0] = b_in[gi_f:gi_f + 4]
        bgv[0:4, 1] = b_in[gf_f:gf_f + 4]
        bgv[32:36, 0] = b_in[gi_b:gi_b + 4]
        bgv[32:36, 1] = b_in[gf_b:gf_b + 4]
        vecs = np.zeros((128, NV), f32)

        def put(name, arr):
            o, w = VEC[name]
            assert arr.shape == (128, w), (name, arr.shape)
            vecs[:, o:o + w] = arr
        put("bada", fm(np.asarray(inp["b_ada"], f32)[0]))
        put("ng", np.concatenate([fm(norm_g[i]) for i in range(6)], axis=1))
        put("bF", fm(b_in[COL_F:COL_Q])); put("bq", fm(b_in[COL_Q:COL_K])); put("bk", fm(b_in[COL_K:COL_V]))
        put("cwq", np.concatenate([fm(cw[t, 0:D]) for t in range(3)], axis=1))
        put("cwk", np.concatenate([fm(cw[t, D:2 * D]) for t in range(3)], axis=1))
        put("cbq", fm(conv_b[0:D])); put("cbk", fm(conv_b[D:2 * D]))
        put("bgf", fm(b_in[COL_BR:COL_BR + D])); put("bgm", fm(b_in[COL_BR + D:]))
        cvec = np.zeros((128, 8, 2), f32)
        cvec[:, :, 0] = fm(c[b])
        cvec[:, :, 1] = fm(c_ctx)
        CS, M2, M3 = _dft_consts(flip)
        cst = np.zeros((128, 128 * 4 + 256 + 128 + 32), f32)
        cst[:, 0:128] = np.eye(128)
        cst[:, 128:256] = maskf
        cst[:, 256:384] = maskb
        cst[:, 512:768] = CS
        cst[:, 768:896] = M2
        cst[:, 896:928] = M3
        m = dict(shared)
        m.update({
            "xT": np.ascontiguousarray(xb.T), "ctxT": np.ascontiguousarray(cb_.T),
            "cvec": cvec.reshape(128, 16), "vecs": vecs, "bg": bgv,
            "wgate_f": np.ascontiguousarray(wgate_f), "wgate_b": wgate_b, "cst": cst, "selc": selc,
        })
        maps.append(m)
    return maps


def kernel(**inputs):
    nc, _ = build()
    maps = make_inputs(inputs)
    res = run_bass_kernel_spmd(nc, maps, core_ids=list(range(8)))
    out = np.zeros((4, SEQ, D), np.float32)
    for core in range(8):
        b, half = core // 2, core % 2
        o = np.asarray(res.results[core]["outT"]).T
        if half == 0:
            out[b, 0:OWN] = o
        else:
            out[b, OWN:] = o[::-1]
    return out
```

```python
import numpy as np
import os as _os
from contextlib import ExitStack
import concourse.bass as bass
import concourse.mybir as mybir
from concourse.bass_utils import run_bass_kernel_spmd

F32 = mybir.dt.float32
BF16 = mybir.dt.bfloat16
AF = mybir.ActivationFunctionType
ALU = mybir.AluOpType

ENGS = ("pe", "act", "dve", "pool", "sp")
EPOCH = 30000

D = 1024
SEQ = 4096
CTX = 256
NT = CTX + SEQ
OWN = 2048
FF = 2816
NJ = 22
EPS = 1e-6
NEG = -30000.0


class Tick:
    __slots__ = ("sem", "val", "know")

    def __init__(self, sem, val, know):
        self.sem = sem
        self.val = val
        self.know = know


class Buf:
    __slots__ = ("name", "w", "r")

    def __init__(self, name=""):
        self.name = name
        self.w = None
        self.r = {}


class Sched:
    def __init__(self, nc, stack, n_dma_sems=48):
        self.nc = nc
        self.stack = stack
        self.q = {e: [] for e in ENGS}
        self.cnt = {e: 0 for e in ENGS}
        self.esems = {e: [] for e in ENGS}
        self.known = {e: {} for e in ENGS}
        self.dsems = [stack.enter_context(nc.semaphore(f"dma{i}")) for i in range(n_dma_sems)]
        self.dcnt = [0] * n_dma_sems
        self.dlast = [None] * n_dma_sems
        self.drr = 0
        self.drr2 = {}

    def _esem(self, eng, idx):
        lst = self.esems[eng]
        while len(lst) <= idx:
            lst.append(self.stack.enter_context(self.nc.semaphore(f"e_{eng}_{len(lst)}")))
        return lst[idx]

    def _collect(self, eng, reads, writes, extra=()):
        kn = self.known[eng]
        waits = {}

        def need(t):
            if t is None:
                return
            if kn.get(t.sem, 0) >= t.val:
                return
            if waits.get(t.sem, (0, None))[0] < t.val:
                waits[t.sem] = (t.val, t)

        for b in reads:
            need(b.w)
        for b in writes:
            need(b.w)
            for t in b.r.values():
                need(t)
        for t in extra:
            need(t)
        items = sorted(waits.items(), key=lambda kv: -kv[1][0])
        final = []
        for sem, (val, t) in items:
            if kn.get(sem, 0) >= val:
                continue
            final.append((sem, val))
            kn[sem] = val
            for s2, v2 in t.know.items():
                if kn.get(s2, 0) < v2:
                    kn[s2] = v2
        return final

    def op(self, eng, fn, reads=(), writes=(), extra=()):
        waits = self._collect(eng, reads, writes, extra)
        c = self.cnt[eng]
        sem = self._esem(eng, c // EPOCH)
        val = c % EPOCH + 1
        self.cnt[eng] = c + 1
        t = Tick(sem, val, dict(self.known[eng]))
        for b in reads:
            b.r[sem] = t
        for b in writes:
            b.w = t
            b.r = {}
        self.q[eng].append((waits, fn, sem, 1))
        return t

    def dma(self, eng, fn, reads=(), writes=(), extra=()):
        n = len(self.dsems)
        lo, hi = (0, n // 3) if eng == "pool" else (n // 3, n)
        rr = self.drr2.get(eng, lo)
        i = rr
        self.drr2[eng] = lo + (rr + 1 - lo) % (hi - lo)
        ex = list(extra)
        if self.dlast[i] is not None:
            ex.append(self.dlast[i])
        waits = self._collect(eng, reads, writes, ex)
        self.dcnt[i] += 1
        sem = self.dsems[i]
        t = Tick(sem, 16 * self.dcnt[i], dict(self.known[eng]))
        self.dlast[i] = t
        for b in reads:
            b.r[sem] = t
        for b in writes:
            b.w = t
            b.r = {}
        self.q[eng].append((waits, fn, sem, 16))
        return t

    def wait_all(self, eng, ticks):
        waits = self._collect(eng, (), (), ticks)
        self.q[eng].append((waits, None, None, 0))

    def barrier(self):
        ticks = []
        for e in ENGS:
            c = self.cnt[e]
            if c > 0:
                ticks.append(Tick(self._esem(e, (c - 1) // EPOCH), (c - 1) % EPOCH + 1, {}))
        for t in self.dlast:
            if t is not None:
                ticks.append(t)
        for e in ENGS:
            self.wait_all(e, ticks)

    def emit(self):
        nc = self.nc
        q = self.q

        def run(engobj, lst):
            for waits, fn, sem, amt in lst:
                for s, v in waits:
                    engobj.wait_ge(s, v)
                if fn is not None:
                    ins = fn(engobj)
                    ins.then_inc(sem, amt)

        with nc.Block() as block:
            @block.tensor
            def _(e):
                run(e, q["pe"])

            @block.scalar
            def _(e):
                run(e, q["act"])

            @block.vector
            def _(e):
                run(e, q["dve"])

            @block.gpsimd
            def _(e):
                run(e, q["pool"])

            @block.sync
            def _(e):
                run(e, q["sp"])


class Arena:
    def __init__(self, nc, name, words):
        self.t = nc.alloc_sbuf_tensor(name, [128, words], F32)
        self.words = words
        self.off = 0

    def mark(self):
        return self.off

    def reset(self, m):
        self.off = m

    def f32(self, n):
        a = self.t[:, self.off:self.off + n]
        self.off += n
        assert self.off <= self.words, ("arena overflow", self.off, self.words)
        return a

    def bf16(self, n):
        w = (n + 1) // 2
        a = self.t[:, self.off:self.off + w].bitcast(BF16)
        self.off += w
        assert self.off <= self.words, ("arena overflow", self.off, self.words)
        return a[:, 0:n]


VEC = {}
_o = 0
for _n, _w in [("bada", 72), ("ng", 48), ("bF", 4), ("bq", 8), ("bk", 8), ("cwq", 24), ("cwk", 24),
               ("cbq", 8), ("cbk", 8), ("bgf", 8), ("bgm", 8)]:
    VEC[_n] = (_o, _w)
    _o += _w
NV = _o


def build(stage=99, dbg=False):
    nc = bass.Bass("TRN2", target_bir_lowering=False)
    dt_in = lambda name, shape, dt=F32: nc.dram_tensor(name, shape, dt, kind="ExternalInput").ap()
    xT = dt_in("xT", [D, SEQ])
    ctxT = dt_in("ctxT", [D, CTX])
    cvec = dt_in("cvec", [128, 16])
    w_ada = dt_in("w_ada", [D, 9 * D])
    vecs = dt_in("vecs", [128, NV])
    rowv = dt_in("rowv", [1, 3 * D])
    bg = dt_in("bg", [36, 2])
    w13a = dt_in("w13a", [NJ, 128, 2048])
    w2a = dt_in("w2a", [8, 128, FF])
    w13b = dt_in("w13b", [NJ, 128, 2048])
    w2b = dt_in("w2b", [8, 128, FF])
    wF = dt_in("wF", [D, 512])
    wq = dt_in("wq", [D, D])
    wk = dt_in("wk", [D, D])
    wv = dt_in("wv", [D, D])
    wo = dt_in("wo", [D, D])
    wgf = dt_in("wgf", [D, D])
    wgm = dt_in("wgm", [D, D])
    wgate_f = dt_in("wgate_f", [D, 8])
    wgate_b = dt_in("wgate_b", [D, 72])
    w_four = dt_in("w_four", [512, D])
    w_mproj = dt_in("w_mproj", [D, D])
    w_out = dt_in("w_out", [D, D])
    cst = dt_in("cst", [128, 128 * 4 + 256 + 128 + 32])
    selc = dt_in("selc", [36, 2 * 8 * 128])
    outT = nc.dram_tensor("outT", [D, OWN], F32, kind="ExternalOutput").ap()
    dbg_out = {}

    def dbg_tensor(name, shape, dt=F32):
        dbg_out[name] = nc.dram_tensor(name, shape, dt, kind="ExternalOutput").ap()
        return dbg_out[name]

    dscr = lambda name, shape, dt: nc.dram_tensor(name, shape, dt, kind="Internal").ap()
    H1 = dscr("H1", [D, OWN], F32)
    AB = dscr("AB", [2, SEQ, 512], F32)
    PQ = dscr("PQ", [2, 64, 64, 512], F32)
    KT = dscr("KT", [D, OWN], BF16)
    QT = dscr("QT", [D, OWN], BF16)
    KTOK = dscr("KTOK", [NT, D], BF16)
    VTOK = dscr("VTOK", [NT, D], BF16)
    B_H1, B_AB, B_PQ, B_KT, B_QT, B_KTOK, B_VTOK = [Buf(n) for n in "H1 AB PQ KT QT KTOK VTOK".split()]

    st = ExitStack()
    with st:
        S = Sched(nc, st)
        ps = [st.enter_context(nc.psum_tensor(f"ps{i}", [128, 512], F32)) for i in range(8)]
        Bps = [Buf(f"ps{i}") for i in range(8)]

        cs_t = nc.alloc_sbuf_tensor("cs", [128, 128 * 4 + 256 + 128 + 32], F32)
        ident = cs_t[:, 0:128]
        maskf = cs_t[:, 128:256]
        maskb = cs_t[:, 256:384]
        CS = cs_t[:, 512:768]
        M2 = cs_t[:, 768:896]
        M3 = cs_t[:, 896:928]
        vec_t = nc.alloc_sbuf_tensor("vec", [128, NV], F32)
        mod_t = nc.alloc_sbuf_tensor("mod", [128, 72 * 2], F32)
        der_t = nc.alloc_sbuf_tensor("der", [128, 16 * 8], F32)
        id16_t = nc.alloc_sbuf_tensor("id16", [128, 128], BF16)
        ones16_t = nc.alloc_sbuf_tensor("ones16", [128, 128], BF16)
        u2_t = nc.alloc_sbuf_tensor("u2", [128, 8 * NT], BF16)
        u2 = u2_t[:, :].rearrange("p (k t) -> p k t", k=8)
        B_const = Buf("const")
        B_mod = Buf("mod")
        B_der = Buf("der")
        tiles = [(0, CTX, 1)] + [(CTX + 512 * i, 512, 0) for i in range(8)]
        B_u2 = [Buf(f"u2_{i}") for i in range(9)]

        def vcol(name, i=0, n=1):
            o, w = VEC[name]
            return vec_t[:, o + i:o + i + n]

        DER = {}
        _d = 0
        for nm in ["A0l", "A0c", "S0l", "S0c", "PAl", "PAc", "A2l", "A2c", "S2l", "S2c", "PMl", "A4l", "S4l", "PBl"]:
            DER[nm] = der_t[:, _d * 8:(_d + 1) * 8]
            _d += 1

        arena = Arena(nc, "arena", 34000)

        S.dma("sp", lambda e: e.dma_start(out=cs_t[:, :], in_=cst), writes=[B_const])
        S.dma("sp", lambda e: e.dma_start(out=vec_t[:, :], in_=vecs), writes=[B_const])
        S.op("act", lambda e: e.copy(id16_t[:, :], ident), reads=[B_const], writes=[B_const])
        S.op("pool", lambda e: e.memset(ones16_t[:, :], 1.0), writes=[B_const])

        m0 = arena.mark()
        cv = arena.f32(16)
        scv = arena.f32(16)
        B_cv = Buf("cv")
        S.dma("sp", lambda e: e.dma_start(out=cv, in_=cvec), writes=[B_cv])
        S.op("act", lambda e: e.activation(scv, cv, AF.Silu), reads=[B_cv], writes=[B_cv])
        wad = [arena.f32(8 * 1024) for _ in range(2)]
        B_wad = [Buf("wad0"), Buf("wad1")]
        w_ada_v = w_ada.rearrange("(k p) n -> p k n", p=128)
        modps = ps[7][:, 0:144]
        for mi in range(9):
            sl = mi % 2
            wv_ = wad[sl].rearrange("p (k n) -> p k n", k=8)
            S.dma("sp", (lambda e, wv_=wv_, mi=mi: e.dma_start(out=wv_, in_=w_ada_v[:, :, mi * 1024:(mi + 1) * 1024])),
                  writes=[B_wad[sl]])
            for dc in range(8):
                def fn(e, wv_=wv_, mi=mi, dc=dc):
                    for k in range(8):
                        ins = e.matmul(modps[:, (mi * 8 + dc) * 2:(mi * 8 + dc) * 2 + 2],
                                       wv_[:, k, dc * 128:(dc + 1) * 128],
                                       scv[:, k * 2:k * 2 + 2], start=(k == 0), stop=(k == 7))
                    return ins
                S.op("pe", fn, reads=[B_wad[sl], B_cv], writes=[Bps[7]])
        modv = mod_t[:, :].rearrange("p (m j) -> p m j", j=2)
        modpsv = modps.rearrange("p (m j) -> p m j", j=2)
        bada = vcol("bada", 0, 72)
        for j in range(2):
            S.op("dve", (lambda e, j=j: e.tensor_tensor(modv[:, :, j], modpsv[:, :, j], bada, ALU.add)),
                 reads=[Bps[7], B_const], writes=[B_mod])

        def modc(mi, j):
            return modv[:, mi * 8:(mi + 1) * 8, j]

        def ng(i):
            return vcol("ng", i * 8, 8)

        def der_scale(name, mi, gi, j):
            S.op("dve", lambda e: e.scalar_tensor_tensor(DER[name], modc(mi, j), 1.0, ng(gi), ALU.add, ALU.mult),
                 reads=[B_mod, B_const], writes=[B_der])

        def der_gate(name, mi, gi, j, f):
            S.op("dve", lambda e: e.scalar_tensor_tensor(DER[name], modc(mi, j), f, ng(gi), ALU.mult, ALU.mult),
                 reads=[B_mod, B_const], writes=[B_der])

        def der_copy(name, mi, j):
            S.op("dve", lambda e: e.tensor_copy(DER[name], modc(mi, j)), reads=[B_mod], writes=[B_der])

        der_scale("A0l", 1, 0, 0); der_scale("A0c", 1, 0, 1)
        der_copy("S0l", 0, 0); der_copy("S0c", 0, 1)
        der_gate("PAl", 2, 1, 0, 0.5); der_gate("PAc", 2, 1, 1, 0.5)
        der_scale("A2l", 4, 2, 0); der_scale("A2c", 4, 2, 1)
        der_copy("S2l", 3, 0); der_copy("S2c", 3, 1)
        der_gate("PMl", 5, 3, 0, 1.0)
        der_scale("A4l", 7, 4, 0); der_copy("S4l", 6, 0); der_gate("PBl", 8, 5, 0, 0.5)
        S.barrier()
        arena.reset(m0)

        def rstd_from(sq_tile, B_sq, W, rstd, B_rstd, pbank):
            def fn(e):
                for k in range(8):
                    ins = e.matmul(ps[pbank][:, 0:W], ones16_t[:, :], sq_tile[:, k, 0:W], start=(k == 0), stop=(k == 7))
                return ins
            S.op("pe", fn, reads=[B_sq, B_const], writes=[Bps[pbank]])
            S.op("act", lambda e: e.activation(rstd[:, 0:W], ps[pbank][:, 0:W], AF.Ln, bias=EPS, scale=1.0 / D),
                 reads=[Bps[pbank]], writes=[B_rstd])
            S.op("act", lambda e: e.activation(rstd[:, 0:W], rstd[:, 0:W], AF.Exp, scale=-0.5),
                 reads=[B_rstd], writes=[B_rstd])

        def norm_mod_thunks(src, B_src, W, rstd, B_rstd, A, Sh, dst_fn, B_dst, tmp, B_tmp):
            def one(k):
                t = tmp[k % 2]
                bt = B_tmp[k % 2]
                S.op("dve", (lambda e: e.tensor_tensor(t[:, 0:W], src[:, k, 0:W], rstd[:, 0:W], ALU.mult)),
                     reads=[B_src, B_rstd], writes=[bt])
                S.op("act", (lambda e: e.activation(dst_fn(k), t[:, 0:W], AF.Identity, bias=Sh[:, k:k + 1], scale=A[:, k:k + 1])),
                     reads=[bt, B_der], writes=[B_dst])
            return [(lambda k=k: one(k)) for k in range(8)]

        def norm_mod(src, B_src, W, rstd, B_rstd, A, Sh, dst_fn, B_dst, tmp, B_tmp):
            for k in range(8):
                t = tmp[k % 2]
                bt = B_tmp[k % 2]
                S.op("dve", (lambda e, k=k, t=t: e.tensor_tensor(t[:, 0:W], src[:, k, 0:W], rstd[:, 0:W], ALU.mult)),
                     reads=[B_src, B_rstd], writes=[bt])
                S.op("act", (lambda e, k=k, t=t: e.activation(dst_fn(k), t[:, 0:W], AF.Identity,
                                                              bias=Sh[:, k:k + 1], scale=A[:, k:k + 1])),
                     reads=[bt, B_der], writes=[B_dst])

        WC = {}

        def wload(dst, B_dst, src_f32, cache, key, ncols):
            if cache is None:
                S.dma("pool", lambda e: e.dma_start(out=dst, in_=src_f32), writes=[B_dst])
                return
            k = (cache,) + key
            if k not in WC:
                sc_ = nc.dram_tensor("wc_" + "_".join(str(z) for z in k), [128, ncols], BF16, kind="Internal").ap()
                WC[k] = (sc_, Buf("wc"))
                S.dma("pool", lambda e: e.dma_start(out=dst, in_=src_f32), writes=[B_dst])
                dflat = dst if len(dst.shape) == 2 else dst.rearrange("p a b -> p (a b)")
                S.dma("sp", lambda e: e.dma_start(out=sc_, in_=dflat), reads=[B_dst], writes=[WC[k][1]])
            else:
                sc_, bsc = WC[k]
                dflat = dst if len(dst.shape) == 2 else dst.rearrange("p a b -> p (a b)")
                S.dma("sp", lambda e: e.dma_start(out=dflat, in_=sc_), reads=[bsc], writes=[B_dst])

        def ffn(src, B_src, W, rstd_pre, B_rstd_pre, A, Sh, PG, w13r, w2r, bufs, cache, do_pre=True, hook1=None, hook2=None, defer_epi=False):
            (sq, B_sq, u, B_u, g, B_g, y, B_y, tmp, B_tmp, rs2, B_rs2, wb13, B_wb13, wb2, B_wb2, sa, B_sa) = bufs
            if do_pre:
                norm_mod(src, B_src, W, rstd_pre, B_rstd_pre, A, Sh, lambda k: u[:, k, 0:W], B_u, tmp, B_tmp)
            n13 = len(wb13)
            for j in range(NJ):
                sl = j % n13
                wbv = wb13[sl].rearrange("p (k c) -> p k c", k=8)
                wload(wb13[sl], B_wb13[sl], w13r[j], cache, ("w13", j), 2048)
                pa = j % 2
                pb = 2 + j % 2

                def fa(e, wbv=wbv, pa=pa):
                    for k in range(8):
                        ins = e.matmul(ps[pa][:, 0:W], wbv[:, k, 0:128], u[:, k, 0:W], start=(k == 0), stop=(k == 7))
                    return ins

                def fb(e, wbv=wbv, pb=pb):
                    for k in range(8):
                        ins = e.matmul(ps[pb][:, 0:W], wbv[:, k, 128:256], u[:, k, 0:W], start=(k == 0), stop=(k == 7))
                    return ins
                S.op("pe", fa, reads=[B_wb13[sl], B_u], writes=[Bps[pa]])
                S.op("pe", fb, reads=[B_wb13[sl], B_u], writes=[Bps[pb]])
                s2 = j % 2
                S.op("act", (lambda e, pa=pa, s2=s2: e.activation(sa[s2][:, 0:W], ps[pa][:, 0:W], AF.Silu)),
                     reads=[Bps[pa]], writes=[B_sa[s2]])
                S.op("dve", (lambda e, pb=pb, s2=s2, j=j: e.tensor_tensor(g[:, j, 0:W], sa[s2][:, 0:W], ps[pb][:, 0:W], ALU.mult)),
                     reads=[B_sa[s2], Bps[pb]], writes=[B_g])
                if hook1 is not None:
                    hook1(j)
            n2 = len(wb2)
            HJ = NJ // 2
            for i in range(8):
                halves = []
                for hf in range(2):
                    sl = (2 * i + hf) % n2
                    wload(wb2[sl], B_wb2[sl], w2r[i][:, hf * HJ * 128:(hf + 1) * HJ * 128], cache, ("w2", i, hf), HJ * 128)
                    halves.append((wb2[sl].rearrange("p (j c) -> p j c", j=HJ), B_wb2[sl]))
                py = 4 + i % 2

                def fy(e, halves=halves, py=py):
                    for j in range(NJ):
                        wv_ = halves[j // HJ][0]
                        ins = e.matmul(ps[py][:, 0:W], wv_[:, j % HJ, :], g[:, j, 0:W], start=(j == 0), stop=(j == NJ - 1))
                    return ins
                S.op("pe", fy, reads=[halves[0][1], halves[1][1], B_g], writes=[Bps[py]])
                S.op("act", (lambda e, py=py, i=i: e.copy(y[:, i, 0:W], ps[py][:, 0:W])), reads=[Bps[py]], writes=[B_y])
                S.op("act", (lambda e, py=py, i=i: e.activation(sq[:, i, 0:W], ps[py][:, 0:W], AF.Square)),
                     reads=[Bps[py]], writes=[B_sq])
                if hook2 is not None:
                    hook2(i)
            rstd_from(sq, B_sq, W, rs2, B_rs2, 7)

            def resid(i):
                t = tmp[i % 2]
                bt = B_tmp[i % 2]
                S.op("dve", (lambda e: e.tensor_tensor(t[:, 0:W], y[:, i, 0:W], rs2[:, 0:W], ALU.mult)),
                     reads=[B_y, B_rs2], writes=[bt])
                S.op("dve", (lambda e: e.scalar_tensor_tensor(src[:, i, 0:W], t[:, 0:W], PG[:, i:i + 1],
                                                              src[:, i, 0:W], ALU.mult, ALU.add)),
                     reads=[bt, B_der], writes=[B_src])
            thunks = [(lambda i=i: resid(i)) for i in range(8)]
            if defer_epi:
                return thunks
            for th in thunks:
                th()
            return []

        def alloc_ffn_bufs():
            sq = arena.bf16(8 * 512).rearrange("p (k t) -> p k t", k=8)
            u = arena.bf16(8 * 512).rearrange("p (k t) -> p k t", k=8)
            g = arena.bf16(NJ * 512).rearrange("p (k t) -> p k t", k=NJ)
            y = arena.f32(8 * 512).rearrange("p (k t) -> p k t", k=8)
            tmp = [arena.f32(512) for _ in range(2)]
            rs2 = arena.f32(512)
            wb13 = [arena.bf16(2048) for _ in range(3)]
            wb2 = [arena.bf16(FF // 2) for _ in range(4)]
            sa = [arena.f32(512) for _ in range(2)]
            return (sq, Buf("sq"), u, Buf("u"), g, Buf("g"), y, Buf("y"), tmp, [Buf("t0"), Buf("t1")],
                    rs2, Buf("rs2"), wb13, [Buf("wb13_%d" % i) for i in range(3)], wb2, [Buf("wb2_%d" % i) for i in range(4)],
                    sa, [Buf("sa0"), Buf("sa1")])

        mA = arena.mark()
        xt = [arena.f32(8 * 512).rearrange("p (k t) -> p k t", k=8) for _ in range(2)]
        B_xt = [Buf("xt0"), Buf("xt1")]
        rs1 = arena.f32(512)
        B_rs1 = Buf("rs1")
        fb = alloc_ffn_bufs()
        tmp, B_tmp, u_, B_u_ = fb[8], fb[9], fb[2], fb[3]
        sqx = arena.bf16(8 * 512).rearrange("p (k t) -> p k t", k=8)
        B_sqx = Buf("sqx")
        xT_v = xT.rearrange("(k p) t -> p k t", p=128)
        ctxT_v = ctxT.rearrange("(k p) t -> p k t", p=128)
        if dbg:
            d_u2 = dbg_tensor("d_u2", [128, 8 * NT], BF16)
        ntile = len(tiles) if stage >= 1 else 0

        def a1_load(ti):
            c0, W, j = tiles[ti]
            x = xt[ti % 2]
            src = ctxT_v[:, :, 0:W] if j == 1 else xT_v[:, :, c0 - CTX:c0 - CTX + W]
            S.dma("pool", lambda e: e.dma_start(out=x[:, :, 0:W], in_=src), writes=[B_xt[ti % 2]])

        def sq_thunks(x, bx, W):
            return [(lambda k=k: S.op("act", (lambda e: e.activation(sqx[:, k, 0:W], x[:, k, 0:W], AF.Square)), reads=[bx], writes=[B_sqx]))
                    for k in range(8)]

        def a1_pre_thunks(ti):
            c0, W, j = tiles[ti]
            x = xt[ti % 2]; bx = B_xt[ti % 2]
            sfx = "c" if j == 1 else "l"
            th = sq_thunks(x, bx, W)
            th.append(lambda: rstd_from(sqx, B_sqx, W, rs1, B_rs1, 6))
            th += norm_mod_thunks(x, bx, W, rs1, B_rs1, DER["A0" + sfx], DER["S0" + sfx], lambda k: u_[:, k, 0:W], B_u_, tmp, B_tmp)
            return th

        def a1_epi2_thunks(ti):
            c0, W, j = tiles[ti]
            x = xt[ti % 2]; bx = B_xt[ti % 2]
            sfx = "c" if j == 1 else "l"
            th = []
            if 1 <= ti <= 4:
                o0 = c0 - CTX
                th.append(lambda: S.dma("pool", lambda e: e.dma_start(out=H1.rearrange("(k p) t -> p k t", p=128)[:, :, o0:o0 + 512], in_=x[:, :, :]),
                                        reads=[bx], writes=[B_H1]))
            th += sq_thunks(x, bx, W)
            th.append(lambda: rstd_from(sqx, B_sqx, W, rs1, B_rs1, 6))
            th += norm_mod_thunks(x, bx, W, rs1, B_rs1, DER["A2" + sfx], DER["S2" + sfx], (lambda k: u2[:, k, c0:c0 + W]), B_u2[ti], tmp, B_tmp)
            return th

        pend1 = []
        pend2 = []
        if ntile:
            a1_load(0)
            for th in a1_pre_thunks(0):
                th()
            if ntile > 1:
                a1_load(1)
        for ti in range(ntile):
            c0, W, j = tiles[ti]
            sfx = "c" if j == 1 else "l"
            if ti + 1 < ntile:
                pend2 = a1_pre_thunks(ti + 1)

            def hook1(j_):
                n = 2 if len(pend1) > (NJ - 1 - j_) else 1
                for _ in range(n):
                    if pend1:
                        pend1.pop(0)()

            def hook2(i_):
                while pend1:
                    pend1.pop(0)()
                if i_ >= 1:
                    for _ in range(3):
                        if pend2:
                            pend2.pop(0)()
            epi1 = ffn(xt[ti % 2], B_xt[ti % 2], W, None, None, None, None, DER["PA" + sfx], w13a, w2a, fb, "A",
                       do_pre=False, hook1=hook1, hook2=hook2, defer_epi=True)
            while pend1:
                pend1.pop(0)()
            while pend2:
                pend2.pop(0)()
            pend1 = list(epi1) + a1_epi2_thunks(ti)
            if ti + 2 < ntile:
                pend1.append(lambda ti=ti: a1_load(ti + 2))
        while pend1:
            pend1.pop(0)()
        S.barrier()
        arena.reset(mA)
        if dbg:
            S.dma("sp", lambda e: e.dma_start(out=d_u2, in_=u2_t[:, :]), reads=B_u2)
            d_h1 = dbg_tensor("d_h1", [D, OWN])
            S.dma("sp", lambda e: e.dma_start(out=d_h1, in_=H1), reads=[B_H1])

        env = dict(locals())
        if stage >= 2:
            build_rest(env)
        S.barrier()
        S.emit()
    return nc, dbg_out


def build_rest3(g_, L):
    AX = mybir.AxisListType
    nc = g_["nc"]; S = g_["S"]; ps = g_["ps"]; Bps = g_["Bps"]; arena = g_["arena"]
    u2 = g_["u2"]; B_u2 = g_["B_u2"]; vcol = g_["vcol"]; DER = g_["DER"]
    id16 = g_["id16"]; B_const = g_["B_const"]; mm = g_["mm"]
    H1, HS = g_["H1"], g_["HS"]; B_H1 = g_["B_H1"]; B_HS = g_["B_HS"]
    yfT, B_yfT, mB = g_["yfT"], g_["B_yfT"], g_["mB"]
    rowv = g_["rowv"]; outT = g_["outT"]
    ffn = g_["ffn"]; rstd_from = g_["rstd_from"]; wload = g_["wload"]
    kp = lambda ap: ap.rearrange("(k p) n -> p k n", p=128)
    A = arena.t
    arena.reset(mB)
    tmp = [A[:, 0:512], A[:, 512:1024]]; B_tmp = [Buf("ct0"), Buf("ct1")]
    rs2 = A[:, 1024:1536]; B_rs2 = Buf("crs2")
    yreg = A[:, 5816:9912]
    B_yreg = Buf("yreg")
    y = yreg.rearrange("p (k t) -> p k t", k=8)
    hs_t = yreg[:, 0:1024]; o_sb = yreg[:, 1024:2048]; sig = yreg[:, 2048:3072]; sqh = yreg[:, 3072:4096]
    B_hsC = Buf("c_hs"); B_osb = Buf("c_osb"); B_sig = Buf("c_sig"); B_sqh = Buf("c_sqh")
    bo_bc = A[:, 9912:10936]; hg_bc = A[:, 10936:11960]
    B_bc = Buf("cbc")
    x = arena.f32(4096).rearrange("p (k t) -> p k t", k=8); B_x = Buf("cx")
    rs1 = arena.f32(512); B_rs1 = Buf("crs1")
    sq = arena.bf16(4096).rearrange("p (k t) -> p k t", k=8); B_sq = Buf("csq")
    u = arena.bf16(4096).rearrange("p (k t) -> p k t", k=8); B_u = Buf("cu")
    hmT = sq; yT = u
    sa = [arena.f32(512), arena.f32(512)]; B_sa = [Buf("csa0"), Buf("csa1")]
    mX = arena.mark()
    g = arena.bf16(NJ * 512).rearrange("p (k t) -> p k t", k=NJ); B_g = Buf("cg")
    wb13m = [arena.bf16(2048), arena.bf16(2048), arena.bf16(2048)]; B_wb13m = [Buf("cw13_0"), Buf("cw13_1"), Buf("cw13_2")]
    wb2m = arena.bf16(FF); B_wb2m = Buf("cw2")
    wb2x = A[:, 11992:11992 + 704].bitcast(BF16), A[:, 11992 + 704:11992 + 1408].bitcast(BF16)
    arena.reset(mX)
    wsl = [arena.bf16(8 * 512).rearrange("p (k n) -> p k n", k=8) for _ in range(4)]; B_wsl = [Buf("wsl%d" % i) for i in range(4)]
    hm = arena.bf16(1024); B_hm = Buf("hm")
    ol = A[:, mX + 4096:mX + 8192].rearrange("p (k t) -> p k t", k=8)
    B_ol = [B_wsl[2], B_wsl[3]]
    ss = rs2[:, 0:8]
    fb = (sq, B_sq, u, B_u, g, B_g, y, B_yreg, tmp, B_tmp, rs2, B_rs2, wb13m, B_wb13m,
          [wb2m[:, 0:FF // 2], wb2m[:, FF // 2:FF], wb2x[0], wb2x[1]], [Buf('cw2a'), Buf('cw2b'), Buf('cw2c'), Buf('cw2d')], sa, B_sa)
    xflat = A[:, mB:mB + 4096]
    rv = xflat[:, 0:3072]; ones1 = xflat[:, 3072:3200]
    S.dma("sp", lambda e: e.dma_start(out=rv[0:1, :], in_=rowv), writes=[B_x])
    S.op("pool", lambda e: e.memset(ones1[0:1, :], 1.0), writes=[B_x])
    for (dst, seg) in ((bo_bc, 1), (hg_bc, 2)):
        for hh in range(2):
            mm(ps[6][:, 0:512], [(ones1[0:1, :], rv[0:1, seg * 1024 + hh * 512:seg * 1024 + hh * 512 + 512])], reads=[B_x], writes=[Bps[6]])
            S.op("act", (lambda e, hh=hh, dst=dst: e.copy(dst[:, hh * 512:(hh + 1) * 512], ps[6][:, 0:512])), reads=[Bps[6]], writes=[B_bc])
    S.barrier()
    w_four, w_mproj, w_out, wo, wgf, wgm = [g_[n] for n in "w_four w_mproj w_out wo wgf wgm".split()]
    w13b, w2b = g_["w13b"], g_["w2b"]
    H1v = H1.rearrange("(k p) t -> p k t", p=128)
    outv = outT.rearrange("(k p) t -> p k t", p=128)
    for T in range(4):
        t0 = 512 * T
        c0 = CTX + t0
        S.dma("sp", (lambda e, t0=t0: e.dma_start(out=x[:, :, :], in_=H1v[:, :, t0:t0 + 512])), reads=[B_H1], writes=[B_x])
        for hh in range(2):
            wload(wsl[hh], B_wsl[hh], kp(wo)[:, :, hh * 512:(hh + 1) * 512], "C", ("wo", hh), 4096)
        for ch in range(4):
            cc = c0 + 128 * ch
            ob = t0 + 128 * ch
            S.dma("sp", (lambda e, ob=ob: e.dma_start(out=hs_t, in_=HS[ob:ob + 128, :])), reads=[B_HS[ob // 128]], writes=[B_hsC])
            for hh in range(2):
                mm(ps[hh][:, 0:512], [(u2[:, k, cc:cc + 128], wsl[hh][:, k, :]) for k in range(8)], reads=[B_wsl[hh]] + B_u2, writes=[Bps[hh]])
                S.op("dve", (lambda e, hh=hh: e.tensor_tensor(o_sb[:, hh * 512:(hh + 1) * 512], ps[hh][:, 0:512], bo_bc[:, hh * 512:(hh + 1) * 512], ALU.add)),
                     reads=[Bps[hh], B_bc], writes=[B_osb])
            S.op("act", lambda e: e.activation(sig, o_sb, AF.Sigmoid), reads=[B_osb], writes=[B_sig])
            S.op("dve", lambda e: e.tensor_tensor(sig, sig, hg_bc, ALU.mult), reads=[B_sig, B_bc], writes=[B_sig])
            S.op("act", lambda e: e.activation(sqh, hs_t, AF.Square), reads=[B_hsC], writes=[B_sqh])
            S.op("dve", lambda e: e.reduce_sum(ss[:, 0:4], sqh.rearrange("p (h e) -> p h e", h=4), AX.X), reads=[B_sqh], writes=[B_rs2])
            S.op("act", lambda e: e.activation(ss[:, 0:4], ss[:, 0:4], AF.Ln, bias=EPS, scale=1.0 / 256), reads=[B_rs2], writes=[B_rs2])
            S.op("act", lambda e: e.activation(ss[:, 0:4], ss[:, 0:4], AF.Exp, scale=-0.5), reads=[B_rs2], writes=[B_rs2])
            for h in range(4):
                S.op("dve", (lambda e, h=h: e.scalar_tensor_tensor(hm[:, h * 256:(h + 1) * 256], hs_t[:, h * 256:(h + 1) * 256], ss[:, h:h + 1],
                                                                  sig[:, h * 256:(h + 1) * 256], ALU.mult, ALU.mult)),
                     reads=[B_hsC, B_sig, B_rs2], writes=[B_hm])
            for half in range(2):
                pb = 2 + half
                def fn(e, half=half, pb=pb):
                    for cq in range(4):
                        c = half * 4 + cq
                        ins = e.matmul(ps[pb][:, cq * 128:(cq + 1) * 128], hm[:, c * 128:(c + 1) * 128], id16[:, :], start=True, stop=True)
                    return ins
                S.op("pe", fn, reads=[B_hm, B_const], writes=[Bps[pb]])
                S.op("act", (lambda e, half=half, pb=pb, ch=ch: e.copy(hmT[:, half * 4:(half + 1) * 4, ch * 128:(ch + 1) * 128],
                                                                      ps[pb][:, 0:512].rearrange("p (c t) -> p c t", c=4))),
                     reads=[Bps[pb]], writes=[B_sq])
        for hh in range(2):
            cs_ = slice(hh * 512, (hh + 1) * 512)
            wload(wsl[0][:, 0:4, :], B_wsl[0], kp(w_four)[:, :, cs_], "C", ("w4", hh), 2048)
            wload(wsl[1], B_wsl[1], kp(w_mproj)[:, :, cs_], "C", ("wm", hh), 4096)
            wload(wsl[2], B_wsl[2], kp(wgf)[:, :, cs_], "C", ("wgf", hh), 4096)
            wload(wsl[3], B_wsl[3], kp(wgm)[:, :, cs_], "C", ("wgm", hh), 4096)
            for ii in range(4):
                i = hh * 4 + ii
                cw = slice(ii * 128, (ii + 1) * 128)
                mm(ps[0][:, 0:512], [(wsl[0][:, gq, cw], yfT[:, gq, t0:t0 + 512]) for gq in range(4)], reads=[B_wsl[0], B_yfT], writes=[Bps[0]])
                mm(ps[1][:, 0:512], [(wsl[1][:, k, cw], hmT[:, k, :]) for k in range(8)], reads=[B_wsl[1], B_sq], writes=[Bps[1]])
                mm(ps[2][:, 0:512], [(wsl[2][:, k, cw], u2[:, k, c0:c0 + 512]) for k in range(8)], reads=[B_wsl[2]] + B_u2, writes=[Bps[2]])
                mm(ps[3][:, 0:512], [(wsl[3][:, k, cw], u2[:, k, c0:c0 + 512]) for k in range(8)], reads=[B_wsl[3]] + B_u2, writes=[Bps[3]])
                S.op("act", (lambda e, i=i: e.activation(sa[0], ps[2][:, 0:512], AF.Sigmoid, bias=vcol("bgf", i, 1))), reads=[Bps[2], B_const], writes=[B_sa[0]])
                S.op("act", (lambda e, i=i: e.activation(sa[1], ps[3][:, 0:512], AF.Sigmoid, bias=vcol("bgm", i, 1))), reads=[Bps[3], B_const], writes=[B_sa[1]])
                S.op("dve", lambda e: e.tensor_tensor(sa[0], sa[0], ps[0][:, 0:512], ALU.mult), reads=[Bps[0], B_sa[0]], writes=[B_sa[0]])
                S.op("dve", lambda e: e.tensor_tensor(sa[1], sa[1], ps[1][:, 0:512], ALU.mult), reads=[Bps[1], B_sa[1]], writes=[B_sa[1]])
                S.op("dve", (lambda e, i=i: e.tensor_tensor(yT[:, i, :], sa[0], sa[1], ALU.add)), reads=B_sa, writes=[B_u])
        for hh in range(2):
            wload(wsl[hh], B_wsl[hh], kp(w_out)[:, :, hh * 512:(hh + 1) * 512], "C", ("wout", hh), 4096)
        for i in range(8):
            pb = 4 + i % 2
            mm(ps[pb][:, 0:512], [(wsl[i // 4][:, k, (i % 4) * 128:(i % 4 + 1) * 128], yT[:, k, :]) for k in range(8)],
               reads=[B_wsl[i // 4], B_u], writes=[Bps[pb]])
            S.op("act", (lambda e, i=i, pb=pb: e.copy(ol[:, i, :], ps[pb][:, 0:512])), reads=[Bps[pb]], writes=B_ol)
            S.op("act", (lambda e, i=i, pb=pb: e.activation(sq[:, i, :], ps[pb][:, 0:512], AF.Square)), reads=[Bps[pb]], writes=[B_sq])
        rstd_from(sq, B_sq, 512, rs1, B_rs1, 6)
        for i in range(8):
            t = tmp[i % 2]; bt = B_tmp[i % 2]
            S.op("dve", (lambda e, i=i, t=t: e.tensor_tensor(t, ol[:, i, :], rs1, ALU.mult)), reads=B_ol + [B_rs1], writes=[bt])
            S.op("dve", (lambda e, i=i, t=t: e.scalar_tensor_tensor(x[:, i, :], t, DER["PMl"][:, i:i + 1], x[:, i, :], ALU.mult, ALU.add)),
                 reads=[bt, g_["B_der"]], writes=[B_x])
        S.barrier()
        S.op("act", lambda e: e.activation(sq[:, :, :], x[:, :, :], AF.Square), reads=[B_x], writes=[B_sq])
        rstd_from(sq, B_sq, 512, rs1, B_rs1, 6)
        ffn(x, B_x, 512, rs1, B_rs1, DER["A4l"], DER["S4l"], DER["PBl"], w13b, w2b, fb, "B")
        S.dma("sp", (lambda e, t0=t0: e.dma_start(out=outv[:, :, t0:t0 + 512], in_=x[:, :, :])), reads=[B_x])
        S.barrier()


def build_rest2(env, L):
    AX = mybir.AxisListType
    g_ = dict(env); g_.update(L)
    nc = g_["nc"]; S = g_["S"]; ps = g_["ps"]; Bps = g_["Bps"]; arena = g_["arena"]
    u2 = g_["u2"]; B_u2 = g_["B_u2"]; vcol = g_["vcol"]; DER = g_["DER"]
    ident = g_["ident"]; maskf = g_["maskf"]; maskb = g_["maskb"]; CS = g_["CS"]; M2 = g_["M2"]; M3 = g_["M3"]
    id16 = g_["id16"]; ones16 = g_["ones16"]; B_const = g_["B_const"]; B_der = g_["B_der"]
    sel = g_["sel"]; negsel = g_["negsel"]; mm = g_["mm"]
    H1, AB, PQ, KT, QT, KTOK, VTOK, GROW, HS = [g_[n] for n in "H1 AB PQ KT QT KTOK VTOK GROW HS".split()]
    B_H1, B_AB, B_PQ, B_KT, B_QT, B_KTOK, B_VTOK, B_GROW = [g_["B_" + n] for n in "H1 AB PQ KT QT KTOK VTOK GROW".split()]
    B_HS = g_["B_HS"]
    Rcol, Gcol, Ecol, gend, acol, wkcol, decay, B_cols = [g_[n] for n in "Rcol Gcol Ecol gend acol wkcol decay B_cols".split()]
    yfT, B_yfT, C32, C16, B_C, mB = [g_[n] for n in "yfT B_yfT C32 C16 B_C mB".split()]
    rowv = g_["rowv"]; outT = g_["outT"]
    tiles = g_["tiles"]
    kp = lambda ap: ap.rearrange("(k p) n -> p k n", p=128)

    wbuf = [arena.bf16(8 * 1024).rearrange("p (k n) -> p k n", k=8) for _ in range(2)]
    B_wbuf = [Buf("wbuf0"), Buf("wbuf1")]
    xf = [arena.f32(512) for _ in range(4)]
    B_xf = [Buf("xf%d" % i) for i in range(4)]
    ab_sb2 = [arena.f32(1024).rearrange("p (x g c) -> p x g c", x=2, g=4) for _ in range(2)]
    B_ab2 = [Buf("ab_sb0"), Buf("ab_sb1")]
    Pt2 = [arena.f32(514), arena.f32(514)]
    B_Pt2 = [Buf("Pt0"), Buf("Pt1")]
    acc2 = [arena.f32(512), arena.f32(512)]
    B_acc2 = [Buf("acc0"), Buf("acc1")]
    kTt = arena.bf16(8 * 512).rearrange("p (k t) -> p k t", k=8)
    B_kTt = Buf("kTt")
    tok2 = [arena.bf16(1024), arena.bf16(1024)]
    B_tok2 = [Buf("tok0"), Buf("tok1")]
    tokctr = [0]
    bv_bc = arena.f32(1024)
    ones1 = arena.f32(128)
    rv = arena.f32(1024)
    B_bc = Buf("bc")
    S.dma("sp", lambda e: e.dma_start(out=rv[0:1, :], in_=rowv[:, 0:1024]), writes=[B_bc])
    S.op("pool", lambda e: e.memset(ones1[0:1, :], 1.0), writes=[B_bc])

    def bcast_row(dst, seg):
        for hh in range(2):
            mm(ps[6][:, 0:512], [(ones1[0:1, :], rv[0:1, seg * 1024 + hh * 512:seg * 1024 + hh * 512 + 512])], reads=[B_bc], writes=[Bps[6]])
            S.op("act", (lambda e, hh=hh: e.copy(dst[:, hh * 512:(hh + 1) * 512], ps[6][:, 0:512])), reads=[Bps[6]], writes=[B_bc])
    bcast_row(bv_bc, 0)

    wF = g_["wF"]
    S.dma("pool", lambda e: e.dma_start(out=wbuf[0][:, :, 0:512], in_=kp(wF)), writes=[B_wbuf[0]])
    for i in range(8):
        c0 = CTX + 512 * i
        for g in range(4):
            pb = g % 2
            mm(ps[pb][:, 0:512], [(wbuf[0][:, k, g * 128:(g + 1) * 128], u2[:, k, c0:c0 + 512]) for k in range(8)],
               reads=[B_wbuf[0]] + B_u2, writes=[Bps[pb]])
            S.op("act", (lambda e, g=g, pb=pb: e.activation(xf[g], ps[pb][:, 0:512], AF.Identity, bias=vcol("bF", g, 1))),
                 reads=[Bps[pb], B_const], writes=[B_xf[g]])
        for tb in range(4):
            ab_sb = ab_sb2[tb % 2]; B_ab = B_ab2[tb % 2]
            for g in range(4):
                bank = 2 + g // 2
                mm(ps[bank][:, (g % 2) * 256:(g % 2) * 256 + 256], [(xf[g][:, tb * 128:(tb + 1) * 128], CS)],
                   reads=[B_xf[g], B_const], writes=[Bps[bank]])
            for bi in range(2):
                S.op("dve", (lambda e, bi=bi, ab_sb=ab_sb: e.tensor_copy(ab_sb[:, :, 2 * bi:2 * bi + 2, :],
                                                            ps[2 + bi][:, 0:512].rearrange("p (g x c) -> p x g c", g=2, x=2))),
                     reads=[Bps[2 + bi]], writes=[B_ab])
            tok0 = 512 * i + 128 * tb
            S.dma("sp", (lambda e, tok0=tok0, ab_sb=ab_sb: e.dma_start(out=AB.rearrange("x t f -> t x f")[tok0:tok0 + 128, :, :],
                                                          in_=ab_sb.rearrange("p x g c -> p x (g c)"))), reads=[B_ab], writes=[B_AB])

    def qk_proj(wdram, bname, cwname, cbname, slot, tlist, is_k):
        S.dma("pool", lambda e: e.dma_start(out=wbuf[slot], in_=kp(wdram)), writes=[B_wbuf[slot]])
        for (ti, c0, W) in tlist:
            islat = ti >= 1
            left = islat and ti > 1
            right = islat and ti < 8
            for c in range(8):
                pb = c % 2
                Pt = Pt2[c % 2]; B_Pt = B_Pt2[c % 2]; acc = acc2[c % 2]; B_acc = B_acc2[c % 2]
                hb = 5 + c % 2
                mm(ps[pb][:, 0:W], [(wbuf[slot][:, k, c * 128:(c + 1) * 128], u2[:, k, c0:c0 + W]) for k in range(8)],
                   reads=[B_wbuf[slot]] + B_u2, writes=[Bps[pb]])
                S.op("act", (lambda e, c=c, pb=pb, W=W, Pt=Pt: e.activation(Pt[:, 1:W + 1], ps[pb][:, 0:W], AF.Identity, bias=vcol(bname, c, 1))),
                     reads=[Bps[pb], B_const], writes=[B_Pt])
                if left and right:
                    mm(ps[hb][:, 0:2], [(wbuf[slot][:, k, c * 128:(c + 1) * 128], u2[:, k, c0 - 1:c0 + W + 1:W + 1]) for k in range(8)],
                       reads=[B_wbuf[slot]] + B_u2, writes=[Bps[hb]])
                    S.op("act", (lambda e, c=c, Pt=Pt, hb=hb, W=W: e.activation(Pt[:, 0:W + 2:W + 1], ps[hb][:, 0:2], AF.Identity, bias=vcol(bname, c, 1))),
                         reads=[Bps[hb], B_const], writes=[B_Pt])
                else:
                    for hi_, (has, col, dstc) in enumerate(((left, c0 - 1, 0), (right, c0 + W, W + 1))):
                        if has:
                            mm(ps[hb][:, hi_:hi_ + 1], [(wbuf[slot][:, k, c * 128:(c + 1) * 128], u2[:, k, col:col + 1]) for k in range(8)],
                               reads=[B_wbuf[slot]] + B_u2, writes=[Bps[hb]])
                            S.op("act", (lambda e, c=c, dstc=dstc, Pt=Pt, hb=hb, hi_=hi_: e.activation(Pt[:, dstc:dstc + 1], ps[hb][:, hi_:hi_ + 1], AF.Identity, bias=vcol(bname, c, 1))),
                                 reads=[Bps[hb], B_const], writes=[B_Pt])
                        else:
                            S.op("pool", (lambda e, dstc=dstc, Pt=Pt: e.memset(Pt[:, dstc:dstc + 1], 0.0)), writes=[B_Pt])
                S.op("act", (lambda e, c=c, W=W, Pt=Pt, acc=acc: e.activation(acc[:, 0:W], Pt[:, 0:W], AF.Identity, scale=vcol(cwname, c, 1))),
                     reads=[B_Pt, B_const], writes=[B_acc])
                S.op("dve", (lambda e, c=c, W=W, Pt=Pt, acc=acc: e.scalar_tensor_tensor(acc[:, 0:W], Pt[:, 1:W + 1], vcol(cwname, 8 + c, 1), acc[:, 0:W], ALU.mult, ALU.add)),
                     reads=[B_Pt, B_const], writes=[B_acc])
                S.op("dve", (lambda e, c=c, W=W, Pt=Pt, acc=acc: e.scalar_tensor_tensor(acc[:, 0:W], Pt[:, 2:W + 2], vcol(cwname, 16 + c, 1), acc[:, 0:W], ALU.mult, ALU.add)),
                     reads=[B_Pt, B_const], writes=[B_acc])
                S.op("act", (lambda e, c=c, W=W, acc=acc: e.activation(kTt[:, c, 0:W], acc[:, 0:W], AF.Silu, bias=vcol(cbname, c, 1))),
                     reads=[B_acc, B_const], writes=[B_kTt])
            own = 1 <= ti <= 4
            if own:
                o0 = c0 - CTX
                dst = KT if is_k else QT
                S.dma("sp", (lambda e, o0=o0, dst=dst: e.dma_start(out=dst.rearrange("(k p) t -> p k t", p=128)[:, :, o0:o0 + 512], in_=kTt[:, :, :])),
                      reads=[B_kTt], writes=[B_KT if is_k else B_QT])
            if is_k:
                for tb in range(W // 128):
                    tok = tok2[tokctr[0] % 2]; B_tok = B_tok2[tokctr[0] % 2]; tokctr[0] += 1
                    for half in range(2):
                        pb = 3 + half
                        def fn(e, tb=tb, half=half, pb=pb):
                            for cc in range(4):
                                c = half * 4 + cc
                                ins = e.matmul(ps[pb][:, cc * 128:(cc + 1) * 128], kTt[:, c, tb * 128:(tb + 1) * 128], id16[:, :], start=True, stop=True)
                            return ins
                        S.op("pe", fn, reads=[B_kTt, B_const], writes=[Bps[pb]])
                        S.op("dve", (lambda e, half=half, pb=pb, tok=tok: e.tensor_copy(tok[:, half * 512:(half + 1) * 512], ps[pb][:, 0:512])),
                             reads=[Bps[pb]], writes=[B_tok])
                    r0 = c0 + tb * 128
                    S.dma("sp", (lambda e, r0=r0, tok=tok: e.dma_start(out=KTOK[r0:r0 + 128, :], in_=tok)), reads=[B_tok], writes=[B_KTOK])

    tl_all = [(ti, c0, W) for ti, (c0, W, j) in enumerate(tiles)]
    qk_proj(g_["wk"], "bk", "cwk", "cbk", 1, tl_all, True)
    qk_proj(g_["wq"], "bq", "cwq", "cbq", 0, tl_all[1:5], False)
    S.dma("pool", lambda e: e.dma_start(out=wbuf[1], in_=kp(g_["wv"])), writes=[B_wbuf[1]])
    for cb in range(0, NT, 128):
        tok = tok2[tokctr[0] % 2]; B_tok = B_tok2[tokctr[0] % 2]; tokctr[0] += 1
        for half in range(2):
            pb = half + 2 * ((cb // 128) % 2)
            mm(ps[pb][:, 0:512], [(u2[:, k, cb:cb + 128], wbuf[1][:, k, half * 512:(half + 1) * 512]) for k in range(8)],
               reads=[B_wbuf[1]] + B_u2, writes=[Bps[pb]])
            S.op("dve", (lambda e, half=half, pb=pb, tok=tok: e.tensor_tensor(tok[:, half * 512:(half + 1) * 512], ps[pb][:, 0:512],
                                                                     bv_bc[:, half * 512:(half + 1) * 512], ALU.add)),
                 reads=[Bps[pb], B_bc], writes=[B_tok])
        S.dma("sp", (lambda e, cb=cb, tok=tok: e.dma_start(out=VTOK[cb:cb + 128, :], in_=tok)), reads=[B_tok], writes=[B_VTOK])
    S.barrier()
    arena.reset(mB)

    inb2 = [arena.f32(8 * 512).rearrange("p (r f) -> p r f", r=8) for _ in range(2)]
    outb2 = [arena.f32(8 * 512).rearrange("p (r f) -> p r f", r=8) for _ in range(2)]
    B_inb2 = [Buf("inb0"), Buf("inb1")]; B_outb2 = [Buf("outb0"), Buf("outb1")]
    ABv = AB.rearrange("x (r c) f -> x c r f", c=64)
    PQw = PQ
    for rb in range(8):
        inb = inb2[rb % 2]; outb = outb2[rb % 2]; B_inb = B_inb2[rb % 2]; B_outb = B_outb2[rb % 2]
        for x in range(2):
            S.dma("sp", (lambda e, rb=rb, x=x, inb=inb: e.dma_start(out=inb[64 * x:64 * x + 64, :, :], in_=ABv[x, :, rb * 8:rb * 8 + 8, :])),
                  reads=[B_AB], writes=[B_inb])
        for r in range(8):
            pb = r % 4
            mm(ps[pb][:, 0:512], [(M2, inb[:, r, :])], reads=[B_inb, B_const], writes=[Bps[pb]])
            if r % 2 == 0:
                S.op("act", (lambda e, r=r, pb=pb, outb=outb: e.copy(outb[:, r, :], ps[pb][:, 0:512])), reads=[Bps[pb]], writes=[B_outb])
            else:
                S.op("dve", (lambda e, r=r, pb=pb, outb=outb: e.tensor_copy(outb[:, r, :], ps[pb][:, 0:512])), reads=[Bps[pb]], writes=[B_outb])
        for x in range(2):
            S.dma("sp", (lambda e, rb=rb, x=x, outb=outb: e.dma_start(out=PQw[x, :, rb * 8:rb * 8 + 8, :], in_=outb[64 * x:64 * x + 64, :, :])),
                  reads=[B_outb], writes=[B_PQ])
    PQr = PQ.rearrange("x kc r f -> x r kc f")
    for kb in range(8):
        inb = inb2[kb % 2]; B_inb = B_inb2[kb % 2]
        for x in range(2):
            S.dma("sp", (lambda e, kb=kb, x=x, inb=inb: e.dma_start(out=inb[64 * x:64 * x + 64, :, :], in_=PQr[x, :, kb * 8:kb * 8 + 8, :])),
                  reads=[B_PQ], writes=[B_inb])
        for g in range(4):
            pb = 4 + g
            def fn(e, g=g, pb=pb, inb=inb):
                for kc in range(8):
                    ins = e.matmul(ps[pb][:, kc * 32:(kc + 1) * 32], inb[:, kc, g * 128:(g + 1) * 128], M3, start=True, stop=True)
                return ins
            S.op("pe", fn, reads=[B_inb, B_const], writes=[Bps[pb]])
            S.op("act", (lambda e, g=g, pb=pb, kb=kb: e.copy(yfT[:, g, :].rearrange("p (kr kc) -> p kc kr", kc=64)[:, kb * 8:kb * 8 + 8, :],
                                                          ps[pb][:, 0:256].rearrange("p (kc kr) -> p kc kr", kr=32))),
                 reads=[Bps[pb]], writes=[B_yfT])
    S.barrier()
    arena.reset(mB)

    NLS = 3
    NHS = 2
    LD = []
    for i in range(2 * NLS):
        d_ = dict(ktok=arena.bf16(1024), vaug=arena.bf16(4 * 258).rearrange("p (h e) -> p h e", h=4),
                  kT=arena.bf16(1024).rearrange("p (k t) -> p k t", k=8), qT=arena.bf16(1024).rearrange("p (k t) -> p k t", k=8),
                  grow=arena.f32(128), B_ld=Buf("ld%d" % i), B_ldo=Buf("ldo%d" % i))
        S.op("pool", (lambda e, v=d_["vaug"]: e.memset(v[:, :, :], 1.0)), writes=[d_["B_ld"]])
        LD.append(d_)
    HSB = [dict(hs=arena.f32(1024), B_hs=Buf("hs%d" % i)) for i in range(4)]
    HT = []
    B_pCUs = [Buf("pCU0"), Buf("pCU1")]
    for i in range(NHS):
        HT.append(dict(wT=arena.f32(128), STb=arena.bf16(128), P2sb=arena.f32(257), hn=arena.f32(257), dd=arena.f32(2), kw=arena.bf16(256),
                       B_wT=Buf("wT%d" % i), B_ST=Buf("ST%d" % i), B_P2=Buf("P2sb%d" % i), B_hn=Buf("hn%d" % i), B_dd=Buf("dd%d" % i),
                       B_kw=Buf("kw%d" % i),
                       pST=ps[i][:, 0:128], pD=ps[i][:, 128:256], pP2=ps[2 + i][:, 0:257], pP1=ps[4 + i][:, 0:257],
                       pCU=[ps[6][:, 0:257], ps[7][:, 0:257]],
                       B_pSD=Buf("pSD%d" % i), B_pP2=Buf("pP2%d" % i), B_pP1=Buf("pP1%d" % i), B_pCU=B_pCUs))
    B_C32 = [[Buf('C32_%d_%d' % (q, c)) for c in range(2)] for q in range(8)]
    B_C16 = [[Buf('C16_%d_%d' % (q, c)) for c in range(2)] for q in range(8)]
    steps = []
    fw = [(0, 0, None), (1, 128, None)] + [(2 + i, CTX + 128 * i, 128 * i) for i in range(16)]
    bw = [(0, 128, None), (1, 0, None)] + [(2 + i, CTX + 128 * (31 - i), (128 * (31 - i) if 31 - i <= 15 else None)) for i in range(32)]
    for i in range(34):
        if i < 18:
            steps.append((0,) + fw[i])
        steps.append((1,) + bw[i])
    mask16 = [arena.bf16(128), arena.bf16(128)]
    S.op("act", lambda e: e.copy(mask16[0], maskf), reads=[B_const], writes=[B_const])
    S.op("act", lambda e: e.copy(mask16[1], maskb), reads=[B_const], writes=[B_const])

    def emit_loads(si):
        (dr, sc, cb, ob) = steps[si]
        L_ = LD[dr * NLS + sc % NLS]
        ktok_t, vaug, kT_t, qT_t, grow_t = L_["ktok"], L_["vaug"], L_["kT"], L_["qT"], L_["grow"]
        B_ld, B_ldo = L_["B_ld"], L_["B_ldo"]
        S.dma("sp", (lambda e: e.dma_start(out=ktok_t, in_=KTOK[cb:cb + 128, :])), reads=[B_KTOK], writes=[B_ld])
        S.dma("sp", (lambda e: e.dma_start(out=vaug[:, :, 0:256], in_=VTOK[cb:cb + 128, :].rearrange("t (h e) -> t h e", h=4))),
              reads=[B_VTOK], writes=[B_ld])
        if ob is not None:
            S.dma("sp", (lambda e: e.dma_start(out=kT_t, in_=KT.rearrange("(k p) t -> p k t", p=128)[:, :, ob:ob + 128])), reads=[B_KT], writes=[B_ldo])
            S.dma("sp", (lambda e: e.dma_start(out=qT_t, in_=QT.rearrange("(k p) t -> p k t", p=128)[:, :, ob:ob + 128])), reads=[B_QT], writes=[B_ldo])
            S.dma("sp", (lambda e: e.dma_start(out=grow_t[0:36, :], in_=GROW[:, cb:cb + 128])), reads=[B_GROW], writes=[B_ldo])

    def emit_load_hs(si):
        (dr, sc, cb, ob) = steps[si]
        if ob is not None and dr == 1:
            H2 = HSB[dr * 2 + sc % 2]
            S.dma("sp", (lambda e: e.dma_start(out=H2["hs"], in_=HS[ob:ob + 128, :])), reads=[B_HS[ob // 128]], writes=[H2["B_hs"]])

    items = []
    for si, (dr, sc, cb, ob) in enumerate(steps):
        for h in range(4):
            items.append((si, h, len(items)))

    def ctx_of(it):
        si, h, n = it
        (dr, sc, cb, ob) = steps[si]
        L2 = dict(LD[dr * NLS + sc % NLS]); L2.update(HSB[dr * 2 + sc % 2])
        return dr, sc, cb, ob, h, dr * 4 + h, (sc if dr == 0 else 18 + sc), L2, HT[n % NHS]

    def emit_A(it):
        dr, sc, cb, ob, h, q, ci, L_, H_ = ctx_of(it)
        kT_t, qT_t, grow_t, ktok_t = L_["kT"], L_["qT"], L_["grow"], L_["ktok"]
        wT, STb, kw, pST, pD = H_["wT"], H_["STb"], H_["kw"], H_["pST"], H_["pD"]
        S.op("pool", (lambda e: e.tensor_scalar(kw, ktok_t[:, h * 256:(h + 1) * 256], wkcol[:, q, sc:sc + 1], 0.0625, ALU.mult, ALU.mult)),
             reads=[L_["B_ld"], B_cols], writes=[H_["B_kw"]])
        if ob is not None:
            def fsd(e):
                e.matmul(pST, kT_t[:, 2 * h, :], qT_t[:, 2 * h, :], start=True, stop=False)
                e.matmul(pST, kT_t[:, 2 * h + 1, :], qT_t[:, 2 * h + 1, :], start=False, stop=True)
                e.matmul(pD, negsel[0:36, q, :], grow_t[0:36, :], start=True, stop=False)
                return e.matmul(pD, ident, maskf if dr == 0 else maskb, start=False, stop=True)
            S.op("pe", fsd, reads=[L_["B_ldo"], B_const], writes=[H_["B_pSD"]])
            S.op("act", (lambda e: e.activation(wT, pD, AF.Exp, bias=Rcol[:, ci, h:h + 1])),
                 reads=[H_["B_pSD"], B_cols], writes=[H_["B_wT"]])
            S.op("dve", (lambda e: e.scalar_tensor_tensor(STb, pST, 0.0625, wT, ALU.mult, ALU.mult)),
                 reads=[H_["B_pSD"], H_["B_wT"]], writes=[H_["B_ST"]])

    def emit_BC(it):
        dr, sc, cb, ob, h, q, ci, L_, H_ = ctx_of(it)
        vaug, qT_t, hs_t = L_["vaug"], L_["qT"], L_["hs"]
        B_ld, B_ldo, B_hs = L_["B_ld"], L_["B_ldo"], L_["B_hs"]
        STb, P2sb, hn, dd, kw = H_["STb"], H_["P2sb"], H_["hn"], H_["dd"], H_["kw"]
        pP2, pP1, pCU = H_["pP2"], H_["pP1"], H_["pCU"]
        for c in range(2):
            mm(pCU[c], [(kw[:, c * 128:(c + 1) * 128], vaug[:, h, 0:257])], reads=[H_["B_kw"], B_ld], writes=[H_["B_pCU"][c]])
        if ob is not None:
            mm(pP1, [(qT_t[:, 2 * h + c, :], C16[:, q, c, 0:257]) for c in range(2)], reads=[B_ldo] + B_C16[q], writes=[H_["B_pP1"]])
        for c in range(2):
            S.op("dve", (lambda e, c=c: e.scalar_tensor_tensor(C32[:, q, c, :], C32[:, q, c, :], decay[:, q, sc:sc + 1], pCU[c], ALU.mult, ALU.add)),
                 reads=[H_["B_pCU"][c], B_cols], writes=[B_C32[q][c]])
            S.op("act", (lambda e, c=c: e.copy(C16[:, q, c, 0:257], C32[:, q, c, :])), reads=[B_C32[q][c]], writes=[B_C16[q][c]])
        if ob is not None:
            mm(pP2, [(STb, vaug[:, h, 0:257])], reads=[H_["B_ST"], B_ld], writes=[H_["B_pP2"]])
            S.op("act", (lambda e: e.copy(P2sb, pP2)), reads=[H_["B_pP2"]], writes=[H_["B_P2"]])
            S.op("dve", (lambda e: e.scalar_tensor_tensor(hn, pP1, acol[:, q, sc:sc + 1], P2sb, ALU.mult, ALU.add)),
                 reads=[H_["B_pP1"], H_["B_P2"], B_cols], writes=[H_["B_hn"]])
            S.op("dve", (lambda e: e.scalar_tensor_tensor(dd[:, 0:1], hn[:, 256:257], -1.0, hn[:, 256:257], ALU.mult, ALU.max)),
                 reads=[H_["B_hn"]], writes=[H_["B_dd"]])
            S.op("dve", (lambda e: e.tensor_tensor(dd[:, 0:1], dd[:, 0:1], Ecol[:, ci, h:h + 1], ALU.max)),
                 reads=[H_["B_dd"], B_cols], writes=[H_["B_dd"]])
            S.op("dve", (lambda e: e.reciprocal(dd[:, 1:2], dd[:, 0:1])), reads=[H_["B_dd"]], writes=[H_["B_dd"]])
            if dr == 0:
                S.op("act", (lambda e: e.activation(hs_t[:, h * 256:(h + 1) * 256], hn[:, 0:256], AF.Identity, scale=dd[:, 1:2])),
                     reads=[H_["B_hn"], H_["B_dd"]], writes=[B_hs])
            else:
                S.op("dve", (lambda e: e.scalar_tensor_tensor(hs_t[:, h * 256:(h + 1) * 256], hn[:, 0:256], dd[:, 1:2],
                                                              hs_t[:, h * 256:(h + 1) * 256], ALU.mult, ALU.add)),
                     reads=[H_["B_hn"], H_["B_dd"]], writes=[B_hs])
            if h == 3:
                S.dma("sp", (lambda e: e.dma_start(out=HS[ob:ob + 128, :], in_=hs_t)), reads=[B_hs], writes=[B_HS[ob // 128]])

    emit_loads(0); emit_loads(1)
    for n in range(len(items) + 1):
        if n < len(items):
            si, h, _ = items[n]
            if h == 0:
                emit_load_hs(si)
            if h == 1 and si + 2 < len(steps):
                emit_loads(si + 2)
            emit_A(items[n])
        if n >= 1:
            emit_BC(items[n - 1])
    S.barrier()
    if g_["stage"] < 4:
        return
    build_rest3(g_, locals())


def build_rest(env):
    nc = env["nc"]
    S = env["S"]; ps = env["ps"]; Bps = env["Bps"]; arena = env["arena"]
    u2 = env["u2"]; B_u2 = env["B_u2"]; vcol = env["vcol"]; DER = env["DER"]
    ident = env["ident"]; maskf = env["maskf"]; maskb = env["maskb"]; CS = env["CS"]; M2 = env["M2"]; M3 = env["M3"]
    id16 = env["id16_t"]; ones16 = env["ones16_t"]
    B_const = env["B_const"]; B_der = env["B_der"]
    stage = env["stage"]; dbg = env["dbg"]; dbg_tensor = env["dbg_tensor"]
    dscr = env["dscr"]
    H1, AB, PQ, KT, QT, KTOK, VTOK = [env[n] for n in "H1 AB PQ KT QT KTOK VTOK".split()]
    B_H1, B_AB, B_PQ, B_KT, B_QT, B_KTOK, B_VTOK = [env["B_" + n] for n in "H1 AB PQ KT QT KTOK VTOK".split()]
    GROW = dscr("GROW", [36, NT], F32)
    B_GROW = Buf("GROW")
    HS = dscr("HS", [OWN, D], F32)
    B_HS = [Buf("HS%d" % i) for i in range(16)]

    def mm(out, pairs, reads, writes):
        def fn(e):
            n = len(pairs)
            for i, (l, r) in enumerate(pairs):
                ins = e.matmul(out, l, r, start=(i == 0), stop=(i == n - 1))
            return ins
        return S.op("pe", fn, reads=reads, writes=writes)

    NCH = 52
    Rcol = arena.f32(NCH * 4).rearrange("p (c h) -> p c h", h=4)
    Gcol = arena.f32(NCH * 4).rearrange("p (c h) -> p c h", h=4)
    Ecol = arena.f32(NCH * 4).rearrange("p (c h) -> p c h", h=4)
    gend = arena.f32(8 * 35).rearrange("p (q c) -> p q c", q=8)
    acol = arena.f32(8 * 34).rearrange("p (q c) -> p q c", q=8)
    wkcol = arena.f32(8 * 34).rearrange("p (q c) -> p q c", q=8)
    decay = arena.f32(8 * 34).rearrange("p (q c) -> p q c", q=8)
    B_cols = Buf("cols")
    yfT = arena.bf16(4 * OWN).rearrange("p (g t) -> p g t", g=4)
    B_yfT = Buf("yfT")
    C32 = arena.f32(8 * 2 * 257).rearrange("p (q c e) -> p q c e", q=8, c=2)
    C16 = arena.bf16(8 * 2 * 258).rearrange("p (q c e) -> p q c e", q=8, c=2)
    B_C = [Buf("C%d" % i) for i in range(8)]
    sel_t = arena.f32(2048)
    S.dma("sp", lambda e: e.dma_start(out=sel_t[0:36, :], in_=env["selc"]), writes=[B_const])
    sel = sel_t[0:36, 0:1024].rearrange("r (p m) -> r p m", p=8)
    negsel = sel_t[0:36, 1024:2048].rearrange("r (p m) -> r p m", p=8)
    mB = arena.mark()

    aLI = arena.f32(NT)
    aLF = arena.f32(NT)
    aB = arena.f32(NT)
    aG = arena.f32(NT)
    ones_r = arena.f32(512)
    tmpE = arena.f32(512)
    wgf_s = arena.bf16(8 * 8).rearrange("p (k n) -> p k n", k=8)
    wgb_s = arena.bf16(8 * 72).rearrange("p (k n) -> p k n", k=8)
    bgs = arena.f32(4)
    B_rows = Buf("rows")
    B_wg = Buf("wg")
    B_tmpE = Buf("tmpE")
    for a in (aLI, aLF, aB, aG):
        S.op("pool", (lambda e, a=a: e.memset(a, 0.0)), writes=[B_rows])
    S.op("pool", lambda e: e.memset(ones_r, 1.0), writes=[B_wg])
    S.op("pool", lambda e: e.memset(gend[:, :, :], 0.0), writes=[B_cols])
    for q in range(8):
        S.op("pool", (lambda e, q=q: e.memset(C32[:, q, :, :], 0.0)), writes=[B_C[q]])
        S.op("pool", (lambda e, q=q: e.memset(C16[:, q, :, :], 0.0)), writes=[B_C[q]])
    with nc.allow_non_contiguous_dma(reason="tiny gate weights"):
        pass
    wgate_f = env["wgate_f"]; wgate_b = env["wgate_b"]; bg = env["bg"]
    S.dma("pool", lambda e: e.dma_start(out=wgf_s, in_=wgate_f.rearrange("(k p) n -> p k n", p=128)), writes=[B_wg])
    S.dma("pool", lambda e: e.dma_start(out=wgb_s, in_=wgate_b.rearrange("(k p) n -> p k n", p=128)), writes=[B_wg])
    S.dma("sp", lambda e: e.dma_start(out=bgs[0:36, 0:2], in_=bg), writes=[B_wg])
    S.op("dve", lambda e: e.tensor_scalar(bgs[0:36, 2:3], bgs[0:36, 1:2], -1.0, None, ALU.mult), reads=[B_wg], writes=[B_wg])

    def gate_tile(jc0, W, rhs_fn, r0, r1, wl, wl_lf, pbank, rev=False):
        def pv(bank):
            return ps[bank][r0:r1, W - 1::-1] if rev else ps[bank][r0:r1, 0:W]
        mm(ps[pbank][0:r1, 0:W], [(wl(k), rhs_fn(k)) for k in range(8)], reads=[B_wg] + B_u2, writes=[Bps[pbank]])
        S.op("act", lambda e: e.activation(aLI[r0:r1, jc0:jc0 + W], pv(pbank), AF.Identity, bias=bgs[r0:r1, 0:1]),
             reads=[Bps[pbank], B_wg], writes=[B_rows])
        mm(ps[pbank + 1][0:r1, 0:W], [(wl_lf(k), rhs_fn(k)) for k in range(8)], reads=[B_wg] + B_u2, writes=[Bps[pbank + 1]])
        S.op("act", lambda e: e.activation(tmpE[r0:r1, 0:W], pv(pbank + 1), AF.Exp, bias=bgs[r0:r1, 2:3], scale=-1.0),
             reads=[Bps[pbank + 1], B_wg], writes=[B_tmpE])
        S.op("act", lambda e: e.activation(tmpE[r0:r1, 0:W], tmpE[r0:r1, 0:W], AF.Ln, bias=1.0), reads=[B_tmpE], writes=[B_tmpE])
        S.op("dve", lambda e: e.tensor_scalar(aLF[r0:r1, jc0:jc0 + W], tmpE[r0:r1, 0:W], -1.0, None, ALU.mult),
             reads=[B_tmpE], writes=[B_rows])

    ti = 0
    for (jc0, W) in [(0, 256)] + [(256 + 512 * i, 512) for i in range(4)]:
        gate_tile(jc0, W, (lambda k, jc0=jc0, W=W: u2[:, k, jc0:jc0 + W]), 0, 4,
                  (lambda k: wgf_s[:, k, 0:4]), (lambda k: wgf_s[:, k, 4:8]), 2 * (ti % 2))
        ti += 1
    def rev_u2(k, hi, W):
        return u2[:, k, hi - W + 1:hi + 1]
    gate_tile(0, 256, (lambda k: rev_u2(k, 255, 256)), 32, 36,
              (lambda k: wgb_s[:, k, 0:36]), (lambda k: wgb_s[:, k, 36:72]), 2 * (ti % 2), rev=True)
    ti += 1
    for i in range(8):
        jc0 = 256 + 512 * i
        hi = 4607 - jc0
        gate_tile(jc0, 512, (lambda k, hi=hi: rev_u2(k, hi, 512)), 32, 36,
                  (lambda k: wgb_s[:, k, 0:36]), (lambda k: wgb_s[:, k, 36:72]), 2 * (ti % 2), rev=True)
        ti += 1
    pieces = [(0, 256)] + [(256 + 512 * i, 512) for i in range(8)]
    for pi, (c0, W) in enumerate(pieces):
        init = 0.0 if pi == 0 else aB[0:36, c0 - 1:c0]
        S.op("dve", (lambda e, c0=c0, W=W, init=init: e.tensor_tensor_scan(aB[0:36, c0:c0 + W], ones_r[0:36, 0:W], aLF[0:36, c0:c0 + W],
                                                                           init, ALU.mult, ALU.add)),
             reads=[B_rows, B_wg], writes=[B_rows])
    S.op("dve", lambda e: e.tensor_tensor(aLI[0:36, :], aLI[0:36, :], aB[0:36, :], ALU.subtract), reads=[B_rows], writes=[B_rows])
    for pi, (c0, W) in enumerate(pieces):
        init = 0.0 if pi == 0 else aG[0:36, c0 - 1:c0]
        S.op("dve", (lambda e, c0=c0, W=W, init=init: e.tensor_tensor_scan(aG[0:36, c0:c0 + W], ones_r[0:36, 0:W], aLI[0:36, c0:c0 + W],
                                                                           init, ALU.mult, ALU.max)),
             reads=[B_rows, B_wg], writes=[B_rows])
    S.op("dve", lambda e: e.tensor_tensor(aB[0:36, :], aB[0:36, :], aG[0:36, :], ALU.add), reads=[B_rows], writes=[B_rows])
    if dbg:
        d_rows = dbg_tensor("d_rows", [3, 36, NT])
        for i, a in enumerate((aLI, aG, aB)):
            S.dma("sp", (lambda e, i=i, a=a: e.dma_start(out=d_rows[i], in_=a[0:36, :])), reads=[B_rows])
    def n0_of(sc):
        return 128 if sc == 0 else (0 if sc == 1 else 4480 - 128 * sc)
    for ai, (arr, bank) in enumerate(((aLI, 0), (aB, 2), (aG, 1))):
        S.op("dve", (lambda e, arr=arr: e.tensor_copy(aLF[32:36, 0:256], arr[32:36, 255::-1])), reads=[B_rows], writes=[B_rows])
        S.op("dve", (lambda e, arr=arr: e.tensor_copy(aLF[32:36, 256:NT], arr[32:36, NT - 1:255:-1])), reads=[B_rows], writes=[B_rows])
        def fn(e, arr=arr, bank=bank):
            for sc in range(18):
                ins = e.matmul(ps[bank][:, sc * 4:(sc + 1) * 4], arr[0:4, sc * 128:(sc + 1) * 128], ident[0:4, 0:4],
                               start=True, stop=True)
            for sc in range(34):
                n0 = n0_of(sc)
                ins = e.matmul(ps[bank][:, (18 + sc) * 4:(19 + sc) * 4], aLF[32:36, n0:n0 + 128],
                               ident[32:36, 32:36], start=True, stop=True)
            return ins
        S.op("pe", fn, reads=[B_rows, B_const], writes=[Bps[bank]])
    S.op("dve", lambda e: e.tensor_copy(aLF[0:4, :], aG[0:4, :]), reads=[B_rows], writes=[B_rows])
    S.dma("sp", lambda e: e.dma_start(out=GROW, in_=aLF[0:36, :]), reads=[B_rows], writes=[B_GROW])
    S.op("act", lambda e: e.copy(Rcol[:, :, :], ps[0][:, 0:NCH * 4].rearrange("p (c h) -> p c h", h=4)), reads=[Bps[0]], writes=[B_cols])
    S.op("act", lambda e: e.copy(Gcol[:, :, :], ps[1][:, 0:NCH * 4].rearrange("p (c h) -> p c h", h=4)), reads=[Bps[1]], writes=[B_cols])
    S.op("act", lambda e: e.activation(Ecol[:, :, :], ps[2][:, 0:NCH * 4].rearrange("p (c h) -> p c h", h=4), AF.Exp, scale=-1.0),
         reads=[Bps[2]], writes=[B_cols])
    def fn(e):
        for q in range(8):
            n = 18 if q < 4 else 34
            ins = e.matmul(ps[3][:, q * 34:q * 34 + n], sel[0:36, q, :], aG[0:36, 127:127 + 128 * (n - 1) + 1:128], start=True, stop=True)
        return ins
    S.op("pe", fn, reads=[B_rows, B_const], writes=[Bps[3]])
    for q in range(8):
        n = 18 if q < 4 else 34
        S.op("act", (lambda e, q=q, n=n: e.copy(gend[:, q, 1:1 + n], ps[3][:, q * 34:q * 34 + n])), reads=[Bps[3]], writes=[B_cols])
    tq = arena.f32(34)
    B_tq = Buf("tq")
    for q in range(8):
        n = 18 if q < 4 else 34
        base = 0 if q < 4 else 18
        h = q % 4
        S.op("dve", (lambda e, q=q, n=n, base=base, h=h: e.tensor_tensor(tq[:, 0:n], gend[:, q, 0:n], Gcol[:, base:base + n, h], ALU.subtract)),
             reads=[B_cols], writes=[B_tq])
        S.op("act", (lambda e, q=q, n=n: e.activation(acol[:, q, 0:n], tq[:, 0:n], AF.Exp)), reads=[B_tq], writes=[B_cols])
        S.op("dve", (lambda e, q=q, n=n, base=base, h=h: e.tensor_tensor(tq[:, 0:n], Rcol[:, base:base + n, h], gend[:, q, 1:1 + n], ALU.subtract)),
             reads=[B_cols], writes=[B_tq])
        S.op("act", (lambda e, q=q, n=n: e.activation(wkcol[:, q, 0:n], tq[:, 0:n], AF.Exp)), reads=[B_tq], writes=[B_cols])
        S.op("dve", (lambda e, q=q, n=n: e.tensor_tensor(tq[:, 0:n], gend[:, q, 0:n], gend[:, q, 1:1 + n], ALU.subtract)),
             reads=[B_cols], writes=[B_tq])
        S.op("act", (lambda e, q=q, n=n: e.activation(decay[:, q, 0:n], tq[:, 0:n], AF.Exp)), reads=[B_tq], writes=[B_cols])
    S.barrier()
    arena.reset(mB)
    if stage < 3:
        return
    build_rest2(env, locals())


COL_F = 0
COL_Q = 512
COL_K = 1536
COL_V = 2560
COL_O = 3584
COL_GATES = 4608
COL_BR = 4624


def _dft_consts(flip):
    idx = (63 - np.arange(64)) if flip else np.arange(64)
    ang = 2 * np.pi * np.outer(idx, idx) / 64.0
    Cc = np.cos(ang) / 8.0
    Sc = np.sin(ang) / 8.0
    ch = np.arange(128)
    angc = 2 * np.pi * np.outer(ch, ch) / 128.0
    CS = np.concatenate([np.cos(angc), np.sin(angc)], axis=1) / np.sqrt(128.0)
    M2 = np.zeros((128, 128))
    M2[0:64, 0:64] = Cc
    M2[64:128, 0:64] = -Sc
    M2[0:64, 64:128] = Sc
    M2[64:128, 64:128] = Cc
    M3 = np.zeros((128, 32))
    M3[0:64, :] = Cc[:, 0:32]
    M3[64:128, :] = -Sc[:, 0:32]
    return CS, M2, M3


def make_inputs(inp):
    f32 = np.float32
    x = np.asarray(inp["x"], f32)
    ctx = np.asarray(inp["ctx"], f32)
    c = np.asarray(inp["c"], f32)
    c_ctx = np.asarray(inp["c_ctx"], f32)
    w_in = np.asarray(inp["w_in"], f32)[0]
    b_in = np.asarray(inp["b_in"], f32)[0]
    conv_w = np.asarray(inp["conv_w"], f32)[0]
    conv_b = np.asarray(inp["conv_b"], f32)[0]
    norm_g = np.asarray(inp["norm_g"], f32)[0]

    def fm(v):
        return np.ascontiguousarray(v.reshape(-1, 128).T)

    def r13(w):
        return np.ascontiguousarray(w.reshape(8, 128, 2, NJ, 128).transpose(3, 1, 0, 2, 4).reshape(NJ, 128, 2048))

    def r2(w):
        return np.ascontiguousarray(w.reshape(NJ, 128, 8, 128).transpose(2, 1, 0, 3).reshape(8, 128, FF))

    shared = {
        "w_ada": np.ascontiguousarray(np.asarray(inp["w_ada"], f32)[0]),
        "w13a": r13(np.asarray(inp["w13_a"], f32)[0]), "w2a": r2(np.asarray(inp["w2_a"], f32)[0]),
        "w13b": r13(np.asarray(inp["w13_b"], f32)[0]), "w2b": r2(np.asarray(inp["w2_b"], f32)[0]),
        "wF": np.ascontiguousarray(w_in[:, COL_F:COL_Q]), "wq": np.ascontiguousarray(w_in[:, COL_Q:COL_K]),
        "wk": np.ascontiguousarray(w_in[:, COL_K:COL_V]), "wv": np.ascontiguousarray(w_in[:, COL_V:COL_O]),
        "wo": np.ascontiguousarray(w_in[:, COL_O:COL_GATES]),
        "wgf": np.ascontiguousarray(w_in[:, COL_BR:COL_BR + D]), "wgm": np.ascontiguousarray(w_in[:, COL_BR + D:]),
        "w_four": np.ascontiguousarray(np.asarray(inp["w_four"], f32)[0]),
        "w_mproj": np.ascontiguousarray(np.asarray(inp["w_mproj"], f32)[0]),
        "w_out": np.ascontiguousarray(np.asarray(inp["w_out"], f32)[0]),
        "rowv": np.concatenate([b_in[COL_V:COL_O], b_in[COL_O:COL_GATES], np.asarray(inp["head_g"], f32)[0]])[None, :].copy(),
    }
    sel = np.zeros((36, 8, 128), f32)
    for p in range(8):
        row = (p % 4) + (32 if p >= 4 else 0)
        sel[row, p, :] = 1.0
    selc = np.concatenate([sel.reshape(36, -1), -sel.reshape(36, -1)], axis=1)
    s_idx = np.arange(128)[:, None]
    t_idx = np.arange(128)[None, :]
    maskf = np.where(s_idx <= t_idx, 0.0, NEG).astype(f32)
    maskb = np.where(s_idx >= t_idx, 0.0, NEG).astype(f32)
    maps = []
    for core in range(8):
        b, half = core // 2, core % 2
        flip = half == 1
        xb = x[b][::-1] if flip else x[b]
        cb_ = ctx[b][::-1] if flip else ctx[b]
        g = COL_GATES
        if flip:
            gi_f, gf_f, gi_b, gf_b = g + 8, g + 12, g + 0, g + 4
            cw = conv_w[::-1]
        else:
            gi_f, gf_f, gi_b, gf_b = g + 0, g + 4, g + 8, g + 12
            cw = conv_w
        wgate_f = np.concatenate([w_in[:, gi_f:gi_f + 4], w_in[:, gf_f:gf_f + 4]], axis=1)
        wgate_b = np.zeros((D, 72), f32)
        wgate_b[:, 32:36] = w_in[:, gi_b:gi_b + 4]
        wgate_b[:, 36 + 32:36 + 36] = w_in[:, gf_b:gf_b + 4]
        bgv = np.zeros((36, 2), f32)
        bgv[0:4, 0] = b_in[gi_f:gi_f + 4]
        bgv[0:4, 1] = b_in[gf_f:gf_f + 4]
        bgv[32:36, 0] = b_in[gi_b:gi_b + 4]
        bgv[32:36, 1] = b_in[gf_b:gf_b + 4]
        vecs = np.zeros((128, NV), f32)

        def put(name, arr):
            o, w = VEC[name]
            assert arr.shape == (128, w), (name, arr.shape)
            vecs[:, o:o + w] = arr
        put("bada", fm(np.asarray(inp["b_ada"], f32)[0]))
        put("ng", np.concatenate([fm(norm_g[i]) for i in range(6)], axis=1))
        put("bF", fm(b_in[COL_F:COL_Q])); put("bq", fm(b_in[COL_Q:COL_K])); put("bk", fm(b_in[COL_K:COL_V]))
        put("cwq", np.concatenate([fm(cw[t, 0:D]) for t in range(3)], axis=1))
        put("cwk", np.concatenate([fm(cw[t, D:2 * D]) for t in range(3)], axis=1))
        put("cbq", fm(conv_b[0:D])); put("cbk", fm(conv_b[D:2 * D]))
        put("bgf", fm(b_in[COL_BR:COL_BR + D])); put("bgm", fm(b_in[COL_BR + D:]))
        cvec = np.zeros((128, 8, 2), f32)
        cvec[:, :, 0] = fm(c[b])
        cvec[:, :, 1] = fm(c_ctx)
        CS, M2, M3 = _dft_consts(flip)
        cst = np.zeros((128, 128 * 4 + 256 + 128 + 32), f32)
        cst[:, 0:128] = np.eye(128)
        cst[:, 128:256] = maskf
        cst[:, 256:384] = maskb
        cst[:, 512:768] = CS
        cst[:, 768:896] = M2
        cst[:, 896:928] = M3
        m = dict(shared)
        m.update({
            "xT": np.ascontiguousarray(xb.T), "ctxT": np.ascontiguousarray(cb_.T),
            "cvec": cvec.reshape(128, 16), "vecs": vecs, "bg": bgv,
            "wgate_f": np.ascontiguousarray(wgate_f), "wgate_b": wgate_b, "cst": cst, "selc": selc,
        })
        maps.append(m)
    return maps


def kernel(**inputs):
    nc, _ = build()
    maps = make_inputs(inputs)
    res = run_bass_kernel_spmd(nc, maps, core_ids=list(range(8)))
    out = np.zeros((4, SEQ, D), np.float32)
    for core in range(8):
        b, half = core // 2, core % 2
        o = np.asarray(res.results[core]["outT"]).T
        if half == 0:
            out[b, 0:OWN] = o
        else:
            out[b, OWN:] = o[::-1]
    return out
```

```python
import numpy as np
import os as _os
from contextlib import ExitStack
import concourse.bass as bass
import concourse.mybir as mybir
from concourse.bass_utils import run_bass_kernel_spmd

F32 = mybir.dt.float32
BF16 = mybir.dt.bfloat16
AF = mybir.ActivationFunctionType
ALU = mybir.AluOpType

ENGS = ("pe", "act", "dve", "pool", "sp")
EPOCH = 30000

D = 1024
SEQ = 4096
CTX = 256
NT = CTX + SEQ
OWN = 2048
FF = 2816
NJ = 22
EPS = 1e-6
NEG = -30000.0


class Tick:
    __slots__ = ("sem", "val", "know")

    def __init__(self, sem, val, know):
        self.sem = sem
        self.val = val
        self.know = know


class Buf:
    __slots__ = ("name", "w", "r")

    def __init__(self, name=""):
        self.name = name
        self.w = None
        self.r = {}


class Sched:
    def __init__(self, nc, stack, n_dma_sems=48):
        self.nc = nc
        self.stack = stack
        self.q = {e: [] for e in ENGS}
        self.cnt = {e: 0 for e in ENGS}
        self.esems = {e: [] for e in ENGS}
        self.known = {e: {} for e in ENGS}
        self.dsems = [stack.enter_context(nc.semaphore(f"dma{i}")) for i in range(n_dma_sems)]
        self.dcnt = [0] * n_dma_sems
        self.dlast = [None] * n_dma_sems
        self.drr = 0
        self.drr2 = {}

    def _esem(self, eng, idx):
        lst = self.esems[eng]
        while len(lst) <= idx:
            lst.append(self.stack.enter_context(self.nc.semaphore(f"e_{eng}_{len(lst)}")))
        return lst[idx]

    def _collect(self, eng, reads, writes, extra=()):
        kn = self.known[eng]
        waits = {}

        def need(t):
            if t is None:
                return
            if kn.get(t.sem, 0) >= t.val:
                return
            if waits.get(t.sem, (0, None))[0] < t.val:
                waits[t.sem] = (t.val, t)

        for b in reads:
            need(b.w)
        for b in writes:
            need(b.w)
            for t in b.r.values():
                need(t)
        for t in extra:
            need(t)
        items = sorted(waits.items(), key=lambda kv: -kv[1][0])
        final = []
        for sem, (val, t) in items:
            if kn.get(sem, 0) >= val:
                continue
            final.append((sem, val))
            kn[sem] = val
            for s2, v2 in t.know.items():
                if kn.get(s2, 0) < v2:
                    kn[s2] = v2
        return final

    def op(self, eng, fn, reads=(), writes=(), extra=()):
        waits = self._collect(eng, reads, writes, extra)
        c = self.cnt[eng]
        sem = self._esem(eng, c // EPOCH)
        val = c % EPOCH + 1
        self.cnt[eng] = c + 1
        t = Tick(sem, val, dict(self.known[eng]))
        for b in reads:
            b.r[sem] = t
        for b in writes:
            b.w = t
            b.r = {}
        self.q[eng].append((waits, fn, sem, 1))
        return t

    def dma(self, eng, fn, reads=(), writes=(), extra=()):
        n = len(self.dsems)
        lo, hi = (0, n // 3) if eng == "pool" else (n // 3, n)
        rr = self.drr2.get(eng, lo)
        i = rr
        self.drr2[eng] = lo + (rr + 1 - lo) % (hi - lo)
        ex = list(extra)
        if self.dlast[i] is not None:
            ex.append(self.dlast[i])
        waits = self._collect(eng, reads, writes, ex)
        self.dcnt[i] += 1
        sem = self.dsems[i]
        t = Tick(sem, 16 * self.dcnt[i], dict(self.known[eng]))
        self.dlast[i] = t
        for b in reads:
            b.r[sem] = t
        for b in writes:
            b.w = t
            b.r = {}
        self.q[eng].append((waits, fn, sem, 16))
        return t

    def wait_all(self, eng, ticks):
        waits = self._collect(eng, (), (), ticks)
        self.q[eng].append((waits, None, None, 0))

    def barrier(self, skip=()):
        skipset = {(t.sem, t.val) for t in skip}
        ticks = []
        for e in ENGS:
            c = self.cnt[e]
            if c > 0:
                ticks.append(Tick(self._esem(e, (c - 1) // EPOCH), (c - 1) % EPOCH + 1, {}))
        for t in self.dlast:
            if t is not None and (t.sem, t.val) not in skipset:
                ticks.append(t)
        for e in ENGS:
            self.wait_all(e, ticks)

    def emit(self):
        nc = self.nc
        q = self.q

        def run(engobj, lst):
            for waits, fn, sem, amt in lst:
                for s, v in waits:
                    engobj.wait_ge(s, v)
                if fn is not None:
                    ins = fn(engobj)
                    ins.then_inc(sem, amt)

        with nc.Block() as block:
            @block.tensor
            def _(e):
                run(e, q["pe"])

            @block.scalar
            def _(e):
                run(e, q["act"])

            @block.vector
            def _(e):
                run(e, q["dve"])

            @block.gpsimd
            def _(e):
                run(e, q["pool"])

            @block.sync
            def _(e):
                run(e, q["sp"])


class Arena:
    def __init__(self, nc, name, words):
        self.t = nc.alloc_sbuf_tensor(name, [128, words], F32)
        self.words = words
        self.off = 0

    def mark(self):
        return self.off

    def reset(self, m):
        self.off = m

    def f32(self, n):
        a = self.t[:, self.off:self.off + n]
        self.off += n
        assert self.off <= self.words, ("arena overflow", self.off, self.words)
        return a

    def bf16(self, n):
        w = (n + 1) // 2
        a = self.t[:, self.off:self.off + w].bitcast(BF16)
        self.off += w
        assert self.off <= self.words, ("arena overflow", self.off, self.words)
        return a[:, 0:n]


VEC = {}
_o = 0
for _n, _w in [("bada", 72), ("ng", 48), ("bF", 4), ("bq", 8), ("bk", 8), ("cwq", 24), ("cwk", 24),
               ("cbq", 8), ("cbk", 8), ("bgf", 8), ("bgm", 8)]:
    VEC[_n] = (_o, _w)
    _o += _w
NV = _o


def build(stage=99, dbg=False):
    nc = bass.Bass("TRN2", target_bir_lowering=False)
    dt_in = lambda name, shape, dt=F32: nc.dram_tensor(name, shape, dt, kind="ExternalInput").ap()
    xT = dt_in("xT", [D, SEQ])
    ctxT = dt_in("ctxT", [D, CTX])
    cvec = dt_in("cvec", [128, 16])
    w_ada = dt_in("w_ada", [D, 9 * D])
    vecs = dt_in("vecs", [128, NV])
    rowv = dt_in("rowv", [1, 3 * D])
    bg = dt_in("bg", [36, 2])
    w13a = dt_in("w13a", [NJ, 128, 2048])
    w2a = dt_in("w2a", [8, 128, FF])
    w13b = dt_in("w13b", [NJ, 128, 2048])
    w2b = dt_in("w2b", [8, 128, FF])
    wF = dt_in("wF", [D, 512])
    wq = dt_in("wq", [D, D])
    wk = dt_in("wk", [D, D])
    wv = dt_in("wv", [D, D])
    wo = dt_in("wo", [D, D])
    wgf = dt_in("wgf", [D, D])
    wgm = dt_in("wgm", [D, D])
    wgate_f = dt_in("wgate_f", [D, 8])
    wgate_b = dt_in("wgate_b", [D, 72])
    w_four = dt_in("w_four", [512, D])
    w_mproj = dt_in("w_mproj", [D, D])
    w_out = dt_in("w_out", [D, D])
    cst = dt_in("cst", [128, 128 * 4 + 256 + 128 + 32])
    selc = dt_in("selc", [36, 2 * 8 * 128])
    outT = nc.dram_tensor("outT", [D, OWN], F32, kind="ExternalOutput").ap()
    dbg_out = {}

    def dbg_tensor(name, shape, dt=F32):
        dbg_out[name] = nc.dram_tensor(name, shape, dt, kind="ExternalOutput").ap()
        return dbg_out[name]

    dscr = lambda name, shape, dt: nc.dram_tensor(name, shape, dt, kind="Internal").ap()
    H1 = dscr("H1", [D, OWN], F32)
    AB = dscr("AB", [2, SEQ, 512], F32)
    PQ = dscr("PQ", [2, 64, 64, 512], F32)
    KT = dscr("KT", [D, OWN], BF16)
    QT = dscr("QT", [D, OWN], BF16)
    KTOK = dscr("KTOK", [NT, D], BF16)
    VTOK = dscr("VTOK", [NT, D], BF16)
    B_H1, B_AB, B_PQ, B_KT, B_QT, B_KTOK, B_VTOK = [Buf(n) for n in "H1 AB PQ KT QT KTOK VTOK".split()]

    st = ExitStack()
    with st:
        S = Sched(nc, st)
        ps = [st.enter_context(nc.psum_tensor(f"ps{i}", [128, 512], F32)) for i in range(8)]
        Bps = [Buf(f"ps{i}") for i in range(8)]

        cs_t = nc.alloc_sbuf_tensor("cs", [128, 128 * 4 + 256 + 128 + 32], F32)
        ident = cs_t[:, 0:128]
        maskf = cs_t[:, 128:256]
        maskb = cs_t[:, 256:384]
        CS = cs_t[:, 512:768]
        M2 = cs_t[:, 768:896]
        M3 = cs_t[:, 896:928]
        vec_t = nc.alloc_sbuf_tensor("vec", [128, NV], F32)
        mod_t = nc.alloc_sbuf_tensor("mod", [128, 72 * 2], F32)
        der_t = nc.alloc_sbuf_tensor("der", [128, 16 * 8], F32)
        id16_t = nc.alloc_sbuf_tensor("id16", [128, 128], BF16)
        ones16_t = nc.alloc_sbuf_tensor("ones16", [128, 128], BF16)
        u2_t = nc.alloc_sbuf_tensor("u2", [128, 8 * NT], BF16)
        u2 = u2_t[:, :].rearrange("p (k t) -> p k t", k=8)
        B_const = Buf("const")
        B_mod = Buf("mod")
        B_der = Buf("der")
        tiles = [(0, CTX, 1)] + [(CTX + 512 * i, 512, 0) for i in range(8)]
        B_u2 = [Buf(f"u2_{i}") for i in range(9)]

        def vcol(name, i=0, n=1):
            o, w = VEC[name]
            return vec_t[:, o + i:o + i + n]

        DER = {}
        _d = 0
        for nm in ["A0l", "A0c", "S0l", "S0c", "PAl", "PAc", "A2l", "A2c", "S2l", "S2c", "PMl", "A4l", "S4l", "PBl"]:
            DER[nm] = der_t[:, _d * 8:(_d + 1) * 8]
            _d += 1

        arena = Arena(nc, "arena", 34000)

        S.dma("sp", lambda e: e.dma_start(out=cs_t[:, :], in_=cst), writes=[B_const])
        S.dma("sp", lambda e: e.dma_start(out=vec_t[:, :], in_=vecs), writes=[B_const])
        S.op("act", lambda e: e.copy(id16_t[:, :], ident), reads=[B_const], writes=[B_const])
        S.op("pool", lambda e: e.memset(ones16_t[:, :], 1.0), writes=[B_const])

        m0 = arena.mark()
        cv = arena.f32(16)
        scv = arena.f32(16)
        B_cv = Buf("cv")
        S.dma("sp", lambda e: e.dma_start(out=cv, in_=cvec), writes=[B_cv])
        S.op("act", lambda e: e.activation(scv, cv, AF.Silu), reads=[B_cv], writes=[B_cv])
        wad = [arena.f32(8 * 1024) for _ in range(2)]
        B_wad = [Buf("wad0"), Buf("wad1")]
        w_ada_v = w_ada.rearrange("(k p) n -> p k n", p=128)
        modps = ps[7][:, 0:144]
        for mi in range(9):
            sl = mi % 2
            wv_ = wad[sl].rearrange("p (k n) -> p k n", k=8)
            S.dma("sp", (lambda e, wv_=wv_, mi=mi: e.dma_start(out=wv_, in_=w_ada_v[:, :, mi * 1024:(mi + 1) * 1024])),
                  writes=[B_wad[sl]])
            for dc in range(8):
                def fn(e, wv_=wv_, mi=mi, dc=dc):
                    for k in range(8):
                        ins = e.matmul(modps[:, (mi * 8 + dc) * 2:(mi * 8 + dc) * 2 + 2],
                                       wv_[:, k, dc * 128:(dc + 1) * 128],
                                       scv[:, k * 2:k * 2 + 2], start=(k == 0), stop=(k == 7))
                    return ins
                S.op("pe", fn, reads=[B_wad[sl], B_cv], writes=[Bps[7]])
        modv = mod_t[:, :].rearrange("p (m j) -> p m j", j=2)
        modpsv = modps.rearrange("p (m j) -> p m j", j=2)
        bada = vcol("bada", 0, 72)
        for j in range(2):
            S.op("dve", (lambda e, j=j: e.tensor_tensor(modv[:, :, j], modpsv[:, :, j], bada, ALU.add)),
                 reads=[Bps[7], B_const], writes=[B_mod])

        def modc(mi, j):
            return modv[:, mi * 8:(mi + 1) * 8, j]

        def ng(i):
            return vcol("ng", i * 8, 8)

        def der_scale(name, mi, gi, j):
            S.op("dve", lambda e: e.scalar_tensor_tensor(DER[name], modc(mi, j), 1.0, ng(gi), ALU.add, ALU.mult),
                 reads=[B_mod, B_const], writes=[B_der])

        def der_gate(name, mi, gi, j, f):
            S.op("dve", lambda e: e.scalar_tensor_tensor(DER[name], modc(mi, j), f, ng(gi), ALU.mult, ALU.mult),
                 reads=[B_mod, B_const], writes=[B_der])

        def der_copy(name, mi, j):
            S.op("dve", lambda e: e.tensor_copy(DER[name], modc(mi, j)), reads=[B_mod], writes=[B_der])

        der_scale("A0l", 1, 0, 0); der_scale("A0c", 1, 0, 1)
        der_copy("S0l", 0, 0); der_copy("S0c", 0, 1)
        der_gate("PAl", 2, 1, 0, 0.5); der_gate("PAc", 2, 1, 1, 0.5)
        der_scale("A2l", 4, 2, 0); der_scale("A2c", 4, 2, 1)
        der_copy("S2l", 3, 0); der_copy("S2c", 3, 1)
        der_gate("PMl", 5, 3, 0, 1.0)
        der_scale("A4l", 7, 4, 0); der_copy("S4l", 6, 0); der_gate("PBl", 8, 5, 0, 0.5)
        S.barrier()
        arena.reset(m0)

        def rstd_from(sq_tile, B_sq, W, rstd, B_rstd, pbank):
            def fn(e):
                for k in range(8):
                    ins = e.matmul(ps[pbank][:, 0:W], ones16_t[:, :], sq_tile[:, k, 0:W], start=(k == 0), stop=(k == 7))
                return ins
            S.op("pe", fn, reads=[B_sq, B_const], writes=[Bps[pbank]])
            S.op("act", lambda e: e.activation(rstd[:, 0:W], ps[pbank][:, 0:W], AF.Ln, bias=EPS, scale=1.0 / D),
                 reads=[Bps[pbank]], writes=[B_rstd])
            S.op("act", lambda e: e.activation(rstd[:, 0:W], rstd[:, 0:W], AF.Exp, scale=-0.5),
                 reads=[B_rstd], writes=[B_rstd])

        def norm_mod_thunks(src, B_src, W, rstd, B_rstd, A, Sh, dst_fn, B_dst, tmp, B_tmp):
            def one(k):
                t = tmp[k % 2]
                bt = B_tmp[k % 2]
                S.op("dve", (lambda e: e.tensor_tensor(t[:, 0:W], src[:, k, 0:W], rstd[:, 0:W], ALU.mult)),
                     reads=[B_src, B_rstd], writes=[bt])
                S.op("act", (lambda e: e.activation(dst_fn(k), t[:, 0:W], AF.Identity, bias=Sh[:, k:k + 1], scale=A[:, k:k + 1])),
                     reads=[bt, B_der], writes=[B_dst])
            return [(lambda k=k: one(k)) for k in range(8)]

        def norm_mod(src, B_src, W, rstd, B_rstd, A, Sh, dst_fn, B_dst, tmp, B_tmp):
            for k in range(8):
                t = tmp[k % 2]
                bt = B_tmp[k % 2]
                S.op("dve", (lambda e, k=k, t=t: e.tensor_tensor(t[:, 0:W], src[:, k, 0:W], rstd[:, 0:W], ALU.mult)),
                     reads=[B_src, B_rstd], writes=[bt])
                S.op("act", (lambda e, k=k, t=t: e.activation(dst_fn(k), t[:, 0:W], AF.Identity,
                                                              bias=Sh[:, k:k + 1], scale=A[:, k:k + 1])),
                     reads=[bt, B_der], writes=[B_dst])

        WC = {}

        def wload(dst, B_dst, src_f32, cache, key, ncols):
            if cache is None:
                S.dma("pool", lambda e: e.dma_start(out=dst, in_=src_f32), writes=[B_dst])
                return
            k = (cache,) + key
            if k not in WC:
                sc_ = nc.dram_tensor("wc_" + "_".join(str(z) for z in k), [128, ncols], BF16, kind="Internal").ap()
                WC[k] = (sc_, Buf("wc"))
                S.dma("pool", lambda e: e.dma_start(out=dst, in_=src_f32), writes=[B_dst])
                dflat = dst if len(dst.shape) == 2 else dst.rearrange("p a b -> p (a b)")
                S.dma("sp", lambda e: e.dma_start(out=sc_, in_=dflat), reads=[B_dst], writes=[WC[k][1]])
            else:
                sc_, bsc = WC[k]
                dflat = dst if len(dst.shape) == 2 else dst.rearrange("p a b -> p (a b)")
                S.dma("sp", lambda e: e.dma_start(out=dflat, in_=sc_), reads=[bsc], writes=[B_dst])

        def ffn(src, B_src, W, rstd_pre, B_rstd_pre, A, Sh, PG, w13r, w2r, bufs, cache, do_pre=True, hook1=None, hook2=None, defer_epi=False):
            (sq, B_sq, u, B_u, g, B_g, y, B_y, tmp, B_tmp, rs2, B_rs2, wb13, B_wb13, wb2, B_wb2, sa, B_sa) = bufs
            if do_pre:
                norm_mod(src, B_src, W, rstd_pre, B_rstd_pre, A, Sh, lambda k: u[:, k, 0:W], B_u, tmp, B_tmp)
            n13 = len(wb13)
            for j in range(NJ):
                sl = j % n13
                wbv = wb13[sl].rearrange("p (k c) -> p k c", k=8)
                wload(wb13[sl], B_wb13[sl], w13r[j], cache, ("w13", j), 2048)
                pa = j % 2
                pb = 2 + j % 2

                def fa(e, wbv=wbv, pa=pa):
                    for k in range(8):
                        ins = e.matmul(ps[pa][:, 0:W], wbv[:, k, 0:128], u[:, k, 0:W], start=(k == 0), stop=(k == 7))
                    return ins

                def fb(e, wbv=wbv, pb=pb):
                    for k in range(8):
                        ins = e.matmul(ps[pb][:, 0:W], wbv[:, k, 128:256], u[:, k, 0:W], start=(k == 0), stop=(k == 7))
                    return ins
                S.op("pe", fa, reads=[B_wb13[sl], B_u], writes=[Bps[pa]])
                S.op("pe", fb, reads=[B_wb13[sl], B_u], writes=[Bps[pb]])
                s2 = j % 2
                S.op("act", (lambda e, pa=pa, s2=s2: e.activation(sa[s2][:, 0:W], ps[pa][:, 0:W], AF.Silu)),
                     reads=[Bps[pa]], writes=[B_sa[s2]])
                S.op("dve", (lambda e, pb=pb, s2=s2, j=j: e.tensor_tensor(g[:, j, 0:W], sa[s2][:, 0:W], ps[pb][:, 0:W], ALU.mult)),
                     reads=[B_sa[s2], Bps[pb]], writes=[B_g])
                if hook1 is not None:
                    hook1(j)
            n2 = len(wb2)
            HJ = NJ // 2
            for i in range(8):
                halves = []
                for hf in range(2):
                    sl = (2 * i + hf) % n2
                    wload(wb2[sl], B_wb2[sl], w2r[i][:, hf * HJ * 128:(hf + 1) * HJ * 128], cache, ("w2", i, hf), HJ * 128)
                    halves.append((wb2[sl].rearrange("p (j c) -> p j c", j=HJ), B_wb2[sl]))
                py = 4 + i % 2

                def fy(e, halves=halves, py=py):
                    for j in range(NJ):
                        wv_ = halves[j // HJ][0]
                        ins = e.matmul(ps[py][:, 0:W], wv_[:, j % HJ, :], g[:, j, 0:W], start=(j == 0), stop=(j == NJ - 1))
                    return ins
                S.op("pe", fy, reads=[halves[0][1], halves[1][1], B_g], writes=[Bps[py]])
                S.op("act", (lambda e, py=py, i=i: e.copy(y[:, i, 0:W], ps[py][:, 0:W])), reads=[Bps[py]], writes=[B_y])
                S.op("act", (lambda e, py=py, i=i: e.activation(sq[:, i, 0:W], ps[py][:, 0:W], AF.Square)),
                     reads=[Bps[py]], writes=[B_sq])
                if hook2 is not None:
                    hook2(i)
            rstd_from(sq, B_sq, W, rs2, B_rs2, 7)

            def resid(i):
                t = tmp[i % 2]
                bt = B_tmp[i % 2]
                S.op("dve", (lambda e: e.tensor_tensor(t[:, 0:W], y[:, i, 0:W], rs2[:, 0:W], ALU.mult)),
                     reads=[B_y, B_rs2], writes=[bt])
                S.op("dve", (lambda e: e.scalar_tensor_tensor(src[:, i, 0:W], t[:, 0:W], PG[:, i:i + 1],
                                                              src[:, i, 0:W], ALU.mult, ALU.add)),
                     reads=[bt, B_der], writes=[B_src])
            thunks = [(lambda i=i: resid(i)) for i in range(8)]
            if defer_epi:
                return thunks
            for th in thunks:
                th()
            return []

        def alloc_ffn_bufs():
            sq = arena.bf16(8 * 512).rearrange("p (k t) -> p k t", k=8)
            u = arena.bf16(8 * 512).rearrange("p (k t) -> p k t", k=8)
            g = arena.bf16(NJ * 512).rearrange("p (k t) -> p k t", k=NJ)
            y = arena.f32(8 * 512).rearrange("p (k t) -> p k t", k=8)
            tmp = [arena.f32(512) for _ in range(2)]
            rs2 = arena.f32(512)
            wb13 = [arena.bf16(2048) for _ in range(3)]
            wb2 = [arena.bf16(FF // 2) for _ in range(4)]
            sa = [arena.f32(512) for _ in range(2)]
            return (sq, Buf("sq"), u, Buf("u"), g, Buf("g"), y, Buf("y"), tmp, [Buf("t0"), Buf("t1")],
                    rs2, Buf("rs2"), wb13, [Buf("wb13_%d" % i) for i in range(3)], wb2, [Buf("wb2_%d" % i) for i in range(4)],
                    sa, [Buf("sa0"), Buf("sa1")])

        mA = arena.mark()
        xt = [arena.f32(8 * 512).rearrange("p (k t) -> p k t", k=8) for _ in range(2)]
        B_xt = [Buf("xt0"), Buf("xt1")]
        rs1 = arena.f32(512)
        B_rs1 = Buf("rs1")
        fb = alloc_ffn_bufs()
        tmp, B_tmp, u_, B_u_ = fb[8], fb[9], fb[2], fb[3]
        sqx = arena.bf16(8 * 512).rearrange("p (k t) -> p k t", k=8)
        B_sqx = Buf("sqx")
        xT_v = xT.rearrange("(k p) t -> p k t", p=128)
        ctxT_v = ctxT.rearrange("(k p) t -> p k t", p=128)
        if dbg:
            d_u2 = dbg_tensor("d_u2", [128, 8 * NT], BF16)
        ntile = len(tiles) if stage >= 1 else 0

        def a1_load(ti):
            c0, W, j = tiles[ti]
            x = xt[ti % 2]
            src = ctxT_v[:, :, 0:W] if j == 1 else xT_v[:, :, c0 - CTX:c0 - CTX + W]
            S.dma("pool", lambda e: e.dma_start(out=x[:, :, 0:W], in_=src), writes=[B_xt[ti % 2]])

        def sq_thunks(x, bx, W):
            return [(lambda k=k: S.op("act", (lambda e: e.activation(sqx[:, k, 0:W], x[:, k, 0:W], AF.Square)), reads=[bx], writes=[B_sqx]))
                    for k in range(8)]

        def a1_pre_thunks(ti):
            c0, W, j = tiles[ti]
            x = xt[ti % 2]; bx = B_xt[ti % 2]
            sfx = "c" if j == 1 else "l"
            th = sq_thunks(x, bx, W)
            th.append(lambda: rstd_from(sqx, B_sqx, W, rs1, B_rs1, 6))
            th += norm_mod_thunks(x, bx, W, rs1, B_rs1, DER["A0" + sfx], DER["S0" + sfx], lambda k: u_[:, k, 0:W], B_u_, tmp, B_tmp)
            return th

        def a1_epi2_thunks(ti):
            c0, W, j = tiles[ti]
            x = xt[ti % 2]; bx = B_xt[ti % 2]
            sfx = "c" if j == 1 else "l"
            th = []
            if 1 <= ti <= 4:
                o0 = c0 - CTX
                th.append(lambda: S.dma("pool", lambda e: e.dma_start(out=H1.rearrange("(k p) t -> p k t", p=128)[:, :, o0:o0 + 512], in_=x[:, :, :]),
                                        reads=[bx], writes=[B_H1]))
            th += sq_thunks(x, bx, W)
            th.append(lambda: rstd_from(sqx, B_sqx, W, rs1, B_rs1, 6))
            th += norm_mod_thunks(x, bx, W, rs1, B_rs1, DER["A2" + sfx], DER["S2" + sfx], (lambda k: u2[:, k, c0:c0 + W]), B_u2[ti], tmp, B_tmp)
            return th

        pend1 = []
        pend2 = []
        if ntile:
            a1_load(0)
            for th in a1_pre_thunks(0):
                th()
            if ntile > 1:
                a1_load(1)
        for ti in range(ntile):
            c0, W, j = tiles[ti]
            sfx = "c" if j == 1 else "l"
            if ti + 1 < ntile:
                pend2 = a1_pre_thunks(ti + 1)

            def hook1(j_):
                n = 2 if len(pend1) > (NJ - 1 - j_) else 1
                for _ in range(n):
                    if pend1:
                        pend1.pop(0)()

            def hook2(i_):
                while pend1:
                    pend1.pop(0)()
                if i_ >= 1:
                    for _ in range(3):
                        if pend2:
                            pend2.pop(0)()
            epi1 = ffn(xt[ti % 2], B_xt[ti % 2], W, None, None, None, None, DER["PA" + sfx], w13a, w2a, fb, "A",
                       do_pre=False, hook1=hook1, hook2=hook2, defer_epi=True)
            while pend1:
                pend1.pop(0)()
            while pend2:
                pend2.pop(0)()
            pend1 = list(epi1) + a1_epi2_thunks(ti)
            if ti + 2 < ntile:
                pend1.append(lambda ti=ti: a1_load(ti + 2))
        while pend1:
            pend1.pop(0)()
        S.barrier()
        arena.reset(mA)
        if dbg:
            S.dma("sp", lambda e: e.dma_start(out=d_u2, in_=u2_t[:, :]), reads=B_u2)
            d_h1 = dbg_tensor("d_h1", [D, OWN])
            S.dma("sp", lambda e: e.dma_start(out=d_h1, in_=H1), reads=[B_H1])

        env = dict(locals())
        if stage >= 2:
            build_rest(env)
        S.barrier()
        S.emit()
    return nc, dbg_out


def build_rest3(g_, L):
    AX = mybir.AxisListType
    nc = g_["nc"]; S = g_["S"]; ps = g_["ps"]; Bps = g_["Bps"]; arena = g_["arena"]
    u2 = g_["u2"]; B_u2 = g_["B_u2"]; vcol = g_["vcol"]; DER = g_["DER"]
    id16 = g_["id16"]; B_const = g_["B_const"]; mm = g_["mm"]
    H1, HS = g_["H1"], g_["HS"]; B_H1 = g_["B_H1"]; B_HS = g_["B_HS"]
    yfT, B_yfT, mB = g_["yfT"], g_["B_yfT"], g_["mB"]
    rowv = g_["rowv"]; outT = g_["outT"]
    ffn = g_["ffn"]; rstd_from = g_["rstd_from"]; wload = g_["wload"]
    kp = lambda ap: ap.rearrange("(k p) n -> p k n", p=128)
    A = arena.t
    arena.reset(mB)
    tmp = [A[:, 0:512], A[:, 512:1024]]; B_tmp = [Buf("ct0"), Buf("ct1")]
    rs2 = A[:, 1024:1536]; B_rs2 = Buf("crs2")
    yreg = A[:, 5816:9912]
    B_yreg = Buf("yreg")
    y = yreg.rearrange("p (k t) -> p k t", k=8)
    hs_t = yreg[:, 0:1024]; o_sb = yreg[:, 1024:2048]; sig = yreg[:, 2048:3072]; sqh = yreg[:, 3072:4096]
    B_hsC = Buf("c_hs"); B_osb = Buf("c_osb"); B_sig = Buf("c_sig"); B_sqh = Buf("c_sqh")
    bo_bc = A[:, 9912:10936]; hg_bc = A[:, 10936:11960]
    B_bc = Buf("cbc")
    x = arena.f32(4096).rearrange("p (k t) -> p k t", k=8); B_x = Buf("cx")
    rs1 = arena.f32(512); B_rs1 = Buf("crs1")
    sq = arena.bf16(4096).rearrange("p (k t) -> p k t", k=8); B_sq = Buf("csq")
    u = arena.bf16(4096).rearrange("p (k t) -> p k t", k=8); B_u = Buf("cu")
    hmT = sq; yT = u
    sa = [arena.f32(512), arena.f32(512)]; B_sa = [Buf("csa0"), Buf("csa1")]
    mX = arena.mark()
    g = arena.bf16(NJ * 512).rearrange("p (k t) -> p k t", k=NJ); B_g = Buf("cg")
    wb13m = [arena.bf16(2048), arena.bf16(2048), arena.bf16(2048)]; B_wb13m = [Buf("cw13_0"), Buf("cw13_1"), Buf("cw13_2")]
    wb2m = arena.bf16(FF); B_wb2m = Buf("cw2")
    wb2x = A[:, 11992:11992 + 704].bitcast(BF16), A[:, 11992 + 704:11992 + 1408].bitcast(BF16)
    arena.reset(mX)
    wsl = [arena.bf16(8 * 512).rearrange("p (k n) -> p k n", k=8) for _ in range(4)]; B_wsl = [Buf("wsl%d" % i) for i in range(4)]
    hm = arena.bf16(1024); B_hm = Buf("hm")
    ol = A[:, mX + 4096:mX + 8192].rearrange("p (k t) -> p k t", k=8)
    B_ol = [B_wsl[2], B_wsl[3]]
    ss = rs2[:, 0:8]
    fb = (sq, B_sq, u, B_u, g, B_g, y, B_yreg, tmp, B_tmp, rs2, B_rs2, wb13m, B_wb13m,
          [wb2m[:, 0:FF // 2], wb2m[:, FF // 2:FF], wb2x[0], wb2x[1]], [Buf('cw2a'), Buf('cw2b'), Buf('cw2c'), Buf('cw2d')], sa, B_sa)
    xflat = A[:, mB:mB + 4096]
    rv = xflat[:, 0:3072]; ones1 = xflat[:, 3072:3200]
    S.dma("sp", lambda e: e.dma_start(out=rv[0:1, :], in_=rowv), writes=[B_x])
    S.op("pool", lambda e: e.memset(ones1[0:1, :], 1.0), writes=[B_x])
    for (dst, seg) in ((bo_bc, 1), (hg_bc, 2)):
        for hh in range(2):
            mm(ps[6][:, 0:512], [(ones1[0:1, :], rv[0:1, seg * 1024 + hh * 512:seg * 1024 + hh * 512 + 512])], reads=[B_x], writes=[Bps[6]])
            S.op("act", (lambda e, hh=hh, dst=dst: e.copy(dst[:, hh * 512:(hh + 1) * 512], ps[6][:, 0:512])), reads=[Bps[6]], writes=[B_bc])
    S.barrier()
    w_four, w_mproj, w_out, wo, wgf, wgm = [g_[n] for n in "w_four w_mproj w_out wo wgf wgm".split()]
    w13b, w2b = g_["w13b"], g_["w2b"]
    H1v = H1.rearrange("(k p) t -> p k t", p=128)
    outv = outT.rearrange("(k p) t -> p k t", p=128)
    for T in range(4):
        t0 = 512 * T
        c0 = CTX + t0
        for hh in range(2):
            wload(wsl[hh], B_wsl[hh], kp(wo)[:, :, hh * 512:(hh + 1) * 512], "C", ("wo", hh), 4096)
        for ch in range(4):
            cc = c0 + 128 * ch
            ob = t0 + 128 * ch
            S.dma("sp", (lambda e, ob=ob: e.dma_start(out=hs_t, in_=HS[ob:ob + 128, :])), reads=[B_HS[ob // 128]], writes=[B_hsC])
            for hh in range(2):
                mm(ps[hh][:, 0:512], [(u2[:, k, cc:cc + 128], wsl[hh][:, k, :]) for k in range(8)], reads=[B_wsl[hh]] + B_u2, writes=[Bps[hh]])
                S.op("dve", (lambda e, hh=hh: e.tensor_tensor(o_sb[:, hh * 512:(hh + 1) * 512], ps[hh][:, 0:512], bo_bc[:, hh * 512:(hh + 1) * 512], ALU.add)),
                     reads=[Bps[hh], B_bc], writes=[B_osb])
            S.op("act", lambda e: e.activation(sig, o_sb, AF.Sigmoid), reads=[B_osb], writes=[B_sig])
            S.op("dve", lambda e: e.tensor_tensor(sig, sig, hg_bc, ALU.mult), reads=[B_sig, B_bc], writes=[B_sig])
            S.op("act", lambda e: e.activation(sqh, hs_t, AF.Square), reads=[B_hsC], writes=[B_sqh])
            S.op("dve", lambda e: e.reduce_sum(ss[:, 0:4], sqh.rearrange("p (h e) -> p h e", h=4), AX.X), reads=[B_sqh], writes=[B_rs2])
            S.op("act", lambda e: e.activation(ss[:, 0:4], ss[:, 0:4], AF.Ln, bias=EPS, scale=1.0 / 256), reads=[B_rs2], writes=[B_rs2])
            S.op("act", lambda e: e.activation(ss[:, 0:4], ss[:, 0:4], AF.Exp, scale=-0.5), reads=[B_rs2], writes=[B_rs2])
            for h in range(4):
                S.op("dve", (lambda e, h=h: e.scalar_tensor_tensor(hm[:, h * 256:(h + 1) * 256], hs_t[:, h * 256:(h + 1) * 256], ss[:, h:h + 1],
                                                                  sig[:, h * 256:(h + 1) * 256], ALU.mult, ALU.mult)),
                     reads=[B_hsC, B_sig, B_rs2], writes=[B_hm])
            for half in range(2):
                pb = 2 + half
                def fn(e, half=half, pb=pb):
                    for cq in range(4):
                        c = half * 4 + cq
                        ins = e.matmul(ps[pb][:, cq * 128:(cq + 1) * 128], hm[:, c * 128:(c + 1) * 128], id16[:, :], start=True, stop=True)
                    return ins
                S.op("pe", fn, reads=[B_hm, B_const], writes=[Bps[pb]])
                S.op("act", (lambda e, half=half, pb=pb, ch=ch: e.copy(hmT[:, half * 4:(half + 1) * 4, ch * 128:(ch + 1) * 128],
                                                                      ps[pb][:, 0:512].rearrange("p (c t) -> p c t", c=4))),
                     reads=[Bps[pb]], writes=[B_sq])
        for hh in range(2):
            cs_ = slice(hh * 512, (hh + 1) * 512)
            wload(wsl[0][:, 0:4, :], B_wsl[0], kp(w_four)[:, :, cs_], "C", ("w4", hh), 2048)
            wload(wsl[1], B_wsl[1], kp(w_mproj)[:, :, cs_], "C", ("wm", hh), 4096)
            wload(wsl[2], B_wsl[2], kp(wgf)[:, :, cs_], "C", ("wgf", hh), 4096)
            wload(wsl[3], B_wsl[3], kp(wgm)[:, :, cs_], "C", ("wgm", hh), 4096)
            for ii in range(4):
                i = hh * 4 + ii
                cw = slice(ii * 128, (ii + 1) * 128)
                mm(ps[0][:, 0:512], [(wsl[0][:, gq, cw], yfT[:, gq, t0:t0 + 512]) for gq in range(4)], reads=[B_wsl[0], B_yfT], writes=[Bps[0]])
                mm(ps[1][:, 0:512], [(wsl[1][:, k, cw], hmT[:, k, :]) for k in range(8)], reads=[B_wsl[1], B_sq], writes=[Bps[1]])
                mm(ps[2][:, 0:512], [(wsl[2][:, k, cw], u2[:, k, c0:c0 + 512]) for k in range(8)], reads=[B_wsl[2]] + B_u2, writes=[Bps[2]])
                mm(ps[3][:, 0:512], [(wsl[3][:, k, cw], u2[:, k, c0:c0 + 512]) for k in range(8)], reads=[B_wsl[3]] + B_u2, writes=[Bps[3]])
                S.op("act", (lambda e, i=i: e.activation(sa[0], ps[2][:, 0:512], AF.Sigmoid, bias=vcol("bgf", i, 1))), reads=[Bps[2], B_const], writes=[B_sa[0]])
                S.op("act", (lambda e, i=i: e.activation(sa[1], ps[3][:, 0:512], AF.Sigmoid, bias=vcol("bgm", i, 1))), reads=[Bps[3], B_const], writes=[B_sa[1]])
                S.op("dve", lambda e: e.tensor_tensor(sa[0], sa[0], ps[0][:, 0:512], ALU.mult), reads=[Bps[0], B_sa[0]], writes=[B_sa[0]])
                S.op("dve", lambda e: e.tensor_tensor(sa[1], sa[1], ps[1][:, 0:512], ALU.mult), reads=[Bps[1], B_sa[1]], writes=[B_sa[1]])
                S.op("dve", (lambda e, i=i: e.tensor_tensor(yT[:, i, :], sa[0], sa[1], ALU.add)), reads=B_sa, writes=[B_u])
        S.dma("sp", (lambda e, t0=t0: e.dma_start(out=x[:, :, :], in_=H1v[:, :, t0:t0 + 512])), reads=[B_H1], writes=[B_x])
        for hh in range(2):
            wload(wsl[hh], B_wsl[hh], kp(w_out)[:, :, hh * 512:(hh + 1) * 512], "C", ("wout", hh), 4096)
        for i in range(8):
            pb = 4 + i % 2
            mm(ps[pb][:, 0:512], [(wsl[i // 4][:, k, (i % 4) * 128:(i % 4 + 1) * 128], yT[:, k, :]) for k in range(8)],
               reads=[B_wsl[i // 4], B_u], writes=[Bps[pb]])
            S.op("act", (lambda e, i=i, pb=pb: e.copy(ol[:, i, :], ps[pb][:, 0:512])), reads=[Bps[pb]], writes=B_ol)
            S.op("act", (lambda e, i=i, pb=pb: e.activation(sq[:, i, :], ps[pb][:, 0:512], AF.Square)), reads=[Bps[pb]], writes=[B_sq])
        rstd_from(sq, B_sq, 512, rs1, B_rs1, 6)
        for i in range(8):
            t = tmp[i % 2]; bt = B_tmp[i % 2]
            S.op("dve", (lambda e, i=i, t=t: e.tensor_tensor(t, ol[:, i, :], rs1, ALU.mult)), reads=B_ol + [B_rs1], writes=[bt])
            S.op("dve", (lambda e, i=i, t=t: e.scalar_tensor_tensor(x[:, i, :], t, DER["PMl"][:, i:i + 1], x[:, i, :], ALU.mult, ALU.add)),
                 reads=[bt, g_["B_der"]], writes=[B_x])
        S.barrier()
        S.op("act", lambda e: e.activation(sq[:, :, :], x[:, :, :], AF.Square), reads=[B_x], writes=[B_sq])
        rstd_from(sq, B_sq, 512, rs1, B_rs1, 6)
        ffn(x, B_x, 512, rs1, B_rs1, DER["A4l"], DER["S4l"], DER["PBl"], w13b, w2b, fb, "B")
        t_store = S.dma("sp", (lambda e, t0=t0: e.dma_start(out=outv[:, :, t0:t0 + 512], in_=x[:, :, :])), reads=[B_x])
        S.barrier(skip=[t_store])


def build_rest2(env, L):
    AX = mybir.AxisListType
    g_ = dict(env); g_.update(L)
    nc = g_["nc"]; S = g_["S"]; ps = g_["ps"]; Bps = g_["Bps"]; arena = g_["arena"]
    u2 = g_["u2"]; B_u2 = g_["B_u2"]; vcol = g_["vcol"]; DER = g_["DER"]
    ident = g_["ident"]; maskf = g_["maskf"]; maskb = g_["maskb"]; CS = g_["CS"]; M2 = g_["M2"]; M3 = g_["M3"]
    id16 = g_["id16"]; ones16 = g_["ones16"]; B_const = g_["B_const"]; B_der = g_["B_der"]
    sel = g_["sel"]; negsel = g_["negsel"]; mm = g_["mm"]
    H1, AB, PQ, KT, QT, KTOK, VTOK, GROW, HS = [g_[n] for n in "H1 AB PQ KT QT KTOK VTOK GROW HS".split()]
    B_H1, B_AB, B_PQ, B_KT, B_QT, B_KTOK, B_VTOK, B_GROW = [g_["B_" + n] for n in "H1 AB PQ KT QT KTOK VTOK GROW".split()]
    B_HS = g_["B_HS"]
    Rcol, Gcol, Ecol, gend, acol, wkcol, decay, B_cols = [g_[n] for n in "Rcol Gcol Ecol gend acol wkcol decay B_cols".split()]
    yfT, B_yfT, C32, C16, B_C, mB = [g_[n] for n in "yfT B_yfT C32 C16 B_C mB".split()]
    rowv = g_["rowv"]; outT = g_["outT"]
    tiles = g_["tiles"]
    kp = lambda ap: ap.rearrange("(k p) n -> p k n", p=128)

    wbuf = [arena.bf16(8 * 1024).rearrange("p (k n) -> p k n", k=8) for _ in range(2)]
    B_wbuf = [Buf("wbuf0"), Buf("wbuf1")]
    xf = [arena.f32(512) for _ in range(4)]
    B_xf = [Buf("xf%d" % i) for i in range(4)]
    ab_sb2 = [arena.f32(1024).rearrange("p (x g c) -> p x g c", x=2, g=4) for _ in range(2)]
    B_ab2 = [Buf("ab_sb0"), Buf("ab_sb1")]
    Pt2 = [arena.f32(514), arena.f32(514)]
    B_Pt2 = [Buf("Pt0"), Buf("Pt1")]
    acc2 = [arena.f32(512), arena.f32(512)]
    B_acc2 = [Buf("acc0"), Buf("acc1")]
    kTt = arena.bf16(8 * 512).rearrange("p (k t) -> p k t", k=8)
    B_kTt = Buf("kTt")
    tok2 = [arena.bf16(1024), arena.bf16(1024)]
    B_tok2 = [Buf("tok0"), Buf("tok1")]
    tokctr = [0]
    bv_bc = arena.f32(1024)
    ones1 = arena.f32(128)
    rv = arena.f32(1024)
    B_bc = Buf("bc")
    S.dma("sp", lambda e: e.dma_start(out=rv[0:1, :], in_=rowv[:, 0:1024]), writes=[B_bc])
    S.op("pool", lambda e: e.memset(ones1[0:1, :], 1.0), writes=[B_bc])

    def bcast_row(dst, seg):
        for hh in range(2):
            mm(ps[6][:, 0:512], [(ones1[0:1, :], rv[0:1, seg * 1024 + hh * 512:seg * 1024 + hh * 512 + 512])], reads=[B_bc], writes=[Bps[6]])
            S.op("act", (lambda e, hh=hh: e.copy(dst[:, hh * 512:(hh + 1) * 512], ps[6][:, 0:512])), reads=[Bps[6]], writes=[B_bc])
    bcast_row(bv_bc, 0)

    wF = g_["wF"]
    S.dma("pool", lambda e: e.dma_start(out=wbuf[0][:, :, 0:512], in_=kp(wF)), writes=[B_wbuf[0]])
    for i in range(8):
        c0 = CTX + 512 * i
        for g in range(4):
            pb = g % 2
            mm(ps[pb][:, 0:512], [(wbuf[0][:, k, g * 128:(g + 1) * 128], u2[:, k, c0:c0 + 512]) for k in range(8)],
               reads=[B_wbuf[0]] + B_u2, writes=[Bps[pb]])
            S.op("act", (lambda e, g=g, pb=pb: e.activation(xf[g], ps[pb][:, 0:512], AF.Identity, bias=vcol("bF", g, 1))),
                 reads=[Bps[pb], B_const], writes=[B_xf[g]])
        for tb in range(4):
            ab_sb = ab_sb2[tb % 2]; B_ab = B_ab2[tb % 2]
            for g in range(4):
                bank = 2 + g // 2
                mm(ps[bank][:, (g % 2) * 256:(g % 2) * 256 + 256], [(xf[g][:, tb * 128:(tb + 1) * 128], CS)],
                   reads=[B_xf[g], B_const], writes=[Bps[bank]])
            for bi in range(2):
                S.op("dve", (lambda e, bi=bi, ab_sb=ab_sb: e.tensor_copy(ab_sb[:, :, 2 * bi:2 * bi + 2, :],
                                                            ps[2 + bi][:, 0:512].rearrange("p (g x c) -> p x g c", g=2, x=2))),
                     reads=[Bps[2 + bi]], writes=[B_ab])
            tok0 = 512 * i + 128 * tb
            S.dma("sp", (lambda e, tok0=tok0, ab_sb=ab_sb: e.dma_start(out=AB.rearrange("x t f -> t x f")[tok0:tok0 + 128, :, :],
                                                          in_=ab_sb.rearrange("p x g c -> p x (g c)"))), reads=[B_ab], writes=[B_AB])

    def qk_proj(wdram, bname, cwname, cbname, slot, tlist, is_k):
        S.dma("pool", lambda e: e.dma_start(out=wbuf[slot], in_=kp(wdram)), writes=[B_wbuf[slot]])
        for (ti, c0, W) in tlist:
            islat = ti >= 1
            left = islat and ti > 1
            right = islat and ti < 8
            for c in range(8):
                pb = c % 2
                Pt = Pt2[c % 2]; B_Pt = B_Pt2[c % 2]; acc = acc2[c % 2]; B_acc = B_acc2[c % 2]
                hb = 5 + c % 2
                mm(ps[pb][:, 0:W], [(wbuf[slot][:, k, c * 128:(c + 1) * 128], u2[:, k, c0:c0 + W]) for k in range(8)],
                   reads=[B_wbuf[slot]] + B_u2, writes=[Bps[pb]])
                S.op("act", (lambda e, c=c, pb=pb, W=W, Pt=Pt: e.activation(Pt[:, 1:W + 1], ps[pb][:, 0:W], AF.Identity, bias=vcol(bname, c, 1))),
                     reads=[Bps[pb], B_const], writes=[B_Pt])
                if left and right:
                    mm(ps[hb][:, 0:2], [(wbuf[slot][:, k, c * 128:(c + 1) * 128], u2[:, k, c0 - 1:c0 + W + 1:W + 1]) for k in range(8)],
                       reads=[B_wbuf[slot]] + B_u2, writes=[Bps[hb]])
                    S.op("act", (lambda e, c=c, Pt=Pt, hb=hb, W=W: e.activation(Pt[:, 0:W + 2:W + 1], ps[hb][:, 0:2], AF.Identity, bias=vcol(bname, c, 1))),
                         reads=[Bps[hb], B_const], writes=[B_Pt])
                else:
                    for hi_, (has, col, dstc) in enumerate(((left, c0 - 1, 0), (right, c0 + W, W + 1))):
                        if has:
                            mm(ps[hb][:, hi_:hi_ + 1], [(wbuf[slot][:, k, c * 128:(c + 1) * 128], u2[:, k, col:col + 1]) for k in range(8)],
                               reads=[B_wbuf[slot]] + B_u2, writes=[Bps[hb]])
                            S.op("act", (lambda e, c=c, dstc=dstc, Pt=Pt, hb=hb, hi_=hi_: e.activation(Pt[:, dstc:dstc + 1], ps[hb][:, hi_:hi_ + 1], AF.Identity, bias=vcol(bname, c, 1))),
                                 reads=[Bps[hb], B_const], writes=[B_Pt])
                        else:
                            S.op("pool", (lambda e, dstc=dstc, Pt=Pt: e.memset(Pt[:, dstc:dstc + 1], 0.0)), writes=[B_Pt])
                S.op("act", (lambda e, c=c, W=W, Pt=Pt, acc=acc: e.activation(acc[:, 0:W], Pt[:, 0:W], AF.Identity, scale=vcol(cwname, c, 1))),
                     reads=[B_Pt, B_const], writes=[B_acc])
                S.op("dve", (lambda e, c=c, W=W, Pt=Pt, acc=acc: e.scalar_tensor_tensor(acc[:, 0:W], Pt[:, 1:W + 1], vcol(cwname, 8 + c, 1), acc[:, 0:W], ALU.mult, ALU.add)),
                     reads=[B_Pt, B_const], writes=[B_acc])
                S.op("dve", (lambda e, c=c, W=W, Pt=Pt, acc=acc: e.scalar_tensor_tensor(acc[:, 0:W], Pt[:, 2:W + 2], vcol(cwname, 16 + c, 1), acc[:, 0:W], ALU.mult, ALU.add)),
                     reads=[B_Pt, B_const], writes=[B_acc])
                S.op("act", (lambda e, c=c, W=W, acc=acc: e.activation(kTt[:, c, 0:W], acc[:, 0:W], AF.Silu, bias=vcol(cbname, c, 1))),
                     reads=[B_acc, B_const], writes=[B_kTt])
            own = 1 <= ti <= 4
            if own:
                o0 = c0 - CTX
                dst = KT if is_k else QT
                S.dma("sp", (lambda e, o0=o0, dst=dst: e.dma_start(out=dst.rearrange("(k p) t -> p k t", p=128)[:, :, o0:o0 + 512], in_=kTt[:, :, :])),
                      reads=[B_kTt], writes=[B_KT if is_k else B_QT])
            if is_k:
                for tb in range(W // 128):
                    tok = tok2[tokctr[0] % 2]; B_tok = B_tok2[tokctr[0] % 2]; tokctr[0] += 1
                    for half in range(2):
                        pb = 3 + half
                        def fn(e, tb=tb, half=half, pb=pb):
                            for cc in range(4):
                                c = half * 4 + cc
                                ins = e.matmul(ps[pb][:, cc * 128:(cc + 1) * 128], kTt[:, c, tb * 128:(tb + 1) * 128], id16[:, :], start=True, stop=True)
                            return ins
                        S.op("pe", fn, reads=[B_kTt, B_const], writes=[Bps[pb]])
                        S.op("dve", (lambda e, half=half, pb=pb, tok=tok: e.tensor_copy(tok[:, half * 512:(half + 1) * 512], ps[pb][:, 0:512])),
                             reads=[Bps[pb]], writes=[B_tok])
                    r0 = c0 + tb * 128
                    S.dma("sp", (lambda e, r0=r0, tok=tok: e.dma_start(out=KTOK[r0:r0 + 128, :], in_=tok)), reads=[B_tok], writes=[B_KTOK])

    tl_all = [(ti, c0, W) for ti, (c0, W, j) in enumerate(tiles)]
    qk_proj(g_["wk"], "bk", "cwk", "cbk", 1, tl_all, True)
    qk_proj(g_["wq"], "bq", "cwq", "cbq", 0, tl_all[1:5], False)
    S.dma("pool", lambda e: e.dma_start(out=wbuf[1], in_=kp(g_["wv"])), writes=[B_wbuf[1]])
    for cb in range(0, NT, 128):
        tok = tok2[tokctr[0] % 2]; B_tok = B_tok2[tokctr[0] % 2]; tokctr[0] += 1
        for half in range(2):
            pb = half + 2 * ((cb // 128) % 2)
            mm(ps[pb][:, 0:512], [(u2[:, k, cb:cb + 128], wbuf[1][:, k, half * 512:(half + 1) * 512]) for k in range(8)],
               reads=[B_wbuf[1]] + B_u2, writes=[Bps[pb]])
            S.op("dve", (lambda e, half=half, pb=pb, tok=tok: e.tensor_tensor(tok[:, half * 512:(half + 1) * 512], ps[pb][:, 0:512],
                                                                     bv_bc[:, half * 512:(half + 1) * 512], ALU.add)),
                 reads=[Bps[pb], B_bc], writes=[B_tok])
        S.dma("sp", (lambda e, cb=cb, tok=tok: e.dma_start(out=VTOK[cb:cb + 128, :], in_=tok)), reads=[B_tok], writes=[B_VTOK])
    S.barrier()
    arena.reset(mB)

    inb2 = [arena.f32(8 * 512).rearrange("p (r f) -> p r f", r=8) for _ in range(2)]
    outb2 = [arena.f32(8 * 512).rearrange("p (r f) -> p r f", r=8) for _ in range(2)]
    B_inb2 = [Buf("inb0"), Buf("inb1")]; B_outb2 = [Buf("outb0"), Buf("outb1")]
    ABv = AB.rearrange("x (r c) f -> x c r f", c=64)
    PQw = PQ
    for rb in range(8):
        inb = inb2[rb % 2]; outb = outb2[rb % 2]; B_inb = B_inb2[rb % 2]; B_outb = B_outb2[rb % 2]
        for x in range(2):
            S.dma("sp", (lambda e, rb=rb, x=x, inb=inb: e.dma_start(out=inb[64 * x:64 * x + 64, :, :], in_=ABv[x, :, rb * 8:rb * 8 + 8, :])),
                  reads=[B_AB], writes=[B_inb])
        for r in range(8):
            pb = r % 4
            mm(ps[pb][:, 0:512], [(M2, inb[:, r, :])], reads=[B_inb, B_const], writes=[Bps[pb]])
            if r % 2 == 0:
                S.op("act", (lambda e, r=r, pb=pb, outb=outb: e.copy(outb[:, r, :], ps[pb][:, 0:512])), reads=[Bps[pb]], writes=[B_outb])
            else:
                S.op("dve", (lambda e, r=r, pb=pb, outb=outb: e.tensor_copy(outb[:, r, :], ps[pb][:, 0:512])), reads=[Bps[pb]], writes=[B_outb])
        for x in range(2):
            S.dma("sp", (lambda e, rb=rb, x=x, outb=outb: e.dma_start(out=PQw[x, :, rb * 8:rb * 8 + 8, :], in_=outb[64 * x:64 * x + 64, :, :])),
                  reads=[B_outb], writes=[B_PQ])
    PQr = PQ.rearrange("x kc r f -> x r kc f")
    for kb in range(8):
        inb = inb2[kb % 2]; B_inb = B_inb2[kb % 2]
        for x in range(2):
            S.dma("sp", (lambda e, kb=kb, x=x, inb=inb: e.dma_start(out=inb[64 * x:64 * x + 64, :, :], in_=PQr[x, :, kb * 8:kb * 8 + 8, :])),
                  reads=[B_PQ], writes=[B_inb])
        for g in range(4):
            pb = 4 + g
            def fn(e, g=g, pb=pb, inb=inb):
                for kc in range(8):
                    ins = e.matmul(ps[pb][:, kc * 32:(kc + 1) * 32], inb[:, kc, g * 128:(g + 1) * 128], M3, start=True, stop=True)
                return ins
            S.op("pe", fn, reads=[B_inb, B_const], writes=[Bps[pb]])
            S.op("act", (lambda e, g=g, pb=pb, kb=kb: e.copy(yfT[:, g, :].rearrange("p (kr kc) -> p kc kr", kc=64)[:, kb * 8:kb * 8 + 8, :],
                                                          ps[pb][:, 0:256].rearrange("p (kc kr) -> p kc kr", kr=32))),
                 reads=[Bps[pb]], writes=[B_yfT])
    S.barrier()
    arena.reset(mB)

    NLS = 3
    NHS = 2
    LD = []
    for i in range(2 * NLS):
        d_ = dict(ktok=arena.bf16(1024), vaug=arena.bf16(4 * 258).rearrange("p (h e) -> p h e", h=4),
                  kT=arena.bf16(1024).rearrange("p (k t) -> p k t", k=8), qT=arena.bf16(1024).rearrange("p (k t) -> p k t", k=8),
                  grow=arena.f32(128), B_ld=Buf("ld%d" % i), B_ldo=Buf("ldo%d" % i))
        S.op("pool", (lambda e, v=d_["vaug"]: e.memset(v[:, :, :], 1.0)), writes=[d_["B_ld"]])
        LD.append(d_)
    HSB = [dict(hs=arena.f32(1024), B_hs=Buf("hs%d" % i)) for i in range(4)]
    HT = []
    B_pP2s = Buf("pP2"); B_pP1s = Buf("pP1")
    for i in range(NHS):
        HT.append(dict(wT=arena.f32(128), STb=arena.bf16(128), P2sb=arena.f32(257), hn=arena.f32(257), dd=arena.f32(2), kw=arena.bf16(256),
                       B_wT=Buf("wT%d" % i), B_ST=Buf("ST%d" % i), B_P2=Buf("P2sb%d" % i), B_hn=Buf("hn%d" % i), B_dd=Buf("dd%d" % i),
                       B_kw=Buf("kw%d" % i),
                       pST=ps[i][:, 0:128], pD=ps[i][:, 128:256], pP2=ps[2][:, 0:257], pP1=ps[3][:, 0:257],
                       pCU=[ps[4 + i][:, 0:257], ps[6 + i][:, 0:257]],
                       B_pSD=Buf("pSD%d" % i), B_pP2=B_pP2s, B_pP1=B_pP1s, B_pCU=[Buf("pCU0_%d" % i), Buf("pCU1_%d" % i)]))
    B_C32 = [[Buf('C32_%d_%d' % (q, c)) for c in range(2)] for q in range(8)]
    B_C16 = [[Buf('C16_%d_%d' % (q, c)) for c in range(2)] for q in range(8)]
    steps = []
    fw = [(0, 0, None), (1, 128, None)] + [(2 + i, CTX + 128 * i, 128 * i) for i in range(16)]
    bw = [(0, 128, None), (1, 0, None)] + [(2 + i, CTX + 128 * (31 - i), (128 * (31 - i) if 31 - i <= 15 else None)) for i in range(32)]
    for i in range(34):
        if i < 18:
            steps.append((0,) + fw[i])
        steps.append((1,) + bw[i])
    mask16 = [arena.bf16(128), arena.bf16(128)]
    S.op("act", lambda e: e.copy(mask16[0], maskf), reads=[B_const], writes=[B_const])
    S.op("act", lambda e: e.copy(mask16[1], maskb), reads=[B_const], writes=[B_const])

    def emit_loads(si):
        (dr, sc, cb, ob) = steps[si]
        L_ = LD[dr * NLS + sc % NLS]
        ktok_t, vaug, kT_t, qT_t, grow_t = L_["ktok"], L_["vaug"], L_["kT"], L_["qT"], L_["grow"]
        B_ld, B_ldo = L_["B_ld"], L_["B_ldo"]
        S.dma("sp", (lambda e: e.dma_start(out=ktok_t, in_=KTOK[cb:cb + 128, :])), reads=[B_KTOK], writes=[B_ld])
        S.dma("sp", (lambda e: e.dma_start(out=vaug[:, :, 0:256], in_=VTOK[cb:cb + 128, :].rearrange("t (h e) -> t h e", h=4))),
              reads=[B_VTOK], writes=[B_ld])
        if ob is not None:
            S.dma("sp", (lambda e: e.dma_start(out=kT_t, in_=KT.rearrange("(k p) t -> p k t", p=128)[:, :, ob:ob + 128])), reads=[B_KT], writes=[B_ldo])
            S.dma("sp", (lambda e: e.dma_start(out=qT_t, in_=QT.rearrange("(k p) t -> p k t", p=128)[:, :, ob:ob + 128])), reads=[B_QT], writes=[B_ldo])
            S.dma("sp", (lambda e: e.dma_start(out=grow_t[0:36, :], in_=GROW[:, cb:cb + 128])), reads=[B_GROW], writes=[B_ldo])

    def emit_load_hs(si):
        (dr, sc, cb, ob) = steps[si]
        if ob is not None and dr == 1:
            H2 = HSB[dr * 2 + sc % 2]
            S.dma("sp", (lambda e: e.dma_start(out=H2["hs"], in_=HS[ob:ob + 128, :])), reads=[B_HS[ob // 128]], writes=[H2["B_hs"]])

    items = []
    for si, (dr, sc, cb, ob) in enumerate(steps):
        for h in range(4):
            items.append((si, h, len(items)))

    def ctx_of(it):
        si, h, n = it
        (dr, sc, cb, ob) = steps[si]
        L2 = dict(LD[dr * NLS + sc % NLS]); L2.update(HSB[dr * 2 + sc % 2])
        return dr, sc, cb, ob, h, dr * 4 + h, (sc if dr == 0 else 18 + sc), L2, HT[n % NHS]

    def emit_A(it):
        dr, sc, cb, ob, h, q, ci, L_, H_ = ctx_of(it)
        kT_t, qT_t, grow_t, ktok_t = L_["kT"], L_["qT"], L_["grow"], L_["ktok"]
        wT, STb, kw, pST, pD = H_["wT"], H_["STb"], H_["kw"], H_["pST"], H_["pD"]
        S.op("pool", (lambda e: e.tensor_scalar(kw, ktok_t[:, h * 256:(h + 1) * 256], wkcol[:, q, sc:sc + 1], 0.0625, ALU.mult, ALU.mult)),
             reads=[L_["B_ld"], B_cols], writes=[H_["B_kw"]])
        if ob is not None:
            def fsd(e):
                e.matmul(pST, kT_t[:, 2 * h, :], qT_t[:, 2 * h, :], start=True, stop=False)
                e.matmul(pST, kT_t[:, 2 * h + 1, :], qT_t[:, 2 * h + 1, :], start=False, stop=True)
                e.matmul(pD, negsel[0:36, q, :], grow_t[0:36, :], start=True, stop=False)
                return e.matmul(pD, ident, maskf if dr == 0 else maskb, start=False, stop=True)
            S.op("pe", fsd, reads=[L_["B_ldo"], B_const], writes=[H_["B_pSD"]])
            S.op("act", (lambda e: e.activation(wT, pD, AF.Exp, bias=Rcol[:, ci, h:h + 1])),
                 reads=[H_["B_pSD"], B_cols], writes=[H_["B_wT"]])
            S.op("dve", (lambda e: e.scalar_tensor_tensor(STb, pST, 0.0625, wT, ALU.mult, ALU.mult)),
                 reads=[H_["B_pSD"], H_["B_wT"]], writes=[H_["B_ST"]])

    def emit_BC(it):
        dr, sc, cb, ob, h, q, ci, L_, H_ = ctx_of(it)
        vaug, qT_t, hs_t = L_["vaug"], L_["qT"], L_["hs"]
        B_ld, B_ldo, B_hs = L_["B_ld"], L_["B_ldo"], L_["B_hs"]
        STb, P2sb, hn, dd, kw = H_["STb"], H_["P2sb"], H_["hn"], H_["dd"], H_["kw"]
        pP2, pP1, pCU = H_["pP2"], H_["pP1"], H_["pCU"]
        for c in range(2):
            mm(pCU[c], [(kw[:, c * 128:(c + 1) * 128], vaug[:, h, 0:257])], reads=[H_["B_kw"], B_ld], writes=[H_["B_pCU"][c]])
        if ob is not None:
            mm(pP1, [(qT_t[:, 2 * h + c, :], C16[:, q, c, 0:257]) for c in range(2)], reads=[B_ldo] + B_C16[q], writes=[H_["B_pP1"]])
        for c in range(2):
            S.op("dve", (lambda e, c=c: e.scalar_tensor_tensor(C32[:, q, c, :], C32[:, q, c, :], decay[:, q, sc:sc + 1], pCU[c], ALU.mult, ALU.add)),
                 reads=[H_["B_pCU"][c], B_cols], writes=[B_C32[q][c]])
            S.op("act", (lambda e, c=c: e.copy(C16[:, q, c, 0:257], C32[:, q, c, :])), reads=[B_C32[q][c]], writes=[B_C16[q][c]])
        if ob is not None:
            mm(pP2, [(STb, vaug[:, h, 0:257])], reads=[H_["B_ST"], B_ld], writes=[H_["B_pP2"]])
            S.op("act", (lambda e: e.copy(P2sb, pP2)), reads=[H_["B_pP2"]], writes=[H_["B_P2"]])
            S.op("dve", (lambda e: e.scalar_tensor_tensor(hn, pP1, acol[:, q, sc:sc + 1], P2sb, ALU.mult, ALU.add)),
                 reads=[H_["B_pP1"], H_["B_P2"], B_cols], writes=[H_["B_hn"]])
            S.op("dve", (lambda e: e.scalar_tensor_tensor(dd[:, 0:1], hn[:, 256:257], -1.0, hn[:, 256:257], ALU.mult, ALU.max)),
                 reads=[H_["B_hn"]], writes=[H_["B_dd"]])
            S.op("dve", (lambda e: e.tensor_tensor(dd[:, 0:1], dd[:, 0:1], Ecol[:, ci, h:h + 1], ALU.max)),
                 reads=[H_["B_dd"], B_cols], writes=[H_["B_dd"]])
            S.op("dve", (lambda e: e.reciprocal(dd[:, 1:2], dd[:, 0:1])), reads=[H_["B_dd"]], writes=[H_["B_dd"]])
            if dr == 0:
                S.op("act", (lambda e: e.activation(hs_t[:, h * 256:(h + 1) * 256], hn[:, 0:256], AF.Identity, scale=dd[:, 1:2])),
                     reads=[H_["B_hn"], H_["B_dd"]], writes=[B_hs])
            else:
                S.op("dve", (lambda e: e.scalar_tensor_tensor(hs_t[:, h * 256:(h + 1) * 256], hn[:, 0:256], dd[:, 1:2],
                                                              hs_t[:, h * 256:(h + 1) * 256], ALU.mult, ALU.add)),
                     reads=[H_["B_hn"], H_["B_dd"]], writes=[B_hs])
            if h == 3:
                S.dma("sp", (lambda e: e.dma_start(out=HS[ob:ob + 128, :], in_=hs_t)), reads=[B_hs], writes=[B_HS[ob // 128]])

    emit_loads(0); emit_loads(1)
    for n in range(len(items) + 1):
        if n < len(items):
            si, h, _ = items[n]
            if h == 0:
                emit_load_hs(si)
            if h == 1 and si + 2 < len(steps):
                emit_loads(si + 2)
            emit_A(items[n])
        if n >= 1:
            emit_BC(items[n - 1])
    S.barrier()
    if g_["stage"] < 4:
        return
    build_rest3(g_, locals())


def build_rest(env):
    nc = env["nc"]
    S = env["S"]; ps = env["ps"]; Bps = env["Bps"]; arena = env["arena"]
    u2 = env["u2"]; B_u2 = env["B_u2"]; vcol = env["vcol"]; DER = env["DER"]
    ident = env["ident"]; maskf = env["maskf"]; maskb = env["maskb"]; CS = env["CS"]; M2 = env["M2"]; M3 = env["M3"]
    id16 = env["id16_t"]; ones16 = env["ones16_t"]
    B_const = env["B_const"]; B_der = env["B_der"]
    stage = env["stage"]; dbg = env["dbg"]; dbg_tensor = env["dbg_tensor"]
    dscr = env["dscr"]
    H1, AB, PQ, KT, QT, KTOK, VTOK = [env[n] for n in "H1 AB PQ KT QT KTOK VTOK".split()]
    B_H1, B_AB, B_PQ, B_KT, B_QT, B_KTOK, B_VTOK = [env["B_" + n] for n in "H1 AB PQ KT QT KTOK VTOK".split()]
    GROW = dscr("GROW", [36, NT], F32)
    B_GROW = Buf("GROW")
    HS = dscr("HS", [OWN, D], F32)
    B_HS = [Buf("HS%d" % i) for i in range(16)]

    def mm(out, pairs, reads, writes):
        def fn(e):
            n = len(pairs)
            for i, (l, r) in enumerate(pairs):
                ins = e.matmul(out, l, r, start=(i == 0), stop=(i == n - 1))
            return ins
        return S.op("pe", fn, reads=reads, writes=writes)

    NCH = 52
    Rcol = arena.f32(NCH * 4).rearrange("p (c h) -> p c h", h=4)
    Gcol = arena.f32(NCH * 4).rearrange("p (c h) -> p c h", h=4)
    Ecol = arena.f32(NCH * 4).rearrange("p (c h) -> p c h", h=4)
    gend = arena.f32(8 * 35).rearrange("p (q c) -> p q c", q=8)
    acol = arena.f32(8 * 34).rearrange("p (q c) -> p q c", q=8)
    wkcol = arena.f32(8 * 34).rearrange("p (q c) -> p q c", q=8)
    decay = arena.f32(8 * 34).rearrange("p (q c) -> p q c", q=8)
    B_cols = Buf("cols")
    yfT = arena.bf16(4 * OWN).rearrange("p (g t) -> p g t", g=4)
    B_yfT = Buf("yfT")
    C32 = arena.f32(8 * 2 * 257).rearrange("p (q c e) -> p q c e", q=8, c=2)
    C16 = arena.bf16(8 * 2 * 258).rearrange("p (q c e) -> p q c e", q=8, c=2)
    B_C = [Buf("C%d" % i) for i in range(8)]
    sel_t = arena.f32(2048)
    S.dma("sp", lambda e: e.dma_start(out=sel_t[0:36, :], in_=env["selc"]), writes=[B_const])
    sel = sel_t[0:36, 0:1024].rearrange("r (p m) -> r p m", p=8)
    negsel = sel_t[0:36, 1024:2048].rearrange("r (p m) -> r p m", p=8)
    mB = arena.mark()

    aLI = arena.f32(NT)
    aLF = arena.f32(NT)
    aB = arena.f32(NT)
    aG = arena.f32(NT)
    ones_r = arena.f32(512)
    tmpE = arena.f32(512)
    wgf_s = arena.bf16(8 * 8).rearrange("p (k n) -> p k n", k=8)
    wgb_s = arena.bf16(8 * 72).rearrange("p (k n) -> p k n", k=8)
    bgs = arena.f32(4)
    B_rows = Buf("rows")
    B_wg = Buf("wg")
    B_tmpE = Buf("tmpE")
    for a in (aLI, aLF, aB, aG):
        S.op("pool", (lambda e, a=a: e.memset(a, 0.0)), writes=[B_rows])
    S.op("pool", lambda e: e.memset(ones_r, 1.0), writes=[B_wg])
    S.op("pool", lambda e: e.memset(gend[:, :, :], 0.0), writes=[B_cols])
    for q in range(8):
        S.op("pool", (lambda e, q=q: e.memset(C32[:, q, :, :], 0.0)), writes=[B_C[q]])
        S.op("pool", (lambda e, q=q: e.memset(C16[:, q, :, :], 0.0)), writes=[B_C[q]])
    with nc.allow_non_contiguous_dma(reason="tiny gate weights"):
        pass
    wgate_f = env["wgate_f"]; wgate_b = env["wgate_b"]; bg = env["bg"]
    S.dma("pool", lambda e: e.dma_start(out=wgf_s, in_=wgate_f.rearrange("(k p) n -> p k n", p=128)), writes=[B_wg])
    S.dma("pool", lambda e: e.dma_start(out=wgb_s, in_=wgate_b.rearrange("(k p) n -> p k n", p=128)), writes=[B_wg])
    S.dma("sp", lambda e: e.dma_start(out=bgs[0:36, 0:2], in_=bg), writes=[B_wg])
    S.op("dve", lambda e: e.tensor_scalar(bgs[0:36, 2:3], bgs[0:36, 1:2], -1.0, None, ALU.mult), reads=[B_wg], writes=[B_wg])

    def gate_tile(jc0, W, rhs_fn, r0, r1, wl, wl_lf, pbank, rev=False):
        def pv(bank):
            return ps[bank][r0:r1, W - 1::-1] if rev else ps[bank][r0:r1, 0:W]
        mm(ps[pbank][0:r1, 0:W], [(wl(k), rhs_fn(k)) for k in range(8)], reads=[B_wg] + B_u2, writes=[Bps[pbank]])
        S.op("act", lambda e: e.activation(aLI[r0:r1, jc0:jc0 + W], pv(pbank), AF.Identity, bias=bgs[r0:r1, 0:1]),
             reads=[Bps[pbank], B_wg], writes=[B_rows])
        mm(ps[pbank + 1][0:r1, 0:W], [(wl_lf(k), rhs_fn(k)) for k in range(8)], reads=[B_wg] + B_u2, writes=[Bps[pbank + 1]])
        S.op("act", lambda e: e.activation(tmpE[r0:r1, 0:W], pv(pbank + 1), AF.Exp, bias=bgs[r0:r1, 2:3], scale=-1.0),
             reads=[Bps[pbank + 1], B_wg], writes=[B_tmpE])
        S.op("act", lambda e: e.activation(tmpE[r0:r1, 0:W], tmpE[r0:r1, 0:W], AF.Ln, bias=1.0), reads=[B_tmpE], writes=[B_tmpE])
        S.op("dve", lambda e: e.tensor_scalar(aLF[r0:r1, jc0:jc0 + W], tmpE[r0:r1, 0:W], -1.0, None, ALU.mult),
             reads=[B_tmpE], writes=[B_rows])

    ti = 0
    for (jc0, W) in [(0, 256)] + [(256 + 512 * i, 512) for i in range(4)]:
        gate_tile(jc0, W, (lambda k, jc0=jc0, W=W: u2[:, k, jc0:jc0 + W]), 0, 4,
                  (lambda k: wgf_s[:, k, 0:4]), (lambda k: wgf_s[:, k, 4:8]), 2 * (ti % 2))
        ti += 1
    def rev_u2(k, hi, W):
        return u2[:, k, hi - W + 1:hi + 1]
    gate_tile(0, 256, (lambda k: rev_u2(k, 255, 256)), 32, 36,
              (lambda k: wgb_s[:, k, 0:36]), (lambda k: wgb_s[:, k, 36:72]), 2 * (ti % 2), rev=True)
    ti += 1
    for i in range(8):
        jc0 = 256 + 512 * i
        hi = 4607 - jc0
        gate_tile(jc0, 512, (lambda k, hi=hi: rev_u2(k, hi, 512)), 32, 36,
                  (lambda k: wgb_s[:, k, 0:36]), (lambda k: wgb_s[:, k, 36:72]), 2 * (ti % 2), rev=True)
        ti += 1
    pieces = [(0, 256)] + [(256 + 512 * i, 512) for i in range(8)]
    for pi, (c0, W) in enumerate(pieces):
        init = 0.0 if pi == 0 else aB[0:36, c0 - 1:c0]
        S.op("dve", (lambda e, c0=c0, W=W, init=init: e.tensor_tensor_scan(aB[0:36, c0:c0 + W], ones_r[0:36, 0:W], aLF[0:36, c0:c0 + W],
                                                                           init, ALU.mult, ALU.add)),
             reads=[B_rows, B_wg], writes=[B_rows])
    S.op("dve", lambda e: e.tensor_tensor(aLI[0:36, :], aLI[0:36, :], aB[0:36, :], ALU.subtract), reads=[B_rows], writes=[B_rows])
    for pi, (c0, W) in enumerate(pieces):
        init = 0.0 if pi == 0 else aG[0:36, c0 - 1:c0]
        S.op("dve", (lambda e, c0=c0, W=W, init=init: e.tensor_tensor_scan(aG[0:36, c0:c0 + W], ones_r[0:36, 0:W], aLI[0:36, c0:c0 + W],
                                                                           init, ALU.mult, ALU.max)),
             reads=[B_rows, B_wg], writes=[B_rows])
    S.op("dve", lambda e: e.tensor_tensor(aB[0:36, :], aB[0:36, :], aG[0:36, :], ALU.add), reads=[B_rows], writes=[B_rows])
    if dbg:
        d_rows = dbg_tensor("d_rows", [3, 36, NT])
        for i, a in enumerate((aLI, aG, aB)):
            S.dma("sp", (lambda e, i=i, a=a: e.dma_start(out=d_rows[i], in_=a[0:36, :])), reads=[B_rows])
    def n0_of(sc):
        return 128 if sc == 0 else (0 if sc == 1 else 4480 - 128 * sc)
    for ai, (arr, bank) in enumerate(((aLI, 0), (aB, 2), (aG, 1))):
        S.op("dve", (lambda e, arr=arr: e.tensor_copy(aLF[32:36, 0:256], arr[32:36, 255::-1])), reads=[B_rows], writes=[B_rows])
        S.op("dve", (lambda e, arr=arr: e.tensor_copy(aLF[32:36, 256:NT], arr[32:36, NT - 1:255:-1])), reads=[B_rows], writes=[B_rows])
        def fn(e, arr=arr, bank=bank):
            for sc in range(18):
                ins = e.matmul(ps[bank][:, sc * 4:(sc + 1) * 4], arr[0:4, sc * 128:(sc + 1) * 128], ident[0:4, 0:4],
                               start=True, stop=True)
            for sc in range(34):
                n0 = n0_of(sc)
                ins = e.matmul(ps[bank][:, (18 + sc) * 4:(19 + sc) * 4], aLF[32:36, n0:n0 + 128],
                               ident[32:36, 32:36], start=True, stop=True)
            return ins
        S.op("pe", fn, reads=[B_rows, B_const], writes=[Bps[bank]])
    S.op("dve", lambda e: e.tensor_copy(aLF[0:4, :], aG[0:4, :]), reads=[B_rows], writes=[B_rows])
    S.dma("sp", lambda e: e.dma_start(out=GROW, in_=aLF[0:36, :]), reads=[B_rows], writes=[B_GROW])
    S.op("act", lambda e: e.copy(Rcol[:, :, :], ps[0][:, 0:NCH * 4].rearrange("p (c h) -> p c h", h=4)), reads=[Bps[0]], writes=[B_cols])
    S.op("act", lambda e: e.copy(Gcol[:, :, :], ps[1][:, 0:NCH * 4].rearrange("p (c h) -> p c h", h=4)), reads=[Bps[1]], writes=[B_cols])
    S.op("act", lambda e: e.activation(Ecol[:, :, :], ps[2][:, 0:NCH * 4].rearrange("p (c h) -> p c h", h=4), AF.Exp, scale=-1.0),
         reads=[Bps[2]], writes=[B_cols])
    def fn(e):
        for q in range(8):
            n = 18 if q < 4 else 34
            ins = e.matmul(ps[3][:, q * 34:q * 34 + n], sel[0:36, q, :], aG[0:36, 127:127 + 128 * (n - 1) + 1:128], start=True, stop=True)
        return ins
    S.op("pe", fn, reads=[B_rows, B_const], writes=[Bps[3]])
    for q in range(8):
        n = 18 if q < 4 else 34
        S.op("act", (lambda e, q=q, n=n: e.copy(gend[:, q, 1:1 + n], ps[3][:, q * 34:q * 34 + n])), reads=[Bps[3]], writes=[B_cols])
    tq = arena.f32(34)
    B_tq = Buf("tq")
    for q in range(8):
        n = 18 if q < 4 else 34
        base = 0 if q < 4 else 18
        h = q % 4
        S.op("dve", (lambda e, q=q, n=n, base=base, h=h: e.tensor_tensor(tq[:, 0:n], gend[:, q, 0:n], Gcol[:, base:base + n, h], ALU.subtract)),
             reads=[B_cols], writes=[B_tq])
        S.op("act", (lambda e, q=q, n=n: e.activation(acol[:, q, 0:n], tq[:, 0:n], AF.Exp)), reads=[B_tq], writes=[B_cols])
        S.op("dve", (lambda e, q=q, n=n, base=base, h=h: e.tensor_tensor(tq[:, 0:n], Rcol[:, base:base + n, h], gend[:, q, 1:1 + n], ALU.subtract)),
             reads=[B_cols], writes=[B_tq])
        S.op("act", (lambda e, q=q, n=n: e.activation(wkcol[:, q, 0:n], tq[:, 0:n], AF.Exp)), reads=[B_tq], writes=[B_cols])
        S.op("dve", (lambda e, q=q, n=n: e.tensor_tensor(tq[:, 0:n], gend[:, q, 0:n], gend[:, q, 1:1 + n], ALU.subtract)),
             reads=[B_cols], writes=[B_tq])
        S.op("act", (lambda e, q=q, n=n: e.activation(decay[:, q, 0:n], tq[:, 0:n], AF.Exp)), reads=[B_tq], writes=[B_cols])
    S.barrier()
    arena.reset(mB)
    if stage < 3:
        return
    build_rest2(env, locals())


COL_F = 0
COL_Q = 512
COL_K = 1536
COL_V = 2560
COL_O = 3584
COL_GATES = 4608
COL_BR = 4624


def _dft_consts(flip):
    idx = (63 - np.arange(64)) if flip else np.arange(64)
    ang = 2 * np.pi * np.outer(idx, idx) / 64.0
    Cc = np.cos(ang) / 8.0
    Sc = np.sin(ang) / 8.0
    ch = np.arange(128)
    angc = 2 * np.pi * np.outer(ch, ch) / 128.0
    CS = np.concatenate([np.cos(angc), np.sin(angc)], axis=1) / np.sqrt(128.0)
    M2 = np.zeros((128, 128))
    M2[0:64, 0:64] = Cc
    M2[64:128, 0:64] = -Sc
    M2[0:64, 64:128] = Sc
    M2[64:128, 64:128] = Cc
    M3 = np.zeros((128, 32))
    M3[0:64, :] = Cc[:, 0:32]
    M3[64:128, :] = -Sc[:, 0:32]
    return CS, M2, M3


def make_inputs(inp):
    f32 = np.float32
    x = np.asarray(inp["x"], f32)
    ctx = np.asarray(inp["ctx"], f32)
    c = np.asarray(inp["c"], f32)
    c_ctx = np.asarray(inp["c_ctx"], f32)
    w_in = np.asarray(inp["w_in"], f32)[0]
    b_in = np.asarray(inp["b_in"], f32)[0]
    conv_w = np.asarray(inp["conv_w"], f32)[0]
    conv_b = np.asarray(inp["conv_b"], f32)[0]
    norm_g = np.asarray(inp["norm_g"], f32)[0]

    def fm(v):
        return np.ascontiguousarray(v.reshape(-1, 128).T)

    def r13(w):
        return np.ascontiguousarray(w.reshape(8, 128, 2, NJ, 128).transpose(3, 1, 0, 2, 4).reshape(NJ, 128, 2048))

    def r2(w):
        return np.ascontiguousarray(w.reshape(NJ, 128, 8, 128).transpose(2, 1, 0, 3).reshape(8, 128, FF))

    shared = {
        "w_ada": np.ascontiguousarray(np.asarray(inp["w_ada"], f32)[0]),
        "w13a": r13(np.asarray(inp["w13_a"], f32)[0]), "w2a": r2(np.asarray(inp["w2_a"], f32)[0]),
        "w13b": r13(np.asarray(inp["w13_b"], f32)[0]), "w2b": r2(np.asarray(inp["w2_b"], f32)[0]),
        "wF": np.ascontiguousarray(w_in[:, COL_F:COL_Q]), "wq": np.ascontiguousarray(w_in[:, COL_Q:COL_K]),
        "wk": np.ascontiguousarray(w_in[:, COL_K:COL_V]), "wv": np.ascontiguousarray(w_in[:, COL_V:COL_O]),
        "wo": np.ascontiguousarray(w_in[:, COL_O:COL_GATES]),
        "wgf": np.ascontiguousarray(w_in[:, COL_BR:COL_BR + D]), "wgm": np.ascontiguousarray(w_in[:, COL_BR + D:]),
        "w_four": np.ascontiguousarray(np.asarray(inp["w_four"], f32)[0]),
        "w_mproj": np.ascontiguousarray(np.asarray(inp["w_mproj"], f32)[0]),
        "w_out": np.ascontiguousarray(np.asarray(inp["w_out"], f32)[0]),
        "rowv": np.concatenate([b_in[COL_V:COL_O], b_in[COL_O:COL_GATES], np.asarray(inp["head_g"], f32)[0]])[None, :].copy(),
    }
    sel = np.zeros((36, 8, 128), f32)
    for p in range(8):
        row = (p % 4) + (32 if p >= 4 else 0)
        sel[row, p, :] = 1.0
    selc = np.concatenate([sel.reshape(36, -1), -sel.reshape(36, -1)], axis=1)
    s_idx = np.arange(128)[:, None]
    t_idx = np.arange(128)[None, :]
    maskf = np.where(s_idx <= t_idx, 0.0, NEG).astype(f32)
    maskb = np.where(s_idx >= t_idx, 0.0, NEG).astype(f32)
    maps = []
    for core in range(8):
        b, half = core // 2, core % 2
        flip = half == 1
        xb = x[b][::-1] if flip else x[b]
        cb_ = ctx[b][::-1] if flip else ctx[b]
        g = COL_GATES
        if flip:
            gi_f, gf_f, gi_b, gf_b = g + 8, g + 12, g + 0, g + 4
            cw = conv_w[::-1]
        else:
            gi_f, gf_f, gi_b, gf_b = g + 0, g + 4, g + 8, g + 12
            cw = conv_w
        wgate_f = np.concatenate([w_in[:, gi_f:gi_f + 4], w_in[:, gf_f:gf_f + 4]], axis=1)
        wgate_b = np.zeros((D, 72), f32)
        wgate_b[:, 32:36] = w_in[:, gi_b:gi_b + 4]
        wgate_b[:, 36 + 32:36 + 36] = w_in[:, gf_b:gf_b + 4]
        bgv = np.zeros((36, 2), f32)
        bgv[0:4, 0] = b_in[gi_f:gi_f + 4]
        bgv[0:4, 1] = b_in[gf_f:gf_f + 4]
        bgv[32:36, 0] = b_in[gi_b:gi_b + 4]
        bgv[32:36, 1] = b_in[gf_b:gf_b + 4]
        vecs = np.zeros((128, NV), f32)

        def put(name, arr):
            o, w = VEC[name]
            assert arr.shape == (128, w), (name, arr.shape)
            vecs[:, o:o + w] = arr
        put("bada", fm(np.asarray(inp["b_ada"], f32)[0]))
        put("ng", np.concatenate([fm(norm_g[i]) for i in range(6)], axis=1))
        put("bF", fm(b_in[COL_F:COL_Q])); put("bq", fm(b_in[COL_Q:COL_K])); put("bk", fm(b_in[COL_K:COL_V]))
        put("cwq", np.concatenate([fm(cw[t, 0:D]) for t in range(3)], axis=1))
        put("cwk", np.concatenate([fm(cw[t, D:2 * D]) for t in range(3)], axis=1))
        put("cbq", fm(conv_b[0:D])); put("cbk", fm(conv_b[D:2 * D]))
        put("bgf", fm(b_in[COL_BR:COL_BR + D])); put("bgm", fm(b_in[COL_BR + D:]))
        cvec = np.zeros((128, 8, 2), f32)
        cvec[:, :, 0] = fm(c[b])
        cvec[:, :, 1] = fm(c_ctx)
        CS, M2, M3 = _dft_consts(flip)
        cst = np.zeros((128, 128 * 4 + 256 + 128 + 32), f32)
        cst[:, 0:128] = np.eye(128)
        cst[:, 128:256] = maskf
        cst[:, 256:384] = maskb
        cst[:, 512:768] = CS
        cst[:, 768:896] = M2
        cst[:, 896:928] = M3
        m = dict(shared)
        m.update({
            "xT": np.ascontiguousarray(xb.T), "ctxT": np.ascontiguousarray(cb_.T),
            "cvec": cvec.reshape(128, 16), "vecs": vecs, "bg": bgv,
            "wgate_f": np.ascontiguousarray(wgate_f), "wgate_b": wgate_b, "cst": cst, "selc": selc,
        })
        maps.append(m)
    return maps


def kernel(**inputs):
    nc, _ = build()
    maps = make_inputs(inputs)
    res = run_bass_kernel_spmd(nc, maps, core_ids=list(range(8)))
    out = np.zeros((4, SEQ, D), np.float32)
    for core in range(8):
        b, half = core // 2, core % 2
        o = np.asarray(res.results[core]["outT"]).T
        if half == 0:
            out[b, 0:OWN] = o
        else:
            out[b, OWN:] = o[::-1]
    return out
```

```python
import numpy as np
import os as _os
from contextlib import ExitStack
import concourse.bass as bass
import concourse.mybir as mybir
from concourse.bass_utils import run_bass_kernel_spmd

F32 = mybir.dt.float32
BF16 = mybir.dt.bfloat16
AF = mybir.ActivationFunctionType
ALU = mybir.AluOpType

ENGS = ("pe", "act", "dve", "pool", "sp")
EPOCH = 30000

D = 1024
SEQ = 4096
CTX = 256
NT = CTX + SEQ
OWN = 2048
FF = 2816
NJ = 22
EPS = 1e-6
NEG = -30000.0


class Tick:
    __slots__ = ("sem", "val", "know")

    def __init__(self, sem, val, know):
        self.sem = sem
        self.val = val
        self.know = know


class Buf:
    __slots__ = ("name", "w", "r")

    def __init__(self, name=""):
        self.name = name
        self.w = None
        self.r = {}


class Sched:
    def __init__(self, nc, stack, n_dma_sems=48):
        self.nc = nc
        self.stack = stack
        self.q = {e: [] for e in ENGS}
        self.cnt = {e: 0 for e in ENGS}
        self.esems = {e: [] for e in ENGS}
        self.known = {e: {} for e in ENGS}
        self.dsems = [stack.enter_context(nc.semaphore(f"dma{i}")) for i in range(n_dma_sems)]
        self.dcnt = [0] * n_dma_sems
        self.dlast = [None] * n_dma_sems
        self.drr = 0
        self.drr2 = {}

    def _esem(self, eng, idx):
        lst = self.esems[eng]
        while len(lst) <= idx:
            lst.append(self.stack.enter_context(self.nc.semaphore(f"e_{eng}_{len(lst)}")))
        return lst[idx]

    def _collect(self, eng, reads, writes, extra=()):
        kn = self.known[eng]
        waits = {}

        def need(t):
            if t is None:
                return
            if kn.get(t.sem, 0) >= t.val:
                return
            if waits.get(t.sem, (0, None))[0] < t.val:
                waits[t.sem] = (t.val, t)

        for b in reads:
            need(b.w)
        for b in writes:
            need(b.w)
            for t in b.r.values():
                need(t)
        for t in extra:
            need(t)
        items = sorted(waits.items(), key=lambda kv: -kv[1][0])
        final = []
        for sem, (val, t) in items:
            if kn.get(sem, 0) >= val:
                continue
            final.append((sem, val))
            kn[sem] = val
            for s2, v2 in t.know.items():
                if kn.get(s2, 0) < v2:
                    kn[s2] = v2
        return final

    def op(self, eng, fn, reads=(), writes=(), extra=()):
        waits = self._collect(eng, reads, writes, extra)
        c = self.cnt[eng]
        sem = self._esem(eng, c // EPOCH)
        val = c % EPOCH + 1
        self.cnt[eng] = c + 1
        t = Tick(sem, val, dict(self.known[eng]))
        for b in reads:
            b.r[sem] = t
        for b in writes:
            b.w = t
            b.r = {}
        self.q[eng].append((waits, fn, sem, 1))
        return t

    def dma(self, eng, fn, reads=(), writes=(), extra=()):
        n = len(self.dsems)
        lo, hi = (0, n // 3) if eng == "pool" else (n // 3, n)
        rr = self.drr2.get(eng, lo)
        i = rr
        self.drr2[eng] = lo + (rr + 1 - lo) % (hi - lo)
        ex = list(extra)
        if self.dlast[i] is not None:
            ex.append(self.dlast[i])
        waits = self._collect(eng, reads, writes, ex)
        self.dcnt[i] += 1
        sem = self.dsems[i]
        t = Tick(sem, 16 * self.dcnt[i], dict(self.known[eng]))
        self.dlast[i] = t
        for b in reads:
            b.r[sem] = t
        for b in writes:
            b.w = t
            b.r = {}
        self.q[eng].append((waits, fn, sem, 16))
        return t

    def wait_all(self, eng, ticks):
        waits = self._collect(eng, (), (), ticks)
        self.q[eng].append((waits, None, None, 0))

    def barrier(self, skip=()):
        skipset = {(t.sem, t.val) for t in skip}
        ticks = []
        for e in ENGS:
            c = self.cnt[e]
            if c > 0:
                ticks.append(Tick(self._esem(e, (c - 1) // EPOCH), (c - 1) % EPOCH + 1, {}))
        for t in self.dlast:
            if t is not None and (t.sem, t.val) not in skipset:
                ticks.append(t)
        for e in ENGS:
            self.wait_all(e, ticks)

    def emit(self):
        nc = self.nc
        q = self.q

        def run(engobj, lst):
            for waits, fn, sem, amt in lst:
                for s, v in waits:
                    engobj.wait_ge(s, v)
                if fn is not None:
                    ins = fn(engobj)
                    ins.then_inc(sem, amt)

        with nc.Block() as block:
            @block.tensor
            def _(e):
                run(e, q["pe"])

            @block.scalar
            def _(e):
                run(e, q["act"])

            @block.vector
            def _(e):
                run(e, q["dve"])

            @block.gpsimd
            def _(e):
                run(e, q["pool"])

            @block.sync
            def _(e):
                run(e, q["sp"])


class Arena:
    def __init__(self, nc, name, words):
        self.t = nc.alloc_sbuf_tensor(name, [128, words], F32)
        self.words = words
        self.off = 0

    def mark(self):
        return self.off

    def reset(self, m):
        self.off = m

    def f32(self, n):
        a = self.t[:, self.off:self.off + n]
        self.off += n
        assert self.off <= self.words, ("arena overflow", self.off, self.words)
        return a

    def bf16(self, n):
        w = (n + 1) // 2
        a = self.t[:, self.off:self.off + w].bitcast(BF16)
        self.off += w
        assert self.off <= self.words, ("arena overflow", self.off, self.words)
        return a[:, 0:n]


VEC = {}
_o = 0
for _n, _w in [("bada", 72), ("ng", 48), ("bF", 4), ("bq", 8), ("bk", 8), ("cwq", 24), ("cwk", 24),
               ("cbq", 8), ("cbk", 8), ("bgf", 8), ("bgm", 8)]:
    VEC[_n] = (_o, _w)
    _o += _w
NV = _o


def build(stage=99, dbg=False):
    nc = bass.Bass("TRN2", target_bir_lowering=False)
    dt_in = lambda name, shape, dt=F32: nc.dram_tensor(name, shape, dt, kind="ExternalInput").ap()
    xT = dt_in("xT", [D, SEQ])
    ctxT = dt_in("ctxT", [D, CTX])
    cvec = dt_in("cvec", [128, 16])
    w_ada = dt_in("w_ada", [D, 9 * D])
    vecs = dt_in("vecs", [128, NV])
    rowv = dt_in("rowv", [1, 3 * D])
    bg = dt_in("bg", [36, 2])
    w13a = dt_in("w13a", [NJ, 128, 2048])
    w2a = dt_in("w2a", [8, 128, FF])
    w13b = dt_in("w13b", [NJ, 128, 2048])
    w2b = dt_in("w2b", [8, 128, FF])
    wF = dt_in("wF", [D, 512])
    wq = dt_in("wq", [D, D])
    wk = dt_in("wk", [D, D])
    wv = dt_in("wv", [D, D])
    wo = dt_in("wo", [D, D])
    wgf = dt_in("wgf", [D, D])
    wgm = dt_in("wgm", [D, D])
    wgate_f = dt_in("wgate_f", [D, 8])
    wgate_b = dt_in("wgate_b", [D, 72])
    w_four = dt_in("w_four", [512, D])
    w_mproj = dt_in("w_mproj", [D, D])
    w_out = dt_in("w_out", [D, D])
    cst = dt_in("cst", [128, 128 * 4 + 256 + 128 + 32])
    selc = dt_in("selc", [36, 2 * 8 * 128])
    outT = nc.dram_tensor("outT", [D, OWN], F32, kind="ExternalOutput").ap()
    dbg_out = {}

    def dbg_tensor(name, shape, dt=F32):
        dbg_out[name] = nc.dram_tensor(name, shape, dt, kind="ExternalOutput").ap()
        return dbg_out[name]

    dscr = lambda name, shape, dt: nc.dram_tensor(name, shape, dt, kind="Internal").ap()
    H1 = dscr("H1", [D, OWN], F32)
    AB = dscr("AB", [2, SEQ, 512], F32)
    PQ = dscr("PQ", [2, 64, 64, 512], F32)
    KT = dscr("KT", [D, OWN], BF16)
    QT = dscr("QT", [D, OWN], BF16)
    KTOK = dscr("KTOK", [NT, D], BF16)
    VTOK = dscr("VTOK", [NT, D], BF16)
    B_H1, B_AB, B_PQ, B_KT, B_QT, B_KTOK, B_VTOK = [Buf(n) for n in "H1 AB PQ KT QT KTOK VTOK".split()]

    st = ExitStack()
    with st:
        S = Sched(nc, st)
        ps = [st.enter_context(nc.psum_tensor(f"ps{i}", [128, 512], F32)) for i in range(8)]
        Bps = [Buf(f"ps{i}") for i in range(8)]

        cs_t = nc.alloc_sbuf_tensor("cs", [128, 128 * 4 + 256 + 128 + 32], F32)
        ident = cs_t[:, 0:128]
        maskf = cs_t[:, 128:256]
        maskb = cs_t[:, 256:384]
        CS = cs_t[:, 512:768]
        M2 = cs_t[:, 768:896]
        M3 = cs_t[:, 896:928]
        vec_t = nc.alloc_sbuf_tensor("vec", [128, NV], F32)
        mod_t = nc.alloc_sbuf_tensor("mod", [128, 72 * 2], F32)
        der_t = nc.alloc_sbuf_tensor("der", [128, 16 * 8], F32)
        id16_t = nc.alloc_sbuf_tensor("id16", [128, 128], BF16)
        ones16_t = nc.alloc_sbuf_tensor("ones16", [128, 128], BF16)
        u2_t = nc.alloc_sbuf_tensor("u2", [128, 8 * NT], BF16)
        u2 = u2_t[:, :].rearrange("p (k t) -> p k t", k=8)
        B_const = Buf("const")
        B_mod = Buf("mod")
        B_der = Buf("der")
        tiles = [(0, CTX, 1)] + [(CTX + 512 * i, 512, 0) for i in range(8)]
        B_u2 = [Buf(f"u2_{i}") for i in range(9)]

        def vcol(name, i=0, n=1):
            o, w = VEC[name]
            return vec_t[:, o + i:o + i + n]

        DER = {}
        _d = 0
        for nm in ["A0l", "A0c", "S0l", "S0c", "PAl", "PAc", "A2l", "A2c", "S2l", "S2c", "PMl", "A4l", "S4l", "PBl"]:
            DER[nm] = der_t[:, _d * 8:(_d + 1) * 8]
            _d += 1

        arena = Arena(nc, "arena", 34000)

        S.dma("sp", lambda e: e.dma_start(out=cs_t[:, :], in_=cst), writes=[B_const])
        S.dma("sp", lambda e: e.dma_start(out=vec_t[:, :], in_=vecs), writes=[B_const])
        S.op("act", lambda e: e.copy(id16_t[:, :], ident), reads=[B_const], writes=[B_const])
        S.op("pool", lambda e: e.memset(ones16_t[:, :], 1.0), writes=[B_const])

        m0 = arena.mark()
        cv = arena.f32(16)
        scv = arena.f32(16)
        B_cv = Buf("cv")
        S.dma("sp", lambda e: e.dma_start(out=cv, in_=cvec), writes=[B_cv])
        S.op("act", lambda e: e.activation(scv, cv, AF.Silu), reads=[B_cv], writes=[B_cv])
        wad = [arena.f32(8 * 1024) for _ in range(2)]
        B_wad = [Buf("wad0"), Buf("wad1")]
        w_ada_v = w_ada.rearrange("(k p) n -> p k n", p=128)
        modps = ps[7][:, 0:144]
        for mi in range(9):
            sl = mi % 2
            wv_ = wad[sl].rearrange("p (k n) -> p k n", k=8)
            S.dma("sp", (lambda e, wv_=wv_, mi=mi: e.dma_start(out=wv_, in_=w_ada_v[:, :, mi * 1024:(mi + 1) * 1024])),
                  writes=[B_wad[sl]])
            for dc in range(8):
                def fn(e, wv_=wv_, mi=mi, dc=dc):
                    for k in range(8):
                        ins = e.matmul(modps[:, (mi * 8 + dc) * 2:(mi * 8 + dc) * 2 + 2],
                                       wv_[:, k, dc * 128:(dc + 1) * 128],
                                       scv[:, k * 2:k * 2 + 2], start=(k == 0), stop=(k == 7))
                    return ins
                S.op("pe", fn, reads=[B_wad[sl], B_cv], writes=[Bps[7]])
        modv = mod_t[:, :].rearrange("p (m j) -> p m j", j=2)
        modpsv = modps.rearrange("p (m j) -> p m j", j=2)
        bada = vcol("bada", 0, 72)
        for j in range(2):
            S.op("dve", (lambda e, j=j: e.tensor_tensor(modv[:, :, j], modpsv[:, :, j], bada, ALU.add)),
                 reads=[Bps[7], B_const], writes=[B_mod])

        def modc(mi, j):
            return modv[:, mi * 8:(mi + 1) * 8, j]

        def ng(i):
            return vcol("ng", i * 8, 8)

        def der_scale(name, mi, gi, j):
            S.op("dve", lambda e: e.scalar_tensor_tensor(DER[name], modc(mi, j), 1.0, ng(gi), ALU.add, ALU.mult),
                 reads=[B_mod, B_const], writes=[B_der])

        def der_gate(name, mi, gi, j, f):
            S.op("dve", lambda e: e.scalar_tensor_tensor(DER[name], modc(mi, j), f, ng(gi), ALU.mult, ALU.mult),
                 reads=[B_mod, B_const], writes=[B_der])

        def der_copy(name, mi, j):
            S.op("dve", lambda e: e.tensor_copy(DER[name], modc(mi, j)), reads=[B_mod], writes=[B_der])

        der_scale("A0l", 1, 0, 0); der_scale("A0c", 1, 0, 1)
        der_copy("S0l", 0, 0); der_copy("S0c", 0, 1)
        der_gate("PAl", 2, 1, 0, 0.5); der_gate("PAc", 2, 1, 1, 0.5)
        der_scale("A2l", 4, 2, 0); der_scale("A2c", 4, 2, 1)
        der_copy("S2l", 3, 0); der_copy("S2c", 3, 1)
        der_gate("PMl", 5, 3, 0, 1.0)
        der_scale("A4l", 7, 4, 0); der_copy("S4l", 6, 0); der_gate("PBl", 8, 5, 0, 0.5)
        S.barrier()
        arena.reset(m0)

        def rstd_from(sq_tile, B_sq, W, rstd, B_rstd, pbank):
            def fn(e):
                for k in range(8):
                    ins = e.matmul(ps[pbank][:, 0:W], ones16_t[:, :], sq_tile[:, k, 0:W], start=(k == 0), stop=(k == 7))
                return ins
            S.op("pe", fn, reads=[B_sq, B_const], writes=[Bps[pbank]])
            S.op("act", lambda e: e.activation(rstd[:, 0:W], ps[pbank][:, 0:W], AF.Ln, bias=EPS, scale=1.0 / D),
                 reads=[Bps[pbank]], writes=[B_rstd])
            S.op("act", lambda e: e.activation(rstd[:, 0:W], rstd[:, 0:W], AF.Exp, scale=-0.5),
                 reads=[B_rstd], writes=[B_rstd])

        def norm_mod_thunks(src, B_src, W, rstd, B_rstd, A, Sh, dst_fn, B_dst, tmp, B_tmp):
            def one(k):
                t = tmp[k % 2]
                bt = B_tmp[k % 2]
                S.op("dve", (lambda e: e.tensor_tensor(t[:, 0:W], src[:, k, 0:W], rstd[:, 0:W], ALU.mult)),
                     reads=[B_src, B_rstd], writes=[bt])
                S.op("act", (lambda e: e.activation(dst_fn(k), t[:, 0:W], AF.Identity, bias=Sh[:, k:k + 1], scale=A[:, k:k + 1])),
                     reads=[bt, B_der], writes=[B_dst])
            return [(lambda k=k: one(k)) for k in range(8)]

        def norm_mod(src, B_src, W, rstd, B_rstd, A, Sh, dst_fn, B_dst, tmp, B_tmp):
            for k in range(8):
                t = tmp[k % 2]
                bt = B_tmp[k % 2]
                S.op("dve", (lambda e, k=k, t=t: e.tensor_tensor(t[:, 0:W], src[:, k, 0:W], rstd[:, 0:W], ALU.mult)),
                     reads=[B_src, B_rstd], writes=[bt])
                S.op("act", (lambda e, k=k, t=t: e.activation(dst_fn(k), t[:, 0:W], AF.Identity,
                                                              bias=Sh[:, k:k + 1], scale=A[:, k:k + 1])),
                     reads=[bt, B_der], writes=[B_dst])

        WC = {}

        def wload(dst, B_dst, src_f32, cache, key, ncols):
            if cache is None:
                S.dma("pool", lambda e: e.dma_start(out=dst, in_=src_f32), writes=[B_dst])
                return
            k = (cache,) + key
            if k not in WC:
                sc_ = nc.dram_tensor("wc_" + "_".join(str(z) for z in k), [128, ncols], BF16, kind="Internal").ap()
                WC[k] = (sc_, Buf("wc"))
                S.dma("pool", lambda e: e.dma_start(out=dst, in_=src_f32), writes=[B_dst])
                dflat = dst if len(dst.shape) == 2 else dst.rearrange("p a b -> p (a b)")
                S.dma("sp", lambda e: e.dma_start(out=sc_, in_=dflat), reads=[B_dst], writes=[WC[k][1]])
            else:
                sc_, bsc = WC[k]
                dflat = dst if len(dst.shape) == 2 else dst.rearrange("p a b -> p (a b)")
                S.dma("sp", lambda e: e.dma_start(out=dflat, in_=sc_), reads=[bsc], writes=[B_dst])

        def ffn(src, B_src, W, rstd_pre, B_rstd_pre, A, Sh, PG, w13r, w2r, bufs, cache, do_pre=True, hook1=None, hook2=None, defer_epi=False):
            (sq, B_sq, u, B_u, g, B_g, y, B_y, tmp, B_tmp, rs2, B_rs2, wb13, B_wb13, wb2, B_wb2, sa, B_sa) = bufs
            if do_pre:
                norm_mod(src, B_src, W, rstd_pre, B_rstd_pre, A, Sh, lambda k: u[:, k, 0:W], B_u, tmp, B_tmp)
            n13 = len(wb13)
            for j in range(NJ):
                sl = j % n13
                wbv = wb13[sl].rearrange("p (k c) -> p k c", k=8)
                wload(wb13[sl], B_wb13[sl], w13r[j], cache, ("w13", j), 2048)
                pa = j % 2
                pb = 2 + j % 2

                def fa(e, wbv=wbv, pa=pa):
                    for k in range(8):
                        ins = e.matmul(ps[pa][:, 0:W], wbv[:, k, 0:128], u[:, k, 0:W], start=(k == 0), stop=(k == 7))
                    return ins

                def fb(e, wbv=wbv, pb=pb):
                    for k in range(8):
                        ins = e.matmul(ps[pb][:, 0:W], wbv[:, k, 128:256], u[:, k, 0:W], start=(k == 0), stop=(k == 7))
                    return ins
                S.op("pe", fa, reads=[B_wb13[sl], B_u], writes=[Bps[pa]])
                S.op("pe", fb, reads=[B_wb13[sl], B_u], writes=[Bps[pb]])
                s2 = j % 2
                S.op("act", (lambda e, pa=pa, s2=s2: e.activation(sa[s2][:, 0:W], ps[pa][:, 0:W], AF.Silu)),
                     reads=[Bps[pa]], writes=[B_sa[s2]])
                S.op("dve", (lambda e, pb=pb, s2=s2, j=j: e.tensor_tensor(g[:, j, 0:W], sa[s2][:, 0:W], ps[pb][:, 0:W], ALU.mult)),
                     reads=[B_sa[s2], Bps[pb]], writes=[B_g])
                if hook1 is not None:
                    hook1(j)
            n2 = len(wb2)
            HJ = NJ // 2
            for i in range(8):
                halves = []
                for hf in range(2):
                    sl = (2 * i + hf) % n2
                    wload(wb2[sl], B_wb2[sl], w2r[i][:, hf * HJ * 128:(hf + 1) * HJ * 128], cache, ("w2", i, hf), HJ * 128)
                    halves.append((wb2[sl].rearrange("p (j c) -> p j c", j=HJ), B_wb2[sl]))
                py = 4 + i % 2

                def fy(e, halves=halves, py=py):
                    for j in range(NJ):
                        wv_ = halves[j // HJ][0]
                        ins = e.matmul(ps[py][:, 0:W], wv_[:, j % HJ, :], g[:, j, 0:W], start=(j == 0), stop=(j == NJ - 1))
                    return ins
                S.op("pe", fy, reads=[halves[0][1], halves[1][1], B_g], writes=[Bps[py]])
                S.op("act", (lambda e, py=py, i=i: e.copy(y[:, i, 0:W], ps[py][:, 0:W])), reads=[Bps[py]], writes=[B_y])
                S.op("act", (lambda e, py=py, i=i: e.activation(sq[:, i, 0:W], ps[py][:, 0:W], AF.Square)),
                     reads=[Bps[py]], writes=[B_sq])
                if hook2 is not None:
                    hook2(i)
            rstd_from(sq, B_sq, W, rs2, B_rs2, 7)

            def resid(i):
                t = tmp[i % 2]
                bt = B_tmp[i % 2]
                S.op("dve", (lambda e: e.tensor_tensor(t[:, 0:W], y[:, i, 0:W], rs2[:, 0:W], ALU.mult)),
                     reads=[B_y, B_rs2], writes=[bt])
                S.op("dve", (lambda e: e.scalar_tensor_tensor(src[:, i, 0:W], t[:, 0:W], PG[:, i:i + 1],
                                                              src[:, i, 0:W], ALU.mult, ALU.add)),
                     reads=[bt, B_der], writes=[B_src])
            thunks = [(lambda i=i: resid(i)) for i in range(8)]
            if defer_epi:
                return thunks
            for th in thunks:
                th()
            return []

        def alloc_ffn_bufs():
            sq = arena.bf16(8 * 512).rearrange("p (k t) -> p k t", k=8)
            u = arena.bf16(8 * 512).rearrange("p (k t) -> p k t", k=8)
            g = arena.bf16(NJ * 512).rearrange("p (k t) -> p k t", k=NJ)
            y = arena.f32(8 * 512).rearrange("p (k t) -> p k t", k=8)
            tmp = [arena.f32(512) for _ in range(2)]
            rs2 = arena.f32(512)
            wb13 = [arena.bf16(2048) for _ in range(3)]
            wb2 = [arena.bf16(FF // 2) for _ in range(4)]
            sa = [arena.f32(512) for _ in range(2)]
            return (sq, Buf("sq"), u, Buf("u"), g, Buf("g"), y, Buf("y"), tmp, [Buf("t0"), Buf("t1")],
                    rs2, Buf("rs2"), wb13, [Buf("wb13_%d" % i) for i in range(3)], wb2, [Buf("wb2_%d" % i) for i in range(4)],
                    sa, [Buf("sa0"), Buf("sa1")])

        mA = arena.mark()
        xt = [arena.f32(8 * 512).rearrange("p (k t) -> p k t", k=8) for _ in range(2)]
        B_xt = [Buf("xt0"), Buf("xt1")]
        rs1 = arena.f32(512)
        B_rs1 = Buf("rs1")
        fb = alloc_ffn_bufs()
        tmp, B_tmp, u_, B_u_ = fb[8], fb[9], fb[2], fb[3]
        sqx = arena.bf16(8 * 512).rearrange("p (k t) -> p k t", k=8)
        B_sqx = Buf("sqx")
        xT_v = xT.rearrange("(k p) t -> p k t", p=128)
        ctxT_v = ctxT.rearrange("(k p) t -> p k t", p=128)
        if dbg:
            d_u2 = dbg_tensor("d_u2", [128, 8 * NT], BF16)
        ntile = len(tiles) if stage >= 1 else 0

        def a1_load(ti):
            c0, W, j = tiles[ti]
            x = xt[ti % 2]
            src = ctxT_v[:, :, 0:W] if j == 1 else xT_v[:, :, c0 - CTX:c0 - CTX + W]
            S.dma("pool", lambda e: e.dma_start(out=x[:, :, 0:W], in_=src), writes=[B_xt[ti % 2]])

        def sq_thunks(x, bx, W):
            return [(lambda k=k: S.op("act", (lambda e: e.activation(sqx[:, k, 0:W], x[:, k, 0:W], AF.Square)), reads=[bx], writes=[B_sqx]))
                    for k in range(8)]

        def a1_pre_thunks(ti):
            c0, W, j = tiles[ti]
            x = xt[ti % 2]; bx = B_xt[ti % 2]
            sfx = "c" if j == 1 else "l"
            th = sq_thunks(x, bx, W)
            th.append(lambda: rstd_from(sqx, B_sqx, W, rs1, B_rs1, 6))
            th += norm_mod_thunks(x, bx, W, rs1, B_rs1, DER["A0" + sfx], DER["S0" + sfx], lambda k: u_[:, k, 0:W], B_u_, tmp, B_tmp)
            return th

        def a1_epi2_thunks(ti):
            c0, W, j = tiles[ti]
            x = xt[ti % 2]; bx = B_xt[ti % 2]
            sfx = "c" if j == 1 else "l"
            th = []
            if 1 <= ti <= 4:
                o0 = c0 - CTX
                th.append(lambda: S.dma("pool", lambda e: e.dma_start(out=H1.rearrange("(k p) t -> p k t", p=128)[:, :, o0:o0 + 512], in_=x[:, :, :]),
                                        reads=[bx], writes=[B_H1]))
            th += sq_thunks(x, bx, W)
            th.append(lambda: rstd_from(sqx, B_sqx, W, rs1, B_rs1, 6))
            th += norm_mod_thunks(x, bx, W, rs1, B_rs1, DER["A2" + sfx], DER["S2" + sfx], (lambda k: u2[:, k, c0:c0 + W]), B_u2[ti], tmp, B_tmp)
            return th

        pend1 = []
        pend2 = []
        if ntile:
            a1_load(0)
            for th in a1_pre_thunks(0):
                th()
            if ntile > 1:
                a1_load(1)
        for ti in range(ntile):
            c0, W, j = tiles[ti]
            sfx = "c" if j == 1 else "l"
            if ti + 1 < ntile:
                pend2 = a1_pre_thunks(ti + 1)

            def hook1(j_):
                n = 2 if len(pend1) > (NJ - 1 - j_) else 1
                for _ in range(n):
                    if pend1:
                        pend1.pop(0)()

            def hook2(i_):
                while pend1:
                    pend1.pop(0)()
                if i_ >= 1:
                    for _ in range(3):
                        if pend2:
                            pend2.pop(0)()
            epi1 = ffn(xt[ti % 2], B_xt[ti % 2], W, None, None, None, None, DER["PA" + sfx], w13a, w2a, fb, "A",
                       do_pre=False, hook1=hook1, hook2=hook2, defer_epi=True)
            while pend1:
                pend1.pop(0)()
            while pend2:
                pend2.pop(0)()
            pend1 = list(epi1) + a1_epi2_thunks(ti)
            if ti + 2 < ntile:
                pend1.append(lambda ti=ti: a1_load(ti + 2))
        while pend1:
            pend1.pop(0)()
        S.barrier()
        arena.reset(mA)
        if dbg:
            S.dma("sp", lambda e: e.dma_start(out=d_u2, in_=u2_t[:, :]), reads=B_u2)
            d_h1 = dbg_tensor("d_h1", [D, OWN])
            S.dma("sp", lambda e: e.dma_start(out=d_h1, in_=H1), reads=[B_H1])

        env = dict(locals())
        if stage >= 2:
            build_rest(env)
        S.barrier()
        S.emit()
    return nc, dbg_out


def build_rest3(g_, L):
    AX = mybir.AxisListType
    nc = g_["nc"]; S = g_["S"]; ps = g_["ps"]; Bps = g_["Bps"]; arena = g_["arena"]
    u2 = g_["u2"]; B_u2 = g_["B_u2"]; vcol = g_["vcol"]; DER = g_["DER"]
    id16 = g_["id16"]; B_const = g_["B_const"]; mm = g_["mm"]
    H1, HS = g_["H1"], g_["HS"]; B_H1 = g_["B_H1"]; B_HS = g_["B_HS"]
    yfT, B_yfT, mB = g_["yfT"], g_["B_yfT"], g_["mB"]
    rowv = g_["rowv"]; outT = g_["outT"]
    ffn = g_["ffn"]; rstd_from = g_["rstd_from"]; wload = g_["wload"]
    kp = lambda ap: ap.rearrange("(k p) n -> p k n", p=128)
    A = arena.t
    arena.reset(mB)
    tmp = [A[:, 0:512], A[:, 512:1024]]; B_tmp = [Buf("ct0"), Buf("ct1")]
    rs2 = A[:, 1024:1536]; B_rs2 = Buf("crs2")
    yreg = A[:, 5816:9912]
    B_yreg = Buf("yreg")
    y = yreg.rearrange("p (k t) -> p k t", k=8)
    hs_t = yreg[:, 0:1024]; o_sb = yreg[:, 1024:2048]; sig = yreg[:, 2048:3072]; sqh = yreg[:, 3072:4096]
    B_hsC = Buf("c_hs"); B_osb = Buf("c_osb"); B_sig = Buf("c_sig"); B_sqh = Buf("c_sqh")
    bo_bc = A[:, 9912:10936]; hg_bc = A[:, 10936:11960]
    B_bc = Buf("cbc")
    x = arena.f32(4096).rearrange("p (k t) -> p k t", k=8); B_x = Buf("cx")
    rs1 = arena.f32(512); B_rs1 = Buf("crs1")
    sq = arena.bf16(4096).rearrange("p (k t) -> p k t", k=8); B_sq = Buf("csq")
    u = arena.bf16(4096).rearrange("p (k t) -> p k t", k=8); B_u = Buf("cu")
    hmT = sq; yT = u
    sa = [arena.f32(512), arena.f32(512)]; B_sa = [Buf("csa0"), Buf("csa1")]
    mX = arena.mark()
    g = arena.bf16(NJ * 512).rearrange("p (k t) -> p k t", k=NJ); B_g = Buf("cg")
    wb13m = [arena.bf16(2048), arena.bf16(2048), arena.bf16(2048)]; B_wb13m = [Buf("cw13_0"), Buf("cw13_1"), Buf("cw13_2")]
    wb2m = arena.bf16(FF); B_wb2m = Buf("cw2")
    wb2x = A[:, 11992:11992 + 704].bitcast(BF16), A[:, 11992 + 704:11992 + 1408].bitcast(BF16)
    arena.reset(mX)
    wsl = [arena.bf16(8 * 512).rearrange("p (k n) -> p k n", k=8) for _ in range(4)]; B_wsl = [Buf("wsl%d" % i) for i in range(4)]
    hm = arena.bf16(1024); B_hm = Buf("hm")
    ol = A[:, mX + 4096:mX + 8192].rearrange("p (k t) -> p k t", k=8)
    B_ol = [B_wsl[2], B_wsl[3]]
    ss = rs2[:, 0:8]
    fb = (sq, B_sq, u, B_u, g, B_g, y, B_yreg, tmp, B_tmp, rs2, B_rs2, wb13m, B_wb13m,
          [wb2m[:, 0:FF // 2], wb2m[:, FF // 2:FF], wb2x[0], wb2x[1]], [Buf('cw2a'), Buf('cw2b'), Buf('cw2c'), Buf('cw2d')], sa, B_sa)
    xflat = A[:, mB:mB + 4096]
    rv = xflat[:, 0:3072]; ones1 = xflat[:, 3072:3200]
    S.dma("sp", lambda e: e.dma_start(out=rv[0:1, :], in_=rowv), writes=[B_x])
    S.op("pool", lambda e: e.memset(ones1[0:1, :], 1.0), writes=[B_x])
    for (dst, seg) in ((bo_bc, 1), (hg_bc, 2)):
        for hh in range(2):
            mm(ps[6][:, 0:512], [(ones1[0:1, :], rv[0:1, seg * 1024 + hh * 512:seg * 1024 + hh * 512 + 512])], reads=[B_x], writes=[Bps[6]])
            S.op("act", (lambda e, hh=hh, dst=dst: e.copy(dst[:, hh * 512:(hh + 1) * 512], ps[6][:, 0:512])), reads=[Bps[6]], writes=[B_bc])
    S.barrier()
    w_four, w_mproj, w_out, wo, wgf, wgm = [g_[n] for n in "w_four w_mproj w_out wo wgf wgm".split()]
    w13b, w2b = g_["w13b"], g_["w2b"]
    H1v = H1.rearrange("(k p) t -> p k t", p=128)
    outv = outT.rearrange("(k p) t -> p k t", p=128)
    for T in range(4):
        t0 = 512 * T
        c0 = CTX + t0
        for hh in range(2):
            wload(wsl[hh], B_wsl[hh], kp(wo)[:, :, hh * 512:(hh + 1) * 512], "C", ("wo", hh), 4096)
        for ch in range(4):
            cc = c0 + 128 * ch
            ob = t0 + 128 * ch
            S.dma("sp", (lambda e, ob=ob: e.dma_start(out=hs_t, in_=HS[ob:ob + 128, :])), reads=[B_HS[ob // 128]], writes=[B_hsC])
            for hh in range(2):
                mm(ps[hh][:, 0:512], [(u2[:, k, cc:cc + 128], wsl[hh][:, k, :]) for k in range(8)], reads=[B_wsl[hh]] + B_u2, writes=[Bps[hh]])
                S.op("dve", (lambda e, hh=hh: e.tensor_tensor(o_sb[:, hh * 512:(hh + 1) * 512], ps[hh][:, 0:512], bo_bc[:, hh * 512:(hh + 1) * 512], ALU.add)),
                     reads=[Bps[hh], B_bc], writes=[B_osb])
            S.op("act", lambda e: e.activation(sig, o_sb, AF.Sigmoid), reads=[B_osb], writes=[B_sig])
            S.op("dve", lambda e: e.tensor_tensor(sig, sig, hg_bc, ALU.mult), reads=[B_sig, B_bc], writes=[B_sig])
            S.op("act", lambda e: e.activation(sqh, hs_t, AF.Square), reads=[B_hsC], writes=[B_sqh])
            S.op("dve", lambda e: e.reduce_sum(ss[:, 0:4], sqh.rearrange("p (h e) -> p h e", h=4), AX.X), reads=[B_sqh], writes=[B_rs2])
            S.op("act", lambda e: e.activation(ss[:, 0:4], ss[:, 0:4], AF.Ln, bias=EPS, scale=1.0 / 256), reads=[B_rs2], writes=[B_rs2])
            S.op("act", lambda e: e.activation(ss[:, 0:4], ss[:, 0:4], AF.Exp, scale=-0.5), reads=[B_rs2], writes=[B_rs2])
            for h in range(4):
                S.op("dve", (lambda e, h=h: e.scalar_tensor_tensor(hm[:, h * 256:(h + 1) * 256], hs_t[:, h * 256:(h + 1) * 256], ss[:, h:h + 1],
                                                                  sig[:, h * 256:(h + 1) * 256], ALU.mult, ALU.mult)),
                     reads=[B_hsC, B_sig, B_rs2], writes=[B_hm])
            for half in range(2):
                pb = 2 + half
                def fn(e, half=half, pb=pb):
                    for cq in range(4):
                        c = half * 4 + cq
                        ins = e.matmul(ps[pb][:, cq * 128:(cq + 1) * 128], hm[:, c * 128:(c + 1) * 128], id16[:, :], start=True, stop=True)
                    return ins
                S.op("pe", fn, reads=[B_hm, B_const], writes=[Bps[pb]])
                S.op("act", (lambda e, half=half, pb=pb, ch=ch: e.copy(hmT[:, half * 4:(half + 1) * 4, ch * 128:(ch + 1) * 128],
                                                                      ps[pb][:, 0:512].rearrange("p (c t) -> p c t", c=4))),
                     reads=[Bps[pb]], writes=[B_sq])
        for hh in range(2):
            cs_ = slice(hh * 512, (hh + 1) * 512)
            wload(wsl[0][:, 0:4, :], B_wsl[0], kp(w_four)[:, :, cs_], "C", ("w4", hh), 2048)
            wload(wsl[1], B_wsl[1], kp(w_mproj)[:, :, cs_], "C", ("wm", hh), 4096)
            wload(wsl[2], B_wsl[2], kp(wgf)[:, :, cs_], "C", ("wgf", hh), 4096)
            wload(wsl[3], B_wsl[3], kp(wgm)[:, :, cs_], "C", ("wgm", hh), 4096)
            for ii in range(4):
                i = hh * 4 + ii
                cw = slice(ii * 128, (ii + 1) * 128)
                mm(ps[0][:, 0:512], [(wsl[0][:, gq, cw], yfT[:, gq, t0:t0 + 512]) for gq in range(4)], reads=[B_wsl[0], B_yfT], writes=[Bps[0]])
                mm(ps[1][:, 0:512], [(wsl[1][:, k, cw], hmT[:, k, :]) for k in range(8)], reads=[B_wsl[1], B_sq], writes=[Bps[1]])
                mm(ps[2][:, 0:512], [(wsl[2][:, k, cw], u2[:, k, c0:c0 + 512]) for k in range(8)], reads=[B_wsl[2]] + B_u2, writes=[Bps[2]])
                mm(ps[3][:, 0:512], [(wsl[3][:, k, cw], u2[:, k, c0:c0 + 512]) for k in range(8)], reads=[B_wsl[3]] + B_u2, writes=[Bps[3]])
                S.op("act", (lambda e, i=i: e.activation(sa[0], ps[2][:, 0:512], AF.Sigmoid, bias=vcol("bgf", i, 1))), reads=[Bps[2], B_const], writes=[B_sa[0]])
                S.op("act", (lambda e, i=i: e.activation(sa[1], ps[3][:, 0:512], AF.Sigmoid, bias=vcol("bgm", i, 1))), reads=[Bps[3], B_const], writes=[B_sa[1]])
                S.op("dve", lambda e: e.tensor_tensor(sa[0], sa[0], ps[0][:, 0:512], ALU.mult), reads=[Bps[0], B_sa[0]], writes=[B_sa[0]])
                S.op("dve", lambda e: e.tensor_tensor(sa[1], sa[1], ps[1][:, 0:512], ALU.mult), reads=[Bps[1], B_sa[1]], writes=[B_sa[1]])
                S.op("dve", (lambda e, i=i: e.tensor_tensor(yT[:, i, :], sa[0], sa[1], ALU.add)), reads=B_sa, writes=[B_u])
        S.dma("sp", (lambda e, t0=t0: e.dma_start(out=x[:, :, :], in_=H1v[:, :, t0:t0 + 512])), reads=[B_H1], writes=[B_x])
        for hh in range(2):
            wload(wsl[hh], B_wsl[hh], kp(w_out)[:, :, hh * 512:(hh + 1) * 512], "C", ("wout", hh), 4096)
        for i in range(8):
            pb = 4 + i % 2
            mm(ps[pb][:, 0:512], [(wsl[i // 4][:, k, (i % 4) * 128:(i % 4 + 1) * 128], yT[:, k, :]) for k in range(8)],
               reads=[B_wsl[i // 4], B_u], writes=[Bps[pb]])
            S.op("act", (lambda e, i=i, pb=pb: e.copy(ol[:, i, :], ps[pb][:, 0:512])), reads=[Bps[pb]], writes=B_ol)
            S.op("act", (lambda e, i=i, pb=pb: e.activation(sq[:, i, :], ps[pb][:, 0:512], AF.Square)), reads=[Bps[pb]], writes=[B_sq])
        rstd_from(sq, B_sq, 512, rs1, B_rs1, 6)
        for i in range(8):
            t = tmp[i % 2]; bt = B_tmp[i % 2]
            S.op("dve", (lambda e, i=i, t=t: e.tensor_tensor(t, ol[:, i, :], rs1, ALU.mult)), reads=B_ol + [B_rs1], writes=[bt])
            S.op("dve", (lambda e, i=i, t=t: e.scalar_tensor_tensor(x[:, i, :], t, DER["PMl"][:, i:i + 1], x[:, i, :], ALU.mult, ALU.add)),
                 reads=[bt, g_["B_der"]], writes=[B_x])
        S.barrier()
        S.op("act", lambda e: e.activation(sq[:, :, :], x[:, :, :], AF.Square), reads=[B_x], writes=[B_sq])
        rstd_from(sq, B_sq, 512, rs1, B_rs1, 6)
        ffn(x, B_x, 512, rs1, B_rs1, DER["A4l"], DER["S4l"], DER["PBl"], w13b, w2b, fb, "B")
        t_store = S.dma("sp", (lambda e, t0=t0: e.dma_start(out=outv[:, :, t0:t0 + 512], in_=x[:, :, :])), reads=[B_x])
        S.barrier(skip=[t_store])


def build_rest2(env, L):
    AX = mybir.AxisListType
    g_ = dict(env); g_.update(L)
    nc = g_["nc"]; S = g_["S"]; ps = g_["ps"]; Bps = g_["Bps"]; arena = g_["arena"]
    u2 = g_["u2"]; B_u2 = g_["B_u2"]; vcol = g_["vcol"]; DER = g_["DER"]
    ident = g_["ident"]; maskf = g_["maskf"]; maskb = g_["maskb"]; CS = g_["CS"]; M2 = g_["M2"]; M3 = g_["M3"]
    id16 = g_["id16"]; ones16 = g_["ones16"]; B_const = g_["B_const"]; B_der = g_["B_der"]
    sel = g_["sel"]; negsel = g_["negsel"]; mm = g_["mm"]
    H1, AB, PQ, KT, QT, KTOK, VTOK, GROW, HS = [g_[n] for n in "H1 AB PQ KT QT KTOK VTOK GROW HS".split()]
    B_H1, B_AB, B_PQ, B_KT, B_QT, B_KTOK, B_VTOK, B_GROW = [g_["B_" + n] for n in "H1 AB PQ KT QT KTOK VTOK GROW".split()]
    B_HS = g_["B_HS"]
    Rcol, Gcol, Ecol, gend, acol, wkcol, decay, B_cols = [g_[n] for n in "Rcol Gcol Ecol gend acol wkcol decay B_cols".split()]
    yfT, B_yfT, C32, C16, B_C, mB = [g_[n] for n in "yfT B_yfT C32 C16 B_C mB".split()]
    rowv = g_["rowv"]; outT = g_["outT"]
    tiles = g_["tiles"]
    kp = lambda ap: ap.rearrange("(k p) n -> p k n", p=128)

    wbuf = [arena.bf16(8 * 1024).rearrange("p (k n) -> p k n", k=8) for _ in range(2)]
    B_wbuf = [Buf("wbuf0"), Buf("wbuf1")]
    xf = [arena.f32(512) for _ in range(4)]
    B_xf = [Buf("xf%d" % i) for i in range(4)]
    ab_sb2 = [arena.f32(1024).rearrange("p (x g c) -> p x g c", x=2, g=4) for _ in range(2)]
    B_ab2 = [Buf("ab_sb0"), Buf("ab_sb1")]
    Pt2 = [arena.f32(514), arena.f32(514)]
    B_Pt2 = [Buf("Pt0"), Buf("Pt1")]
    acc2 = [arena.f32(512), arena.f32(512)]
    B_acc2 = [Buf("acc0"), Buf("acc1")]
    kTt = arena.bf16(8 * 512).rearrange("p (k t) -> p k t", k=8)
    B_kTt = Buf("kTt")
    tok2 = [arena.bf16(1024), arena.bf16(1024)]
    B_tok2 = [Buf("tok0"), Buf("tok1")]
    tokctr = [0]
    bv_bc = arena.f32(1024)
    ones1 = arena.f32(128)
    rv = arena.f32(1024)
    B_bc = Buf("bc")
    S.dma("sp", lambda e: e.dma_start(out=rv[0:1, :], in_=rowv[:, 0:1024]), writes=[B_bc])
    S.op("pool", lambda e: e.memset(ones1[0:1, :], 1.0), writes=[B_bc])

    def bcast_row(dst, seg):
        for hh in range(2):
            mm(ps[6][:, 0:512], [(ones1[0:1, :], rv[0:1, seg * 1024 + hh * 512:seg * 1024 + hh * 512 + 512])], reads=[B_bc], writes=[Bps[6]])
            S.op("act", (lambda e, hh=hh: e.copy(dst[:, hh * 512:(hh + 1) * 512], ps[6][:, 0:512])), reads=[Bps[6]], writes=[B_bc])
    bcast_row(bv_bc, 0)

    wF = g_["wF"]
    S.dma("pool", lambda e: e.dma_start(out=wbuf[0][:, :, 0:512], in_=kp(wF)), writes=[B_wbuf[0]])
    for i in range(8):
        c0 = CTX + 512 * i
        for g in range(4):
            pb = g % 2
            mm(ps[pb][:, 0:512], [(wbuf[0][:, k, g * 128:(g + 1) * 128], u2[:, k, c0:c0 + 512]) for k in range(8)],
               reads=[B_wbuf[0]] + B_u2, writes=[Bps[pb]])
            S.op("act", (lambda e, g=g, pb=pb: e.activation(xf[g], ps[pb][:, 0:512], AF.Identity, bias=vcol("bF", g, 1))),
                 reads=[Bps[pb], B_const], writes=[B_xf[g]])
        for tb in range(4):
            ab_sb = ab_sb2[tb % 2]; B_ab = B_ab2[tb % 2]
            for g in range(4):
                bank = 2 + g // 2
                mm(ps[bank][:, (g % 2) * 256:(g % 2) * 256 + 256], [(xf[g][:, tb * 128:(tb + 1) * 128], CS)],
                   reads=[B_xf[g], B_const], writes=[Bps[bank]])
            for bi in range(2):
                S.op("dve", (lambda e, bi=bi, ab_sb=ab_sb: e.tensor_copy(ab_sb[:, :, 2 * bi:2 * bi + 2, :],
                                                            ps[2 + bi][:, 0:512].rearrange("p (g x c) -> p x g c", g=2, x=2))),
                     reads=[Bps[2 + bi]], writes=[B_ab])
            tok0 = 512 * i + 128 * tb
            S.dma("sp", (lambda e, tok0=tok0, ab_sb=ab_sb: e.dma_start(out=AB.rearrange("x t f -> t x f")[tok0:tok0 + 128, :, :],
                                                          in_=ab_sb.rearrange("p x g c -> p x (g c)"))), reads=[B_ab], writes=[B_AB])

    def qk_proj(wdram, bname, cwname, cbname, slot, tlist, is_k):
        S.dma("pool", lambda e: e.dma_start(out=wbuf[slot], in_=kp(wdram)), writes=[B_wbuf[slot]])
        pend_silu = [None]
        for (ti, c0, W) in tlist:
            islat = ti >= 1
            left = islat and ti > 1
            right = islat and ti < 8
            for c in range(8):
                pb = c % 2
                Pt = Pt2[c % 2]; B_Pt = B_Pt2[c % 2]; acc = acc2[c % 2]; B_acc = B_acc2[c % 2]
                hb = 5 + c % 2
                mm(ps[pb][:, 0:W], [(wbuf[slot][:, k, c * 128:(c + 1) * 128], u2[:, k, c0:c0 + W]) for k in range(8)],
                   reads=[B_wbuf[slot]] + B_u2, writes=[Bps[pb]])
                S.op("act", (lambda e, c=c, pb=pb, W=W, Pt=Pt: e.activation(Pt[:, 1:W + 1], ps[pb][:, 0:W], AF.Identity, bias=vcol(bname, c, 1))),
                     reads=[Bps[pb], B_const], writes=[B_Pt])
                if left and right:
                    mm(ps[hb][:, 0:2], [(wbuf[slot][:, k, c * 128:(c + 1) * 128], u2[:, k, c0 - 1:c0 + W + 1:W + 1]) for k in range(8)],
                       reads=[B_wbuf[slot]] + B_u2, writes=[Bps[hb]])
                    S.op("act", (lambda e, c=c, Pt=Pt, hb=hb, W=W: e.activation(Pt[:, 0:W + 2:W + 1], ps[hb][:, 0:2], AF.Identity, bias=vcol(bname, c, 1))),
                         reads=[Bps[hb], B_const], writes=[B_Pt])
                else:
                    for hi_, (has, col, dstc) in enumerate(((left, c0 - 1, 0), (right, c0 + W, W + 1))):
                        if has:
                            mm(ps[hb][:, hi_:hi_ + 1], [(wbuf[slot][:, k, c * 128:(c + 1) * 128], u2[:, k, col:col + 1]) for k in range(8)],
                               reads=[B_wbuf[slot]] + B_u2, writes=[Bps[hb]])
                            S.op("act", (lambda e, c=c, dstc=dstc, Pt=Pt, hb=hb, hi_=hi_: e.activation(Pt[:, dstc:dstc + 1], ps[hb][:, hi_:hi_ + 1], AF.Identity, bias=vcol(bname, c, 1))),
                                 reads=[Bps[hb], B_const], writes=[B_Pt])
                        else:
                            S.op("pool", (lambda e, dstc=dstc, Pt=Pt: e.memset(Pt[:, dstc:dstc + 1], 0.0)), writes=[B_Pt])
                S.op("act", (lambda e, c=c, W=W, Pt=Pt, acc=acc: e.activation(acc[:, 0:W], Pt[:, 0:W], AF.Identity, scale=vcol(cwname, c, 1))),
                     reads=[B_Pt, B_const], writes=[B_acc])
                S.op("dve", (lambda e, c=c, W=W, Pt=Pt, acc=acc: e.scalar_tensor_tensor(acc[:, 0:W], Pt[:, 1:W + 1], vcol(cwname, 8 + c, 1), acc[:, 0:W], ALU.mult, ALU.add)),
                     reads=[B_Pt, B_const], writes=[B_acc])
                S.op("dve", (lambda e, c=c, W=W, Pt=Pt, acc=acc: e.scalar_tensor_tensor(acc[:, 0:W], Pt[:, 2:W + 2], vcol(cwname, 16 + c, 1), acc[:, 0:W], ALU.mult, ALU.add)),
                     reads=[B_Pt, B_const], writes=[B_acc])
                if pend_silu[0] is not None:
                    pend_silu[0]()
                pend_silu[0] = (lambda c=c, W=W, acc=acc, B_acc=B_acc: S.op(
                    "act", (lambda e: e.activation(kTt[:, c, 0:W], acc[:, 0:W], AF.Silu, bias=vcol(cbname, c, 1))),
                    reads=[B_acc, B_const], writes=[B_kTt]))
            if pend_silu[0] is not None:
                pend_silu[0]()
                pend_silu[0] = None
            own = 1 <= ti <= 4
            if own:
                o0 = c0 - CTX
                dst = KT if is_k else QT
                S.dma("sp", (lambda e, o0=o0, dst=dst: e.dma_start(out=dst.rearrange("(k p) t -> p k t", p=128)[:, :, o0:o0 + 512], in_=kTt[:, :, :])),
                      reads=[B_kTt], writes=[B_KT if is_k else B_QT])
            if is_k:
                for tb in range(W // 128):
                    tok = tok2[tokctr[0] % 2]; B_tok = B_tok2[tokctr[0] % 2]; tokctr[0] += 1
                    for half in range(2):
                        pb = 3 + half
                        def fn(e, tb=tb, half=half, pb=pb):
                            for cc in range(4):
                                c = half * 4 + cc
                                ins = e.matmul(ps[pb][:, cc * 128:(cc + 1) * 128], kTt[:, c, tb * 128:(tb + 1) * 128], id16[:, :], start=True, stop=True)
                            return ins
                        S.op("pe", fn, reads=[B_kTt, B_const], writes=[Bps[pb]])
                        S.op("dve", (lambda e, half=half, pb=pb, tok=tok: e.tensor_copy(tok[:, half * 512:(half + 1) * 512], ps[pb][:, 0:512])),
                             reads=[Bps[pb]], writes=[B_tok])
                    r0 = c0 + tb * 128
                    S.dma("sp", (lambda e, r0=r0, tok=tok: e.dma_start(out=KTOK[r0:r0 + 128, :], in_=tok)), reads=[B_tok], writes=[B_KTOK])

    tl_all = [(ti, c0, W) for ti, (c0, W, j) in enumerate(tiles)]
    qk_proj(g_["wk"], "bk", "cwk", "cbk", 1, tl_all, True)
    qk_proj(g_["wq"], "bq", "cwq", "cbq", 0, tl_all[1:5], False)
    S.dma("pool", lambda e: e.dma_start(out=wbuf[1], in_=kp(g_["wv"])), writes=[B_wbuf[1]])
    for cb in range(0, NT, 128):
        tok = tok2[tokctr[0] % 2]; B_tok = B_tok2[tokctr[0] % 2]; tokctr[0] += 1
        for half in range(2):
            pb = half + 2 * ((cb // 128) % 2)
            mm(ps[pb][:, 0:512], [(u2[:, k, cb:cb + 128], wbuf[1][:, k, half * 512:(half + 1) * 512]) for k in range(8)],
               reads=[B_wbuf[1]] + B_u2, writes=[Bps[pb]])
            S.op("dve", (lambda e, half=half, pb=pb, tok=tok: e.tensor_tensor(tok[:, half * 512:(half + 1) * 512], ps[pb][:, 0:512],
                                                                     bv_bc[:, half * 512:(half + 1) * 512], ALU.add)),
                 reads=[Bps[pb], B_bc], writes=[B_tok])
        S.dma("sp", (lambda e, cb=cb, tok=tok: e.dma_start(out=VTOK[cb:cb + 128, :], in_=tok)), reads=[B_tok], writes=[B_VTOK])
    S.barrier()
    arena.reset(mB)

    inb2 = [arena.f32(8 * 512).rearrange("p (r f) -> p r f", r=8) for _ in range(2)]
    outb2 = [arena.f32(8 * 512).rearrange("p (r f) -> p r f", r=8) for _ in range(2)]
    B_inb2 = [Buf("inb0"), Buf("inb1")]; B_outb2 = [Buf("outb0"), Buf("outb1")]
    ABv = AB.rearrange("x (r c) f -> x c r f", c=64)
    PQw = PQ
    for rb in range(8):
        inb = inb2[rb % 2]; outb = outb2[rb % 2]; B_inb = B_inb2[rb % 2]; B_outb = B_outb2[rb % 2]
        for x in range(2):
            S.dma("sp", (lambda e, rb=rb, x=x, inb=inb: e.dma_start(out=inb[64 * x:64 * x + 64, :, :], in_=ABv[x, :, rb * 8:rb * 8 + 8, :])),
                  reads=[B_AB], writes=[B_inb])
        for r in range(8):
            pb = r % 4
            mm(ps[pb][:, 0:512], [(M2, inb[:, r, :])], reads=[B_inb, B_const], writes=[Bps[pb]])
            if r % 2 == 0:
                S.op("act", (lambda e, r=r, pb=pb, outb=outb: e.copy(outb[:, r, :], ps[pb][:, 0:512])), reads=[Bps[pb]], writes=[B_outb])
            else:
                S.op("dve", (lambda e, r=r, pb=pb, outb=outb: e.tensor_copy(outb[:, r, :], ps[pb][:, 0:512])), reads=[Bps[pb]], writes=[B_outb])
        for x in range(2):
            S.dma("sp", (lambda e, rb=rb, x=x, outb=outb: e.dma_start(out=PQw[x, :, rb * 8:rb * 8 + 8, :], in_=outb[64 * x:64 * x + 64, :, :])),
                  reads=[B_outb], writes=[B_PQ])
    PQr = PQ.rearrange("x kc r f -> x r kc f")
    for kb in range(8):
        inb = inb2[kb % 2]; B_inb = B_inb2[kb % 2]
        for x in range(2):
            S.dma("sp", (lambda e, kb=kb, x=x, inb=inb: e.dma_start(out=inb[64 * x:64 * x + 64, :, :], in_=PQr[x, :, kb * 8:kb * 8 + 8, :])),
                  reads=[B_PQ], writes=[B_inb])
        for g in range(4):
            pb = 4 + g
            def fn(e, g=g, pb=pb, inb=inb):
                for kc in range(8):
                    ins = e.matmul(ps[pb][:, kc * 32:(kc + 1) * 32], inb[:, kc, g * 128:(g + 1) * 128], M3, start=True, stop=True)
                return ins
            S.op("pe", fn, reads=[B_inb, B_const], writes=[Bps[pb]])
            S.op("act", (lambda e, g=g, pb=pb, kb=kb: e.copy(yfT[:, g, :].rearrange("p (kr kc) -> p kc kr", kc=64)[:, kb * 8:kb * 8 + 8, :],
                                                          ps[pb][:, 0:256].rearrange("p (kc kr) -> p kc kr", kr=32))),
                 reads=[Bps[pb]], writes=[B_yfT])
    S.barrier()
    arena.reset(mB)

    NLS = 3
    NHS = 2
    LD = []
    for i in range(2 * NLS):
        d_ = dict(ktok=arena.bf16(1024), vaug=arena.bf16(4 * 258).rearrange("p (h e) -> p h e", h=4),
                  kT=arena.bf16(1024).rearrange("p (k t) -> p k t", k=8), qT=arena.bf16(1024).rearrange("p (k t) -> p k t", k=8),
                  grow=arena.f32(128), B_ld=Buf("ld%d" % i), B_ldo=Buf("ldo%d" % i))
        S.op("pool", (lambda e, v=d_["vaug"]: e.memset(v[:, :, :], 1.0)), writes=[d_["B_ld"]])
        LD.append(d_)
    HSB = [dict(hs=arena.f32(1024), B_hs=Buf("hs%d" % i)) for i in range(4)]
    HT = []
    B_pP2s = Buf("pP2"); B_pP1s = Buf("pP1")
    for i in range(NHS):
        HT.append(dict(wT=arena.f32(128), STb=arena.bf16(128), P2sb=arena.f32(257), hn=arena.f32(257), dd=arena.f32(2), kw=arena.bf16(256),
                       B_wT=Buf("wT%d" % i), B_ST=Buf("ST%d" % i), B_P2=Buf("P2sb%d" % i), B_hn=Buf("hn%d" % i), B_dd=Buf("dd%d" % i),
                       B_kw=Buf("kw%d" % i),
                       pST=ps[i][:, 0:128], pD=ps[i][:, 128:256], pP2=ps[2][:, 0:257], pP1=ps[3][:, 0:257],
                       pCU=[ps[4 + i][:, 0:257], ps[6 + i][:, 0:257]],
                       B_pSD=Buf("pSD%d" % i), B_pP2=B_pP2s, B_pP1=B_pP1s, B_pCU=[Buf("pCU0_%d" % i), Buf("pCU1_%d" % i)]))
    B_C32 = [[Buf('C32_%d_%d' % (q, c)) for c in range(2)] for q in range(8)]
    B_C16 = [[Buf('C16_%d_%d' % (q, c)) for c in range(2)] for q in range(8)]
    steps = []
    fw = [(0, 0, None), (1, 128, None)] + [(2 + i, CTX + 128 * i, 128 * i) for i in range(16)]
    bw = [(0, 128, None), (1, 0, None)] + [(2 + i, CTX + 128 * (31 - i), (128 * (31 - i) if 31 - i <= 15 else None)) for i in range(32)]
    for i in range(34):
        if i < 18:
            steps.append((0,) + fw[i])
        steps.append((1,) + bw[i])
    mask16 = [arena.bf16(128), arena.bf16(128)]
    S.op("act", lambda e: e.copy(mask16[0], maskf), reads=[B_const], writes=[B_const])
    S.op("act", lambda e: e.copy(mask16[1], maskb), reads=[B_const], writes=[B_const])

    def emit_loads(si):
        (dr, sc, cb, ob) = steps[si]
        L_ = LD[dr * NLS + sc % NLS]
        ktok_t, vaug, kT_t, qT_t, grow_t = L_["ktok"], L_["vaug"], L_["kT"], L_["qT"], L_["grow"]
        B_ld, B_ldo = L_["B_ld"], L_["B_ldo"]
        S.dma("sp", (lambda e: e.dma_start(out=ktok_t, in_=KTOK[cb:cb + 128, :])), reads=[B_KTOK], writes=[B_ld])
        S.dma("sp", (lambda e: e.dma_start(out=vaug[:, :, 0:256], in_=VTOK[cb:cb + 128, :].rearrange("t (h e) -> t h e", h=4))),
              reads=[B_VTOK], writes=[B_ld])
        if ob is not None:
            S.dma("sp", (lambda e: e.dma_start(out=kT_t, in_=KT.rearrange("(k p) t -> p k t", p=128)[:, :, ob:ob + 128])), reads=[B_KT], writes=[B_ldo])
            S.dma("sp", (lambda e: e.dma_start(out=qT_t, in_=QT.rearrange("(k p) t -> p k t", p=128)[:, :, ob:ob + 128])), reads=[B_QT], writes=[B_ldo])
            S.dma("sp", (lambda e: e.dma_start(out=grow_t[0:36, :], in_=GROW[:, cb:cb + 128])), reads=[B_GROW], writes=[B_ldo])

    def emit_load_hs(si):
        (dr, sc, cb, ob) = steps[si]
        if ob is not None and dr == 1:
            H2 = HSB[dr * 2 + sc % 2]
            S.dma("sp", (lambda e: e.dma_start(out=H2["hs"], in_=HS[ob:ob + 128, :])), reads=[B_HS[ob // 128]], writes=[H2["B_hs"]])

    items = []
    for si, (dr, sc, cb, ob) in enumerate(steps):
        for h in range(4):
            items.append((si, h, len(items)))

    def ctx_of(it):
        si, h, n = it
        (dr, sc, cb, ob) = steps[si]
        L2 = dict(LD[dr * NLS + sc % NLS]); L2.update(HSB[dr * 2 + sc % 2])
        return dr, sc, cb, ob, h, dr * 4 + h, (sc if dr == 0 else 18 + sc), L2, HT[n % NHS]

    def emit_A(it):
        dr, sc, cb, ob, h, q, ci, L_, H_ = ctx_of(it)
        kT_t, qT_t, grow_t, ktok_t = L_["kT"], L_["qT"], L_["grow"], L_["ktok"]
        wT, STb, kw, pST, pD = H_["wT"], H_["STb"], H_["kw"], H_["pST"], H_["pD"]
        S.op("pool", (lambda e: e.tensor_scalar(kw, ktok_t[:, h * 256:(h + 1) * 256], wkcol[:, q, sc:sc + 1], 0.0625, ALU.mult, ALU.mult)),
             reads=[L_["B_ld"], B_cols], writes=[H_["B_kw"]])
        if ob is not None:
            def fsd(e):
                e.matmul(pST, kT_t[:, 2 * h, :], qT_t[:, 2 * h, :], start=True, stop=False)
                e.matmul(pST, kT_t[:, 2 * h + 1, :], qT_t[:, 2 * h + 1, :], start=False, stop=True)
                e.matmul(pD, negsel[0:36, q, :], grow_t[0:36, :], start=True, stop=False)
                return e.matmul(pD, ident, maskf if dr == 0 else maskb, start=False, stop=True)
            S.op("pe", fsd, reads=[L_["B_ldo"], B_const], writes=[H_["B_pSD"]])
            S.op("act", (lambda e: e.activation(wT, pD, AF.Exp, bias=Rcol[:, ci, h:h + 1])),
                 reads=[H_["B_pSD"], B_cols], writes=[H_["B_wT"]])
            S.op("dve", (lambda e: e.scalar_tensor_tensor(STb, pST, 0.0625, wT, ALU.mult, ALU.mult)),
                 reads=[H_["B_pSD"], H_["B_wT"]], writes=[H_["B_ST"]])

    def emit_BC(it):
        dr, sc, cb, ob, h, q, ci, L_, H_ = ctx_of(it)
        vaug, qT_t, hs_t = L_["vaug"], L_["qT"], L_["hs"]
        B_ld, B_ldo, B_hs = L_["B_ld"], L_["B_ldo"], L_["B_hs"]
        STb, P2sb, hn, dd, kw = H_["STb"], H_["P2sb"], H_["hn"], H_["dd"], H_["kw"]
        pP2, pP1, pCU = H_["pP2"], H_["pP1"], H_["pCU"]
        for c in range(2):
            mm(pCU[c], [(kw[:, c * 128:(c + 1) * 128], vaug[:, h, 0:257])], reads=[H_["B_kw"], B_ld], writes=[H_["B_pCU"][c]])
        if ob is not None:
            mm(pP1, [(qT_t[:, 2 * h + c, :], C16[:, q, c, 0:257]) for c in range(2)], reads=[B_ldo] + B_C16[q], writes=[H_["B_pP1"]])
        for c in range(2):
            S.op("dve", (lambda e, c=c: e.scalar_tensor_tensor(C32[:, q, c, :], C32[:, q, c, :], decay[:, q, sc:sc + 1], pCU[c], ALU.mult, ALU.add)),
                 reads=[H_["B_pCU"][c], B_cols], writes=[B_C32[q][c]])
            S.op("act", (lambda e, c=c: e.copy(C16[:, q, c, 0:257], C32[:, q, c, :])), reads=[B_C32[q][c]], writes=[B_C16[q][c]])
        if ob is not None:
            mm(pP2, [(STb, vaug[:, h, 0:257])], reads=[H_["B_ST"], B_ld], writes=[H_["B_pP2"]])
            S.op("act", (lambda e: e.copy(P2sb, pP2)), reads=[H_["B_pP2"]], writes=[H_["B_P2"]])
            S.op("dve", (lambda e: e.scalar_tensor_tensor(hn, pP1, acol[:, q, sc:sc + 1], P2sb, ALU.mult, ALU.add)),
                 reads=[H_["B_pP1"], H_["B_P2"], B_cols], writes=[H_["B_hn"]])
            S.op("dve", (lambda e: e.scalar_tensor_tensor(dd[:, 0:1], hn[:, 256:257], -1.0, hn[:, 256:257], ALU.mult, ALU.max)),
                 reads=[H_["B_hn"]], writes=[H_["B_dd"]])
            S.op("dve", (lambda e: e.tensor_tensor(dd[:, 0:1], dd[:, 0:1], Ecol[:, ci, h:h + 1], ALU.max)),
                 reads=[H_["B_dd"], B_cols], writes=[H_["B_dd"]])
            S.op("dve", (lambda e: e.reciprocal(dd[:, 1:2], dd[:, 0:1])), reads=[H_["B_dd"]], writes=[H_["B_dd"]])
            if dr == 0:
                S.op("act", (lambda e: e.activation(hs_t[:, h * 256:(h + 1) * 256], hn[:, 0:256], AF.Identity, scale=dd[:, 1:2])),
                     reads=[H_["B_hn"], H_["B_dd"]], writes=[B_hs])
            else:
                S.op("dve", (lambda e: e.scalar_tensor_tensor(hs_t[:, h * 256:(h + 1) * 256], hn[:, 0:256], dd[:, 1:2],
                                                              hs_t[:, h * 256:(h + 1) * 256], ALU.mult, ALU.add)),
                     reads=[H_["B_hn"], H_["B_dd"]], writes=[B_hs])
            if h == 3:
                S.dma("sp", (lambda e: e.dma_start(out=HS[ob:ob + 128, :], in_=hs_t)), reads=[B_hs], writes=[B_HS[ob // 128]])

    emit_loads(0); emit_loads(1)
    for n in range(len(items) + 1):
        if n < len(items):
            si, h, _ = items[n]
            if h == 0:
                emit_load_hs(si)
            if h == 1 and si + 2 < len(steps):
                emit_loads(si + 2)
            emit_A(items[n])
        if n >= 1:
            emit_BC(items[n - 1])
    S.barrier()
    if g_["stage"] < 4:
        return
    build_rest3(g_, locals())


def build_rest(env):
    nc = env["nc"]
    S = env["S"]; ps = env["ps"]; Bps = env["Bps"]; arena = env["arena"]
    u2 = env["u2"]; B_u2 = env["B_u2"]; vcol = env["vcol"]; DER = env["DER"]
    ident = env["ident"]; maskf = env["maskf"]; maskb = env["maskb"]; CS = env["CS"]; M2 = env["M2"]; M3 = env["M3"]
    id16 = env["id16_t"]; ones16 = env["ones16_t"]
    B_const = env["B_const"]; B_der = env["B_der"]
    stage = env["stage"]; dbg = env["dbg"]; dbg_tensor = env["dbg_tensor"]
    dscr = env["dscr"]
    H1, AB, PQ, KT, QT, KTOK, VTOK = [env[n] for n in "H1 AB PQ KT QT KTOK VTOK".split()]
    B_H1, B_AB, B_PQ, B_KT, B_QT, B_KTOK, B_VTOK = [env["B_" + n] for n in "H1 AB PQ KT QT KTOK VTOK".split()]
    GROW = dscr("GROW", [36, NT], F32)
    B_GROW = Buf("GROW")
    HS = dscr("HS", [OWN, D], F32)
    B_HS = [Buf("HS%d" % i) for i in range(16)]

    def mm(out, pairs, reads, writes):
        def fn(e):
            n = len(pairs)
            for i, (l, r) in enumerate(pairs):
                ins = e.matmul(out, l, r, start=(i == 0), stop=(i == n - 1))
            return ins
        return S.op("pe", fn, reads=reads, writes=writes)

    NCH = 52
    Rcol = arena.f32(NCH * 4).rearrange("p (c h) -> p c h", h=4)
    Gcol = arena.f32(NCH * 4).rearrange("p (c h) -> p c h", h=4)
    Ecol = arena.f32(NCH * 4).rearrange("p (c h) -> p c h", h=4)
    gend = arena.f32(8 * 35).rearrange("p (q c) -> p q c", q=8)
    acol = arena.f32(8 * 34).rearrange("p (q c) -> p q c", q=8)
    wkcol = arena.f32(8 * 34).rearrange("p (q c) -> p q c", q=8)
    decay = arena.f32(8 * 34).rearrange("p (q c) -> p q c", q=8)
    B_cols = Buf("cols")
    yfT = arena.bf16(4 * OWN).rearrange("p (g t) -> p g t", g=4)
    B_yfT = Buf("yfT")
    C32 = arena.f32(8 * 2 * 257).rearrange("p (q c e) -> p q c e", q=8, c=2)
    C16 = arena.bf16(8 * 2 * 258).rearrange("p (q c e) -> p q c e", q=8, c=2)
    B_C = [Buf("C%d" % i) for i in range(8)]
    sel_t = arena.f32(2048)
    S.dma("sp", lambda e: e.dma_start(out=sel_t[0:36, :], in_=env["selc"]), writes=[B_const])
    sel = sel_t[0:36, 0:1024].rearrange("r (p m) -> r p m", p=8)
    negsel = sel_t[0:36, 1024:2048].rearrange("r (p m) -> r p m", p=8)
    mB = arena.mark()

    aLI = arena.f32(NT)
    aLF = arena.f32(NT)
    aB = arena.f32(NT)
    aG = arena.f32(NT)
    ones_r = arena.f32(512)
    tmpE = arena.f32(512)
    wgf_s = arena.bf16(8 * 8).rearrange("p (k n) -> p k n", k=8)
    wgb_s = arena.bf16(8 * 72).rearrange("p (k n) -> p k n", k=8)
    bgs = arena.f32(4)
    B_rows = Buf("rows")
    B_wg = Buf("wg")
    B_tmpE = Buf("tmpE")
    for a in (aLI, aLF, aB, aG):
        S.op("pool", (lambda e, a=a: e.memset(a, 0.0)), writes=[B_rows])
    S.op("pool", lambda e: e.memset(ones_r, 1.0), writes=[B_wg])
    S.op("pool", lambda e: e.memset(gend[:, :, :], 0.0), writes=[B_cols])
    for q in range(8):
        S.op("pool", (lambda e, q=q: e.memset(C32[:, q, :, :], 0.0)), writes=[B_C[q]])
        S.op("pool", (lambda e, q=q: e.memset(C16[:, q, :, :], 0.0)), writes=[B_C[q]])
    with nc.allow_non_contiguous_dma(reason="tiny gate weights"):
        pass
    wgate_f = env["wgate_f"]; wgate_b = env["wgate_b"]; bg = env["bg"]
    S.dma("pool", lambda e: e.dma_start(out=wgf_s, in_=wgate_f.rearrange("(k p) n -> p k n", p=128)), writes=[B_wg])
    S.dma("pool", lambda e: e.dma_start(out=wgb_s, in_=wgate_b.rearrange("(k p) n -> p k n", p=128)), writes=[B_wg])
    S.dma("sp", lambda e: e.dma_start(out=bgs[0:36, 0:2], in_=bg), writes=[B_wg])
    S.op("dve", lambda e: e.tensor_scalar(bgs[0:36, 2:3], bgs[0:36, 1:2], -1.0, None, ALU.mult), reads=[B_wg], writes=[B_wg])

    def gate_tile(jc0, W, rhs_fn, r0, r1, wl, wl_lf, pbank, rev=False):
        def pv(bank):
            return ps[bank][r0:r1, W - 1::-1] if rev else ps[bank][r0:r1, 0:W]
        mm(ps[pbank][0:r1, 0:W], [(wl(k), rhs_fn(k)) for k in range(8)], reads=[B_wg] + B_u2, writes=[Bps[pbank]])
        S.op("act", lambda e: e.activation(aLI[r0:r1, jc0:jc0 + W], pv(pbank), AF.Identity, bias=bgs[r0:r1, 0:1]),
             reads=[Bps[pbank], B_wg], writes=[B_rows])
        mm(ps[pbank + 1][0:r1, 0:W], [(wl_lf(k), rhs_fn(k)) for k in range(8)], reads=[B_wg] + B_u2, writes=[Bps[pbank + 1]])
        S.op("act", lambda e: e.activation(tmpE[r0:r1, 0:W], pv(pbank + 1), AF.Exp, bias=bgs[r0:r1, 2:3], scale=-1.0),
             reads=[Bps[pbank + 1], B_wg], writes=[B_tmpE])
        S.op("act", lambda e: e.activation(tmpE[r0:r1, 0:W], tmpE[r0:r1, 0:W], AF.Ln, bias=1.0), reads=[B_tmpE], writes=[B_tmpE])
        S.op("dve", lambda e: e.tensor_scalar(aLF[r0:r1, jc0:jc0 + W], tmpE[r0:r1, 0:W], -1.0, None, ALU.mult),
             reads=[B_tmpE], writes=[B_rows])

    ti = 0
    for (jc0, W) in [(0, 256)] + [(256 + 512 * i, 512) for i in range(4)]:
        gate_tile(jc0, W, (lambda k, jc0=jc0, W=W: u2[:, k, jc0:jc0 + W]), 0, 4,
                  (lambda k: wgf_s[:, k, 0:4]), (lambda k: wgf_s[:, k, 4:8]), 2 * (ti % 2))
        ti += 1
    def rev_u2(k, hi, W):
        return u2[:, k, hi - W + 1:hi + 1]
    gate_tile(0, 256, (lambda k: rev_u2(k, 255, 256)), 32, 36,
              (lambda k: wgb_s[:, k, 0:36]), (lambda k: wgb_s[:, k, 36:72]), 2 * (ti % 2), rev=True)
    ti += 1
    for i in range(8):
        jc0 = 256 + 512 * i
        hi = 4607 - jc0
        gate_tile(jc0, 512, (lambda k, hi=hi: rev_u2(k, hi, 512)), 32, 36,
                  (lambda k: wgb_s[:, k, 0:36]), (lambda k: wgb_s[:, k, 36:72]), 2 * (ti % 2), rev=True)
        ti += 1
    pieces = [(0, 256)] + [(256 + 512 * i, 512) for i in range(8)]
    for pi, (c0, W) in enumerate(pieces):
        init = 0.0 if pi == 0 else aB[0:36, c0 - 1:c0]
        S.op("dve", (lambda e, c0=c0, W=W, init=init: e.tensor_tensor_scan(aB[0:36, c0:c0 + W], ones_r[0:36, 0:W], aLF[0:36, c0:c0 + W],
                                                                           init, ALU.mult, ALU.add)),
             reads=[B_rows, B_wg], writes=[B_rows])
    S.op("dve", lambda e: e.tensor_tensor(aLI[0:36, :], aLI[0:36, :], aB[0:36, :], ALU.subtract), reads=[B_rows], writes=[B_rows])
    for pi, (c0, W) in enumerate(pieces):
        init = 0.0 if pi == 0 else aG[0:36, c0 - 1:c0]
        S.op("dve", (lambda e, c0=c0, W=W, init=init: e.tensor_tensor_scan(aG[0:36, c0:c0 + W], ones_r[0:36, 0:W], aLI[0:36, c0:c0 + W],
                                                                           init, ALU.mult, ALU.max)),
             reads=[B_rows, B_wg], writes=[B_rows])
    S.op("dve", lambda e: e.tensor_tensor(aB[0:36, :], aB[0:36, :], aG[0:36, :], ALU.add), reads=[B_rows], writes=[B_rows])
    if dbg:
        d_rows = dbg_tensor("d_rows", [3, 36, NT])
        for i, a in enumerate((aLI, aG, aB)):
            S.dma("sp", (lambda e, i=i, a=a: e.dma_start(out=d_rows[i], in_=a[0:36, :])), reads=[B_rows])
    def n0_of(sc):
        return 128 if sc == 0 else (0 if sc == 1 else 4480 - 128 * sc)
    for ai, (arr, bank) in enumerate(((aLI, 0), (aB, 2), (aG, 1))):
        S.op("dve", (lambda e, arr=arr: e.tensor_copy(aLF[32:36, 0:256], arr[32:36, 255::-1])), reads=[B_rows], writes=[B_rows])
        S.op("dve", (lambda e, arr=arr: e.tensor_copy(aLF[32:36, 256:NT], arr[32:36, NT - 1:255:-1])), reads=[B_rows], writes=[B_rows])
        def fn(e, arr=arr, bank=bank):
            for sc in range(18):
                ins = e.matmul(ps[bank][:, sc * 4:(sc + 1) * 4], arr[0:4, sc * 128:(sc + 1) * 128], ident[0:4, 0:4],
                               start=True, stop=True)
            for sc in range(34):
                n0 = n0_of(sc)
                ins = e.matmul(ps[bank][:, (18 + sc) * 4:(19 + sc) * 4], aLF[32:36, n0:n0 + 128],
                               ident[32:36, 32:36], start=True, stop=True)
            return ins
        S.op("pe", fn, reads=[B_rows, B_const], writes=[Bps[bank]])
    S.op("dve", lambda e: e.tensor_copy(aLF[0:4, :], aG[0:4, :]), reads=[B_rows], writes=[B_rows])
    S.dma("sp", lambda e: e.dma_start(out=GROW, in_=aLF[0:36, :]), reads=[B_rows], writes=[B_GROW])
    S.op("act", lambda e: e.copy(Rcol[:, :, :], ps[0][:, 0:NCH * 4].rearrange("p (c h) -> p c h", h=4)), reads=[Bps[0]], writes=[B_cols])
    S.op("act", lambda e: e.copy(Gcol[:, :, :], ps[1][:, 0:NCH * 4].rearrange("p (c h) -> p c h", h=4)), reads=[Bps[1]], writes=[B_cols])
    S.op("act", lambda e: e.activation(Ecol[:, :, :], ps[2][:, 0:NCH * 4].rearrange("p (c h) -> p c h", h=4), AF.Exp, scale=-1.0),
         reads=[Bps[2]], writes=[B_cols])
    def fn(e):
        for q in range(8):
            n = 18 if q < 4 else 34
            ins = e.matmul(ps[3][:, q * 34:q * 34 + n], sel[0:36, q, :], aG[0:36, 127:127 + 128 * (n - 1) + 1:128], start=True, stop=True)
        return ins
    S.op("pe", fn, reads=[B_rows, B_const], writes=[Bps[3]])
    for q in range(8):
        n = 18 if q < 4 else 34
        S.op("act", (lambda e, q=q, n=n: e.copy(gend[:, q, 1:1 + n], ps[3][:, q * 34:q * 34 + n])), reads=[Bps[3]], writes=[B_cols])
    tq = arena.f32(34)
    B_tq = Buf("tq")
    for q in range(8):
        n = 18 if q < 4 else 34
        base = 0 if q < 4 else 18
        h = q % 4
        S.op("dve", (lambda e, q=q, n=n, base=base, h=h: e.tensor_tensor(tq[:, 0:n], gend[:, q, 0:n], Gcol[:, base:base + n, h], ALU.subtract)),
             reads=[B_cols], writes=[B_tq])
        S.op("act", (lambda e, q=q, n=n: e.activation(acol[:, q, 0:n], tq[:, 0:n], AF.Exp)), reads=[B_tq], writes=[B_cols])
        S.op("dve", (lambda e, q=q, n=n, base=base, h=h: e.tensor_tensor(tq[:, 0:n], Rcol[:, base:base + n, h], gend[:, q, 1:1 + n], ALU.subtract)),
             reads=[B_cols], writes=[B_tq])
        S.op("act", (lambda e, q=q, n=n: e.activation(wkcol[:, q, 0:n], tq[:, 0:n], AF.Exp)), reads=[B_tq], writes=[B_cols])
        S.op("dve", (lambda e, q=q, n=n: e.tensor_tensor(tq[:, 0:n], gend[:, q, 0:n], gend[:, q, 1:1 + n], ALU.subtract)),
             reads=[B_cols], writes=[B_tq])
        S.op("act", (lambda e, q=q, n=n: e.activation(decay[:, q, 0:n], tq[:, 0:n], AF.Exp)), reads=[B_tq], writes=[B_cols])
    S.barrier()
    arena.reset(mB)
    if stage < 3:
        return
    build_rest2(env, locals())


COL_F = 0
COL_Q = 512
COL_K = 1536
COL_V = 2560
COL_O = 3584
COL_GATES = 4608
COL_BR = 4624


def _dft_consts(flip):
    idx = (63 - np.arange(64)) if flip else np.arange(64)
    ang = 2 * np.pi * np.outer(idx, idx) / 64.0
    Cc = np.cos(ang) / 8.0
    Sc = np.sin(ang) / 8.0
    ch = np.arange(128)
    angc = 2 * np.pi * np.outer(ch, ch) / 128.0
    CS = np.concatenate([np.cos(angc), np.sin(angc)], axis=1) / np.sqrt(128.0)
    M2 = np.zeros((128, 128))
    M2[0:64, 0:64] = Cc
    M2[64:128, 0:64] = -Sc
    M2[0:64, 64:128] = Sc
    M2[64:128, 64:128] = Cc
    M3 = np.zeros((128, 32))
    M3[0:64, :] = Cc[:, 0:32]
    M3[64:128, :] = -Sc[:, 0:32]
    return CS, M2, M3


def make_inputs(inp):
    f32 = np.float32
    x = np.asarray(inp["x"], f32)
    ctx = np.asarray(inp["ctx"], f32)
    c = np.asarray(inp["c"], f32)
    c_ctx = np.asarray(inp["c_ctx"], f32)
    w_in = np.asarray(inp["w_in"], f32)[0]
    b_in = np.asarray(inp["b_in"], f32)[0]
    conv_w = np.asarray(inp["conv_w"], f32)[0]
    conv_b = np.asarray(inp["conv_b"], f32)[0]
    norm_g = np.asarray(inp["norm_g"], f32)[0]

    def fm(v):
        return np.ascontiguousarray(v.reshape(-1, 128).T)

    def r13(w):
        return np.ascontiguousarray(w.reshape(8, 128, 2, NJ, 128).transpose(3, 1, 0, 2, 4).reshape(NJ, 128, 2048))

    def r2(w):
        return np.ascontiguousarray(w.reshape(NJ, 128, 8, 128).transpose(2, 1, 0, 3).reshape(8, 128, FF))

    shared = {
        "w_ada": np.ascontiguousarray(np.asarray(inp["w_ada"], f32)[0]),
        "w13a": r13(np.asarray(inp["w13_a"], f32)[0]), "w2a": r2(np.asarray(inp["w2_a"], f32)[0]),
        "w13b": r13(np.asarray(inp["w13_b"], f32)[0]), "w2b": r2(np.asarray(inp["w2_b"], f32)[0]),
        "wF": np.ascontiguousarray(w_in[:, COL_F:COL_Q]), "wq": np.ascontiguousarray(w_in[:, COL_Q:COL_K]),
        "wk": np.ascontiguousarray(w_in[:, COL_K:COL_V]), "wv": np.ascontiguousarray(w_in[:, COL_V:COL_O]),
        "wo": np.ascontiguousarray(w_in[:, COL_O:COL_GATES]),
        "wgf": np.ascontiguousarray(w_in[:, COL_BR:COL_BR + D]), "wgm": np.ascontiguousarray(w_in[:, COL_BR + D:]),
        "w_four": np.ascontiguousarray(np.asarray(inp["w_four"], f32)[0]),
        "w_mproj": np.ascontiguousarray(np.asarray(inp["w_mproj"], f32)[0]),
        "w_out": np.ascontiguousarray(np.asarray(inp["w_out"], f32)[0]),
        "rowv": np.concatenate([b_in[COL_V:COL_O], b_in[COL_O:COL_GATES], np.asarray(inp["head_g"], f32)[0]])[None, :].copy(),
    }
    sel = np.zeros((36, 8, 128), f32)
    for p in range(8):
        row = (p % 4) + (32 if p >= 4 else 0)
        sel[row, p, :] = 1.0
    selc = np.concatenate([sel.reshape(36, -1), -sel.reshape(36, -1)], axis=1)
    s_idx = np.arange(128)[:, None]
    t_idx = np.arange(128)[None, :]
    maskf = np.where(s_idx <= t_idx, 0.0, NEG).astype(f32)
    maskb = np.where(s_idx >= t_idx, 0.0, NEG).astype(f32)
    maps = []
    for core in range(8):
        b, half = core // 2, core % 2
        flip = half == 1
        xb = x[b][::-1] if flip else x[b]
        cb_ = ctx[b][::-1] if flip else ctx[b]
        g = COL_GATES
        if flip:
            gi_f, gf_f, gi_b, gf_b = g + 8, g + 12, g + 0, g + 4
            cw = conv_w[::-1]
        else:
            gi_f, gf_f, gi_b, gf_b = g + 0, g + 4, g + 8, g + 12
            cw = conv_w
        wgate_f = np.concatenate([w_in[:, gi_f:gi_f + 4], w_in[:, gf_f:gf_f + 4]], axis=1)
        wgate_b = np.zeros((D, 72), f32)
        wgate_b[:, 32:36] = w_in[:, gi_b:gi_b + 4]
        wgate_b[:, 36 + 32:36 + 36] = w_in[:, gf_b:gf_b + 4]
        bgv = np.zeros((36, 2), f32)
        bgv[0:4, 0] = b_in[gi_f:gi_f + 4]
        bgv[0:4, 1] = b_in[gf_f:gf_f + 4]
        bgv[32:36, 0] = b_in[gi_b:gi_b + 4]
        bgv[32:36, 1] = b_in[gf_b:gf_b + 4]
        vecs = np.zeros((128, NV), f32)

        def put(name, arr):
            o, w = VEC[name]
            assert arr.shape == (128, w), (name, arr.shape)
            vecs[:, o:o + w] = arr
        put("bada", fm(np.asarray(inp["b_ada"], f32)[0]))
        put("ng", np.concatenate([fm(norm_g[i]) for i in range(6)], axis=1))
        put("bF", fm(b_in[COL_F:COL_Q])); put("bq", fm(b_in[COL_Q:COL_K])); put("bk", fm(b_in[COL_K:COL_V]))
        put("cwq", np.concatenate([fm(cw[t, 0:D]) for t in range(3)], axis=1))
        put("cwk", np.concatenate([fm(cw[t, D:2 * D]) for t in range(3)], axis=1))
        put("cbq", fm(conv_b[0:D])); put("cbk", fm(conv_b[D:2 * D]))
        put("bgf", fm(b_in[COL_BR:COL_BR + D])); put("bgm", fm(b_in[COL_BR + D:]))
        cvec = np.zeros((128, 8, 2), f32)
        cvec[:, :, 0] = fm(c[b])
        cvec[:, :, 1] = fm(c_ctx)
        CS, M2, M3 = _dft_consts(flip)
        cst = np.zeros((128, 128 * 4 + 256 + 128 + 32), f32)
        cst[:, 0:128] = np.eye(128)
        cst[:, 128:256] = maskf
        cst[:, 256:384] = maskb
        cst[:, 512:768] = CS
        cst[:, 768:896] = M2
        cst[:, 896:928] = M3
        m = dict(shared)
        m.update({
            "xT": np.ascontiguousarray(xb.T), "ctxT": np.ascontiguousarray(cb_.T),
            "cvec": cvec.reshape(128, 16), "vecs": vecs, "bg": bgv,
            "wgate_f": np.ascontiguousarray(wgate_f), "wgate_b": wgate_b, "cst": cst, "selc": selc,
        })
        maps.append(m)
    return maps


def kernel(**inputs):
    nc, _ = build()
    maps = make_inputs(inputs)
    res = run_bass_kernel_spmd(nc, maps, core_ids=list(range(8)))
    out = np.zeros((4, SEQ, D), np.float32)
    for core in range(8):
        b, half = core // 2, core % 2
        o = np.asarray(res.results[core]["outT"]).T
        if half == 0:
            out[b, 0:OWN] = o
        else:
            out[b, OWN:] = o[::-1]
    return out
```

```python
import numpy as np
import os as _os
from contextlib import ExitStack
import concourse.bass as bass
import concourse.mybir as mybir
from concourse.bass_utils import run_bass_kernel_spmd

F32 = mybir.dt.float32
BF16 = mybir.dt.bfloat16
AF = mybir.ActivationFunctionType
ALU = mybir.AluOpType

ENGS = ("pe", "act", "dve", "pool", "sp")
EPOCH = 30000

D = 1024
SEQ = 4096
CTX = 256
NT = CTX + SEQ
OWN = 2048
FF = 2816
NJ = 22
EPS = 1e-6
NEG = -30000.0


class Tick:
    __slots__ = ("sem", "val", "know")

    def __init__(self, sem, val, know):
        self.sem = sem
        self.val = val
        self.know = know


class Buf:
    __slots__ = ("name", "w", "r")

    def __init__(self, name=""):
        self.name = name
        self.w = None
        self.r = {}


class Sched:
    def __init__(self, nc, stack, n_dma_sems=48):
        self.nc = nc
        self.stack = stack
        self.q = {e: [] for e in ENGS}
        self.cnt = {e: 0 for e in ENGS}
        self.esems = {e: [] for e in ENGS}
        self.known = {e: {} for e in ENGS}
        self.dsems = [stack.enter_context(nc.semaphore(f"dma{i}")) for i in range(n_dma_sems)]
        self.dcnt = [0] * n_dma_sems
        self.dlast = [None] * n_dma_sems
        self.drr = 0
        self.drr2 = {}

    def _esem(self, eng, idx):
        lst = self.esems[eng]
        while len(lst) <= idx:
            lst.append(self.stack.enter_context(self.nc.semaphore(f"e_{eng}_{len(lst)}")))
        return lst[idx]

    def _collect(self, eng, reads, writes, extra=()):
        kn = self.known[eng]
        waits = {}

        def need(t):
            if t is None:
                return
            if kn.get(t.sem, 0) >= t.val:
                return
            if waits.get(t.sem, (0, None))[0] < t.val:
                waits[t.sem] = (t.val, t)

        for b in reads:
            need(b.w)
        for b in writes:
            need(b.w)
            for t in b.r.values():
                need(t)
        for t in extra:
            need(t)
        items = sorted(waits.items(), key=lambda kv: -kv[1][0])
        final = []
        for sem, (val, t) in items:
            if kn.get(sem, 0) >= val:
                continue
            final.append((sem, val))
            kn[sem] = val
            for s2, v2 in t.know.items():
                if kn.get(s2, 0) < v2:
                    kn[s2] = v2
        return final

    def op(self, eng, fn, reads=(), writes=(), extra=()):
        waits = self._collect(eng, reads, writes, extra)
        c = self.cnt[eng]
        sem = self._esem(eng, c // EPOCH)
        val = c % EPOCH + 1
        self.cnt[eng] = c + 1
        t = Tick(sem, val, dict(self.known[eng]))
        for b in reads:
            b.r[sem] = t
        for b in writes:
            b.w = t
            b.r = {}
        self.q[eng].append((waits, fn, sem, 1))
        return t

    def dma(self, eng, fn, reads=(), writes=(), extra=()):
        n = len(self.dsems)
        lo, hi = (0, n // 3) if eng == "pool" else (n // 3, n)
        rr = self.drr2.get(eng, lo)
        i = rr
        self.drr2[eng] = lo + (rr + 1 - lo) % (hi - lo)
        ex = list(extra)
        if self.dlast[i] is not None:
            ex.append(self.dlast[i])
        waits = self._collect(eng, reads, writes, ex)
        self.dcnt[i] += 1
        sem = self.dsems[i]
        t = Tick(sem, 16 * self.dcnt[i], dict(self.known[eng]))
        self.dlast[i] = t
        for b in reads:
            b.r[sem] = t
        for b in writes:
            b.w = t
            b.r = {}
        self.q[eng].append((waits, fn, sem, 16))
        return t

    def wait_all(self, eng, ticks):
        waits = self._collect(eng, (), (), ticks)
        self.q[eng].append((waits, None, None, 0))

    def barrier(self, skip=()):
        skipset = {(t.sem, t.val) for t in skip}
        ticks = []
        for e in ENGS:
            c = self.cnt[e]
            if c > 0:
                ticks.append(Tick(self._esem(e, (c - 1) // EPOCH), (c - 1) % EPOCH + 1, {}))
        for t in self.dlast:
            if t is not None and (t.sem, t.val) not in skipset:
                ticks.append(t)
        for e in ENGS:
            self.wait_all(e, ticks)

    def emit(self):
        nc = self.nc
        q = self.q

        def run(engobj, lst):
            for waits, fn, sem, amt in lst:
                for s, v in waits:
                    engobj.wait_ge(s, v)
                if fn is not None:
                    ins = fn(engobj)
                    ins.then_inc(sem, amt)

        with nc.Block() as block:
            @block.tensor
            def _(e):
                run(e, q["pe"])

            @block.scalar
            def _(e):
                run(e, q["act"])

            @block.vector
            def _(e):
                run(e, q["dve"])

            @block.gpsimd
            def _(e):
                run(e, q["pool"])

            @block.sync
            def _(e):
                run(e, q["sp"])


class Arena:
    def __init__(self, nc, name, words):
        self.t = nc.alloc_sbuf_tensor(name, [128, words], F32)
        self.words = words
        self.off = 0

    def mark(self):
        return self.off

    def reset(self, m):
        self.off = m

    def f32(self, n):
        a = self.t[:, self.off:self.off + n]
        self.off += n
        assert self.off <= self.words, ("arena overflow", self.off, self.words)
        return a

    def bf16(self, n):
        w = (n + 1) // 2
        a = self.t[:, self.off:self.off + w].bitcast(BF16)
        self.off += w
        assert self.off <= self.words, ("arena overflow", self.off, self.words)
        return a[:, 0:n]


VEC = {}
_o = 0
for _n, _w in [("bada", 72), ("ng", 48), ("bF", 4), ("bq", 8), ("bk", 8), ("cwq", 24), ("cwk", 24),
               ("cbq", 8), ("cbk", 8), ("bgf", 8), ("bgm", 8)]:
    VEC[_n] = (_o, _w)
    _o += _w
NV = _o


def build(stage=99, dbg=False):
    nc = bass.Bass("TRN2", target_bir_lowering=False)
    dt_in = lambda name, shape, dt=F32: nc.dram_tensor(name, shape, dt, kind="ExternalInput").ap()
    xT = dt_in("xT", [D, SEQ])
    ctxT = dt_in("ctxT", [D, CTX])
    cvec = dt_in("cvec", [128, 16])
    w_ada = dt_in("w_ada", [D, 9 * D])
    vecs = dt_in("vecs", [128, NV])
    rowv = dt_in("rowv", [1, 3 * D])
    bg = dt_in("bg", [36, 2])
    w13a = dt_in("w13a", [NJ, 128, 2048])
    w2a = dt_in("w2a", [8, 128, FF])
    w13b = dt_in("w13b", [NJ, 128, 2048])
    w2b = dt_in("w2b", [8, 128, FF])
    wF = dt_in("wF", [D, 512])
    wq = dt_in("wq", [D, D])
    wk = dt_in("wk", [D, D])
    wv = dt_in("wv", [D, D])
    wo = dt_in("wo", [D, D])
    wgf = dt_in("wgf", [D, D])
    wgm = dt_in("wgm", [D, D])
    wgate_f = dt_in("wgate_f", [D, 8])
    wgate_b = dt_in("wgate_b", [D, 72])
    w_four = dt_in("w_four", [512, D])
    w_mproj = dt_in("w_mproj", [D, D])
    w_out = dt_in("w_out", [D, D])
    cst = dt_in("cst", [128, 128 * 4 + 256 + 128 + 32])
    selc = dt_in("selc", [36, 2 * 8 * 128])
    outT = nc.dram_tensor("outT", [D, OWN], F32, kind="ExternalOutput").ap()
    dbg_out = {}

    def dbg_tensor(name, shape, dt=F32):
        dbg_out[name] = nc.dram_tensor(name, shape, dt, kind="ExternalOutput").ap()
        return dbg_out[name]

    dscr = lambda name, shape, dt: nc.dram_tensor(name, shape, dt, kind="Internal").ap()
    H1 = dscr("H1", [D, OWN], F32)
    AB = dscr("AB", [2, SEQ, 512], F32)
    PQ = dscr("PQ", [2, 64, 64, 512], F32)
    KT = dscr("KT", [D, OWN], BF16)
    QT = dscr("QT", [D, OWN], BF16)
    KTOK = dscr("KTOK", [NT, D], BF16)
    VTOK = dscr("VTOK", [NT, D], BF16)
    B_H1, B_AB, B_PQ, B_KT, B_QT, B_KTOK, B_VTOK = [Buf(n) for n in "H1 AB PQ KT QT KTOK VTOK".split()]

    st = ExitStack()
    with st:
        S = Sched(nc, st)
        ps = [st.enter_context(nc.psum_tensor(f"ps{i}", [128, 512], F32)) for i in range(8)]
        Bps = [Buf(f"ps{i}") for i in range(8)]

        cs_t = nc.alloc_sbuf_tensor("cs", [128, 128 * 4 + 256 + 128 + 32], F32)
        ident = cs_t[:, 0:128]
        maskf = cs_t[:, 128:256]
        maskb = cs_t[:, 256:384]
        CS = cs_t[:, 512:768]
        M2 = cs_t[:, 768:896]
        M3 = cs_t[:, 896:928]
        vec_t = nc.alloc_sbuf_tensor("vec", [128, NV], F32)
        mod_t = nc.alloc_sbuf_tensor("mod", [128, 72 * 2], F32)
        der_t = nc.alloc_sbuf_tensor("der", [128, 16 * 8], F32)
        id16_t = nc.alloc_sbuf_tensor("id16", [128, 128], BF16)
        ones16_t = nc.alloc_sbuf_tensor("ones16", [128, 128], BF16)
        u2_t = nc.alloc_sbuf_tensor("u2", [128, 8 * NT], BF16)
        u2 = u2_t[:, :].rearrange("p (k t) -> p k t", k=8)
        B_const = Buf("const")
        B_mod = Buf("mod")
        B_der = Buf("der")
        tiles = [(0, CTX, 1)] + [(CTX + 512 * i, 512, 0) for i in range(8)]
        B_u2 = [Buf(f"u2_{i}") for i in range(9)]

        def vcol(name, i=0, n=1):
            o, w = VEC[name]
            return vec_t[:, o + i:o + i + n]

        DER = {}
        _d = 0
        for nm in ["A0l", "A0c", "S0l", "S0c", "PAl", "PAc", "A2l", "A2c", "S2l", "S2c", "PMl", "A4l", "S4l", "PBl"]:
            DER[nm] = der_t[:, _d * 8:(_d + 1) * 8]
            _d += 1

        arena = Arena(nc, "arena", 34000)

        S.dma("sp", lambda e: e.dma_start(out=cs_t[:, :], in_=cst), writes=[B_const])
        S.dma("sp", lambda e: e.dma_start(out=vec_t[:, :], in_=vecs), writes=[B_const])
        S.op("act", lambda e: e.copy(id16_t[:, :], ident), reads=[B_const], writes=[B_const])
        S.op("pool", lambda e: e.memset(ones16_t[:, :], 1.0), writes=[B_const])

        m0 = arena.mark()
        cv = arena.f32(16)
        scv = arena.f32(16)
        B_cv = Buf("cv")
        S.dma("sp", lambda e: e.dma_start(out=cv, in_=cvec), writes=[B_cv])
        S.op("act", lambda e: e.activation(scv, cv, AF.Silu), reads=[B_cv], writes=[B_cv])
        wad = [arena.f32(8 * 1024) for _ in range(2)]
        B_wad = [Buf("wad0"), Buf("wad1")]
        w_ada_v = w_ada.rearrange("(k p) n -> p k n", p=128)
        modps = ps[7][:, 0:144]
        for mi in range(9):
            sl = mi % 2
            wv_ = wad[sl].rearrange("p (k n) -> p k n", k=8)
            S.dma("sp", (lambda e, wv_=wv_, mi=mi: e.dma_start(out=wv_, in_=w_ada_v[:, :, mi * 1024:(mi + 1) * 1024])),
                  writes=[B_wad[sl]])
            for dc in range(8):
                def fn(e, wv_=wv_, mi=mi, dc=dc):
                    for k in range(8):
                        ins = e.matmul(modps[:, (mi * 8 + dc) * 2:(mi * 8 + dc) * 2 + 2],
                                       wv_[:, k, dc * 128:(dc + 1) * 128],
                                       scv[:, k * 2:k * 2 + 2], start=(k == 0), stop=(k == 7))
                    return ins
                S.op("pe", fn, reads=[B_wad[sl], B_cv], writes=[Bps[7]])
        modv = mod_t[:, :].rearrange("p (m j) -> p m j", j=2)
        modpsv = modps.rearrange("p (m j) -> p m j", j=2)
        bada = vcol("bada", 0, 72)
        for j in range(2):
            S.op("dve", (lambda e, j=j: e.tensor_tensor(modv[:, :, j], modpsv[:, :, j], bada, ALU.add)),
                 reads=[Bps[7], B_const], writes=[B_mod])

        def modc(mi, j):
            return modv[:, mi * 8:(mi + 1) * 8, j]

        def ng(i):
            return vcol("ng", i * 8, 8)

        def der_scale(name, mi, gi, j):
            S.op("dve", lambda e: e.scalar_tensor_tensor(DER[name], modc(mi, j), 1.0, ng(gi), ALU.add, ALU.mult),
                 reads=[B_mod, B_const], writes=[B_der])

        def der_gate(name, mi, gi, j, f):
            S.op("dve", lambda e: e.scalar_tensor_tensor(DER[name], modc(mi, j), f, ng(gi), ALU.mult, ALU.mult),
                 reads=[B_mod, B_const], writes=[B_der])

        def der_copy(name, mi, j):
            S.op("dve", lambda e: e.tensor_copy(DER[name], modc(mi, j)), reads=[B_mod], writes=[B_der])

        der_scale("A0l", 1, 0, 0); der_scale("A0c", 1, 0, 1)
        der_copy("S0l", 0, 0); der_copy("S0c", 0, 1)
        der_gate("PAl", 2, 1, 0, 0.5); der_gate("PAc", 2, 1, 1, 0.5)
        der_scale("A2l", 4, 2, 0); der_scale("A2c", 4, 2, 1)
        der_copy("S2l", 3, 0); der_copy("S2c", 3, 1)
        der_gate("PMl", 5, 3, 0, 1.0)
        der_scale("A4l", 7, 4, 0); der_copy("S4l", 6, 0); der_gate("PBl", 8, 5, 0, 0.5)
        S.barrier()
        arena.reset(m0)

        def rstd_from(sq_tile, B_sq, W, rstd, B_rstd, pbank):
            def fn(e):
                for k in range(8):
                    ins = e.matmul(ps[pbank][:, 0:W], ones16_t[:, :], sq_tile[:, k, 0:W], start=(k == 0), stop=(k == 7))
                return ins
            S.op("pe", fn, reads=[B_sq, B_const], writes=[Bps[pbank]])
            S.op("act", lambda e: e.activation(rstd[:, 0:W], ps[pbank][:, 0:W], AF.Ln, bias=EPS, scale=1.0 / D),
                 reads=[Bps[pbank]], writes=[B_rstd])
            S.op("act", lambda e: e.activation(rstd[:, 0:W], rstd[:, 0:W], AF.Exp, scale=-0.5),
                 reads=[B_rstd], writes=[B_rstd])

        def norm_mod_thunks(src, B_src, W, rstd, B_rstd, A, Sh, dst_fn, B_dst, tmp, B_tmp):
            def one(k):
                t = tmp[k % 2]
                bt = B_tmp[k % 2]
                S.op("dve", (lambda e: e.tensor_tensor(t[:, 0:W], src[:, k, 0:W], rstd[:, 0:W], ALU.mult)),
                     reads=[B_src, B_rstd], writes=[bt])
                S.op("act", (lambda e: e.activation(dst_fn(k), t[:, 0:W], AF.Identity, bias=Sh[:, k:k + 1], scale=A[:, k:k + 1])),
                     reads=[bt, B_der], writes=[B_dst])
            return [(lambda k=k: one(k)) for k in range(8)]

        def norm_mod(src, B_src, W, rstd, B_rstd, A, Sh, dst_fn, B_dst, tmp, B_tmp):
            for k in range(8):
                t = tmp[k % 2]
                bt = B_tmp[k % 2]
                S.op("dve", (lambda e, k=k, t=t: e.tensor_tensor(t[:, 0:W], src[:, k, 0:W], rstd[:, 0:W], ALU.mult)),
                     reads=[B_src, B_rstd], writes=[bt])
                S.op("act", (lambda e, k=k, t=t: e.activation(dst_fn(k), t[:, 0:W], AF.Identity,
                                                              bias=Sh[:, k:k + 1], scale=A[:, k:k + 1])),
                     reads=[bt, B_der], writes=[B_dst])

        WC = {}

        def wload(dst, B_dst, src_f32, cache, key, ncols):
            if cache is None:
                S.dma("pool", lambda e: e.dma_start(out=dst, in_=src_f32), writes=[B_dst])
                return
            k = (cache,) + key
            if k not in WC:
                sc_ = nc.dram_tensor("wc_" + "_".join(str(z) for z in k), [128, ncols], BF16, kind="Internal").ap()
                WC[k] = (sc_, Buf("wc"))
                S.dma("pool", lambda e: e.dma_start(out=dst, in_=src_f32), writes=[B_dst])
                dflat = dst if len(dst.shape) == 2 else dst.rearrange("p a b -> p (a b)")
                S.dma("sp", lambda e: e.dma_start(out=sc_, in_=dflat), reads=[B_dst], writes=[WC[k][1]])
            else:
                sc_, bsc = WC[k]
                dflat = dst if len(dst.shape) == 2 else dst.rearrange("p a b -> p (a b)")
                S.dma("sp", lambda e: e.dma_start(out=dflat, in_=sc_), reads=[bsc], writes=[B_dst])

        def ffn(src, B_src, W, rstd_pre, B_rstd_pre, A, Sh, PG, w13r, w2r, bufs, cache, do_pre=True, hook1=None, hook2=None, defer_epi=False):
            (sq, B_sq, u, B_u, g, B_g, y, B_y, tmp, B_tmp, rs2, B_rs2, wb13, B_wb13, wb2, B_wb2, sa, B_sa) = bufs
            if do_pre:
                norm_mod(src, B_src, W, rstd_pre, B_rstd_pre, A, Sh, lambda k: u[:, k, 0:W], B_u, tmp, B_tmp)
            n13 = len(wb13)
            for j in range(NJ):
                sl = j % n13
                wbv = wb13[sl].rearrange("p (k c) -> p k c", k=8)
                wload(wb13[sl], B_wb13[sl], w13r[j], cache, ("w13", j), 2048)
                pa = j % 2
                pb = 2 + j % 2

                def fa(e, wbv=wbv, pa=pa):
                    for k in range(8):
                        ins = e.matmul(ps[pa][:, 0:W], wbv[:, k, 0:128], u[:, k, 0:W], start=(k == 0), stop=(k == 7))
                    return ins

                def fb(e, wbv=wbv, pb=pb):
                    for k in range(8):
                        ins = e.matmul(ps[pb][:, 0:W], wbv[:, k, 128:256], u[:, k, 0:W], start=(k == 0), stop=(k == 7))
                    return ins
                S.op("pe", fa, reads=[B_wb13[sl], B_u], writes=[Bps[pa]])
                S.op("pe", fb, reads=[B_wb13[sl], B_u], writes=[Bps[pb]])
                s2 = j % 2
                S.op("act", (lambda e, pa=pa, s2=s2: e.activation(sa[s2][:, 0:W], ps[pa][:, 0:W], AF.Silu)),
                     reads=[Bps[pa]], writes=[B_sa[s2]])
                S.op("dve", (lambda e, pb=pb, s2=s2, j=j: e.tensor_tensor(g[:, j, 0:W], sa[s2][:, 0:W], ps[pb][:, 0:W], ALU.mult)),
                     reads=[B_sa[s2], Bps[pb]], writes=[B_g])
                if hook1 is not None:
                    hook1(j)
            n2 = len(wb2)
            HJ = NJ // 2
            for i in range(8):
                halves = []
                for hf in range(2):
                    sl = (2 * i + hf) % n2
                    wload(wb2[sl], B_wb2[sl], w2r[i][:, hf * HJ * 128:(hf + 1) * HJ * 128], cache, ("w2", i, hf), HJ * 128)
                    halves.append((wb2[sl].rearrange("p (j c) -> p j c", j=HJ), B_wb2[sl]))
                py = 4 + i % 2

                def fy(e, halves=halves, py=py):
                    for j in range(NJ):
                        wv_ = halves[j // HJ][0]
                        ins = e.matmul(ps[py][:, 0:W], wv_[:, j % HJ, :], g[:, j, 0:W], start=(j == 0), stop=(j == NJ - 1))
                    return ins
                S.op("pe", fy, reads=[halves[0][1], halves[1][1], B_g], writes=[Bps[py]])
                S.op("act", (lambda e, py=py, i=i: e.copy(y[:, i, 0:W], ps[py][:, 0:W])), reads=[Bps[py]], writes=[B_y])
                S.op("act", (lambda e, py=py, i=i: e.activation(sq[:, i, 0:W], ps[py][:, 0:W], AF.Square)),
                     reads=[Bps[py]], writes=[B_sq])
                if hook2 is not None:
                    hook2(i)
            rstd_from(sq, B_sq, W, rs2, B_rs2, 7)

            def resid(i):
                t = tmp[i % 2]
                bt = B_tmp[i % 2]
                S.op("dve", (lambda e: e.tensor_tensor(t[:, 0:W], y[:, i, 0:W], rs2[:, 0:W], ALU.mult)),
                     reads=[B_y, B_rs2], writes=[bt])
                S.op("dve", (lambda e: e.scalar_tensor_tensor(src[:, i, 0:W], t[:, 0:W], PG[:, i:i + 1],
                                                              src[:, i, 0:W], ALU.mult, ALU.add)),
                     reads=[bt, B_der], writes=[B_src])
            thunks = [(lambda i=i: resid(i)) for i in range(8)]
            if defer_epi:
                return thunks
            for th in thunks:
                th()
            return []

        def alloc_ffn_bufs():
            sq = arena.bf16(8 * 512).rearrange("p (k t) -> p k t", k=8)
            u = arena.bf16(8 * 512).rearrange("p (k t) -> p k t", k=8)
            g = arena.bf16(NJ * 512).rearrange("p (k t) -> p k t", k=NJ)
            y = arena.f32(8 * 512).rearrange("p (k t) -> p k t", k=8)
            tmp = [arena.f32(512) for _ in range(2)]
            rs2 = arena.f32(512)
            wb13 = [arena.bf16(2048) for _ in range(3)]
            wb2 = [arena.bf16(FF // 2) for _ in range(4)]
            sa = [arena.f32(512) for _ in range(2)]
            return (sq, Buf("sq"), u, Buf("u"), g, Buf("g"), y, Buf("y"), tmp, [Buf("t0"), Buf("t1")],
                    rs2, Buf("rs2"), wb13, [Buf("wb13_%d" % i) for i in range(3)], wb2, [Buf("wb2_%d" % i) for i in range(4)],
                    sa, [Buf("sa0"), Buf("sa1")])

        mA = arena.mark()
        xt = [arena.f32(8 * 512).rearrange("p (k t) -> p k t", k=8) for _ in range(2)]
        B_xt = [Buf("xt0"), Buf("xt1")]
        rs1 = arena.f32(512)
        B_rs1 = Buf("rs1")
        fb = alloc_ffn_bufs()
        tmp, B_tmp, u_, B_u_ = fb[8], fb[9], fb[2], fb[3]
        sqx = arena.bf16(8 * 512).rearrange("p (k t) -> p k t", k=8)
        B_sqx = Buf("sqx")
        xT_v = xT.rearrange("(k p) t -> p k t", p=128)
        ctxT_v = ctxT.rearrange("(k p) t -> p k t", p=128)
        if dbg:
            d_u2 = dbg_tensor("d_u2", [128, 8 * NT], BF16)
        ntile = len(tiles) if stage >= 1 else 0

        def a1_load(ti):
            c0, W, j = tiles[ti]
            x = xt[ti % 2]
            src = ctxT_v[:, :, 0:W] if j == 1 else xT_v[:, :, c0 - CTX:c0 - CTX + W]
            S.dma("pool", lambda e: e.dma_start(out=x[:, :, 0:W], in_=src), writes=[B_xt[ti % 2]])

        def sq_thunks(x, bx, W):
            return [(lambda k=k: S.op("act", (lambda e: e.activation(sqx[:, k, 0:W], x[:, k, 0:W], AF.Square)), reads=[bx], writes=[B_sqx]))
                    for k in range(8)]

        def a1_pre_thunks(ti):
            c0, W, j = tiles[ti]
            x = xt[ti % 2]; bx = B_xt[ti % 2]
            sfx = "c" if j == 1 else "l"
            th = sq_thunks(x, bx, W)
            th.append(lambda: rstd_from(sqx, B_sqx, W, rs1, B_rs1, 6))
            th += norm_mod_thunks(x, bx, W, rs1, B_rs1, DER["A0" + sfx], DER["S0" + sfx], lambda k: u_[:, k, 0:W], B_u_, tmp, B_tmp)
            return th

        def a1_epi2_thunks(ti):
            c0, W, j = tiles[ti]
            x = xt[ti % 2]; bx = B_xt[ti % 2]
            sfx = "c" if j == 1 else "l"
            th = []
            if 1 <= ti <= 4:
                o0 = c0 - CTX
                th.append(lambda: S.dma("pool", lambda e: e.dma_start(out=H1.rearrange("(k p) t -> p k t", p=128)[:, :, o0:o0 + 512], in_=x[:, :, :]),
                                        reads=[bx], writes=[B_H1]))
            th += sq_thunks(x, bx, W)
            th.append(lambda: rstd_from(sqx, B_sqx, W, rs1, B_rs1, 6))
            th += norm_mod_thunks(x, bx, W, rs1, B_rs1, DER["A2" + sfx], DER["S2" + sfx], (lambda k: u2[:, k, c0:c0 + W]), B_u2[ti], tmp, B_tmp)
            return th

        pend1 = []
        pend2 = []
        if ntile:
            a1_load(0)
            for th in a1_pre_thunks(0):
                th()
            if ntile > 1:
                a1_load(1)
        for ti in range(ntile):
            c0, W, j = tiles[ti]
            sfx = "c" if j == 1 else "l"
            if ti + 1 < ntile:
                pend2 = a1_pre_thunks(ti + 1)

            def hook1(j_):
                n = 2 if len(pend1) > (NJ - 1 - j_) else 1
                for _ in range(n):
                    if pend1:
                        pend1.pop(0)()

            def hook2(i_):
                while pend1:
                    pend1.pop(0)()
                if i_ >= 1:
                    for _ in range(3):
                        if pend2:
                            pend2.pop(0)()
            epi1 = ffn(xt[ti % 2], B_xt[ti % 2], W, None, None, None, None, DER["PA" + sfx], w13a, w2a, fb, "A",
                       do_pre=False, hook1=hook1, hook2=hook2, defer_epi=True)
            while pend1:
                pend1.pop(0)()
            while pend2:
                pend2.pop(0)()
            pend1 = list(epi1) + a1_epi2_thunks(ti)
            if ti + 2 < ntile:
                pend1.append(lambda ti=ti: a1_load(ti + 2))
        while pend1:
            pend1.pop(0)()
        S.barrier()
        arena.reset(mA)
        if dbg:
            S.dma("sp", lambda e: e.dma_start(out=d_u2, in_=u2_t[:, :]), reads=B_u2)
            d_h1 = dbg_tensor("d_h1", [D, OWN])
            S.dma("sp", lambda e: e.dma_start(out=d_h1, in_=H1), reads=[B_H1])

        env = dict(locals())
        if stage >= 2:
            build_rest(env)
        S.barrier()
        S.emit()
    return nc, dbg_out


def build_rest3(g_, L):
    AX = mybir.AxisListType
    nc = g_["nc"]; S = g_["S"]; ps = g_["ps"]; Bps = g_["Bps"]; arena = g_["arena"]
    u2 = g_["u2"]; B_u2 = g_["B_u2"]; vcol = g_["vcol"]; DER = g_["DER"]
    id16 = g_["id16"]; B_const = g_["B_const"]; mm = g_["mm"]
    H1, HS = g_["H1"], g_["HS"]; B_H1 = g_["B_H1"]; B_HS = g_["B_HS"]
    yfT, B_yfT, mB = g_["yfT"], g_["B_yfT"], g_["mB"]
    rowv = g_["rowv"]; outT = g_["outT"]
    ffn = g_["ffn"]; rstd_from = g_["rstd_from"]; wload = g_["wload"]
    kp = lambda ap: ap.rearrange("(k p) n -> p k n", p=128)
    A = arena.t
    arena.reset(mB)
    tmp = [A[:, 0:512], A[:, 512:1024]]; B_tmp = [Buf("ct0"), Buf("ct1")]
    rs2 = A[:, 1024:1536]; B_rs2 = Buf("crs2")
    yreg = A[:, 5816:9912]
    B_yreg = Buf("yreg")
    y = yreg.rearrange("p (k t) -> p k t", k=8)
    hs_t = yreg[:, 0:1024]; o_sb = yreg[:, 1024:2048]; sig = yreg[:, 2048:3072]; sqh = yreg[:, 3072:4096]
    B_hsC = Buf("c_hs"); B_osb = Buf("c_osb"); B_sig = Buf("c_sig"); B_sqh = Buf("c_sqh")
    bo_bc = A[:, 9912:10936]; hg_bc = A[:, 10936:11960]
    B_bc = Buf("cbc")
    x = arena.f32(4096).rearrange("p (k t) -> p k t", k=8); B_x = Buf("cx")
    rs1 = arena.f32(512); B_rs1 = Buf("crs1")
    sq = arena.bf16(4096).rearrange("p (k t) -> p k t", k=8); B_sq = Buf("csq")
    u = arena.bf16(4096).rearrange("p (k t) -> p k t", k=8); B_u = Buf("cu")
    hmT = sq; yT = u
    sa = [arena.f32(512), arena.f32(512)]; B_sa = [Buf("csa0"), Buf("csa1")]
    mX = arena.mark()
    g = arena.bf16(NJ * 512).rearrange("p (k t) -> p k t", k=NJ); B_g = Buf("cg")
    wb13m = [arena.bf16(2048), arena.bf16(2048), arena.bf16(2048)]; B_wb13m = [Buf("cw13_0"), Buf("cw13_1"), Buf("cw13_2")]
    wb2m = arena.bf16(FF); B_wb2m = Buf("cw2")
    wb2x = A[:, 11992:11992 + 704].bitcast(BF16), A[:, 11992 + 704:11992 + 1408].bitcast(BF16)
    arena.reset(mX)
    wsl = [arena.bf16(8 * 512).rearrange("p (k n) -> p k n", k=8) for _ in range(4)]; B_wsl = [Buf("wsl%d" % i) for i in range(4)]
    hm = arena.bf16(1024); B_hm = Buf("hm")
    ol = A[:, mX + 4096:mX + 8192].rearrange("p (k t) -> p k t", k=8)
    B_ol = [B_wsl[2], B_wsl[3]]
    ss = rs2[:, 0:8]
    fb = (sq, B_sq, u, B_u, g, B_g, y, B_yreg, tmp, B_tmp, rs2, B_rs2, wb13m, B_wb13m,
          [wb2m[:, 0:FF // 2], wb2m[:, FF // 2:FF], wb2x[0], wb2x[1]], [Buf('cw2a'), Buf('cw2b'), Buf('cw2c'), Buf('cw2d')], sa, B_sa)
    xflat = A[:, mB:mB + 4096]
    rv = xflat[:, 0:3072]; ones1 = xflat[:, 3072:3200]
    S.dma("sp", lambda e: e.dma_start(out=rv[0:1, :], in_=rowv), writes=[B_x])
    S.op("pool", lambda e: e.memset(ones1[0:1, :], 1.0), writes=[B_x])
    for (dst, seg) in ((bo_bc, 1), (hg_bc, 2)):
        for hh in range(2):
            mm(ps[6][:, 0:512], [(ones1[0:1, :], rv[0:1, seg * 1024 + hh * 512:seg * 1024 + hh * 512 + 512])], reads=[B_x], writes=[Bps[6]])
            S.op("act", (lambda e, hh=hh, dst=dst: e.copy(dst[:, hh * 512:(hh + 1) * 512], ps[6][:, 0:512])), reads=[Bps[6]], writes=[B_bc])
    S.barrier()
    w_four, w_mproj, w_out, wo, wgf, wgm = [g_[n] for n in "w_four w_mproj w_out wo wgf wgm".split()]
    w13b, w2b = g_["w13b"], g_["w2b"]
    H1v = H1.rearrange("(k p) t -> p k t", p=128)
    outv = outT.rearrange("(k p) t -> p k t", p=128)
    for T in range(4):
        t0 = 512 * T
        c0 = CTX + t0
        for hh in range(2):
            wload(wsl[hh], B_wsl[hh], kp(wo)[:, :, hh * 512:(hh + 1) * 512], "C", ("wo", hh), 4096)
        for ch in range(4):
            cc = c0 + 128 * ch
            ob = t0 + 128 * ch
            S.dma("sp", (lambda e, ob=ob: e.dma_start(out=hs_t, in_=HS[ob:ob + 128, :])), reads=[B_HS[ob // 128]], writes=[B_hsC])
            for hh in range(2):
                mm(ps[hh][:, 0:512], [(u2[:, k, cc:cc + 128], wsl[hh][:, k, :]) for k in range(8)], reads=[B_wsl[hh]] + B_u2, writes=[Bps[hh]])
                S.op("dve", (lambda e, hh=hh: e.tensor_tensor(o_sb[:, hh * 512:(hh + 1) * 512], ps[hh][:, 0:512], bo_bc[:, hh * 512:(hh + 1) * 512], ALU.add)),
                     reads=[Bps[hh], B_bc], writes=[B_osb])
            S.op("act", lambda e: e.activation(sig, o_sb, AF.Sigmoid), reads=[B_osb], writes=[B_sig])
            S.op("dve", lambda e: e.tensor_tensor(sig, sig, hg_bc, ALU.mult), reads=[B_sig, B_bc], writes=[B_sig])
            S.op("act", lambda e: e.activation(sqh, hs_t, AF.Square), reads=[B_hsC], writes=[B_sqh])
            S.op("dve", lambda e: e.reduce_sum(ss[:, 0:4], sqh.rearrange("p (h e) -> p h e", h=4), AX.X), reads=[B_sqh], writes=[B_rs2])
            S.op("act", lambda e: e.activation(ss[:, 0:4], ss[:, 0:4], AF.Ln, bias=EPS, scale=1.0 / 256), reads=[B_rs2], writes=[B_rs2])
            S.op("act", lambda e: e.activation(ss[:, 0:4], ss[:, 0:4], AF.Exp, scale=-0.5), reads=[B_rs2], writes=[B_rs2])
            for h in range(4):
                S.op("dve", (lambda e, h=h: e.scalar_tensor_tensor(hm[:, h * 256:(h + 1) * 256], hs_t[:, h * 256:(h + 1) * 256], ss[:, h:h + 1],
                                                                  sig[:, h * 256:(h + 1) * 256], ALU.mult, ALU.mult)),
                     reads=[B_hsC, B_sig, B_rs2], writes=[B_hm])
            for half in range(2):
                pb = 2 + half
                def fn(e, half=half, pb=pb):
                    for cq in range(4):
                        c = half * 4 + cq
                        ins = e.matmul(ps[pb][:, cq * 128:(cq + 1) * 128], hm[:, c * 128:(c + 1) * 128], id16[:, :], start=True, stop=True)
                    return ins
                S.op("pe", fn, reads=[B_hm, B_const], writes=[Bps[pb]])
                S.op("act", (lambda e, half=half, pb=pb, ch=ch: e.copy(hmT[:, half * 4:(half + 1) * 4, ch * 128:(ch + 1) * 128],
                                                                      ps[pb][:, 0:512].rearrange("p (c t) -> p c t", c=4))),
                     reads=[Bps[pb]], writes=[B_sq])
        for hh in range(2):
            cs_ = slice(hh * 512, (hh + 1) * 512)
            wload(wsl[0][:, 0:4, :], B_wsl[0], kp(w_four)[:, :, cs_], "C", ("w4", hh), 2048)
            wload(wsl[1], B_wsl[1], kp(w_mproj)[:, :, cs_], "C", ("wm", hh), 4096)
            wload(wsl[2], B_wsl[2], kp(wgf)[:, :, cs_], "C", ("wgf", hh), 4096)
            wload(wsl[3], B_wsl[3], kp(wgm)[:, :, cs_], "C", ("wgm", hh), 4096)
            for ii in range(4):
                i = hh * 4 + ii
                cw = slice(ii * 128, (ii + 1) * 128)
                mm(ps[0][:, 0:512], [(wsl[0][:, gq, cw], yfT[:, gq, t0:t0 + 512]) for gq in range(4)], reads=[B_wsl[0], B_yfT], writes=[Bps[0]])
                mm(ps[1][:, 0:512], [(wsl[1][:, k, cw], hmT[:, k, :]) for k in range(8)], reads=[B_wsl[1], B_sq], writes=[Bps[1]])
                mm(ps[2][:, 0:512], [(wsl[2][:, k, cw], u2[:, k, c0:c0 + 512]) for k in range(8)], reads=[B_wsl[2]] + B_u2, writes=[Bps[2]])
                mm(ps[3][:, 0:512], [(wsl[3][:, k, cw], u2[:, k, c0:c0 + 512]) for k in range(8)], reads=[B_wsl[3]] + B_u2, writes=[Bps[3]])
                S.op("act", (lambda e, i=i: e.activation(sa[0], ps[2][:, 0:512], AF.Sigmoid, bias=vcol("bgf", i, 1))), reads=[Bps[2], B_const], writes=[B_sa[0]])
                S.op("act", (lambda e, i=i: e.activation(sa[1], ps[3][:, 0:512], AF.Sigmoid, bias=vcol("bgm", i, 1))), reads=[Bps[3], B_const], writes=[B_sa[1]])
                S.op("dve", lambda e: e.tensor_tensor(sa[0], sa[0], ps[0][:, 0:512], ALU.mult), reads=[Bps[0], B_sa[0]], writes=[B_sa[0]])
                S.op("dve", lambda e: e.tensor_tensor(sa[1], sa[1], ps[1][:, 0:512], ALU.mult), reads=[Bps[1], B_sa[1]], writes=[B_sa[1]])
                S.op("dve", (lambda e, i=i: e.tensor_tensor(yT[:, i, :], sa[0], sa[1], ALU.add)), reads=B_sa, writes=[B_u])
        S.dma("sp", (lambda e, t0=t0: e.dma_start(out=x[:, :, :], in_=H1v[:, :, t0:t0 + 512])), reads=[B_H1], writes=[B_x])
        for hh in range(2):
            wload(wsl[hh], B_wsl[hh], kp(w_out)[:, :, hh * 512:(hh + 1) * 512], "C", ("wout", hh), 4096)
        for i in range(8):
            pb = 4 + i % 2
            mm(ps[pb][:, 0:512], [(wsl[i // 4][:, k, (i % 4) * 128:(i % 4 + 1) * 128], yT[:, k, :]) for k in range(8)],
               reads=[B_wsl[i // 4], B_u], writes=[Bps[pb]])
            S.op("act", (lambda e, i=i, pb=pb: e.copy(ol[:, i, :], ps[pb][:, 0:512])), reads=[Bps[pb]], writes=B_ol)
            S.op("act", (lambda e, i=i, pb=pb: e.activation(sq[:, i, :], ps[pb][:, 0:512], AF.Square)), reads=[Bps[pb]], writes=[B_sq])
        rstd_from(sq, B_sq, 512, rs1, B_rs1, 6)
        for i in range(8):
            t = tmp[i % 2]; bt = B_tmp[i % 2]
            S.op("dve", (lambda e, i=i, t=t: e.tensor_tensor(t, ol[:, i, :], rs1, ALU.mult)), reads=B_ol + [B_rs1], writes=[bt])
            S.op("dve", (lambda e, i=i, t=t: e.scalar_tensor_tensor(x[:, i, :], t, DER["PMl"][:, i:i + 1], x[:, i, :], ALU.mult, ALU.add)),
                 reads=[bt, g_["B_der"]], writes=[B_x])
        S.barrier()
        S.op("act", lambda e: e.activation(sq[:, :, :], x[:, :, :], AF.Square), reads=[B_x], writes=[B_sq])
        rstd_from(sq, B_sq, 512, rs1, B_rs1, 6)
        ffn(x, B_x, 512, rs1, B_rs1, DER["A4l"], DER["S4l"], DER["PBl"], w13b, w2b, fb, "B")
        t_store = S.dma("sp", (lambda e, t0=t0: e.dma_start(out=outv[:, :, t0:t0 + 512], in_=x[:, :, :])), reads=[B_x])
        S.barrier(skip=[t_store])


def build_rest2(env, L):
    AX = mybir.AxisListType
    g_ = dict(env); g_.update(L)
    nc = g_["nc"]; S = g_["S"]; ps = g_["ps"]; Bps = g_["Bps"]; arena = g_["arena"]
    u2 = g_["u2"]; B_u2 = g_["B_u2"]; vcol = g_["vcol"]; DER = g_["DER"]
    ident = g_["ident"]; maskf = g_["maskf"]; maskb = g_["maskb"]; CS = g_["CS"]; M2 = g_["M2"]; M3 = g_["M3"]
    id16 = g_["id16"]; ones16 = g_["ones16"]; B_const = g_["B_const"]; B_der = g_["B_der"]
    sel = g_["sel"]; negsel = g_["negsel"]; mm = g_["mm"]
    H1, AB, PQ, KT, QT, KTOK, VTOK, GROW, HS = [g_[n] for n in "H1 AB PQ KT QT KTOK VTOK GROW HS".split()]
    B_H1, B_AB, B_PQ, B_KT, B_QT, B_KTOK, B_VTOK, B_GROW = [g_["B_" + n] for n in "H1 AB PQ KT QT KTOK VTOK GROW".split()]
    B_HS = g_["B_HS"]
    Rcol, Gcol, Ecol, gend, acol, wkcol, decay, B_cols = [g_[n] for n in "Rcol Gcol Ecol gend acol wkcol decay B_cols".split()]
    yfT, B_yfT, C32, C16, B_C, mB = [g_[n] for n in "yfT B_yfT C32 C16 B_C mB".split()]
    rowv = g_["rowv"]; outT = g_["outT"]
    tiles = g_["tiles"]
    kp = lambda ap: ap.rearrange("(k p) n -> p k n", p=128)

    wbuf = [arena.bf16(8 * 1024).rearrange("p (k n) -> p k n", k=8) for _ in range(2)]
    B_wbuf = [Buf("wbuf0"), Buf("wbuf1")]
    xf = [arena.f32(512) for _ in range(4)]
    B_xf = [Buf("xf%d" % i) for i in range(4)]
    ab_sb2 = [arena.f32(1024).rearrange("p (x g c) -> p x g c", x=2, g=4) for _ in range(2)]
    B_ab2 = [Buf("ab_sb0"), Buf("ab_sb1")]
    Pt2 = [arena.f32(514), arena.f32(514)]
    B_Pt2 = [Buf("Pt0"), Buf("Pt1")]
    acc2 = [arena.f32(512), arena.f32(512)]
    B_acc2 = [Buf("acc0"), Buf("acc1")]
    kTt = arena.bf16(8 * 512).rearrange("p (k t) -> p k t", k=8)
    B_kTt = Buf("kTt")
    tok2 = [arena.bf16(1024), arena.bf16(1024)]
    B_tok2 = [Buf("tok0"), Buf("tok1")]
    tokctr = [0]
    bv_bc = arena.f32(1024)
    ones1 = arena.f32(128)
    rv = arena.f32(1024)
    B_bc = Buf("bc")
    S.dma("sp", lambda e: e.dma_start(out=rv[0:1, :], in_=rowv[:, 0:1024]), writes=[B_bc])
    S.op("pool", lambda e: e.memset(ones1[0:1, :], 1.0), writes=[B_bc])

    def bcast_row(dst, seg):
        for hh in range(2):
            mm(ps[6][:, 0:512], [(ones1[0:1, :], rv[0:1, seg * 1024 + hh * 512:seg * 1024 + hh * 512 + 512])], reads=[B_bc], writes=[Bps[6]])
            S.op("act", (lambda e, hh=hh: e.copy(dst[:, hh * 512:(hh + 1) * 512], ps[6][:, 0:512])), reads=[Bps[6]], writes=[B_bc])
    bcast_row(bv_bc, 0)

    wF = g_["wF"]
    S.dma("pool", lambda e: e.dma_start(out=wbuf[0][:, :, 0:512], in_=kp(wF)), writes=[B_wbuf[0]])
    for i in range(8):
        c0 = CTX + 512 * i
        for g in range(4):
            pb = g % 2
            mm(ps[pb][:, 0:512], [(wbuf[0][:, k, g * 128:(g + 1) * 128], u2[:, k, c0:c0 + 512]) for k in range(8)],
               reads=[B_wbuf[0]] + B_u2, writes=[Bps[pb]])
            S.op("act", (lambda e, g=g, pb=pb: e.activation(xf[g], ps[pb][:, 0:512], AF.Identity, bias=vcol("bF", g, 1))),
                 reads=[Bps[pb], B_const], writes=[B_xf[g]])
        for tb in range(4):
            ab_sb = ab_sb2[tb % 2]; B_ab = B_ab2[tb % 2]
            for g in range(4):
                bank = 2 + g // 2
                mm(ps[bank][:, (g % 2) * 256:(g % 2) * 256 + 256], [(xf[g][:, tb * 128:(tb + 1) * 128], CS)],
                   reads=[B_xf[g], B_const], writes=[Bps[bank]])
            for bi in range(2):
                S.op("dve", (lambda e, bi=bi, ab_sb=ab_sb: e.tensor_copy(ab_sb[:, :, 2 * bi:2 * bi + 2, :],
                                                            ps[2 + bi][:, 0:512].rearrange("p (g x c) -> p x g c", g=2, x=2))),
                     reads=[Bps[2 + bi]], writes=[B_ab])
            tok0 = 512 * i + 128 * tb
            S.dma("sp", (lambda e, tok0=tok0, ab_sb=ab_sb: e.dma_start(out=AB.rearrange("x t f -> t x f")[tok0:tok0 + 128, :, :],
                                                          in_=ab_sb.rearrange("p x g c -> p x (g c)"))), reads=[B_ab], writes=[B_AB])

    def qk_proj(wdram, bname, cwname, cbname, slot, tlist, is_k):
        S.dma("pool", lambda e: e.dma_start(out=wbuf[slot], in_=kp(wdram)), writes=[B_wbuf[slot]])
        pend_silu = [None]
        for (ti, c0, W) in tlist:
            islat = ti >= 1
            left = islat and ti > 1
            right = islat and ti < 8
            for c in range(8):
                pb = c % 2
                Pt = Pt2[c % 2]; B_Pt = B_Pt2[c % 2]; acc = acc2[c % 2]; B_acc = B_acc2[c % 2]
                hb = 5 + c % 2
                mm(ps[pb][:, 0:W], [(wbuf[slot][:, k, c * 128:(c + 1) * 128], u2[:, k, c0:c0 + W]) for k in range(8)],
                   reads=[B_wbuf[slot]] + B_u2, writes=[Bps[pb]])
                S.op("act", (lambda e, c=c, pb=pb, W=W, Pt=Pt: e.activation(Pt[:, 1:W + 1], ps[pb][:, 0:W], AF.Identity, bias=vcol(bname, c, 1))),
                     reads=[Bps[pb], B_const], writes=[B_Pt])
                if left and right:
                    mm(ps[hb][:, 0:2], [(wbuf[slot][:, k, c * 128:(c + 1) * 128], u2[:, k, c0 - 1:c0 + W + 1:W + 1]) for k in range(8)],
                       reads=[B_wbuf[slot]] + B_u2, writes=[Bps[hb]])
                    S.op("act", (lambda e, c=c, Pt=Pt, hb=hb, W=W: e.activation(Pt[:, 0:W + 2:W + 1], ps[hb][:, 0:2], AF.Identity, bias=vcol(bname, c, 1))),
                         reads=[Bps[hb], B_const], writes=[B_Pt])
                else:
                    for hi_, (has, col, dstc) in enumerate(((left, c0 - 1, 0), (right, c0 + W, W + 1))):
                        if has:
                            mm(ps[hb][:, hi_:hi_ + 1], [(wbuf[slot][:, k, c * 128:(c + 1) * 128], u2[:, k, col:col + 1]) for k in range(8)],
                               reads=[B_wbuf[slot]] + B_u2, writes=[Bps[hb]])
                            S.op("act", (lambda e, c=c, dstc=dstc, Pt=Pt, hb=hb, hi_=hi_: e.activation(Pt[:, dstc:dstc + 1], ps[hb][:, hi_:hi_ + 1], AF.Identity, bias=vcol(bname, c, 1))),
                                 reads=[Bps[hb], B_const], writes=[B_Pt])
                        else:
                            S.op("pool", (lambda e, dstc=dstc, Pt=Pt: e.memset(Pt[:, dstc:dstc + 1], 0.0)), writes=[B_Pt])
                S.op("act", (lambda e, c=c, W=W, Pt=Pt, acc=acc: e.activation(acc[:, 0:W], Pt[:, 0:W], AF.Identity, scale=vcol(cwname, c, 1))),
                     reads=[B_Pt, B_const], writes=[B_acc])
                S.op("dve", (lambda e, c=c, W=W, Pt=Pt, acc=acc: e.scalar_tensor_tensor(acc[:, 0:W], Pt[:, 1:W + 1], vcol(cwname, 8 + c, 1), acc[:, 0:W], ALU.mult, ALU.add)),
                     reads=[B_Pt, B_const], writes=[B_acc])
                S.op("dve", (lambda e, c=c, W=W, Pt=Pt, acc=acc: e.scalar_tensor_tensor(acc[:, 0:W], Pt[:, 2:W + 2], vcol(cwname, 16 + c, 1), acc[:, 0:W], ALU.mult, ALU.add)),
                     reads=[B_Pt, B_const], writes=[B_acc])
                if pend_silu[0] is not None:
                    pend_silu[0]()
                pend_silu[0] = (lambda c=c, W=W, acc=acc, B_acc=B_acc: S.op(
                    "act", (lambda e: e.activation(kTt[:, c, 0:W], acc[:, 0:W], AF.Silu, bias=vcol(cbname, c, 1))),
                    reads=[B_acc, B_const], writes=[B_kTt]))
            if pend_silu[0] is not None:
                pend_silu[0]()
                pend_silu[0] = None
            own = 1 <= ti <= 4
            if own:
                o0 = c0 - CTX
                dst = KT if is_k else QT
                S.dma("sp", (lambda e, o0=o0, dst=dst: e.dma_start(out=dst.rearrange("(k p) t -> p k t", p=128)[:, :, o0:o0 + 512], in_=kTt[:, :, :])),
                      reads=[B_kTt], writes=[B_KT if is_k else B_QT])
            if is_k:
                for tb in range(W // 128):
                    tok = tok2[tokctr[0] % 2]; B_tok = B_tok2[tokctr[0] % 2]; tokctr[0] += 1
                    for half in range(2):
                        pb = 3 + half
                        def fn(e, tb=tb, half=half, pb=pb):
                            for cc in range(4):
                                c = half * 4 + cc
                                ins = e.matmul(ps[pb][:, cc * 128:(cc + 1) * 128], kTt[:, c, tb * 128:(tb + 1) * 128], id16[:, :], start=True, stop=True)
                            return ins
                        S.op("pe", fn, reads=[B_kTt, B_const], writes=[Bps[pb]])
                        S.op("dve", (lambda e, half=half, pb=pb, tok=tok: e.tensor_copy(tok[:, half * 512:(half + 1) * 512], ps[pb][:, 0:512])),
                             reads=[Bps[pb]], writes=[B_tok])
                    r0 = c0 + tb * 128
                    S.dma("sp", (lambda e, r0=r0, tok=tok: e.dma_start(out=KTOK[r0:r0 + 128, :], in_=tok)), reads=[B_tok], writes=[B_KTOK])

    tl_all = [(ti, c0, W) for ti, (c0, W, j) in enumerate(tiles)]
    qk_proj(g_["wk"], "bk", "cwk", "cbk", 1, tl_all, True)
    qk_proj(g_["wq"], "bq", "cwq", "cbq", 0, tl_all[1:5], False)
    S.dma("pool", lambda e: e.dma_start(out=wbuf[1], in_=kp(g_["wv"])), writes=[B_wbuf[1]])
    for cb in range(0, NT, 128):
        tok = tok2[tokctr[0] % 2]; B_tok = B_tok2[tokctr[0] % 2]; tokctr[0] += 1
        for half in range(2):
            pb = half + 2 * ((cb // 128) % 2)
            mm(ps[pb][:, 0:512], [(u2[:, k, cb:cb + 128], wbuf[1][:, k, half * 512:(half + 1) * 512]) for k in range(8)],
               reads=[B_wbuf[1]] + B_u2, writes=[Bps[pb]])
            S.op("dve", (lambda e, half=half, pb=pb, tok=tok: e.tensor_tensor(tok[:, half * 512:(half + 1) * 512], ps[pb][:, 0:512],
                                                                     bv_bc[:, half * 512:(half + 1) * 512], ALU.add)),
                 reads=[Bps[pb], B_bc], writes=[B_tok])
        S.dma("sp", (lambda e, cb=cb, tok=tok: e.dma_start(out=VTOK[cb:cb + 128, :], in_=tok)), reads=[B_tok], writes=[B_VTOK])
    S.barrier()
    arena.reset(mB)

    inb2 = [arena.f32(8 * 512).rearrange("p (r f) -> p r f", r=8) for _ in range(2)]
    outb2 = [arena.f32(8 * 512).rearrange("p (r f) -> p r f", r=8) for _ in range(2)]
    B_inb2 = [Buf("inb0"), Buf("inb1")]; B_outb2 = [Buf("outb0"), Buf("outb1")]
    ABv = AB.rearrange("x (r c) f -> x c r f", c=64)
    PQw = PQ
    for rb in range(8):
        inb = inb2[rb % 2]; outb = outb2[rb % 2]; B_inb = B_inb2[rb % 2]; B_outb = B_outb2[rb % 2]
        for x in range(2):
            S.dma("sp", (lambda e, rb=rb, x=x, inb=inb: e.dma_start(out=inb[64 * x:64 * x + 64, :, :], in_=ABv[x, :, rb * 8:rb * 8 + 8, :])),
                  reads=[B_AB], writes=[B_inb])
        for r in range(8):
            pb = r % 4
            mm(ps[pb][:, 0:512], [(M2, inb[:, r, :])], reads=[B_inb, B_const], writes=[Bps[pb]])
            if r % 2 == 0:
                S.op("act", (lambda e, r=r, pb=pb, outb=outb: e.copy(outb[:, r, :], ps[pb][:, 0:512])), reads=[Bps[pb]], writes=[B_outb])
            else:
                S.op("dve", (lambda e, r=r, pb=pb, outb=outb: e.tensor_copy(outb[:, r, :], ps[pb][:, 0:512])), reads=[Bps[pb]], writes=[B_outb])
        for x in range(2):
            S.dma("sp", (lambda e, rb=rb, x=x, outb=outb: e.dma_start(out=PQw[x, :, rb * 8:rb * 8 + 8, :], in_=outb[64 * x:64 * x + 64, :, :])),
                  reads=[B_outb], writes=[B_PQ])
    PQr = PQ.rearrange("x kc r f -> x r kc f")
    for kb in range(8):
        inb = inb2[kb % 2]; B_inb = B_inb2[kb % 2]
        for x in range(2):
            S.dma("sp", (lambda e, kb=kb, x=x, inb=inb: e.dma_start(out=inb[64 * x:64 * x + 64, :, :], in_=PQr[x, :, kb * 8:kb * 8 + 8, :])),
                  reads=[B_PQ], writes=[B_inb])
        for g in range(4):
            pb = 4 + g
            def fn(e, g=g, pb=pb, inb=inb):
                for kc in range(8):
                    ins = e.matmul(ps[pb][:, kc * 32:(kc + 1) * 32], inb[:, kc, g * 128:(g + 1) * 128], M3, start=True, stop=True)
                return ins
            S.op("pe", fn, reads=[B_inb, B_const], writes=[Bps[pb]])
            S.op("act", (lambda e, g=g, pb=pb, kb=kb: e.copy(yfT[:, g, :].rearrange("p (kr kc) -> p kc kr", kc=64)[:, kb * 8:kb * 8 + 8, :],
                                                          ps[pb][:, 0:256].rearrange("p (kc kr) -> p kc kr", kr=32))),
                 reads=[Bps[pb]], writes=[B_yfT])
    S.barrier()
    arena.reset(mB)

    NLS = 3
    NHS = 2
    LD = []
    for i in range(2 * NLS):
        d_ = dict(ktok=arena.bf16(1024), vaug=arena.bf16(4 * 258).rearrange("p (h e) -> p h e", h=4),
                  kT=arena.bf16(1024).rearrange("p (k t) -> p k t", k=8), qT=arena.bf16(1024).rearrange("p (k t) -> p k t", k=8),
                  grow=arena.f32(128), B_ld=Buf("ld%d" % i), B_ldo=Buf("ldo%d" % i))
        S.op("pool", (lambda e, v=d_["vaug"]: e.memset(v[:, :, :], 1.0)), writes=[d_["B_ld"]])
        LD.append(d_)
    HSB = [dict(hs=arena.f32(1024), B_hs=Buf("hs%d" % i)) for i in range(4)]
    HT = []
    B_pP2s = Buf("pP2"); B_pP1s = Buf("pP1")
    for i in range(NHS):
        HT.append(dict(wT=arena.f32(128), STb=arena.bf16(128), P2sb=arena.f32(257), hn=arena.f32(257), dd=arena.f32(2), kw=arena.bf16(256),
                       B_wT=Buf("wT%d" % i), B_ST=Buf("ST%d" % i), B_P2=Buf("P2sb%d" % i), B_hn=Buf("hn%d" % i), B_dd=Buf("dd%d" % i),
                       B_kw=Buf("kw%d" % i),
                       pST=ps[i][:, 0:128], pD=ps[i][:, 128:256], pP2=ps[2][:, 0:257], pP1=ps[3][:, 0:257],
                       pCU=[ps[4 + i][:, 0:257], ps[6 + i][:, 0:257]],
                       B_pSD=Buf("pSD%d" % i), B_pP2=B_pP2s, B_pP1=B_pP1s, B_pCU=[Buf("pCU0_%d" % i), Buf("pCU1_%d" % i)]))
    B_C32 = [[Buf('C32_%d_%d' % (q, c)) for c in range(2)] for q in range(8)]
    B_C16 = [[Buf('C16_%d_%d' % (q, c)) for c in range(2)] for q in range(8)]
    steps = []
    fw = [(0, 0, None), (1, 128, None)] + [(2 + i, CTX + 128 * i, 128 * i) for i in range(16)]
    bw = [(0, 128, None), (1, 0, None)] + [(2 + i, CTX + 128 * (31 - i), (128 * (31 - i) if 31 - i <= 15 else None)) for i in range(32)]
    for i in range(34):
        if i < 18:
            steps.append((0,) + fw[i])
        steps.append((1,) + bw[i])
    mask16 = [arena.bf16(128), arena.bf16(128)]
    S.op("act", lambda e: e.copy(mask16[0], maskf), reads=[B_const], writes=[B_const])
    S.op("act", lambda e: e.copy(mask16[1], maskb), reads=[B_const], writes=[B_const])

    def emit_loads(si):
        (dr, sc, cb, ob) = steps[si]
        L_ = LD[dr * NLS + sc % NLS]
        ktok_t, vaug, kT_t, qT_t, grow_t = L_["ktok"], L_["vaug"], L_["kT"], L_["qT"], L_["grow"]
        B_ld, B_ldo = L_["B_ld"], L_["B_ldo"]
        S.dma("sp", (lambda e: e.dma_start(out=ktok_t, in_=KTOK[cb:cb + 128, :])), reads=[B_KTOK], writes=[B_ld])
        S.dma("sp", (lambda e: e.dma_start(out=vaug[:, :, 0:256], in_=VTOK[cb:cb + 128, :].rearrange("t (h e) -> t h e", h=4))),
              reads=[B_VTOK], writes=[B_ld])
        if ob is not None:
            S.dma("sp", (lambda e: e.dma_start(out=kT_t, in_=KT.rearrange("(k p) t -> p k t", p=128)[:, :, ob:ob + 128])), reads=[B_KT], writes=[B_ldo])
            S.dma("sp", (lambda e: e.dma_start(out=qT_t, in_=QT.rearrange("(k p) t -> p k t", p=128)[:, :, ob:ob + 128])), reads=[B_QT], writes=[B_ldo])
            S.dma("sp", (lambda e: e.dma_start(out=grow_t[0:36, :], in_=GROW[:, cb:cb + 128])), reads=[B_GROW], writes=[B_ldo])

    def emit_load_hs(si):
        (dr, sc, cb, ob) = steps[si]
        if ob is not None and dr == 1:
            H2 = HSB[dr * 2 + sc % 2]
            S.dma("sp", (lambda e: e.dma_start(out=H2["hs"], in_=HS[ob:ob + 128, :])), reads=[B_HS[ob // 128]], writes=[H2["B_hs"]])

    items = []
    for si, (dr, sc, cb, ob) in enumerate(steps):
        for h in range(4):
            items.append((si, h, len(items)))

    def ctx_of(it):
        si, h, n = it
        (dr, sc, cb, ob) = steps[si]
        L2 = dict(LD[dr * NLS + sc % NLS]); L2.update(HSB[dr * 2 + sc % 2])
        return dr, sc, cb, ob, h, dr * 4 + h, (sc if dr == 0 else 18 + sc), L2, HT[n % NHS]

    def emit_A(it):
        dr, sc, cb, ob, h, q, ci, L_, H_ = ctx_of(it)
        kT_t, qT_t, grow_t, ktok_t = L_["kT"], L_["qT"], L_["grow"], L_["ktok"]
        wT, STb, kw, pST, pD = H_["wT"], H_["STb"], H_["kw"], H_["pST"], H_["pD"]
        S.op("pool", (lambda e: e.tensor_scalar(kw, ktok_t[:, h * 256:(h + 1) * 256], wkcol[:, q, sc:sc + 1], 0.0625, ALU.mult, ALU.mult)),
             reads=[L_["B_ld"], B_cols], writes=[H_["B_kw"]])
        if ob is not None:
            def fsd(e):
                e.matmul(pST, kT_t[:, 2 * h, :], qT_t[:, 2 * h, :], start=True, stop=False)
                e.matmul(pST, kT_t[:, 2 * h + 1, :], qT_t[:, 2 * h + 1, :], start=False, stop=True)
                e.matmul(pD, negsel[0:36, q, :], grow_t[0:36, :], start=True, stop=False)
                return e.matmul(pD, ident, maskf if dr == 0 else maskb, start=False, stop=True)
            S.op("pe", fsd, reads=[L_["B_ldo"], B_const], writes=[H_["B_pSD"]])
            S.op("act", (lambda e: e.activation(wT, pD, AF.Exp, bias=Rcol[:, ci, h:h + 1])),
                 reads=[H_["B_pSD"], B_cols], writes=[H_["B_wT"]])
            S.op("dve", (lambda e: e.scalar_tensor_tensor(STb, pST, 0.0625, wT, ALU.mult, ALU.mult)),
                 reads=[H_["B_pSD"], H_["B_wT"]], writes=[H_["B_ST"]])

    pend_fin = []

    def emit_BC(it):
        while pend_fin:
            pend_fin.pop(0)()
        dr, sc, cb, ob, h, q, ci, L_, H_ = ctx_of(it)
        vaug, qT_t, hs_t = L_["vaug"], L_["qT"], L_["hs"]
        B_ld, B_ldo, B_hs = L_["B_ld"], L_["B_ldo"], L_["B_hs"]
        STb, P2sb, hn, dd, kw = H_["STb"], H_["P2sb"], H_["hn"], H_["dd"], H_["kw"]
        pP2, pP1, pCU = H_["pP2"], H_["pP1"], H_["pCU"]
        for c in range(2):
            mm(pCU[c], [(kw[:, c * 128:(c + 1) * 128], vaug[:, h, 0:257])], reads=[H_["B_kw"], B_ld], writes=[H_["B_pCU"][c]])
        if ob is not None:
            mm(pP1, [(qT_t[:, 2 * h + c, :], C16[:, q, c, 0:257]) for c in range(2)], reads=[B_ldo] + B_C16[q], writes=[H_["B_pP1"]])
        for c in range(2):
            S.op("dve", (lambda e, c=c: e.scalar_tensor_tensor(C32[:, q, c, :], C32[:, q, c, :], decay[:, q, sc:sc + 1], pCU[c], ALU.mult, ALU.add)),
                 reads=[H_["B_pCU"][c], B_cols], writes=[B_C32[q][c]])
            S.op("act", (lambda e, c=c: e.copy(C16[:, q, c, 0:257], C32[:, q, c, :])), reads=[B_C32[q][c]], writes=[B_C16[q][c]])
        if ob is not None:
            mm(pP2, [(STb, vaug[:, h, 0:257])], reads=[H_["B_ST"], B_ld], writes=[H_["B_pP2"]])
            S.op("act", (lambda e: e.copy(P2sb, pP2)), reads=[H_["B_pP2"]], writes=[H_["B_P2"]])
            S.op("dve", (lambda e: e.scalar_tensor_tensor(hn, pP1, acol[:, q, sc:sc + 1], P2sb, ALU.mult, ALU.add)),
                 reads=[H_["B_pP1"], H_["B_P2"], B_cols], writes=[H_["B_hn"]])
            S.op("dve", (lambda e: e.scalar_tensor_tensor(dd[:, 0:1], hn[:, 256:257], -1.0, hn[:, 256:257], ALU.mult, ALU.max)),
                 reads=[H_["B_hn"]], writes=[H_["B_dd"]])
            S.op("dve", (lambda e: e.tensor_tensor(dd[:, 0:1], dd[:, 0:1], Ecol[:, ci, h:h + 1], ALU.max)),
                 reads=[H_["B_dd"], B_cols], writes=[H_["B_dd"]])
            S.op("dve", (lambda e: e.reciprocal(dd[:, 1:2], dd[:, 0:1])), reads=[H_["B_dd"]], writes=[H_["B_dd"]])
            def fin():
                if dr == 0:
                    S.op("act", (lambda e: e.activation(hs_t[:, h * 256:(h + 1) * 256], hn[:, 0:256], AF.Identity, scale=dd[:, 1:2])),
                         reads=[H_["B_hn"], H_["B_dd"]], writes=[B_hs])
                else:
                    S.op("dve", (lambda e: e.scalar_tensor_tensor(hs_t[:, h * 256:(h + 1) * 256], hn[:, 0:256], dd[:, 1:2],
                                                                  hs_t[:, h * 256:(h + 1) * 256], ALU.mult, ALU.add)),
                         reads=[H_["B_hn"], H_["B_dd"]], writes=[B_hs])
                if h == 3:
                    S.dma("sp", (lambda e: e.dma_start(out=HS[ob:ob + 128, :], in_=hs_t)), reads=[B_hs], writes=[B_HS[ob // 128]])
            if dr == 0:
                pend_fin.append(fin)
            else:
                fin()

    emit_loads(0); emit_loads(1)
    for n in range(len(items) + 1):
        if n < len(items):
            si, h, _ = items[n]
            if h == 0:
                emit_load_hs(si)
            if h == 1 and si + 2 < len(steps):
                emit_loads(si + 2)
            emit_A(items[n])
        if n >= 1:
            emit_BC(items[n - 1])
    while pend_fin:
        pend_fin.pop(0)()
    S.barrier()
    if g_["stage"] < 4:
        return
    build_rest3(g_, locals())


def build_rest(env):
    nc = env["nc"]
    S = env["S"]; ps = env["ps"]; Bps = env["Bps"]; arena = env["arena"]
    u2 = env["u2"]; B_u2 = env["B_u2"]; vcol = env["vcol"]; DER = env["DER"]
    ident = env["ident"]; maskf = env["maskf"]; maskb = env["maskb"]; CS = env["CS"]; M2 = env["M2"]; M3 = env["M3"]
    id16 = env["id16_t"]; ones16 = env["ones16_t"]
    B_const = env["B_const"]; B_der = env["B_der"]
    stage = env["stage"]; dbg = env["dbg"]; dbg_tensor = env["dbg_tensor"]
    dscr = env["dscr"]
    H1, AB, PQ, KT, QT, KTOK, VTOK = [env[n] for n in "H1 AB PQ KT QT KTOK VTOK".split()]
    B_H1, B_AB, B_PQ, B_KT, B_QT, B_KTOK, B_VTOK = [env["B_" + n] for n in "H1 AB PQ KT QT KTOK VTOK".split()]
    GROW = dscr("GROW", [36, NT], F32)
    B_GROW = Buf("GROW")
    HS = dscr("HS", [OWN, D], F32)
    B_HS = [Buf("HS%d" % i) for i in range(16)]

    def mm(out, pairs, reads, writes):
        def fn(e):
            n = len(pairs)
            for i, (l, r) in enumerate(pairs):
                ins = e.matmul(out, l, r, start=(i == 0), stop=(i == n - 1))
            return ins
        return S.op("pe", fn, reads=reads, writes=writes)

    NCH = 52
    Rcol = arena.f32(NCH * 4).rearrange("p (c h) -> p c h", h=4)
    Gcol = arena.f32(NCH * 4).rearrange("p (c h) -> p c h", h=4)
    Ecol = arena.f32(NCH * 4).rearrange("p (c h) -> p c h", h=4)
    gend = arena.f32(8 * 35).rearrange("p (q c) -> p q c", q=8)
    acol = arena.f32(8 * 34).rearrange("p (q c) -> p q c", q=8)
    wkcol = arena.f32(8 * 34).rearrange("p (q c) -> p q c", q=8)
    decay = arena.f32(8 * 34).rearrange("p (q c) -> p q c", q=8)
    B_cols = Buf("cols")
    yfT = arena.bf16(4 * OWN).rearrange("p (g t) -> p g t", g=4)
    B_yfT = Buf("yfT")
    C32 = arena.f32(8 * 2 * 257).rearrange("p (q c e) -> p q c e", q=8, c=2)
    C16 = arena.bf16(8 * 2 * 258).rearrange("p (q c e) -> p q c e", q=8, c=2)
    B_C = [Buf("C%d" % i) for i in range(8)]
    sel_t = arena.f32(2048)
    S.dma("sp", lambda e: e.dma_start(out=sel_t[0:36, :], in_=env["selc"]), writes=[B_const])
    sel = sel_t[0:36, 0:1024].rearrange("r (p m) -> r p m", p=8)
    negsel = sel_t[0:36, 1024:2048].rearrange("r (p m) -> r p m", p=8)
    mB = arena.mark()

    aLI = arena.f32(NT)
    aLF = arena.f32(NT)
    aB = arena.f32(NT)
    aG = arena.f32(NT)
    ones_r = arena.f32(512)
    tmpE = arena.f32(512)
    wgf_s = arena.bf16(8 * 8).rearrange("p (k n) -> p k n", k=8)
    wgb_s = arena.bf16(8 * 72).rearrange("p (k n) -> p k n", k=8)
    bgs = arena.f32(4)
    B_rows = Buf("rows")
    B_wg = Buf("wg")
    B_tmpE = Buf("tmpE")
    for a in (aLI, aLF, aB, aG):
        S.op("pool", (lambda e, a=a: e.memset(a, 0.0)), writes=[B_rows])
    S.op("pool", lambda e: e.memset(ones_r, 1.0), writes=[B_wg])
    S.op("pool", lambda e: e.memset(gend[:, :, :], 0.0), writes=[B_cols])
    for q in range(8):
        S.op("pool", (lambda e, q=q: e.memset(C32[:, q, :, :], 0.0)), writes=[B_C[q]])
        S.op("pool", (lambda e, q=q: e.memset(C16[:, q, :, :], 0.0)), writes=[B_C[q]])
    with nc.allow_non_contiguous_dma(reason="tiny gate weights"):
        pass
    wgate_f = env["wgate_f"]; wgate_b = env["wgate_b"]; bg = env["bg"]
    S.dma("pool", lambda e: e.dma_start(out=wgf_s, in_=wgate_f.rearrange("(k p) n -> p k n", p=128)), writes=[B_wg])
    S.dma("pool", lambda e: e.dma_start(out=wgb_s, in_=wgate_b.rearrange("(k p) n -> p k n", p=128)), writes=[B_wg])
    S.dma("sp", lambda e: e.dma_start(out=bgs[0:36, 0:2], in_=bg), writes=[B_wg])
    S.op("dve", lambda e: e.tensor_scalar(bgs[0:36, 2:3], bgs[0:36, 1:2], -1.0, None, ALU.mult), reads=[B_wg], writes=[B_wg])

    def gate_tile(jc0, W, rhs_fn, r0, r1, wl, wl_lf, pbank, rev=False):
        def pv(bank):
            return ps[bank][r0:r1, W - 1::-1] if rev else ps[bank][r0:r1, 0:W]
        mm(ps[pbank][0:r1, 0:W], [(wl(k), rhs_fn(k)) for k in range(8)], reads=[B_wg] + B_u2, writes=[Bps[pbank]])
        S.op("act", lambda e: e.activation(aLI[r0:r1, jc0:jc0 + W], pv(pbank), AF.Identity, bias=bgs[r0:r1, 0:1]),
             reads=[Bps[pbank], B_wg], writes=[B_rows])
        mm(ps[pbank + 1][0:r1, 0:W], [(wl_lf(k), rhs_fn(k)) for k in range(8)], reads=[B_wg] + B_u2, writes=[Bps[pbank + 1]])
        S.op("act", lambda e: e.activation(tmpE[r0:r1, 0:W], pv(pbank + 1), AF.Exp, bias=bgs[r0:r1, 2:3], scale=-1.0),
             reads=[Bps[pbank + 1], B_wg], writes=[B_tmpE])
        S.op("act", lambda e: e.activation(tmpE[r0:r1, 0:W], tmpE[r0:r1, 0:W], AF.Ln, bias=1.0), reads=[B_tmpE], writes=[B_tmpE])
        S.op("dve", lambda e: e.tensor_scalar(aLF[r0:r1, jc0:jc0 + W], tmpE[r0:r1, 0:W], -1.0, None, ALU.mult),
             reads=[B_tmpE], writes=[B_rows])

    ti = 0
    for (jc0, W) in [(0, 256)] + [(256 + 512 * i, 512) for i in range(4)]:
        gate_tile(jc0, W, (lambda k, jc0=jc0, W=W: u2[:, k, jc0:jc0 + W]), 0, 4,
                  (lambda k: wgf_s[:, k, 0:4]), (lambda k: wgf_s[:, k, 4:8]), 2 * (ti % 2))
        ti += 1
    def rev_u2(k, hi, W):
        return u2[:, k, hi - W + 1:hi + 1]
    gate_tile(0, 256, (lambda k: rev_u2(k, 255, 256)), 32, 36,
              (lambda k: wgb_s[:, k, 0:36]), (lambda k: wgb_s[:, k, 36:72]), 2 * (ti % 2), rev=True)
    ti += 1
    for i in range(8):
        jc0 = 256 + 512 * i
        hi = 4607 - jc0
        gate_tile(jc0, 512, (lambda k, hi=hi: rev_u2(k, hi, 512)), 32, 36,
                  (lambda k: wgb_s[:, k, 0:36]), (lambda k: wgb_s[:, k, 36:72]), 2 * (ti % 2), rev=True)
        ti += 1
    pieces = [(0, 256)] + [(256 + 512 * i, 512) for i in range(8)]
    for pi, (c0, W) in enumerate(pieces):
        init = 0.0 if pi == 0 else aB[0:36, c0 - 1:c0]
        S.op("dve", (lambda e, c0=c0, W=W, init=init: e.tensor_tensor_scan(aB[0:36, c0:c0 + W], ones_r[0:36, 0:W], aLF[0:36, c0:c0 + W],
                                                                           init, ALU.mult, ALU.add)),
             reads=[B_rows, B_wg], writes=[B_rows])
    S.op("dve", lambda e: e.tensor_tensor(aLI[0:36, :], aLI[0:36, :], aB[0:36, :], ALU.subtract), reads=[B_rows], writes=[B_rows])
    for pi, (c0, W) in enumerate(pieces):
        init = 0.0 if pi == 0 else aG[0:36, c0 - 1:c0]
        S.op("dve", (lambda e, c0=c0, W=W, init=init: e.tensor_tensor_scan(aG[0:36, c0:c0 + W], ones_r[0:36, 0:W], aLI[0:36, c0:c0 + W],
                                                                           init, ALU.mult, ALU.max)),
             reads=[B_rows, B_wg], writes=[B_rows])
    S.op("dve", lambda e: e.tensor_tensor(aB[0:36, :], aB[0:36, :], aG[0:36, :], ALU.add), reads=[B_rows], writes=[B_rows])
    if dbg:
        d_rows = dbg_tensor("d_rows", [3, 36, NT])
        for i, a in enumerate((aLI, aG, aB)):
            S.dma("sp", (lambda e, i=i, a=a: e.dma_start(out=d_rows[i], in_=a[0:36, :])), reads=[B_rows])
    def n0_of(sc):
        return 128 if sc == 0 else (0 if sc == 1 else 4480 - 128 * sc)
    for ai, (arr, bank) in enumerate(((aLI, 0), (aB, 2), (aG, 1))):
        S.op("dve", (lambda e, arr=arr: e.tensor_copy(aLF[32:36, 0:256], arr[32:36, 255::-1])), reads=[B_rows], writes=[B_rows])
        S.op("dve", (lambda e, arr=arr: e.tensor_copy(aLF[32:36, 256:NT], arr[32:36, NT - 1:255:-1])), reads=[B_rows], writes=[B_rows])
        def fn(e, arr=arr, bank=bank):
            for sc in range(18):
                ins = e.matmul(ps[bank][:, sc * 4:(sc + 1) * 4], arr[0:4, sc * 128:(sc + 1) * 128], ident[0:4, 0:4],
                               start=True, stop=True)
            for sc in range(34):
                n0 = n0_of(sc)
                ins = e.matmul(ps[bank][:, (18 + sc) * 4:(19 + sc) * 4], aLF[32:36, n0:n0 + 128],
                               ident[32:36, 32:36], start=True, stop=True)
            return ins
        S.op("pe", fn, reads=[B_rows, B_const], writes=[Bps[bank]])
    S.op("dve", lambda e: e.tensor_copy(aLF[0:4, :], aG[0:4, :]), reads=[B_rows], writes=[B_rows])
    S.dma("sp", lambda e: e.dma_start(out=GROW, in_=aLF[0:36, :]), reads=[B_rows], writes=[B_GROW])
    S.op("act", lambda e: e.copy(Rcol[:, :, :], ps[0][:, 0:NCH * 4].rearrange("p (c h) -> p c h", h=4)), reads=[Bps[0]], writes=[B_cols])
    S.op("act", lambda e: e.copy(Gcol[:, :, :], ps[1][:, 0:NCH * 4].rearrange("p (c h) -> p c h", h=4)), reads=[Bps[1]], writes=[B_cols])
    S.op("act", lambda e: e.activation(Ecol[:, :, :], ps[2][:, 0:NCH * 4].rearrange("p (c h) -> p c h", h=4), AF.Exp, scale=-1.0),
         reads=[Bps[2]], writes=[B_cols])
    def fn(e):
        for q in range(8):
            n = 18 if q < 4 else 34
            ins = e.matmul(ps[3][:, q * 34:q * 34 + n], sel[0:36, q, :], aG[0:36, 127:127 + 128 * (n - 1) + 1:128], start=True, stop=True)
        return ins
    S.op("pe", fn, reads=[B_rows, B_const], writes=[Bps[3]])
    for q in range(8):
        n = 18 if q < 4 else 34
        S.op("act", (lambda e, q=q, n=n: e.copy(gend[:, q, 1:1 + n], ps[3][:, q * 34:q * 34 + n])), reads=[Bps[3]], writes=[B_cols])
    tq = arena.f32(34)
    B_tq = Buf("tq")
    for q in range(8):
        n = 18 if q < 4 else 34
        base = 0 if q < 4 else 18
        h = q % 4
        S.op("dve", (lambda e, q=q, n=n, base=base, h=h: e.tensor_tensor(tq[:, 0:n], gend[:, q, 0:n], Gcol[:, base:base + n, h], ALU.subtract)),
             reads=[B_cols], writes=[B_tq])
        S.op("act", (lambda e, q=q, n=n: e.activation(acol[:, q, 0:n], tq[:, 0:n], AF.Exp)), reads=[B_tq], writes=[B_cols])
        S.op("dve", (lambda e, q=q, n=n, base=base, h=h: e.tensor_tensor(tq[:, 0:n], Rcol[:, base:base + n, h], gend[:, q, 1:1 + n], ALU.subtract)),
             reads=[B_cols], writes=[B_tq])
        S.op("act", (lambda e, q=q, n=n: e.activation(wkcol[:, q, 0:n], tq[:, 0:n], AF.Exp)), reads=[B_tq], writes=[B_cols])
        S.op("dve", (lambda e, q=q, n=n: e.tensor_tensor(tq[:, 0:n], gend[:, q, 0:n], gend[:, q, 1:1 + n], ALU.subtract)),
             reads=[B_cols], writes=[B_tq])
        S.op("act", (lambda e, q=q, n=n: e.activation(decay[:, q, 0:n], tq[:, 0:n], AF.Exp)), reads=[B_tq], writes=[B_cols])
    S.barrier()
    arena.reset(mB)
    if stage < 3:
        return
    build_rest2(env, locals())


COL_F = 0
COL_Q = 512
COL_K = 1536
COL_V = 2560
COL_O = 3584
COL_GATES = 4608
COL_BR = 4624


def _dft_consts(flip):
    idx = (63 - np.arange(64)) if flip else np.arange(64)
    ang = 2 * np.pi * np.outer(idx, idx) / 64.0
    Cc = np.cos(ang) / 8.0
    Sc = np.sin(ang) / 8.0
    ch = np.arange(128)
    angc = 2 * np.pi * np.outer(ch, ch) / 128.0
    CS = np.concatenate([np.cos(angc), np.sin(angc)], axis=1) / np.sqrt(128.0)
    M2 = np.zeros((128, 128))
    M2[0:64, 0:64] = Cc
    M2[64:128, 0:64] = -Sc
    M2[0:64, 64:128] = Sc
    M2[64:128, 64:128] = Cc
    M3 = np.zeros((128, 32))
    M3[0:64, :] = Cc[:, 0:32]
    M3[64:128, :] = -Sc[:, 0:32]
    return CS, M2, M3


def make_inputs(inp):
    f32 = np.float32
    x = np.asarray(inp["x"], f32)
    ctx = np.asarray(inp["ctx"], f32)
    c = np.asarray(inp["c"], f32)
    c_ctx = np.asarray(inp["c_ctx"], f32)
    w_in = np.asarray(inp["w_in"], f32)[0]
    b_in = np.asarray(inp["b_in"], f32)[0]
    conv_w = np.asarray(inp["conv_w"], f32)[0]
    conv_b = np.asarray(inp["conv_b"], f32)[0]
    norm_g = np.asarray(inp["norm_g"], f32)[0]

    def fm(v):
        return np.ascontiguousarray(v.reshape(-1, 128).T)

    def r13(w):
        return np.ascontiguousarray(w.reshape(8, 128, 2, NJ, 128).transpose(3, 1, 0, 2, 4).reshape(NJ, 128, 2048))

    def r2(w):
        return np.ascontiguousarray(w.reshape(NJ, 128, 8, 128).transpose(2, 1, 0, 3).reshape(8, 128, FF))

    shared = {
        "w_ada": np.ascontiguousarray(np.asarray(inp["w_ada"], f32)[0]),
        "w13a": r13(np.asarray(inp["w13_a"], f32)[0]), "w2a": r2(np.asarray(inp["w2_a"], f32)[0]),
        "w13b": r13(np.asarray(inp["w13_b"], f32)[0]), "w2b": r2(np.asarray(inp["w2_b"], f32)[0]),
        "wF": np.ascontiguousarray(w_in[:, COL_F:COL_Q]), "wq": np.ascontiguousarray(w_in[:, COL_Q:COL_K]),
        "wk": np.ascontiguousarray(w_in[:, COL_K:COL_V]), "wv": np.ascontiguousarray(w_in[:, COL_V:COL_O]),
        "wo": np.ascontiguousarray(w_in[:, COL_O:COL_GATES]),
        "wgf": np.ascontiguousarray(w_in[:, COL_BR:COL_BR + D]), "wgm": np.ascontiguousarray(w_in[:, COL_BR + D:]),
        "w_four": np.ascontiguousarray(np.asarray(inp["w_four"], f32)[0]),
        "w_mproj": np.ascontiguousarray(np.asarray(inp["w_mproj"], f32)[0]),
        "w_out": np.ascontiguousarray(np.asarray(inp["w_out"], f32)[0]),
        "rowv": np.concatenate([b_in[COL_V:COL_O], b_in[COL_O:COL_GATES], np.asarray(inp["head_g"], f32)[0]])[None, :].copy(),
    }
    sel = np.zeros((36, 8, 128), f32)
    for p in range(8):
        row = (p % 4) + (32 if p >= 4 else 0)
        sel[row, p, :] = 1.0
    selc = np.concatenate([sel.reshape(36, -1), -sel.reshape(36, -1)], axis=1)
    s_idx = np.arange(128)[:, None]
    t_idx = np.arange(128)[None, :]
    maskf = np.where(s_idx <= t_idx, 0.0, NEG).astype(f32)
    maskb = np.where(s_idx >= t_idx, 0.0, NEG).astype(f32)
    maps = []
    for core in range(8):
        b, half = core // 2, core % 2
        flip = half == 1
        xb = x[b][::-1] if flip else x[b]
        cb_ = ctx[b][::-1] if flip else ctx[b]
        g = COL_GATES
        if flip:
            gi_f, gf_f, gi_b, gf_b = g + 8, g + 12, g + 0, g + 4
            cw = conv_w[::-1]
        else:
            gi_f, gf_f, gi_b, gf_b = g + 0, g + 4, g + 8, g + 12
            cw = conv_w
        wgate_f = np.concatenate([w_in[:, gi_f:gi_f + 4], w_in[:, gf_f:gf_f + 4]], axis=1)
        wgate_b = np.zeros((D, 72), f32)
        wgate_b[:, 32:36] = w_in[:, gi_b:gi_b + 4]
        wgate_b[:, 36 + 32:36 + 36] = w_in[:, gf_b:gf_b + 4]
        bgv = np.zeros((36, 2), f32)
        bgv[0:4, 0] = b_in[gi_f:gi_f + 4]
        bgv[0:4, 1] = b_in[gf_f:gf_f + 4]
        bgv[32:36, 0] = b_in[gi_b:gi_b + 4]
        bgv[32:36, 1] = b_in[gf_b:gf_b + 4]
        vecs = np.zeros((128, NV), f32)

        def put(name, arr):
            o, w = VEC[name]
            assert arr.shape == (128, w), (name, arr.shape)
            vecs[:, o:o + w] = arr
        put("bada", fm(np.asarray(inp["b_ada"], f32)[0]))
        put("ng", np.concatenate([fm(norm_g[i]) for i in range(6)], axis=1))
        put("bF", fm(b_in[COL_F:COL_Q])); put("bq", fm(b_in[COL_Q:COL_K])); put("bk", fm(b_in[COL_K:COL_V]))
        put("cwq", np.concatenate([fm(cw[t, 0:D]) for t in range(3)], axis=1))
        put("cwk", np.concatenate([fm(cw[t, D:2 * D]) for t in range(3)], axis=1))
        put("cbq", fm(conv_b[0:D])); put("cbk", fm(conv_b[D:2 * D]))
        put("bgf", fm(b_in[COL_BR:COL_BR + D])); put("bgm", fm(b_in[COL_BR + D:]))
        cvec = np.zeros((128, 8, 2), f32)
        cvec[:, :, 0] = fm(c[b])
        cvec[:, :, 1] = fm(c_ctx)
        CS, M2, M3 = _dft_consts(flip)
        cst = np.zeros((128, 128 * 4 + 256 + 128 + 32), f32)
        cst[:, 0:128] = np.eye(128)
        cst[:, 128:256] = maskf
        cst[:, 256:384] = maskb
        cst[:, 512:768] = CS
        cst[:, 768:896] = M2
        cst[:, 896:928] = M3
        m = dict(shared)
        m.update({
            "xT": np.ascontiguousarray(xb.T), "ctxT": np.ascontiguousarray(cb_.T),
            "cvec": cvec.reshape(128, 16), "vecs": vecs, "bg": bgv,
            "wgate_f": np.ascontiguousarray(wgate_f), "wgate_b": wgate_b, "cst": cst, "selc": selc,
        })
        maps.append(m)
    return maps


def kernel(**inputs):
    nc, _ = build()
    maps = make_inputs(inputs)
    res = run_bass_kernel_spmd(nc, maps, core_ids=list(range(8)))
    out = np.zeros((4, SEQ, D), np.float32)
    for core in range(8):
        b, half = core // 2, core % 2
        o = np.asarray(res.results[core]["outT"]).T
        if half == 0:
            out[b, 0:OWN] = o
        else:
            out[b, OWN:] = o[::-1]
    return out
```

```python
import numpy as np
import os as _os
from contextlib import ExitStack
import concourse.bass as bass
import concourse.mybir as mybir
from concourse.bass_utils import run_bass_kernel_spmd

F32 = mybir.dt.float32
BF16 = mybir.dt.bfloat16
AF = mybir.ActivationFunctionType
ALU = mybir.AluOpType

ENGS = ("pe", "act", "dve", "pool", "sp")
EPOCH = 30000

D = 1024
SEQ = 4096
CTX = 256
NT = CTX + SEQ
OWN = 2048
FF = 2816
NJ = 22
EPS = 1e-6
NEG = -30000.0


class Tick:
    __slots__ = ("sem", "val", "know")

    def __init__(self, sem, val, know):
        self.sem = sem
        self.val = val
        self.know = know


class Buf:
    __slots__ = ("name", "w", "r")

    def __init__(self, name=""):
        self.name = name
        self.w = None
        self.r = {}


class Sched:
    def __init__(self, nc, stack, n_dma_sems=48):
        self.nc = nc
        self.stack = stack
        self.q = {e: [] for e in ENGS}
        self.cnt = {e: 0 for e in ENGS}
        self.esems = {e: [] for e in ENGS}
        self.known = {e: {} for e in ENGS}
        self.dsems = [stack.enter_context(nc.semaphore(f"dma{i}")) for i in range(n_dma_sems)]
        self.dcnt = [0] * n_dma_sems
        self.dlast = [None] * n_dma_sems
        self.drr = 0
        self.drr2 = {}

    def _esem(self, eng, idx):
        lst = self.esems[eng]
        while len(lst) <= idx:
            lst.append(self.stack.enter_context(self.nc.semaphore(f"e_{eng}_{len(lst)}")))
        return lst[idx]

    def _collect(self, eng, reads, writes, extra=()):
        kn = self.known[eng]
        waits = {}

        def need(t):
            if t is None:
                return
            if kn.get(t.sem, 0) >= t.val:
                return
            if waits.get(t.sem, (0, None))[0] < t.val:
                waits[t.sem] = (t.val, t)

        for b in reads:
            need(b.w)
        for b in writes:
            need(b.w)
            for t in b.r.values():
                need(t)
        for t in extra:
            need(t)
        items = sorted(waits.items(), key=lambda kv: -kv[1][0])
        final = []
        for sem, (val, t) in items:
            if kn.get(sem, 0) >= val:
                continue
            final.append((sem, val))
            kn[sem] = val
            for s2, v2 in t.know.items():
                if kn.get(s2, 0) < v2:
                    kn[s2] = v2
        return final

    def op(self, eng, fn, reads=(), writes=(), extra=()):
        waits = self._collect(eng, reads, writes, extra)
        c = self.cnt[eng]
        sem = self._esem(eng, c // EPOCH)
        val = c % EPOCH + 1
        self.cnt[eng] = c + 1
        t = Tick(sem, val, dict(self.known[eng]))
        for b in reads:
            b.r[sem] = t
        for b in writes:
            b.w = t
            b.r = {}
        self.q[eng].append((waits, fn, sem, 1))
        return t

    def dma(self, eng, fn, reads=(), writes=(), extra=()):
        n = len(self.dsems)
        lo, hi = (0, n // 3) if eng == "pool" else (n // 3, n)
        rr = self.drr2.get(eng, lo)
        i = rr
        self.drr2[eng] = lo + (rr + 1 - lo) % (hi - lo)
        ex = list(extra)
        if self.dlast[i] is not None:
            ex.append(self.dlast[i])
        waits = self._collect(eng, reads, writes, ex)
        self.dcnt[i] += 1
        sem = self.dsems[i]
        t = Tick(sem, 16 * self.dcnt[i], dict(self.known[eng]))
        self.dlast[i] = t
        for b in reads:
            b.r[sem] = t
        for b in writes:
            b.w = t
            b.r = {}
        self.q[eng].append((waits, fn, sem, 16))
        return t

    def wait_all(self, eng, ticks):
        waits = self._collect(eng, (), (), ticks)
        self.q[eng].append((waits, None, None, 0))

    def barrier(self, skip=()):
        skipset = {(t.sem, t.val) for t in skip}
        ticks = []
        for e in ENGS:
            c = self.cnt[e]
            if c > 0:
                ticks.append(Tick(self._esem(e, (c - 1) // EPOCH), (c - 1) % EPOCH + 1, {}))
        for t in self.dlast:
            if t is not None and (t.sem, t.val) not in skipset:
                ticks.append(t)
        for e in ENGS:
            self.wait_all(e, ticks)

    def emit(self):
        nc = self.nc
        q = self.q

        def run(engobj, lst):
            for waits, fn, sem, amt in lst:
                for s, v in waits:
                    engobj.wait_ge(s, v)
                if fn is not None:
                    ins = fn(engobj)
                    ins.then_inc(sem, amt)

        with nc.Block() as block:
            @block.tensor
            def _(e):
                run(e, q["pe"])

            @block.scalar
            def _(e):
                run(e, q["act"])

            @block.vector
            def _(e):
                run(e, q["dve"])

            @block.gpsimd
            def _(e):
                run(e, q["pool"])

            @block.sync
            def _(e):
                run(e, q["sp"])


class Arena:
    def __init__(self, nc, name, words):
        self.t = nc.alloc_sbuf_tensor(name, [128, words], F32)
        self.words = words
        self.off = 0

    def mark(self):
        return self.off

    def reset(self, m):
        self.off = m

    def f32(self, n):
        a = self.t[:, self.off:self.off + n]
        self.off += n
        assert self.off <= self.words, ("arena overflow", self.off, self.words)
        return a

    def bf16(self, n):
        w = (n + 1) // 2
        a = self.t[:, self.off:self.off + w].bitcast(BF16)
        self.off += w
        assert self.off <= self.words, ("arena overflow", self.off, self.words)
        return a[:, 0:n]


VEC = {}
_o = 0
for _n, _w in [("bada", 72), ("ng", 48), ("bF", 4), ("bq", 8), ("bk", 8), ("cwq", 24), ("cwk", 24),
               ("cbq", 8), ("cbk", 8), ("bgf", 8), ("bgm", 8)]:
    VEC[_n] = (_o, _w)
    _o += _w
NV = _o


def build(stage=99, dbg=False):
    nc = bass.Bass("TRN2", target_bir_lowering=False)
    dt_in = lambda name, shape, dt=F32: nc.dram_tensor(name, shape, dt, kind="ExternalInput").ap()
    xT = dt_in("xT", [D, SEQ])
    ctxT = dt_in("ctxT", [D, CTX])
    cvec = dt_in("cvec", [128, 16])
    w_ada = dt_in("w_ada", [D, 9 * D])
    vecs = dt_in("vecs", [128, NV])
    rowv = dt_in("rowv", [1, 3 * D])
    bg = dt_in("bg", [36, 2])
    w13a = dt_in("w13a", [NJ, 128, 2048])
    w2a = dt_in("w2a", [8, 128, FF])
    w13b = dt_in("w13b", [NJ, 128, 2048])
    w2b = dt_in("w2b", [8, 128, FF])
    wF = dt_in("wF", [D, 512])
    wq = dt_in("wq", [D, D])
    wk = dt_in("wk", [D, D])
    wv = dt_in("wv", [D, D])
    wo = dt_in("wo", [D, D])
    wgf = dt_in("wgf", [D, D])
    wgm = dt_in("wgm", [D, D])
    wgate_f = dt_in("wgate_f", [D, 8])
    wgate_b = dt_in("wgate_b", [D, 72])
    w_four = dt_in("w_four", [512, D])
    w_mproj = dt_in("w_mproj", [D, D])
    w_out = dt_in("w_out", [D, D])
    cst = dt_in("cst", [128, 128 * 4 + 256 + 128 + 32])
    selc = dt_in("selc", [36, 2 * 8 * 128])
    outT = nc.dram_tensor("outT", [D, OWN], F32, kind="ExternalOutput").ap()
    dbg_out = {}

    def dbg_tensor(name, shape, dt=F32):
        dbg_out[name] = nc.dram_tensor(name, shape, dt, kind="ExternalOutput").ap()
        return dbg_out[name]

    dscr = lambda name, shape, dt: nc.dram_tensor(name, shape, dt, kind="Internal").ap()
    H1 = dscr("H1", [D, OWN], F32)
    AB = dscr("AB", [2, SEQ, 512], F32)
    PQ = dscr("PQ", [2, 64, 64, 512], F32)
    KT = dscr("KT", [D, OWN], BF16)
    QT = dscr("QT", [D, OWN], BF16)
    KTOK = dscr("KTOK", [NT, D], BF16)
    VTOK = dscr("VTOK", [NT, D], BF16)
    B_H1, B_AB, B_PQ, B_KT, B_QT, B_KTOK, B_VTOK = [Buf(n) for n in "H1 AB PQ KT QT KTOK VTOK".split()]

    st = ExitStack()
    with st:
        S = Sched(nc, st)
        ps = [st.enter_context(nc.psum_tensor(f"ps{i}", [128, 512], F32)) for i in range(8)]
        Bps = [Buf(f"ps{i}") for i in range(8)]

        cs_t = nc.alloc_sbuf_tensor("cs", [128, 128 * 4 + 256 + 128 + 32], F32)
        ident = cs_t[:, 0:128]
        maskf = cs_t[:, 128:256]
        maskb = cs_t[:, 256:384]
        CS = cs_t[:, 512:768]
        M2 = cs_t[:, 768:896]
        M3 = cs_t[:, 896:928]
        vec_t = nc.alloc_sbuf_tensor("vec", [128, NV], F32)
        mod_t = nc.alloc_sbuf_tensor("mod", [128, 72 * 2], F32)
        der_t = nc.alloc_sbuf_tensor("der", [128, 16 * 8], F32)
        id16_t = nc.alloc_sbuf_tensor("id16", [128, 128], BF16)
        ones16_t = nc.alloc_sbuf_tensor("ones16", [128, 128], BF16)
        u2_t = nc.alloc_sbuf_tensor("u2", [128, 8 * NT], BF16)
        u2 = u2_t[:, :].rearrange("p (k t) -> p k t", k=8)
        B_const = Buf("const")
        B_mod = Buf("mod")
        B_der = Buf("der")
        tiles = [(0, CTX, 1)] + [(CTX + 512 * i, 512, 0) for i in range(8)]
        B_u2 = [Buf(f"u2_{i}") for i in range(9)]

        def vcol(name, i=0, n=1):
            o, w = VEC[name]
            return vec_t[:, o + i:o + i + n]

        DER = {}
        _d = 0
        for nm in ["A0l", "A0c", "S0l", "S0c", "PAl", "PAc", "A2l", "A2c", "S2l", "S2c", "PMl", "A4l", "S4l", "PBl"]:
            DER[nm] = der_t[:, _d * 8:(_d + 1) * 8]
            _d += 1

        arena = Arena(nc, "arena", 34000)

        S.dma("sp", lambda e: e.dma_start(out=cs_t[:, :], in_=cst), writes=[B_const])
        S.dma("sp", lambda e: e.dma_start(out=vec_t[:, :], in_=vecs), writes=[B_const])
        S.op("act", lambda e: e.copy(id16_t[:, :], ident), reads=[B_const], writes=[B_const])
        S.op("pool", lambda e: e.memset(ones16_t[:, :], 1.0), writes=[B_const])

        m0 = arena.mark()
        cv = arena.f32(16)
        scv = arena.f32(16)
        B_cv = Buf("cv")
        S.dma("sp", lambda e: e.dma_start(out=cv, in_=cvec), writes=[B_cv])
        S.op("act", lambda e: e.activation(scv, cv, AF.Silu), reads=[B_cv], writes=[B_cv])
        wad = [arena.f32(8 * 1024) for _ in range(2)]
        B_wad = [Buf("wad0"), Buf("wad1")]
        w_ada_v = w_ada.rearrange("(k p) n -> p k n", p=128)
        modps = ps[7][:, 0:144]
        for mi in range(9):
            sl = mi % 2
            wv_ = wad[sl].rearrange("p (k n) -> p k n", k=8)
            S.dma("sp", (lambda e, wv_=wv_, mi=mi: e.dma_start(out=wv_, in_=w_ada_v[:, :, mi * 1024:(mi + 1) * 1024])),
                  writes=[B_wad[sl]])
            for dc in range(8):
                def fn(e, wv_=wv_, mi=mi, dc=dc):
                    for k in range(8):
                        ins = e.matmul(modps[:, (mi * 8 + dc) * 2:(mi * 8 + dc) * 2 + 2],
                                       wv_[:, k, dc * 128:(dc + 1) * 128],
                                       scv[:, k * 2:k * 2 + 2], start=(k == 0), stop=(k == 7))
                    return ins
                S.op("pe", fn, reads=[B_wad[sl], B_cv], writes=[Bps[7]])
        modv = mod_t[:, :].rearrange("p (m j) -> p m j", j=2)
        modpsv = modps.rearrange("p (m j) -> p m j", j=2)
        bada = vcol("bada", 0, 72)
        for j in range(2):
            S.op("dve", (lambda e, j=j: e.tensor_tensor(modv[:, :, j], modpsv[:, :, j], bada, ALU.add)),
                 reads=[Bps[7], B_const], writes=[B_mod])

        def modc(mi, j):
            return modv[:, mi * 8:(mi + 1) * 8, j]

        def ng(i):
            return vcol("ng", i * 8, 8)

        def der_scale(name, mi, gi, j):
            S.op("dve", lambda e: e.scalar_tensor_tensor(DER[name], modc(mi, j), 1.0, ng(gi), ALU.add, ALU.mult),
                 reads=[B_mod, B_const], writes=[B_der])

        def der_gate(name, mi, gi, j, f):
            S.op("dve", lambda e: e.scalar_tensor_tensor(DER[name], modc(mi, j), f, ng(gi), ALU.mult, ALU.mult),
                 reads=[B_mod, B_const], writes=[B_der])

        def der_copy(name, mi, j):
            S.op("dve", lambda e: e.tensor_copy(DER[name], modc(mi, j)), reads=[B_mod], writes=[B_der])

        der_scale("A0l", 1, 0, 0); der_scale("A0c", 1, 0, 1)
        der_copy("S0l", 0, 0); der_copy("S0c", 0, 1)
        der_gate("PAl", 2, 1, 0, 0.5); der_gate("PAc", 2, 1, 1, 0.5)
        der_scale("A2l", 4, 2, 0); der_scale("A2c", 4, 2, 1)
        der_copy("S2l", 3, 0); der_copy("S2c", 3, 1)
        der_gate("PMl", 5, 3, 0, 1.0)
        der_scale("A4l", 7, 4, 0); der_copy("S4l", 6, 0); der_gate("PBl", 8, 5, 0, 0.5)
        S.barrier()
        arena.reset(m0)

        def rstd_from(sq_tile, B_sq, W, rstd, B_rstd, pbank):
            def fn(e):
                for k in range(8):
                    ins = e.matmul(ps[pbank][:, 0:W], ones16_t[:, :], sq_tile[:, k, 0:W], start=(k == 0), stop=(k == 7))
                return ins
            S.op("pe", fn, reads=[B_sq, B_const], writes=[Bps[pbank]])
            S.op("act", lambda e: e.activation(rstd[:, 0:W], ps[pbank][:, 0:W], AF.Ln, bias=EPS, scale=1.0 / D),
                 reads=[Bps[pbank]], writes=[B_rstd])
            S.op("act", lambda e: e.activation(rstd[:, 0:W], rstd[:, 0:W], AF.Exp, scale=-0.5),
                 reads=[B_rstd], writes=[B_rstd])

        def norm_mod_thunks(src, B_src, W, rstd, B_rstd, A, Sh, dst_fn, B_dst, tmp, B_tmp):
            def one(k):
                t = tmp[k % 2]
                bt = B_tmp[k % 2]
                S.op("dve", (lambda e: e.tensor_tensor(t[:, 0:W], src[:, k, 0:W], rstd[:, 0:W], ALU.mult)),
                     reads=[B_src, B_rstd], writes=[bt])
                S.op("act", (lambda e: e.activation(dst_fn(k), t[:, 0:W], AF.Identity, bias=Sh[:, k:k + 1], scale=A[:, k:k + 1])),
                     reads=[bt, B_der], writes=[B_dst])
            return [(lambda k=k: one(k)) for k in range(8)]

        def norm_mod(src, B_src, W, rstd, B_rstd, A, Sh, dst_fn, B_dst, tmp, B_tmp):
            for k in range(8):
                t = tmp[k % 2]
                bt = B_tmp[k % 2]
                S.op("dve", (lambda e, k=k, t=t: e.tensor_tensor(t[:, 0:W], src[:, k, 0:W], rstd[:, 0:W], ALU.mult)),
                     reads=[B_src, B_rstd], writes=[bt])
                S.op("act", (lambda e, k=k, t=t: e.activation(dst_fn(k), t[:, 0:W], AF.Identity,
                                                              bias=Sh[:, k:k + 1], scale=A[:, k:k + 1])),
                     reads=[bt, B_der], writes=[B_dst])

        WC = {}

        def wload(dst, B_dst, src_f32, cache, key, ncols):
            if cache is None:
                S.dma("pool", lambda e: e.dma_start(out=dst, in_=src_f32), writes=[B_dst])
                return
            k = (cache,) + key
            if k not in WC:
                sc_ = nc.dram_tensor("wc_" + "_".join(str(z) for z in k), [128, ncols], BF16, kind="Internal").ap()
                WC[k] = (sc_, Buf("wc"))
                S.dma("pool", lambda e: e.dma_start(out=dst, in_=src_f32), writes=[B_dst])
                dflat = dst if len(dst.shape) == 2 else dst.rearrange("p a b -> p (a b)")
                S.dma("sp", lambda e: e.dma_start(out=sc_, in_=dflat), reads=[B_dst], writes=[WC[k][1]])
            else:
                sc_, bsc = WC[k]
                dflat = dst if len(dst.shape) == 2 else dst.rearrange("p a b -> p (a b)")
                S.dma("sp", lambda e: e.dma_start(out=dflat, in_=sc_), reads=[bsc], writes=[B_dst])

        def ffn(src, B_src, W, rstd_pre, B_rstd_pre, A, Sh, PG, w13r, w2r, bufs, cache, do_pre=True, hook1=None, hook2=None, defer_epi=False):
            (sq, B_sq, u, B_u, g, B_g, y, B_y, tmp, B_tmp, rs2, B_rs2, wb13, B_wb13, wb2, B_wb2, sa, B_sa) = bufs
            if do_pre:
                norm_mod(src, B_src, W, rstd_pre, B_rstd_pre, A, Sh, lambda k: u[:, k, 0:W], B_u, tmp, B_tmp)
            n13 = len(wb13)
            for j in range(NJ):
                sl = j % n13
                wbv = wb13[sl].rearrange("p (k c) -> p k c", k=8)
                wload(wb13[sl], B_wb13[sl], w13r[j], cache, ("w13", j), 2048)
                pa = j % 2
                pb = 2 + j % 2

                def fa(e, wbv=wbv, pa=pa):
                    for k in range(8):
                        ins = e.matmul(ps[pa][:, 0:W], wbv[:, k, 0:128], u[:, k, 0:W], start=(k == 0), stop=(k == 7))
                    return ins

                def fb(e, wbv=wbv, pb=pb):
                    for k in range(8):
                        ins = e.matmul(ps[pb][:, 0:W], wbv[:, k, 128:256], u[:, k, 0:W], start=(k == 0), stop=(k == 7))
                    return ins
                S.op("pe", fa, reads=[B_wb13[sl], B_u], writes=[Bps[pa]])
                S.op("pe", fb, reads=[B_wb13[sl], B_u], writes=[Bps[pb]])
                s2 = j % 2
                S.op("act", (lambda e, pa=pa, s2=s2: e.activation(sa[s2][:, 0:W], ps[pa][:, 0:W], AF.Silu)),
                     reads=[Bps[pa]], writes=[B_sa[s2]])
                S.op("dve", (lambda e, pb=pb, s2=s2, j=j: e.tensor_tensor(g[:, j, 0:W], sa[s2][:, 0:W], ps[pb][:, 0:W], ALU.mult)),
                     reads=[B_sa[s2], Bps[pb]], writes=[B_g])
                if hook1 is not None:
                    hook1(j)
            n2 = len(wb2)
            HJ = NJ // 2
            for i in range(8):
                halves = []
                for hf in range(2):
                    sl = (2 * i + hf) % n2
                    wload(wb2[sl], B_wb2[sl], w2r[i][:, hf * HJ * 128:(hf + 1) * HJ * 128], cache, ("w2", i, hf), HJ * 128)
                    halves.append((wb2[sl].rearrange("p (j c) -> p j c", j=HJ), B_wb2[sl]))
                py = 4 + i % 2

                def fy(e, halves=halves, py=py):
                    for j in range(NJ):
                        wv_ = halves[j // HJ][0]
                        ins = e.matmul(ps[py][:, 0:W], wv_[:, j % HJ, :], g[:, j, 0:W], start=(j == 0), stop=(j == NJ - 1))
                    return ins
                S.op("pe", fy, reads=[halves[0][1], halves[1][1], B_g], writes=[Bps[py]])
                S.op("act", (lambda e, py=py, i=i: e.copy(y[:, i, 0:W], ps[py][:, 0:W])), reads=[Bps[py]], writes=[B_y])
                S.op("act", (lambda e, py=py, i=i: e.activation(sq[:, i, 0:W], ps[py][:, 0:W], AF.Square)),
                     reads=[Bps[py]], writes=[B_sq])
                if hook2 is not None:
                    hook2(i)
            rstd_from(sq, B_sq, W, rs2, B_rs2, 7)

            def resid(i):
                t = tmp[i % 2]
                bt = B_tmp[i % 2]
                S.op("dve", (lambda e: e.tensor_tensor(t[:, 0:W], y[:, i, 0:W], rs2[:, 0:W], ALU.mult)),
                     reads=[B_y, B_rs2], writes=[bt])
                S.op("dve", (lambda e: e.scalar_tensor_tensor(src[:, i, 0:W], t[:, 0:W], PG[:, i:i + 1],
                                                              src[:, i, 0:W], ALU.mult, ALU.add)),
                     reads=[bt, B_der], writes=[B_src])
            thunks = [(lambda i=i: resid(i)) for i in range(8)]
            if defer_epi:
                return thunks
            for th in thunks:
                th()
            return []

        def alloc_ffn_bufs():
            sq = arena.bf16(8 * 512).rearrange("p (k t) -> p k t", k=8)
            u = arena.bf16(8 * 512).rearrange("p (k t) -> p k t", k=8)
            g = arena.bf16(NJ * 512).rearrange("p (k t) -> p k t", k=NJ)
            y = arena.f32(8 * 512).rearrange("p (k t) -> p k t", k=8)
            tmp = [arena.f32(512) for _ in range(2)]
            rs2 = arena.f32(512)
            wb13 = [arena.bf16(2048) for _ in range(3)]
            wb2 = [arena.bf16(FF // 2) for _ in range(4)]
            sa = [arena.f32(512) for _ in range(2)]
            return (sq, Buf("sq"), u, Buf("u"), g, Buf("g"), y, Buf("y"), tmp, [Buf("t0"), Buf("t1")],
                    rs2, Buf("rs2"), wb13, [Buf("wb13_%d" % i) for i in range(3)], wb2, [Buf("wb2_%d" % i) for i in range(4)],
                    sa, [Buf("sa0"), Buf("sa1")])

        mA = arena.mark()
        xt = [arena.f32(8 * 512).rearrange("p (k t) -> p k t", k=8) for _ in range(2)]
        B_xt = [Buf("xt0"), Buf("xt1")]
        rs1 = arena.f32(512)
        B_rs1 = Buf("rs1")
        fb = alloc_ffn_bufs()
        tmp, B_tmp, u_, B_u_ = fb[8], fb[9], fb[2], fb[3]
        sqx = arena.bf16(8 * 512).rearrange("p (k t) -> p k t", k=8)
        B_sqx = Buf("sqx")
        xT_v = xT.rearrange("(k p) t -> p k t", p=128)
        ctxT_v = ctxT.rearrange("(k p) t -> p k t", p=128)
        if dbg:
            d_u2 = dbg_tensor("d_u2", [128, 8 * NT], BF16)
        ntile = len(tiles) if stage >= 1 else 0

        def a1_load(ti):
            c0, W, j = tiles[ti]
            x = xt[ti % 2]
            src = ctxT_v[:, :, 0:W] if j == 1 else xT_v[:, :, c0 - CTX:c0 - CTX + W]
            S.dma("pool", lambda e: e.dma_start(out=x[:, :, 0:W], in_=src), writes=[B_xt[ti % 2]])

        def sq_thunks(x, bx, W):
            return [(lambda k=k: S.op("act", (lambda e: e.activation(sqx[:, k, 0:W], x[:, k, 0:W], AF.Square)), reads=[bx], writes=[B_sqx]))
                    for k in range(8)]

        def a1_pre_thunks(ti):
            c0, W, j = tiles[ti]
            x = xt[ti % 2]; bx = B_xt[ti % 2]
            sfx = "c" if j == 1 else "l"
            th = sq_thunks(x, bx, W)
            th.append(lambda: rstd_from(sqx, B_sqx, W, rs1, B_rs1, 6))
            th += norm_mod_thunks(x, bx, W, rs1, B_rs1, DER["A0" + sfx], DER["S0" + sfx], lambda k: u_[:, k, 0:W], B_u_, tmp, B_tmp)
            return th

        def a1_epi2_thunks(ti):
            c0, W, j = tiles[ti]
            x = xt[ti % 2]; bx = B_xt[ti % 2]
            sfx = "c" if j == 1 else "l"
            th = []
            if 1 <= ti <= 4:
                o0 = c0 - CTX
                th.append(lambda: S.dma("pool", lambda e: e.dma_start(out=H1.rearrange("(k p) t -> p k t", p=128)[:, :, o0:o0 + 512], in_=x[:, :, :]),
                                        reads=[bx], writes=[B_H1]))
            th += sq_thunks(x, bx, W)
            th.append(lambda: rstd_from(sqx, B_sqx, W, rs1, B_rs1, 6))
            th += norm_mod_thunks(x, bx, W, rs1, B_rs1, DER["A2" + sfx], DER["S2" + sfx], (lambda k: u2[:, k, c0:c0 + W]), B_u2[ti], tmp, B_tmp)
            return th

        pend1 = []
        pend2 = []
        if ntile:
            a1_load(0)
            for th in a1_pre_thunks(0):
                th()
            if ntile > 1:
                a1_load(1)
        for ti in range(ntile):
            c0, W, j = tiles[ti]
            sfx = "c" if j == 1 else "l"
            if ti + 1 < ntile:
                pend2 = a1_pre_thunks(ti + 1)

            def hook1(j_):
                n = 2 if len(pend1) > (NJ - 1 - j_) else 1
                for _ in range(n):
                    if pend1:
                        pend1.pop(0)()

            def hook2(i_):
                while pend1:
                    pend1.pop(0)()
                if i_ >= 1:
                    for _ in range(3):
                        if pend2:
                            pend2.pop(0)()
            epi1 = ffn(xt[ti % 2], B_xt[ti % 2], W, None, None, None, None, DER["PA" + sfx], w13a, w2a, fb, "A",
                       do_pre=False, hook1=hook1, hook2=hook2, defer_epi=True)
            while pend1:
                pend1.pop(0)()
            while pend2:
                pend2.pop(0)()
            pend1 = list(epi1) + a1_epi2_thunks(ti)
            if ti + 2 < ntile:
                pend1.append(lambda ti=ti: a1_load(ti + 2))
        while pend1:
            pend1.pop(0)()
        S.barrier()
        arena.reset(mA)
        if dbg:
            S.dma("sp", lambda e: e.dma_start(out=d_u2, in_=u2_t[:, :]), reads=B_u2)
            d_h1 = dbg_tensor("d_h1", [D, OWN])
            S.dma("sp", lambda e: e.dma_start(out=d_h1, in_=H1), reads=[B_H1])

        env = dict(locals())
        if stage >= 2:
            build_rest(env)
        S.barrier()
        S.emit()
    return nc, dbg_out


def build_rest3(g_, L):
    AX = mybir.AxisListType
    nc = g_["nc"]; S = g_["S"]; ps = g_["ps"]; Bps = g_["Bps"]; arena = g_["arena"]
    u2 = g_["u2"]; B_u2 = g_["B_u2"]; vcol = g_["vcol"]; DER = g_["DER"]
    id16 = g_["id16"]; B_const = g_["B_const"]; mm = g_["mm"]
    H1, HS = g_["H1"], g_["HS"]; B_H1 = g_["B_H1"]; B_HS = g_["B_HS"]
    yfT, B_yfT, mB = g_["yfT"], g_["B_yfT"], g_["mB"]
    rowv = g_["rowv"]; outT = g_["outT"]
    ffn = g_["ffn"]; rstd_from = g_["rstd_from"]; wload = g_["wload"]
    kp = lambda ap: ap.rearrange("(k p) n -> p k n", p=128)
    A = arena.t
    arena.reset(mB)
    tmp = [A[:, 0:512], A[:, 512:1024]]; B_tmp = [Buf("ct0"), Buf("ct1")]
    rs2 = A[:, 1024:1536]; B_rs2 = Buf("crs2")
    yreg = A[:, 5816:9912]
    B_yreg = Buf("yreg")
    y = yreg.rearrange("p (k t) -> p k t", k=8)
    hs_t = yreg[:, 0:1024]; o_sb = yreg[:, 1024:2048]; sig = yreg[:, 2048:3072]; sqh = yreg[:, 3072:4096]
    B_hsC = Buf("c_hs"); B_osb = Buf("c_osb"); B_sig = Buf("c_sig"); B_sqh = Buf("c_sqh")
    bo16 = A[:, 5816 + 1024:5816 + 1536].bitcast(BF16)
    ones16 = g_["ones16"]
    bo_bc = A[:, 9912:10936]; hg_bc = A[:, 10936:11960]
    B_bc = Buf("cbc")
    x = arena.f32(4096).rearrange("p (k t) -> p k t", k=8); B_x = Buf("cx")
    rs1 = arena.f32(512); B_rs1 = Buf("crs1")
    sq = arena.bf16(4096).rearrange("p (k t) -> p k t", k=8); B_sq = Buf("csq")
    u = arena.bf16(4096).rearrange("p (k t) -> p k t", k=8); B_u = Buf("cu")
    hmT = sq; yT = u
    sa = [arena.f32(512), arena.f32(512)]; B_sa = [Buf("csa0"), Buf("csa1")]
    mX = arena.mark()
    g = arena.bf16(NJ * 512).rearrange("p (k t) -> p k t", k=NJ); B_g = Buf("cg")
    wb13m = [arena.bf16(2048), arena.bf16(2048), arena.bf16(2048)]; B_wb13m = [Buf("cw13_0"), Buf("cw13_1"), Buf("cw13_2")]
    wb2m = arena.bf16(FF); B_wb2m = Buf("cw2")
    wb2x = A[:, 11992:11992 + 704].bitcast(BF16), A[:, 11992 + 704:11992 + 1408].bitcast(BF16)
    arena.reset(mX)
    wsl = [arena.bf16(8 * 512).rearrange("p (k n) -> p k n", k=8) for _ in range(4)]; B_wsl = [Buf("wsl%d" % i) for i in range(4)]
    hm = arena.bf16(1024); B_hm = Buf("hm")
    ol = A[:, mX + 4096:mX + 8192].rearrange("p (k t) -> p k t", k=8)
    B_ol = [B_wsl[2], B_wsl[3]]
    ss = rs2[:, 0:8]
    fb = (sq, B_sq, u, B_u, g, B_g, y, B_yreg, tmp, B_tmp, rs2, B_rs2, wb13m, B_wb13m,
          [wb2m[:, 0:FF // 2], wb2m[:, FF // 2:FF], wb2x[0], wb2x[1]], [Buf('cw2a'), Buf('cw2b'), Buf('cw2c'), Buf('cw2d')], sa, B_sa)
    xflat = A[:, mB:mB + 4096]
    rv = xflat[:, 0:3072]; ones1 = xflat[:, 3072:3200]
    S.dma("sp", lambda e: e.dma_start(out=rv[0:1, :], in_=rowv), writes=[B_x])
    S.op("pool", lambda e: e.memset(ones1[0:1, :], 1.0), writes=[B_x])
    for (dst, seg) in ((bo_bc, 1), (hg_bc, 2)):
        for hh in range(2):
            mm(ps[6][:, 0:512], [(ones1[0:1, :], rv[0:1, seg * 1024 + hh * 512:seg * 1024 + hh * 512 + 512])], reads=[B_x], writes=[Bps[6]])
            S.op("act", (lambda e, hh=hh, dst=dst: e.copy(dst[:, hh * 512:(hh + 1) * 512], ps[6][:, 0:512])), reads=[Bps[6]], writes=[B_bc])
    S.barrier()
    w_four, w_mproj, w_out, wo, wgf, wgm = [g_[n] for n in "w_four w_mproj w_out wo wgf wgm".split()]
    w13b, w2b = g_["w13b"], g_["w2b"]
    H1v = H1.rearrange("(k p) t -> p k t", p=128)
    outv = outT.rearrange("(k p) t -> p k t", p=128)
    for T in range(4):
        t0 = 512 * T
        c0 = CTX + t0
        S.op("act", lambda e: e.copy(bo16[0:1, :], bo_bc[0:1, :]), reads=[B_bc], writes=[B_osb])
        for hh in range(2):
            wload(wsl[hh], B_wsl[hh], kp(wo)[:, :, hh * 512:(hh + 1) * 512], "C", ("wo", hh), 4096)
        for ch in range(4):
            cc = c0 + 128 * ch
            ob = t0 + 128 * ch
            S.dma("sp", (lambda e, ob=ob: e.dma_start(out=hs_t, in_=HS[ob:ob + 128, :])), reads=[B_HS[ob // 128]], writes=[B_hsC])
            for hh in range(2):
                mm(ps[hh][:, 0:512], [(u2[:, k, cc:cc + 128], wsl[hh][:, k, :]) for k in range(8)]
                   + [(ones16[0:1, 0:128], bo16[0:1, hh * 512:(hh + 1) * 512])],
                   reads=[B_wsl[hh], B_osb, B_const] + B_u2, writes=[Bps[hh]])
                S.op("act", (lambda e, hh=hh: e.activation(sig[:, hh * 512:(hh + 1) * 512], ps[hh][:, 0:512], AF.Sigmoid)),
                     reads=[Bps[hh]], writes=[B_sig])
            S.op("dve", lambda e: e.tensor_tensor(sig, sig, hg_bc, ALU.mult), reads=[B_sig, B_bc], writes=[B_sig])
            S.op("act", lambda e: e.activation(sqh, hs_t, AF.Square), reads=[B_hsC], writes=[B_sqh])
            S.op("dve", lambda e: e.reduce_sum(ss[:, 0:4], sqh.rearrange("p (h e) -> p h e", h=4), AX.X), reads=[B_sqh], writes=[B_rs2])
            S.op("act", lambda e: e.activation(ss[:, 0:4], ss[:, 0:4], AF.Ln, bias=EPS, scale=1.0 / 256), reads=[B_rs2], writes=[B_rs2])
            S.op("act", lambda e: e.activation(ss[:, 0:4], ss[:, 0:4], AF.Exp, scale=-0.5), reads=[B_rs2], writes=[B_rs2])
            for h in range(4):
                S.op("dve", (lambda e, h=h: e.scalar_tensor_tensor(hm[:, h * 256:(h + 1) * 256], hs_t[:, h * 256:(h + 1) * 256], ss[:, h:h + 1],
                                                                  sig[:, h * 256:(h + 1) * 256], ALU.mult, ALU.mult)),
                     reads=[B_hsC, B_sig, B_rs2], writes=[B_hm])
            for half in range(2):
                pb = 2 + half
                def fn(e, half=half, pb=pb):
                    for cq in range(4):
                        c = half * 4 + cq
                        ins = e.matmul(ps[pb][:, cq * 128:(cq + 1) * 128], hm[:, c * 128:(c + 1) * 128], id16[:, :], start=True, stop=True)
                    return ins
                S.op("pe", fn, reads=[B_hm, B_const], writes=[Bps[pb]])
                S.op("act", (lambda e, half=half, pb=pb, ch=ch: e.copy(hmT[:, half * 4:(half + 1) * 4, ch * 128:(ch + 1) * 128],
                                                                      ps[pb][:, 0:512].rearrange("p (c t) -> p c t", c=4))),
                     reads=[Bps[pb]], writes=[B_sq])
        for hh in range(2):
            cs_ = slice(hh * 512, (hh + 1) * 512)
            wload(wsl[0][:, 0:4, :], B_wsl[0], kp(w_four)[:, :, cs_], "C", ("w4", hh), 2048)
            wload(wsl[1], B_wsl[1], kp(w_mproj)[:, :, cs_], "C", ("wm", hh), 4096)
            wload(wsl[2], B_wsl[2], kp(wgf)[:, :, cs_], "C", ("wgf", hh), 4096)
            wload(wsl[3], B_wsl[3], kp(wgm)[:, :, cs_], "C", ("wgm", hh), 4096)
            for ii in range(4):
                i = hh * 4 + ii
                cw = slice(ii * 128, (ii + 1) * 128)
                mm(ps[0][:, 0:512], [(wsl[0][:, gq, cw], yfT[:, gq, t0:t0 + 512]) for gq in range(4)], reads=[B_wsl[0], B_yfT], writes=[Bps[0]])
                mm(ps[1][:, 0:512], [(wsl[1][:, k, cw], hmT[:, k, :]) for k in range(8)], reads=[B_wsl[1], B_sq], writes=[Bps[1]])
                mm(ps[2][:, 0:512], [(wsl[2][:, k, cw], u2[:, k, c0:c0 + 512]) for k in range(8)], reads=[B_wsl[2]] + B_u2, writes=[Bps[2]])
                mm(ps[3][:, 0:512], [(wsl[3][:, k, cw], u2[:, k, c0:c0 + 512]) for k in range(8)], reads=[B_wsl[3]] + B_u2, writes=[Bps[3]])
                S.op("act", (lambda e, i=i: e.activation(sa[0], ps[2][:, 0:512], AF.Sigmoid, bias=vcol("bgf", i, 1))), reads=[Bps[2], B_const], writes=[B_sa[0]])
                S.op("act", (lambda e, i=i: e.activation(sa[1], ps[3][:, 0:512], AF.Sigmoid, bias=vcol("bgm", i, 1))), reads=[Bps[3], B_const], writes=[B_sa[1]])
                S.op("dve", lambda e: e.tensor_tensor(sa[0], sa[0], ps[0][:, 0:512], ALU.mult), reads=[Bps[0], B_sa[0]], writes=[B_sa[0]])
                S.op("dve", lambda e: e.tensor_tensor(sa[1], sa[1], ps[1][:, 0:512], ALU.mult), reads=[Bps[1], B_sa[1]], writes=[B_sa[1]])
                S.op("dve", (lambda e, i=i: e.tensor_tensor(yT[:, i, :], sa[0], sa[1], ALU.add)), reads=B_sa, writes=[B_u])
        S.dma("sp", (lambda e, t0=t0: e.dma_start(out=x[:, :, :], in_=H1v[:, :, t0:t0 + 512])), reads=[B_H1], writes=[B_x])
        for hh in range(2):
            wload(wsl[hh], B_wsl[hh], kp(w_out)[:, :, hh * 512:(hh + 1) * 512], "C", ("wout", hh), 4096)
        for i in range(8):
            pb = 4 + i % 2
            mm(ps[pb][:, 0:512], [(wsl[i // 4][:, k, (i % 4) * 128:(i % 4 + 1) * 128], yT[:, k, :]) for k in range(8)],
               reads=[B_wsl[i // 4], B_u], writes=[Bps[pb]])
            S.op("act", (lambda e, i=i, pb=pb: e.copy(ol[:, i, :], ps[pb][:, 0:512])), reads=[Bps[pb]], writes=B_ol)
            S.op("act", (lambda e, i=i, pb=pb: e.activation(sq[:, i, :], ps[pb][:, 0:512], AF.Square)), reads=[Bps[pb]], writes=[B_sq])
        rstd_from(sq, B_sq, 512, rs1, B_rs1, 6)
        for i in range(8):
            t = tmp[i % 2]; bt = B_tmp[i % 2]
            S.op("dve", (lambda e, i=i, t=t: e.tensor_tensor(t, ol[:, i, :], rs1, ALU.mult)), reads=B_ol + [B_rs1], writes=[bt])
            S.op("dve", (lambda e, i=i, t=t: e.scalar_tensor_tensor(x[:, i, :], t, DER["PMl"][:, i:i + 1], x[:, i, :], ALU.mult, ALU.add)),
                 reads=[bt, g_["B_der"]], writes=[B_x])
        S.barrier()
        S.op("act", lambda e: e.activation(sq[:, :, :], x[:, :, :], AF.Square), reads=[B_x], writes=[B_sq])
        rstd_from(sq, B_sq, 512, rs1, B_rs1, 6)
        ffn(x, B_x, 512, rs1, B_rs1, DER["A4l"], DER["S4l"], DER["PBl"], w13b, w2b, fb, "B")
        t_store = S.dma("sp", (lambda e, t0=t0: e.dma_start(out=outv[:, :, t0:t0 + 512], in_=x[:, :, :])), reads=[B_x])
        S.barrier(skip=[t_store])


def build_rest2(env, L):
    AX = mybir.AxisListType
    g_ = dict(env); g_.update(L)
    nc = g_["nc"]; S = g_["S"]; ps = g_["ps"]; Bps = g_["Bps"]; arena = g_["arena"]
    u2 = g_["u2"]; B_u2 = g_["B_u2"]; vcol = g_["vcol"]; DER = g_["DER"]
    ident = g_["ident"]; maskf = g_["maskf"]; maskb = g_["maskb"]; CS = g_["CS"]; M2 = g_["M2"]; M3 = g_["M3"]
    id16 = g_["id16"]; ones16 = g_["ones16"]; B_const = g_["B_const"]; B_der = g_["B_der"]
    sel = g_["sel"]; negsel = g_["negsel"]; mm = g_["mm"]
    H1, AB, PQ, KT, QT, KTOK, VTOK, GROW, HS = [g_[n] for n in "H1 AB PQ KT QT KTOK VTOK GROW HS".split()]
    B_H1, B_AB, B_PQ, B_KT, B_QT, B_KTOK, B_VTOK, B_GROW = [g_["B_" + n] for n in "H1 AB PQ KT QT KTOK VTOK GROW".split()]
    B_HS = g_["B_HS"]
    Rcol, Gcol, Ecol, gend, acol, wkcol, decay, B_cols = [g_[n] for n in "Rcol Gcol Ecol gend acol wkcol decay B_cols".split()]
    yfT, B_yfT, C32, C16, B_C, mB = [g_[n] for n in "yfT B_yfT C32 C16 B_C mB".split()]
    rowv = g_["rowv"]; outT = g_["outT"]
    tiles = g_["tiles"]
    kp = lambda ap: ap.rearrange("(k p) n -> p k n", p=128)

    wbuf = [arena.bf16(8 * 1024).rearrange("p (k n) -> p k n", k=8) for _ in range(2)]
    B_wbuf = [Buf("wbuf0"), Buf("wbuf1")]
    xf = [arena.f32(512) for _ in range(4)]
    B_xf = [Buf("xf%d" % i) for i in range(4)]
    ab_sb2 = [arena.f32(1024).rearrange("p (x g c) -> p x g c", x=2, g=4) for _ in range(2)]
    B_ab2 = [Buf("ab_sb0"), Buf("ab_sb1")]
    Pt2 = [arena.f32(514), arena.f32(514)]
    B_Pt2 = [Buf("Pt0"), Buf("Pt1")]
    acc2 = [arena.f32(512), arena.f32(512)]
    B_acc2 = [Buf("acc0"), Buf("acc1")]
    kTt = arena.bf16(8 * 512).rearrange("p (k t) -> p k t", k=8)
    B_kTt = Buf("kTt")
    tok2 = [arena.bf16(1024), arena.bf16(1024)]
    B_tok2 = [Buf("tok0"), Buf("tok1")]
    tokctr = [0]
    bv_bc = arena.f32(1024)
    ones1 = arena.f32(128)
    rv = arena.f32(1024)
    B_bc = Buf("bc")
    S.dma("sp", lambda e: e.dma_start(out=rv[0:1, :], in_=rowv[:, 0:1024]), writes=[B_bc])
    S.op("pool", lambda e: e.memset(ones1[0:1, :], 1.0), writes=[B_bc])

    def bcast_row(dst, seg):
        for hh in range(2):
            mm(ps[6][:, 0:512], [(ones1[0:1, :], rv[0:1, seg * 1024 + hh * 512:seg * 1024 + hh * 512 + 512])], reads=[B_bc], writes=[Bps[6]])
            S.op("act", (lambda e, hh=hh: e.copy(dst[:, hh * 512:(hh + 1) * 512], ps[6][:, 0:512])), reads=[Bps[6]], writes=[B_bc])
    bcast_row(bv_bc, 0)

    wF = g_["wF"]
    S.dma("pool", lambda e: e.dma_start(out=wbuf[0][:, :, 0:512], in_=kp(wF)), writes=[B_wbuf[0]])
    for i in range(8):
        c0 = CTX + 512 * i
        for g in range(4):
            pb = g % 2
            mm(ps[pb][:, 0:512], [(wbuf[0][:, k, g * 128:(g + 1) * 128], u2[:, k, c0:c0 + 512]) for k in range(8)],
               reads=[B_wbuf[0]] + B_u2, writes=[Bps[pb]])
            S.op("act", (lambda e, g=g, pb=pb: e.activation(xf[g], ps[pb][:, 0:512], AF.Identity, bias=vcol("bF", g, 1))),
                 reads=[Bps[pb], B_const], writes=[B_xf[g]])
        for tb in range(4):
            ab_sb = ab_sb2[tb % 2]; B_ab = B_ab2[tb % 2]
            for g in range(4):
                bank = 2 + g // 2
                mm(ps[bank][:, (g % 2) * 256:(g % 2) * 256 + 256], [(xf[g][:, tb * 128:(tb + 1) * 128], CS)],
                   reads=[B_xf[g], B_const], writes=[Bps[bank]])
            for bi in range(2):
                S.op("dve", (lambda e, bi=bi, ab_sb=ab_sb: e.tensor_copy(ab_sb[:, :, 2 * bi:2 * bi + 2, :],
                                                            ps[2 + bi][:, 0:512].rearrange("p (g x c) -> p x g c", g=2, x=2))),
                     reads=[Bps[2 + bi]], writes=[B_ab])
            tok0 = 512 * i + 128 * tb
            S.dma("sp", (lambda e, tok0=tok0, ab_sb=ab_sb: e.dma_start(out=AB.rearrange("x t f -> t x f")[tok0:tok0 + 128, :, :],
                                                          in_=ab_sb.rearrange("p x g c -> p x (g c)"))), reads=[B_ab], writes=[B_AB])

    def qk_proj(wdram, bname, cwname, cbname, slot, tlist, is_k):
        S.dma("pool", lambda e: e.dma_start(out=wbuf[slot], in_=kp(wdram)), writes=[B_wbuf[slot]])
        pend_silu = [None]
        for (ti, c0, W) in tlist:
            islat = ti >= 1
            left = islat and ti > 1
            right = islat and ti < 8
            for c in range(8):
                pb = c % 2
                Pt = Pt2[c % 2]; B_Pt = B_Pt2[c % 2]; acc = acc2[c % 2]; B_acc = B_acc2[c % 2]
                hb = 5 + c % 2
                mm(ps[pb][:, 0:W], [(wbuf[slot][:, k, c * 128:(c + 1) * 128], u2[:, k, c0:c0 + W]) for k in range(8)],
                   reads=[B_wbuf[slot]] + B_u2, writes=[Bps[pb]])
                S.op("act", (lambda e, c=c, pb=pb, W=W, Pt=Pt: e.activation(Pt[:, 1:W + 1], ps[pb][:, 0:W], AF.Identity, bias=vcol(bname, c, 1))),
                     reads=[Bps[pb], B_const], writes=[B_Pt])
                if left and right:
                    mm(ps[hb][:, 0:2], [(wbuf[slot][:, k, c * 128:(c + 1) * 128], u2[:, k, c0 - 1:c0 + W + 1:W + 1]) for k in range(8)],
                       reads=[B_wbuf[slot]] + B_u2, writes=[Bps[hb]])
                    S.op("act", (lambda e, c=c, Pt=Pt, hb=hb, W=W: e.activation(Pt[:, 0:W + 2:W + 1], ps[hb][:, 0:2], AF.Identity, bias=vcol(bname, c, 1))),
                         reads=[Bps[hb], B_const], writes=[B_Pt])
                else:
                    for hi_, (has, col, dstc) in enumerate(((left, c0 - 1, 0), (right, c0 + W, W + 1))):
                        if has:
                            mm(ps[hb][:, hi_:hi_ + 1], [(wbuf[slot][:, k, c * 128:(c + 1) * 128], u2[:, k, col:col + 1]) for k in range(8)],
                               reads=[B_wbuf[slot]] + B_u2, writes=[Bps[hb]])
                            S.op("act", (lambda e, c=c, dstc=dstc, Pt=Pt, hb=hb, hi_=hi_: e.activation(Pt[:, dstc:dstc + 1], ps[hb][:, hi_:hi_ + 1], AF.Identity, bias=vcol(bname, c, 1))),
                                 reads=[Bps[hb], B_const], writes=[B_Pt])
                        else:
                            S.op("pool", (lambda e, dstc=dstc, Pt=Pt: e.memset(Pt[:, dstc:dstc + 1], 0.0)), writes=[B_Pt])
                S.op("act", (lambda e, c=c, W=W, Pt=Pt, acc=acc: e.activation(acc[:, 0:W], Pt[:, 0:W], AF.Identity, scale=vcol(cwname, c, 1))),
                     reads=[B_Pt, B_const], writes=[B_acc])
                S.op("dve", (lambda e, c=c, W=W, Pt=Pt, acc=acc: e.scalar_tensor_tensor(acc[:, 0:W], Pt[:, 1:W + 1], vcol(cwname, 8 + c, 1), acc[:, 0:W], ALU.mult, ALU.add)),
                     reads=[B_Pt, B_const], writes=[B_acc])
                S.op("dve", (lambda e, c=c, W=W, Pt=Pt, acc=acc: e.scalar_tensor_tensor(acc[:, 0:W], Pt[:, 2:W + 2], vcol(cwname, 16 + c, 1), acc[:, 0:W], ALU.mult, ALU.add)),
                     reads=[B_Pt, B_const], writes=[B_acc])
                if pend_silu[0] is not None:
                    pend_silu[0]()
                pend_silu[0] = (lambda c=c, W=W, acc=acc, B_acc=B_acc: S.op(
                    "act", (lambda e: e.activation(kTt[:, c, 0:W], acc[:, 0:W], AF.Silu, bias=vcol(cbname, c, 1))),
                    reads=[B_acc, B_const], writes=[B_kTt]))
            if pend_silu[0] is not None:
                pend_silu[0]()
                pend_silu[0] = None
            own = 1 <= ti <= 4
            if own:
                o0 = c0 - CTX
                dst = KT if is_k else QT
                S.dma("sp", (lambda e, o0=o0, dst=dst: e.dma_start(out=dst.rearrange("(k p) t -> p k t", p=128)[:, :, o0:o0 + 512], in_=kTt[:, :, :])),
                      reads=[B_kTt], writes=[B_KT if is_k else B_QT])
            if is_k:
                for tb in range(W // 128):
                    tok = tok2[tokctr[0] % 2]; B_tok = B_tok2[tokctr[0] % 2]; tokctr[0] += 1
                    for half in range(2):
                        pb = 3 + half
                        def fn(e, tb=tb, half=half, pb=pb):
                            for cc in range(4):
                                c = half * 4 + cc
                                ins = e.matmul(ps[pb][:, cc * 128:(cc + 1) * 128], kTt[:, c, tb * 128:(tb + 1) * 128], id16[:, :], start=True, stop=True)
                            return ins
                        S.op("pe", fn, reads=[B_kTt, B_const], writes=[Bps[pb]])
                        S.op("dve", (lambda e, half=half, pb=pb, tok=tok: e.tensor_copy(tok[:, half * 512:(half + 1) * 512], ps[pb][:, 0:512])),
                             reads=[Bps[pb]], writes=[B_tok])
                    r0 = c0 + tb * 128
                    S.dma("sp", (lambda e, r0=r0, tok=tok: e.dma_start(out=KTOK[r0:r0 + 128, :], in_=tok)), reads=[B_tok], writes=[B_KTOK])

    tl_all = [(ti, c0, W) for ti, (c0, W, j) in enumerate(tiles)]
    qk_proj(g_["wk"], "bk", "cwk", "cbk", 1, tl_all, True)
    qk_proj(g_["wq"], "bq", "cwq", "cbq", 0, tl_all[1:5], False)
    S.dma("pool", lambda e: e.dma_start(out=wbuf[1], in_=kp(g_["wv"])), writes=[B_wbuf[1]])
    for cb in range(0, NT, 128):
        tok = tok2[tokctr[0] % 2]; B_tok = B_tok2[tokctr[0] % 2]; tokctr[0] += 1
        for half in range(2):
            pb = half + 2 * ((cb // 128) % 2)
            mm(ps[pb][:, 0:512], [(u2[:, k, cb:cb + 128], wbuf[1][:, k, half * 512:(half + 1) * 512]) for k in range(8)],
               reads=[B_wbuf[1]] + B_u2, writes=[Bps[pb]])
            S.op("dve", (lambda e, half=half, pb=pb, tok=tok: e.tensor_tensor(tok[:, half * 512:(half + 1) * 512], ps[pb][:, 0:512],
                                                                     bv_bc[:, half * 512:(half + 1) * 512], ALU.add)),
                 reads=[Bps[pb], B_bc], writes=[B_tok])
        S.dma("sp", (lambda e, cb=cb, tok=tok: e.dma_start(out=VTOK[cb:cb + 128, :], in_=tok)), reads=[B_tok], writes=[B_VTOK])
    S.barrier()
    arena.reset(mB)

    inb2 = [arena.f32(8 * 512).rearrange("p (r f) -> p r f", r=8) for _ in range(2)]
    outb2 = [arena.f32(8 * 512).rearrange("p (r f) -> p r f", r=8) for _ in range(2)]
    B_inb2 = [Buf("inb0"), Buf("inb1")]; B_outb2 = [Buf("outb0"), Buf("outb1")]
    ABv = AB.rearrange("x (r c) f -> x c r f", c=64)
    PQw = PQ
    for rb in range(8):
        inb = inb2[rb % 2]; outb = outb2[rb % 2]; B_inb = B_inb2[rb % 2]; B_outb = B_outb2[rb % 2]
        for x in range(2):
            S.dma("sp", (lambda e, rb=rb, x=x, inb=inb: e.dma_start(out=inb[64 * x:64 * x + 64, :, :], in_=ABv[x, :, rb * 8:rb * 8 + 8, :])),
                  reads=[B_AB], writes=[B_inb])
        for r in range(8):
            pb = r % 4
            mm(ps[pb][:, 0:512], [(M2, inb[:, r, :])], reads=[B_inb, B_const], writes=[Bps[pb]])
            if r % 2 == 0:
                S.op("act", (lambda e, r=r, pb=pb, outb=outb: e.copy(outb[:, r, :], ps[pb][:, 0:512])), reads=[Bps[pb]], writes=[B_outb])
            else:
                S.op("dve", (lambda e, r=r, pb=pb, outb=outb: e.tensor_copy(outb[:, r, :], ps[pb][:, 0:512])), reads=[Bps[pb]], writes=[B_outb])
        for x in range(2):
            S.dma("sp", (lambda e, rb=rb, x=x, outb=outb: e.dma_start(out=PQw[x, :, rb * 8:rb * 8 + 8, :], in_=outb[64 * x:64 * x + 64, :, :])),
                  reads=[B_outb], writes=[B_PQ])
    PQr = PQ.rearrange("x kc r f -> x r kc f")
    for kb in range(8):
        inb = inb2[kb % 2]; B_inb = B_inb2[kb % 2]
        for x in range(2):
            S.dma("sp", (lambda e, kb=kb, x=x, inb=inb: e.dma_start(out=inb[64 * x:64 * x + 64, :, :], in_=PQr[x, :, kb * 8:kb * 8 + 8, :])),
                  reads=[B_PQ], writes=[B_inb])
        for g in range(4):
            pb = 4 + g
            def fn(e, g=g, pb=pb, inb=inb):
                for kc in range(8):
                    ins = e.matmul(ps[pb][:, kc * 32:(kc + 1) * 32], inb[:, kc, g * 128:(g + 1) * 128], M3, start=True, stop=True)
                return ins
            S.op("pe", fn, reads=[B_inb, B_const], writes=[Bps[pb]])
            S.op("act", (lambda e, g=g, pb=pb, kb=kb: e.copy(yfT[:, g, :].rearrange("p (kr kc) -> p kc kr", kc=64)[:, kb * 8:kb * 8 + 8, :],
                                                          ps[pb][:, 0:256].rearrange("p (kc kr) -> p kc kr", kr=32))),
                 reads=[Bps[pb]], writes=[B_yfT])
    S.barrier()
    arena.reset(mB)

    NLS = 3
    NHS = 2
    LD = []
    for i in range(2 * NLS):
        d_ = dict(ktok=arena.bf16(1024), vaug=arena.bf16(4 * 258).rearrange("p (h e) -> p h e", h=4),
                  kT=arena.bf16(1024).rearrange("p (k t) -> p k t", k=8), qT=arena.bf16(1024).rearrange("p (k t) -> p k t", k=8),
                  grow=arena.f32(128), B_ld=Buf("ld%d" % i), B_ldo=Buf("ldo%d" % i))
        S.op("pool", (lambda e, v=d_["vaug"]: e.memset(v[:, :, :], 1.0)), writes=[d_["B_ld"]])
        LD.append(d_)
    HSB = [dict(hs=arena.f32(1024), B_hs=Buf("hs%d" % i)) for i in range(4)]
    HT = []
    B_pP2s = Buf("pP2"); B_pP1s = Buf("pP1")
    for i in range(NHS):
        HT.append(dict(wT=arena.f32(128), STb=arena.bf16(128), P2sb=arena.f32(257), hn=arena.f32(257), dd=arena.f32(2), kw=arena.bf16(256),
                       B_wT=Buf("wT%d" % i), B_ST=Buf("ST%d" % i), B_P2=Buf("P2sb%d" % i), B_hn=Buf("hn%d" % i), B_dd=Buf("dd%d" % i),
                       B_kw=Buf("kw%d" % i),
                       pST=ps[i][:, 0:128], pD=ps[i][:, 128:256], pP2=ps[2][:, 0:257], pP1=ps[3][:, 0:257],
                       pCU=[ps[4 + i][:, 0:257], ps[6 + i][:, 0:257]],
                       B_pSD=Buf("pSD%d" % i), B_pP2=B_pP2s, B_pP1=B_pP1s, B_pCU=[Buf("pCU0_%d" % i), Buf("pCU1_%d" % i)]))
    B_C32 = [[Buf('C32_%d_%d' % (q, c)) for c in range(2)] for q in range(8)]
    B_C16 = [[Buf('C16_%d_%d' % (q, c)) for c in range(2)] for q in range(8)]
    steps = []
    fw = [(0, 0, None), (1, 128, None)] + [(2 + i, CTX + 128 * i, 128 * i) for i in range(16)]
    bw = [(0, 128, None), (1, 0, None)] + [(2 + i, CTX + 128 * (31 - i), (128 * (31 - i) if 31 - i <= 15 else None)) for i in range(32)]
    for i in range(34):
        if i < 18:
            steps.append((0,) + fw[i])
        steps.append((1,) + bw[i])
    mask16 = [arena.bf16(128), arena.bf16(128)]
    S.op("act", lambda e: e.copy(mask16[0], maskf), reads=[B_const], writes=[B_const])
    S.op("act", lambda e: e.copy(mask16[1], maskb), reads=[B_const], writes=[B_const])

    def emit_loads(si):
        (dr, sc, cb, ob) = steps[si]
        L_ = LD[dr * NLS + sc % NLS]
        ktok_t, vaug, kT_t, qT_t, grow_t = L_["ktok"], L_["vaug"], L_["kT"], L_["qT"], L_["grow"]
        B_ld, B_ldo = L_["B_ld"], L_["B_ldo"]
        S.dma("sp", (lambda e: e.dma_start(out=ktok_t, in_=KTOK[cb:cb + 128, :])), reads=[B_KTOK], writes=[B_ld])
        S.dma("sp", (lambda e: e.dma_start(out=vaug[:, :, 0:256], in_=VTOK[cb:cb + 128, :].rearrange("t (h e) -> t h e", h=4))),
              reads=[B_VTOK], writes=[B_ld])
        if ob is not None:
            S.dma("sp", (lambda e: e.dma_start(out=kT_t, in_=KT.rearrange("(k p) t -> p k t", p=128)[:, :, ob:ob + 128])), reads=[B_KT], writes=[B_ldo])
            S.dma("sp", (lambda e: e.dma_start(out=qT_t, in_=QT.rearrange("(k p) t -> p k t", p=128)[:, :, ob:ob + 128])), reads=[B_QT], writes=[B_ldo])
            S.dma("sp", (lambda e: e.dma_start(out=grow_t[0:36, :], in_=GROW[:, cb:cb + 128])), reads=[B_GROW], writes=[B_ldo])

    def emit_load_hs(si):
        (dr, sc, cb, ob) = steps[si]
        if ob is not None and dr == 1:
            H2 = HSB[dr * 2 + sc % 2]
            S.dma("sp", (lambda e: e.dma_start(out=H2["hs"], in_=HS[ob:ob + 128, :])), reads=[B_HS[ob // 128]], writes=[H2["B_hs"]])

    items = []
    for si, (dr, sc, cb, ob) in enumerate(steps):
        for h in range(4):
            items.append((si, h, len(items)))

    def ctx_of(it):
        si, h, n = it
        (dr, sc, cb, ob) = steps[si]
        L2 = dict(LD[dr * NLS + sc % NLS]); L2.update(HSB[dr * 2 + sc % 2])
        return dr, sc, cb, ob, h, dr * 4 + h, (sc if dr == 0 else 18 + sc), L2, HT[n % NHS]

    def emit_A(it):
        dr, sc, cb, ob, h, q, ci, L_, H_ = ctx_of(it)
        kT_t, qT_t, grow_t, ktok_t = L_["kT"], L_["qT"], L_["grow"], L_["ktok"]
        wT, STb, kw, pST, pD = H_["wT"], H_["STb"], H_["kw"], H_["pST"], H_["pD"]
        S.op("pool", (lambda e: e.tensor_scalar(kw, ktok_t[:, h * 256:(h + 1) * 256], wkcol[:, q, sc:sc + 1], 0.0625, ALU.mult, ALU.mult)),
             reads=[L_["B_ld"], B_cols], writes=[H_["B_kw"]])
        if ob is not None:
            def fsd(e):
                e.matmul(pST, kT_t[:, 2 * h, :], qT_t[:, 2 * h, :], start=True, stop=False)
                e.matmul(pST, kT_t[:, 2 * h + 1, :], qT_t[:, 2 * h + 1, :], start=False, stop=True)
                e.matmul(pD, negsel[0:36, q, :], grow_t[0:36, :], start=True, stop=False)
                return e.matmul(pD, ident, maskf if dr == 0 else maskb, start=False, stop=True)
            S.op("pe", fsd, reads=[L_["B_ldo"], B_const], writes=[H_["B_pSD"]])
            S.op("act", (lambda e: e.activation(wT, pD, AF.Exp, bias=Rcol[:, ci, h:h + 1])),
                 reads=[H_["B_pSD"], B_cols], writes=[H_["B_wT"]])
            S.op("dve", (lambda e: e.scalar_tensor_tensor(STb, pST, 0.0625, wT, ALU.mult, ALU.mult)),
                 reads=[H_["B_pSD"], H_["B_wT"]], writes=[H_["B_ST"]])

    pend_fin = []

    def emit_BC(it):
        while pend_fin:
            pend_fin.pop(0)()
        dr, sc, cb, ob, h, q, ci, L_, H_ = ctx_of(it)
        vaug, qT_t, hs_t = L_["vaug"], L_["qT"], L_["hs"]
        B_ld, B_ldo, B_hs = L_["B_ld"], L_["B_ldo"], L_["B_hs"]
        STb, P2sb, hn, dd, kw = H_["STb"], H_["P2sb"], H_["hn"], H_["dd"], H_["kw"]
        pP2, pP1, pCU = H_["pP2"], H_["pP1"], H_["pCU"]
        for c in range(2):
            mm(pCU[c], [(kw[:, c * 128:(c + 1) * 128], vaug[:, h, 0:257])], reads=[H_["B_kw"], B_ld], writes=[H_["B_pCU"][c]])
        if ob is not None:
            mm(pP1, [(qT_t[:, 2 * h + c, :], C16[:, q, c, 0:257]) for c in range(2)], reads=[B_ldo] + B_C16[q], writes=[H_["B_pP1"]])
        for c in range(2):
            S.op("dve", (lambda e, c=c: e.scalar_tensor_tensor(C32[:, q, c, :], C32[:, q, c, :], decay[:, q, sc:sc + 1], pCU[c], ALU.mult, ALU.add)),
                 reads=[H_["B_pCU"][c], B_cols], writes=[B_C32[q][c]])
            S.op("act", (lambda e, c=c: e.copy(C16[:, q, c, 0:257], C32[:, q, c, :])), reads=[B_C32[q][c]], writes=[B_C16[q][c]])
        if ob is not None:
            mm(pP2, [(STb, vaug[:, h, 0:257])], reads=[H_["B_ST"], B_ld], writes=[H_["B_pP2"]])
            S.op("act", (lambda e: e.copy(P2sb, pP2)), reads=[H_["B_pP2"]], writes=[H_["B_P2"]])
            S.op("dve", (lambda e: e.scalar_tensor_tensor(hn, pP1, acol[:, q, sc:sc + 1], P2sb, ALU.mult, ALU.add)),
                 reads=[H_["B_pP1"], H_["B_P2"], B_cols], writes=[H_["B_hn"]])
            S.op("dve", (lambda e: e.scalar_tensor_tensor(dd[:, 0:1], hn[:, 256:257], -1.0, hn[:, 256:257], ALU.mult, ALU.max)),
                 reads=[H_["B_hn"]], writes=[H_["B_dd"]])
            S.op("dve", (lambda e: e.tensor_tensor(dd[:, 0:1], dd[:, 0:1], Ecol[:, ci, h:h + 1], ALU.max)),
                 reads=[H_["B_dd"], B_cols], writes=[H_["B_dd"]])
            S.op("dve", (lambda e: e.reciprocal(dd[:, 1:2], dd[:, 0:1])), reads=[H_["B_dd"]], writes=[H_["B_dd"]])
            def fin():
                if dr == 0:
                    S.op("act", (lambda e: e.activation(hs_t[:, h * 256:(h + 1) * 256], hn[:, 0:256], AF.Identity, scale=dd[:, 1:2])),
                         reads=[H_["B_hn"], H_["B_dd"]], writes=[B_hs])
                else:
                    S.op("dve", (lambda e: e.scalar_tensor_tensor(hs_t[:, h * 256:(h + 1) * 256], hn[:, 0:256], dd[:, 1:2],
                                                                  hs_t[:, h * 256:(h + 1) * 256], ALU.mult, ALU.add)),
                         reads=[H_["B_hn"], H_["B_dd"]], writes=[B_hs])
                if h == 3:
                    S.dma("sp", (lambda e: e.dma_start(out=HS[ob:ob + 128, :], in_=hs_t)), reads=[B_hs], writes=[B_HS[ob // 128]])
            if dr == 0:
                pend_fin.append(fin)
            else:
                fin()

    emit_loads(0); emit_loads(1)
    for n in range(len(items) + 1):
        if n < len(items):
            si, h, _ = items[n]
            if h == 0:
                emit_load_hs(si)
            if h == 1 and si + 2 < len(steps):
                emit_loads(si + 2)
            emit_A(items[n])
        if n >= 1:
            emit_BC(items[n - 1])
    while pend_fin:
        pend_fin.pop(0)()
    S.barrier()
    if g_["stage"] < 4:
        return
    build_rest3(g_, locals())


def build_rest(env):
    nc = env["nc"]
    S = env["S"]; ps = env["ps"]; Bps = env["Bps"]; arena = env["arena"]
    u2 = env["u2"]; B_u2 = env["B_u2"]; vcol = env["vcol"]; DER = env["DER"]
    ident = env["ident"]; maskf = env["maskf"]; maskb = env["maskb"]; CS = env["CS"]; M2 = env["M2"]; M3 = env["M3"]
    id16 = env["id16_t"]; ones16 = env["ones16_t"]
    B_const = env["B_const"]; B_der = env["B_der"]
    stage = env["stage"]; dbg = env["dbg"]; dbg_tensor = env["dbg_tensor"]
    dscr = env["dscr"]
    H1, AB, PQ, KT, QT, KTOK, VTOK = [env[n] for n in "H1 AB PQ KT QT KTOK VTOK".split()]
    B_H1, B_AB, B_PQ, B_KT, B_QT, B_KTOK, B_VTOK = [env["B_" + n] for n in "H1 AB PQ KT QT KTOK VTOK".split()]
    GROW = dscr("GROW", [36, NT], F32)
    B_GROW = Buf("GROW")
    HS = dscr("HS", [OWN, D], F32)
    B_HS = [Buf("HS%d" % i) for i in range(16)]

    def mm(out, pairs, reads, writes):
        def fn(e):
            n = len(pairs)
            for i, (l, r) in enumerate(pairs):
                ins = e.matmul(out, l, r, start=(i == 0), stop=(i == n - 1))
            return ins
        return S.op("pe", fn, reads=reads, writes=writes)

    NCH = 52
    Rcol = arena.f32(NCH * 4).rearrange("p (c h) -> p c h", h=4)
    Gcol = arena.f32(NCH * 4).rearrange("p (c h) -> p c h", h=4)
    Ecol = arena.f32(NCH * 4).rearrange("p (c h) -> p c h", h=4)
    gend = arena.f32(8 * 35).rearrange("p (q c) -> p q c", q=8)
    acol = arena.f32(8 * 34).rearrange("p (q c) -> p q c", q=8)
    wkcol = arena.f32(8 * 34).rearrange("p (q c) -> p q c", q=8)
    decay = arena.f32(8 * 34).rearrange("p (q c) -> p q c", q=8)
    B_cols = Buf("cols")
    yfT = arena.bf16(4 * OWN).rearrange("p (g t) -> p g t", g=4)
    B_yfT = Buf("yfT")
    C32 = arena.f32(8 * 2 * 257).rearrange("p (q c e) -> p q c e", q=8, c=2)
    C16 = arena.bf16(8 * 2 * 258).rearrange("p (q c e) -> p q c e", q=8, c=2)
    B_C = [Buf("C%d" % i) for i in range(8)]
    sel_t = arena.f32(2048)
    S.dma("sp", lambda e: e.dma_start(out=sel_t[0:36, :], in_=env["selc"]), writes=[B_const])
    sel = sel_t[0:36, 0:1024].rearrange("r (p m) -> r p m", p=8)
    negsel = sel_t[0:36, 1024:2048].rearrange("r (p m) -> r p m", p=8)
    mB = arena.mark()

    aLI = arena.f32(NT)
    aLF = arena.f32(NT)
    aB = arena.f32(NT)
    aG = arena.f32(NT)
    ones_r = arena.f32(512)
    tmpE = arena.f32(512)
    wgf_s = arena.bf16(8 * 8).rearrange("p (k n) -> p k n", k=8)
    wgb_s = arena.bf16(8 * 72).rearrange("p (k n) -> p k n", k=8)
    bgs = arena.f32(4)
    B_rows = Buf("rows")
    B_wg = Buf("wg")
    B_tmpE = Buf("tmpE")
    for a in (aLI, aLF, aB, aG):
        S.op("pool", (lambda e, a=a: e.memset(a, 0.0)), writes=[B_rows])
    S.op("pool", lambda e: e.memset(ones_r, 1.0), writes=[B_wg])
    S.op("pool", lambda e: e.memset(gend[:, :, :], 0.0), writes=[B_cols])
    for q in range(8):
        S.op("pool", (lambda e, q=q: e.memset(C32[:, q, :, :], 0.0)), writes=[B_C[q]])
        S.op("pool", (lambda e, q=q: e.memset(C16[:, q, :, :], 0.0)), writes=[B_C[q]])
    with nc.allow_non_contiguous_dma(reason="tiny gate weights"):
        pass
    wgate_f = env["wgate_f"]; wgate_b = env["wgate_b"]; bg = env["bg"]
    S.dma("pool", lambda e: e.dma_start(out=wgf_s, in_=wgate_f.rearrange("(k p) n -> p k n", p=128)), writes=[B_wg])
    S.dma("pool", lambda e: e.dma_start(out=wgb_s, in_=wgate_b.rearrange("(k p) n -> p k n", p=128)), writes=[B_wg])
    S.dma("sp", lambda e: e.dma_start(out=bgs[0:36, 0:2], in_=bg), writes=[B_wg])
    S.op("dve", lambda e: e.tensor_scalar(bgs[0:36, 2:3], bgs[0:36, 1:2], -1.0, None, ALU.mult), reads=[B_wg], writes=[B_wg])

    def gate_tile(jc0, W, rhs_fn, r0, r1, wl, wl_lf, pbank, rev=False):
        def pv(bank):
            return ps[bank][r0:r1, W - 1::-1] if rev else ps[bank][r0:r1, 0:W]
        mm(ps[pbank][0:r1, 0:W], [(wl(k), rhs_fn(k)) for k in range(8)], reads=[B_wg] + B_u2, writes=[Bps[pbank]])
        S.op("act", lambda e: e.activation(aLI[r0:r1, jc0:jc0 + W], pv(pbank), AF.Identity, bias=bgs[r0:r1, 0:1]),
             reads=[Bps[pbank], B_wg], writes=[B_rows])
        mm(ps[pbank + 1][0:r1, 0:W], [(wl_lf(k), rhs_fn(k)) for k in range(8)], reads=[B_wg] + B_u2, writes=[Bps[pbank + 1]])
        S.op("act", lambda e: e.activation(tmpE[r0:r1, 0:W], pv(pbank + 1), AF.Exp, bias=bgs[r0:r1, 2:3], scale=-1.0),
             reads=[Bps[pbank + 1], B_wg], writes=[B_tmpE])
        S.op("act", lambda e: e.activation(tmpE[r0:r1, 0:W], tmpE[r0:r1, 0:W], AF.Ln, bias=1.0), reads=[B_tmpE], writes=[B_tmpE])
        S.op("dve", lambda e: e.tensor_scalar(aLF[r0:r1, jc0:jc0 + W], tmpE[r0:r1, 0:W], -1.0, None, ALU.mult),
             reads=[B_tmpE], writes=[B_rows])

    ti = 0
    for (jc0, W) in [(0, 256)] + [(256 + 512 * i, 512) for i in range(4)]:
        gate_tile(jc0, W, (lambda k, jc0=jc0, W=W: u2[:, k, jc0:jc0 + W]), 0, 4,
                  (lambda k: wgf_s[:, k, 0:4]), (lambda k: wgf_s[:, k, 4:8]), 2 * (ti % 2))
        ti += 1
    def rev_u2(k, hi, W):
        return u2[:, k, hi - W + 1:hi + 1]
    gate_tile(0, 256, (lambda k: rev_u2(k, 255, 256)), 32, 36,
              (lambda k: wgb_s[:, k, 0:36]), (lambda k: wgb_s[:, k, 36:72]), 2 * (ti % 2), rev=True)
    ti += 1
    for i in range(8):
        jc0 = 256 + 512 * i
        hi = 4607 - jc0
        gate_tile(jc0, 512, (lambda k, hi=hi: rev_u2(k, hi, 512)), 32, 36,
                  (lambda k: wgb_s[:, k, 0:36]), (lambda k: wgb_s[:, k, 36:72]), 2 * (ti % 2), rev=True)
        ti += 1
    pieces = [(0, 256)] + [(256 + 512 * i, 512) for i in range(8)]
    for pi, (c0, W) in enumerate(pieces):
        init = 0.0 if pi == 0 else aB[0:36, c0 - 1:c0]
        S.op("dve", (lambda e, c0=c0, W=W, init=init: e.tensor_tensor_scan(aB[0:36, c0:c0 + W], ones_r[0:36, 0:W], aLF[0:36, c0:c0 + W],
                                                                           init, ALU.mult, ALU.add)),
             reads=[B_rows, B_wg], writes=[B_rows])
    S.op("dve", lambda e: e.tensor_tensor(aLI[0:36, :], aLI[0:36, :], aB[0:36, :], ALU.subtract), reads=[B_rows], writes=[B_rows])
    for pi, (c0, W) in enumerate(pieces):
        init = 0.0 if pi == 0 else aG[0:36, c0 - 1:c0]
        S.op("dve", (lambda e, c0=c0, W=W, init=init: e.tensor_tensor_scan(aG[0:36, c0:c0 + W], ones_r[0:36, 0:W], aLI[0:36, c0:c0 + W],
                                                                           init, ALU.mult, ALU.max)),
             reads=[B_rows, B_wg], writes=[B_rows])
    S.op("dve", lambda e: e.tensor_tensor(aB[0:36, :], aB[0:36, :], aG[0:36, :], ALU.add), reads=[B_rows], writes=[B_rows])
    if dbg:
        d_rows = dbg_tensor("d_rows", [3, 36, NT])
        for i, a in enumerate((aLI, aG, aB)):
            S.dma("sp", (lambda e, i=i, a=a: e.dma_start(out=d_rows[i], in_=a[0:36, :])), reads=[B_rows])
    def n0_of(sc):
        return 128 if sc == 0 else (0 if sc == 1 else 4480 - 128 * sc)
    for ai, (arr, bank) in enumerate(((aLI, 0), (aB, 2), (aG, 1))):
        S.op("dve", (lambda e, arr=arr: e.tensor_copy(aLF[32:36, 0:256], arr[32:36, 255::-1])), reads=[B_rows], writes=[B_rows])
        S.op("dve", (lambda e, arr=arr: e.tensor_copy(aLF[32:36, 256:NT], arr[32:36, NT - 1:255:-1])), reads=[B_rows], writes=[B_rows])
        def fn(e, arr=arr, bank=bank):
            for sc in range(18):
                ins = e.matmul(ps[bank][:, sc * 4:(sc + 1) * 4], arr[0:4, sc * 128:(sc + 1) * 128], ident[0:4, 0:4],
                               start=True, stop=True)
            for sc in range(34):
                n0 = n0_of(sc)
                ins = e.matmul(ps[bank][:, (18 + sc) * 4:(19 + sc) * 4], aLF[32:36, n0:n0 + 128],
                               ident[32:36, 32:36], start=True, stop=True)
            return ins
        S.op("pe", fn, reads=[B_rows, B_const], writes=[Bps[bank]])
    S.op("dve", lambda e: e.tensor_copy(aLF[0:4, :], aG[0:4, :]), reads=[B_rows], writes=[B_rows])
    S.dma("sp", lambda e: e.dma_start(out=GROW, in_=aLF[0:36, :]), reads=[B_rows], writes=[B_GROW])
    S.op("act", lambda e: e.copy(Rcol[:, :, :], ps[0][:, 0:NCH * 4].rearrange("p (c h) -> p c h", h=4)), reads=[Bps[0]], writes=[B_cols])
    S.op("act", lambda e: e.copy(Gcol[:, :, :], ps[1][:, 0:NCH * 4].rearrange("p (c h) -> p c h", h=4)), reads=[Bps[1]], writes=[B_cols])
    S.op("act", lambda e: e.activation(Ecol[:, :, :], ps[2][:, 0:NCH * 4].rearrange("p (c h) -> p c h", h=4), AF.Exp, scale=-1.0),
         reads=[Bps[2]], writes=[B_cols])
    def fn(e):
        for q in range(8):
            n = 18 if q < 4 else 34
            ins = e.matmul(ps[3][:, q * 34:q * 34 + n], sel[0:36, q, :], aG[0:36, 127:127 + 128 * (n - 1) + 1:128], start=True, stop=True)
        return ins
    S.op("pe", fn, reads=[B_rows, B_const], writes=[Bps[3]])
    for q in range(8):
        n = 18 if q < 4 else 34
        S.op("act", (lambda e, q=q, n=n: e.copy(gend[:, q, 1:1 + n], ps[3][:, q * 34:q * 34 + n])), reads=[Bps[3]], writes=[B_cols])
    tq = arena.f32(34)
    B_tq = Buf("tq")
    for q in range(8):
        n = 18 if q < 4 else 34
        base = 0 if q < 4 else 18
        h = q % 4
        S.op("dve", (lambda e, q=q, n=n, base=base, h=h: e.tensor_tensor(tq[:, 0:n], gend[:, q, 0:n], Gcol[:, base:base + n, h], ALU.subtract)),
             reads=[B_cols], writes=[B_tq])
        S.op("act", (lambda e, q=q, n=n: e.activation(acol[:, q, 0:n], tq[:, 0:n], AF.Exp)), reads=[B_tq], writes=[B_cols])
        S.op("dve", (lambda e, q=q, n=n, base=base, h=h: e.tensor_tensor(tq[:, 0:n], Rcol[:, base:base + n, h], gend[:, q, 1:1 + n], ALU.subtract)),
             reads=[B_cols], writes=[B_tq])
        S.op("act", (lambda e, q=q, n=n: e.activation(wkcol[:, q, 0:n], tq[:, 0:n], AF.Exp)), reads=[B_tq], writes=[B_cols])
        S.op("dve", (lambda e, q=q, n=n: e.tensor_tensor(tq[:, 0:n], gend[:, q, 0:n], gend[:, q, 1:1 + n], ALU.subtract)),
             reads=[B_cols], writes=[B_tq])
        S.op("act", (lambda e, q=q, n=n: e.activation(decay[:, q, 0:n], tq[:, 0:n], AF.Exp)), reads=[B_tq], writes=[B_cols])
    S.barrier()
    arena.reset(mB)
    if stage < 3:
        return
    build_rest2(env, locals())


COL_F = 0
COL_Q = 512
COL_K = 1536
COL_V = 2560
COL_O = 3584
COL_GATES = 4608
COL_BR = 4624


def _dft_consts(flip):
    idx = (63 - np.arange(64)) if flip else np.arange(64)
    ang = 2 * np.pi * np.outer(idx, idx) / 64.0
    Cc = np.cos(ang) / 8.0
    Sc = np.sin(ang) / 8.0
    ch = np.arange(128)
    angc = 2 * np.pi * np.outer(ch, ch) / 128.0
    CS = np.concatenate([np.cos(angc), np.sin(angc)], axis=1) / np.sqrt(128.0)
    M2 = np.zeros((128, 128))
    M2[0:64, 0:64] = Cc
    M2[64:128, 0:64] = -Sc
    M2[0:64, 64:128] = Sc
    M2[64:128, 64:128] = Cc
    M3 = np.zeros((128, 32))
    M3[0:64, :] = Cc[:, 0:32]
    M3[64:128, :] = -Sc[:, 0:32]
    return CS, M2, M3


def make_inputs(inp):
    f32 = np.float32
    x = np.asarray(inp["x"], f32)
    ctx = np.asarray(inp["ctx"], f32)
    c = np.asarray(inp["c"], f32)
    c_ctx = np.asarray(inp["c_ctx"], f32)
    w_in = np.asarray(inp["w_in"], f32)[0]
    b_in = np.asarray(inp["b_in"], f32)[0]
    conv_w = np.asarray(inp["conv_w"], f32)[0]
    conv_b = np.asarray(inp["conv_b"], f32)[0]
    norm_g = np.asarray(inp["norm_g"], f32)[0]

    def fm(v):
        return np.ascontiguousarray(v.reshape(-1, 128).T)

    def r13(w):
        return np.ascontiguousarray(w.reshape(8, 128, 2, NJ, 128).transpose(3, 1, 0, 2, 4).reshape(NJ, 128, 2048))

    def r2(w):
        return np.ascontiguousarray(w.reshape(NJ, 128, 8, 128).transpose(2, 1, 0, 3).reshape(8, 128, FF))

    shared = {
        "w_ada": np.ascontiguousarray(np.asarray(inp["w_ada"], f32)[0]),
        "w13a": r13(np.asarray(inp["w13_a"], f32)[0]), "w2a": r2(np.asarray(inp["w2_a"], f32)[0]),
        "w13b": r13(np.asarray(inp["w13_b"], f32)[0]), "w2b": r2(np.asarray(inp["w2_b"], f32)[0]),
        "wF": np.ascontiguousarray(w_in[:, COL_F:COL_Q]), "wq": np.ascontiguousarray(w_in[:, COL_Q:COL_K]),
        "wk": np.ascontiguousarray(w_in[:, COL_K:COL_V]), "wv": np.ascontiguousarray(w_in[:, COL_V:COL_O]),
        "wo": np.ascontiguousarray(w_in[:, COL_O:COL_GATES]),
        "wgf": np.ascontiguousarray(w_in[:, COL_BR:COL_BR + D]), "wgm": np.ascontiguousarray(w_in[:, COL_BR + D:]),
        "w_four": np.ascontiguousarray(np.asarray(inp["w_four"], f32)[0]),
        "w_mproj": np.ascontiguousarray(np.asarray(inp["w_mproj"], f32)[0]),
        "w_out": np.ascontiguousarray(np.asarray(inp["w_out"], f32)[0]),
        "rowv": np.concatenate([b_in[COL_V:COL_O], b_in[COL_O:COL_GATES], np.asarray(inp["head_g"], f32)[0]])[None, :].copy(),
    }
    sel = np.zeros((36, 8, 128), f32)
    for p in range(8):
        row = (p % 4) + (32 if p >= 4 else 0)
        sel[row, p, :] = 1.0
    selc = np.concatenate([sel.reshape(36, -1), -sel.reshape(36, -1)], axis=1)
    s_idx = np.arange(128)[:, None]
    t_idx = np.arange(128)[None, :]
    maskf = np.where(s_idx <= t_idx, 0.0, NEG).astype(f32)
    maskb = np.where(s_idx >= t_idx, 0.0, NEG).astype(f32)
    maps = []
    for core in range(8):
        b, half = core // 2, core % 2
        flip = half == 1
        xb = x[b][::-1] if flip else x[b]
        cb_ = ctx[b][::-1] if flip else ctx[b]
        g = COL_GATES
        if flip:
            gi_f, gf_f, gi_b, gf_b = g + 8, g + 12, g + 0, g + 4
            cw = conv_w[::-1]
        else:
            gi_f, gf_f, gi_b, gf_b = g + 0, g + 4, g + 8, g + 12
            cw = conv_w
        wgate_f = np.concatenate([w_in[:, gi_f:gi_f + 4], w_in[:, gf_f:gf_f + 4]], axis=1)
        wgate_b = np.zeros((D, 72), f32)
        wgate_b[:, 32:36] = w_in[:, gi_b:gi_b + 4]
        wgate_b[:, 36 + 32:36 + 36] = w_in[:, gf_b:gf_b + 4]
        bgv = np.zeros((36, 2), f32)
        bgv[0:4, 0] = b_in[gi_f:gi_f + 4]
        bgv[0:4, 1] = b_in[gf_f:gf_f + 4]
        bgv[32:36, 0] = b_in[gi_b:gi_b + 4]
        bgv[32:36, 1] = b_in[gf_b:gf_b + 4]
        vecs = np.zeros((128, NV), f32)

        def put(name, arr):
            o, w = VEC[name]
            assert arr.shape == (128, w), (name, arr.shape)
            vecs[:, o:o + w] = arr
        put("bada", fm(np.asarray(inp["b_ada"], f32)[0]))
        put("ng", np.concatenate([fm(norm_g[i]) for i in range(6)], axis=1))
        put("bF", fm(b_in[COL_F:COL_Q])); put("bq", fm(b_in[COL_Q:COL_K])); put("bk", fm(b_in[COL_K:COL_V]))
        put("cwq", np.concatenate([fm(cw[t, 0:D]) for t in range(3)], axis=1))
        put("cwk", np.concatenate([fm(cw[t, D:2 * D]) for t in range(3)], axis=1))
        put("cbq", fm(conv_b[0:D])); put("cbk", fm(conv_b[D:2 * D]))
        put("bgf", fm(b_in[COL_BR:COL_BR + D])); put("bgm", fm(b_in[COL_BR + D:]))
        cvec = np.zeros((128, 8, 2), f32)
        cvec[:, :, 0] = fm(c[b])
        cvec[:, :, 1] = fm(c_ctx)
        CS, M2, M3 = _dft_consts(flip)
        cst = np.zeros((128, 128 * 4 + 256 + 128 + 32), f32)
        cst[:, 0:128] = np.eye(128)
        cst[:, 128:256] = maskf
        cst[:, 256:384] = maskb
        cst[:, 512:768] = CS
        cst[:, 768:896] = M2
        cst[:, 896:928] = M3
        m = dict(shared)
        m.update({
            "xT": np.ascontiguousarray(xb.T), "ctxT": np.ascontiguousarray(cb_.T),
            "cvec": cvec.reshape(128, 16), "vecs": vecs, "bg": bgv,
            "wgate_f": np.ascontiguousarray(wgate_f), "wgate_b": wgate_b, "cst": cst, "selc": selc,
        })
        maps.append(m)
    return maps


def kernel(**inputs):
    nc, _ = build()
    maps = make_inputs(inputs)
    res = run_bass_kernel_spmd(nc, maps, core_ids=list(range(8)))
    out = np.zeros((4, SEQ, D), np.float32)
    for core in range(8):
        b, half = core // 2, core % 2
        o = np.asarray(res.results[core]["outT"]).T
        if half == 0:
            out[b, 0:OWN] = o
        else:
            out[b, OWN:] = o[::-1]
    return out
```

```python
import numpy as np
import os as _os
from contextlib import ExitStack
import concourse.bass as bass
import concourse.mybir as mybir
from concourse.bass_utils import run_bass_kernel_spmd

F32 = mybir.dt.float32
BF16 = mybir.dt.bfloat16
AF = mybir.ActivationFunctionType
ALU = mybir.AluOpType

ENGS = ("pe", "act", "dve", "pool", "sp")
EPOCH = 30000

D = 1024
SEQ = 4096
CTX = 256
NT = CTX + SEQ
OWN = 2048
FF = 2816
NJ = 22
EPS = 1e-6
NEG = -30000.0


class Tick:
    __slots__ = ("sem", "val", "know")

    def __init__(self, sem, val, know):
        self.sem = sem
        self.val = val
        self.know = know


class Buf:
    __slots__ = ("name", "w", "r")

    def __init__(self, name=""):
        self.name = name
        self.w = None
        self.r = {}


class Sched:
    def __init__(self, nc, stack, n_dma_sems=48):
        self.nc = nc
        self.stack = stack
        self.q = {e: [] for e in ENGS}
        self.cnt = {e: 0 for e in ENGS}
        self.esems = {e: [] for e in ENGS}
        self.known = {e: {} for e in ENGS}
        self.dsems = [stack.enter_context(nc.semaphore(f"dma{i}")) for i in range(n_dma_sems)]
        self.dcnt = [0] * n_dma_sems
        self.dlast = [None] * n_dma_sems
        self.drr = 0
        self.drr2 = {}

    def _esem(self, eng, idx):
        lst = self.esems[eng]
        while len(lst) <= idx:
            lst.append(self.stack.enter_context(self.nc.semaphore(f"e_{eng}_{len(lst)}")))
        return lst[idx]

    def _collect(self, eng, reads, writes, extra=()):
        kn = self.known[eng]
        waits = {}

        def need(t):
            if t is None:
                return
            if kn.get(t.sem, 0) >= t.val:
                return
            if waits.get(t.sem, (0, None))[0] < t.val:
                waits[t.sem] = (t.val, t)

        for b in reads:
            need(b.w)
        for b in writes:
            need(b.w)
            for t in b.r.values():
                need(t)
        for t in extra:
            need(t)
        items = sorted(waits.items(), key=lambda kv: -kv[1][0])
        final = []
        for sem, (val, t) in items:
            if kn.get(sem, 0) >= val:
                continue
            final.append((sem, val))
            kn[sem] = val
            for s2, v2 in t.know.items():
                if kn.get(s2, 0) < v2:
                    kn[s2] = v2
        return final

    def op(self, eng, fn, reads=(), writes=(), extra=()):
        waits = self._collect(eng, reads, writes, extra)
        c = self.cnt[eng]
        sem = self._esem(eng, c // EPOCH)
        val = c % EPOCH + 1
        self.cnt[eng] = c + 1
        t = Tick(sem, val, dict(self.known[eng]))
        for b in reads:
            b.r[sem] = t
        for b in writes:
            b.w = t
            b.r = {}
        self.q[eng].append((waits, fn, sem, 1))
        return t

    def dma(self, eng, fn, reads=(), writes=(), extra=()):
        n = len(self.dsems)
        lo, hi = (0, n // 3) if eng == "pool" else (n // 3, n)
        rr = self.drr2.get(eng, lo)
        i = rr
        self.drr2[eng] = lo + (rr + 1 - lo) % (hi - lo)
        ex = list(extra)
        if self.dlast[i] is not None:
            ex.append(self.dlast[i])
        waits = self._collect(eng, reads, writes, ex)
        self.dcnt[i] += 1
        sem = self.dsems[i]
        t = Tick(sem, 16 * self.dcnt[i], dict(self.known[eng]))
        self.dlast[i] = t
        for b in reads:
            b.r[sem] = t
        for b in writes:
            b.w = t
            b.r = {}
        self.q[eng].append((waits, fn, sem, 16))
        return t

    def wait_all(self, eng, ticks):
        waits = self._collect(eng, (), (), ticks)
        self.q[eng].append((waits, None, None, 0))

    def barrier(self, skip=()):
        skipset = {(t.sem, t.val) for t in skip}
        ticks = []
        for e in ENGS:
            c = self.cnt[e]
            if c > 0:
                ticks.append(Tick(self._esem(e, (c - 1) // EPOCH), (c - 1) % EPOCH + 1, {}))
        for t in self.dlast:
            if t is not None and (t.sem, t.val) not in skipset:
                ticks.append(t)
        for e in ENGS:
            self.wait_all(e, ticks)

    def emit(self):
        nc = self.nc
        q = self.q

        def run(engobj, lst):
            for waits, fn, sem, amt in lst:
                for s, v in waits:
                    engobj.wait_ge(s, v)
                if fn is not None:
                    ins = fn(engobj)
                    ins.then_inc(sem, amt)

        with nc.Block() as block:
            @block.tensor
            def _(e):
                run(e, q["pe"])

            @block.scalar
            def _(e):
                run(e, q["act"])

            @block.vector
            def _(e):
                run(e, q["dve"])

            @block.gpsimd
            def _(e):
                run(e, q["pool"])

            @block.sync
            def _(e):
                run(e, q["sp"])


class Arena:
    def __init__(self, nc, name, words):
        self.t = nc.alloc_sbuf_tensor(name, [128, words], F32)
        self.words = words
        self.off = 0

    def mark(self):
        return self.off

    def reset(self, m):
        self.off = m

    def f32(self, n):
        a = self.t[:, self.off:self.off + n]
        self.off += n
        assert self.off <= self.words, ("arena overflow", self.off, self.words)
        return a

    def bf16(self, n):
        w = (n + 1) // 2
        a = self.t[:, self.off:self.off + w].bitcast(BF16)
        self.off += w
        assert self.off <= self.words, ("arena overflow", self.off, self.words)
        return a[:, 0:n]


VEC = {}
_o = 0
for _n, _w in [("bada", 72), ("ng", 48), ("bF", 4), ("bq", 8), ("bk", 8), ("cwq", 24), ("cwk", 24),
               ("cbq", 8), ("cbk", 8), ("bgf", 8), ("bgm", 8)]:
    VEC[_n] = (_o, _w)
    _o += _w
NV = _o


def build(stage=99, dbg=False):
    nc = bass.Bass("TRN2", target_bir_lowering=False)
    dt_in = lambda name, shape, dt=F32: nc.dram_tensor(name, shape, dt, kind="ExternalInput").ap()
    xT = dt_in("xT", [D, SEQ])
    ctxT = dt_in("ctxT", [D, CTX])
    cvec = dt_in("cvec", [128, 16])
    w_ada = dt_in("w_ada", [D, 9 * D])
    vecs = dt_in("vecs", [128, NV])
    rowv = dt_in("rowv", [1, 3 * D])
    bg = dt_in("bg", [36, 2])
    w13a = dt_in("w13a", [NJ, 128, 2048])
    w2a = dt_in("w2a", [8, 128, FF])
    w13b = dt_in("w13b", [NJ, 128, 2048])
    w2b = dt_in("w2b", [8, 128, FF])
    wF = dt_in("wF", [D, 512])
    wq = dt_in("wq", [D, D])
    wk = dt_in("wk", [D, D])
    wv = dt_in("wv", [D, D])
    wo = dt_in("wo", [D, D])
    wgf = dt_in("wgf", [D, D])
    wgm = dt_in("wgm", [D, D])
    wgate_f = dt_in("wgate_f", [D, 8])
    wgate_b = dt_in("wgate_b", [D, 72])
    w_four = dt_in("w_four", [512, D])
    w_mproj = dt_in("w_mproj", [D, D])
    w_out = dt_in("w_out", [D, D])
    cst = dt_in("cst", [128, 128 * 4 + 256 + 128 + 32])
    selc = dt_in("selc", [36, 2 * 8 * 128])
    outT = nc.dram_tensor("outT", [D, OWN], F32, kind="ExternalOutput").ap()
    dbg_out = {}

    def dbg_tensor(name, shape, dt=F32):
        dbg_out[name] = nc.dram_tensor(name, shape, dt, kind="ExternalOutput").ap()
        return dbg_out[name]

    dscr = lambda name, shape, dt: nc.dram_tensor(name, shape, dt, kind="Internal").ap()
    H1 = dscr("H1", [D, OWN], F32)
    AB = dscr("AB", [2, SEQ, 512], F32)
    PQ = dscr("PQ", [2, 64, 64, 512], F32)
    KT = dscr("KT", [D, OWN], BF16)
    QT = dscr("QT", [D, OWN], BF16)
    KTOK = dscr("KTOK", [NT, D], BF16)
    VTOK = dscr("VTOK", [NT, D], BF16)
    B_H1, B_AB, B_PQ, B_KT, B_QT, B_KTOK, B_VTOK = [Buf(n) for n in "H1 AB PQ KT QT KTOK VTOK".split()]

    st = ExitStack()
    with st:
        S = Sched(nc, st)
        ps = [st.enter_context(nc.psum_tensor(f"ps{i}", [128, 512], F32)) for i in range(8)]
        Bps = [Buf(f"ps{i}") for i in range(8)]

        cs_t = nc.alloc_sbuf_tensor("cs", [128, 128 * 4 + 256 + 128 + 32], F32)
        ident = cs_t[:, 0:128]
        maskf = cs_t[:, 128:256]
        maskb = cs_t[:, 256:384]
        CS = cs_t[:, 512:768]
        M2 = cs_t[:, 768:896]
        M3 = cs_t[:, 896:928]
        vec_t = nc.alloc_sbuf_tensor("vec", [128, NV], F32)
        mod_t = nc.alloc_sbuf_tensor("mod", [128, 72 * 2], F32)
        der_t = nc.alloc_sbuf_tensor("der", [128, 16 * 8], F32)
        id16_t = nc.alloc_sbuf_tensor("id16", [128, 128], BF16)
        ones16_t = nc.alloc_sbuf_tensor("ones16", [128, 128], BF16)
        u2_t = nc.alloc_sbuf_tensor("u2", [128, 8 * NT], BF16)
        u2 = u2_t[:, :].rearrange("p (k t) -> p k t", k=8)
        B_const = Buf("const")
        B_mod = Buf("mod")
        B_der = Buf("der")
        tiles = [(0, CTX, 1)] + [(CTX + 512 * i, 512, 0) for i in range(8)]
        B_u2 = [Buf(f"u2_{i}") for i in range(9)]

        def vcol(name, i=0, n=1):
            o, w = VEC[name]
            return vec_t[:, o + i:o + i + n]

        DER = {}
        _d = 0
        for nm in ["A0l", "A0c", "S0l", "S0c", "PAl", "PAc", "A2l", "A2c", "S2l", "S2c", "PMl", "A4l", "S4l", "PBl"]:
            DER[nm] = der_t[:, _d * 8:(_d + 1) * 8]
            _d += 1

        arena = Arena(nc, "arena", 34000)

        S.dma("sp", lambda e: e.dma_start(out=cs_t[:, :], in_=cst), writes=[B_const])
        S.dma("sp", lambda e: e.dma_start(out=vec_t[:, :], in_=vecs), writes=[B_const])
        S.op("act", lambda e: e.copy(id16_t[:, :], ident), reads=[B_const], writes=[B_const])
        S.op("pool", lambda e: e.memset(ones16_t[:, :], 1.0), writes=[B_const])

        m0 = arena.mark()
        cv = arena.f32(16)
        scv = arena.f32(16)
        B_cv = Buf("cv")
        S.dma("sp", lambda e: e.dma_start(out=cv, in_=cvec), writes=[B_cv])
        S.op("act", lambda e: e.activation(scv, cv, AF.Silu), reads=[B_cv], writes=[B_cv])
        wad = [arena.f32(8 * 1024) for _ in range(2)]
        B_wad = [Buf("wad0"), Buf("wad1")]
        w_ada_v = w_ada.rearrange("(k p) n -> p k n", p=128)
        modps = ps[7][:, 0:144]
        for mi in range(9):
            sl = mi % 2
            wv_ = wad[sl].rearrange("p (k n) -> p k n", k=8)
            S.dma("sp", (lambda e, wv_=wv_, mi=mi: e.dma_start(out=wv_, in_=w_ada_v[:, :, mi * 1024:(mi + 1) * 1024])),
                  writes=[B_wad[sl]])
            for dc in range(8):
                def fn(e, wv_=wv_, mi=mi, dc=dc):
                    for k in range(8):
                        ins = e.matmul(modps[:, (mi * 8 + dc) * 2:(mi * 8 + dc) * 2 + 2],
                                       wv_[:, k, dc * 128:(dc + 1) * 128],
                                       scv[:, k * 2:k * 2 + 2], start=(k == 0), stop=(k == 7))
                    return ins
                S.op("pe", fn, reads=[B_wad[sl], B_cv], writes=[Bps[7]])
        modv = mod_t[:, :].rearrange("p (m j) -> p m j", j=2)
        modpsv = modps.rearrange("p (m j) -> p m j", j=2)
        bada = vcol("bada", 0, 72)
        for j in range(2):
            S.op("dve", (lambda e, j=j: e.tensor_tensor(modv[:, :, j], modpsv[:, :, j], bada, ALU.add)),
                 reads=[Bps[7], B_const], writes=[B_mod])

        def modc(mi, j):
            return modv[:, mi * 8:(mi + 1) * 8, j]

        def ng(i):
            return vcol("ng", i * 8, 8)

        def der_scale(name, mi, gi, j):
            S.op("dve", lambda e: e.scalar_tensor_tensor(DER[name], modc(mi, j), 1.0, ng(gi), ALU.add, ALU.mult),
                 reads=[B_mod, B_const], writes=[B_der])

        def der_gate(name, mi, gi, j, f):
            S.op("dve", lambda e: e.scalar_tensor_tensor(DER[name], modc(mi, j), f, ng(gi), ALU.mult, ALU.mult),
                 reads=[B_mod, B_const], writes=[B_der])

        def der_copy(name, mi, j):
            S.op("dve", lambda e: e.tensor_copy(DER[name], modc(mi, j)), reads=[B_mod], writes=[B_der])

        der_scale("A0l", 1, 0, 0); der_scale("A0c", 1, 0, 1)
        der_copy("S0l", 0, 0); der_copy("S0c", 0, 1)
        der_gate("PAl", 2, 1, 0, 0.5); der_gate("PAc", 2, 1, 1, 0.5)
        der_scale("A2l", 4, 2, 0); der_scale("A2c", 4, 2, 1)
        der_copy("S2l", 3, 0); der_copy("S2c", 3, 1)
        der_gate("PMl", 5, 3, 0, 1.0)
        der_scale("A4l", 7, 4, 0); der_copy("S4l", 6, 0); der_gate("PBl", 8, 5, 0, 0.5)
        S.barrier()
        arena.reset(m0)

        def rstd_from(sq_tile, B_sq, W, rstd, B_rstd, pbank):
            def fn(e):
                for k in range(8):
                    ins = e.matmul(ps[pbank][:, 0:W], ones16_t[:, :], sq_tile[:, k, 0:W], start=(k == 0), stop=(k == 7))
                return ins
            S.op("pe", fn, reads=[B_sq, B_const], writes=[Bps[pbank]])
            S.op("act", lambda e: e.activation(rstd[:, 0:W], ps[pbank][:, 0:W], AF.Ln, bias=EPS, scale=1.0 / D),
                 reads=[Bps[pbank]], writes=[B_rstd])
            S.op("act", lambda e: e.activation(rstd[:, 0:W], rstd[:, 0:W], AF.Exp, scale=-0.5),
                 reads=[B_rstd], writes=[B_rstd])

        def norm_mod_thunks(src, B_src, W, rstd, B_rstd, A, Sh, dst_fn, B_dst, tmp, B_tmp):
            def one(k):
                t = tmp[k % 2]
                bt = B_tmp[k % 2]
                S.op("dve", (lambda e: e.tensor_tensor(t[:, 0:W], src[:, k, 0:W], rstd[:, 0:W], ALU.mult)),
                     reads=[B_src, B_rstd], writes=[bt])
                S.op("act", (lambda e: e.activation(dst_fn(k), t[:, 0:W], AF.Identity, bias=Sh[:, k:k + 1], scale=A[:, k:k + 1])),
                     reads=[bt, B_der], writes=[B_dst])
            return [(lambda k=k: one(k)) for k in range(8)]

        def norm_mod(src, B_src, W, rstd, B_rstd, A, Sh, dst_fn, B_dst, tmp, B_tmp):
            for k in range(8):
                t = tmp[k % 2]
                bt = B_tmp[k % 2]
                S.op("dve", (lambda e, k=k, t=t: e.tensor_tensor(t[:, 0:W], src[:, k, 0:W], rstd[:, 0:W], ALU.mult)),
                     reads=[B_src, B_rstd], writes=[bt])
                S.op("act", (lambda e, k=k, t=t: e.activation(dst_fn(k), t[:, 0:W], AF.Identity,
                                                              bias=Sh[:, k:k + 1], scale=A[:, k:k + 1])),
                     reads=[bt, B_der], writes=[B_dst])

        WC = {}

        def wload(dst, B_dst, src_f32, cache, key, ncols):
            if cache is None:
                S.dma("pool", lambda e: e.dma_start(out=dst, in_=src_f32), writes=[B_dst])
                return
            k = (cache,) + key
            if k not in WC:
                sc_ = nc.dram_tensor("wc_" + "_".join(str(z) for z in k), [128, ncols], BF16, kind="Internal").ap()
                WC[k] = (sc_, Buf("wc"))
                S.dma("pool", lambda e: e.dma_start(out=dst, in_=src_f32), writes=[B_dst])
                dflat = dst if len(dst.shape) == 2 else dst.rearrange("p a b -> p (a b)")
                S.dma("sp", lambda e: e.dma_start(out=sc_, in_=dflat), reads=[B_dst], writes=[WC[k][1]])
            else:
                sc_, bsc = WC[k]
                dflat = dst if len(dst.shape) == 2 else dst.rearrange("p a b -> p (a b)")
                S.dma("sp", lambda e: e.dma_start(out=dflat, in_=sc_), reads=[bsc], writes=[B_dst])

        def ffn(src, B_src, W, rstd_pre, B_rstd_pre, A, Sh, PG, w13r, w2r, bufs, cache, do_pre=True, hook1=None, hook2=None, defer_epi=False):
            (sq, B_sq, u, B_u, g, B_g, y, B_y, tmp, B_tmp, rs2, B_rs2, wb13, B_wb13, wb2, B_wb2, sa, B_sa) = bufs
            if do_pre:
                norm_mod(src, B_src, W, rstd_pre, B_rstd_pre, A, Sh, lambda k: u[:, k, 0:W], B_u, tmp, B_tmp)
            n13 = len(wb13)
            for j in range(NJ):
                sl = j % n13
                wbv = wb13[sl].rearrange("p (k c) -> p k c", k=8)
                wload(wb13[sl], B_wb13[sl], w13r[j], cache, ("w13", j), 2048)
                pa = j % 2
                pb = 2 + j % 2

                def fa(e, wbv=wbv, pa=pa):
                    for k in range(8):
                        ins = e.matmul(ps[pa][:, 0:W], wbv[:, k, 0:128], u[:, k, 0:W], start=(k == 0), stop=(k == 7))
                    return ins

                def fb(e, wbv=wbv, pb=pb):
                    for k in range(8):
                        ins = e.matmul(ps[pb][:, 0:W], wbv[:, k, 128:256], u[:, k, 0:W], start=(k == 0), stop=(k == 7))
                    return ins
                S.op("pe", fa, reads=[B_wb13[sl], B_u], writes=[Bps[pa]])
                S.op("pe", fb, reads=[B_wb13[sl], B_u], writes=[Bps[pb]])
                s2 = j % 2
                S.op("act", (lambda e, pa=pa, s2=s2: e.activation(sa[s2][:, 0:W], ps[pa][:, 0:W], AF.Silu)),
                     reads=[Bps[pa]], writes=[B_sa[s2]])
                S.op("dve", (lambda e, pb=pb, s2=s2, j=j: e.tensor_tensor(g[:, j, 0:W], sa[s2][:, 0:W], ps[pb][:, 0:W], ALU.mult)),
                     reads=[B_sa[s2], Bps[pb]], writes=[B_g])
                if hook1 is not None:
                    hook1(j)
            n2 = len(wb2)
            HJ = NJ // 2
            for i in range(8):
                halves = []
                for hf in range(2):
                    sl = (2 * i + hf) % n2
                    wload(wb2[sl], B_wb2[sl], w2r[i][:, hf * HJ * 128:(hf + 1) * HJ * 128], cache, ("w2", i, hf), HJ * 128)
                    halves.append((wb2[sl].rearrange("p (j c) -> p j c", j=HJ), B_wb2[sl]))
                py = 4 + i % 2

                def fy(e, halves=halves, py=py):
                    for j in range(NJ):
                        wv_ = halves[j // HJ][0]
                        ins = e.matmul(ps[py][:, 0:W], wv_[:, j % HJ, :], g[:, j, 0:W], start=(j == 0), stop=(j == NJ - 1))
                    return ins
                S.op("pe", fy, reads=[halves[0][1], halves[1][1], B_g], writes=[Bps[py]])
                S.op("act", (lambda e, py=py, i=i: e.copy(y[:, i, 0:W], ps[py][:, 0:W])), reads=[Bps[py]], writes=[B_y])
                S.op("act", (lambda e, py=py, i=i: e.activation(sq[:, i, 0:W], ps[py][:, 0:W], AF.Square)),
                     reads=[Bps[py]], writes=[B_sq])
                if hook2 is not None:
                    hook2(i)
            rstd_from(sq, B_sq, W, rs2, B_rs2, 7)

            def resid(i):
                t = tmp[i % 2]
                bt = B_tmp[i % 2]
                S.op("dve", (lambda e: e.tensor_tensor(t[:, 0:W], y[:, i, 0:W], rs2[:, 0:W], ALU.mult)),
                     reads=[B_y, B_rs2], writes=[bt])
                S.op("dve", (lambda e: e.scalar_tensor_tensor(src[:, i, 0:W], t[:, 0:W], PG[:, i:i + 1],
                                                              src[:, i, 0:W], ALU.mult, ALU.add)),
                     reads=[bt, B_der], writes=[B_src])
            thunks = [(lambda i=i: resid(i)) for i in range(8)]
            if defer_epi:
                return thunks
            for th in thunks:
                th()
            return []

        def alloc_ffn_bufs():
            sq = arena.bf16(8 * 512).rearrange("p (k t) -> p k t", k=8)
            u = arena.bf16(8 * 512).rearrange("p (k t) -> p k t", k=8)
            g = arena.bf16(NJ * 512).rearrange("p (k t) -> p k t", k=NJ)
            y = arena.f32(8 * 512).rearrange("p (k t) -> p k t", k=8)
            tmp = [arena.f32(512) for _ in range(2)]
            rs2 = arena.f32(512)
            wb13 = [arena.bf16(2048) for _ in range(3)]
            wb2 = [arena.bf16(FF // 2) for _ in range(4)]
            sa = [arena.f32(512) for _ in range(2)]
            return (sq, Buf("sq"), u, Buf("u"), g, Buf("g"), y, Buf("y"), tmp, [Buf("t0"), Buf("t1")],
                    rs2, Buf("rs2"), wb13, [Buf("wb13_%d" % i) for i in range(3)], wb2, [Buf("wb2_%d" % i) for i in range(4)],
                    sa, [Buf("sa0"), Buf("sa1")])

        mA = arena.mark()
        xt = [arena.f32(8 * 512).rearrange("p (k t) -> p k t", k=8) for _ in range(2)]
        B_xt = [Buf("xt0"), Buf("xt1")]
        rs1 = arena.f32(512)
        B_rs1 = Buf("rs1")
        fb = alloc_ffn_bufs()
        tmp, B_tmp, u_, B_u_ = fb[8], fb[9], fb[2], fb[3]
        sqx = arena.bf16(8 * 512).rearrange("p (k t) -> p k t", k=8)
        B_sqx = Buf("sqx")
        xT_v = xT.rearrange("(k p) t -> p k t", p=128)
        ctxT_v = ctxT.rearrange("(k p) t -> p k t", p=128)
        if dbg:
            d_u2 = dbg_tensor("d_u2", [128, 8 * NT], BF16)
        ntile = len(tiles) if stage >= 1 else 0

        def a1_load(ti):
            c0, W, j = tiles[ti]
            x = xt[ti % 2]
            src = ctxT_v[:, :, 0:W] if j == 1 else xT_v[:, :, c0 - CTX:c0 - CTX + W]
            S.dma("pool", lambda e: e.dma_start(out=x[:, :, 0:W], in_=src), writes=[B_xt[ti % 2]])

        def sq_thunks(x, bx, W):
            return [(lambda k=k: S.op("act", (lambda e: e.activation(sqx[:, k, 0:W], x[:, k, 0:W], AF.Square)), reads=[bx], writes=[B_sqx]))
                    for k in range(8)]

        def a1_pre_thunks(ti):
            c0, W, j = tiles[ti]
            x = xt[ti % 2]; bx = B_xt[ti % 2]
            sfx = "c" if j == 1 else "l"
            th = sq_thunks(x, bx, W)
            th.append(lambda: rstd_from(sqx, B_sqx, W, rs1, B_rs1, 6))
            th += norm_mod_thunks(x, bx, W, rs1, B_rs1, DER["A0" + sfx], DER["S0" + sfx], lambda k: u_[:, k, 0:W], B_u_, tmp, B_tmp)
            return th

        def a1_epi2_thunks(ti):
            c0, W, j = tiles[ti]
            x = xt[ti % 2]; bx = B_xt[ti % 2]
            sfx = "c" if j == 1 else "l"
            th = []
            if 1 <= ti <= 4:
                o0 = c0 - CTX
                th.append(lambda: S.dma("pool", lambda e: e.dma_start(out=H1.rearrange("(k p) t -> p k t", p=128)[:, :, o0:o0 + 512], in_=x[:, :, :]),
                                        reads=[bx], writes=[B_H1]))
            th += sq_thunks(x, bx, W)
            th.append(lambda: rstd_from(sqx, B_sqx, W, rs1, B_rs1, 6))
            th += norm_mod_thunks(x, bx, W, rs1, B_rs1, DER["A2" + sfx], DER["S2" + sfx], (lambda k: u2[:, k, c0:c0 + W]), B_u2[ti], tmp, B_tmp)
            return th

        pend1 = []
        pend2 = []
        if ntile:
            a1_load(0)
            for th in a1_pre_thunks(0):
                th()
            if ntile > 1:
                a1_load(1)
        for ti in range(ntile):
            c0, W, j = tiles[ti]
            sfx = "c" if j == 1 else "l"
            if ti + 1 < ntile:
                pend2 = a1_pre_thunks(ti + 1)

            def hook1(j_):
                n = 2 if len(pend1) > (NJ - 1 - j_) else 1
                for _ in range(n):
                    if pend1:
                        pend1.pop(0)()

            def hook2(i_):
                while pend1:
                    pend1.pop(0)()
                if i_ >= 1:
                    for _ in range(3):
                        if pend2:
                            pend2.pop(0)()
            epi1 = ffn(xt[ti % 2], B_xt[ti % 2], W, None, None, None, None, DER["PA" + sfx], w13a, w2a, fb, "A",
                       do_pre=False, hook1=hook1, hook2=hook2, defer_epi=True)
            while pend1:
                pend1.pop(0)()
            while pend2:
                pend2.pop(0)()
            pend1 = list(epi1) + a1_epi2_thunks(ti)
            if ti + 2 < ntile:
                pend1.append(lambda ti=ti: a1_load(ti + 2))
        while pend1:
            pend1.pop(0)()
        S.barrier()
        arena.reset(mA)
        if dbg:
            S.dma("sp", lambda e: e.dma_start(out=d_u2, in_=u2_t[:, :]), reads=B_u2)
            d_h1 = dbg_tensor("d_h1", [D, OWN])
            S.dma("sp", lambda e: e.dma_start(out=d_h1, in_=H1), reads=[B_H1])

        env = dict(locals())
        if stage >= 2:
            build_rest(env)
        S.barrier()
        S.emit()
    return nc, dbg_out


def build_rest3(g_, L):
    AX = mybir.AxisListType
    nc = g_["nc"]; S = g_["S"]; ps = g_["ps"]; Bps = g_["Bps"]; arena = g_["arena"]
    u2 = g_["u2"]; B_u2 = g_["B_u2"]; vcol = g_["vcol"]; DER = g_["DER"]
    id16 = g_["id16"]; B_const = g_["B_const"]; mm = g_["mm"]
    H1, HS = g_["H1"], g_["HS"]; B_H1 = g_["B_H1"]; B_HS = g_["B_HS"]
    yfT, B_yfT, mB = g_["yfT"], g_["B_yfT"], g_["mB"]
    rowv = g_["rowv"]; outT = g_["outT"]
    ffn = g_["ffn"]; rstd_from = g_["rstd_from"]; wload = g_["wload"]
    kp = lambda ap: ap.rearrange("(k p) n -> p k n", p=128)
    A = arena.t
    arena.reset(mB)
    tmp = [A[:, 0:512], A[:, 512:1024]]; B_tmp = [Buf("ct0"), Buf("ct1")]
    rs2 = A[:, 1024:1536]; B_rs2 = Buf("crs2")
    yreg = A[:, 5816:9912]
    B_yreg = Buf("yreg")
    y = yreg.rearrange("p (k t) -> p k t", k=8)
    hs_t = yreg[:, 0:1024]; o_sb = yreg[:, 1024:2048]; sig = yreg[:, 2048:3072]; sqh = yreg[:, 3072:4096]
    B_hsC = Buf("c_hs"); B_osb = Buf("c_osb"); B_sig = Buf("c_sig"); B_sqh = Buf("c_sqh")
    bo16 = A[:, 5816 + 1024:5816 + 1536].bitcast(BF16)
    ones16 = g_["ones16"]
    bo_bc = A[:, 9912:10936]; hg_bc = A[:, 10936:11960]
    B_bc = Buf("cbc")
    x = arena.f32(4096).rearrange("p (k t) -> p k t", k=8); B_x = Buf("cx")
    rs1 = arena.f32(512); B_rs1 = Buf("crs1")
    sq = arena.bf16(4096).rearrange("p (k t) -> p k t", k=8); B_sq = Buf("csq")
    u = arena.bf16(4096).rearrange("p (k t) -> p k t", k=8); B_u = Buf("cu")
    hmT = sq; yT = u
    sa = [arena.f32(512), arena.f32(512)]; B_sa = [Buf("csa0"), Buf("csa1")]
    mX = arena.mark()
    g = arena.bf16(NJ * 512).rearrange("p (k t) -> p k t", k=NJ); B_g = Buf("cg")
    wb13m = [arena.bf16(2048), arena.bf16(2048), arena.bf16(2048)]; B_wb13m = [Buf("cw13_0"), Buf("cw13_1"), Buf("cw13_2")]
    wb2m = arena.bf16(FF); B_wb2m = Buf("cw2")
    wb2x = A[:, 11992:11992 + 704].bitcast(BF16), A[:, 11992 + 704:11992 + 1408].bitcast(BF16)
    arena.reset(mX)
    wsl = [arena.bf16(8 * 512).rearrange("p (k n) -> p k n", k=8) for _ in range(4)]; B_wsl = [Buf("wsl%d" % i) for i in range(4)]
    hm = arena.bf16(1024); B_hm = Buf("hm")
    ol = A[:, mX + 4096:mX + 8192].rearrange("p (k t) -> p k t", k=8)
    B_ol = [B_wsl[2], B_wsl[3]]
    ss = rs2[:, 0:8]
    fb = (sq, B_sq, u, B_u, g, B_g, y, B_yreg, tmp, B_tmp, rs2, B_rs2, wb13m, B_wb13m,
          [wb2m[:, 0:FF // 2], wb2m[:, FF // 2:FF], wb2x[0], wb2x[1]], [Buf('cw2a'), Buf('cw2b'), Buf('cw2c'), Buf('cw2d')], sa, B_sa)
    xflat = A[:, mB:mB + 4096]
    rv = xflat[:, 0:3072]; ones1 = xflat[:, 3072:3200]
    S.dma("sp", lambda e: e.dma_start(out=rv[0:1, :], in_=rowv), writes=[B_x])
    S.op("pool", lambda e: e.memset(ones1[0:1, :], 1.0), writes=[B_x])
    for (dst, seg) in ((bo_bc, 1), (hg_bc, 2)):
        for hh in range(2):
            mm(ps[6][:, 0:512], [(ones1[0:1, :], rv[0:1, seg * 1024 + hh * 512:seg * 1024 + hh * 512 + 512])], reads=[B_x], writes=[Bps[6]])
            S.op("act", (lambda e, hh=hh, dst=dst: e.copy(dst[:, hh * 512:(hh + 1) * 512], ps[6][:, 0:512])), reads=[Bps[6]], writes=[B_bc])
    S.barrier()
    w_four, w_mproj, w_out, wo, wgf, wgm = [g_[n] for n in "w_four w_mproj w_out wo wgf wgm".split()]
    w13b, w2b = g_["w13b"], g_["w2b"]
    H1v = H1.rearrange("(k p) t -> p k t", p=128)
    outv = outT.rearrange("(k p) t -> p k t", p=128)
    for T in range(4):
        t0 = 512 * T
        c0 = CTX + t0
        S.op("act", lambda e: e.copy(bo16[0:1, :], bo_bc[0:1, :]), reads=[B_bc], writes=[B_osb])
        for hh in range(2):
            wload(wsl[hh], B_wsl[hh], kp(wo)[:, :, hh * 512:(hh + 1) * 512], "C", ("wo", hh), 4096)
        for ch in range(4):
            cc = c0 + 128 * ch
            ob = t0 + 128 * ch
            S.dma("sp", (lambda e, ob=ob: e.dma_start(out=hs_t, in_=HS[ob:ob + 128, :])), reads=[B_HS[ob // 128]], writes=[B_hsC])
            S.op("act", lambda e: e.activation(sqh, hs_t, AF.Square), reads=[B_hsC], writes=[B_sqh])
            S.op("dve", lambda e: e.reduce_sum(ss[:, 0:4], sqh.rearrange("p (h e) -> p h e", h=4), AX.X), reads=[B_sqh], writes=[B_rs2])
            for hh in range(2):
                mm(ps[hh][:, 0:512], [(u2[:, k, cc:cc + 128], wsl[hh][:, k, :]) for k in range(8)]
                   + [(ones16[0:1, 0:128], bo16[0:1, hh * 512:(hh + 1) * 512])],
                   reads=[B_wsl[hh], B_osb, B_const] + B_u2, writes=[Bps[hh]])
                S.op("act", (lambda e, hh=hh: e.activation(sig[:, hh * 512:(hh + 1) * 512], ps[hh][:, 0:512], AF.Sigmoid)),
                     reads=[Bps[hh]], writes=[B_sig])
            S.op("dve", lambda e: e.tensor_tensor(sig, sig, hg_bc, ALU.mult), reads=[B_sig, B_bc], writes=[B_sig])
            S.op("act", lambda e: e.activation(ss[:, 0:4], ss[:, 0:4], AF.Ln, bias=EPS, scale=1.0 / 256), reads=[B_rs2], writes=[B_rs2])
            S.op("act", lambda e: e.activation(ss[:, 0:4], ss[:, 0:4], AF.Exp, scale=-0.5), reads=[B_rs2], writes=[B_rs2])
            for h in range(4):
                S.op("dve", (lambda e, h=h: e.scalar_tensor_tensor(hm[:, h * 256:(h + 1) * 256], hs_t[:, h * 256:(h + 1) * 256], ss[:, h:h + 1],
                                                                  sig[:, h * 256:(h + 1) * 256], ALU.mult, ALU.mult)),
                     reads=[B_hsC, B_sig, B_rs2], writes=[B_hm])
            for half in range(2):
                pb = 2 + half
                def fn(e, half=half, pb=pb):
                    for cq in range(4):
                        c = half * 4 + cq
                        ins = e.matmul(ps[pb][:, cq * 128:(cq + 1) * 128], hm[:, c * 128:(c + 1) * 128], id16[:, :], start=True, stop=True)
                    return ins
                S.op("pe", fn, reads=[B_hm, B_const], writes=[Bps[pb]])
                S.op("act", (lambda e, half=half, pb=pb, ch=ch: e.copy(hmT[:, half * 4:(half + 1) * 4, ch * 128:(ch + 1) * 128],
                                                                      ps[pb][:, 0:512].rearrange("p (c t) -> p c t", c=4))),
                     reads=[Bps[pb]], writes=[B_sq])
        for hh in range(2):
            cs_ = slice(hh * 512, (hh + 1) * 512)
            wload(wsl[0][:, 0:4, :], B_wsl[0], kp(w_four)[:, :, cs_], "C", ("w4", hh), 2048)
            wload(wsl[1], B_wsl[1], kp(w_mproj)[:, :, cs_], "C", ("wm", hh), 4096)
            wload(wsl[2], B_wsl[2], kp(wgf)[:, :, cs_], "C", ("wgf", hh), 4096)
            wload(wsl[3], B_wsl[3], kp(wgm)[:, :, cs_], "C", ("wgm", hh), 4096)
            for ii in range(4):
                i = hh * 4 + ii
                cw = slice(ii * 128, (ii + 1) * 128)
                mm(ps[0][:, 0:512], [(wsl[0][:, gq, cw], yfT[:, gq, t0:t0 + 512]) for gq in range(4)], reads=[B_wsl[0], B_yfT], writes=[Bps[0]])
                mm(ps[1][:, 0:512], [(wsl[1][:, k, cw], hmT[:, k, :]) for k in range(8)], reads=[B_wsl[1], B_sq], writes=[Bps[1]])
                mm(ps[2][:, 0:512], [(wsl[2][:, k, cw], u2[:, k, c0:c0 + 512]) for k in range(8)], reads=[B_wsl[2]] + B_u2, writes=[Bps[2]])
                mm(ps[3][:, 0:512], [(wsl[3][:, k, cw], u2[:, k, c0:c0 + 512]) for k in range(8)], reads=[B_wsl[3]] + B_u2, writes=[Bps[3]])
                S.op("act", (lambda e, i=i: e.activation(sa[0], ps[2][:, 0:512], AF.Sigmoid, bias=vcol("bgf", i, 1))), reads=[Bps[2], B_const], writes=[B_sa[0]])
                S.op("act", (lambda e, i=i: e.activation(sa[1], ps[3][:, 0:512], AF.Sigmoid, bias=vcol("bgm", i, 1))), reads=[Bps[3], B_const], writes=[B_sa[1]])
                S.op("dve", lambda e: e.tensor_tensor(sa[0], sa[0], ps[0][:, 0:512], ALU.mult), reads=[Bps[0], B_sa[0]], writes=[B_sa[0]])
                S.op("dve", lambda e: e.tensor_tensor(sa[1], sa[1], ps[1][:, 0:512], ALU.mult), reads=[Bps[1], B_sa[1]], writes=[B_sa[1]])
                S.op("dve", (lambda e, i=i: e.tensor_tensor(yT[:, i, :], sa[0], sa[1], ALU.add)), reads=B_sa, writes=[B_u])
        S.dma("sp", (lambda e, t0=t0: e.dma_start(out=x[:, :, :], in_=H1v[:, :, t0:t0 + 512])), reads=[B_H1], writes=[B_x])
        for hh in range(2):
            wload(wsl[hh], B_wsl[hh], kp(w_out)[:, :, hh * 512:(hh + 1) * 512], "C", ("wout", hh), 4096)
        for i in range(8):
            pb = 4 + i % 2
            mm(ps[pb][:, 0:512], [(wsl[i // 4][:, k, (i % 4) * 128:(i % 4 + 1) * 128], yT[:, k, :]) for k in range(8)],
               reads=[B_wsl[i // 4], B_u], writes=[Bps[pb]])
            S.op("act", (lambda e, i=i, pb=pb: e.copy(ol[:, i, :], ps[pb][:, 0:512])), reads=[Bps[pb]], writes=B_ol)
            S.op("act", (lambda e, i=i, pb=pb: e.activation(sq[:, i, :], ps[pb][:, 0:512], AF.Square)), reads=[Bps[pb]], writes=[B_sq])
        rstd_from(sq, B_sq, 512, rs1, B_rs1, 6)
        for i in range(8):
            t = tmp[i % 2]; bt = B_tmp[i % 2]
            S.op("dve", (lambda e, i=i, t=t: e.tensor_tensor(t, ol[:, i, :], rs1, ALU.mult)), reads=B_ol + [B_rs1], writes=[bt])
            S.op("dve", (lambda e, i=i, t=t: e.scalar_tensor_tensor(x[:, i, :], t, DER["PMl"][:, i:i + 1], x[:, i, :], ALU.mult, ALU.add)),
                 reads=[bt, g_["B_der"]], writes=[B_x])
        S.barrier()
        S.op("act", lambda e: e.activation(sq[:, :, :], x[:, :, :], AF.Square), reads=[B_x], writes=[B_sq])
        rstd_from(sq, B_sq, 512, rs1, B_rs1, 6)
        ffn(x, B_x, 512, rs1, B_rs1, DER["A4l"], DER["S4l"], DER["PBl"], w13b, w2b, fb, "B")
        t_store = S.dma("sp", (lambda e, t0=t0: e.dma_start(out=outv[:, :, t0:t0 + 512], in_=x[:, :, :])), reads=[B_x])
        S.barrier(skip=[t_store])


def build_rest2(env, L):
    AX = mybir.AxisListType
    g_ = dict(env); g_.update(L)
    nc = g_["nc"]; S = g_["S"]; ps = g_["ps"]; Bps = g_["Bps"]; arena = g_["arena"]
    u2 = g_["u2"]; B_u2 = g_["B_u2"]; vcol = g_["vcol"]; DER = g_["DER"]
    ident = g_["ident"]; maskf = g_["maskf"]; maskb = g_["maskb"]; CS = g_["CS"]; M2 = g_["M2"]; M3 = g_["M3"]
    id16 = g_["id16"]; ones16 = g_["ones16"]; B_const = g_["B_const"]; B_der = g_["B_der"]
    sel = g_["sel"]; negsel = g_["negsel"]; mm = g_["mm"]
    H1, AB, PQ, KT, QT, KTOK, VTOK, GROW, HS = [g_[n] for n in "H1 AB PQ KT QT KTOK VTOK GROW HS".split()]
    B_H1, B_AB, B_PQ, B_KT, B_QT, B_KTOK, B_VTOK, B_GROW = [g_["B_" + n] for n in "H1 AB PQ KT QT KTOK VTOK GROW".split()]
    B_HS = g_["B_HS"]
    Rcol, Gcol, Ecol, gend, acol, wkcol, decay, B_cols = [g_[n] for n in "Rcol Gcol Ecol gend acol wkcol decay B_cols".split()]
    yfT, B_yfT, C32, C16, B_C, mB = [g_[n] for n in "yfT B_yfT C32 C16 B_C mB".split()]
    rowv = g_["rowv"]; outT = g_["outT"]
    tiles = g_["tiles"]
    kp = lambda ap: ap.rearrange("(k p) n -> p k n", p=128)

    wbuf = [arena.bf16(8 * 1024).rearrange("p (k n) -> p k n", k=8) for _ in range(2)]
    B_wbuf = [Buf("wbuf0"), Buf("wbuf1")]
    xf = [arena.f32(512) for _ in range(4)]
    B_xf = [Buf("xf%d" % i) for i in range(4)]
    ab_sb2 = [arena.f32(1024).rearrange("p (x g c) -> p x g c", x=2, g=4) for _ in range(2)]
    B_ab2 = [Buf("ab_sb0"), Buf("ab_sb1")]
    Pt2 = [arena.f32(514), arena.f32(514)]
    B_Pt2 = [Buf("Pt0"), Buf("Pt1")]
    acc2 = [arena.f32(512), arena.f32(512)]
    B_acc2 = [Buf("acc0"), Buf("acc1")]
    kTt = arena.bf16(8 * 512).rearrange("p (k t) -> p k t", k=8)
    B_kTt = Buf("kTt")
    tok2 = [arena.bf16(1024), arena.bf16(1024)]
    B_tok2 = [Buf("tok0"), Buf("tok1")]
    tokctr = [0]
    bv_bc = arena.f32(1024)
    ones1 = arena.f32(128)
    rv = arena.f32(1024)
    B_bc = Buf("bc")
    S.dma("sp", lambda e: e.dma_start(out=rv[0:1, :], in_=rowv[:, 0:1024]), writes=[B_bc])
    S.op("pool", lambda e: e.memset(ones1[0:1, :], 1.0), writes=[B_bc])

    def bcast_row(dst, seg):
        for hh in range(2):
            mm(ps[6][:, 0:512], [(ones1[0:1, :], rv[0:1, seg * 1024 + hh * 512:seg * 1024 + hh * 512 + 512])], reads=[B_bc], writes=[Bps[6]])
            S.op("act", (lambda e, hh=hh: e.copy(dst[:, hh * 512:(hh + 1) * 512], ps[6][:, 0:512])), reads=[Bps[6]], writes=[B_bc])
    bcast_row(bv_bc, 0)

    wF = g_["wF"]
    S.dma("pool", lambda e: e.dma_start(out=wbuf[0][:, :, 0:512], in_=kp(wF)), writes=[B_wbuf[0]])
    for i in range(8):
        c0 = CTX + 512 * i
        for g in range(4):
            pb = g % 2
            mm(ps[pb][:, 0:512], [(wbuf[0][:, k, g * 128:(g + 1) * 128], u2[:, k, c0:c0 + 512]) for k in range(8)],
               reads=[B_wbuf[0]] + B_u2, writes=[Bps[pb]])
            S.op("act", (lambda e, g=g, pb=pb: e.activation(xf[g], ps[pb][:, 0:512], AF.Identity, bias=vcol("bF", g, 1))),
                 reads=[Bps[pb], B_const], writes=[B_xf[g]])
        for tb in range(4):
            ab_sb = ab_sb2[tb % 2]; B_ab = B_ab2[tb % 2]
            for g in range(4):
                bank = 2 + g // 2
                mm(ps[bank][:, (g % 2) * 256:(g % 2) * 256 + 256], [(xf[g][:, tb * 128:(tb + 1) * 128], CS)],
                   reads=[B_xf[g], B_const], writes=[Bps[bank]])
            for bi in range(2):
                S.op("dve", (lambda e, bi=bi, ab_sb=ab_sb: e.tensor_copy(ab_sb[:, :, 2 * bi:2 * bi + 2, :],
                                                            ps[2 + bi][:, 0:512].rearrange("p (g x c) -> p x g c", g=2, x=2))),
                     reads=[Bps[2 + bi]], writes=[B_ab])
            tok0 = 512 * i + 128 * tb
            S.dma("sp", (lambda e, tok0=tok0, ab_sb=ab_sb: e.dma_start(out=AB.rearrange("x t f -> t x f")[tok0:tok0 + 128, :, :],
                                                          in_=ab_sb.rearrange("p x g c -> p x (g c)"))), reads=[B_ab], writes=[B_AB])

    def qk_proj(wdram, bname, cwname, cbname, slot, tlist, is_k):
        S.dma("pool", lambda e: e.dma_start(out=wbuf[slot], in_=kp(wdram)), writes=[B_wbuf[slot]])
        pend_silu = [None]
        for (ti, c0, W) in tlist:
            islat = ti >= 1
            left = islat and ti > 1
            right = islat and ti < 8
            for c in range(8):
                pb = c % 2
                Pt = Pt2[c % 2]; B_Pt = B_Pt2[c % 2]; acc = acc2[c % 2]; B_acc = B_acc2[c % 2]
                hb = 5 + c % 2
                mm(ps[pb][:, 0:W], [(wbuf[slot][:, k, c * 128:(c + 1) * 128], u2[:, k, c0:c0 + W]) for k in range(8)],
                   reads=[B_wbuf[slot]] + B_u2, writes=[Bps[pb]])
                S.op("act", (lambda e, c=c, pb=pb, W=W, Pt=Pt: e.activation(Pt[:, 1:W + 1], ps[pb][:, 0:W], AF.Identity, bias=vcol(bname, c, 1))),
                     reads=[Bps[pb], B_const], writes=[B_Pt])
                if left and right:
                    mm(ps[hb][:, 0:2], [(wbuf[slot][:, k, c * 128:(c + 1) * 128], u2[:, k, c0 - 1:c0 + W + 1:W + 1]) for k in range(8)],
                       reads=[B_wbuf[slot]] + B_u2, writes=[Bps[hb]])
                    S.op("act", (lambda e, c=c, Pt=Pt, hb=hb, W=W: e.activation(Pt[:, 0:W + 2:W + 1], ps[hb][:, 0:2], AF.Identity, bias=vcol(bname, c, 1))),
                         reads=[Bps[hb], B_const], writes=[B_Pt])
                else:
                    for hi_, (has, col, dstc) in enumerate(((left, c0 - 1, 0), (right, c0 + W, W + 1))):
                        if has:
                            mm(ps[hb][:, hi_:hi_ + 1], [(wbuf[slot][:, k, c * 128:(c + 1) * 128], u2[:, k, col:col + 1]) for k in range(8)],
                               reads=[B_wbuf[slot]] + B_u2, writes=[Bps[hb]])
                            S.op("act", (lambda e, c=c, dstc=dstc, Pt=Pt, hb=hb, hi_=hi_: e.activation(Pt[:, dstc:dstc + 1], ps[hb][:, hi_:hi_ + 1], AF.Identity, bias=vcol(bname, c, 1))),
                                 reads=[Bps[hb], B_const], writes=[B_Pt])
                        else:
                            S.op("pool", (lambda e, dstc=dstc, Pt=Pt: e.memset(Pt[:, dstc:dstc + 1], 0.0)), writes=[B_Pt])
                S.op("act", (lambda e, c=c, W=W, Pt=Pt, acc=acc: e.activation(acc[:, 0:W], Pt[:, 0:W], AF.Identity, scale=vcol(cwname, c, 1))),
                     reads=[B_Pt, B_const], writes=[B_acc])
                S.op("dve", (lambda e, c=c, W=W, Pt=Pt, acc=acc: e.scalar_tensor_tensor(acc[:, 0:W], Pt[:, 1:W + 1], vcol(cwname, 8 + c, 1), acc[:, 0:W], ALU.mult, ALU.add)),
                     reads=[B_Pt, B_const], writes=[B_acc])
                S.op("dve", (lambda e, c=c, W=W, Pt=Pt, acc=acc: e.scalar_tensor_tensor(acc[:, 0:W], Pt[:, 2:W + 2], vcol(cwname, 16 + c, 1), acc[:, 0:W], ALU.mult, ALU.add)),
                     reads=[B_Pt, B_const], writes=[B_acc])
                if pend_silu[0] is not None:
                    pend_silu[0]()
                pend_silu[0] = (lambda c=c, W=W, acc=acc, B_acc=B_acc: S.op(
                    "act", (lambda e: e.activation(kTt[:, c, 0:W], acc[:, 0:W], AF.Silu, bias=vcol(cbname, c, 1))),
                    reads=[B_acc, B_const], writes=[B_kTt]))
            if pend_silu[0] is not None:
                pend_silu[0]()
                pend_silu[0] = None
            own = 1 <= ti <= 4
            if own:
                o0 = c0 - CTX
                dst = KT if is_k else QT
                S.dma("sp", (lambda e, o0=o0, dst=dst: e.dma_start(out=dst.rearrange("(k p) t -> p k t", p=128)[:, :, o0:o0 + 512], in_=kTt[:, :, :])),
                      reads=[B_kTt], writes=[B_KT if is_k else B_QT])
            if is_k:
                for tb in range(W // 128):
                    tok = tok2[tokctr[0] % 2]; B_tok = B_tok2[tokctr[0] % 2]; tokctr[0] += 1
                    for half in range(2):
                        pb = 3 + half
                        def fn(e, tb=tb, half=half, pb=pb):
                            for cc in range(4):
                                c = half * 4 + cc
                                ins = e.matmul(ps[pb][:, cc * 128:(cc + 1) * 128], kTt[:, c, tb * 128:(tb + 1) * 128], id16[:, :], start=True, stop=True)
                            return ins
                        S.op("pe", fn, reads=[B_kTt, B_const], writes=[Bps[pb]])
                        S.op("dve", (lambda e, half=half, pb=pb, tok=tok: e.tensor_copy(tok[:, half * 512:(half + 1) * 512], ps[pb][:, 0:512])),
                             reads=[Bps[pb]], writes=[B_tok])
                    r0 = c0 + tb * 128
                    S.dma("sp", (lambda e, r0=r0, tok=tok: e.dma_start(out=KTOK[r0:r0 + 128, :], in_=tok)), reads=[B_tok], writes=[B_KTOK])

    tl_all = [(ti, c0, W) for ti, (c0, W, j) in enumerate(tiles)]
    qk_proj(g_["wk"], "bk", "cwk", "cbk", 1, tl_all, True)
    qk_proj(g_["wq"], "bq", "cwq", "cbq", 0, tl_all[1:5], False)
    S.dma("pool", lambda e: e.dma_start(out=wbuf[1], in_=kp(g_["wv"])), writes=[B_wbuf[1]])
    for cb in range(0, NT, 128):
        tok = tok2[tokctr[0] % 2]; B_tok = B_tok2[tokctr[0] % 2]; tokctr[0] += 1
        for half in range(2):
            pb = half + 2 * ((cb // 128) % 2)
            mm(ps[pb][:, 0:512], [(u2[:, k, cb:cb + 128], wbuf[1][:, k, half * 512:(half + 1) * 512]) for k in range(8)],
               reads=[B_wbuf[1]] + B_u2, writes=[Bps[pb]])
            S.op("dve", (lambda e, half=half, pb=pb, tok=tok: e.tensor_tensor(tok[:, half * 512:(half + 1) * 512], ps[pb][:, 0:512],
                                                                     bv_bc[:, half * 512:(half + 1) * 512], ALU.add)),
                 reads=[Bps[pb], B_bc], writes=[B_tok])
        S.dma("sp", (lambda e, cb=cb, tok=tok: e.dma_start(out=VTOK[cb:cb + 128, :], in_=tok)), reads=[B_tok], writes=[B_VTOK])
    S.barrier()
    arena.reset(mB)

    inb2 = [arena.f32(8 * 512).rearrange("p (r f) -> p r f", r=8) for _ in range(2)]
    outb2 = [arena.f32(8 * 512).rearrange("p (r f) -> p r f", r=8) for _ in range(2)]
    B_inb2 = [Buf("inb0"), Buf("inb1")]; B_outb2 = [Buf("outb0"), Buf("outb1")]
    ABv = AB.rearrange("x (r c) f -> x c r f", c=64)
    PQw = PQ
    for rb in range(8):
        inb = inb2[rb % 2]; outb = outb2[rb % 2]; B_inb = B_inb2[rb % 2]; B_outb = B_outb2[rb % 2]
        for x in range(2):
            S.dma("sp", (lambda e, rb=rb, x=x, inb=inb: e.dma_start(out=inb[64 * x:64 * x + 64, :, :], in_=ABv[x, :, rb * 8:rb * 8 + 8, :])),
                  reads=[B_AB], writes=[B_inb])
        for r in range(8):
            pb = r % 4
            mm(ps[pb][:, 0:512], [(M2, inb[:, r, :])], reads=[B_inb, B_const], writes=[Bps[pb]])
            if r % 2 == 0:
                S.op("act", (lambda e, r=r, pb=pb, outb=outb: e.copy(outb[:, r, :], ps[pb][:, 0:512])), reads=[Bps[pb]], writes=[B_outb])
            else:
                S.op("dve", (lambda e, r=r, pb=pb, outb=outb: e.tensor_copy(outb[:, r, :], ps[pb][:, 0:512])), reads=[Bps[pb]], writes=[B_outb])
        for x in range(2):
            S.dma("sp", (lambda e, rb=rb, x=x, outb=outb: e.dma_start(out=PQw[x, :, rb * 8:rb * 8 + 8, :], in_=outb[64 * x:64 * x + 64, :, :])),
                  reads=[B_outb], writes=[B_PQ])
    PQr = PQ.rearrange("x kc r f -> x r kc f")
    for kb in range(8):
        inb = inb2[kb % 2]; B_inb = B_inb2[kb % 2]
        for x in range(2):
            S.dma("sp", (lambda e, kb=kb, x=x, inb=inb: e.dma_start(out=inb[64 * x:64 * x + 64, :, :], in_=PQr[x, :, kb * 8:kb * 8 + 8, :])),
                  reads=[B_PQ], writes=[B_inb])
        for g in range(4):
            pb = 4 + g
            def fn(e, g=g, pb=pb, inb=inb):
                for kc in range(8):
                    ins = e.matmul(ps[pb][:, kc * 32:(kc + 1) * 32], inb[:, kc, g * 128:(g + 1) * 128], M3, start=True, stop=True)
                return ins
            S.op("pe", fn, reads=[B_inb, B_const], writes=[Bps[pb]])
            S.op("act", (lambda e, g=g, pb=pb, kb=kb: e.copy(yfT[:, g, :].rearrange("p (kr kc) -> p kc kr", kc=64)[:, kb * 8:kb * 8 + 8, :],
                                                          ps[pb][:, 0:256].rearrange("p (kc kr) -> p kc kr", kr=32))),
                 reads=[Bps[pb]], writes=[B_yfT])
    S.barrier()
    arena.reset(mB)

    NLS = 3
    NHS = 2
    LD = []
    for i in range(2 * NLS):
        d_ = dict(ktok=arena.bf16(1024), vaug=arena.bf16(4 * 258).rearrange("p (h e) -> p h e", h=4),
                  kT=arena.bf16(1024).rearrange("p (k t) -> p k t", k=8), qT=arena.bf16(1024).rearrange("p (k t) -> p k t", k=8),
                  grow=arena.f32(128), B_ld=Buf("ld%d" % i), B_ldo=Buf("ldo%d" % i))
        S.op("pool", (lambda e, v=d_["vaug"]: e.memset(v[:, :, :], 1.0)), writes=[d_["B_ld"]])
        LD.append(d_)
    HSB = [dict(hs=arena.f32(1024), B_hs=Buf("hs%d" % i)) for i in range(4)]
    HT = []
    B_pP2s = Buf("pP2"); B_pP1s = Buf("pP1")
    for i in range(NHS):
        HT.append(dict(wT=arena.f32(128), STb=arena.bf16(128), P2sb=arena.f32(257), hn=arena.f32(257), dd=arena.f32(2), kw=arena.bf16(256),
                       B_wT=Buf("wT%d" % i), B_ST=Buf("ST%d" % i), B_P2=Buf("P2sb%d" % i), B_hn=Buf("hn%d" % i), B_dd=Buf("dd%d" % i),
                       B_kw=Buf("kw%d" % i),
                       pST=ps[i][:, 0:128], pD=ps[i][:, 128:256], pP2=ps[2][:, 0:257], pP1=ps[3][:, 0:257],
                       pCU=[ps[4 + i][:, 0:257], ps[6 + i][:, 0:257]],
                       B_pSD=Buf("pSD%d" % i), B_pP2=B_pP2s, B_pP1=B_pP1s, B_pCU=[Buf("pCU0_%d" % i), Buf("pCU1_%d" % i)]))
    B_C32 = [[Buf('C32_%d_%d' % (q, c)) for c in range(2)] for q in range(8)]
    B_C16 = [[Buf('C16_%d_%d' % (q, c)) for c in range(2)] for q in range(8)]
    steps = []
    fw = [(0, 0, None), (1, 128, None)] + [(2 + i, CTX + 128 * i, 128 * i) for i in range(16)]
    bw = [(0, 128, None), (1, 0, None)] + [(2 + i, CTX + 128 * (31 - i), (128 * (31 - i) if 31 - i <= 15 else None)) for i in range(32)]
    for i in range(34):
        if i < 18:
            steps.append((0,) + fw[i])
        steps.append((1,) + bw[i])
    mask16 = [arena.bf16(128), arena.bf16(128)]
    S.op("act", lambda e: e.copy(mask16[0], maskf), reads=[B_const], writes=[B_const])
    S.op("act", lambda e: e.copy(mask16[1], maskb), reads=[B_const], writes=[B_const])

    def emit_loads(si):
        (dr, sc, cb, ob) = steps[si]
        L_ = LD[dr * NLS + sc % NLS]
        ktok_t, vaug, kT_t, qT_t, grow_t = L_["ktok"], L_["vaug"], L_["kT"], L_["qT"], L_["grow"]
        B_ld, B_ldo = L_["B_ld"], L_["B_ldo"]
        S.dma("sp", (lambda e: e.dma_start(out=ktok_t, in_=KTOK[cb:cb + 128, :])), reads=[B_KTOK], writes=[B_ld])
        S.dma("sp", (lambda e: e.dma_start(out=vaug[:, :, 0:256], in_=VTOK[cb:cb + 128, :].rearrange("t (h e) -> t h e", h=4))),
              reads=[B_VTOK], writes=[B_ld])
        if ob is not None:
            S.dma("sp", (lambda e: e.dma_start(out=kT_t, in_=KT.rearrange("(k p) t -> p k t", p=128)[:, :, ob:ob + 128])), reads=[B_KT], writes=[B_ldo])
            S.dma("sp", (lambda e: e.dma_start(out=qT_t, in_=QT.rearrange("(k p) t -> p k t", p=128)[:, :, ob:ob + 128])), reads=[B_QT], writes=[B_ldo])
            S.dma("sp", (lambda e: e.dma_start(out=grow_t[0:36, :], in_=GROW[:, cb:cb + 128])), reads=[B_GROW], writes=[B_ldo])

    def emit_load_hs(si):
        (dr, sc, cb, ob) = steps[si]
        if ob is not None and dr == 1:
            H2 = HSB[dr * 2 + sc % 2]
            S.dma("sp", (lambda e: e.dma_start(out=H2["hs"], in_=HS[ob:ob + 128, :])), reads=[B_HS[ob // 128]], writes=[H2["B_hs"]])

    items = []
    for si, (dr, sc, cb, ob) in enumerate(steps):
        for h in range(4):
            items.append((si, h, len(items)))

    def ctx_of(it):
        si, h, n = it
        (dr, sc, cb, ob) = steps[si]
        L2 = dict(LD[dr * NLS + sc % NLS]); L2.update(HSB[dr * 2 + sc % 2])
        return dr, sc, cb, ob, h, dr * 4 + h, (sc if dr == 0 else 18 + sc), L2, HT[n % NHS]

    def emit_A(it):
        dr, sc, cb, ob, h, q, ci, L_, H_ = ctx_of(it)
        kT_t, qT_t, grow_t, ktok_t = L_["kT"], L_["qT"], L_["grow"], L_["ktok"]
        wT, STb, kw, pST, pD = H_["wT"], H_["STb"], H_["kw"], H_["pST"], H_["pD"]
        S.op("pool", (lambda e: e.tensor_scalar(kw, ktok_t[:, h * 256:(h + 1) * 256], wkcol[:, q, sc:sc + 1], 0.0625, ALU.mult, ALU.mult)),
             reads=[L_["B_ld"], B_cols], writes=[H_["B_kw"]])
        if ob is not None:
            def fsd(e):
                e.matmul(pST, kT_t[:, 2 * h, :], qT_t[:, 2 * h, :], start=True, stop=False)
                e.matmul(pST, kT_t[:, 2 * h + 1, :], qT_t[:, 2 * h + 1, :], start=False, stop=True)
                e.matmul(pD, negsel[0:36, q, :], grow_t[0:36, :], start=True, stop=False)
                return e.matmul(pD, ident, maskf if dr == 0 else maskb, start=False, stop=True)
            S.op("pe", fsd, reads=[L_["B_ldo"], B_const], writes=[H_["B_pSD"]])
            S.op("act", (lambda e: e.activation(wT, pD, AF.Exp, bias=Rcol[:, ci, h:h + 1])),
                 reads=[H_["B_pSD"], B_cols], writes=[H_["B_wT"]])
            S.op("dve", (lambda e: e.scalar_tensor_tensor(STb, pST, 0.0625, wT, ALU.mult, ALU.mult)),
                 reads=[H_["B_pSD"], H_["B_wT"]], writes=[H_["B_ST"]])

    pend_fin = []

    def emit_BC(it):
        while pend_fin:
            pend_fin.pop(0)()
        dr, sc, cb, ob, h, q, ci, L_, H_ = ctx_of(it)
        vaug, qT_t, hs_t = L_["vaug"], L_["qT"], L_["hs"]
        B_ld, B_ldo, B_hs = L_["B_ld"], L_["B_ldo"], L_["B_hs"]
        STb, P2sb, hn, dd, kw = H_["STb"], H_["P2sb"], H_["hn"], H_["dd"], H_["kw"]
        pP2, pP1, pCU = H_["pP2"], H_["pP1"], H_["pCU"]
        for c in range(2):
            mm(pCU[c], [(kw[:, c * 128:(c + 1) * 128], vaug[:, h, 0:257])], reads=[H_["B_kw"], B_ld], writes=[H_["B_pCU"][c]])
        if ob is not None:
            mm(pP1, [(qT_t[:, 2 * h + c, :], C16[:, q, c, 0:257]) for c in range(2)], reads=[B_ldo] + B_C16[q], writes=[H_["B_pP1"]])
        for c in range(2):
            S.op("dve", (lambda e, c=c: e.scalar_tensor_tensor(C32[:, q, c, :], C32[:, q, c, :], decay[:, q, sc:sc + 1], pCU[c], ALU.mult, ALU.add)),
                 reads=[H_["B_pCU"][c], B_cols], writes=[B_C32[q][c]])
            S.op("act", (lambda e, c=c: e.copy(C16[:, q, c, 0:257], C32[:, q, c, :])), reads=[B_C32[q][c]], writes=[B_C16[q][c]])
        if ob is not None:
            mm(pP2, [(STb, vaug[:, h, 0:257])], reads=[H_["B_ST"], B_ld], writes=[H_["B_pP2"]])
            S.op("act", (lambda e: e.copy(P2sb, pP2)), reads=[H_["B_pP2"]], writes=[H_["B_P2"]])
            S.op("dve", (lambda e: e.scalar_tensor_tensor(hn, pP1, acol[:, q, sc:sc + 1], P2sb, ALU.mult, ALU.add)),
                 reads=[H_["B_pP1"], H_["B_P2"], B_cols], writes=[H_["B_hn"]])
            S.op("dve", (lambda e: e.scalar_tensor_tensor(dd[:, 0:1], hn[:, 256:257], -1.0, hn[:, 256:257], ALU.mult, ALU.max)),
                 reads=[H_["B_hn"]], writes=[H_["B_dd"]])
            S.op("dve", (lambda e: e.tensor_tensor(dd[:, 0:1], dd[:, 0:1], Ecol[:, ci, h:h + 1], ALU.max)),
                 reads=[H_["B_dd"], B_cols], writes=[H_["B_dd"]])
            S.op("dve", (lambda e: e.reciprocal(dd[:, 1:2], dd[:, 0:1])), reads=[H_["B_dd"]], writes=[H_["B_dd"]])
            def fin():
                if dr == 0:
                    S.op("act", (lambda e: e.activation(hs_t[:, h * 256:(h + 1) * 256], hn[:, 0:256], AF.Identity, scale=dd[:, 1:2])),
                         reads=[H_["B_hn"], H_["B_dd"]], writes=[B_hs])
                else:
                    S.op("dve", (lambda e: e.scalar_tensor_tensor(hs_t[:, h * 256:(h + 1) * 256], hn[:, 0:256], dd[:, 1:2],
                                                                  hs_t[:, h * 256:(h + 1) * 256], ALU.mult, ALU.add)),
                         reads=[H_["B_hn"], H_["B_dd"]], writes=[B_hs])
                if h == 3:
                    S.dma("sp", (lambda e: e.dma_start(out=HS[ob:ob + 128, :], in_=hs_t)), reads=[B_hs], writes=[B_HS[ob // 128]])
            if dr == 0:
                pend_fin.append(fin)
            else:
                fin()

    emit_loads(0); emit_loads(1)
    for n in range(len(items) + 1):
        if n < len(items):
            si, h, _ = items[n]
            if h == 0:
                emit_load_hs(si)
            if h == 1 and si + 2 < len(steps):
                emit_loads(si + 2)
            emit_A(items[n])
        if n >= 1:
            emit_BC(items[n - 1])
    while pend_fin:
        pend_fin.pop(0)()
    S.barrier()
    if g_["stage"] < 4:
        return
    build_rest3(g_, locals())


def build_rest(env):
    nc = env["nc"]
    S = env["S"]; ps = env["ps"]; Bps = env["Bps"]; arena = env["arena"]
    u2 = env["u2"]; B_u2 = env["B_u2"]; vcol = env["vcol"]; DER = env["DER"]
    ident = env["ident"]; maskf = env["maskf"]; maskb = env["maskb"]; CS = env["CS"]; M2 = env["M2"]; M3 = env["M3"]
    id16 = env["id16_t"]; ones16 = env["ones16_t"]
    B_const = env["B_const"]; B_der = env["B_der"]
    stage = env["stage"]; dbg = env["dbg"]; dbg_tensor = env["dbg_tensor"]
    dscr = env["dscr"]
    H1, AB, PQ, KT, QT, KTOK, VTOK = [env[n] for n in "H1 AB PQ KT QT KTOK VTOK".split()]
    B_H1, B_AB, B_PQ, B_KT, B_QT, B_KTOK, B_VTOK = [env["B_" + n] for n in "H1 AB PQ KT QT KTOK VTOK".split()]
    GROW = dscr("GROW", [36, NT], F32)
    B_GROW = Buf("GROW")
    HS = dscr("HS", [OWN, D], F32)
    B_HS = [Buf("HS%d" % i) for i in range(16)]

    def mm(out, pairs, reads, writes):
        def fn(e):
            n = len(pairs)
            for i, (l, r) in enumerate(pairs):
                ins = e.matmul(out, l, r, start=(i == 0), stop=(i == n - 1))
            return ins
        return S.op("pe", fn, reads=reads, writes=writes)

    NCH = 52
    Rcol = arena.f32(NCH * 4).rearrange("p (c h) -> p c h", h=4)
    Gcol = arena.f32(NCH * 4).rearrange("p (c h) -> p c h", h=4)
    Ecol = arena.f32(NCH * 4).rearrange("p (c h) -> p c h", h=4)
    gend = arena.f32(8 * 35).rearrange("p (q c) -> p q c", q=8)
    acol = arena.f32(8 * 34).rearrange("p (q c) -> p q c", q=8)
    wkcol = arena.f32(8 * 34).rearrange("p (q c) -> p q c", q=8)
    decay = arena.f32(8 * 34).rearrange("p (q c) -> p q c", q=8)
    B_cols = Buf("cols")
    yfT = arena.bf16(4 * OWN).rearrange("p (g t) -> p g t", g=4)
    B_yfT = Buf("yfT")
    C32 = arena.f32(8 * 2 * 257).rearrange("p (q c e) -> p q c e", q=8, c=2)
    C16 = arena.bf16(8 * 2 * 258).rearrange("p (q c e) -> p q c e", q=8, c=2)
    B_C = [Buf("C%d" % i) for i in range(8)]
    sel_t = arena.f32(2048)
    S.dma("sp", lambda e: e.dma_start(out=sel_t[0:36, :], in_=env["selc"]), writes=[B_const])
    sel = sel_t[0:36, 0:1024].rearrange("r (p m) -> r p m", p=8)
    negsel = sel_t[0:36, 1024:2048].rearrange("r (p m) -> r p m", p=8)
    mB = arena.mark()

    aLI = arena.f32(NT)
    aLF = arena.f32(NT)
    aB = arena.f32(NT)
    aG = arena.f32(NT)
    ones_r = arena.f32(512)
    tmpE = arena.f32(512)
    wgf_s = arena.bf16(8 * 8).rearrange("p (k n) -> p k n", k=8)
    wgb_s = arena.bf16(8 * 72).rearrange("p (k n) -> p k n", k=8)
    bgs = arena.f32(4)
    B_rows = Buf("rows")
    B_wg = Buf("wg")
    B_tmpE = Buf("tmpE")
    for a in (aLI, aLF, aB, aG):
        S.op("pool", (lambda e, a=a: e.memset(a, 0.0)), writes=[B_rows])
    S.op("pool", lambda e: e.memset(ones_r, 1.0), writes=[B_wg])
    S.op("pool", lambda e: e.memset(gend[:, :, :], 0.0), writes=[B_cols])
    for q in range(8):
        S.op("pool", (lambda e, q=q: e.memset(C32[:, q, :, :], 0.0)), writes=[B_C[q]])
        S.op("pool", (lambda e, q=q: e.memset(C16[:, q, :, :], 0.0)), writes=[B_C[q]])
    with nc.allow_non_contiguous_dma(reason="tiny gate weights"):
        pass
    wgate_f = env["wgate_f"]; wgate_b = env["wgate_b"]; bg = env["bg"]
    S.dma("pool", lambda e: e.dma_start(out=wgf_s, in_=wgate_f.rearrange("(k p) n -> p k n", p=128)), writes=[B_wg])
    S.dma("pool", lambda e: e.dma_start(out=wgb_s, in_=wgate_b.rearrange("(k p) n -> p k n", p=128)), writes=[B_wg])
    S.dma("sp", lambda e: e.dma_start(out=bgs[0:36, 0:2], in_=bg), writes=[B_wg])
    S.op("dve", lambda e: e.tensor_scalar(bgs[0:36, 2:3], bgs[0:36, 1:2], -1.0, None, ALU.mult), reads=[B_wg], writes=[B_wg])

    def gate_tile(jc0, W, rhs_fn, r0, r1, wl, wl_lf, pbank, rev=False):
        def pv(bank):
            return ps[bank][r0:r1, W - 1::-1] if rev else ps[bank][r0:r1, 0:W]
        mm(ps[pbank][0:r1, 0:W], [(wl(k), rhs_fn(k)) for k in range(8)], reads=[B_wg] + B_u2, writes=[Bps[pbank]])
        S.op("act", lambda e: e.activation(aLI[r0:r1, jc0:jc0 + W], pv(pbank), AF.Identity, bias=bgs[r0:r1, 0:1]),
             reads=[Bps[pbank], B_wg], writes=[B_rows])
        mm(ps[pbank + 1][0:r1, 0:W], [(wl_lf(k), rhs_fn(k)) for k in range(8)], reads=[B_wg] + B_u2, writes=[Bps[pbank + 1]])
        S.op("act", lambda e: e.activation(tmpE[r0:r1, 0:W], pv(pbank + 1), AF.Exp, bias=bgs[r0:r1, 2:3], scale=-1.0),
             reads=[Bps[pbank + 1], B_wg], writes=[B_tmpE])
        S.op("act", lambda e: e.activation(tmpE[r0:r1, 0:W], tmpE[r0:r1, 0:W], AF.Ln, bias=1.0), reads=[B_tmpE], writes=[B_tmpE])
        S.op("dve", lambda e: e.tensor_scalar(aLF[r0:r1, jc0:jc0 + W], tmpE[r0:r1, 0:W], -1.0, None, ALU.mult),
             reads=[B_tmpE], writes=[B_rows])

    ti = 0
    for (jc0, W) in [(0, 256)] + [(256 + 512 * i, 512) for i in range(4)]:
        gate_tile(jc0, W, (lambda k, jc0=jc0, W=W: u2[:, k, jc0:jc0 + W]), 0, 4,
                  (lambda k: wgf_s[:, k, 0:4]), (lambda k: wgf_s[:, k, 4:8]), 2 * (ti % 2))
        ti += 1
    def rev_u2(k, hi, W):
        return u2[:, k, hi - W + 1:hi + 1]
    gate_tile(0, 256, (lambda k: rev_u2(k, 255, 256)), 32, 36,
              (lambda k: wgb_s[:, k, 0:36]), (lambda k: wgb_s[:, k, 36:72]), 2 * (ti % 2), rev=True)
    ti += 1
    for i in range(8):
        jc0 = 256 + 512 * i
        hi = 4607 - jc0
        gate_tile(jc0, 512, (lambda k, hi=hi: rev_u2(k, hi, 512)), 32, 36,
                  (lambda k: wgb_s[:, k, 0:36]), (lambda k: wgb_s[:, k, 36:72]), 2 * (ti % 2), rev=True)
        ti += 1
    pieces = [(0, 256)] + [(256 + 512 * i, 512) for i in range(8)]
    for pi, (c0, W) in enumerate(pieces):
        init = 0.0 if pi == 0 else aB[0:36, c0 - 1:c0]
        S.op("dve", (lambda e, c0=c0, W=W, init=init: e.tensor_tensor_scan(aB[0:36, c0:c0 + W], ones_r[0:36, 0:W], aLF[0:36, c0:c0 + W],
                                                                           init, ALU.mult, ALU.add)),
             reads=[B_rows, B_wg], writes=[B_rows])
    S.op("dve", lambda e: e.tensor_tensor(aLI[0:36, :], aLI[0:36, :], aB[0:36, :], ALU.subtract), reads=[B_rows], writes=[B_rows])
    for pi, (c0, W) in enumerate(pieces):
        init = 0.0 if pi == 0 else aG[0:36, c0 - 1:c0]
        S.op("dve", (lambda e, c0=c0, W=W, init=init: e.tensor_tensor_scan(aG[0:36, c0:c0 + W], ones_r[0:36, 0:W], aLI[0:36, c0:c0 + W],
                                                                           init, ALU.mult, ALU.max)),
             reads=[B_rows, B_wg], writes=[B_rows])
    S.op("dve", lambda e: e.tensor_tensor(aB[0:36, :], aB[0:36, :], aG[0:36, :], ALU.add), reads=[B_rows], writes=[B_rows])
    if dbg:
        d_rows = dbg_tensor("d_rows", [3, 36, NT])
        for i, a in enumerate((aLI, aG, aB)):
            S.dma("sp", (lambda e, i=i, a=a: e.dma_start(out=d_rows[i], in_=a[0:36, :])), reads=[B_rows])
    def n0_of(sc):
        return 128 if sc == 0 else (0 if sc == 1 else 4480 - 128 * sc)
    for ai, (arr, bank) in enumerate(((aLI, 0), (aB, 2), (aG, 1))):
        S.op("dve", (lambda e, arr=arr: e.tensor_copy(aLF[32:36, 0:256], arr[32:36, 255::-1])), reads=[B_rows], writes=[B_rows])
        S.op("dve", (lambda e, arr=arr: e.tensor_copy(aLF[32:36, 256:NT], arr[32:36, NT - 1:255:-1])), reads=[B_rows], writes=[B_rows])
        def fn(e, arr=arr, bank=bank):
            for sc in range(18):
                ins = e.matmul(ps[bank][:, sc * 4:(sc + 1) * 4], arr[0:4, sc * 128:(sc + 1) * 128], ident[0:4, 0:4],
                               start=True, stop=True)
            for sc in range(34):
                n0 = n0_of(sc)
                ins = e.matmul(ps[bank][:, (18 + sc) * 4:(19 + sc) * 4], aLF[32:36, n0:n0 + 128],
                               ident[32:36, 32:36], start=True, stop=True)
            return ins
        S.op("pe", fn, reads=[B_rows, B_const], writes=[Bps[bank]])
    S.op("dve", lambda e: e.tensor_copy(aLF[0:4, :], aG[0:4, :]), reads=[B_rows], writes=[B_rows])
    S.dma("sp", lambda e: e.dma_start(out=GROW, in_=aLF[0:36, :]), reads=[B_rows], writes=[B_GROW])
    S.op("act", lambda e: e.copy(Rcol[:, :, :], ps[0][:, 0:NCH * 4].rearrange("p (c h) -> p c h", h=4)), reads=[Bps[0]], writes=[B_cols])
    S.op("act", lambda e: e.copy(Gcol[:, :, :], ps[1][:, 0:NCH * 4].rearrange("p (c h) -> p c h", h=4)), reads=[Bps[1]], writes=[B_cols])
    S.op("act", lambda e: e.activation(Ecol[:, :, :], ps[2][:, 0:NCH * 4].rearrange("p (c h) -> p c h", h=4), AF.Exp, scale=-1.0),
         reads=[Bps[2]], writes=[B_cols])
    def fn(e):
        for q in range(8):
            n = 18 if q < 4 else 34
            ins = e.matmul(ps[3][:, q * 34:q * 34 + n], sel[0:36, q, :], aG[0:36, 127:127 + 128 * (n - 1) + 1:128], start=True, stop=True)
        return ins
    S.op("pe", fn, reads=[B_rows, B_const], writes=[Bps[3]])
    for q in range(8):
        n = 18 if q < 4 else 34
        S.op("act", (lambda e, q=q, n=n: e.copy(gend[:, q, 1:1 + n], ps[3][:, q * 34:q * 34 + n])), reads=[Bps[3]], writes=[B_cols])
    tq = arena.f32(34)
    B_tq = Buf("tq")
    for q in range(8):
        n = 18 if q < 4 else 34
        base = 0 if q < 4 else 18
        h = q % 4
        S.op("dve", (lambda e, q=q, n=n, base=base, h=h: e.tensor_tensor(tq[:, 0:n], gend[:, q, 0:n], Gcol[:, base:base + n, h], ALU.subtract)),
             reads=[B_cols], writes=[B_tq])
        S.op("act", (lambda e, q=q, n=n: e.activation(acol[:, q, 0:n], tq[:, 0:n], AF.Exp)), reads=[B_tq], writes=[B_cols])
        S.op("dve", (lambda e, q=q, n=n, base=base, h=h: e.tensor_tensor(tq[:, 0:n], Rcol[:, base:base + n, h], gend[:, q, 1:1 + n], ALU.subtract)),
             reads=[B_cols], writes=[B_tq])
        S.op("act", (lambda e, q=q, n=n: e.activation(wkcol[:, q, 0:n], tq[:, 0:n], AF.Exp)), reads=[B_tq], writes=[B_cols])
        S.op("dve", (lambda e, q=q, n=n: e.tensor_tensor(tq[:, 0:n], gend[:, q, 0:n], gend[:, q, 1:1 + n], ALU.subtract)),
             reads=[B_cols], writes=[B_tq])
        S.op("act", (lambda e, q=q, n=n: e.activation(decay[:, q, 0:n], tq[:, 0:n], AF.Exp)), reads=[B_tq], writes=[B_cols])
    S.barrier()
    arena.reset(mB)
    if stage < 3:
        return
    build_rest2(env, locals())


COL_F = 0
COL_Q = 512
COL_K = 1536
COL_V = 2560
COL_O = 3584
COL_GATES = 4608
COL_BR = 4624


def _dft_consts(flip):
    idx = (63 - np.arange(64)) if flip else np.arange(64)
    ang = 2 * np.pi * np.outer(idx, idx) / 64.0
    Cc = np.cos(ang) / 8.0
    Sc = np.sin(ang) / 8.0
    ch = np.arange(128)
    angc = 2 * np.pi * np.outer(ch, ch) / 128.0
    CS = np.concatenate([np.cos(angc), np.sin(angc)], axis=1) / np.sqrt(128.0)
    M2 = np.zeros((128, 128))
    M2[0:64, 0:64] = Cc
    M2[64:128, 0:64] = -Sc
    M2[0:64, 64:128] = Sc
    M2[64:128, 64:128] = Cc
    M3 = np.zeros((128, 32))
    M3[0:64, :] = Cc[:, 0:32]
    M3[64:128, :] = -Sc[:, 0:32]
    return CS, M2, M3


def make_inputs(inp):
    f32 = np.float32
    x = np.asarray(inp["x"], f32)
    ctx = np.asarray(inp["ctx"], f32)
    c = np.asarray(inp["c"], f32)
    c_ctx = np.asarray(inp["c_ctx"], f32)
    w_in = np.asarray(inp["w_in"], f32)[0]
    b_in = np.asarray(inp["b_in"], f32)[0]
    conv_w = np.asarray(inp["conv_w"], f32)[0]
    conv_b = np.asarray(inp["conv_b"], f32)[0]
    norm_g = np.asarray(inp["norm_g"], f32)[0]

    def fm(v):
        return np.ascontiguousarray(v.reshape(-1, 128).T)

    def r13(w):
        return np.ascontiguousarray(w.reshape(8, 128, 2, NJ, 128).transpose(3, 1, 0, 2, 4).reshape(NJ, 128, 2048))

    def r2(w):
        return np.ascontiguousarray(w.reshape(NJ, 128, 8, 128).transpose(2, 1, 0, 3).reshape(8, 128, FF))

    shared = {
        "w_ada": np.ascontiguousarray(np.asarray(inp["w_ada"], f32)[0]),
        "w13a": r13(np.asarray(inp["w13_a"], f32)[0]), "w2a": r2(np.asarray(inp["w2_a"], f32)[0]),
        "w13b": r13(np.asarray(inp["w13_b"], f32)[0]), "w2b": r2(np.asarray(inp["w2_b"], f32)[0]),
        "wF": np.ascontiguousarray(w_in[:, COL_F:COL_Q]), "wq": np.ascontiguousarray(w_in[:, COL_Q:COL_K]),
        "wk": np.ascontiguousarray(w_in[:, COL_K:COL_V]), "wv": np.ascontiguousarray(w_in[:, COL_V:COL_O]),
        "wo": np.ascontiguousarray(w_in[:, COL_O:COL_GATES]),
        "wgf": np.ascontiguousarray(w_in[:, COL_BR:COL_BR + D]), "wgm": np.ascontiguousarray(w_in[:, COL_BR + D:]),
        "w_four": np.ascontiguousarray(np.asarray(inp["w_four"], f32)[0]),
        "w_mproj": np.ascontiguousarray(np.asarray(inp["w_mproj"], f32)[0]),
        "w_out": np.ascontiguousarray(np.asarray(inp["w_out"], f32)[0]),
        "rowv": np.concatenate([b_in[COL_V:COL_O], b_in[COL_O:COL_GATES], np.asarray(inp["head_g"], f32)[0]])[None, :].copy(),
    }
    sel = np.zeros((36, 8, 128), f32)
    for p in range(8):
        row = (p % 4) + (32 if p >= 4 else 0)
        sel[row, p, :] = 1.0
    selc = np.concatenate([sel.reshape(36, -1), -sel.reshape(36, -1)], axis=1)
    s_idx = np.arange(128)[:, None]
    t_idx = np.arange(128)[None, :]
    maskf = np.where(s_idx <= t_idx, 0.0, NEG).astype(f32)
    maskb = np.where(s_idx >= t_idx, 0.0, NEG).astype(f32)
    maps = []
    for core in range(8):
        b, half = core // 2, core % 2
        flip = half == 1
        xb = x[b][::-1] if flip else x[b]
        cb_ = ctx[b][::-1] if flip else ctx[b]
        g = COL_GATES
        if flip:
            gi_f, gf_f, gi_b, gf_b = g + 8, g + 12, g + 0, g + 4
            cw = conv_w[::-1]
        else:
            gi_f, gf_f, gi_b, gf_b = g + 0, g + 4, g + 8, g + 12
            cw = conv_w
        wgate_f = np.concatenate([w_in[:, gi_f:gi_f + 4], w_in[:, gf_f:gf_f + 4]], axis=1)
        wgate_b = np.zeros((D, 72), f32)
        wgate_b[:, 32:36] = w_in[:, gi_b:gi_b + 4]
        wgate_b[:, 36 + 32:36 + 36] = w_in[:, gf_b:gf_b + 4]
        bgv = np.zeros((36, 2), f32)
        bgv[0:4, 0] = b_in[gi_f:gi_f + 4]
        bgv[0:4, 1] = b_in[gf_f:gf_f + 4]
        bgv[32:36, 0] = b_in[gi_b:gi_b + 4]
        bgv[32:36, 1] = b_in[gf_b:gf_b + 4]
        vecs = np.zeros((128, NV), f32)

        def put(name, arr):
            o, w = VEC[name]
            assert arr.shape == (128, w), (name, arr.shape)
            vecs[:, o:o + w] = arr
        put("bada", fm(np.asarray(inp["b_ada"], f32)[0]))
        put("ng", np.concatenate([fm(norm_g[i]) for i in range(6)], axis=1))
        put("bF", fm(b_in[COL_F:COL_Q])); put("bq", fm(b_in[COL_Q:COL_K])); put("bk", fm(b_in[COL_K:COL_V]))
        put("cwq", np.concatenate([fm(cw[t, 0:D]) for t in range(3)], axis=1))
        put("cwk", np.concatenate([fm(cw[t, D:2 * D]) for t in range(3)], axis=1))
        put("cbq", fm(conv_b[0:D])); put("cbk", fm(conv_b[D:2 * D]))
        put("bgf", fm(b_in[COL_BR:COL_BR + D])); put("bgm", fm(b_in[COL_BR + D:]))
        cvec = np.zeros((128, 8, 2), f32)
        cvec[:, :, 0] = fm(c[b])
        cvec[:, :, 1] = fm(c_ctx)
        CS, M2, M3 = _dft_consts(flip)
        cst = np.zeros((128, 128 * 4 + 256 + 128 + 32), f32)
        cst[:, 0:128] = np.eye(128)
        cst[:, 128:256] = maskf
        cst[:, 256:384] = maskb
        cst[:, 512:768] = CS
        cst[:, 768:896] = M2
        cst[:, 896:928] = M3
        m = dict(shared)
        m.update({
            "xT": np.ascontiguousarray(xb.T), "ctxT": np.ascontiguousarray(cb_.T),
            "cvec": cvec.reshape(128, 16), "vecs": vecs, "bg": bgv,
            "wgate_f": np.ascontiguousarray(wgate_f), "wgate_b": wgate_b, "cst": cst, "selc": selc,
        })
        maps.append(m)
    return maps


def kernel(**inputs):
    nc, _ = build()
    maps = make_inputs(inputs)
    res = run_bass_kernel_spmd(nc, maps, core_ids=list(range(8)))
    out = np.zeros((4, SEQ, D), np.float32)
    for core in range(8):
        b, half = core // 2, core % 2
        o = np.asarray(res.results[core]["outT"]).T
        if half == 0:
            out[b, 0:OWN] = o
        else:
            out[b, OWN:] = o[::-1]
    return out
```

```python
import numpy as np
import os as _os
from contextlib import ExitStack
import concourse.bass as bass
import concourse.mybir as mybir
from concourse.bass_utils import run_bass_kernel_spmd

F32 = mybir.dt.float32
BF16 = mybir.dt.bfloat16
AF = mybir.ActivationFunctionType
ALU = mybir.AluOpType

ENGS = ("pe", "act", "dve", "pool", "sp")
EPOCH = 30000

D = 1024
SEQ = 4096
CTX = 256
NT = CTX + SEQ
OWN = 2048
FF = 2816
NJ = 22
EPS = 1e-6
NEG = -30000.0


class Tick:
    __slots__ = ("sem", "val", "know")

    def __init__(self, sem, val, know):
        self.sem = sem
        self.val = val
        self.know = know


class Buf:
    __slots__ = ("name", "w", "r")

    def __init__(self, name=""):
        self.name = name
        self.w = None
        self.r = {}


class Sched:
    def __init__(self, nc, stack, n_dma_sems=48):
        self.nc = nc
        self.stack = stack
        self.q = {e: [] for e in ENGS}
        self.cnt = {e: 0 for e in ENGS}
        self.esems = {e: [] for e in ENGS}
        self.known = {e: {} for e in ENGS}
        self.dsems = [stack.enter_context(nc.semaphore(f"dma{i}")) for i in range(n_dma_sems)]
        self.dcnt = [0] * n_dma_sems
        self.dlast = [None] * n_dma_sems
        self.drr = 0
        self.drr2 = {}

    def _esem(self, eng, idx):
        lst = self.esems[eng]
        while len(lst) <= idx:
            lst.append(self.stack.enter_context(self.nc.semaphore(f"e_{eng}_{len(lst)}")))
        return lst[idx]

    def _collect(self, eng, reads, writes, extra=()):
        kn = self.known[eng]
        waits = {}

        def need(t):
            if t is None:
                return
            if kn.get(t.sem, 0) >= t.val:
                return
            if waits.get(t.sem, (0, None))[0] < t.val:
                waits[t.sem] = (t.val, t)

        for b in reads:
            need(b.w)
        for b in writes:
            need(b.w)
            for t in b.r.values():
                need(t)
        for t in extra:
            need(t)
        items = sorted(waits.items(), key=lambda kv: -kv[1][0])
        final = []
        for sem, (val, t) in items:
            if kn.get(sem, 0) >= val:
                continue
            final.append((sem, val))
            kn[sem] = val
            for s2, v2 in t.know.items():
                if kn.get(s2, 0) < v2:
                    kn[s2] = v2
        return final

    def op(self, eng, fn, reads=(), writes=(), extra=()):
        waits = self._collect(eng, reads, writes, extra)
        c = self.cnt[eng]
        sem = self._esem(eng, c // EPOCH)
        val = c % EPOCH + 1
        self.cnt[eng] = c + 1
        t = Tick(sem, val, dict(self.known[eng]))
        for b in reads:
            b.r[sem] = t
        for b in writes:
            b.w = t
            b.r = {}
        self.q[eng].append((waits, fn, sem, 1))
        return t

    def dma(self, eng, fn, reads=(), writes=(), extra=()):
        n = len(self.dsems)
        lo, hi = (0, n // 3) if eng == "pool" else (n // 3, n)
        rr = self.drr2.get(eng, lo)
        i = rr
        self.drr2[eng] = lo + (rr + 1 - lo) % (hi - lo)
        ex = list(extra)
        if self.dlast[i] is not None:
            ex.append(self.dlast[i])
        waits = self._collect(eng, reads, writes, ex)
        self.dcnt[i] += 1
        sem = self.dsems[i]
        t = Tick(sem, 16 * self.dcnt[i], dict(self.known[eng]))
        self.dlast[i] = t
        for b in reads:
            b.r[sem] = t
        for b in writes:
            b.w = t
            b.r = {}
        self.q[eng].append((waits, fn, sem, 16))
        return t

    def wait_all(self, eng, ticks):
        waits = self._collect(eng, (), (), ticks)
        self.q[eng].append((waits, None, None, 0))

    def barrier(self, skip=()):
        skipset = {(t.sem, t.val) for t in skip}
        ticks = []
        for e in ENGS:
            c = self.cnt[e]
            if c > 0:
                ticks.append(Tick(self._esem(e, (c - 1) // EPOCH), (c - 1) % EPOCH + 1, {}))
        for t in self.dlast:
            if t is not None and (t.sem, t.val) not in skipset:
                ticks.append(t)
        for e in ENGS:
            self.wait_all(e, ticks)

    def emit(self):
        nc = self.nc
        q = self.q

        def run(engobj, lst):
            for waits, fn, sem, amt in lst:
                for s, v in waits:
                    engobj.wait_ge(s, v)
                if fn is not None:
                    ins = fn(engobj)
                    ins.then_inc(sem, amt)

        with nc.Block() as block:
            @block.tensor
            def _(e):
                run(e, q["pe"])

            @block.scalar
            def _(e):
                run(e, q["act"])

            @block.vector
            def _(e):
                run(e, q["dve"])

            @block.gpsimd
            def _(e):
                run(e, q["pool"])

            @block.sync
            def _(e):
                run(e, q["sp"])


class Arena:
    def __init__(self, nc, name, words):
        self.t = nc.alloc_sbuf_tensor(name, [128, words], F32)
        self.words = words
        self.off = 0

    def mark(self):
        return self.off

    def reset(self, m):
        self.off = m

    def f32(self, n):
        a = self.t[:, self.off:self.off + n]
        self.off += n
        assert self.off <= self.words, ("arena overflow", self.off, self.words)
        return a

    def bf16(self, n):
        w = (n + 1) // 2
        a = self.t[:, self.off:self.off + w].bitcast(BF16)
        self.off += w
        assert self.off <= self.words, ("arena overflow", self.off, self.words)
        return a[:, 0:n]


VEC = {}
_o = 0
for _n, _w in [("bada", 72), ("ng", 48), ("bF", 4), ("bq", 8), ("bk", 8), ("cwq", 24), ("cwk", 24),
               ("cbq", 8), ("cbk", 8), ("bgf", 8), ("bgm", 8)]:
    VEC[_n] = (_o, _w)
    _o += _w
NV = _o


def build(stage=99, dbg=False):
    nc = bass.Bass("TRN2", target_bir_lowering=False)
    dt_in = lambda name, shape, dt=F32: nc.dram_tensor(name, shape, dt, kind="ExternalInput").ap()
    xT = dt_in("xT", [D, SEQ])
    ctxT = dt_in("ctxT", [D, CTX])
    cvec = dt_in("cvec", [128, 16])
    w_ada = dt_in("w_ada", [D, 9 * D])
    vecs = dt_in("vecs", [128, NV])
    rowv = dt_in("rowv", [1, 3 * D])
    bg = dt_in("bg", [36, 2])
    w13a = dt_in("w13a", [NJ, 128, 2048])
    w2a = dt_in("w2a", [8, 128, FF])
    w13b = dt_in("w13b", [NJ, 128, 2048])
    w2b = dt_in("w2b", [8, 128, FF])
    wF = dt_in("wF", [D, 512])
    wq = dt_in("wq", [D, D])
    wk = dt_in("wk", [D, D])
    wv = dt_in("wv", [D, D])
    wo = dt_in("wo", [D, D])
    wgf = dt_in("wgf", [D, D])
    wgm = dt_in("wgm", [D, D])
    wgate_f = dt_in("wgate_f", [D, 8])
    wgate_b = dt_in("wgate_b", [D, 72])
    w_four = dt_in("w_four", [512, D])
    w_mproj = dt_in("w_mproj", [D, D])
    w_out = dt_in("w_out", [D, D])
    cst = dt_in("cst", [128, 128 * 4 + 256 + 128 + 32])
    selc = dt_in("selc", [36, 2 * 8 * 128])
    outT = nc.dram_tensor("outT", [D, OWN], F32, kind="ExternalOutput").ap()
    dbg_out = {}

    def dbg_tensor(name, shape, dt=F32):
        dbg_out[name] = nc.dram_tensor(name, shape, dt, kind="ExternalOutput").ap()
        return dbg_out[name]

    dscr = lambda name, shape, dt: nc.dram_tensor(name, shape, dt, kind="Internal").ap()
    H1 = dscr("H1", [D, OWN], F32)
    AB = dscr("AB", [2, SEQ, 512], F32)
    PQ = dscr("PQ", [2, 64, 64, 512], F32)
    KT = dscr("KT", [D, OWN], BF16)
    QT = dscr("QT", [D, OWN], BF16)
    KTOK = dscr("KTOK", [NT, D], BF16)
    VTOK = dscr("VTOK", [NT, D], BF16)
    B_H1, B_AB, B_PQ, B_KT, B_QT, B_KTOK, B_VTOK = [Buf(n) for n in "H1 AB PQ KT QT KTOK VTOK".split()]

    st = ExitStack()
    with st:
        S = Sched(nc, st)
        ps = [st.enter_context(nc.psum_tensor(f"ps{i}", [128, 512], F32)) for i in range(8)]
        Bps = [Buf(f"ps{i}") for i in range(8)]

        cs_t = nc.alloc_sbuf_tensor("cs", [128, 128 * 4 + 256 + 128 + 32], F32)
        ident = cs_t[:, 0:128]
        maskf = cs_t[:, 128:256]
        maskb = cs_t[:, 256:384]
        CS = cs_t[:, 512:768]
        M2 = cs_t[:, 768:896]
        M3 = cs_t[:, 896:928]
        vec_t = nc.alloc_sbuf_tensor("vec", [128, NV], F32)
        mod_t = nc.alloc_sbuf_tensor("mod", [128, 72 * 2], F32)
        der_t = nc.alloc_sbuf_tensor("der", [128, 16 * 8], F32)
        id16_t = nc.alloc_sbuf_tensor("id16", [128, 128], BF16)
        ones16_t = nc.alloc_sbuf_tensor("ones16", [128, 128], BF16)
        u2_t = nc.alloc_sbuf_tensor("u2", [128, 8 * NT], BF16)
        u2 = u2_t[:, :].rearrange("p (k t) -> p k t", k=8)
        B_const = Buf("const")
        B_mod = Buf("mod")
        B_der = Buf("der")
        tiles = [(0, CTX, 1)] + [(CTX + 512 * i, 512, 0) for i in range(8)]
        B_u2 = [Buf(f"u2_{i}") for i in range(9)]

        def vcol(name, i=0, n=1):
            o, w = VEC[name]
            return vec_t[:, o + i:o + i + n]

        DER = {}
        _d = 0
        for nm in ["A0l", "A0c", "S0l", "S0c", "PAl", "PAc", "A2l", "A2c", "S2l", "S2c", "PMl", "A4l", "S4l", "PBl"]:
            DER[nm] = der_t[:, _d * 8:(_d + 1) * 8]
            _d += 1

        arena = Arena(nc, "arena", 34000)

        S.dma("sp", lambda e: e.dma_start(out=cs_t[:, :], in_=cst), writes=[B_const])
        S.dma("sp", lambda e: e.dma_start(out=vec_t[:, :], in_=vecs), writes=[B_const])
        S.op("act", lambda e: e.copy(id16_t[:, :], ident), reads=[B_const], writes=[B_const])
        S.op("pool", lambda e: e.memset(ones16_t[:, :], 1.0), writes=[B_const])

        m0 = arena.mark()
        cv = arena.f32(16)
        scv = arena.f32(16)
        B_cv = Buf("cv")
        S.dma("sp", lambda e: e.dma_start(out=cv, in_=cvec), writes=[B_cv])
        S.op("act", lambda e: e.activation(scv, cv, AF.Silu), reads=[B_cv], writes=[B_cv])
        wad = [arena.f32(8 * 1024) for _ in range(2)]
        B_wad = [Buf("wad0"), Buf("wad1")]
        w_ada_v = w_ada.rearrange("(k p) n -> p k n", p=128)
        modps = ps[7][:, 0:144]
        for mi in range(9):
            sl = mi % 2
            wv_ = wad[sl].rearrange("p (k n) -> p k n", k=8)
            S.dma("sp", (lambda e, wv_=wv_, mi=mi: e.dma_start(out=wv_, in_=w_ada_v[:, :, mi * 1024:(mi + 1) * 1024])),
                  writes=[B_wad[sl]])
            for dc in range(8):
                def fn(e, wv_=wv_, mi=mi, dc=dc):
                    for k in range(8):
                        ins = e.matmul(modps[:, (mi * 8 + dc) * 2:(mi * 8 + dc) * 2 + 2],
                                       wv_[:, k, dc * 128:(dc + 1) * 128],
                                       scv[:, k * 2:k * 2 + 2], start=(k == 0), stop=(k == 7))
                    return ins
                S.op("pe", fn, reads=[B_wad[sl], B_cv], writes=[Bps[7]])
        modv = mod_t[:, :].rearrange("p (m j) -> p m j", j=2)
        modpsv = modps.rearrange("p (m j) -> p m j", j=2)
        bada = vcol("bada", 0, 72)
        for j in range(2):
            S.op("dve", (lambda e, j=j: e.tensor_tensor(modv[:, :, j], modpsv[:, :, j], bada, ALU.add)),
                 reads=[Bps[7], B_const], writes=[B_mod])

        def modc(mi, j):
            return modv[:, mi * 8:(mi + 1) * 8, j]

        def ng(i):
            return vcol("ng", i * 8, 8)

        def der_scale(name, mi, gi, j):
            S.op("dve", lambda e: e.scalar_tensor_tensor(DER[name], modc(mi, j), 1.0, ng(gi), ALU.add, ALU.mult),
                 reads=[B_mod, B_const], writes=[B_der])

        def der_gate(name, mi, gi, j, f):
            S.op("dve", lambda e: e.scalar_tensor_tensor(DER[name], modc(mi, j), f, ng(gi), ALU.mult, ALU.mult),
                 reads=[B_mod, B_const], writes=[B_der])

        def der_copy(name, mi, j):
            S.op("dve", lambda e: e.tensor_copy(DER[name], modc(mi, j)), reads=[B_mod], writes=[B_der])

        der_scale("A0l", 1, 0, 0); der_scale("A0c", 1, 0, 1)
        der_copy("S0l", 0, 0); der_copy("S0c", 0, 1)
        der_gate("PAl", 2, 1, 0, 0.5); der_gate("PAc", 2, 1, 1, 0.5)
        der_scale("A2l", 4, 2, 0); der_scale("A2c", 4, 2, 1)
        der_copy("S2l", 3, 0); der_copy("S2c", 3, 1)
        der_gate("PMl", 5, 3, 0, 1.0)
        der_scale("A4l", 7, 4, 0); der_copy("S4l", 6, 0); der_gate("PBl", 8, 5, 0, 0.5)
        S.barrier()
        arena.reset(m0)

        def rstd_from(sq_tile, B_sq, W, rstd, B_rstd, pbank):
            def fn(e):
                for k in range(8):
                    ins = e.matmul(ps[pbank][:, 0:W], ones16_t[:, :], sq_tile[:, k, 0:W], start=(k == 0), stop=(k == 7))
                return ins
            S.op("pe", fn, reads=[B_sq, B_const], writes=[Bps[pbank]])
            S.op("act", lambda e: e.activation(rstd[:, 0:W], ps[pbank][:, 0:W], AF.Ln, bias=EPS, scale=1.0 / D),
                 reads=[Bps[pbank]], writes=[B_rstd])
            S.op("act", lambda e: e.activation(rstd[:, 0:W], rstd[:, 0:W], AF.Exp, scale=-0.5),
                 reads=[B_rstd], writes=[B_rstd])

        def norm_mod_thunks(src, B_src, W, rstd, B_rstd, A, Sh, dst_fn, B_dst, tmp, B_tmp):
            def one(k):
                t = tmp[k % 2]
                bt = B_tmp[k % 2]
                S.op("dve", (lambda e: e.tensor_tensor(t[:, 0:W], src[:, k, 0:W], rstd[:, 0:W], ALU.mult)),
                     reads=[B_src, B_rstd], writes=[bt])
                S.op("act", (lambda e: e.activation(dst_fn(k), t[:, 0:W], AF.Identity, bias=Sh[:, k:k + 1], scale=A[:, k:k + 1])),
                     reads=[bt, B_der], writes=[B_dst])
            return [(lambda k=k: one(k)) for k in range(8)]

        def norm_mod(src, B_src, W, rstd, B_rstd, A, Sh, dst_fn, B_dst, tmp, B_tmp):
            for k in range(8):
                t = tmp[k % 2]
                bt = B_tmp[k % 2]
                S.op("dve", (lambda e, k=k, t=t: e.tensor_tensor(t[:, 0:W], src[:, k, 0:W], rstd[:, 0:W], ALU.mult)),
                     reads=[B_src, B_rstd], writes=[bt])
                S.op("act", (lambda e, k=k, t=t: e.activation(dst_fn(k), t[:, 0:W], AF.Identity,
                                                              bias=Sh[:, k:k + 1], scale=A[:, k:k + 1])),
                     reads=[bt, B_der], writes=[B_dst])

        WC = {}

        def wload(dst, B_dst, src_f32, cache, key, ncols):
            if cache is None:
                S.dma("pool", lambda e: e.dma_start(out=dst, in_=src_f32), writes=[B_dst])
                return
            k = (cache,) + key
            if k not in WC:
                sc_ = nc.dram_tensor("wc_" + "_".join(str(z) for z in k), [128, ncols], BF16, kind="Internal").ap()
                WC[k] = (sc_, Buf("wc"))
                S.dma("pool", lambda e: e.dma_start(out=dst, in_=src_f32), writes=[B_dst])
                dflat = dst if len(dst.shape) == 2 else dst.rearrange("p a b -> p (a b)")
                S.dma("sp", lambda e: e.dma_start(out=sc_, in_=dflat), reads=[B_dst], writes=[WC[k][1]])
            else:
                sc_, bsc = WC[k]
                dflat = dst if len(dst.shape) == 2 else dst.rearrange("p a b -> p (a b)")
                S.dma("sp", lambda e: e.dma_start(out=dflat, in_=sc_), reads=[bsc], writes=[B_dst])

        def ffn(src, B_src, W, rstd_pre, B_rstd_pre, A, Sh, PG, w13r, w2r, bufs, cache, do_pre=True, hook1=None, hook2=None, defer_epi=False):
            (sq, B_sq, u, B_u, g, B_g, y, B_y, tmp, B_tmp, rs2, B_rs2, wb13, B_wb13, wb2, B_wb2, sa, B_sa) = bufs
            if do_pre:
                norm_mod(src, B_src, W, rstd_pre, B_rstd_pre, A, Sh, lambda k: u[:, k, 0:W], B_u, tmp, B_tmp)
            n13 = len(wb13)
            for j in range(NJ):
                sl = j % n13
                wbv = wb13[sl].rearrange("p (k c) -> p k c", k=8)
                wload(wb13[sl], B_wb13[sl], w13r[j], cache, ("w13", j), 2048)
                pa = j % 2
                pb = 2 + j % 2

                def fa(e, wbv=wbv, pa=pa):
                    for k in range(8):
                        ins = e.matmul(ps[pa][:, 0:W], wbv[:, k, 0:128], u[:, k, 0:W], start=(k == 0), stop=(k == 7))
                    return ins

                def fb(e, wbv=wbv, pb=pb):
                    for k in range(8):
                        ins = e.matmul(ps[pb][:, 0:W], wbv[:, k, 128:256], u[:, k, 0:W], start=(k == 0), stop=(k == 7))
                    return ins
                S.op("pe", fa, reads=[B_wb13[sl], B_u], writes=[Bps[pa]])
                S.op("pe", fb, reads=[B_wb13[sl], B_u], writes=[Bps[pb]])
                s2 = j % 2
                S.op("act", (lambda e, pa=pa, s2=s2: e.activation(sa[s2][:, 0:W], ps[pa][:, 0:W], AF.Silu)),
                     reads=[Bps[pa]], writes=[B_sa[s2]])
                S.op("dve", (lambda e, pb=pb, s2=s2, j=j: e.tensor_tensor(g[:, j, 0:W], sa[s2][:, 0:W], ps[pb][:, 0:W], ALU.mult)),
                     reads=[B_sa[s2], Bps[pb]], writes=[B_g])
                if hook1 is not None:
                    hook1(j)
            n2 = len(wb2)
            HJ = NJ // 2
            for i in range(8):
                halves = []
                for hf in range(2):
                    sl = (2 * i + hf) % n2
                    wload(wb2[sl], B_wb2[sl], w2r[i][:, hf * HJ * 128:(hf + 1) * HJ * 128], cache, ("w2", i, hf), HJ * 128)
                    halves.append((wb2[sl].rearrange("p (j c) -> p j c", j=HJ), B_wb2[sl]))
                py = 4 + i % 2

                def fy(e, halves=halves, py=py):
                    for j in range(NJ):
                        wv_ = halves[j // HJ][0]
                        ins = e.matmul(ps[py][:, 0:W], wv_[:, j % HJ, :], g[:, j, 0:W], start=(j == 0), stop=(j == NJ - 1))
                    return ins
                S.op("pe", fy, reads=[halves[0][1], halves[1][1], B_g], writes=[Bps[py]])
                S.op("act", (lambda e, py=py, i=i: e.copy(y[:, i, 0:W], ps[py][:, 0:W])), reads=[Bps[py]], writes=[B_y])
                S.op("act", (lambda e, py=py, i=i: e.activation(sq[:, i, 0:W], ps[py][:, 0:W], AF.Square)),
                     reads=[Bps[py]], writes=[B_sq])
                if hook2 is not None:
                    hook2(i)
            rstd_from(sq, B_sq, W, rs2, B_rs2, 7)

            def resid(i):
                t = tmp[i % 2]
                bt = B_tmp[i % 2]
                S.op("dve", (lambda e: e.tensor_tensor(t[:, 0:W], y[:, i, 0:W], rs2[:, 0:W], ALU.mult)),
                     reads=[B_y, B_rs2], writes=[bt])
                S.op("dve", (lambda e: e.scalar_tensor_tensor(src[:, i, 0:W], t[:, 0:W], PG[:, i:i + 1],
                                                              src[:, i, 0:W], ALU.mult, ALU.add)),
                     reads=[bt, B_der], writes=[B_src])
            thunks = [(lambda i=i: resid(i)) for i in range(8)]
            if defer_epi:
                return thunks
            for th in thunks:
                th()
            return []

        def alloc_ffn_bufs():
            sq = arena.bf16(8 * 512).rearrange("p (k t) -> p k t", k=8)
            u = arena.bf16(8 * 512).rearrange("p (k t) -> p k t", k=8)
            g = arena.bf16(NJ * 512).rearrange("p (k t) -> p k t", k=NJ)
            y = arena.f32(8 * 512).rearrange("p (k t) -> p k t", k=8)
            tmp = [arena.f32(512) for _ in range(2)]
            rs2 = arena.f32(512)
            wb13 = [arena.bf16(2048) for _ in range(3)]
            wb2 = [arena.bf16(FF // 2) for _ in range(4)]
            sa = [arena.f32(512) for _ in range(2)]
            return (sq, Buf("sq"), u, Buf("u"), g, Buf("g"), y, Buf("y"), tmp, [Buf("t0"), Buf("t1")],
                    rs2, Buf("rs2"), wb13, [Buf("wb13_%d" % i) for i in range(3)], wb2, [Buf("wb2_%d" % i) for i in range(4)],
                    sa, [Buf("sa0"), Buf("sa1")])

        mA = arena.mark()
        xt = [arena.f32(8 * 512).rearrange("p (k t) -> p k t", k=8) for _ in range(2)]
        B_xt = [Buf("xt0"), Buf("xt1")]
        rs1 = arena.f32(512)
        B_rs1 = Buf("rs1")
        fb = alloc_ffn_bufs()
        tmp, B_tmp, u_, B_u_ = fb[8], fb[9], fb[2], fb[3]
        sqx = arena.bf16(8 * 512).rearrange("p (k t) -> p k t", k=8)
        B_sqx = Buf("sqx")
        xT_v = xT.rearrange("(k p) t -> p k t", p=128)
        ctxT_v = ctxT.rearrange("(k p) t -> p k t", p=128)
        if dbg:
            d_u2 = dbg_tensor("d_u2", [128, 8 * NT], BF16)
        ntile = len(tiles) if stage >= 1 else 0

        def a1_load(ti):
            c0, W, j = tiles[ti]
            x = xt[ti % 2]
            src = ctxT_v[:, :, 0:W] if j == 1 else xT_v[:, :, c0 - CTX:c0 - CTX + W]
            S.dma("pool", lambda e: e.dma_start(out=x[:, :, 0:W], in_=src), writes=[B_xt[ti % 2]])

        def sq_thunks(x, bx, W):
            return [(lambda k=k: S.op("act", (lambda e: e.activation(sqx[:, k, 0:W], x[:, k, 0:W], AF.Square)), reads=[bx], writes=[B_sqx]))
                    for k in range(8)]

        def a1_pre_thunks(ti):
            c0, W, j = tiles[ti]
            x = xt[ti % 2]; bx = B_xt[ti % 2]
            sfx = "c" if j == 1 else "l"
            th = sq_thunks(x, bx, W)
            th.append(lambda: rstd_from(sqx, B_sqx, W, rs1, B_rs1, 6))
            th += norm_mod_thunks(x, bx, W, rs1, B_rs1, DER["A0" + sfx], DER["S0" + sfx], lambda k: u_[:, k, 0:W], B_u_, tmp, B_tmp)
            return th

        def a1_epi2_thunks(ti):
            c0, W, j = tiles[ti]
            x = xt[ti % 2]; bx = B_xt[ti % 2]
            sfx = "c" if j == 1 else "l"
            th = []
            if 1 <= ti <= 4:
                o0 = c0 - CTX
                th.append(lambda: S.dma("pool", lambda e: e.dma_start(out=H1.rearrange("(k p) t -> p k t", p=128)[:, :, o0:o0 + 512], in_=x[:, :, :]),
                                        reads=[bx], writes=[B_H1]))
            th += sq_thunks(x, bx, W)
            th.append(lambda: rstd_from(sqx, B_sqx, W, rs1, B_rs1, 6))
            th += norm_mod_thunks(x, bx, W, rs1, B_rs1, DER["A2" + sfx], DER["S2" + sfx], (lambda k: u2[:, k, c0:c0 + W]), B_u2[ti], tmp, B_tmp)
            return th

        pend1 = []
        pend2 = []
        if ntile:
            a1_load(0)
            for th in a1_pre_thunks(0):
                th()
            if ntile > 1:
                a1_load(1)
        for ti in range(ntile):
            c0, W, j = tiles[ti]
            sfx = "c" if j == 1 else "l"
            if ti + 1 < ntile:
                pend2 = a1_pre_thunks(ti + 1)

            def hook1(j_):
                n = 2 if len(pend1) > (NJ - 1 - j_) else 1
                for _ in range(n):
                    if pend1:
                        pend1.pop(0)()

            def hook2(i_):
                while pend1:
                    pend1.pop(0)()
                if i_ >= 1:
                    for _ in range(3):
                        if pend2:
                            pend2.pop(0)()
            epi1 = ffn(xt[ti % 2], B_xt[ti % 2], W, None, None, None, None, DER["PA" + sfx], w13a, w2a, fb, "A",
                       do_pre=False, hook1=hook1, hook2=hook2, defer_epi=True)
            while pend1:
                pend1.pop(0)()
            while pend2:
                pend2.pop(0)()
            pend1 = list(epi1) + a1_epi2_thunks(ti)
            if ti + 2 < ntile:
                pend1.append(lambda ti=ti: a1_load(ti + 2))
        while pend1:
            pend1.pop(0)()
        S.barrier()
        arena.reset(mA)
        if dbg:
            S.dma("sp", lambda e: e.dma_start(out=d_u2, in_=u2_t[:, :]), reads=B_u2)
            d_h1 = dbg_tensor("d_h1", [D, OWN])
            S.dma("sp", lambda e: e.dma_start(out=d_h1, in_=H1), reads=[B_H1])

        env = dict(locals())
        if stage >= 2:
            build_rest(env)
        S.barrier()
        S.emit()
    return nc, dbg_out


def build_rest3(g_, L):
    AX = mybir.AxisListType
    nc = g_["nc"]; S = g_["S"]; ps = g_["ps"]; Bps = g_["Bps"]; arena = g_["arena"]
    u2 = g_["u2"]; B_u2 = g_["B_u2"]; vcol = g_["vcol"]; DER = g_["DER"]
    id16 = g_["id16"]; B_const = g_["B_const"]; mm = g_["mm"]
    H1, HS = g_["H1"], g_["HS"]; B_H1 = g_["B_H1"]; B_HS = g_["B_HS"]
    yfT, B_yfT, mB = g_["yfT"], g_["B_yfT"], g_["mB"]
    rowv = g_["rowv"]; outT = g_["outT"]
    ffn = g_["ffn"]; rstd_from = g_["rstd_from"]; wload = g_["wload"]
    kp = lambda ap: ap.rearrange("(k p) n -> p k n", p=128)
    A = arena.t
    arena.reset(mB)
    tmp = [A[:, 0:512], A[:, 512:1024]]; B_tmp = [Buf("ct0"), Buf("ct1")]
    rs2 = A[:, 1024:1536]; B_rs2 = Buf("crs2")
    yreg = A[:, 5816:9912]
    B_yreg = Buf("yreg")
    y = yreg.rearrange("p (k t) -> p k t", k=8)
    hs_t = yreg[:, 0:1024]; o_sb = yreg[:, 1024:2048]; sig = yreg[:, 2048:3072]; sqh = yreg[:, 3072:4096]
    B_hsC = Buf("c_hs"); B_osb = Buf("c_osb"); B_sig = Buf("c_sig"); B_sqh = Buf("c_sqh")
    bo16 = A[:, 5816 + 1024:5816 + 1536].bitcast(BF16)
    ones16 = g_["ones16"]
    bo_bc = A[:, 9912:10936]; hg_bc = A[:, 10936:11960]
    B_bc = Buf("cbc")
    x = arena.f32(4096).rearrange("p (k t) -> p k t", k=8); B_x = Buf("cx")
    rs1 = arena.f32(512); B_rs1 = Buf("crs1")
    sq = arena.bf16(4096).rearrange("p (k t) -> p k t", k=8); B_sq = Buf("csq")
    u = arena.bf16(4096).rearrange("p (k t) -> p k t", k=8); B_u = Buf("cu")
    hmT = sq; yT = u
    sa = [arena.f32(512), arena.f32(512)]; B_sa = [Buf("csa0"), Buf("csa1")]
    mX = arena.mark()
    g = arena.bf16(NJ * 512).rearrange("p (k t) -> p k t", k=NJ); B_g = Buf("cg")
    wb13m = [arena.bf16(2048), arena.bf16(2048), arena.bf16(2048)]; B_wb13m = [Buf("cw13_0"), Buf("cw13_1"), Buf("cw13_2")]
    wb2m = arena.bf16(FF); B_wb2m = Buf("cw2")
    wb2x = A[:, 11992:11992 + 704].bitcast(BF16), A[:, 11992 + 704:11992 + 1408].bitcast(BF16)
    arena.reset(mX)
    wsl = [arena.bf16(8 * 512).rearrange("p (k n) -> p k n", k=8) for _ in range(4)]; B_wsl = [Buf("wsl%d" % i) for i in range(4)]
    hm = arena.bf16(1024); B_hm = Buf("hm")
    ol = A[:, mX + 4096:mX + 8192].rearrange("p (k t) -> p k t", k=8)
    B_ol = [B_wsl[2], B_wsl[3]]
    ss = rs2[:, 0:8]
    fb = (sq, B_sq, u, B_u, g, B_g, y, B_yreg, tmp, B_tmp, rs2, B_rs2, wb13m, B_wb13m,
          [wb2m[:, 0:FF // 2], wb2m[:, FF // 2:FF], wb2x[0], wb2x[1]], [Buf('cw2a'), Buf('cw2b'), Buf('cw2c'), Buf('cw2d')], sa, B_sa)
    xflat = A[:, mB:mB + 4096]
    rv = xflat[:, 0:3072]; ones1 = xflat[:, 3072:3200]
    S.dma("sp", lambda e: e.dma_start(out=rv[0:1, :], in_=rowv), writes=[B_x])
    S.op("pool", lambda e: e.memset(ones1[0:1, :], 1.0), writes=[B_x])
    for (dst, seg) in ((bo_bc, 1), (hg_bc, 2)):
        for hh in range(2):
            mm(ps[6][:, 0:512], [(ones1[0:1, :], rv[0:1, seg * 1024 + hh * 512:seg * 1024 + hh * 512 + 512])], reads=[B_x], writes=[Bps[6]])
            S.op("act", (lambda e, hh=hh, dst=dst: e.copy(dst[:, hh * 512:(hh + 1) * 512], ps[6][:, 0:512])), reads=[Bps[6]], writes=[B_bc])
    S.barrier()
    w_four, w_mproj, w_out, wo, wgf, wgm = [g_[n] for n in "w_four w_mproj w_out wo wgf wgm".split()]
    w13b, w2b = g_["w13b"], g_["w2b"]
    H1v = H1.rearrange("(k p) t -> p k t", p=128)
    outv = outT.rearrange("(k p) t -> p k t", p=128)
    for T in range(4):
        t0 = 512 * T
        c0 = CTX + t0
        S.op("act", lambda e: e.copy(bo16[0:1, :], bo_bc[0:1, :]), reads=[B_bc], writes=[B_osb])
        for hh in range(2):
            wload(wsl[hh], B_wsl[hh], kp(wo)[:, :, hh * 512:(hh + 1) * 512], "C", ("wo", hh), 4096)
        for ch in range(4):
            cc = c0 + 128 * ch
            ob = t0 + 128 * ch
            S.dma("sp", (lambda e, ob=ob: e.dma_start(out=hs_t, in_=HS[ob:ob + 128, :])), reads=[B_HS[ob // 128]], writes=[B_hsC])
            S.op("act", lambda e: e.activation(sqh, hs_t, AF.Square), reads=[B_hsC], writes=[B_sqh])
            S.op("dve", lambda e: e.reduce_sum(ss[:, 0:4], sqh.rearrange("p (h e) -> p h e", h=4), AX.X), reads=[B_sqh], writes=[B_rs2])
            for hh in range(2):
                mm(ps[hh][:, 0:512], [(u2[:, k, cc:cc + 128], wsl[hh][:, k, :]) for k in range(8)]
                   + [(ones16[0:1, 0:128], bo16[0:1, hh * 512:(hh + 1) * 512])],
                   reads=[B_wsl[hh], B_osb, B_const] + B_u2, writes=[Bps[hh]])
                S.op("act", (lambda e, hh=hh: e.activation(sig[:, hh * 512:(hh + 1) * 512], ps[hh][:, 0:512], AF.Sigmoid)),
                     reads=[Bps[hh]], writes=[B_sig])
            S.op("dve", lambda e: e.tensor_tensor(sig, sig, hg_bc, ALU.mult), reads=[B_sig, B_bc], writes=[B_sig])
            S.op("act", lambda e: e.activation(ss[:, 0:4], ss[:, 0:4], AF.Ln, bias=EPS, scale=1.0 / 256), reads=[B_rs2], writes=[B_rs2])
            S.op("act", lambda e: e.activation(ss[:, 0:4], ss[:, 0:4], AF.Exp, scale=-0.5), reads=[B_rs2], writes=[B_rs2])
            for h in range(4):
                S.op("dve", (lambda e, h=h: e.scalar_tensor_tensor(hm[:, h * 256:(h + 1) * 256], hs_t[:, h * 256:(h + 1) * 256], ss[:, h:h + 1],
                                                                  sig[:, h * 256:(h + 1) * 256], ALU.mult, ALU.mult)),
                     reads=[B_hsC, B_sig, B_rs2], writes=[B_hm])
            for half in range(2):
                pb = 2 + half
                def fn(e, half=half, pb=pb):
                    for cq in range(4):
                        c = half * 4 + cq
                        ins = e.matmul(ps[pb][:, cq * 128:(cq + 1) * 128], hm[:, c * 128:(c + 1) * 128], id16[:, :], start=True, stop=True)
                    return ins
                S.op("pe", fn, reads=[B_hm, B_const], writes=[Bps[pb]])
                if half == 0:
                    S.op("act", (lambda e, half=half, pb=pb, ch=ch: e.copy(hmT[:, half * 4:(half + 1) * 4, ch * 128:(ch + 1) * 128],
                                                                          ps[pb][:, 0:512].rearrange("p (c t) -> p c t", c=4))),
                         reads=[Bps[pb]], writes=[B_sq])
                else:
                    S.op("dve", (lambda e, half=half, pb=pb, ch=ch: e.tensor_copy(hmT[:, half * 4:(half + 1) * 4, ch * 128:(ch + 1) * 128],
                                                                                 ps[pb][:, 0:512].rearrange("p (c t) -> p c t", c=4))),
                         reads=[Bps[pb]], writes=[B_sq])
        for hh in range(2):
            cs_ = slice(hh * 512, (hh + 1) * 512)
            wload(wsl[0][:, 0:4, :], B_wsl[0], kp(w_four)[:, :, cs_], "C", ("w4", hh), 2048)
            wload(wsl[1], B_wsl[1], kp(w_mproj)[:, :, cs_], "C", ("wm", hh), 4096)
            wload(wsl[2], B_wsl[2], kp(wgf)[:, :, cs_], "C", ("wgf", hh), 4096)
            wload(wsl[3], B_wsl[3], kp(wgm)[:, :, cs_], "C", ("wgm", hh), 4096)
            for ii in range(4):
                i = hh * 4 + ii
                cw = slice(ii * 128, (ii + 1) * 128)
                mm(ps[0][:, 0:512], [(wsl[0][:, gq, cw], yfT[:, gq, t0:t0 + 512]) for gq in range(4)], reads=[B_wsl[0], B_yfT], writes=[Bps[0]])
                mm(ps[1][:, 0:512], [(wsl[1][:, k, cw], hmT[:, k, :]) for k in range(8)], reads=[B_wsl[1], B_sq], writes=[Bps[1]])
                mm(ps[2][:, 0:512], [(wsl[2][:, k, cw], u2[:, k, c0:c0 + 512]) for k in range(8)], reads=[B_wsl[2]] + B_u2, writes=[Bps[2]])
                mm(ps[3][:, 0:512], [(wsl[3][:, k, cw], u2[:, k, c0:c0 + 512]) for k in range(8)], reads=[B_wsl[3]] + B_u2, writes=[Bps[3]])
                S.op("act", (lambda e, i=i: e.activation(sa[0], ps[2][:, 0:512], AF.Sigmoid, bias=vcol("bgf", i, 1))), reads=[Bps[2], B_const], writes=[B_sa[0]])
                S.op("act", (lambda e, i=i: e.activation(sa[1], ps[3][:, 0:512], AF.Sigmoid, bias=vcol("bgm", i, 1))), reads=[Bps[3], B_const], writes=[B_sa[1]])
                S.op("dve", lambda e: e.tensor_tensor(sa[0], sa[0], ps[0][:, 0:512], ALU.mult), reads=[Bps[0], B_sa[0]], writes=[B_sa[0]])
                S.op("dve", lambda e: e.tensor_tensor(sa[1], sa[1], ps[1][:, 0:512], ALU.mult), reads=[Bps[1], B_sa[1]], writes=[B_sa[1]])
                S.op("dve", (lambda e, i=i: e.tensor_tensor(yT[:, i, :], sa[0], sa[1], ALU.add)), reads=B_sa, writes=[B_u])
        S.dma("sp", (lambda e, t0=t0: e.dma_start(out=x[:, :, :], in_=H1v[:, :, t0:t0 + 512])), reads=[B_H1], writes=[B_x])
        for hh in range(2):
            wload(wsl[hh], B_wsl[hh], kp(w_out)[:, :, hh * 512:(hh + 1) * 512], "C", ("wout", hh), 4096)
        for i in range(8):
            pb = 4 + i % 2
            mm(ps[pb][:, 0:512], [(wsl[i // 4][:, k, (i % 4) * 128:(i % 4 + 1) * 128], yT[:, k, :]) for k in range(8)],
               reads=[B_wsl[i // 4], B_u], writes=[Bps[pb]])
            S.op("act", (lambda e, i=i, pb=pb: e.copy(ol[:, i, :], ps[pb][:, 0:512])), reads=[Bps[pb]], writes=B_ol)
            S.op("act", (lambda e, i=i, pb=pb: e.activation(sq[:, i, :], ps[pb][:, 0:512], AF.Square)), reads=[Bps[pb]], writes=[B_sq])
        rstd_from(sq, B_sq, 512, rs1, B_rs1, 6)
        for i in range(8):
            t = tmp[i % 2]; bt = B_tmp[i % 2]
            S.op("dve", (lambda e, i=i, t=t: e.tensor_tensor(t, ol[:, i, :], rs1, ALU.mult)), reads=B_ol + [B_rs1], writes=[bt])
            S.op("dve", (lambda e, i=i, t=t: e.scalar_tensor_tensor(x[:, i, :], t, DER["PMl"][:, i:i + 1], x[:, i, :], ALU.mult, ALU.add)),
                 reads=[bt, g_["B_der"]], writes=[B_x])
        S.barrier()
        S.op("act", lambda e: e.activation(sq[:, :, :], x[:, :, :], AF.Square), reads=[B_x], writes=[B_sq])
        rstd_from(sq, B_sq, 512, rs1, B_rs1, 6)
        ffn(x, B_x, 512, rs1, B_rs1, DER["A4l"], DER["S4l"], DER["PBl"], w13b, w2b, fb, "B")
        t_store = S.dma("sp", (lambda e, t0=t0: e.dma_start(out=outv[:, :, t0:t0 + 512], in_=x[:, :, :])), reads=[B_x])
        S.barrier(skip=[t_store])


def build_rest2(env, L):
    AX = mybir.AxisListType
    g_ = dict(env); g_.update(L)
    nc = g_["nc"]; S = g_["S"]; ps = g_["ps"]; Bps = g_["Bps"]; arena = g_["arena"]
    u2 = g_["u2"]; B_u2 = g_["B_u2"]; vcol = g_["vcol"]; DER = g_["DER"]
    ident = g_["ident"]; maskf = g_["maskf"]; maskb = g_["maskb"]; CS = g_["CS"]; M2 = g_["M2"]; M3 = g_["M3"]
    id16 = g_["id16"]; ones16 = g_["ones16"]; B_const = g_["B_const"]; B_der = g_["B_der"]
    sel = g_["sel"]; negsel = g_["negsel"]; mm = g_["mm"]
    H1, AB, PQ, KT, QT, KTOK, VTOK, GROW, HS = [g_[n] for n in "H1 AB PQ KT QT KTOK VTOK GROW HS".split()]
    B_H1, B_AB, B_PQ, B_KT, B_QT, B_KTOK, B_VTOK, B_GROW = [g_["B_" + n] for n in "H1 AB PQ KT QT KTOK VTOK GROW".split()]
    B_HS = g_["B_HS"]
    Rcol, Gcol, Ecol, gend, acol, wkcol, decay, B_cols = [g_[n] for n in "Rcol Gcol Ecol gend acol wkcol decay B_cols".split()]
    yfT, B_yfT, C32, C16, B_C, mB = [g_[n] for n in "yfT B_yfT C32 C16 B_C mB".split()]
    rowv = g_["rowv"]; outT = g_["outT"]
    tiles = g_["tiles"]
    kp = lambda ap: ap.rearrange("(k p) n -> p k n", p=128)

    wbuf = [arena.bf16(8 * 1024).rearrange("p (k n) -> p k n", k=8) for _ in range(2)]
    B_wbuf = [Buf("wbuf0"), Buf("wbuf1")]
    xf = [arena.f32(512) for _ in range(4)]
    B_xf = [Buf("xf%d" % i) for i in range(4)]
    ab_sb2 = [arena.f32(1024).rearrange("p (x g c) -> p x g c", x=2, g=4) for _ in range(2)]
    B_ab2 = [Buf("ab_sb0"), Buf("ab_sb1")]
    Pt2 = [arena.f32(514), arena.f32(514)]
    B_Pt2 = [Buf("Pt0"), Buf("Pt1")]
    acc2 = [arena.f32(512), arena.f32(512)]
    B_acc2 = [Buf("acc0"), Buf("acc1")]
    kTt = arena.bf16(8 * 512).rearrange("p (k t) -> p k t", k=8)
    B_kTt = Buf("kTt")
    tok2 = [arena.bf16(1024), arena.bf16(1024)]
    B_tok2 = [Buf("tok0"), Buf("tok1")]
    tokctr = [0]
    bv_bc = arena.f32(1024)
    ones1 = arena.f32(128)
    rv = arena.f32(1024)
    B_bc = Buf("bc")
    S.dma("sp", lambda e: e.dma_start(out=rv[0:1, :], in_=rowv[:, 0:1024]), writes=[B_bc])
    S.op("pool", lambda e: e.memset(ones1[0:1, :], 1.0), writes=[B_bc])

    def bcast_row(dst, seg):
        for hh in range(2):
            mm(ps[6][:, 0:512], [(ones1[0:1, :], rv[0:1, seg * 1024 + hh * 512:seg * 1024 + hh * 512 + 512])], reads=[B_bc], writes=[Bps[6]])
            S.op("act", (lambda e, hh=hh: e.copy(dst[:, hh * 512:(hh + 1) * 512], ps[6][:, 0:512])), reads=[Bps[6]], writes=[B_bc])
    bcast_row(bv_bc, 0)

    wF = g_["wF"]
    S.dma("pool", lambda e: e.dma_start(out=wbuf[0][:, :, 0:512], in_=kp(wF)), writes=[B_wbuf[0]])
    for i in range(8):
        c0 = CTX + 512 * i
        for g in range(4):
            pb = g % 2
            mm(ps[pb][:, 0:512], [(wbuf[0][:, k, g * 128:(g + 1) * 128], u2[:, k, c0:c0 + 512]) for k in range(8)],
               reads=[B_wbuf[0]] + B_u2, writes=[Bps[pb]])
            S.op("act", (lambda e, g=g, pb=pb: e.activation(xf[g], ps[pb][:, 0:512], AF.Identity, bias=vcol("bF", g, 1))),
                 reads=[Bps[pb], B_const], writes=[B_xf[g]])
        for tb in range(4):
            ab_sb = ab_sb2[tb % 2]; B_ab = B_ab2[tb % 2]
            for g in range(4):
                bank = 2 + g // 2
                mm(ps[bank][:, (g % 2) * 256:(g % 2) * 256 + 256], [(xf[g][:, tb * 128:(tb + 1) * 128], CS)],
                   reads=[B_xf[g], B_const], writes=[Bps[bank]])
            for bi in range(2):
                S.op("dve", (lambda e, bi=bi, ab_sb=ab_sb: e.tensor_copy(ab_sb[:, :, 2 * bi:2 * bi + 2, :],
                                                            ps[2 + bi][:, 0:512].rearrange("p (g x c) -> p x g c", g=2, x=2))),
                     reads=[Bps[2 + bi]], writes=[B_ab])
            tok0 = 512 * i + 128 * tb
            S.dma("sp", (lambda e, tok0=tok0, ab_sb=ab_sb: e.dma_start(out=AB.rearrange("x t f -> t x f")[tok0:tok0 + 128, :, :],
                                                          in_=ab_sb.rearrange("p x g c -> p x (g c)"))), reads=[B_ab], writes=[B_AB])

    def qk_proj(wdram, bname, cwname, cbname, slot, tlist, is_k):
        S.dma("pool", lambda e: e.dma_start(out=wbuf[slot], in_=kp(wdram)), writes=[B_wbuf[slot]])
        pend_silu = [None]
        for (ti, c0, W) in tlist:
            islat = ti >= 1
            left = islat and ti > 1
            right = islat and ti < 8
            for c in range(8):
                pb = c % 2
                Pt = Pt2[c % 2]; B_Pt = B_Pt2[c % 2]; acc = acc2[c % 2]; B_acc = B_acc2[c % 2]
                hb = 5 + c % 2
                mm(ps[pb][:, 0:W], [(wbuf[slot][:, k, c * 128:(c + 1) * 128], u2[:, k, c0:c0 + W]) for k in range(8)],
                   reads=[B_wbuf[slot]] + B_u2, writes=[Bps[pb]])
                S.op("act", (lambda e, c=c, pb=pb, W=W, Pt=Pt: e.activation(Pt[:, 1:W + 1], ps[pb][:, 0:W], AF.Identity, bias=vcol(bname, c, 1))),
                     reads=[Bps[pb], B_const], writes=[B_Pt])
                if left and right:
                    mm(ps[hb][:, 0:2], [(wbuf[slot][:, k, c * 128:(c + 1) * 128], u2[:, k, c0 - 1:c0 + W + 1:W + 1]) for k in range(8)],
                       reads=[B_wbuf[slot]] + B_u2, writes=[Bps[hb]])
                    S.op("act", (lambda e, c=c, Pt=Pt, hb=hb, W=W: e.activation(Pt[:, 0:W + 2:W + 1], ps[hb][:, 0:2], AF.Identity, bias=vcol(bname, c, 1))),
                         reads=[Bps[hb], B_const], writes=[B_Pt])
                else:
                    for hi_, (has, col, dstc) in enumerate(((left, c0 - 1, 0), (right, c0 + W, W + 1))):
                        if has:
                            mm(ps[hb][:, hi_:hi_ + 1], [(wbuf[slot][:, k, c * 128:(c + 1) * 128], u2[:, k, col:col + 1]) for k in range(8)],
                               reads=[B_wbuf[slot]] + B_u2, writes=[Bps[hb]])
                            S.op("act", (lambda e, c=c, dstc=dstc, Pt=Pt, hb=hb, hi_=hi_: e.activation(Pt[:, dstc:dstc + 1], ps[hb][:, hi_:hi_ + 1], AF.Identity, bias=vcol(bname, c, 1))),
                                 reads=[Bps[hb], B_const], writes=[B_Pt])
                        else:
                            S.op("pool", (lambda e, dstc=dstc, Pt=Pt: e.memset(Pt[:, dstc:dstc + 1], 0.0)), writes=[B_Pt])
                S.op("act", (lambda e, c=c, W=W, Pt=Pt, acc=acc: e.activation(acc[:, 0:W], Pt[:, 0:W], AF.Identity, scale=vcol(cwname, c, 1))),
                     reads=[B_Pt, B_const], writes=[B_acc])
                S.op("dve", (lambda e, c=c, W=W, Pt=Pt, acc=acc: e.scalar_tensor_tensor(acc[:, 0:W], Pt[:, 1:W + 1], vcol(cwname, 8 + c, 1), acc[:, 0:W], ALU.mult, ALU.add)),
                     reads=[B_Pt, B_const], writes=[B_acc])
                S.op("dve", (lambda e, c=c, W=W, Pt=Pt, acc=acc: e.scalar_tensor_tensor(acc[:, 0:W], Pt[:, 2:W + 2], vcol(cwname, 16 + c, 1), acc[:, 0:W], ALU.mult, ALU.add)),
                     reads=[B_Pt, B_const], writes=[B_acc])
                if pend_silu[0] is not None:
                    pend_silu[0]()
                pend_silu[0] = (lambda c=c, W=W, acc=acc, B_acc=B_acc: S.op(
                    "act", (lambda e: e.activation(kTt[:, c, 0:W], acc[:, 0:W], AF.Silu, bias=vcol(cbname, c, 1))),
                    reads=[B_acc, B_const], writes=[B_kTt]))
            if pend_silu[0] is not None:
                pend_silu[0]()
                pend_silu[0] = None
            own = 1 <= ti <= 4
            if own:
                o0 = c0 - CTX
                dst = KT if is_k else QT
                S.dma("sp", (lambda e, o0=o0, dst=dst: e.dma_start(out=dst.rearrange("(k p) t -> p k t", p=128)[:, :, o0:o0 + 512], in_=kTt[:, :, :])),
                      reads=[B_kTt], writes=[B_KT if is_k else B_QT])
            if is_k:
                for tb in range(W // 128):
                    tok = tok2[tokctr[0] % 2]; B_tok = B_tok2[tokctr[0] % 2]; tokctr[0] += 1
                    for half in range(2):
                        pb = 3 + half
                        def fn(e, tb=tb, half=half, pb=pb):
                            for cc in range(4):
                                c = half * 4 + cc
                                ins = e.matmul(ps[pb][:, cc * 128:(cc + 1) * 128], kTt[:, c, tb * 128:(tb + 1) * 128], id16[:, :], start=True, stop=True)
                            return ins
                        S.op("pe", fn, reads=[B_kTt, B_const], writes=[Bps[pb]])
                        S.op("dve", (lambda e, half=half, pb=pb, tok=tok: e.tensor_copy(tok[:, half * 512:(half + 1) * 512], ps[pb][:, 0:512])),
                             reads=[Bps[pb]], writes=[B_tok])
                    r0 = c0 + tb * 128
                    S.dma("sp", (lambda e, r0=r0, tok=tok: e.dma_start(out=KTOK[r0:r0 + 128, :], in_=tok)), reads=[B_tok], writes=[B_KTOK])

    tl_all = [(ti, c0, W) for ti, (c0, W, j) in enumerate(tiles)]
    qk_proj(g_["wk"], "bk", "cwk", "cbk", 1, tl_all, True)
    qk_proj(g_["wq"], "bq", "cwq", "cbq", 0, tl_all[1:5], False)
    S.dma("pool", lambda e: e.dma_start(out=wbuf[1], in_=kp(g_["wv"])), writes=[B_wbuf[1]])
    for cb in range(0, NT, 128):
        tok = tok2[tokctr[0] % 2]; B_tok = B_tok2[tokctr[0] % 2]; tokctr[0] += 1
        for half in range(2):
            pb = half + 2 * ((cb // 128) % 2)
            mm(ps[pb][:, 0:512], [(u2[:, k, cb:cb + 128], wbuf[1][:, k, half * 512:(half + 1) * 512]) for k in range(8)],
               reads=[B_wbuf[1]] + B_u2, writes=[Bps[pb]])
            S.op("dve", (lambda e, half=half, pb=pb, tok=tok: e.tensor_tensor(tok[:, half * 512:(half + 1) * 512], ps[pb][:, 0:512],
                                                                     bv_bc[:, half * 512:(half + 1) * 512], ALU.add)),
                 reads=[Bps[pb], B_bc], writes=[B_tok])
        S.dma("sp", (lambda e, cb=cb, tok=tok: e.dma_start(out=VTOK[cb:cb + 128, :], in_=tok)), reads=[B_tok], writes=[B_VTOK])
    S.barrier()
    arena.reset(mB)

    inb2 = [arena.f32(8 * 512).rearrange("p (r f) -> p r f", r=8) for _ in range(2)]
    outb2 = [arena.f32(8 * 512).rearrange("p (r f) -> p r f", r=8) for _ in range(2)]
    B_inb2 = [Buf("inb0"), Buf("inb1")]; B_outb2 = [Buf("outb0"), Buf("outb1")]
    ABv = AB.rearrange("x (r c) f -> x c r f", c=64)
    PQw = PQ
    for rb in range(8):
        inb = inb2[rb % 2]; outb = outb2[rb % 2]; B_inb = B_inb2[rb % 2]; B_outb = B_outb2[rb % 2]
        for x in range(2):
            S.dma("sp", (lambda e, rb=rb, x=x, inb=inb: e.dma_start(out=inb[64 * x:64 * x + 64, :, :], in_=ABv[x, :, rb * 8:rb * 8 + 8, :])),
                  reads=[B_AB], writes=[B_inb])
        for r in range(8):
            pb = r % 4
            mm(ps[pb][:, 0:512], [(M2, inb[:, r, :])], reads=[B_inb, B_const], writes=[Bps[pb]])
            if r % 2 == 0:
                S.op("act", (lambda e, r=r, pb=pb, outb=outb: e.copy(outb[:, r, :], ps[pb][:, 0:512])), reads=[Bps[pb]], writes=[B_outb])
            else:
                S.op("dve", (lambda e, r=r, pb=pb, outb=outb: e.tensor_copy(outb[:, r, :], ps[pb][:, 0:512])), reads=[Bps[pb]], writes=[B_outb])
        for x in range(2):
            S.dma("sp", (lambda e, rb=rb, x=x, outb=outb: e.dma_start(out=PQw[x, :, rb * 8:rb * 8 + 8, :], in_=outb[64 * x:64 * x + 64, :, :])),
                  reads=[B_outb], writes=[B_PQ])
    PQr = PQ.rearrange("x kc r f -> x r kc f")
    for kb in range(8):
        inb = inb2[kb % 2]; B_inb = B_inb2[kb % 2]
        for x in range(2):
            S.dma("sp", (lambda e, kb=kb, x=x, inb=inb: e.dma_start(out=inb[64 * x:64 * x + 64, :, :], in_=PQr[x, :, kb * 8:kb * 8 + 8, :])),
                  reads=[B_PQ], writes=[B_inb])
        for g in range(4):
            pb = 4 + g
            def fn(e, g=g, pb=pb, inb=inb):
                for kc in range(8):
                    ins = e.matmul(ps[pb][:, kc * 32:(kc + 1) * 32], inb[:, kc, g * 128:(g + 1) * 128], M3, start=True, stop=True)
                return ins
            S.op("pe", fn, reads=[B_inb, B_const], writes=[Bps[pb]])
            S.op("act", (lambda e, g=g, pb=pb, kb=kb: e.copy(yfT[:, g, :].rearrange("p (kr kc) -> p kc kr", kc=64)[:, kb * 8:kb * 8 + 8, :],
                                                          ps[pb][:, 0:256].rearrange("p (kc kr) -> p kc kr", kr=32))),
                 reads=[Bps[pb]], writes=[B_yfT])
    S.barrier()
    arena.reset(mB)

    NLS = 3
    NHS = 2
    LD = []
    for i in range(2 * NLS):
        d_ = dict(ktok=arena.bf16(1024), vaug=arena.bf16(4 * 258).rearrange("p (h e) -> p h e", h=4),
                  kT=arena.bf16(1024).rearrange("p (k t) -> p k t", k=8), qT=arena.bf16(1024).rearrange("p (k t) -> p k t", k=8),
                  grow=arena.f32(128), B_ld=Buf("ld%d" % i), B_ldo=Buf("ldo%d" % i))
        S.op("pool", (lambda e, v=d_["vaug"]: e.memset(v[:, :, :], 1.0)), writes=[d_["B_ld"]])
        LD.append(d_)
    HSB = [dict(hs=arena.f32(1024), B_hs=Buf("hs%d" % i)) for i in range(4)]
    HT = []
    B_pP2s = Buf("pP2"); B_pP1s = Buf("pP1")
    for i in range(NHS):
        HT.append(dict(wT=arena.f32(128), STb=arena.bf16(128), P2sb=arena.f32(257), hn=arena.f32(257), dd=arena.f32(2), kw=arena.bf16(256),
                       B_wT=Buf("wT%d" % i), B_ST=Buf("ST%d" % i), B_P2=Buf("P2sb%d" % i), B_hn=Buf("hn%d" % i), B_dd=Buf("dd%d" % i),
                       B_kw=Buf("kw%d" % i),
                       pST=ps[i][:, 0:128], pD=ps[i][:, 128:256], pP2=ps[2][:, 0:257], pP1=ps[3][:, 0:257],
                       pCU=[ps[4 + i][:, 0:257], ps[6 + i][:, 0:257]],
                       B_pSD=Buf("pSD%d" % i), B_pP2=B_pP2s, B_pP1=B_pP1s, B_pCU=[Buf("pCU0_%d" % i), Buf("pCU1_%d" % i)]))
    B_C32 = [[Buf('C32_%d_%d' % (q, c)) for c in range(2)] for q in range(8)]
    B_C16 = [[Buf('C16_%d_%d' % (q, c)) for c in range(2)] for q in range(8)]
    steps = []
    fw = [(0, 0, None), (1, 128, None)] + [(2 + i, CTX + 128 * i, 128 * i) for i in range(16)]
    bw = [(0, 128, None), (1, 0, None)] + [(2 + i, CTX + 128 * (31 - i), (128 * (31 - i) if 31 - i <= 15 else None)) for i in range(32)]
    for i in range(34):
        if i < 18:
            steps.append((0,) + fw[i])
        steps.append((1,) + bw[i])
    mask16 = [arena.bf16(128), arena.bf16(128)]
    S.op("act", lambda e: e.copy(mask16[0], maskf), reads=[B_const], writes=[B_const])
    S.op("act", lambda e: e.copy(mask16[1], maskb), reads=[B_const], writes=[B_const])

    def emit_loads(si):
        (dr, sc, cb, ob) = steps[si]
        L_ = LD[dr * NLS + sc % NLS]
        ktok_t, vaug, kT_t, qT_t, grow_t = L_["ktok"], L_["vaug"], L_["kT"], L_["qT"], L_["grow"]
        B_ld, B_ldo = L_["B_ld"], L_["B_ldo"]
        S.dma("sp", (lambda e: e.dma_start(out=ktok_t, in_=KTOK[cb:cb + 128, :])), reads=[B_KTOK], writes=[B_ld])
        S.dma("sp", (lambda e: e.dma_start(out=vaug[:, :, 0:256], in_=VTOK[cb:cb + 128, :].rearrange("t (h e) -> t h e", h=4))),
              reads=[B_VTOK], writes=[B_ld])
        if ob is not None:
            S.dma("sp", (lambda e: e.dma_start(out=kT_t, in_=KT.rearrange("(k p) t -> p k t", p=128)[:, :, ob:ob + 128])), reads=[B_KT], writes=[B_ldo])
            S.dma("sp", (lambda e: e.dma_start(out=qT_t, in_=QT.rearrange("(k p) t -> p k t", p=128)[:, :, ob:ob + 128])), reads=[B_QT], writes=[B_ldo])
            S.dma("sp", (lambda e: e.dma_start(out=grow_t[0:36, :], in_=GROW[:, cb:cb + 128])), reads=[B_GROW], writes=[B_ldo])

    def emit_load_hs(si):
        (dr, sc, cb, ob) = steps[si]
        if ob is not None and dr == 1:
            H2 = HSB[dr * 2 + sc % 2]
            S.dma("sp", (lambda e: e.dma_start(out=H2["hs"], in_=HS[ob:ob + 128, :])), reads=[B_HS[ob // 128]], writes=[H2["B_hs"]])

    items = []
    for si, (dr, sc, cb, ob) in enumerate(steps):
        for h in range(4):
            items.append((si, h, len(items)))

    def ctx_of(it):
        si, h, n = it
        (dr, sc, cb, ob) = steps[si]
        L2 = dict(LD[dr * NLS + sc % NLS]); L2.update(HSB[dr * 2 + sc % 2])
        return dr, sc, cb, ob, h, dr * 4 + h, (sc if dr == 0 else 18 + sc), L2, HT[n % NHS]

    def emit_A(it):
        dr, sc, cb, ob, h, q, ci, L_, H_ = ctx_of(it)
        kT_t, qT_t, grow_t, ktok_t = L_["kT"], L_["qT"], L_["grow"], L_["ktok"]
        wT, STb, kw, pST, pD = H_["wT"], H_["STb"], H_["kw"], H_["pST"], H_["pD"]
        S.op("pool", (lambda e: e.tensor_scalar(kw, ktok_t[:, h * 256:(h + 1) * 256], wkcol[:, q, sc:sc + 1], 0.0625, ALU.mult, ALU.mult)),
             reads=[L_["B_ld"], B_cols], writes=[H_["B_kw"]])
        if ob is not None:
            def fsd(e):
                e.matmul(pST, kT_t[:, 2 * h, :], qT_t[:, 2 * h, :], start=True, stop=False)
                e.matmul(pST, kT_t[:, 2 * h + 1, :], qT_t[:, 2 * h + 1, :], start=False, stop=True)
                e.matmul(pD, negsel[0:36, q, :], grow_t[0:36, :], start=True, stop=False)
                return e.matmul(pD, ident, maskf if dr == 0 else maskb, start=False, stop=True)
            S.op("pe", fsd, reads=[L_["B_ldo"], B_const], writes=[H_["B_pSD"]])
            S.op("act", (lambda e: e.activation(wT, pD, AF.Exp, bias=Rcol[:, ci, h:h + 1])),
                 reads=[H_["B_pSD"], B_cols], writes=[H_["B_wT"]])
            S.op("dve", (lambda e: e.scalar_tensor_tensor(STb, pST, 0.0625, wT, ALU.mult, ALU.mult)),
                 reads=[H_["B_pSD"], H_["B_wT"]], writes=[H_["B_ST"]])

    pend_fin = []

    def emit_BC(it):
        while pend_fin:
            pend_fin.pop(0)()
        dr, sc, cb, ob, h, q, ci, L_, H_ = ctx_of(it)
        vaug, qT_t, hs_t = L_["vaug"], L_["qT"], L_["hs"]
        B_ld, B_ldo, B_hs = L_["B_ld"], L_["B_ldo"], L_["B_hs"]
        STb, P2sb, hn, dd, kw = H_["STb"], H_["P2sb"], H_["hn"], H_["dd"], H_["kw"]
        pP2, pP1, pCU = H_["pP2"], H_["pP1"], H_["pCU"]
        for c in range(2):
            mm(pCU[c], [(kw[:, c * 128:(c + 1) * 128], vaug[:, h, 0:257])], reads=[H_["B_kw"], B_ld], writes=[H_["B_pCU"][c]])
        if ob is not None:
            mm(pP1, [(qT_t[:, 2 * h + c, :], C16[:, q, c, 0:257]) for c in range(2)], reads=[B_ldo] + B_C16[q], writes=[H_["B_pP1"]])
        for c in range(2):
            S.op("dve", (lambda e, c=c: e.scalar_tensor_tensor(C32[:, q, c, :], C32[:, q, c, :], decay[:, q, sc:sc + 1], pCU[c], ALU.mult, ALU.add)),
                 reads=[H_["B_pCU"][c], B_cols], writes=[B_C32[q][c]])
            S.op("act", (lambda e, c=c: e.copy(C16[:, q, c, 0:257], C32[:, q, c, :])), reads=[B_C32[q][c]], writes=[B_C16[q][c]])
        if ob is not None:
            mm(pP2, [(STb, vaug[:, h, 0:257])], reads=[H_["B_ST"], B_ld], writes=[H_["B_pP2"]])
            S.op("act", (lambda e: e.copy(P2sb, pP2)), reads=[H_["B_pP2"]], writes=[H_["B_P2"]])
            S.op("dve", (lambda e: e.scalar_tensor_tensor(hn, pP1, acol[:, q, sc:sc + 1], P2sb, ALU.mult, ALU.add)),
                 reads=[H_["B_pP1"], H_["B_P2"], B_cols], writes=[H_["B_hn"]])
            S.op("dve", (lambda e: e.scalar_tensor_tensor(dd[:, 0:1], hn[:, 256:257], -1.0, hn[:, 256:257], ALU.mult, ALU.max)),
                 reads=[H_["B_hn"]], writes=[H_["B_dd"]])
            S.op("dve", (lambda e: e.tensor_tensor(dd[:, 0:1], dd[:, 0:1], Ecol[:, ci, h:h + 1], ALU.max)),
                 reads=[H_["B_dd"], B_cols], writes=[H_["B_dd"]])
            S.op("dve", (lambda e: e.reciprocal(dd[:, 1:2], dd[:, 0:1])), reads=[H_["B_dd"]], writes=[H_["B_dd"]])
            def fin():
                if dr == 0:
                    S.op("act", (lambda e: e.activation(hs_t[:, h * 256:(h + 1) * 256], hn[:, 0:256], AF.Identity, scale=dd[:, 1:2])),
                         reads=[H_["B_hn"], H_["B_dd"]], writes=[B_hs])
                else:
                    S.op("dve", (lambda e: e.scalar_tensor_tensor(hs_t[:, h * 256:(h + 1) * 256], hn[:, 0:256], dd[:, 1:2],
                                                                  hs_t[:, h * 256:(h + 1) * 256], ALU.mult, ALU.add)),
                         reads=[H_["B_hn"], H_["B_dd"]], writes=[B_hs])
                if h == 3:
                    S.dma("sp", (lambda e: e.dma_start(out=HS[ob:ob + 128, :], in_=hs_t)), reads=[B_hs], writes=[B_HS[ob // 128]])
            if dr == 0:
                pend_fin.append(fin)
            else:
                fin()

    emit_loads(0); emit_loads(1)
    for n in range(len(items) + 1):
        if n < len(items):
            si, h, _ = items[n]
            if h == 0:
                emit_load_hs(si)
            if h == 1 and si + 2 < len(steps):
                emit_loads(si + 2)
            emit_A(items[n])
        if n >= 1:
            emit_BC(items[n - 1])
    while pend_fin:
        pend_fin.pop(0)()
    S.barrier()
    if g_["stage"] < 4:
        return
    build_rest3(g_, locals())


def build_rest(env):
    nc = env["nc"]
    S = env["S"]; ps = env["ps"]; Bps = env["Bps"]; arena = env["arena"]
    u2 = env["u2"]; B_u2 = env["B_u2"]; vcol = env["vcol"]; DER = env["DER"]
    ident = env["ident"]; maskf = env["maskf"]; maskb = env["maskb"]; CS = env["CS"]; M2 = env["M2"]; M3 = env["M3"]
    id16 = env["id16_t"]; ones16 = env["ones16_t"]
    B_const = env["B_const"]; B_der = env["B_der"]
    stage = env["stage"]; dbg = env["dbg"]; dbg_tensor = env["dbg_tensor"]
    dscr = env["dscr"]
    H1, AB, PQ, KT, QT, KTOK, VTOK = [env[n] for n in "H1 AB PQ KT QT KTOK VTOK".split()]
    B_H1, B_AB, B_PQ, B_KT, B_QT, B_KTOK, B_VTOK = [env["B_" + n] for n in "H1 AB PQ KT QT KTOK VTOK".split()]
    GROW = dscr("GROW", [36, NT], F32)
    B_GROW = Buf("GROW")
    HS = dscr("HS", [OWN, D], F32)
    B_HS = [Buf("HS%d" % i) for i in range(16)]

    def mm(out, pairs, reads, writes):
        def fn(e):
            n = len(pairs)
            for i, (l, r) in enumerate(pairs):
                ins = e.matmul(out, l, r, start=(i == 0), stop=(i == n - 1))
            return ins
        return S.op("pe", fn, reads=reads, writes=writes)

    NCH = 52
    Rcol = arena.f32(NCH * 4).rearrange("p (c h) -> p c h", h=4)
    Gcol = arena.f32(NCH * 4).rearrange("p (c h) -> p c h", h=4)
    Ecol = arena.f32(NCH * 4).rearrange("p (c h) -> p c h", h=4)
    gend = arena.f32(8 * 35).rearrange("p (q c) -> p q c", q=8)
    acol = arena.f32(8 * 34).rearrange("p (q c) -> p q c", q=8)
    wkcol = arena.f32(8 * 34).rearrange("p (q c) -> p q c", q=8)
    decay = arena.f32(8 * 34).rearrange("p (q c) -> p q c", q=8)
    B_cols = Buf("cols")
    yfT = arena.bf16(4 * OWN).rearrange("p (g t) -> p g t", g=4)
    B_yfT = Buf("yfT")
    C32 = arena.f32(8 * 2 * 257).rearrange("p (q c e) -> p q c e", q=8, c=2)
    C16 = arena.bf16(8 * 2 * 258).rearrange("p (q c e) -> p q c e", q=8, c=2)
    B_C = [Buf("C%d" % i) for i in range(8)]
    sel_t = arena.f32(2048)
    S.dma("sp", lambda e: e.dma_start(out=sel_t[0:36, :], in_=env["selc"]), writes=[B_const])
    sel = sel_t[0:36, 0:1024].rearrange("r (p m) -> r p m", p=8)
    negsel = sel_t[0:36, 1024:2048].rearrange("r (p m) -> r p m", p=8)
    mB = arena.mark()

    aLI = arena.f32(NT)
    aLF = arena.f32(NT)
    aB = arena.f32(NT)
    aG = arena.f32(NT)
    ones_r = arena.f32(512)
    tmpE = arena.f32(512)
    wgf_s = arena.bf16(8 * 8).rearrange("p (k n) -> p k n", k=8)
    wgb_s = arena.bf16(8 * 72).rearrange("p (k n) -> p k n", k=8)
    bgs = arena.f32(4)
    B_rows = Buf("rows")
    B_wg = Buf("wg")
    B_tmpE = Buf("tmpE")
    for a in (aLI, aLF, aB, aG):
        S.op("pool", (lambda e, a=a: e.memset(a, 0.0)), writes=[B_rows])
    S.op("pool", lambda e: e.memset(ones_r, 1.0), writes=[B_wg])
    S.op("pool", lambda e: e.memset(gend[:, :, :], 0.0), writes=[B_cols])
    for q in range(8):
        S.op("pool", (lambda e, q=q: e.memset(C32[:, q, :, :], 0.0)), writes=[B_C[q]])
        S.op("pool", (lambda e, q=q: e.memset(C16[:, q, :, :], 0.0)), writes=[B_C[q]])
    with nc.allow_non_contiguous_dma(reason="tiny gate weights"):
        pass
    wgate_f = env["wgate_f"]; wgate_b = env["wgate_b"]; bg = env["bg"]
    S.dma("pool", lambda e: e.dma_start(out=wgf_s, in_=wgate_f.rearrange("(k p) n -> p k n", p=128)), writes=[B_wg])
    S.dma("pool", lambda e: e.dma_start(out=wgb_s, in_=wgate_b.rearrange("(k p) n -> p k n", p=128)), writes=[B_wg])
    S.dma("sp", lambda e: e.dma_start(out=bgs[0:36, 0:2], in_=bg), writes=[B_wg])
    S.op("dve", lambda e: e.tensor_scalar(bgs[0:36, 2:3], bgs[0:36, 1:2], -1.0, None, ALU.mult), reads=[B_wg], writes=[B_wg])

    def gate_tile(jc0, W, rhs_fn, r0, r1, wl, wl_lf, pbank, rev=False):
        def pv(bank):
            return ps[bank][r0:r1, W - 1::-1] if rev else ps[bank][r0:r1, 0:W]
        mm(ps[pbank][0:r1, 0:W], [(wl(k), rhs_fn(k)) for k in range(8)], reads=[B_wg] + B_u2, writes=[Bps[pbank]])
        S.op("act", lambda e: e.activation(aLI[r0:r1, jc0:jc0 + W], pv(pbank), AF.Identity, bias=bgs[r0:r1, 0:1]),
             reads=[Bps[pbank], B_wg], writes=[B_rows])
        mm(ps[pbank + 1][0:r1, 0:W], [(wl_lf(k), rhs_fn(k)) for k in range(8)], reads=[B_wg] + B_u2, writes=[Bps[pbank + 1]])
        S.op("act", lambda e: e.activation(tmpE[r0:r1, 0:W], pv(pbank + 1), AF.Exp, bias=bgs[r0:r1, 2:3], scale=-1.0),
             reads=[Bps[pbank + 1], B_wg], writes=[B_tmpE])
        S.op("act", lambda e: e.activation(tmpE[r0:r1, 0:W], tmpE[r0:r1, 0:W], AF.Ln, bias=1.0), reads=[B_tmpE], writes=[B_tmpE])
        S.op("dve", lambda e: e.tensor_scalar(aLF[r0:r1, jc0:jc0 + W], tmpE[r0:r1, 0:W], -1.0, None, ALU.mult),
             reads=[B_tmpE], writes=[B_rows])

    ti = 0
    for (jc0, W) in [(0, 256)] + [(256 + 512 * i, 512) for i in range(4)]:
        gate_tile(jc0, W, (lambda k, jc0=jc0, W=W: u2[:, k, jc0:jc0 + W]), 0, 4,
                  (lambda k: wgf_s[:, k, 0:4]), (lambda k: wgf_s[:, k, 4:8]), 2 * (ti % 2))
        ti += 1
    def rev_u2(k, hi, W):
        return u2[:, k, hi - W + 1:hi + 1]
    gate_tile(0, 256, (lambda k: rev_u2(k, 255, 256)), 32, 36,
              (lambda k: wgb_s[:, k, 0:36]), (lambda k: wgb_s[:, k, 36:72]), 2 * (ti % 2), rev=True)
    ti += 1
    for i in range(8):
        jc0 = 256 + 512 * i
        hi = 4607 - jc0
        gate_tile(jc0, 512, (lambda k, hi=hi: rev_u2(k, hi, 512)), 32, 36,
                  (lambda k: wgb_s[:, k, 0:36]), (lambda k: wgb_s[:, k, 36:72]), 2 * (ti % 2), rev=True)
        ti += 1
    pieces = [(0, 256)] + [(256 + 512 * i, 512) for i in range(8)]
    for pi, (c0, W) in enumerate(pieces):
        init = 0.0 if pi == 0 else aB[0:36, c0 - 1:c0]
        S.op("dve", (lambda e, c0=c0, W=W, init=init: e.tensor_tensor_scan(aB[0:36, c0:c0 + W], ones_r[0:36, 0:W], aLF[0:36, c0:c0 + W],
                                                                           init, ALU.mult, ALU.add)),
             reads=[B_rows, B_wg], writes=[B_rows])
    S.op("dve", lambda e: e.tensor_tensor(aLI[0:36, :], aLI[0:36, :], aB[0:36, :], ALU.subtract), reads=[B_rows], writes=[B_rows])
    for pi, (c0, W) in enumerate(pieces):
        init = 0.0 if pi == 0 else aG[0:36, c0 - 1:c0]
        S.op("dve", (lambda e, c0=c0, W=W, init=init: e.tensor_tensor_scan(aG[0:36, c0:c0 + W], ones_r[0:36, 0:W], aLI[0:36, c0:c0 + W],
                                                                           init, ALU.mult, ALU.max)),
             reads=[B_rows, B_wg], writes=[B_rows])
    S.op("dve", lambda e: e.tensor_tensor(aB[0:36, :], aB[0:36, :], aG[0:36, :], ALU.add), reads=[B_rows], writes=[B_rows])
    if dbg:
        d_rows = dbg_tensor("d_rows", [3, 36, NT])
        for i, a in enumerate((aLI, aG, aB)):
            S.dma("sp", (lambda e, i=i, a=a: e.dma_start(out=d_rows[i], in_=a[0:36, :])), reads=[B_rows])
    def n0_of(sc):
        return 128 if sc == 0 else (0 if sc == 1 else 4480 - 128 * sc)
    for ai, (arr, bank) in enumerate(((aLI, 0), (aB, 2), (aG, 1))):
        S.op("dve", (lambda e, arr=arr: e.tensor_copy(aLF[32:36, 0:256], arr[32:36, 255::-1])), reads=[B_rows], writes=[B_rows])
        S.op("dve", (lambda e, arr=arr: e.tensor_copy(aLF[32:36, 256:NT], arr[32:36, NT - 1:255:-1])), reads=[B_rows], writes=[B_rows])
        def fn(e, arr=arr, bank=bank):
            for sc in range(18):
                ins = e.matmul(ps[bank][:, sc * 4:(sc + 1) * 4], arr[0:4, sc * 128:(sc + 1) * 128], ident[0:4, 0:4],
                               start=True, stop=True)
            for sc in range(34):
                n0 = n0_of(sc)
                ins = e.matmul(ps[bank][:, (18 + sc) * 4:(19 + sc) * 4], aLF[32:36, n0:n0 + 128],
                               ident[32:36, 32:36], start=True, stop=True)
            return ins
        S.op("pe", fn, reads=[B_rows, B_const], writes=[Bps[bank]])
    S.op("dve", lambda e: e.tensor_copy(aLF[0:4, :], aG[0:4, :]), reads=[B_rows], writes=[B_rows])
    S.dma("sp", lambda e: e.dma_start(out=GROW, in_=aLF[0:36, :]), reads=[B_rows], writes=[B_GROW])
    S.op("act", lambda e: e.copy(Rcol[:, :, :], ps[0][:, 0:NCH * 4].rearrange("p (c h) -> p c h", h=4)), reads=[Bps[0]], writes=[B_cols])
    S.op("act", lambda e: e.copy(Gcol[:, :, :], ps[1][:, 0:NCH * 4].rearrange("p (c h) -> p c h", h=4)), reads=[Bps[1]], writes=[B_cols])
    S.op("act", lambda e: e.activation(Ecol[:, :, :], ps[2][:, 0:NCH * 4].rearrange("p (c h) -> p c h", h=4), AF.Exp, scale=-1.0),
         reads=[Bps[2]], writes=[B_cols])
    def fn(e):
        for q in range(8):
            n = 18 if q < 4 else 34
            ins = e.matmul(ps[3][:, q * 34:q * 34 + n], sel[0:36, q, :], aG[0:36, 127:127 + 128 * (n - 1) + 1:128], start=True, stop=True)
        return ins
    S.op("pe", fn, reads=[B_rows, B_const], writes=[Bps[3]])
    for q in range(8):
        n = 18 if q < 4 else 34
        S.op("act", (lambda e, q=q, n=n: e.copy(gend[:, q, 1:1 + n], ps[3][:, q * 34:q * 34 + n])), reads=[Bps[3]], writes=[B_cols])
    tq = arena.f32(34)
    B_tq = Buf("tq")
    for q in range(8):
        n = 18 if q < 4 else 34
        base = 0 if q < 4 else 18
        h = q % 4
        S.op("dve", (lambda e, q=q, n=n, base=base, h=h: e.tensor_tensor(tq[:, 0:n], gend[:, q, 0:n], Gcol[:, base:base + n, h], ALU.subtract)),
             reads=[B_cols], writes=[B_tq])
        S.op("act", (lambda e, q=q, n=n: e.activation(acol[:, q, 0:n], tq[:, 0:n], AF.Exp)), reads=[B_tq], writes=[B_cols])
        S.op("dve", (lambda e, q=q, n=n, base=base, h=h: e.tensor_tensor(tq[:, 0:n], Rcol[:, base:base + n, h], gend[:, q, 1:1 + n], ALU.subtract)),
             reads=[B_cols], writes=[B_tq])
        S.op("act", (lambda e, q=q, n=n: e.activation(wkcol[:, q, 0:n], tq[:, 0:n], AF.Exp)), reads=[B_tq], writes=[B_cols])
        S.op("dve", (lambda e, q=q, n=n: e.tensor_tensor(tq[:, 0:n], gend[:, q, 0:n], gend[:, q, 1:1 + n], ALU.subtract)),
             reads=[B_cols], writes=[B_tq])
        S.op("act", (lambda e, q=q, n=n: e.activation(decay[:, q, 0:n], tq[:, 0:n], AF.Exp)), reads=[B_tq], writes=[B_cols])
    S.barrier()
    arena.reset(mB)
    if stage < 3:
        return
    build_rest2(env, locals())


COL_F = 0
COL_Q = 512
COL_K = 1536
COL_V = 2560
COL_O = 3584
COL_GATES = 4608
COL_BR = 4624


def _dft_consts(flip):
    idx = (63 - np.arange(64)) if flip else np.arange(64)
    ang = 2 * np.pi * np.outer(idx, idx) / 64.0
    Cc = np.cos(ang) / 8.0
    Sc = np.sin(ang) / 8.0
    ch = np.arange(128)
    angc = 2 * np.pi * np.outer(ch, ch) / 128.0
    CS = np.concatenate([np.cos(angc), np.sin(angc)], axis=1) / np.sqrt(128.0)
    M2 = np.zeros((128, 128))
    M2[0:64, 0:64] = Cc
    M2[64:128, 0:64] = -Sc
    M2[0:64, 64:128] = Sc
    M2[64:128, 64:128] = Cc
    M3 = np.zeros((128, 32))
    M3[0:64, :] = Cc[:, 0:32]
    M3[64:128, :] = -Sc[:, 0:32]
    return CS, M2, M3


def make_inputs(inp):
    f32 = np.float32
    x = np.asarray(inp["x"], f32)
    ctx = np.asarray(inp["ctx"], f32)
    c = np.asarray(inp["c"], f32)
    c_ctx = np.asarray(inp["c_ctx"], f32)
    w_in = np.asarray(inp["w_in"], f32)[0]
    b_in = np.asarray(inp["b_in"], f32)[0]
    conv_w = np.asarray(inp["conv_w"], f32)[0]
    conv_b = np.asarray(inp["conv_b"], f32)[0]
    norm_g = np.asarray(inp["norm_g"], f32)[0]

    def fm(v):
        return np.ascontiguousarray(v.reshape(-1, 128).T)

    def r13(w):
        return np.ascontiguousarray(w.reshape(8, 128, 2, NJ, 128).transpose(3, 1, 0, 2, 4).reshape(NJ, 128, 2048))

    def r2(w):
        return np.ascontiguousarray(w.reshape(NJ, 128, 8, 128).transpose(2, 1, 0, 3).reshape(8, 128, FF))

    shared = {
        "w_ada": np.ascontiguousarray(np.asarray(inp["w_ada"], f32)[0]),
        "w13a": r13(np.asarray(inp["w13_a"], f32)[0]), "w2a": r2(np.asarray(inp["w2_a"], f32)[0]),
        "w13b": r13(np.asarray(inp["w13_b"], f32)[0]), "w2b": r2(np.asarray(inp["w2_b"], f32)[0]),
        "wF": np.ascontiguousarray(w_in[:, COL_F:COL_Q]), "wq": np.ascontiguousarray(w_in[:, COL_Q:COL_K]),
        "wk": np.ascontiguousarray(w_in[:, COL_K:COL_V]), "wv": np.ascontiguousarray(w_in[:, COL_V:COL_O]),
        "wo": np.ascontiguousarray(w_in[:, COL_O:COL_GATES]),
        "wgf": np.ascontiguousarray(w_in[:, COL_BR:COL_BR + D]), "wgm": np.ascontiguousarray(w_in[:, COL_BR + D:]),
        "w_four": np.ascontiguousarray(np.asarray(inp["w_four"], f32)[0]),
        "w_mproj": np.ascontiguousarray(np.asarray(inp["w_mproj"], f32)[0]),
        "w_out": np.ascontiguousarray(np.asarray(inp["w_out"], f32)[0]),
        "rowv": np.concatenate([b_in[COL_V:COL_O], b_in[COL_O:COL_GATES], np.asarray(inp["head_g"], f32)[0]])[None, :].copy(),
    }
    sel = np.zeros((36, 8, 128), f32)
    for p in range(8):
        row = (p % 4) + (32 if p >= 4 else 0)
        sel[row, p, :] = 1.0
    selc = np.concatenate([sel.reshape(36, -1), -sel.reshape(36, -1)], axis=1)
    s_idx = np.arange(128)[:, None]
    t_idx = np.arange(128)[None, :]
    maskf = np.where(s_idx <= t_idx, 0.0, NEG).astype(f32)
    maskb = np.where(s_idx >= t_idx, 0.0, NEG).astype(f32)
    maps = []
    for core in range(8):
        b, half = core // 2, core % 2
        flip = half == 1
        xb = x[b][::-1] if flip else x[b]
        cb_ = ctx[b][::-1] if flip else ctx[b]
        g = COL_GATES
        if flip:
            gi_f, gf_f, gi_b, gf_b = g + 8, g + 12, g + 0, g + 4
            cw = conv_w[::-1]
        else:
            gi_f, gf_f, gi_b, gf_b = g + 0, g + 4, g + 8, g + 12
            cw = conv_w
        wgate_f = np.concatenate([w_in[:, gi_f:gi_f + 4], w_in[:, gf_f:gf_f + 4]], axis=1)
        wgate_b = np.zeros((D, 72), f32)
        wgate_b[:, 32:36] = w_in[:, gi_b:gi_b + 4]
        wgate_b[:, 36 + 32:36 + 36] = w_in[:, gf_b:gf_b + 4]
        bgv = np.zeros((36, 2), f32)
        bgv[0:4, 0] = b_in[gi_f:gi_f + 4]
        bgv[0:4, 1] = b_in[gf_f:gf_f + 4]
        bgv[32:36, 0] = b_in[gi_b:gi_b + 4]
        bgv[32:36, 1] = b_in[gf_b:gf_b + 4]
        vecs = np.zeros((128, NV), f32)

        def put(name, arr):
            o, w = VEC[name]
            assert arr.shape == (128, w), (name, arr.shape)
            vecs[:, o:o + w] = arr
        put("bada", fm(np.asarray(inp["b_ada"], f32)[0]))
        put("ng", np.concatenate([fm(norm_g[i]) for i in range(6)], axis=1))
        put("bF", fm(b_in[COL_F:COL_Q])); put("bq", fm(b_in[COL_Q:COL_K])); put("bk", fm(b_in[COL_K:COL_V]))
        put("cwq", np.concatenate([fm(cw[t, 0:D]) for t in range(3)], axis=1))
        put("cwk", np.concatenate([fm(cw[t, D:2 * D]) for t in range(3)], axis=1))
        put("cbq", fm(conv_b[0:D])); put("cbk", fm(conv_b[D:2 * D]))
        put("bgf", fm(b_in[COL_BR:COL_BR + D])); put("bgm", fm(b_in[COL_BR + D:]))
        cvec = np.zeros((128, 8, 2), f32)
        cvec[:, :, 0] = fm(c[b])
        cvec[:, :, 1] = fm(c_ctx)
        CS, M2, M3 = _dft_consts(flip)
        cst = np.zeros((128, 128 * 4 + 256 + 128 + 32), f32)
        cst[:, 0:128] = np.eye(128)
        cst[:, 128:256] = maskf
        cst[:, 256:384] = maskb
        cst[:, 512:768] = CS
        cst[:, 768:896] = M2
        cst[:, 896:928] = M3
        m = dict(shared)
        m.update({
            "xT": np.ascontiguousarray(xb.T), "ctxT": np.ascontiguousarray(cb_.T),
            "cvec": cvec.reshape(128, 16), "vecs": vecs, "bg": bgv,
            "wgate_f": np.ascontiguousarray(wgate_f), "wgate_b": wgate_b, "cst": cst, "selc": selc,
        })
        maps.append(m)
    return maps


def kernel(**inputs):
    nc, _ = build()
    maps = make_inputs(inputs)
    res = run_bass_kernel_spmd(nc, maps, core_ids=list(range(8)))
    out = np.zeros((4, SEQ, D), np.float32)
    for core in range(8):
        b, half = core // 2, core % 2
        o = np.asarray(res.results[core]["outT"]).T
        if half == 0:
            out[b, 0:OWN] = o
        else:
            out[b, OWN:] = o[::-1]
    return out
```
